# Optimizing a Trainium2 kernel written in Bass

```python
import math
import jax, jax.numpy as jnp
from jax import lax
import numpy as np

D_MODEL = 1024
BATCH = 8
SEQ = 4096
DEPTH = 2

GRID_W = 64
CTX_LEN = 256
MIX_WIDTH = 2 * D_MODEL
SSD_WIDTH = MIX_WIDTH // 2
SSD_HEAD_DIM = 64
SSD_HEADS = SSD_WIDTH // SSD_HEAD_DIM
SSD_GROUPS = 2
SSD_STATE = 128
SSD_CONV = 5
SSD_CONV_DIM = SSD_WIDTH + 2 * SSD_GROUPS * SSD_STATE
MLA_WIDTH = MIX_WIDTH // 4
MLA_V_DIM = 64
MLA_HEADS = MLA_WIDTH // MLA_V_DIM
MLA_NOPE_DIM = 64
MLA_ROPE_DIM = 32
MLA_Q_LORA = 384
MLA_KV_LORA = 256
MLA_SCALE = (MLA_NOPE_DIM + MLA_ROPE_DIM) ** -0.5
RET_WIDTH = MIX_WIDTH // 4
RET_HEADS = 4
RET_V_DIM = RET_WIDTH // RET_HEADS
RET_QK_DIM = RET_V_DIM // 2
RET_QK_WIDTH = RET_HEADS * RET_QK_DIM

CHUNK = 128
Q_BLOCK = 128
ROPE_THETA = 10000.0
ALPHA = (2 * DEPTH) ** 0.25
BETA = (8 * DEPTH) ** -0.25
LN_EPS = 1e-5
RMS_EPS = 1e-6

IN_WIDTHS = (SSD_WIDTH, SSD_CONV_DIM, SSD_HEADS,
             MLA_Q_LORA, MLA_KV_LORA, MLA_ROPE_DIM, MLA_WIDTH,
             RET_QK_WIDTH, RET_QK_WIDTH, RET_WIDTH, RET_WIDTH)
IN_WIDTH = sum(IN_WIDTHS)

kernel_name = "hymba_ssd_mla_retention_prefix_trunk"


def in_splits():
    return [int(s) for s in np.cumsum(IN_WIDTHS)[:-1]]


def layer_norm(x, g, b):
    xf = x.astype(jnp.float32)
    mu = jnp.mean(xf, -1, keepdims=True)
    var = jnp.mean(jnp.square(xf - mu), -1, keepdims=True)
    return ((xf - mu) * lax.rsqrt(var + LN_EPS)).astype(x.dtype) * g + b


def head_norm(x):
    xf = x.astype(jnp.float32)
    mu = jnp.mean(xf, -1, keepdims=True)
    var = jnp.mean(jnp.square(xf - mu), -1, keepdims=True)
    return ((xf - mu) * lax.rsqrt(var + LN_EPS)).astype(x.dtype)


def rms_norm(x, g):
    xf = x.astype(jnp.float32)
    return (xf * lax.rsqrt(jnp.mean(xf * xf, -1, keepdims=True) + RMS_EPS)).astype(x.dtype) * g


def axial_rope(rows, rot_dim):
    t = jnp.arange(rows * GRID_W)
    row = (t // GRID_W).astype(jnp.float32)
    col = (t % GRID_W).astype(jnp.float32)
    n_freq = rot_dim // 4
    inv = ROPE_THETA ** (-jnp.arange(n_freq, dtype=jnp.float32) / n_freq)
    ang = jnp.concatenate([row[:, None] * inv, col[:, None] * inv], -1)
    return jnp.cos(ang), jnp.sin(ang)


def apply_rope(u, cos, sin):
    half = u.shape[-1] // 2
    u1, u2 = u[..., :half], u[..., half:]
    cos = cos.astype(u.dtype)
    sin = sin.astype(u.dtype)
    return jnp.concatenate([u1 * cos - u2 * sin, u2 * cos + u1 * sin], -1)


def dwconv_centred(u, w, b):
    pad = w.shape[0] // 2
    out = lax.conv_general_dilated(u, w[:, None, :], window_strides=(1,), padding=[(pad, pad)],
                                   dimension_numbers=('NWC', 'WIO', 'NWC'),
                                   feature_group_count=u.shape[-1])
    return out + b


def chunked_scan(q, k, v, log_a, h0):
    b, n, g, d_state = q.shape
    hg, p = v.shape[-2:]
    nc = n // CHUNK
    qc = q.reshape(b, nc, CHUNK, g, d_state)
    kc = k.reshape(b, nc, CHUNK, g, d_state)
    vc = v.reshape(b, nc, CHUNK, g, hg, p)
    cum = jnp.cumsum(log_a.astype(jnp.float32).reshape(b, nc, CHUNK, g, hg), axis=2)
    lower = jnp.tril(jnp.ones((CHUNK, CHUNK), dtype=bool))[:, :, None, None]
    seg = cum[:, :, :, None] - cum[:, :, None, :]
    decay = jnp.exp(jnp.where(lower, seg, -jnp.inf))
    scores = jnp.einsum('bcign,bcjgn->bcijg', qc, kc)
    y_intra = jnp.einsum('bcijgh,bcjghp->bcighp', scores[..., None] * decay, vc)
    to_end = jnp.exp(cum[:, :, -1:] - cum)
    states = jnp.einsum('bcjgn,bcjghp->bcghnp', kc, vc * to_end[..., None])
    chunk_decay = jnp.exp(cum[:, :, -1])

    def step(h, inp):
        s, dcy = inp
        return dcy[..., None, None] * h + s, h

    h_last, h_in = lax.scan(step, h0, (jnp.moveaxis(states, 1, 0), jnp.moveaxis(chunk_decay, 1, 0)))
    h_in = jnp.moveaxis(h_in, 0, 1)
    y_inter = jnp.einsum('bcign,bcghnp->bcighp', qc, h_in) * jnp.exp(cum)[..., None]
    y = (y_intra + y_inter).reshape(b, n, g, hg, p)
    return y.astype(v.dtype), h_last


def directional_scan(q_c, k_c, v_c, la_c, q_l, k_l, v_l, la_l, reverse):
    f = (lambda a: jnp.flip(a, 1)) if reverse else (lambda a: a)
    b, _, g, n = q_c.shape
    hg, p = v_c.shape[-2:]
    h0 = jnp.zeros((b, g, hg, n, p), jnp.float32)
    y_c, h_c = chunked_scan(f(q_c), f(k_c), f(v_c), f(la_c), h0)
    y_l, _ = chunked_scan(f(q_l), f(k_l), f(v_l), f(la_l), h_c)
    return f(y_c), f(y_l)


def ssd_prep(xbc, conv_w, conv_b):
    u = jax.nn.silu(dwconv_centred(xbc, conv_w, conv_b))
    b, n, _ = u.shape
    hg = SSD_HEADS // SSD_GROUPS
    xs = u[..., :SSD_WIDTH].reshape(b, n, SSD_GROUPS, hg, SSD_HEAD_DIM)
    bm = u[..., SSD_WIDTH:SSD_WIDTH + SSD_GROUPS * SSD_STATE].reshape(b, n, SSD_GROUPS, SSD_STATE)
    cm = u[..., SSD_WIDTH + SSD_GROUPS * SSD_STATE:].reshape(b, n, SSD_GROUPS, SSD_STATE)
    return xs, bm, cm


def ssd_mixer(p_c, p_l, conv_w, conv_b, a_log_f, a_log_b, dt_bias_f, dt_bias_b, d_skip, norm_w, with_ctx):
    z_c, xbc_c, dt_c = p_c
    z_l, xbc_l, dt_l = p_l
    xs_c, b_c, c_c = ssd_prep(xbc_c, conv_w, conv_b)
    xs_l, b_l, c_l = ssd_prep(xbc_l, conv_w, conv_b)
    hg = SSD_HEADS // SSD_GROUPS
    d = d_skip.reshape(SSD_GROUPS, hg)[..., None]
    y_c = xs_c * d
    y_l = xs_l * d
    for a_log, dt_bias, rev in ((a_log_f, dt_bias_f, False), (a_log_b, dt_bias_b, True)):
        a = -jnp.exp(a_log.astype(jnp.float32)).reshape(SSD_GROUPS, hg)
        dtc = jax.nn.softplus((dt_c + dt_bias).astype(jnp.float32)).reshape(*dt_c.shape[:2], SSD_GROUPS, hg)
        dtl = jax.nn.softplus((dt_l + dt_bias).astype(jnp.float32)).reshape(*dt_l.shape[:2], SSD_GROUPS, hg)
        yc, yl = directional_scan(c_c, b_c, (xs_c * dtc[..., None]).astype(xs_c.dtype), dtc * a,
                                  c_l, b_l, (xs_l * dtl[..., None]).astype(xs_l.dtype), dtl * a, rev)
        y_c = y_c + yc
        y_l = y_l + yl
    b, n = z_l.shape[:2]
    out_l = rms_norm(y_l.reshape(b, n, SSD_WIDTH).astype(z_l.dtype) * jax.nn.silu(z_l), norm_w)
    out_c = None
    if with_ctx:
        out_c = rms_norm(y_c.reshape(b, z_c.shape[1], SSD_WIDTH).astype(z_c.dtype) * jax.nn.silu(z_c), norm_w)
    return out_l, out_c


def mla_attend(q_nope, q_rope, k_nope, k_rope, v):
    s = (jnp.einsum('bqhd,bkhd->bhqk', q_nope, k_nope)
         + jnp.einsum('bqhr,bkr->bhqk', q_rope, k_rope)).astype(jnp.float32) * MLA_SCALE
    pr = jax.nn.softmax(s, axis=-1).astype(v.dtype)
    return jnp.einsum('bhqk,bkhd->bqhd', pr, v)


def mla_mixer(p_c, p_l, q_norm, w_uq, kv_norm, w_ukv, cos, sin, with_ctx):
    def project(cq, ckv, kr, rope):
        b, n, _ = cq.shape
        q = (rms_norm(cq, q_norm) @ w_uq).reshape(b, n, MLA_HEADS, MLA_NOPE_DIM + MLA_ROPE_DIM)
        kv = (rms_norm(ckv, kv_norm) @ w_ukv).reshape(b, n, MLA_HEADS, MLA_NOPE_DIM + MLA_V_DIM)
        q_nope, q_rope = q[..., :MLA_NOPE_DIM], q[..., MLA_NOPE_DIM:]
        k_nope, v = kv[..., :MLA_NOPE_DIM], kv[..., MLA_NOPE_DIM:]
        if rope:
            q_rope = apply_rope(q_rope, cos[:, None], sin[:, None])
            kr = apply_rope(kr, cos, sin)
        return q_nope, q_rope, k_nope, kr, v

    cq_c, ckv_c, kr_c, gate_c = p_c
    cq_l, ckv_l, kr_l, gate_l = p_l
    qn_c, qr_c, kn_c, krr_c, v_c = project(cq_c, ckv_c, kr_c, False)
    qn_l, qr_l, kn_l, krr_l, v_l = project(cq_l, ckv_l, kr_l, True)
    kn_all = jnp.concatenate([kn_c, kn_l], 1)
    kr_all = jnp.concatenate([krr_c, krr_l], 1)
    v_all = jnp.concatenate([v_c, v_l], 1)
    b, n = qn_l.shape[:2]
    nb = n // Q_BLOCK
    blk = lambda a: jnp.moveaxis(a.reshape(b, nb, Q_BLOCK, *a.shape[2:]), 1, 0)
    o = lax.map(lambda qs: mla_attend(qs[0], qs[1], kn_all, kr_all, v_all), (blk(qn_l), blk(qr_l)))
    out_l = jnp.moveaxis(o, 0, 1).reshape(b, n, MLA_WIDTH) * jax.nn.silu(gate_l)
    out_c = None
    if with_ctx:
        o_c = mla_attend(qn_c, qr_c, kn_c, krr_c, v_c)
        out_c = o_c.reshape(b, qn_c.shape[1], MLA_WIDTH) * jax.nn.silu(gate_c)
    return out_l, out_c


def retention_mixer(p_c, p_l, log_rate_f, log_rate_b, cos, sin, with_ctx):
    def heads(q, k, v, rope):
        b, n, _ = q.shape
        q = q.reshape(b, n, RET_HEADS, RET_QK_DIM)
        k = k.reshape(b, n, RET_HEADS, RET_QK_DIM)
        if rope:
            q = apply_rope(q, cos[:, None], sin[:, None])
            k = apply_rope(k, cos[:, None], sin[:, None])
        k = k * RET_QK_DIM ** -0.5
        v = v.reshape(b, n, RET_HEADS, 1, RET_V_DIM)
        return q, k, v

    q_c, k_c, v_c, gate_c = p_c
    q_l, k_l, v_l, gate_l = p_l
    qc, kc, vc = heads(q_c, k_c, v_c, False)
    ql, kl, vl = heads(q_l, k_l, v_l, True)
    b, n = ql.shape[:2]
    lc = qc.shape[1]
    ys_c, ys_l = [], []
    for log_rate, rev in ((log_rate_f, False), (log_rate_b, True)):
        la = -jnp.exp(log_rate.astype(jnp.float32))[:, None]
        yc, yl = directional_scan(qc, kc, vc, jnp.broadcast_to(la, (b, lc, RET_HEADS, 1)),
                                  ql, kl, vl, jnp.broadcast_to(la, (b, n, RET_HEADS, 1)), rev)
        ys_c.append(yc)
        ys_l.append(yl)
    y_l = head_norm((ys_l[0] + ys_l[1]).reshape(b, n, RET_HEADS, RET_V_DIM)).astype(gate_l.dtype)
    out_l = y_l.reshape(b, n, RET_WIDTH) * jax.nn.silu(gate_l)
    out_c = None
    if with_ctx:
        y_c = head_norm((ys_c[0] + ys_c[1]).reshape(b, lc, RET_HEADS, RET_V_DIM)).astype(gate_c.dtype)
        out_c = y_c.reshape(b, lc, RET_WIDTH) * jax.nn.silu(gate_c)
    return out_l, out_c


def trunk_layer(x, cx, mod_l, mod_c, w_in, ssd_conv_w, ssd_conv_b, ssd_a_log_f, ssd_a_log_b,
                ssd_dt_bias_f, ssd_dt_bias_b, ssd_d, ssd_norm_w, mla_q_norm, mla_w_uq, mla_kv_norm,
                mla_w_ukv, ret_log_rate_f, ret_log_rate_b, w_out, ln_g, ln_b,
                cos_mla, sin_mla, cos_ret, sin_ret, with_ctx):
    sh_l, sc_l, g_l = jnp.split(mod_l[:, None, :], 3, axis=-1)
    sh_c, sc_c, g_c = jnp.split(mod_c, 3, axis=-1)
    h_l = x * (1 + sc_l) + sh_l
    h_c = cx * (1 + sc_c) + sh_c
    parts_l = jnp.split(h_l @ w_in, in_splits(), axis=-1)
    parts_c = jnp.split(h_c @ w_in, in_splits(), axis=-1)
    ssd_l, ssd_c = ssd_mixer(parts_c[0:3], parts_l[0:3], ssd_conv_w, ssd_conv_b, ssd_a_log_f, ssd_a_log_b,
                             ssd_dt_bias_f, ssd_dt_bias_b, ssd_d, ssd_norm_w, with_ctx)
    mla_l, mla_c = mla_mixer(parts_c[3:7], parts_l[3:7], mla_q_norm, mla_w_uq, mla_kv_norm, mla_w_ukv,
                             cos_mla, sin_mla, with_ctx)
    ret_l, ret_c = retention_mixer(parts_c[7:11], parts_l[7:11], ret_log_rate_f, ret_log_rate_b,
                                   cos_ret, sin_ret, with_ctx)
    out_l = jnp.concatenate([ssd_l, mla_l, ret_l], -1) @ w_out
    x_new = layer_norm(ALPHA * x + g_l * out_l, ln_g, ln_b)
    cx_new = None
    if with_ctx:
        out_c = jnp.concatenate([ssd_c, mla_c, ret_c], -1) @ w_out
        cx_new = layer_norm(ALPHA * cx + g_c * out_c, ln_g, ln_b)
    return x_new, cx_new


def setup_inputs(seed: int = 0) -> dict:
    key = jax.random.key(seed)
    ks = jax.random.split(key, 32)
    f32 = jnp.float32
    nrm = lambda k, shape, std: std * jax.random.normal(k, shape, f32)
    x = nrm(ks[0], (BATCH, SEQ, D_MODEL), 1.0)
    c = nrm(ks[1], (BATCH, D_MODEL), 1.0)
    ctx = nrm(ks[2], (BATCH, CTX_LEN, D_MODEL), 1.0)
    c_ctx = nrm(ks[3], (D_MODEL,), 1.0)
    w_ada = nrm(ks[4], (DEPTH, D_MODEL, 3 * D_MODEL), D_MODEL ** -0.5)
    b_ada = nrm(ks[5], (DEPTH, 3 * D_MODEL), 0.02)
    w_in = nrm(ks[6], (DEPTH, D_MODEL, IN_WIDTH), D_MODEL ** -0.5)
    ssd_conv_w = nrm(ks[7], (DEPTH, SSD_CONV, SSD_CONV_DIM), SSD_CONV ** -0.5)
    ssd_conv_b = nrm(ks[8], (DEPTH, SSD_CONV_DIM), 0.02)
    ssd_a_log_f = jnp.log(jax.random.uniform(ks[9], (DEPTH, SSD_HEADS), f32, 1.0, 16.0))
    ssd_a_log_b = jnp.log(jax.random.uniform(ks[10], (DEPTH, SSD_HEADS), f32, 1.0, 16.0))
    dt_f = jnp.exp(jax.random.uniform(ks[11], (DEPTH, SSD_HEADS), f32, math.log(1e-3), math.log(1e-1)))
    dt_b = jnp.exp(jax.random.uniform(ks[12], (DEPTH, SSD_HEADS), f32, math.log(1e-3), math.log(1e-1)))
    ssd_dt_bias_f = dt_f + jnp.log(-jnp.expm1(-dt_f))
    ssd_dt_bias_b = dt_b + jnp.log(-jnp.expm1(-dt_b))
    ssd_d = 1.0 + nrm(ks[13], (DEPTH, SSD_HEADS), 0.1)
    ssd_norm_w = 1.0 + nrm(ks[14], (DEPTH, SSD_WIDTH), 0.02)
    mla_q_norm = 1.0 + nrm(ks[15], (DEPTH, MLA_Q_LORA), 0.02)
    mla_w_uq = nrm(ks[16], (DEPTH, MLA_Q_LORA, MLA_HEADS * (MLA_NOPE_DIM + MLA_ROPE_DIM)), MLA_Q_LORA ** -0.5)
    mla_kv_norm = 1.0 + nrm(ks[17], (DEPTH, MLA_KV_LORA), 0.02)
    mla_w_ukv = nrm(ks[18], (DEPTH, MLA_KV_LORA, MLA_HEADS * (MLA_NOPE_DIM + MLA_V_DIM)), MLA_KV_LORA ** -0.5)
    base_rate = jnp.log(-jnp.log1p(-(2.0 ** -(5.0 + jnp.arange(RET_HEADS, dtype=f32)))))
    ret_log_rate_f = base_rate + nrm(ks[19], (DEPTH, RET_HEADS), 0.05)
    ret_log_rate_b = base_rate + nrm(ks[20], (DEPTH, RET_HEADS), 0.05)
    w_out = nrm(ks[21], (DEPTH, MIX_WIDTH, D_MODEL), BETA * MIX_WIDTH ** -0.5)
    ln_g = 1.0 + nrm(ks[22], (DEPTH, D_MODEL), 0.02)
    ln_b = nrm(ks[23], (DEPTH, D_MODEL), 0.02)
    return {"x": x, "c": c, "ctx": ctx, "c_ctx": c_ctx, "w_ada": w_ada, "b_ada": b_ada, "w_in": w_in,
            "ssd_conv_w": ssd_conv_w, "ssd_conv_b": ssd_conv_b, "ssd_a_log_f": ssd_a_log_f,
            "ssd_a_log_b": ssd_a_log_b, "ssd_dt_bias_f": ssd_dt_bias_f, "ssd_dt_bias_b": ssd_dt_bias_b,
            "ssd_d": ssd_d, "ssd_norm_w": ssd_norm_w, "mla_q_norm": mla_q_norm, "mla_w_uq": mla_w_uq,
            "mla_kv_norm": mla_kv_norm, "mla_w_ukv": mla_w_ukv, "ret_log_rate_f": ret_log_rate_f,
            "ret_log_rate_b": ret_log_rate_b, "w_out": w_out, "ln_g": ln_g, "ln_b": ln_b}


def reference(x, c, ctx, c_ctx, w_ada, b_ada, w_in, ssd_conv_w, ssd_conv_b, ssd_a_log_f, ssd_a_log_b,
              ssd_dt_bias_f, ssd_dt_bias_b, ssd_d, ssd_norm_w, mla_q_norm, mla_w_uq, mla_kv_norm, mla_w_ukv,
              ret_log_rate_f, ret_log_rate_b, w_out, ln_g, ln_b):
    rows = x.shape[1] // GRID_W
    cos_mla, sin_mla = axial_rope(rows, MLA_ROPE_DIM)
    cos_ret, sin_ret = axial_rope(rows, RET_QK_DIM)
    sc = jax.nn.silu(c)
    sc_ctx = jax.nn.silu(c_ctx)
    cx = ctx
    for i in range(DEPTH):
        mod_l = sc @ w_ada[i] + b_ada[i]
        mod_c = sc_ctx @ w_ada[i] + b_ada[i]
        x, cx = trunk_layer(x, cx, mod_l, mod_c, w_in[i], ssd_conv_w[i], ssd_conv_b[i], ssd_a_log_f[i],
                            ssd_a_log_b[i], ssd_dt_bias_f[i], ssd_dt_bias_b[i], ssd_d[i], ssd_norm_w[i],
                            mla_q_norm[i], mla_w_uq[i], mla_kv_norm[i], mla_w_ukv[i], ret_log_rate_f[i],
                            ret_log_rate_b[i], w_out[i], ln_g[i], ln_b[i],
                            cos_mla, sin_mla, cos_ret, sin_ret, i < DEPTH - 1)
    return x
```

```python
import math
from contextlib import ExitStack
import numpy as np
import concourse.bass as bass
import concourse.mybir as mybir
from concourse.bass_utils import run_bass_kernel_spmd

F32 = mybir.dt.float32
BF16 = mybir.dt.bfloat16
AF = mybir.ActivationFunctionType
ALU = mybir.AluOpType

PE, ACT, DVE, POOL, SP = "pe", "act", "dve", "pool", "sp"
EPOCH = 30000
DMA_K = 8
DMA_EPOCH = 1800

D = 1024
SEQ = 4096
CTX = 256
NTOK = SEQ + CTX
NCH = NTOK // 128
HC = NTOK + 8
DEPTH = 2
ALPHA = (2 * DEPTH) ** 0.25
LN_EPS = 1e-5
RMS_EPS = 1e-6
MLA_SCALE = 96 ** -0.5
WEXT = 5296 + 32 + 256 + 256


def colof(t):
    return t + 2 if t < CTX else t + 6


class Buf:
    __slots__ = ("name", "lw", "rd", "psum")

    def __init__(self, name=""):
        self.name = name
        self.lw = None
        self.rd = []
        self.psum = False


class Op:
    __slots__ = ("eng", "fn", "deps", "sig", "idx", "dma_slot", "dma_prev")

    def __init__(self, eng, fn):
        self.eng = eng
        self.fn = fn
        self.deps = set()
        self.sig = None
        self.dma_slot = None
        self.dma_prev = None


class Prog:
    def __init__(self, nc):
        self.nc = nc
        self.ops = []
        self.eng = {PE: nc.tensor, ACT: nc.scalar, DVE: nc.vector, POOL: nc.gpsimd, SP: nc.sync}
        self.dma_lists = {}
        self.last = {}

    def op(self, eng, fn, reads=(), writes=(), dma=False):
        o = Op(eng, fn)
        o.idx = len(self.ops)
        for b in reads:
            if b.lw is not None:
                o.deps.add(b.lw)
            if b.psum:
                for r in b.rd:
                    if self.ops[r].eng != eng:
                        o.deps.add(r)
        for b in writes:
            if b.lw is not None:
                o.deps.add(b.lw)
            for r in b.rd:
                o.deps.add(r)
        for b in reads:
            b.rd.append(o.idx)
        for b in writes:
            b.lw = o.idx
            b.rd = []
        if dma:
            lst = self.dma_lists.setdefault(eng, [])
            o.dma_slot = len(lst)
            if len(lst) >= DMA_K:
                o.dma_prev = lst[len(lst) - DMA_K]
            lst.append(o.idx)
        o.deps.discard(o.idx)
        self.ops.append(o)
        self.last[eng] = o.idx
        return o

    def barrier(self):
        bufs = {}
        for e in (PE, ACT, DVE, POOL, SP):
            bufs[e] = Buf("bar" + e)
            o = self.op(e, lambda en: en.nop(), writes=[bufs[e]])
            for lst in self.dma_lists.values():
                for d in lst[-DMA_K:]:
                    if d != o.idx:
                        o.deps.add(d)
        for e in (PE, ACT, DVE, POOL, SP):
            self.op(e, lambda en: en.nop(), reads=list(bufs.values()))

    def emit(self, stack):
        nc = self.nc
        ops = self.ops
        needed = set()
        for o in ops:
            for d in o.deps:
                do = ops[d]
                if do.eng == o.eng and o.eng == PE and do.dma_slot is None:
                    continue
                needed.add(d)
            if o.dma_prev is not None:
                needed.add(o.dma_prev)
        cnt = {}
        sems = {}
        dma_sems = {}
        for o in ops:
            if o.dma_slot is not None:
                k = o.dma_slot % DMA_K
                n = o.dma_slot // DMA_K
                key = (o.eng, k, n // DMA_EPOCH)
                if key not in dma_sems:
                    dma_sems[key] = stack.enter_context(nc.semaphore("dq%s%d_%d" % key))
                o.sig = (dma_sems[key], 16 * (n % DMA_EPOCH + 1))
            elif o.idx in needed:
                c = cnt.get(o.eng, 0)
                key = (o.eng, c // EPOCH)
                if key not in sems:
                    sems[key] = stack.enter_context(nc.semaphore("s%s_%d" % key))
                o.sig = (sems[key], c % EPOCH + 1)
                cnt[o.eng] = c + 1
        waited = {}
        nw = 0
        for o in ops:
            e = self.eng[o.eng]
            deps = set(o.deps)
            if o.dma_prev is not None:
                deps.add(o.dma_prev)
            for d in sorted(deps):
                do = ops[d]
                if do.sig is None:
                    continue
                sem, val = do.sig
                key = (o.eng, id(sem))
                if waited.get(key, 0) >= val:
                    continue
                waited[key] = val
                e.wait_ge(sem, val)
                nw += 1
            ins = o.fn(e)
            if o.sig is not None:
                sem, val = o.sig
                ins.then_inc(sem, 16 if o.dma_slot is not None else 1)
        self.nwaits = nw


class T:
    __slots__ = ("t", "b")

    def __init__(self, t, name=""):
        self.t = t
        self.b = Buf(name)

    def __getitem__(self, k):
        return self.t[k]


class K:
    def __init__(self, debug=False, stop_after=None, skip=()):
        self.skip = set(skip)
        self.debug = debug
        self.stop_after = stop_after
        self.nc = bass.Bass("TRN2", target_bir_lowering=False)
        self.P = Prog(self.nc)
        self.uid = 0

    def dram(self, name, shape, dt, kind="Internal"):
        return T(self.nc.dram_tensor(name, list(shape), dt, kind=kind).ap(), name)

    def sb(self, st, shape, dt, name=None):
        self.uid += 1
        name = "%s_%d" % (name or "t", self.uid)
        return T(st.enter_context(self.nc.sbuf_tensor(name, list(shape), dt)), name)

    def ps(self, st, shape, dt=F32, name=None):
        self.uid += 1
        name = "%s_%d" % (name or "p", self.uid)
        t = T(st.enter_context(self.nc.psum_tensor(name, list(shape), dt)), name)
        t.b.psum = True
        return t

    def dma(self, out, in_, reads, writes, eng=SP, slow=False):
        if slow:
            return self.P.op(eng, lambda e: e.dma_start(out=out, in_=in_, allow_slow_non_contiguous=True),
                             [x.b for x in reads], [x.b for x in writes], dma=True)
        return self.P.op(eng, lambda e: e.dma_start(out=out, in_=in_), [x.b for x in reads], [x.b for x in writes], dma=True)

    def mm(self, out, lhsT, rhs, start, stop, reads, writes):
        return self.P.op(PE, lambda e: e.matmul(out, lhsT=lhsT, rhs=rhs, start=start, stop=stop),
                         [x.b for x in reads], [x.b for x in writes])

    def tr(self, out, in_, ident, reads, writes):
        return self.P.op(PE, lambda e: e.transpose(out=out, in_=in_, identity=ident),
                         [x.b for x in reads], [x.b for x in writes])

    def act(self, out, in_, func, reads, writes, bias=None, scale=None, accum_out=None):
        kw = {}
        if bias is not None:
            kw["bias"] = bias
        if scale is not None:
            kw["scale"] = scale
        if accum_out is not None:
            kw["accum_out"] = accum_out
        return self.P.op(ACT, lambda e: e.activation(out=out, in_=in_, func=func, **kw),
                         [x.b for x in reads], [x.b for x in writes])

    def tt(self, out, in0, in1, op, reads, writes, eng=DVE):
        return self.P.op(eng, lambda e: e.tensor_tensor(out=out, in0=in0, in1=in1, op=op),
                         [x.b for x in reads], [x.b for x in writes])

    def ts(self, out, in0, s1, s2, op0, op1, reads, writes, eng=DVE):
        if op1 is None:
            return self.P.op(eng, lambda e: e.tensor_scalar(out=out, in0=in0, scalar1=s1, scalar2=None, op0=op0),
                             [x.b for x in reads], [x.b for x in writes])
        return self.P.op(eng, lambda e: e.tensor_scalar(out=out, in0=in0, scalar1=s1, scalar2=s2, op0=op0, op1=op1),
                         [x.b for x in reads], [x.b for x in writes])

    def stt(self, out, in0, scalar, in1, op0, op1, reads, writes, eng=DVE):
        return self.P.op(eng, lambda e: e.scalar_tensor_tensor(out=out, in0=in0, scalar=scalar, in1=in1, op0=op0, op1=op1),
                         [x.b for x in reads], [x.b for x in writes])

    def cp(self, out, in_, reads, writes, eng=DVE):
        return self.P.op(eng, lambda e: e.tensor_copy(out=out, in_=in_), [x.b for x in reads], [x.b for x in writes])

    def memset(self, out, val, writes, eng=POOL):
        return self.P.op(eng, lambda e: e.memset(out, val), [], [x.b for x in writes])

    def sigm(self, ap, t):
        self.ts(ap, ap, 1.0, None, ALU.add, None, [t], [t])
        self.P.op(DVE, lambda e: e.reciprocal(out=ap, in_=ap), [t.b], [t.b])

    def rsqrt(self, ap, t):
        self.act(ap, ap, AF.Ln, [t], [t])
        self.act(ap, ap, AF.Exp, [t], [t], scale=-0.5)

    def build(self):
        nc = self.nc
        dbg = self.debug
        I = {}

        def inp(name, shape):
            I[name] = self.dram(name, shape, F32, kind="ExternalInput")
        inp("x", [SEQ, D]); inp("ctx", [CTX, D]); inp("cvec", [2, D])
        inp("w_ada", [2, D, 3 * D]); inp("b_ada", [2, 3 * D]); inp("w_in", [2, D, WEXT])
        inp("ssd_conv_w", [2, 5, 1536]); inp("ssd_conv_b", [2, 1536])
        for n in ("ssd_a_log_f", "ssd_a_log_b", "ssd_dt_bias_f", "ssd_dt_bias_b", "ssd_d"):
            inp(n, [2, 16])
        inp("ssd_norm_w", [2, 1024]); inp("mla_q_norm", [2, 384]); inp("mla_w_uq", [2, 384, 768])
        inp("w_uq_sw", [2, 384, 256]); inp("mla_kv_norm", [2, 256]); inp("mla_w_ukv", [2, 256, 1024])
        inp("ret_log_rate_f", [2, 4]); inp("ret_log_rate_b", [2, 4]); inp("w_out", [2, 2048, D])
        inp("ln_g", [2, D]); inp("ln_b", [2, D])
        inp("cst", [128, CST_W]); inp("mla_tab", [32, 2, NTOK]); inp("ret_tab", [128, 4, NTOK])
        self.I = I
        okind = "ExternalOutput" if dbg else "Internal"
        self.out = self.dram("out", [SEQ, D], F32, kind="ExternalOutput")
        self.hT_d = self.dram("hT_d", [128, 8 * HC], BF16, kind=okind)
        self.ypart_d = self.dram("ypart_d", [NTOK, 1024], F32)
        self.rpart_d = self.dram("rpart_d", [NTOK, 512], F32)
        self.ypartb_d = self.dram("ypartb_d", [NTOK, 1024], F32)
        self.rpartb_d = self.dram("rpartb_d", [NTOK, 512], F32)
        self.mixT_d = self.dram("mixT_d", [2048, NTOK], BF16, kind=okind)
        self.qT_d = self.dram("qT_d", [8, 96, NTOK], BF16)
        self.kfT_d = self.dram("kfT_d", [8, 96, NTOK], BF16)
        self.sgT_d = self.dram("sgT_d", [512, NTOK], F32)
        self.xres_d = self.dram("xres_d", [NTOK, D], F32, kind=okind)

        with ExitStack() as gst:
            self.gst = gst
            self.cst = self.sb(gst, [128, CST_W], F32, "cst")
            self.dma(self.cst[:], I["cst"][:], [I["cst"]], [self.cst])
            self.identb = self.sb(gst, [128, 128], BF16, "identb")
            self.cp(self.identb[:], self.cst[:, C_ID:C_ID + 128], [self.cst], [self.identb])
            self.ones = self.sb(gst, [128, 128], F32, "ones")
            self.memset(self.ones[:], 1.0, [self.ones])
            self.gB = self.sb(gst, [128, 2, 1024], F32, "gB")
            zt = self.sb(gst, [128, 8, 4], BF16, "zt")
            self.memset(zt[:], 0.0, [zt])
            hv = self.hT_d[:].rearrange("p (k c) -> p k c", k=8)
            self.hv = hv
            for (a, b) in ((0, 2), (258, 262), (4358, 4360)):
                self.dma(hv[:, :, a:b], zt[:, :, 0:b - a], [zt], [self.hT_d], slow=True)
            self.P.barrier()
            stages = []
            for l in range(DEPTH):
                stages += [("A", l), ("S", l), ("M", l), ("R", l), ("E", l)]
            for (s, l) in stages:
                if s in self.skip:
                    continue
                if s == "A":
                    self.stage_A(l)
                elif s == "S":
                    self.stage_S(l)
                elif s == "M":
                    self.stage_M(l)
                elif s == "R":
                    self.stage_R(l)
                else:
                    self.stage_E(l)
                self.P.barrier()
                if self.stop_after == (s, l):
                    break
            self.P.barrier()
            self.P.emit(gst)
        return nc

    def silu_psum(self, st, src_ap, src_t, out_ap, out_t, e_t, e_ap, r_ap):
        self.act(e_ap, src_ap, AF.Exp, [src_t], [e_t], scale=-1.0)
        self.sigm(e_ap, e_t)
        self.tt(out_ap, src_ap, r_ap, ALU.mult, [src_t, e_t], [out_t])

    def stage_A(self, l):
        I = self.I
        with ExitStack() as st:
            wada = [self.sb(st, [128, 8, 512], F32, "wada") for _ in range(2)]
            craw = self.sb(st, [128, 8, 2], F32, "craw")
            ce = self.sb(st, [128, 8, 2], F32, "ce")
            scT = self.sb(st, [128, 8, 2], F32, "scT")
            modT = self.sb(st, [128, 24, 2], F32, "modT")
            scale1 = self.sb(st, [128, 8, 2], F32, "scale1")
            brow = self.sb(st, [1, 3 * D], F32, "brow")
            pm = self.ps(st, [128, 512], F32, "pm")
            pg = [self.ps(st, [128, 512], F32, "pg") for _ in range(2)]
            for j in range(2):
                self.dma(craw[:, :, j], I["cvec"][j].rearrange("(k p) -> p k", p=128), [I["cvec"]], [craw], slow=True)
            self.dma(brow[:], I["b_ada"][l:l + 1, :], [I["b_ada"]], [brow])
            self.act(ce[:], craw[:], AF.Exp, [craw], [ce], scale=-1.0)
            self.sigm(ce[:], ce)
            self.tt(scT[:], craw[:], ce[:], ALU.mult, [craw, ce], [scT])
            wv = I["w_ada"][l].rearrange("(k p) c -> p k c", p=128)
            for cb in range(6):
                w = wada[cb % 2]
                self.dma(w[:], wv[:, :, cb * 512:(cb + 1) * 512], [I["w_ada"]], [w])
                if cb < 4:
                    for dj in range(4):
                        j = cb * 4 + dj
                        for kc in range(8):
                            self.mm(pm[:, 2 * dj:2 * dj + 2], w[:, kc, dj * 128:(dj + 1) * 128], scT[:, kc, :],
                                    kc == 0, False, [w, scT], [pm])
                        self.mm(pm[:, 2 * dj:2 * dj + 2], brow[0:1, j * 128:(j + 1) * 128], self.ones[0:1, 0:2],
                                False, True, [brow, self.ones], [pm])
                        self.cp(modT[:, j, :], pm[:, 2 * dj:2 * dj + 2], [pm], [modT])
                else:
                    for typ in range(2):
                        p = pg[typ]
                        for kc in range(8):
                            self.mm(p[:], scT[:, kc, typ:typ + 1].to_broadcast([128, 128]), w[:, kc, :],
                                    kc == 0, False, [w, scT], [p])
                        self.mm(p[:], self.ones[0:1, 0:128], brow[0:1, cb * 512:(cb + 1) * 512], False, True,
                                [brow, self.ones], [p])
                        self.cp(self.gB[:, typ, (cb - 4) * 512:(cb - 3) * 512], p[:], [p], [self.gB])
            self.ts(scale1[:], modT[:, 8:16, :], 1.0, None, ALU.add, None, [modT], [scale1])
            xt = [self.sb(st, [128, D], F32, "xt") for _ in range(2)]
            ht = [self.sb(st, [128, 8, 128], BF16, "ht") for _ in range(2)]
            pT = [self.ps(st, [128, 1024], F32, "pT") for _ in range(2)]
            for t in range(NCH):
                typ = 1 if t < 2 else 0
                if l == 0:
                    src_t = I["ctx"] if t < 2 else I["x"]
                    src = src_t[t * 128:(t + 1) * 128, :] if t < 2 else src_t[(t - 2) * 128:(t - 1) * 128, :]
                else:
                    src_t = self.xres_d
                    src = src_t[t * 128:(t + 1) * 128, :]
                x_ = xt[t % 2]; h_ = ht[t % 2]; p_ = pT[t % 2]
                self.dma(x_[:], src, [src_t], [x_])
                for kc in range(8):
                    self.tr(p_[:, kc * 128:(kc + 1) * 128], x_[:, kc * 128:(kc + 1) * 128], self.cst[:, C_ID:C_ID + 128],
                            [x_, self.cst], [p_])
                for kc in range(8):
                    self.act(h_[:, kc, :], p_[:, kc * 128:(kc + 1) * 128], AF.Identity, [p_, scale1, modT], [h_],
                             bias=modT[:, kc, typ:typ + 1], scale=scale1[:, kc, typ:typ + 1])
                c0 = colof(t * 128)
                self.dma(self.hv[:, :, c0:c0 + 128], h_[:], [h_], [self.hT_d])

    def load_w(self, dst_ap, dst_t, src_ap, src_t):
        self.dma(dst_ap, src_ap, [src_t], [dst_t], eng=POOL, slow=True)

    def bvec(self, st, name, l, n):
        t = self.sb(st, [128, n], F32, name)
        self.dma(t[:], self.I[name][l:l + 1, :].to_broadcast([128, n]), [self.I[name]], [t], slow=True)
        return t

    def interleave(self, factories, width):
        pending = list(factories)
        active = []
        for s in range(width):
            if pending:
                active.append((s, pending.pop(0)(s)))
        while active:
            nxt = []
            for (s, g) in active:
                try:
                    next(g)
                    nxt.append((s, g))
                except StopIteration:
                    if pending:
                        nxt.append((s, pending.pop(0)(s)))
            active = nxt

    def stage_S(self, l):
        I = self.I
        cst = self.cst
        with ExitStack() as st:
            wv = I["w_in"][l].rearrange("(k p) c -> p k c", p=128)
            wx = self.sb(st, [128, 8, 1536], BF16, "wx")
            wdt = self.sb(st, [128, 8, 16], BF16, "wdt")
            for kc in range(8):
                self.load_w(wx[:, kc, :], wx, wv[:, kc, 1024:2560], I["w_in"])
            self.load_w(wdt[:], wdt, wv[:, :, 2560:2576], I["w_in"])
            convw = self.sb(st, [128, 12, 5], F32, "convw")
            for k in range(5):
                self.dma(convw[:, :, k], I["ssd_conv_w"][l, k].rearrange("(r p) -> p r", p=128), [I["ssd_conv_w"]], [convw], slow=True)
            dg = self.sb(st, [128, 60, 128], BF16, "dg")
            for r in range(12):
                for k in range(5):
                    self.ts(dg[:, r * 5 + k, :], self.identb[:], convw[:, r, k:k + 1], None, ALU.mult, None,
                            [self.identb, convw], [dg])
            cbrow = self.sb(st, [1, 1536], F32, "cbrow")
            self.dma(cbrow[:], I["ssd_conv_b"][l:l + 1, :], [I["ssd_conv_b"]], [cbrow])
            Dsk = self.bvec(st, "ssd_d", l, 16)
            prm = {}
            for d_, sfx in ((0, "f"), (1, "b")):
                al = self.bvec(st, "ssd_a_log_" + sfx, l, 16)
                self.act(al[:], al[:], AF.Exp, [al], [al])
                self.ts(al[:], al[:], -1.0, None, ALU.mult, None, [al], [al])
                dtb = self.bvec(st, "ssd_dt_bias_" + sfx, l, 16)
                prm[d_] = (al, dtb)
            with ExitStack() as st2:
                gens = []
                import os as _os
                for d_ in [int(ch) for ch in _os.environ.get("S_DIRS", "01")]:
                    gens.append((lambda dd: (lambda slot: self.ssd_sweep(l, dd, st2, wx, wdt, dg, cbrow, Dsk, prm[dd])))(d_))
                self.interleave(gens, int(_os.environ.get("S_WIDTH", "2")))
            self.P.barrier()
            if _os.environ.get("S_NOP3"):
                return
            with ExitStack() as st3:
                wz = self.sb(st3, [128, 8, 1024], BF16, "wz")
                for kc in range(8):
                    self.load_w(wz[:, kc, :], wz, wv[:, kc, 0:1024], I["w_in"])
                nwB = self.bvec(st3, "ssd_norm_w", l, 1024)
                sets = []
                for s in range(2):
                    B = {}
                    B["hc"] = self.sb(st3, [128, 8, 128], BF16, "hc3")
                    B["ypf"] = self.sb(st3, [128, 1024], F32, "ypf")
                    B["ypb"] = self.sb(st3, [128, 1024], F32, "ypb")
                    B["e"] = self.sb(st3, [128, 1024], F32, "e3")
                    B["t1"] = self.sb(st3, [128, 1024], F32, "t13")
                    B["bst"] = self.sb(st3, [128, 2, 6], F32, "bst3")
                    B["ssq"] = self.sb(st3, [128, 2], F32, "ssq3")
                    B["ob"] = self.sb(st3, [128, 1024], BF16, "ob3")
                    B["oT"] = self.sb(st3, [128, 8, 128], BF16, "oT3")
                    B["PZ"] = [self.ps(st3, [128, 512], F32, "PZ3") for _ in range(2)]
                    B["PT"] = self.ps(st3, [128, 1024], BF16, "PT3")
                    sets.append(B)
                gens = [(lambda cc: (lambda slot: self.ssd_final(cc, sets[slot], wz, nwB)))(c)
                        for c in range(int(_os.environ.get("S_P3_N", NCH)))]
                self.interleave(gens, int(_os.environ.get("S_P3_W", "2")))

    def ssd_final(self, c, B, wz, nwB):
        h_ = B["hc"]; ypf = B["ypf"]; ypb = B["ypb"]; e = B["e"]; t1 = B["t1"]; bst = B["bst"]; ssq = B["ssq"]
        ob = B["ob"]; oT_ = B["oT"]; PZ = B["PZ"]; PT = B["PT"]
        mixv = self.mixT_d[:].rearrange("(r p) t -> p r t", p=128)
        c0 = colof(c * 128)
        tok = slice(c * 128, (c + 1) * 128)
        self.dma(h_[:], self.hv[:, :, c0:c0 + 128], [self.hT_d], [h_], slow=True)
        self.dma(ypf[:], self.ypart_d[tok, :], [self.ypart_d], [ypf])
        self.dma(ypb[:], self.ypartb_d[tok, :], [self.ypartb_d], [ypb])
        yield
        for n in range(2):
            for kc in range(8):
                self.mm(PZ[n][:], h_[:, kc, :], wz[:, kc, n * 512:(n + 1) * 512], kc == 0, kc == 7, [h_, wz], [PZ[n]])
        self.tt(ypf[:], ypf[:], ypb[:], ALU.add, [ypf, ypb], [ypf])
        yield
        for n in range(2):
            self.act(e[:, n * 512:(n + 1) * 512], PZ[n][:], AF.Exp, [PZ[n]], [e], scale=-1.0)
        yield
        self.sigm(e[:], e)
        yield
        for n in range(2):
            self.tt(t1[:, n * 512:(n + 1) * 512], PZ[n][:], e[:, n * 512:(n + 1) * 512], ALU.mult, [PZ[n], e], [t1])
        yield
        self.tt(t1[:], t1[:], ypf[:], ALU.mult, [t1, ypf], [t1])
        yield
        for s_ in range(2):
            self.P.op(DVE, (lambda ss: (lambda en: en.bn_stats(out=bst[:, ss, :], in_=t1[:, ss * 512:(ss + 1) * 512])))(s_),
                      [t1.b], [bst.b])
        self.P.op(DVE, lambda en: en.bn_aggr(out=ssq[:], in_=bst[:]), [bst.b], [ssq.b])
        self.stt(ssq[:, 1:2], ssq[:, 0:1], ssq[:, 0:1], ssq[:, 1:2], ALU.mult, ALU.add, [ssq], [ssq])
        self.ts(ssq[:, 1:2], ssq[:, 1:2], RMS_EPS, None, ALU.add, None, [ssq], [ssq])
        yield
        self.rsqrt(ssq[:, 1:2], ssq)
        yield
        self.stt(ob[:], t1[:], ssq[:, 1:2], nwB[:], ALU.mult, ALU.mult, [t1, ssq, nwB], [ob])
        yield
        for r in range(8):
            self.tr(PT[:, r * 128:(r + 1) * 128], ob[:, r * 128:(r + 1) * 128], self.identb[:], [ob, self.identb], [PT])
        yield
        self.act(oT_[:], PT[:].rearrange("p (a b) -> p a b", a=8), AF.Copy, [PT], [oT_])
        yield
        self.dma(mixv[:, 0:8, tok], oT_[:], [oT_], [self.mixT_d], slow=True)
        yield

    def ssd_sweep(self, l, d_, st, wx, wdt, dg, cbrow, Dsk, prm):
        cst = self.cst
        al, dtb = prm
        H = self.sb(st, [128, 1024], F32, "H")
        Hbf = self.sb(st, [128, 1024], BF16, "Hbf")
        hc = [self.sb(st, [128, 8, 132], BF16, "hc") for _ in range(2)]
        xbc = self.sb(st, [128, 12, 132], BF16, "xbc")
        esb = self.sb(st, [128, 2048], F32, "esb")
        ubf = self.sb(st, [128, 12, 128], BF16, "ubf")
        dtx = self.sb(st, [128, 16], F32, "dtx")
        dt = self.sb(st, [128, 16], F32, "dt")
        la = self.sb(st, [128, 16], F32, "la")
        E3 = self.sb(st, [128, 48], F32, "E3")
        xs_tok = self.sb(st, [128, 1024], BF16, "xs_tok")
        B_tok = self.sb(st, [128, 256], BF16, "B_tok")
        scm = self.sb(st, [128, 2, 128], F32, "scm")
        LaL = self.sb(st, [128, 16, 128], F32, "LaL")
        M = self.sb(st, [128, 16, 128], BF16, "M")
        v = self.sb(st, [128, 1024], BF16, "v")
        vte = self.sb(st, [128, 1024], BF16, "vte")
        t1 = self.sb(st, [128, 1024], F32, "t1")
        t2 = self.sb(st, [128, 1024], F32, "t2")
        yp = [self.sb(st, [128, 1024], F32, "yp") for _ in range(2)]
        Q = [self.ps(st, [128, 512], F32, "Q") for _ in range(4)]
        Qb0 = Q[0][:].bitcast(BF16)
        Qb1 = Q[1][:].bitcast(BF16)
        ydst = self.ypart_d if d_ == 0 else self.ypartb_d
        order = list(range(NCH)) if d_ == 0 else [1, 0] + list(range(NCH - 1, 1, -1))
        Uo = C_UF if d_ == 0 else C_UB
        Lo = C_LF if d_ == 0 else C_LB
        Mo = C_MF if d_ == 0 else C_MB
        self.memset(H[:], 0.0, [H])
        self.memset(Hbf[:], 0.0, [Hbf])
        it = 0
        c0 = colof(order[0] * 128)
        self.dma(hc[0][:], self.hv[:, :, c0 - 2:c0 + 130], [self.hT_d], [hc[0]], slow=True)
        for ci, c in enumerate(order):
            h_ = hc[it % 2]
            ypt = yp[it % 2]
            it += 1
            tok = slice(c * 128, (c + 1) * 128)
            if ci + 1 < len(order):
                cn = colof(order[ci + 1] * 128)
                self.dma(hc[it % 2][:], self.hv[:, :, cn - 2:cn + 130], [self.hT_d], [hc[it % 2]], slow=True)
            for r in range(12):
                q_ = Q[r // 3]
                o_ = q_[:, (r % 3) * 132:(r % 3) * 132 + 132]
                for kc in range(8):
                    self.mm(o_, wx[:, kc, r * 128:(r + 1) * 128], h_[:, kc, :], kc == 0, kc == 7, [wx, h_], [q_])
                if r % 3 == 2:
                    yield
            for q in range(4):
                self.act(xbc[:, 3 * q:3 * q + 3, :], Q[q][:, 0:396].rearrange("p (a b) -> p a b", a=3), AF.Copy, [Q[q]], [xbc])
            yield
            for kc in range(8):
                self.mm(Q[3][:, 0:16], h_[:, kc, 2:130], wdt[:, kc, :], kc == 0, kc == 7, [h_, wdt], [Q[3]])
            yield
            for r in range(12):
                q_ = Q[r // 4]
                o_ = q_[:, (r % 4) * 128:(r % 4 + 1) * 128]
                for k in range(5):
                    self.mm(o_, dg[:, r * 5 + k, :], xbc[:, r, k:k + 128], k == 0, False, [dg, xbc], [q_])
                self.mm(o_, cbrow[0:1, r * 128:(r + 1) * 128], self.ones[0:1, 0:128], False, True, [cbrow, self.ones], [q_])
                if r % 4 == 3:
                    yield
            self.tt(dtx[:], Q[3][:, 0:16], dtb[:], ALU.add, [Q[3], dtb], [dtx])
            yield
            self.act(dtx[:], dtx[:], AF.Exp, [dtx], [dtx])
            for q in range(3):
                self.act(esb[:, q * 512:(q + 1) * 512], Q[q][:], AF.Exp, [Q[q]], [esb], scale=-1.0)
            yield
            self.act(dt[:], dtx[:], AF.Ln, [dtx], [dt], bias=1.0)
            self.sigm(esb[:, 0:1536], esb)
            yield
            self.tt(la[:], dt[:], al[:], ALU.mult, [dt, al], [la])
            for q in range(3):
                self.tt(ubf[:, 4 * q:4 * q + 4, :], Q[q][:].rearrange("p (a b) -> p a b", a=4),
                        esb[:, q * 512:(q + 1) * 512].rearrange("p (a b) -> p a b", a=4), ALU.mult, [Q[q], esb], [ubf])
            yield
            self.mm(Q[3][:, 16:32], cst[:, Uo:Uo + 128], la[:], True, True, [cst, la], [Q[3]])
            self.mm(Q[3][:, 32:48], cst[:, Lo:Lo + 128], la[:], True, True, [cst, la], [Q[3]])
            self.mm(Q[3][:, 48:64], self.ones[:], la[:], True, True, [self.ones, la], [Q[3]])
            for g in range(2):
                self.mm(Q[3][:, 256 + g * 128:256 + (g + 1) * 128], ubf[:, 8 + g, :], ubf[:, 10 + g, :], True, True, [ubf], [Q[3]])
            self.tt(LaL[:], la[:].unsqueeze(2).to_broadcast([128, 16, 128]),
                    cst[:, Lo:Lo + 128].unsqueeze(1).to_broadcast([128, 16, 128]), ALU.mult, [la, cst], [LaL], eng=POOL)
            yield
            for r in range(8):
                self.tr(Qb0[:, r * 128:(r + 1) * 128], ubf[:, r, :], self.identb[:], [ubf, self.identb], [Q[0]])
            for r in range(2):
                self.tr(Qb1[:, r * 128:(r + 1) * 128], ubf[:, 8 + r, :], self.identb[:], [ubf, self.identb], [Q[1]])
            self.act(E3[:], Q[3][:, 16:64], AF.Exp, [Q[3]], [E3])
            yield
            self.tt(scm[:], Q[3][:, 256:512].rearrange("p (a b) -> p a b", a=2),
                    cst[:, Mo:Mo + 128].unsqueeze(1).to_broadcast([128, 2, 128]), ALU.mult, [Q[3], cst], [scm])
            self.act(xs_tok[:], Qb0[:, 0:1024], AF.Copy, [Q[0]], [xs_tok])
            self.cp(B_tok[:], Qb1[:, 0:256], [Q[1]], [B_tok])
            yield
            for h in range(16):
                q_ = Q[h // 4]
                self.mm(q_[:, (h % 4) * 128:(h % 4 + 1) * 128], LaL[:, h, :], cst[:, Uo:Uo + 128], True, True, [LaL, cst], [q_])
                if h % 4 == 3:
                    yield
            self.tt(v[:].rearrange("p (h e) -> p h e", h=16), xs_tok[:].rearrange("p (h e) -> p h e", h=16),
                    dt[:].unsqueeze(2).to_broadcast([128, 16, 64]), ALU.mult, [xs_tok, dt], [v], eng=POOL)
            for q in range(4):
                self.act(esb[:, q * 512:(q + 1) * 512], Q[q][:], AF.Exp, [Q[q]], [esb])
            yield
            self.tt(vte[:].rearrange("p (h e) -> p h e", h=16), v[:].rearrange("p (h e) -> p h e", h=16),
                    E3[:, 16:32].unsqueeze(2).to_broadcast([128, 16, 64]), ALU.mult, [v, E3], [vte], eng=POOL)
            self.tt(M[:].rearrange("p (g a) b -> p g a b", g=2), esb[:].rearrange("p (g a b) -> p g a b", g=2, a=8),
                    scm[:].unsqueeze(2).to_broadcast([128, 2, 8, 128]), ALU.mult, [esb, scm], [M])
            yield
            for g in range(2):
                self.mm(Q[2 + g][:], ubf[:, 10 + g, :], Hbf[:, g * 512:(g + 1) * 512], True, True, [ubf, Hbf], [Q[2 + g]])
            for h in range(16):
                q_ = Q[h // 8]
                self.mm(q_[:, (h % 8) * 64:(h % 8 + 1) * 64], M[:, h, :], v[:, h * 64:(h + 1) * 64], True, True, [M, v], [q_])
                if h % 8 == 7:
                    yield
            for g in range(2):
                self.tt(t1[:, g * 512:(g + 1) * 512].rearrange("p (h e) -> p h e", h=8),
                        Q[2 + g][:].rearrange("p (h e) -> p h e", h=8),
                        E3[:, g * 8:(g + 1) * 8].unsqueeze(2).to_broadcast([128, 8, 64]), ALU.mult, [Q[2 + g], E3], [t1])
            yield
            for g in range(2):
                self.tt(t2[:, g * 512:(g + 1) * 512], Q[g][:], t1[:, g * 512:(g + 1) * 512], ALU.add, [Q[g], t1], [t2])
            yield
            for g in range(2):
                self.mm(Q[g][:], B_tok[:, g * 128:(g + 1) * 128], vte[:, g * 512:(g + 1) * 512], True, True, [B_tok, vte], [Q[g]])
            if d_ == 0:
                self.tt(t1[:].rearrange("p (h e) -> p h e", h=16), xs_tok[:].rearrange("p (h e) -> p h e", h=16),
                        Dsk[:].unsqueeze(2).to_broadcast([128, 16, 64]), ALU.mult, [xs_tok, Dsk], [t1], eng=POOL)
                self.tt(ypt[:], t1[:], t2[:], ALU.add, [t1, t2], [ypt], eng=POOL)
                self.dma(ydst[tok, :], ypt[:], [ypt], [ydst])
            else:
                self.cp(ypt[:], t2[:], [t2], [ypt], eng=POOL)
                self.dma(ydst[tok, :], ypt[:], [ypt], [ydst])
            self.tt(H[:].rearrange("p (h e) -> p h e", h=16), H[:].rearrange("p (h e) -> p h e", h=16),
                    E3[:, 32:48].unsqueeze(2).to_broadcast([128, 16, 64]), ALU.mult, [H, E3], [H])
            yield
            for g in range(2):
                self.tt(H[:, g * 512:(g + 1) * 512], H[:, g * 512:(g + 1) * 512], Q[g][:], ALU.add, [H, Q[g]], [H])
            yield
            self.act(Hbf[:], H[:], AF.Copy, [H], [Hbf])
            yield

    def stage_R(self, l):
        I = self.I
        cst = self.cst
        with ExitStack() as st:
            wv = I["w_in"][l].rearrange("(k p) c -> p k c", p=128)
            wqk = self.sb(st, [128, 8, 1024], BF16, "wqk")
            wvg = self.sb(st, [128, 8, 1024], BF16, "wvg")
            for kc in range(8):
                self.load_w(wqk[:, kc, 0:512], wqk, wv[:, kc, 3760:4272], I["w_in"])
                self.load_w(wqk[:, kc, 512:1024], wqk, wv[:, kc, 5328:5840], I["w_in"])
                self.load_w(wvg[:, kc, :], wvg, wv[:, kc, 4272:5296], I["w_in"])
            prm = {}
            for d_, sfx in ((0, "f"), (1, "b")):
                nm = "ret_log_rate_" + sfx
                lgB = self.bvec(st, nm, l, 4)
                self.act(lgB[:], lgB[:], AF.Exp, [lgB], [lgB])
                self.ts(lgB[:], lgB[:], -1.0, None, ALU.mult, None, [lgB], [lgB])
                lgs = self.sb(st, [128, 2], F32, "lgs")
                src = I[nm][l:l + 1, :].rearrange("o (p two) -> o p two", two=2)
                self.dma(lgs[0:64, :], src[:, :, 0].to_broadcast([64, 2]), [I[nm]], [lgs], slow=True)
                self.dma(lgs[64:128, :], src[:, :, 1].to_broadcast([64, 2]), [I[nm]], [lgs], slow=True)
                self.act(lgs[:], lgs[:], AF.Exp, [lgs], [lgs])
                self.ts(lgs[:], lgs[:], -1.0, None, ALU.mult, None, [lgs], [lgs])
                RIo = C_RIF if d_ == 0 else C_RIB
                Mo = C_MF if d_ == 0 else C_MB
                Go = C_GF if d_ == 0 else C_GB
                To = C_TEF if d_ == 0 else C_TEB
                DmT = self.sb(st, [128, 4, 128], F32, "DmT")
                for h in range(4):
                    self.act(DmT[:, h, :], cst[:, RIo:RIo + 128], AF.Exp, [cst, lgB], [DmT], scale=lgB[:, h:h + 1])
                self.tt(DmT[:], DmT[:], cst[:, Mo:Mo + 128].unsqueeze(1).to_broadcast([128, 4, 128]), ALU.mult, [DmT, cst], [DmT])
                Gam = self.sb(st, [128, 2, 128], F32, "Gam")
                for p in range(2):
                    self.act(Gam[:, p, :], cst[:, Go:Go + 128], AF.Exp, [cst, lgs], [Gam], scale=lgs[:, p:p + 1])
                te = self.sb(st, [128, 4], F32, "te")
                self.act(te[:], lgB[:], AF.Exp, [lgB, cst], [te], scale=cst[:, To:To + 1])
                g128 = self.sb(st, [128, 2], F32, "g128")
                self.act(g128[:], lgs[:], AF.Exp, [lgs], [g128], scale=128.0)
                prm[d_] = (DmT, Gam, te, g128)
            S = self.sb(st, [128, 2, 128], F32, "S")
            Sbf = self.sb(st, [128, 2, 128], BF16, "Sbf")
            hc = [self.sb(st, [128, 8, 128], BF16, "hc") for _ in range(2)]
            tab = [self.sb(st, [128, 4, 128], F32, "tab") for _ in range(2)]
            r1 = self.sb(st, [128, 4, 128], F32, "r1")
            r2 = self.sb(st, [128, 4, 128], F32, "r2")
            qk = self.sb(st, [128, 4, 128], BF16, "qk")
            qz = [self.sb(st, [128, 2, 128], BF16, "qz") for _ in range(2)]
            qdz = [self.sb(st, [128, 2, 128], BF16, "qdz") for _ in range(2)]
            for par in range(2):
                self.memset(qz[par][:], 0.0, [qz[par]])
                self.memset(qdz[par][:], 0.0, [qdz[par]])
            k_tok = self.sb(st, [128, 256], BF16, "k_tok")
            vb = self.sb(st, [128, 512], BF16, "vb")
            vte = self.sb(st, [128, 512], BF16, "vte")
            ge = self.sb(st, [128, 512], F32, "ge")
            sg = self.sb(st, [128, 512], F32, "sg")
            Mr = self.sb(st, [128, 4, 128], BF16, "Mr")
            yp = [self.sb(st, [128, 512], F32, "yp") for _ in range(2)]
            yt = self.sb(st, [128, 512], F32, "yt")
            stats = self.sb(st, [128, 4, 6], F32, "stats")
            mv = self.sb(st, [128, 4, 2], F32, "mv")
            ob = self.sb(st, [128, 512], BF16, "ob")
            oT = [self.sb(st, [128, 4, 128], BF16, "oT") for _ in range(2)]
            PQ = self.ps(st, [128, 1024], F32, "PQ")
            PV = self.ps(st, [128, 1024], F32, "PV")
            PS_ = self.ps(st, [128, 512], F32, "PS")
            PY = self.ps(st, [128, 512], F32, "PY")
            PST = self.ps(st, [128, 512], F32, "PST")
            PT = self.ps(st, [128, 1024], BF16, "PT")
            mixv = self.mixT_d[:].rearrange("(r p) t -> p r t", p=128)
            it = 0
            for d_ in (0, 1):
                DmT, Gam, te, g128 = prm[d_]
                order = list(range(NCH)) if d_ == 0 else [1, 0] + list(range(NCH - 1, 1, -1))
                self.memset(S[:], 0.0, [S])
                self.memset(Sbf[:], 0.0, [Sbf])
                for c in order:
                    h_ = hc[it % 2]; tb = tab[it % 2]; ypt = yp[it % 2]; oT_ = oT[it % 2]
                    it += 1
                    c0 = colof(c * 128)
                    tok = slice(c * 128, (c + 1) * 128)
                    self.dma(h_[:], self.hv[:, :, c0:c0 + 128], [self.hT_d], [h_], slow=True)
                    self.dma(tb[:], I["ret_tab"][:, :, tok], [I["ret_tab"]], [tb], slow=True)
                    if d_ == 1:
                        self.dma(ypt[:], self.rpart_d[tok, :], [self.rpart_d], [ypt])
                    for rc in range(8):
                        for kc in range(8):
                            self.mm(PQ[:, rc * 128:(rc + 1) * 128], wqk[:, kc, rc * 128:(rc + 1) * 128], h_[:, kc, :],
                                    kc == 0, kc == 7, [wqk, h_], [PQ])
                    PQv = PQ[:].rearrange("p (a b) -> p a b", a=8)
                    for half, (ci, si) in enumerate(((0, 1), (2, 3))):
                        self.tt(r1[:, 2 * half:2 * half + 2, :], PQv[:, 2 * half:2 * half + 2, :],
                                tb[:, ci, :].unsqueeze(1).to_broadcast([128, 2, 128]), ALU.mult, [PQ, tb], [r1])
                        self.tt(r2[:, 2 * half:2 * half + 2, :], PQv[:, 4 + 2 * half:6 + 2 * half, :],
                                tb[:, si, :].unsqueeze(1).to_broadcast([128, 2, 128]), ALU.mult, [PQ, tb], [r2])
                    self.tt(qk[:], r1[:], r2[:], ALU.add, [r1, r2], [qk], eng=POOL)
                    for par in range(2):
                        rr = 64 * par
                        self.cp(qz[par][rr:rr + 64, :, :], qk[rr:rr + 64, 0:2, :], [qk], [qz[par]], eng=POOL)
                        self.tt(qdz[par][rr:rr + 64, :, :], qk[rr:rr + 64, 0:2, :], Gam[rr:rr + 64, :, :], ALU.mult,
                                [qk, Gam], [qdz[par]], eng=POOL)
                    for p in range(2):
                        self.tr(PT[:, p * 128:(p + 1) * 128], qk[:, 2 + p, :], self.identb[:], [qk, self.identb], [PT])
                    self.cp(k_tok[:], PT[:, 0:256], [PT], [k_tok])
                    for n in range(2):
                        for kc in range(8):
                            self.mm(PV[:, n * 512:(n + 1) * 512], h_[:, kc, :], wvg[:, kc, n * 512:(n + 1) * 512],
                                    kc == 0, kc == 7, [h_, wvg], [PV])
                    self.act(vb[:], PV[:, 0:512], AF.Copy, [PV], [vb])
                    self.tt(vte[:].rearrange("p (h e) -> p h e", h=4), PV[:, 0:512].rearrange("p (h e) -> p h e", h=4),
                            te[:].unsqueeze(2).to_broadcast([128, 4, 128]), ALU.mult, [PV, te], [vte])
                    if d_ == 1:
                        self.act(ge[:], PV[:, 512:1024], AF.Exp, [PV], [ge], scale=-1.0)
                        self.sigm(ge[:], ge)
                        self.tt(sg[:], PV[:, 512:1024], ge[:], ALU.mult, [PV, ge], [sg])
                    for h in range(4):
                        p, r0 = h // 2, (h % 2) * 64
                        self.mm(PS_[:, h * 128:(h + 1) * 128], qk[:, 2 + p, :], qz[h % 2][:, p, :], True, True, [qk, qz[h % 2]], [PS_])
                    self.tt(Mr[:], PS_[:].rearrange("p (a b) -> p a b", a=4), DmT[:], ALU.mult, [PS_, DmT], [Mr])
                    for h in range(4):
                        p, r0 = h // 2, (h % 2) * 64
                        self.mm(PY[:, h * 128:(h + 1) * 128], Mr[:, h, :], vb[:, h * 128:(h + 1) * 128], True, False, [Mr, vb], [PY])
                        self.mm(PY[:, h * 128:(h + 1) * 128], qdz[h % 2][:, p, :], Sbf[:, p, :], False, True, [qdz[h % 2], Sbf], [PY])
                    for h in range(4):
                        p = h // 2
                        self.mm(PST[:, h * 128:(h + 1) * 128], k_tok[:, p * 128:(p + 1) * 128], vte[:, h * 128:(h + 1) * 128],
                                True, True, [k_tok, vte], [PST])
                    if d_ == 0:
                        self.cp(ypt[:], PY[:], [PY], [ypt])
                        self.dma(self.rpart_d[tok, :], ypt[:], [ypt], [self.rpart_d])
                    else:
                        self.tt(yt[:], PY[:], ypt[:], ALU.add, [PY, ypt], [yt])
                    for h in range(4):
                        p, r0 = h // 2, (h % 2) * 64
                        self.stt(S[r0:r0 + 64, p, :], S[r0:r0 + 64, p, :], g128[r0:r0 + 64, p:p + 1], PST[r0:r0 + 64, h * 128:(h + 1) * 128],
                                 ALU.mult, ALU.add, [S, g128, PST], [S])
                    self.act(Sbf[:], S[:], AF.Copy, [S], [Sbf])
                    if d_ == 1:
                        for h in range(4):
                            self.P.op(DVE, (lambda hh: (lambda e: e.bn_stats(out=stats[:, hh, :], in_=yt[:, hh * 128:(hh + 1) * 128])))(h),
                                      [yt.b], [stats.b])
                            self.P.op(DVE, (lambda hh: (lambda e: e.bn_aggr(out=mv[:, hh, :], in_=stats[:, hh, :])))(h),
                                      [stats.b], [mv.b])
                        self.ts(mv[:, :, 1], mv[:, :, 1], LN_EPS, None, ALU.add, None, [mv], [mv])
                        self.rsqrt(mv[:, :, 1], mv)
                        for h in range(4):
                            self.ts(yt[:, h * 128:(h + 1) * 128], yt[:, h * 128:(h + 1) * 128], mv[:, h, 0:1], mv[:, h, 1:2],
                                    ALU.subtract, ALU.mult, [yt, mv], [yt])
                        self.tt(ob[:], yt[:], sg[:], ALU.mult, [yt, sg], [ob])
                        for h in range(4):
                            self.tr(PT[:, 256 + h * 128:256 + (h + 1) * 128], ob[:, h * 128:(h + 1) * 128], self.identb[:],
                                    [ob, self.identb], [PT])
                        self.act(oT_[:], PT[:, 256:768].rearrange("p (a b) -> p a b", a=4), AF.Copy, [PT], [oT_])
                        self.dma(mixv[:, 12:16, tok], oT_[:], [oT_], [self.mixT_d], slow=True)

    def stage_M(self, l):
        I = self.I
        cst = self.cst
        blocks = [(0, 256)] + [(256 + 512 * i, 512) for i in range(8)]
        with ExitStack() as st1:
            v_all = self.sb(st1, [128, NCH, 512], BF16, "v_all")
            with ExitStack() as st:
                wv = I["w_in"][l].rearrange("(k p) c -> p k c", p=128)
                wm = self.sb(st, [128, 8, 704], BF16, "wm")
                wgate = self.sb(st, [128, 8, 512], BF16, "wgate")
                self.load_w(wm[:, :, 0:672], wm, wv[:, :, 2576:3248], I["w_in"])
                self.load_w(wm[:, :, 672:704], wm, wv[:, :, 5296:5328], I["w_in"])
                self.load_w(wgate[:], wgate, wv[:, :, 3248:3760], I["w_in"])
                wuq = self.sb(st, [128, 3, 8, 96], BF16, "wuq")
                wuqs = self.sb(st, [128, 3, 8, 96], BF16, "wuqs")
                wkp = self.sb(st, [128, 2, 8, 96], BF16, "wkp")
                wvv = self.sb(st, [128, 2, 8, 64], BF16, "wvv")
                self.memset(wuqs[:], 0.0, [wuqs])
                self.memset(wkp[:], 0.0, [wkp])
                uqv = I["mla_w_uq"][l].rearrange("(k p) (h e) -> p k h e", p=128, h=8)
                uqs = I["w_uq_sw"][l].rearrange("(k p) (h e) -> p k h e", p=128, h=8)
                ukv = I["mla_w_ukv"][l].rearrange("(k p) (h e) -> p k h e", p=128, h=8)
                for kc in range(3):
                    self.load_w(wuq[:, kc, :, :], wuq, uqv[:, kc, :, :], I["mla_w_uq"])
                    self.load_w(wuqs[:, kc, :, 64:96], wuqs, uqs[:, kc, :, :], I["w_uq_sw"])
                for kc in range(2):
                    self.load_w(wkp[:, kc, :, 0:64], wkp, ukv[:, kc, :, 0:64], I["mla_w_ukv"])
                    self.load_w(wvv[:, kc, :, :], wvv, ukv[:, kc, :, 64:128], I["mla_w_ukv"])
                esel = self.sb(st, [32, 96], BF16, "esel")
                self.memset(esel[:], 0.0, [esel])
                self.cp(esel[:, 64:96], self.identb[0:32, 0:32], [self.identb, esel], [esel])
                qn = self.sb(st, [128, 3], F32, "qn")
                kvn = self.sb(st, [128, 2], F32, "kvn")
                self.dma(qn[:], I["mla_q_norm"][l].rearrange("(k p) -> p k", p=128), [I["mla_q_norm"]], [qn], slow=True)
                self.dma(kvn[:], I["mla_kv_norm"][l].rearrange("(k p) -> p k", p=128), [I["mla_kv_norm"]], [kvn], slow=True)
                hb = [self.sb(st, [128, 8, 512], BF16, "hb") for _ in range(2)]
                tq = self.sb(st, [96, 2, 512], F32, "tq")
                self.memset(tq[0:64, 0, :], 1.0, [tq])
                self.memset(tq[0:64, 1, :], 0.0, [tq])
                tk = self.sb(st, [32, 2, 512], F32, "tk")
                cqs = self.sb(st, [128, 3, 512], F32, "cqs")
                sqs = self.sb(st, [128, 512], F32, "sqs")
                rstd = self.sb(st, [128, 512], F32, "rstd")
                cqn = self.sb(st, [128, 3, 512], BF16, "cqn")
                ckvn = self.sb(st, [128, 2, 512], BF16, "ckvn")
                kr1 = self.sb(st, [32, 512], F32, "kr1")
                kr2 = self.sb(st, [32, 512], F32, "kr2")
                krr = self.sb(st, [32, 512], BF16, "krr")
                q1 = self.sb(st, [96, 512], F32, "q1")
                q2 = self.sb(st, [96, 512], F32, "q2")
                qf = [self.sb(st, [96, 512], BF16, "qf") for _ in range(2)]
                kf = [self.sb(st, [96, 512], BF16, "kf") for _ in range(2)]
                ge = self.sb(st, [128, 512], F32, "ge")
                sgo = [self.sb(st, [128, 512], F32, "sgo") for _ in range(2)]
                P0 = [self.ps(st, [128, 512], F32, "P0") for _ in range(2)]
                PSS = self.ps(st, [128, 512], F32, "PSS")
                PQ1 = self.ps(st, [128, 512], F32, "PQ1")
                PQ2 = self.ps(st, [128, 512], F32, "PQ2")
                PK = self.ps(st, [128, 512], F32, "PK")
                PVv = self.ps(st, [128, 512], F32, "PVv")
                PG = self.ps(st, [128, 512], F32, "PG")
                sgv = self.sgT_d[:].rearrange("(r p) t -> p r t", p=128)
                ctr = 0
                for bi, (t0, n) in enumerate(blocks):
                    h_ = hb[bi % 2]
                    c0 = colof(t0)
                    self.dma(h_[:, :, 0:n], self.hv[:, :, c0:c0 + n], [self.hT_d], [h_], slow=True)
                    self.dma(tq[64:96, :, 0:n], I["mla_tab"][:, :, t0:t0 + n], [I["mla_tab"]], [tq], slow=True)
                    self.dma(tk[:, :, 0:n], I["mla_tab"][:, :, t0:t0 + n], [I["mla_tab"]], [tk], slow=True)
                    for (nrc, off, dst, nrm, dim, keep) in ((3, 0, cqn, qn, 384.0, None), (2, 384, ckvn, kvn, 256.0, None)):
                        for rc in range(nrc):
                            p_ = P0[ctr % 2]; ctr += 1
                            for kc in range(8):
                                self.mm(p_[:, 0:n], wm[:, kc, off + rc * 128:off + (rc + 1) * 128], h_[:, kc, 0:n], kc == 0, kc == 7, [wm, h_], [p_])
                            self.act(cqs[:, rc, 0:n], p_[:, 0:n], AF.Copy, [p_], [cqs])
                            self.act(sqs[:, 0:n], p_[:, 0:n], AF.Square, [p_], [sqs])
                            self.mm(PSS[:, 0:n], self.ones[:], sqs[:, 0:n], rc == 0, rc == nrc - 1, [self.ones, sqs], [PSS])
                        self.ts(rstd[:, 0:n], PSS[:, 0:n], 1.0 / dim, RMS_EPS, ALU.mult, ALU.add, [PSS], [rstd])
                        self.rsqrt(rstd[:, 0:n], rstd)
                        for rc in range(nrc):
                            self.stt(dst[:, rc, 0:n], cqs[:, rc, 0:n], nrm[:, rc:rc + 1], rstd[:, 0:n], ALU.mult, ALU.mult, [cqs, nrm, rstd], [dst])
                    self.mm_group_kr(h_, n, wm, PQ1, PQ2)
                    self.tt(kr1[:, 0:n], PQ1[0:32, 0:n], tk[:, 0, 0:n], ALU.mult, [PQ1, tk], [kr1])
                    self.tt(kr2[:, 0:n], PQ2[0:32, 0:n], tk[:, 1, 0:n], ALU.mult, [PQ2, tk], [kr2])
                    self.tt(krr[:, 0:n], kr1[:, 0:n], kr2[:, 0:n], ALU.add, [kr1, kr2], [krr])
                    for h in range(8):
                        qf_ = qf[h % 2]; kf_ = kf[h % 2]
                        for kc in range(3):
                            self.mm(PQ1[0:96, 0:n], wuq[:, kc, h, :], cqn[:, kc, 0:n], kc == 0, kc == 2, [wuq, cqn], [PQ1])
                        for kc in range(3):
                            self.mm(PQ2[0:96, 0:n], wuqs[:, kc, h, :], cqn[:, kc, 0:n], kc == 0, kc == 2, [wuqs, cqn], [PQ2])
                        self.tt(q1[:, 0:n], PQ1[0:96, 0:n], tq[:, 0, 0:n], ALU.mult, [PQ1, tq], [q1])
                        self.tt(q2[:, 0:n], PQ2[0:96, 0:n], tq[:, 1, 0:n], ALU.mult, [PQ2, tq], [q2])
                        self.tt(qf_[:, 0:n], q1[:, 0:n], q2[:, 0:n], ALU.add, [q1, q2], [qf_], eng=POOL)
                        self.dma(self.qT_d[h, :, t0:t0 + n], qf_[:, 0:n], [qf_], [self.qT_d])
                        for kc in range(2):
                            self.mm(PK[0:96, 0:n], wkp[:, kc, h, :], ckvn[:, kc, 0:n], kc == 0, False, [wkp, ckvn], [PK])
                        self.mm(PK[0:96, 0:n], esel[:], krr[:, 0:n], False, True, [esel, krr], [PK])
                        self.act(kf_[:, 0:n], PK[0:96, 0:n], AF.Copy, [PK], [kf_])
                        self.dma(self.kfT_d[h, :, t0:t0 + n], kf_[:, 0:n], [kf_], [self.kfT_d])
                    for s in range(n // 128):
                        ch = (t0 + s * 128) // 128
                        for kc in range(2):
                            self.mm(PVv[:], ckvn[:, kc, s * 128:(s + 1) * 128], wvv[:, kc, :, :].rearrange("p h e -> p (h e)"),
                                    kc == 0, kc == 1, [ckvn, wvv], [PVv])
                        self.act(v_all[:, ch, :], PVv[:], AF.Copy, [PVv], [v_all])
                    for rc in range(4):
                        sg_ = sgo[rc % 2]
                        for kc in range(8):
                            self.mm(PG[:, 0:n], wgate[:, kc, rc * 128:(rc + 1) * 128], h_[:, kc, 0:n], kc == 0, kc == 7, [wgate, h_], [PG])
                        self.act(ge[:, 0:n], PG[:, 0:n], AF.Exp, [PG], [ge], scale=-1.0)
                        self.sigm(ge[:, 0:n], ge)
                        self.tt(sg_[:, 0:n], PG[:, 0:n], ge[:, 0:n], ALU.mult, [PG, ge], [sg_])
                        self.dma(sgv[:, rc, t0:t0 + n], sg_[:, 0:n], [sg_], [self.sgT_d])
            self.P.barrier()
            import os as _os
            if _os.environ.get("NO_M2"):
                return
            with ExitStack() as st:
                kfh = [self.sb(st, [96, NTOK], BF16, "kfh") for _ in range(2)]
                qh = [self.sb(st, [96, NTOK], BF16, "qh") for _ in range(2)]
                vaug = [self.sb(st, [128, NCH, 128], BF16, "vaug") for _ in range(2)]
                self.memset(vaug[0][:, :, 64:128], 1.0, [vaug[0]])
                self.memset(vaug[1][:, :, 0:64], 1.0, [vaug[1]])
                pT = [self.sb(st, [128, 512], BF16, "pT") for _ in range(3)]
                sgh = [self.sb(st, [128, 512], F32, "sgh") for _ in range(2)]
                rden = self.sb(st, [128, 512], F32, "rden")
                ot = self.sb(st, [128, 512], F32, "ot")
                ob = [self.sb(st, [128, 512], BF16, "ob") for _ in range(2)]
                PSc = [self.ps(st, [128, 512], F32, "PSc") for _ in range(3)]
                PO = [self.ps(st, [128, 512], F32, "PO") for _ in range(2)]
                ci = 0
                bi_ = 0
                for h in range(8):
                    par = h % 2
                    r0 = 64 * par
                    d0 = 64 - r0
                    kf_ = kfh[h % 2]; q_ = qh[h % 2]; va = vaug[par]
                    self.dma(kf_[:], self.kfT_d[h], [self.kfT_d], [kf_])
                    self.dma(q_[:], self.qT_d[h], [self.qT_d], [q_])
                    self.cp(va[:, :, r0:r0 + 64], v_all[:, :, h * 64:(h + 1) * 64], [v_all], [va], eng=POOL)
                    for (t0, n) in blocks:
                        if t0 == 0:
                            if l == DEPTH - 1:
                                continue
                            kcs = [0, 1]
                        else:
                            kcs = list(range(NCH))
                        po = PO[bi_ % 2]; sg_ = sgh[bi_ % 2]; ob_ = ob[bi_ % 2]
                        bi_ += 1
                        self.dma(sg_[r0:r0 + 64, 0:n], self.sgT_d[512 // 8 * h:512 // 8 * (h + 1), t0:t0 + n], [self.sgT_d], [sg_])
                        for i, kc in enumerate(kcs):
                            psc = PSc[ci % 3]; pt = pT[ci % 3]
                            ci += 1
                            self.mm(psc[:, 0:n], kf_[:, kc * 128:(kc + 1) * 128], q_[:, t0:t0 + n], True, True, [kf_, q_], [psc])
                            self.act(pt[:, 0:n], psc[:, 0:n], AF.Exp, [psc], [pt], scale=MLA_SCALE)
                            self.mm(po[:, 0:n], va[:, kc, :], pt[:, 0:n], i == 0, i == len(kcs) - 1, [va, pt], [po])
                        self.P.op(DVE, (lambda a, b: (lambda e: e.reciprocal(out=a, in_=b)))(rden[d0:d0 + 64, 0:n], po[d0:d0 + 64, 0:n]),
                                  [po.b], [rden.b])
                        self.tt(ot[r0:r0 + 64, 0:n], po[r0:r0 + 64, 0:n], rden[d0:d0 + 64, 0:n], ALU.mult, [po, rden], [ot])
                        self.tt(ob_[r0:r0 + 64, 0:n], ot[r0:r0 + 64, 0:n], sg_[r0:r0 + 64, 0:n], ALU.mult, [ot, sg_], [ob_], eng=POOL)
                        self.dma(self.mixT_d[1024 + h * 64:1024 + (h + 1) * 64, t0:t0 + n], ob_[r0:r0 + 64, 0:n], [ob_], [self.mixT_d])

    def mm_group_kr(self, h_, n, wm, PQ1, PQ2):
        for kc in range(8):
            self.mm(PQ1[0:32, 0:n], wm[:, kc, 640:672], h_[:, kc, 0:n], kc == 0, kc == 7, [wm, h_], [PQ1])
        for kc in range(8):
            self.mm(PQ2[0:32, 0:n], wm[:, kc, 672:704], h_[:, kc, 0:n], kc == 0, kc == 7, [wm, h_], [PQ2])

    def stage_E(self, l):
        I = self.I
        with ExitStack() as st:
            wo = self.sb(st, [128, 16, 1024], BF16, "wo")
            wov = I["w_out"][l].rearrange("(k p) c -> p k c", p=128)
            for kc in range(16):
                self.load_w(wo[:, kc, :], wo, wov[:, kc, :], I["w_out"])
            lng = self.bvec(st, "ln_g", l, 1024)
            lnb = self.bvec(st, "ln_b", l, 1024)
            mt = [self.sb(st, [128, 16, 128], BF16, "mt") for _ in range(2)]
            xt = [self.sb(st, [128, D], F32, "xt") for _ in range(2)]
            v1 = self.sb(st, [128, D], F32, "v1")
            v2 = self.sb(st, [128, D], F32, "v2")
            xo = [self.sb(st, [128, D], F32, "xo") for _ in range(2)]
            stats = self.sb(st, [128, 2, 6], F32, "stats")
            mv = self.sb(st, [128, 2], F32, "mv")
            PZ = [self.ps(st, [128, 1024], F32, "PZ") for _ in range(2)]
            mixv = self.mixT_d[:].rearrange("(r p) t -> p r t", p=128)
            tiles = list(range(NCH)) if l < DEPTH - 1 else list(range(2, NCH))
            for i, t in enumerate(tiles):
                typ = 1 if t < 2 else 0
                m_ = mt[i % 2]; x_ = xt[i % 2]; pz = PZ[i % 2]; xo_ = xo[i % 2]
                tok = slice(t * 128, (t + 1) * 128)
                self.dma(m_[:], mixv[:, :, tok], [self.mixT_d], [m_], slow=True)
                if l == 0:
                    src_t = I["ctx"] if t < 2 else I["x"]
                    src = src_t[t * 128:(t + 1) * 128, :] if t < 2 else src_t[(t - 2) * 128:(t - 1) * 128, :]
                else:
                    src_t = self.xres_d
                    src = src_t[tok, :]
                self.dma(x_[:], src, [src_t], [x_])
                for nb in range(2):
                    for kc in range(16):
                        self.mm(pz[:, nb * 512:(nb + 1) * 512], m_[:, kc, :], wo[:, kc, nb * 512:(nb + 1) * 512], kc == 0, kc == 15, [m_, wo], [pz])
                self.tt(v1[:], pz[:], self.gB[:, typ, :], ALU.mult, [pz, self.gB], [v1])
                self.stt(v2[:], x_[:], ALPHA, v1[:], ALU.mult, ALU.add, [x_, v1], [v2])
                for s in range(2):
                    self.P.op(DVE, (lambda ss: (lambda e: e.bn_stats(out=stats[:, ss, :], in_=v2[:, ss * 512:(ss + 1) * 512])))(s),
                              [v2.b], [stats.b])
                self.P.op(DVE, lambda e: e.bn_aggr(out=mv[:], in_=stats[:]), [stats.b], [mv.b])
                self.ts(mv[:, 1:2], mv[:, 1:2], LN_EPS, None, ALU.add, None, [mv], [mv])
                self.rsqrt(mv[:, 1:2], mv)
                self.ts(v1[:], v2[:], mv[:, 0:1], mv[:, 1:2], ALU.subtract, ALU.mult, [v2, mv], [v1])
                self.tt(v1[:], v1[:], lng[:], ALU.mult, [v1, lng], [v1], eng=POOL)
                self.tt(xo_[:], v1[:], lnb[:], ALU.add, [v1, lnb], [xo_], eng=POOL)
                if l < DEPTH - 1:
                    self.dma(self.xres_d[tok, :], xo_[:], [xo_], [self.xres_d])
                else:
                    self.dma(self.out[(t - 2) * 128:(t - 1) * 128, :], xo_[:], [xo_], [self.out])


C_ID = 0
C_UF = 128
C_LF = 256
C_UB = 384
C_LB = 512
C_MF = 640
C_MB = 768
C_RIF = 896
C_RIB = 1024
C_GF = 1152
C_GB = 1280
C_TEF = 1408
C_TEB = 1409
CST_W = 1410


def make_consts():
    k = np.arange(128)[:, None].astype(np.float32)
    i = np.arange(128)[None, :].astype(np.float32)
    cst = np.zeros((128, CST_W), np.float32)
    cst[:, C_ID:C_ID + 128] = (k == i)
    cst[:, C_UF:C_UF + 128] = (k <= i)
    cst[:, C_LF:C_LF + 128] = (k > i)
    cst[:, C_UB:C_UB + 128] = (k >= i)
    cst[:, C_LB:C_LB + 128] = (k < i)
    cst[:, C_MF:C_MF + 128] = (k <= i)
    cst[:, C_MB:C_MB + 128] = (k >= i)
    cst[:, C_RIF:C_RIF + 128] = np.maximum(i - k, 0)
    cst[:, C_RIB:C_RIB + 128] = np.maximum(k - i, 0)
    cst[:, C_GF:C_GF + 128] = np.broadcast_to(i + 1, (128, 128))
    cst[:, C_GB:C_GB + 128] = np.broadcast_to(128 - i, (128, 128))
    cst[:, C_TEF] = 127 - k[:, 0]
    cst[:, C_TEB] = k[:, 0]
    return cst


def rope_tables():
    rows = SEQ // 64
    t = np.arange(rows * 64)
    row = (t // 64).astype(np.float32)
    col = (t % 64).astype(np.float32)

    def cs(rot):
        nf = rot // 4
        inv = (np.float32(10000.0) ** (-np.arange(nf, dtype=np.float32) / np.float32(nf))).astype(np.float32)
        ang = np.concatenate([row[:, None] * inv, col[:, None] * inv], -1).astype(np.float32)
        return np.cos(ang).astype(np.float32), np.sin(ang).astype(np.float32)
    cm, sm = cs(32)
    mla = np.zeros((32, 2, NTOK), np.float32)
    mla[:, 0, :CTX] = 1.0
    mla[0:16, 0, CTX:] = cm.T; mla[16:32, 0, CTX:] = cm.T
    mla[0:16, 1, CTX:] = -sm.T; mla[16:32, 1, CTX:] = sm.T
    cr, sr = cs(64)
    ret = np.zeros((128, 4, NTOK), np.float32)
    ret[:, 0, :CTX] = 1.0
    for hh in range(2):
        b = hh * 64
        ret[b:b + 32, 0, CTX:] = cr.T; ret[b + 32:b + 64, 0, CTX:] = cr.T
        ret[b:b + 32, 1, CTX:] = -sr.T; ret[b + 32:b + 64, 1, CTX:] = sr.T
    ret[:, 2] = ret[:, 0] * 0.125
    ret[:, 3] = ret[:, 1] * 0.125
    return mla, ret


_CACHE = {}


def prep_inputs(inputs):
    f = lambda a: np.ascontiguousarray(np.asarray(a, dtype=np.float32))
    w_in = f(inputs["w_in"])
    kr = w_in[:, :, 3216:3248]
    kr_sw = np.concatenate([kr[:, :, 16:32], kr[:, :, 0:16]], -1)

    def sw64(a):
        a = a.reshape(2, D, 4, 2, 32)
        return np.ascontiguousarray(a[:, :, :, ::-1, :]).reshape(2, D, 256)
    q_sw = sw64(w_in[:, :, 3760:4016])
    k_sw = sw64(w_in[:, :, 4016:4272])
    w_ext = np.ascontiguousarray(np.concatenate([w_in, kr_sw, q_sw, k_sw], -1))
    uq = f(inputs["mla_w_uq"]).reshape(2, 384, 8, 96)
    uq_r = uq[:, :, :, 64:96]
    uq_sw = np.ascontiguousarray(np.concatenate([uq_r[..., 16:32], uq_r[..., 0:16]], -1)).reshape(2, 384, 256)
    mla_tab, ret_tab = rope_tables()
    shared = {
        "w_ada": f(inputs["w_ada"]), "b_ada": f(inputs["b_ada"]), "w_in": w_ext,
        "ssd_conv_w": f(inputs["ssd_conv_w"]), "ssd_conv_b": f(inputs["ssd_conv_b"]),
        "ssd_norm_w": f(inputs["ssd_norm_w"]), "mla_q_norm": f(inputs["mla_q_norm"]),
        "mla_w_uq": f(inputs["mla_w_uq"]), "w_uq_sw": uq_sw, "mla_kv_norm": f(inputs["mla_kv_norm"]),
        "mla_w_ukv": f(inputs["mla_w_ukv"]), "ret_log_rate_f": f(inputs["ret_log_rate_f"]),
        "ret_log_rate_b": f(inputs["ret_log_rate_b"]), "w_out": f(inputs["w_out"]),
        "ln_g": f(inputs["ln_g"]), "ln_b": f(inputs["ln_b"]),
        "cst": make_consts(), "mla_tab": mla_tab, "ret_tab": ret_tab,
    }
    for n in ("ssd_a_log_f", "ssd_a_log_b", "ssd_dt_bias_f", "ssd_dt_bias_b", "ssd_d"):
        shared[n] = f(inputs[n])
    x = f(inputs["x"]); c = f(inputs["c"]); ctx = f(inputs["ctx"]); c_ctx = f(inputs["c_ctx"])
    maps = []
    for b in range(8):
        m = dict(shared)
        m["x"] = x[b]
        m["ctx"] = ctx[b]
        m["cvec"] = np.ascontiguousarray(np.stack([c[b], c_ctx], 0))
        maps.append(m)
    return maps


def kernel(**inputs):
    if "nc" not in _CACHE:
        _CACHE["nc"] = K().build()
    nc = _CACHE["nc"]
    maps = prep_inputs(inputs)
    res = run_bass_kernel_spmd(nc, maps, core_ids=list(range(8)))
    return np.stack([np.asarray(r["out"], dtype=np.float32) for r in res.results], 0)
```

```python
import math
from contextlib import ExitStack
import numpy as np
import concourse.bass as bass
import concourse.mybir as mybir
from concourse.bass_utils import run_bass_kernel_spmd

F32 = mybir.dt.float32
BF16 = mybir.dt.bfloat16
AF = mybir.ActivationFunctionType
ALU = mybir.AluOpType

PE, ACT, DVE, POOL, SP = "pe", "act", "dve", "pool", "sp"
EPOCH = 30000
DMA_K = 8
DMA_EPOCH = 1800

D = 1024
SEQ = 4096
CTX = 256
NTOK = SEQ + CTX
NCH = NTOK // 128
HC = NTOK + 8
DEPTH = 2
ALPHA = (2 * DEPTH) ** 0.25
LN_EPS = 1e-5
RMS_EPS = 1e-6
MLA_SCALE = 96 ** -0.5
WEXT = 5296 + 32 + 256 + 256


def colof(t):
    return t + 2 if t < CTX else t + 6


class Buf:
    __slots__ = ("name", "lw", "rd", "psum")

    def __init__(self, name=""):
        self.name = name
        self.lw = None
        self.rd = []
        self.psum = False


class Op:
    __slots__ = ("eng", "fn", "deps", "sig", "idx", "dma_slot", "dma_prev")

    def __init__(self, eng, fn):
        self.eng = eng
        self.fn = fn
        self.deps = set()
        self.sig = None
        self.dma_slot = None
        self.dma_prev = None


class Prog:
    def __init__(self, nc):
        self.nc = nc
        self.ops = []
        self.eng = {PE: nc.tensor, ACT: nc.scalar, DVE: nc.vector, POOL: nc.gpsimd, SP: nc.sync}
        self.dma_lists = {}
        self.last = {}

    def op(self, eng, fn, reads=(), writes=(), dma=False):
        o = Op(eng, fn)
        o.idx = len(self.ops)
        for b in reads:
            if b.lw is not None:
                o.deps.add(b.lw)
            if b.psum:
                for r in b.rd:
                    if self.ops[r].eng != eng:
                        o.deps.add(r)
        for b in writes:
            if b.lw is not None:
                o.deps.add(b.lw)
            for r in b.rd:
                o.deps.add(r)
        for b in reads:
            b.rd.append(o.idx)
        for b in writes:
            b.lw = o.idx
            b.rd = []
        if dma:
            lst = self.dma_lists.setdefault(eng, [])
            o.dma_slot = len(lst)
            if len(lst) >= DMA_K:
                o.dma_prev = lst[len(lst) - DMA_K]
            lst.append(o.idx)
        o.deps.discard(o.idx)
        self.ops.append(o)
        self.last[eng] = o.idx
        return o

    def barrier(self):
        bufs = {}
        for e in (PE, ACT, DVE, POOL, SP):
            bufs[e] = Buf("bar" + e)
            o = self.op(e, lambda en: en.nop(), writes=[bufs[e]])
            for lst in self.dma_lists.values():
                for d in lst[-DMA_K:]:
                    if d != o.idx:
                        o.deps.add(d)
        for e in (PE, ACT, DVE, POOL, SP):
            self.op(e, lambda en: en.nop(), reads=list(bufs.values()))

    def emit(self, stack):
        nc = self.nc
        ops = self.ops
        needed = set()
        for o in ops:
            for d in o.deps:
                do = ops[d]
                if do.eng == o.eng and o.eng == PE and do.dma_slot is None:
                    continue
                needed.add(d)
            if o.dma_prev is not None:
                needed.add(o.dma_prev)
        cnt = {}
        sems = {}
        dma_sems = {}
        for o in ops:
            if o.dma_slot is not None:
                k = o.dma_slot % DMA_K
                n = o.dma_slot // DMA_K
                key = (o.eng, k, n // DMA_EPOCH)
                if key not in dma_sems:
                    dma_sems[key] = stack.enter_context(nc.semaphore("dq%s%d_%d" % key))
                o.sig = (dma_sems[key], 16 * (n % DMA_EPOCH + 1))
            elif o.idx in needed:
                c = cnt.get(o.eng, 0)
                key = (o.eng, c // EPOCH)
                if key not in sems:
                    sems[key] = stack.enter_context(nc.semaphore("s%s_%d" % key))
                o.sig = (sems[key], c % EPOCH + 1)
                cnt[o.eng] = c + 1
        waited = {}
        nw = 0
        for o in ops:
            e = self.eng[o.eng]
            deps = set(o.deps)
            if o.dma_prev is not None:
                deps.add(o.dma_prev)
            for d in sorted(deps):
                do = ops[d]
                if do.sig is None:
                    continue
                sem, val = do.sig
                key = (o.eng, id(sem))
                if waited.get(key, 0) >= val:
                    continue
                waited[key] = val
                e.wait_ge(sem, val)
                nw += 1
            ins = o.fn(e)
            if o.sig is not None:
                sem, val = o.sig
                ins.then_inc(sem, 16 if o.dma_slot is not None else 1)
        self.nwaits = nw


class T:
    __slots__ = ("t", "b")

    def __init__(self, t, name=""):
        self.t = t
        self.b = Buf(name)

    def __getitem__(self, k):
        return self.t[k]


class K:
    def __init__(self, debug=False, stop_after=None, skip=()):
        self.skip = set(skip)
        self.debug = debug
        self.stop_after = stop_after
        self.nc = bass.Bass("TRN2", target_bir_lowering=False)
        self.P = Prog(self.nc)
        self.uid = 0

    def dram(self, name, shape, dt, kind="Internal"):
        return T(self.nc.dram_tensor(name, list(shape), dt, kind=kind).ap(), name)

    def sb(self, st, shape, dt, name=None):
        self.uid += 1
        name = "%s_%d" % (name or "t", self.uid)
        return T(st.enter_context(self.nc.sbuf_tensor(name, list(shape), dt)), name)

    def ps(self, st, shape, dt=F32, name=None):
        self.uid += 1
        name = "%s_%d" % (name or "p", self.uid)
        t = T(st.enter_context(self.nc.psum_tensor(name, list(shape), dt)), name)
        t.b.psum = True
        return t

    def dma(self, out, in_, reads, writes, eng=SP, slow=False):
        if slow:
            return self.P.op(eng, lambda e: e.dma_start(out=out, in_=in_, allow_slow_non_contiguous=True),
                             [x.b for x in reads], [x.b for x in writes], dma=True)
        return self.P.op(eng, lambda e: e.dma_start(out=out, in_=in_), [x.b for x in reads], [x.b for x in writes], dma=True)

    def mm(self, out, lhsT, rhs, start, stop, reads, writes):
        return self.P.op(PE, lambda e: e.matmul(out, lhsT=lhsT, rhs=rhs, start=start, stop=stop),
                         [x.b for x in reads], [x.b for x in writes])

    def tr(self, out, in_, ident, reads, writes):
        return self.P.op(PE, lambda e: e.transpose(out=out, in_=in_, identity=ident),
                         [x.b for x in reads], [x.b for x in writes])

    def act(self, out, in_, func, reads, writes, bias=None, scale=None, accum_out=None):
        kw = {}
        if bias is not None:
            kw["bias"] = bias
        if scale is not None:
            kw["scale"] = scale
        if accum_out is not None:
            kw["accum_out"] = accum_out
        return self.P.op(ACT, lambda e: e.activation(out=out, in_=in_, func=func, **kw),
                         [x.b for x in reads], [x.b for x in writes])

    def tt(self, out, in0, in1, op, reads, writes, eng=DVE):
        return self.P.op(eng, lambda e: e.tensor_tensor(out=out, in0=in0, in1=in1, op=op),
                         [x.b for x in reads], [x.b for x in writes])

    def ts(self, out, in0, s1, s2, op0, op1, reads, writes, eng=DVE):
        if op1 is None:
            return self.P.op(eng, lambda e: e.tensor_scalar(out=out, in0=in0, scalar1=s1, scalar2=None, op0=op0),
                             [x.b for x in reads], [x.b for x in writes])
        return self.P.op(eng, lambda e: e.tensor_scalar(out=out, in0=in0, scalar1=s1, scalar2=s2, op0=op0, op1=op1),
                         [x.b for x in reads], [x.b for x in writes])

    def stt(self, out, in0, scalar, in1, op0, op1, reads, writes, eng=DVE):
        return self.P.op(eng, lambda e: e.scalar_tensor_tensor(out=out, in0=in0, scalar=scalar, in1=in1, op0=op0, op1=op1),
                         [x.b for x in reads], [x.b for x in writes])

    def cp(self, out, in_, reads, writes, eng=DVE):
        return self.P.op(eng, lambda e: e.tensor_copy(out=out, in_=in_), [x.b for x in reads], [x.b for x in writes])

    def memset(self, out, val, writes, eng=POOL):
        return self.P.op(eng, lambda e: e.memset(out, val), [], [x.b for x in writes])

    def sigm(self, ap, t):
        self.ts(ap, ap, 1.0, None, ALU.add, None, [t], [t])
        self.P.op(DVE, lambda e: e.reciprocal(out=ap, in_=ap), [t.b], [t.b])

    def rsqrt(self, ap, t):
        self.act(ap, ap, AF.Ln, [t], [t])
        self.act(ap, ap, AF.Exp, [t], [t], scale=-0.5)

    def build(self):
        nc = self.nc
        dbg = self.debug
        I = {}

        def inp(name, shape):
            I[name] = self.dram(name, shape, F32, kind="ExternalInput")
        inp("x", [SEQ, D]); inp("ctx", [CTX, D]); inp("cvec", [2, D])
        inp("w_ada", [2, D, 3 * D]); inp("b_ada", [2, 3 * D]); inp("w_in", [2, D, WEXT])
        inp("ssd_conv_w", [2, 5, 1536]); inp("ssd_conv_b", [2, 1536])
        for n in ("ssd_a_log_f", "ssd_a_log_b", "ssd_dt_bias_f", "ssd_dt_bias_b", "ssd_d"):
            inp(n, [2, 16])
        inp("ssd_norm_w", [2, 1024]); inp("mla_q_norm", [2, 384]); inp("mla_w_uq", [2, 384, 768])
        inp("w_uq_sw", [2, 384, 256]); inp("mla_kv_norm", [2, 256]); inp("mla_w_ukv", [2, 256, 1024])
        inp("ret_log_rate_f", [2, 4]); inp("ret_log_rate_b", [2, 4]); inp("w_out", [2, 2048, D])
        inp("ln_g", [2, D]); inp("ln_b", [2, D])
        inp("cst", [128, CST_W]); inp("mla_tab", [32, 2, NTOK]); inp("ret_tab", [128, 4, NTOK])
        self.I = I
        okind = "ExternalOutput" if dbg else "Internal"
        self.out = self.dram("out", [SEQ, D], F32, kind="ExternalOutput")
        self.hT_d = self.dram("hT_d", [128, 8 * HC], BF16, kind=okind)
        self.ypart_d = self.dram("ypart_d", [NTOK, 1024], F32)
        self.rpart_d = self.dram("rpart_d", [NTOK, 512], F32)
        self.ypartb_d = self.dram("ypartb_d", [NTOK, 1024], F32)
        self.rpartb_d = self.dram("rpartb_d", [NTOK, 512], F32)
        self.mixT_d = self.dram("mixT_d", [2048, NTOK], BF16, kind=okind)
        self.qT_d = self.dram("qT_d", [8, 96, NTOK], BF16)
        self.kfT_d = self.dram("kfT_d", [8, 96, NTOK], BF16)
        self.sgT_d = self.dram("sgT_d", [512, NTOK], F32)
        self.xres_d = self.dram("xres_d", [NTOK, D], F32, kind=okind)

        with ExitStack() as gst:
            self.gst = gst
            self.cst = self.sb(gst, [128, CST_W], F32, "cst")
            self.dma(self.cst[:], I["cst"][:], [I["cst"]], [self.cst])
            self.identb = self.sb(gst, [128, 128], BF16, "identb")
            self.cp(self.identb[:], self.cst[:, C_ID:C_ID + 128], [self.cst], [self.identb])
            self.ones = self.sb(gst, [128, 128], F32, "ones")
            self.memset(self.ones[:], 1.0, [self.ones])
            self.gB = self.sb(gst, [128, 2, 1024], F32, "gB")
            zt = self.sb(gst, [128, 8, 4], BF16, "zt")
            self.memset(zt[:], 0.0, [zt])
            hv = self.hT_d[:].rearrange("p (k c) -> p k c", k=8)
            self.hv = hv
            for (a, b) in ((0, 2), (258, 262), (4358, 4360)):
                self.dma(hv[:, :, a:b], zt[:, :, 0:b - a], [zt], [self.hT_d], slow=True)
            self.P.barrier()
            stages = []
            for l in range(DEPTH):
                stages += [("A", l), ("S", l), ("M", l), ("R", l), ("E", l)]
            for (s, l) in stages:
                if s in self.skip:
                    continue
                if s == "A":
                    self.stage_A(l)
                elif s == "S":
                    self.stage_S(l)
                elif s == "M":
                    self.stage_M(l)
                elif s == "R":
                    self.stage_R(l)
                else:
                    self.stage_E(l)
                self.P.barrier()
                if self.stop_after == (s, l):
                    break
            self.P.barrier()
            self.P.emit(gst)
        return nc

    def silu_psum(self, st, src_ap, src_t, out_ap, out_t, e_t, e_ap, r_ap):
        self.act(e_ap, src_ap, AF.Exp, [src_t], [e_t], scale=-1.0)
        self.sigm(e_ap, e_t)
        self.tt(out_ap, src_ap, r_ap, ALU.mult, [src_t, e_t], [out_t])

    def stage_A(self, l):
        I = self.I
        with ExitStack() as st:
            wada = [self.sb(st, [128, 8, 512], F32, "wada") for _ in range(2)]
            craw = self.sb(st, [128, 8, 2], F32, "craw")
            ce = self.sb(st, [128, 8, 2], F32, "ce")
            scT = self.sb(st, [128, 8, 2], F32, "scT")
            modT = self.sb(st, [128, 24, 2], F32, "modT")
            scale1 = self.sb(st, [128, 8, 2], F32, "scale1")
            brow = self.sb(st, [1, 3 * D], F32, "brow")
            pm = self.ps(st, [128, 512], F32, "pm")
            pg = [self.ps(st, [128, 512], F32, "pg") for _ in range(2)]
            for j in range(2):
                self.dma(craw[:, :, j], I["cvec"][j].rearrange("(k p) -> p k", p=128), [I["cvec"]], [craw], slow=True)
            self.dma(brow[:], I["b_ada"][l:l + 1, :], [I["b_ada"]], [brow])
            self.act(ce[:], craw[:], AF.Exp, [craw], [ce], scale=-1.0)
            self.sigm(ce[:], ce)
            self.tt(scT[:], craw[:], ce[:], ALU.mult, [craw, ce], [scT])
            wv = I["w_ada"][l].rearrange("(k p) c -> p k c", p=128)
            for cb in range(6):
                w = wada[cb % 2]
                self.dma(w[:], wv[:, :, cb * 512:(cb + 1) * 512], [I["w_ada"]], [w])
                if cb < 4:
                    for dj in range(4):
                        j = cb * 4 + dj
                        for kc in range(8):
                            self.mm(pm[:, 2 * dj:2 * dj + 2], w[:, kc, dj * 128:(dj + 1) * 128], scT[:, kc, :],
                                    kc == 0, False, [w, scT], [pm])
                        self.mm(pm[:, 2 * dj:2 * dj + 2], brow[0:1, j * 128:(j + 1) * 128], self.ones[0:1, 0:2],
                                False, True, [brow, self.ones], [pm])
                        self.cp(modT[:, j, :], pm[:, 2 * dj:2 * dj + 2], [pm], [modT])
                else:
                    for typ in range(2):
                        p = pg[typ]
                        for kc in range(8):
                            self.mm(p[:], scT[:, kc, typ:typ + 1].to_broadcast([128, 128]), w[:, kc, :],
                                    kc == 0, False, [w, scT], [p])
                        self.mm(p[:], self.ones[0:1, 0:128], brow[0:1, cb * 512:(cb + 1) * 512], False, True,
                                [brow, self.ones], [p])
                        self.cp(self.gB[:, typ, (cb - 4) * 512:(cb - 3) * 512], p[:], [p], [self.gB])
            self.ts(scale1[:], modT[:, 8:16, :], 1.0, None, ALU.add, None, [modT], [scale1])
            xt = [self.sb(st, [128, D], F32, "xt") for _ in range(2)]
            ht = [self.sb(st, [128, 8, 128], BF16, "ht") for _ in range(2)]
            pT = [self.ps(st, [128, 1024], F32, "pT") for _ in range(2)]
            for t in range(NCH):
                typ = 1 if t < 2 else 0
                if l == 0:
                    src_t = I["ctx"] if t < 2 else I["x"]
                    src = src_t[t * 128:(t + 1) * 128, :] if t < 2 else src_t[(t - 2) * 128:(t - 1) * 128, :]
                else:
                    src_t = self.xres_d
                    src = src_t[t * 128:(t + 1) * 128, :]
                x_ = xt[t % 2]; h_ = ht[t % 2]; p_ = pT[t % 2]
                self.dma(x_[:], src, [src_t], [x_])
                for kc in range(8):
                    self.tr(p_[:, kc * 128:(kc + 1) * 128], x_[:, kc * 128:(kc + 1) * 128], self.cst[:, C_ID:C_ID + 128],
                            [x_, self.cst], [p_])
                for kc in range(8):
                    self.act(h_[:, kc, :], p_[:, kc * 128:(kc + 1) * 128], AF.Identity, [p_, scale1, modT], [h_],
                             bias=modT[:, kc, typ:typ + 1], scale=scale1[:, kc, typ:typ + 1])
                c0 = colof(t * 128)
                self.dma(self.hv[:, :, c0:c0 + 128], h_[:], [h_], [self.hT_d])

    def load_w(self, dst_ap, dst_t, src_ap, src_t):
        self.dma(dst_ap, src_ap, [src_t], [dst_t], eng=POOL, slow=True)

    def bvec(self, st, name, l, n):
        t = self.sb(st, [128, n], F32, name)
        self.dma(t[:], self.I[name][l:l + 1, :].to_broadcast([128, n]), [self.I[name]], [t], slow=True)
        return t

    def interleave(self, factories, width):
        pending = list(factories)
        active = []
        for s in range(width):
            if pending:
                active.append((s, pending.pop(0)(s)))
        while active:
            nxt = []
            for (s, g) in active:
                try:
                    next(g)
                    nxt.append((s, g))
                except StopIteration:
                    if pending:
                        nxt.append((s, pending.pop(0)(s)))
            active = nxt

    def stage_S(self, l):
        I = self.I
        cst = self.cst
        with ExitStack() as st:
            wv = I["w_in"][l].rearrange("(k p) c -> p k c", p=128)
            wx = self.sb(st, [128, 8, 1536], BF16, "wx")
            wdt = self.sb(st, [128, 8, 16], BF16, "wdt")
            for kc in range(8):
                self.load_w(wx[:, kc, :], wx, wv[:, kc, 1024:2560], I["w_in"])
            self.load_w(wdt[:], wdt, wv[:, :, 2560:2576], I["w_in"])
            convw = self.sb(st, [128, 12, 5], F32, "convw")
            for k in range(5):
                self.dma(convw[:, :, k], I["ssd_conv_w"][l, k].rearrange("(r p) -> p r", p=128), [I["ssd_conv_w"]], [convw], slow=True)
            dg = self.sb(st, [128, 60, 128], BF16, "dg")
            for r in range(12):
                for k in range(5):
                    self.ts(dg[:, r * 5 + k, :], self.identb[:], convw[:, r, k:k + 1], None, ALU.mult, None,
                            [self.identb, convw], [dg])
            cbrow = self.sb(st, [1, 1536], F32, "cbrow")
            self.dma(cbrow[:], I["ssd_conv_b"][l:l + 1, :], [I["ssd_conv_b"]], [cbrow])
            Dsk = self.bvec(st, "ssd_d", l, 16)
            prm = {}
            for d_, sfx in ((0, "f"), (1, "b")):
                al = self.bvec(st, "ssd_a_log_" + sfx, l, 16)
                self.act(al[:], al[:], AF.Exp, [al], [al])
                self.ts(al[:], al[:], -1.0, None, ALU.mult, None, [al], [al])
                dtb = self.bvec(st, "ssd_dt_bias_" + sfx, l, 16)
                prm[d_] = (al, dtb)
            with ExitStack() as st2:
                gens = []
                import os as _os
                for d_ in [int(ch) for ch in _os.environ.get("S_DIRS", "01")]:
                    gens.append((lambda dd: (lambda slot: self.ssd_sweep(l, dd, st2, wx, wdt, dg, cbrow, Dsk, prm[dd])))(d_))
                self.interleave(gens, int(_os.environ.get("S_WIDTH", "2")))
            self.P.barrier()
            if _os.environ.get("S_NOP3"):
                return
            with ExitStack() as st3:
                wz = self.sb(st3, [128, 8, 1024], BF16, "wz")
                for kc in range(8):
                    self.load_w(wz[:, kc, :], wz, wv[:, kc, 0:1024], I["w_in"])
                nwB = self.bvec(st3, "ssd_norm_w", l, 1024)
                sets = []
                for s in range(2):
                    B = {}
                    B["hc"] = self.sb(st3, [128, 8, 128], BF16, "hc3")
                    B["ypf"] = self.sb(st3, [128, 1024], F32, "ypf")
                    B["ypb"] = self.sb(st3, [128, 1024], F32, "ypb")
                    B["e"] = self.sb(st3, [128, 1024], F32, "e3")
                    B["t1"] = self.sb(st3, [128, 1024], F32, "t13")
                    B["bst"] = self.sb(st3, [128, 2, 6], F32, "bst3")
                    B["ssq"] = self.sb(st3, [128, 2], F32, "ssq3")
                    B["ob"] = self.sb(st3, [128, 1024], BF16, "ob3")
                    B["oT"] = self.sb(st3, [128, 8, 128], BF16, "oT3")
                    B["PZ"] = [self.ps(st3, [128, 512], F32, "PZ3") for _ in range(2)]
                    B["PT"] = self.ps(st3, [128, 1024], BF16, "PT3")
                    sets.append(B)
                gens = [(lambda cc: (lambda slot: self.ssd_final(cc, sets[slot], wz, nwB)))(c)
                        for c in range(int(_os.environ.get("S_P3_N", NCH)))]
                self.interleave(gens, int(_os.environ.get("S_P3_W", "2")))

    def ssd_final(self, c, B, wz, nwB):
        h_ = B["hc"]; ypf = B["ypf"]; ypb = B["ypb"]; e = B["e"]; t1 = B["t1"]; bst = B["bst"]; ssq = B["ssq"]
        ob = B["ob"]; oT_ = B["oT"]; PZ = B["PZ"]; PT = B["PT"]
        mixv = self.mixT_d[:].rearrange("(r p) t -> p r t", p=128)
        c0 = colof(c * 128)
        tok = slice(c * 128, (c + 1) * 128)
        self.dma(h_[:], self.hv[:, :, c0:c0 + 128], [self.hT_d], [h_], slow=True)
        self.dma(ypf[:], self.ypart_d[tok, :], [self.ypart_d], [ypf])
        self.dma(ypb[:], self.ypartb_d[tok, :], [self.ypartb_d], [ypb])
        yield
        for n in range(2):
            for kc in range(8):
                self.mm(PZ[n][:], h_[:, kc, :], wz[:, kc, n * 512:(n + 1) * 512], kc == 0, kc == 7, [h_, wz], [PZ[n]])
        self.tt(ypf[:], ypf[:], ypb[:], ALU.add, [ypf, ypb], [ypf])
        yield
        for n in range(2):
            self.act(e[:, n * 512:(n + 1) * 512], PZ[n][:], AF.Exp, [PZ[n]], [e], scale=-1.0)
        yield
        self.sigm(e[:], e)
        yield
        for n in range(2):
            self.tt(t1[:, n * 512:(n + 1) * 512], PZ[n][:], e[:, n * 512:(n + 1) * 512], ALU.mult, [PZ[n], e], [t1])
        yield
        self.tt(t1[:], t1[:], ypf[:], ALU.mult, [t1, ypf], [t1])
        yield
        for s_ in range(2):
            self.P.op(DVE, (lambda ss: (lambda en: en.bn_stats(out=bst[:, ss, :], in_=t1[:, ss * 512:(ss + 1) * 512])))(s_),
                      [t1.b], [bst.b])
        self.P.op(DVE, lambda en: en.bn_aggr(out=ssq[:], in_=bst[:]), [bst.b], [ssq.b])
        self.stt(ssq[:, 1:2], ssq[:, 0:1], ssq[:, 0:1], ssq[:, 1:2], ALU.mult, ALU.add, [ssq], [ssq])
        self.ts(ssq[:, 1:2], ssq[:, 1:2], RMS_EPS, None, ALU.add, None, [ssq], [ssq])
        yield
        self.rsqrt(ssq[:, 1:2], ssq)
        yield
        self.stt(ob[:], t1[:], ssq[:, 1:2], nwB[:], ALU.mult, ALU.mult, [t1, ssq, nwB], [ob])
        yield
        for r in range(8):
            self.tr(PT[:, r * 128:(r + 1) * 128], ob[:, r * 128:(r + 1) * 128], self.identb[:], [ob, self.identb], [PT])
        yield
        self.act(oT_[:], PT[:].rearrange("p (a b) -> p a b", a=8), AF.Copy, [PT], [oT_])
        yield
        self.dma(mixv[:, 0:8, tok], oT_[:], [oT_], [self.mixT_d], slow=True)
        yield

    def ssd_sweep(self, l, d_, st, wx, wdt, dg, cbrow, Dsk, prm):
        cst = self.cst
        al, dtb = prm
        H = self.sb(st, [128, 1024], F32, "H")
        Hbf = self.sb(st, [128, 1024], BF16, "Hbf")
        hc = [self.sb(st, [128, 8, 132], BF16, "hc") for _ in range(2)]
        xbc = self.sb(st, [128, 12, 132], BF16, "xbc")
        esb = self.sb(st, [128, 2048], F32, "esb")
        ubf = self.sb(st, [128, 12, 128], BF16, "ubf")
        dtx = self.sb(st, [128, 16], F32, "dtx")
        dt = self.sb(st, [128, 16], F32, "dt")
        la = self.sb(st, [128, 16], F32, "la")
        E3 = self.sb(st, [128, 48], F32, "E3")
        xs_tok = self.sb(st, [128, 1024], BF16, "xs_tok")
        B_tok = self.sb(st, [128, 256], BF16, "B_tok")
        scm = self.sb(st, [128, 2, 128], F32, "scm")
        LaL = self.sb(st, [128, 16, 128], F32, "LaL")
        M = self.sb(st, [128, 16, 128], BF16, "M")
        v = self.sb(st, [128, 1024], BF16, "v")
        vte = self.sb(st, [128, 1024], BF16, "vte")
        t1 = self.sb(st, [128, 1024], F32, "t1")
        t2 = self.sb(st, [128, 1024], F32, "t2")
        yp = [self.sb(st, [128, 1024], F32, "yp") for _ in range(2)]
        Q = [self.ps(st, [128, 512], F32, "Q") for _ in range(4)]
        Qb0 = Q[0][:].bitcast(BF16)
        Qb1 = Q[1][:].bitcast(BF16)
        ydst = self.ypart_d if d_ == 0 else self.ypartb_d
        order = list(range(NCH)) if d_ == 0 else [1, 0] + list(range(NCH - 1, 1, -1))
        Uo = C_UF if d_ == 0 else C_UB
        Lo = C_LF if d_ == 0 else C_LB
        Mo = C_MF if d_ == 0 else C_MB
        self.memset(H[:], 0.0, [H])
        self.memset(Hbf[:], 0.0, [Hbf])
        it = 0
        c0 = colof(order[0] * 128)
        self.dma(hc[0][:], self.hv[:, :, c0 - 2:c0 + 130], [self.hT_d], [hc[0]], slow=True)
        for ci, c in enumerate(order):
            h_ = hc[it % 2]
            ypt = yp[it % 2]
            it += 1
            tok = slice(c * 128, (c + 1) * 128)
            if ci + 1 < len(order):
                cn = colof(order[ci + 1] * 128)
                self.dma(hc[it % 2][:], self.hv[:, :, cn - 2:cn + 130], [self.hT_d], [hc[it % 2]], slow=True)
            for r in range(12):
                q_ = Q[r // 3]
                o_ = q_[:, (r % 3) * 132:(r % 3) * 132 + 132]
                for kc in range(8):
                    self.mm(o_, wx[:, kc, r * 128:(r + 1) * 128], h_[:, kc, :], kc == 0, kc == 7, [wx, h_], [q_])
                if r % 3 == 2:
                    yield
            for q in range(4):
                self.act(xbc[:, 3 * q:3 * q + 3, :], Q[q][:, 0:396].rearrange("p (a b) -> p a b", a=3), AF.Copy, [Q[q]], [xbc])
            yield
            for kc in range(8):
                self.mm(Q[3][:, 0:16], h_[:, kc, 2:130], wdt[:, kc, :], kc == 0, kc == 7, [h_, wdt], [Q[3]])
            yield
            for r in range(12):
                q_ = Q[r // 4]
                o_ = q_[:, (r % 4) * 128:(r % 4 + 1) * 128]
                for k in range(5):
                    self.mm(o_, dg[:, r * 5 + k, :], xbc[:, r, k:k + 128], k == 0, False, [dg, xbc], [q_])
                self.mm(o_, cbrow[0:1, r * 128:(r + 1) * 128], self.ones[0:1, 0:128], False, True, [cbrow, self.ones], [q_])
                if r % 4 == 3:
                    yield
            self.tt(dtx[:], Q[3][:, 0:16], dtb[:], ALU.add, [Q[3], dtb], [dtx])
            yield
            self.act(dtx[:], dtx[:], AF.Exp, [dtx], [dtx])
            for q in range(3):
                self.act(esb[:, q * 512:(q + 1) * 512], Q[q][:], AF.Exp, [Q[q]], [esb], scale=-1.0)
            yield
            self.act(dt[:], dtx[:], AF.Ln, [dtx], [dt], bias=1.0)
            self.sigm(esb[:, 0:1536], esb)
            yield
            self.tt(la[:], dt[:], al[:], ALU.mult, [dt, al], [la])
            for q in range(3):
                self.tt(ubf[:, 4 * q:4 * q + 4, :], Q[q][:].rearrange("p (a b) -> p a b", a=4),
                        esb[:, q * 512:(q + 1) * 512].rearrange("p (a b) -> p a b", a=4), ALU.mult, [Q[q], esb], [ubf])
            yield
            self.mm(Q[3][:, 16:32], cst[:, Uo:Uo + 128], la[:], True, True, [cst, la], [Q[3]])
            self.mm(Q[3][:, 32:48], cst[:, Lo:Lo + 128], la[:], True, True, [cst, la], [Q[3]])
            self.mm(Q[3][:, 48:64], self.ones[:], la[:], True, True, [self.ones, la], [Q[3]])
            for g in range(2):
                self.mm(Q[3][:, 256 + g * 128:256 + (g + 1) * 128], ubf[:, 8 + g, :], ubf[:, 10 + g, :], True, True, [ubf], [Q[3]])
            self.tt(LaL[:], la[:].unsqueeze(2).to_broadcast([128, 16, 128]),
                    cst[:, Lo:Lo + 128].unsqueeze(1).to_broadcast([128, 16, 128]), ALU.mult, [la, cst], [LaL], eng=POOL)
            yield
            for r in range(8):
                self.tr(Qb0[:, r * 128:(r + 1) * 128], ubf[:, r, :], self.identb[:], [ubf, self.identb], [Q[0]])
            for r in range(2):
                self.tr(Qb1[:, r * 128:(r + 1) * 128], ubf[:, 8 + r, :], self.identb[:], [ubf, self.identb], [Q[1]])
            self.act(E3[:], Q[3][:, 16:64], AF.Exp, [Q[3]], [E3])
            yield
            self.tt(scm[:], Q[3][:, 256:512].rearrange("p (a b) -> p a b", a=2),
                    cst[:, Mo:Mo + 128].unsqueeze(1).to_broadcast([128, 2, 128]), ALU.mult, [Q[3], cst], [scm])
            self.act(xs_tok[:], Qb0[:, 0:1024], AF.Copy, [Q[0]], [xs_tok])
            self.cp(B_tok[:], Qb1[:, 0:256], [Q[1]], [B_tok])
            yield
            for h in range(16):
                q_ = Q[h // 4]
                self.mm(q_[:, (h % 4) * 128:(h % 4 + 1) * 128], LaL[:, h, :], cst[:, Uo:Uo + 128], True, True, [LaL, cst], [q_])
                if h % 4 == 3:
                    yield
            self.tt(v[:].rearrange("p (h e) -> p h e", h=16), xs_tok[:].rearrange("p (h e) -> p h e", h=16),
                    dt[:].unsqueeze(2).to_broadcast([128, 16, 64]), ALU.mult, [xs_tok, dt], [v], eng=POOL)
            for q in range(4):
                self.act(esb[:, q * 512:(q + 1) * 512], Q[q][:], AF.Exp, [Q[q]], [esb])
            yield
            self.tt(vte[:].rearrange("p (h e) -> p h e", h=16), v[:].rearrange("p (h e) -> p h e", h=16),
                    E3[:, 16:32].unsqueeze(2).to_broadcast([128, 16, 64]), ALU.mult, [v, E3], [vte], eng=POOL)
            self.tt(M[:].rearrange("p (g a) b -> p g a b", g=2), esb[:].rearrange("p (g a b) -> p g a b", g=2, a=8),
                    scm[:].unsqueeze(2).to_broadcast([128, 2, 8, 128]), ALU.mult, [esb, scm], [M])
            yield
            for g in range(2):
                self.mm(Q[2 + g][:], ubf[:, 10 + g, :], Hbf[:, g * 512:(g + 1) * 512], True, True, [ubf, Hbf], [Q[2 + g]])
            for h in range(16):
                q_ = Q[h // 8]
                self.mm(q_[:, (h % 8) * 64:(h % 8 + 1) * 64], M[:, h, :], v[:, h * 64:(h + 1) * 64], True, True, [M, v], [q_])
                if h % 8 == 7:
                    yield
            for g in range(2):
                self.tt(t1[:, g * 512:(g + 1) * 512].rearrange("p (h e) -> p h e", h=8),
                        Q[2 + g][:].rearrange("p (h e) -> p h e", h=8),
                        E3[:, g * 8:(g + 1) * 8].unsqueeze(2).to_broadcast([128, 8, 64]), ALU.mult, [Q[2 + g], E3], [t1])
            yield
            for g in range(2):
                self.tt(t2[:, g * 512:(g + 1) * 512], Q[g][:], t1[:, g * 512:(g + 1) * 512], ALU.add, [Q[g], t1], [t2])
            yield
            for g in range(2):
                self.mm(Q[g][:], B_tok[:, g * 128:(g + 1) * 128], vte[:, g * 512:(g + 1) * 512], True, True, [B_tok, vte], [Q[g]])
            if d_ == 0:
                self.tt(t1[:].rearrange("p (h e) -> p h e", h=16), xs_tok[:].rearrange("p (h e) -> p h e", h=16),
                        Dsk[:].unsqueeze(2).to_broadcast([128, 16, 64]), ALU.mult, [xs_tok, Dsk], [t1], eng=POOL)
                self.tt(ypt[:], t1[:], t2[:], ALU.add, [t1, t2], [ypt], eng=POOL)
                self.dma(ydst[tok, :], ypt[:], [ypt], [ydst])
            else:
                self.cp(ypt[:], t2[:], [t2], [ypt], eng=POOL)
                self.dma(ydst[tok, :], ypt[:], [ypt], [ydst])
            self.tt(H[:].rearrange("p (h e) -> p h e", h=16), H[:].rearrange("p (h e) -> p h e", h=16),
                    E3[:, 32:48].unsqueeze(2).to_broadcast([128, 16, 64]), ALU.mult, [H, E3], [H])
            yield
            for g in range(2):
                self.tt(H[:, g * 512:(g + 1) * 512], H[:, g * 512:(g + 1) * 512], Q[g][:], ALU.add, [H, Q[g]], [H])
            yield
            self.act(Hbf[:], H[:], AF.Copy, [H], [Hbf])
            yield

    def stage_R(self, l):
        I = self.I
        cst = self.cst
        import os as _os
        with ExitStack() as st:
            wv = I["w_in"][l].rearrange("(k p) c -> p k c", p=128)
            wqk = self.sb(st, [128, 8, 1024], BF16, "wqk")
            wvv = self.sb(st, [128, 8, 512], BF16, "wvr")
            for kc in range(8):
                self.load_w(wqk[:, kc, 0:512], wqk, wv[:, kc, 3760:4272], I["w_in"])
                self.load_w(wqk[:, kc, 512:1024], wqk, wv[:, kc, 5328:5840], I["w_in"])
                self.load_w(wvv[:, kc, :], wvv, wv[:, kc, 4272:4784], I["w_in"])
            prm = {}
            for d_, sfx in ((0, "f"), (1, "b")):
                nm = "ret_log_rate_" + sfx
                lgB = self.bvec(st, nm, l, 4)
                self.act(lgB[:], lgB[:], AF.Exp, [lgB], [lgB])
                self.ts(lgB[:], lgB[:], -1.0, None, ALU.mult, None, [lgB], [lgB])
                lgs = self.sb(st, [128, 2], F32, "lgs")
                src = I[nm][l:l + 1, :].rearrange("o (p two) -> o p two", two=2)
                self.dma(lgs[0:64, :], src[:, :, 0].to_broadcast([64, 2]), [I[nm]], [lgs], slow=True)
                self.dma(lgs[64:128, :], src[:, :, 1].to_broadcast([64, 2]), [I[nm]], [lgs], slow=True)
                self.act(lgs[:], lgs[:], AF.Exp, [lgs], [lgs])
                self.ts(lgs[:], lgs[:], -1.0, None, ALU.mult, None, [lgs], [lgs])
                RIo = C_RIF if d_ == 0 else C_RIB
                Mo = C_MF if d_ == 0 else C_MB
                Go = C_GF if d_ == 0 else C_GB
                To = C_TEF if d_ == 0 else C_TEB
                DmT = self.sb(st, [128, 4, 128], F32, "DmT")
                for h in range(4):
                    self.act(DmT[:, h, :], cst[:, RIo:RIo + 128], AF.Exp, [cst, lgB], [DmT], scale=lgB[:, h:h + 1])
                self.tt(DmT[:], DmT[:], cst[:, Mo:Mo + 128].unsqueeze(1).to_broadcast([128, 4, 128]), ALU.mult, [DmT, cst], [DmT])
                Gam = self.sb(st, [128, 2, 128], F32, "Gam")
                for p in range(2):
                    self.act(Gam[:, p, :], cst[:, Go:Go + 128], AF.Exp, [cst, lgs], [Gam], scale=lgs[:, p:p + 1])
                te = self.sb(st, [128, 4], F32, "te")
                self.act(te[:], lgB[:], AF.Exp, [lgB, cst], [te], scale=cst[:, To:To + 1])
                g128 = self.sb(st, [128, 2], F32, "g128")
                self.act(g128[:], lgs[:], AF.Exp, [lgs], [g128], scale=128.0)
                prm[d_] = (DmT, Gam, te, g128)
            with ExitStack() as st2:
                gens = []
                for d_ in (0, 1):
                    gens.append((lambda dd: (lambda slot: self.ret_sweep(l, dd, st2, wqk, wvv, prm[dd])))(d_))
                self.interleave(gens, 2)
            self.P.barrier()
            with ExitStack() as st3:
                wg = self.sb(st3, [128, 8, 512], BF16, "wg")
                for kc in range(8):
                    self.load_w(wg[:, kc, :], wg, wv[:, kc, 4784:5296], I["w_in"])
                sets = []
                for s in range(2):
                    B = {}
                    B["hc"] = self.sb(st3, [128, 8, 128], BF16, "hcr3")
                    B["rpf"] = self.sb(st3, [128, 512], F32, "rpf")
                    B["rpb"] = self.sb(st3, [128, 512], F32, "rpb")
                    B["ge"] = self.sb(st3, [128, 512], F32, "ge3")
                    B["sg"] = self.sb(st3, [128, 512], F32, "sg3")
                    B["stats"] = self.sb(st3, [128, 4, 6], F32, "stats3")
                    B["mv"] = self.sb(st3, [128, 4, 2], F32, "mv3")
                    B["yn"] = self.sb(st3, [128, 512], F32, "yn3")
                    B["ob"] = self.sb(st3, [128, 512], BF16, "obr3")
                    B["oT"] = self.sb(st3, [128, 4, 128], BF16, "oTr3")
                    B["PG"] = self.ps(st3, [128, 512], F32, "PGr3")
                    B["PT"] = self.ps(st3, [128, 1024], BF16, "PTr3")
                    sets.append(B)
                gens = [(lambda cc: (lambda slot: self.ret_final(cc, sets[slot], wg)))(c) for c in range(NCH)]
                self.interleave(gens, 2)

    def ret_final(self, c, B, wg):
        h_ = B["hc"]; rpf = B["rpf"]; rpb = B["rpb"]; ge = B["ge"]; sg = B["sg"]; stats = B["stats"]; mv = B["mv"]
        yn = B["yn"]; ob = B["ob"]; oT_ = B["oT"]; PG = B["PG"]; PT = B["PT"]
        mixv = self.mixT_d[:].rearrange("(r p) t -> p r t", p=128)
        c0 = colof(c * 128)
        tok = slice(c * 128, (c + 1) * 128)
        self.dma(h_[:], self.hv[:, :, c0:c0 + 128], [self.hT_d], [h_], slow=True)
        self.dma(rpf[:], self.rpart_d[tok, :], [self.rpart_d], [rpf])
        self.dma(rpb[:], self.rpartb_d[tok, :], [self.rpartb_d], [rpb])
        yield
        for kc in range(8):
            self.mm(PG[:], h_[:, kc, :], wg[:, kc, :], kc == 0, kc == 7, [h_, wg], [PG])
        self.tt(rpf[:], rpf[:], rpb[:], ALU.add, [rpf, rpb], [rpf])
        yield
        self.act(ge[:], PG[:], AF.Exp, [PG], [ge], scale=-1.0)
        for h in range(4):
            self.P.op(DVE, (lambda hh: (lambda e: e.bn_stats(out=stats[:, hh, :], in_=rpf[:, hh * 128:(hh + 1) * 128])))(h),
                      [rpf.b], [stats.b])
            self.P.op(DVE, (lambda hh: (lambda e: e.bn_aggr(out=mv[:, hh, :], in_=stats[:, hh, :])))(h),
                      [stats.b], [mv.b])
        self.ts(mv[:, :, 1], mv[:, :, 1], LN_EPS, None, ALU.add, None, [mv], [mv])
        yield
        self.rsqrt(mv[:, :, 1], mv)
        self.sigm(ge[:], ge)
        yield
        self.tt(sg[:], PG[:], ge[:], ALU.mult, [PG, ge], [sg])
        for h in range(4):
            self.ts(yn[:, h * 128:(h + 1) * 128], rpf[:, h * 128:(h + 1) * 128], mv[:, h, 0:1], mv[:, h, 1:2],
                    ALU.subtract, ALU.mult, [rpf, mv], [yn])
        yield
        self.tt(ob[:], yn[:], sg[:], ALU.mult, [yn, sg], [ob])
        yield
        for h in range(4):
            self.tr(PT[:, h * 128:(h + 1) * 128], ob[:, h * 128:(h + 1) * 128], self.identb[:], [ob, self.identb], [PT])
        yield
        self.act(oT_[:], PT[:, 0:512].rearrange("p (a b) -> p a b", a=4), AF.Copy, [PT], [oT_])
        yield
        self.dma(mixv[:, 12:16, tok], oT_[:], [oT_], [self.mixT_d], slow=True)
        yield

    def ret_sweep(self, l, d_, st, wqk, wvv, prm):
        I = self.I
        DmT, Gam, te, g128 = prm
        S = self.sb(st, [128, 2, 128], F32, "S")
        Sbf = self.sb(st, [128, 2, 128], BF16, "Sbf")
        hc = [self.sb(st, [128, 8, 128], BF16, "hcr") for _ in range(2)]
        tab = [self.sb(st, [128, 4, 128], F32, "tab") for _ in range(2)]
        r1 = self.sb(st, [128, 4, 128], F32, "r1")
        r2 = self.sb(st, [128, 4, 128], F32, "r2")
        qk = self.sb(st, [128, 4, 128], BF16, "qk")
        qz = [self.sb(st, [128, 2, 128], BF16, "qz") for _ in range(2)]
        qdz = [self.sb(st, [128, 2, 128], BF16, "qdz") for _ in range(2)]
        for par in range(2):
            self.memset(qz[par][:], 0.0, [qz[par]])
            self.memset(qdz[par][:], 0.0, [qdz[par]])
        k_tok = self.sb(st, [128, 256], BF16, "k_tok")
        vb = self.sb(st, [128, 512], BF16, "vb")
        vte = self.sb(st, [128, 512], BF16, "vte")
        Mr = self.sb(st, [128, 4, 128], BF16, "Mr")
        yp = [self.sb(st, [128, 512], F32, "ypr") for _ in range(2)]
        Q = [self.ps(st, [128, 512], F32, "QR") for _ in range(4)]
        Qb3 = Q[3][:].bitcast(BF16)
        ydst = self.rpart_d if d_ == 0 else self.rpartb_d
        order = list(range(NCH)) if d_ == 0 else [1, 0] + list(range(NCH - 1, 1, -1))
        self.memset(S[:], 0.0, [S])
        self.memset(Sbf[:], 0.0, [Sbf])
        it = 0

        def loads(ci, slot):
            c = order[ci]
            c0 = colof(c * 128)
            self.dma(hc[slot][:], self.hv[:, :, c0:c0 + 128], [self.hT_d], [hc[slot]], slow=True)
            self.dma(tab[slot][:], I["ret_tab"][:, :, c * 128:(c + 1) * 128], [I["ret_tab"]], [tab[slot]], slow=True)
        loads(0, 0)
        for ci, c in enumerate(order):
            h_ = hc[it % 2]; tb = tab[it % 2]; ypt = yp[it % 2]
            it += 1
            tok = slice(c * 128, (c + 1) * 128)
            if ci + 1 < len(order):
                loads(ci + 1, it % 2)
            for rc in range(8):
                q_ = Q[rc // 4]
                for kc in range(8):
                    self.mm(q_[:, (rc % 4) * 128:(rc % 4 + 1) * 128], wqk[:, kc, rc * 128:(rc + 1) * 128], h_[:, kc, :],
                            kc == 0, kc == 7, [wqk, h_], [q_])
                if rc % 2 == 1:
                    yield
            for kc in range(8):
                self.mm(Q[2][:], h_[:, kc, :], wvv[:, kc, :], kc == 0, kc == 7, [h_, wvv], [Q[2]])
            yield
            Q0v = Q[0][:].rearrange("p (a b) -> p a b", a=4)
            Q1v = Q[1][:].rearrange("p (a b) -> p a b", a=4)
            for half, (ci_, si_) in enumerate(((0, 1), (2, 3))):
                self.tt(r1[:, 2 * half:2 * half + 2, :], Q0v[:, 2 * half:2 * half + 2, :],
                        tb[:, ci_, :].unsqueeze(1).to_broadcast([128, 2, 128]), ALU.mult, [Q[0], tb], [r1])
                self.tt(r2[:, 2 * half:2 * half + 2, :], Q1v[:, 2 * half:2 * half + 2, :],
                        tb[:, si_, :].unsqueeze(1).to_broadcast([128, 2, 128]), ALU.mult, [Q[1], tb], [r2])
            yield
            self.act(vb[:], Q[2][:], AF.Copy, [Q[2]], [vb])
            self.tt(qk[:], r1[:], r2[:], ALU.add, [r1, r2], [qk], eng=POOL)
            yield
            self.tt(vte[:].rearrange("p (h e) -> p h e", h=4), vb[:].rearrange("p (h e) -> p h e", h=4),
                    te[:].unsqueeze(2).to_broadcast([128, 4, 128]), ALU.mult, [vb, te], [vte], eng=POOL)
            for par in range(2):
                rr = 64 * par
                self.cp(qz[par][rr:rr + 64, :, :], qk[rr:rr + 64, 0:2, :], [qk], [qz[par]], eng=POOL)
                self.tt(qdz[par][rr:rr + 64, :, :], qk[rr:rr + 64, 0:2, :], Gam[rr:rr + 64, :, :], ALU.mult,
                        [qk, Gam], [qdz[par]], eng=POOL)
            for p in range(2):
                self.tr(Qb3[:, p * 128:(p + 1) * 128], qk[:, 2 + p, :], self.identb[:], [qk, self.identb], [Q[3]])
            yield
            self.cp(k_tok[:], Qb3[:, 0:256], [Q[3]], [k_tok])
            yield
            for h in range(4):
                p = h // 2
                self.mm(Q[3][:, h * 128:(h + 1) * 128], qk[:, 2 + p, :], qz[h % 2][:, p, :], True, True, [qk, qz[h % 2]], [Q[3]])
            yield
            self.tt(Mr[:], Q[3][:].rearrange("p (a b) -> p a b", a=4), DmT[:], ALU.mult, [Q[3], DmT], [Mr])
            yield
            for h in range(4):
                p = h // 2
                self.mm(Q[0][:, h * 128:(h + 1) * 128], Mr[:, h, :], vb[:, h * 128:(h + 1) * 128], True, False, [Mr, vb], [Q[0]])
                self.mm(Q[0][:, h * 128:(h + 1) * 128], qdz[h % 2][:, p, :], Sbf[:, p, :], False, True, [qdz[h % 2], Sbf], [Q[0]])
            for h in range(4):
                p = h // 2
                self.mm(Q[1][:, h * 128:(h + 1) * 128], k_tok[:, p * 128:(p + 1) * 128], vte[:, h * 128:(h + 1) * 128],
                        True, True, [k_tok, vte], [Q[1]])
            yield
            self.cp(ypt[:], Q[0][:], [Q[0]], [ypt])
            self.dma(ydst[tok, :], ypt[:], [ypt], [ydst])
            for h in range(4):
                p, r0 = h // 2, (h % 2) * 64
                self.stt(S[r0:r0 + 64, p, :], S[r0:r0 + 64, p, :], g128[r0:r0 + 64, p:p + 1], Q[1][r0:r0 + 64, h * 128:(h + 1) * 128],
                         ALU.mult, ALU.add, [S, g128, Q[1]], [S])
            yield
            self.act(Sbf[:], S[:], AF.Copy, [S], [Sbf])
            yield

    def stage_M(self, l):
        I = self.I
        cst = self.cst
        blocks = [(0, 256)] + [(256 + 512 * i, 512) for i in range(8)]
        with ExitStack() as st1:
            v_all = self.sb(st1, [128, NCH, 512], BF16, "v_all")
            with ExitStack() as st:
                wv = I["w_in"][l].rearrange("(k p) c -> p k c", p=128)
                wm = self.sb(st, [128, 8, 704], BF16, "wm")
                wgate = self.sb(st, [128, 8, 512], BF16, "wgate")
                self.load_w(wm[:, :, 0:672], wm, wv[:, :, 2576:3248], I["w_in"])
                self.load_w(wm[:, :, 672:704], wm, wv[:, :, 5296:5328], I["w_in"])
                self.load_w(wgate[:], wgate, wv[:, :, 3248:3760], I["w_in"])
                wuq = self.sb(st, [128, 3, 8, 96], BF16, "wuq")
                wuqs = self.sb(st, [128, 3, 8, 96], BF16, "wuqs")
                wkp = self.sb(st, [128, 2, 8, 96], BF16, "wkp")
                wvv = self.sb(st, [128, 2, 8, 64], BF16, "wvv")
                self.memset(wuqs[:], 0.0, [wuqs])
                self.memset(wkp[:], 0.0, [wkp])
                uqv = I["mla_w_uq"][l].rearrange("(k p) (h e) -> p k h e", p=128, h=8)
                uqs = I["w_uq_sw"][l].rearrange("(k p) (h e) -> p k h e", p=128, h=8)
                ukv = I["mla_w_ukv"][l].rearrange("(k p) (h e) -> p k h e", p=128, h=8)
                for kc in range(3):
                    self.load_w(wuq[:, kc, :, :], wuq, uqv[:, kc, :, :], I["mla_w_uq"])
                    self.load_w(wuqs[:, kc, :, 64:96], wuqs, uqs[:, kc, :, :], I["w_uq_sw"])
                for kc in range(2):
                    self.load_w(wkp[:, kc, :, 0:64], wkp, ukv[:, kc, :, 0:64], I["mla_w_ukv"])
                    self.load_w(wvv[:, kc, :, :], wvv, ukv[:, kc, :, 64:128], I["mla_w_ukv"])
                esel = self.sb(st, [32, 96], BF16, "esel")
                self.memset(esel[:], 0.0, [esel])
                self.cp(esel[:, 64:96], self.identb[0:32, 0:32], [self.identb, esel], [esel])
                qn = self.sb(st, [128, 3], F32, "qn")
                kvn = self.sb(st, [128, 2], F32, "kvn")
                self.dma(qn[:], I["mla_q_norm"][l].rearrange("(k p) -> p k", p=128), [I["mla_q_norm"]], [qn], slow=True)
                self.dma(kvn[:], I["mla_kv_norm"][l].rearrange("(k p) -> p k", p=128), [I["mla_kv_norm"]], [kvn], slow=True)
                hb = [self.sb(st, [128, 8, 512], BF16, "hb") for _ in range(2)]
                tq = self.sb(st, [96, 2, 512], F32, "tq")
                self.memset(tq[0:64, 0, :], 1.0, [tq])
                self.memset(tq[0:64, 1, :], 0.0, [tq])
                tk = self.sb(st, [32, 2, 512], F32, "tk")
                cqs = self.sb(st, [128, 3, 512], F32, "cqs")
                sqs = self.sb(st, [128, 512], F32, "sqs")
                rstd = self.sb(st, [128, 512], F32, "rstd")
                cqn = self.sb(st, [128, 3, 512], BF16, "cqn")
                ckvn = self.sb(st, [128, 2, 512], BF16, "ckvn")
                kr1 = self.sb(st, [32, 512], F32, "kr1")
                kr2 = self.sb(st, [32, 512], F32, "kr2")
                krr = self.sb(st, [32, 512], BF16, "krr")
                q1 = self.sb(st, [96, 512], F32, "q1")
                q2 = self.sb(st, [96, 512], F32, "q2")
                qf = [self.sb(st, [96, 512], BF16, "qf") for _ in range(2)]
                kf = [self.sb(st, [96, 512], BF16, "kf") for _ in range(2)]
                ge = self.sb(st, [128, 512], F32, "ge")
                sgo = [self.sb(st, [128, 512], F32, "sgo") for _ in range(2)]
                P0 = [self.ps(st, [128, 512], F32, "P0") for _ in range(2)]
                PSS = self.ps(st, [128, 512], F32, "PSS")
                PQ1 = self.ps(st, [128, 512], F32, "PQ1")
                PQ2 = self.ps(st, [128, 512], F32, "PQ2")
                PK = self.ps(st, [128, 512], F32, "PK")
                PVv = self.ps(st, [128, 512], F32, "PVv")
                PG = self.ps(st, [128, 512], F32, "PG")
                sgv = self.sgT_d[:].rearrange("(r p) t -> p r t", p=128)
                ctr = 0
                for bi, (t0, n) in enumerate(blocks):
                    h_ = hb[bi % 2]
                    c0 = colof(t0)
                    self.dma(h_[:, :, 0:n], self.hv[:, :, c0:c0 + n], [self.hT_d], [h_], slow=True)
                    self.dma(tq[64:96, :, 0:n], I["mla_tab"][:, :, t0:t0 + n], [I["mla_tab"]], [tq], slow=True)
                    self.dma(tk[:, :, 0:n], I["mla_tab"][:, :, t0:t0 + n], [I["mla_tab"]], [tk], slow=True)
                    for (nrc, off, dst, nrm, dim, keep) in ((3, 0, cqn, qn, 384.0, None), (2, 384, ckvn, kvn, 256.0, None)):
                        for rc in range(nrc):
                            p_ = P0[ctr % 2]; ctr += 1
                            for kc in range(8):
                                self.mm(p_[:, 0:n], wm[:, kc, off + rc * 128:off + (rc + 1) * 128], h_[:, kc, 0:n], kc == 0, kc == 7, [wm, h_], [p_])
                            self.act(cqs[:, rc, 0:n], p_[:, 0:n], AF.Copy, [p_], [cqs])
                            self.act(sqs[:, 0:n], p_[:, 0:n], AF.Square, [p_], [sqs])
                            self.mm(PSS[:, 0:n], self.ones[:], sqs[:, 0:n], rc == 0, rc == nrc - 1, [self.ones, sqs], [PSS])
                        self.ts(rstd[:, 0:n], PSS[:, 0:n], 1.0 / dim, RMS_EPS, ALU.mult, ALU.add, [PSS], [rstd])
                        self.rsqrt(rstd[:, 0:n], rstd)
                        for rc in range(nrc):
                            self.stt(dst[:, rc, 0:n], cqs[:, rc, 0:n], nrm[:, rc:rc + 1], rstd[:, 0:n], ALU.mult, ALU.mult, [cqs, nrm, rstd], [dst])
                    self.mm_group_kr(h_, n, wm, PQ1, PQ2)
                    self.tt(kr1[:, 0:n], PQ1[0:32, 0:n], tk[:, 0, 0:n], ALU.mult, [PQ1, tk], [kr1])
                    self.tt(kr2[:, 0:n], PQ2[0:32, 0:n], tk[:, 1, 0:n], ALU.mult, [PQ2, tk], [kr2])
                    self.tt(krr[:, 0:n], kr1[:, 0:n], kr2[:, 0:n], ALU.add, [kr1, kr2], [krr])
                    for h in range(8):
                        qf_ = qf[h % 2]; kf_ = kf[h % 2]
                        for kc in range(3):
                            self.mm(PQ1[0:96, 0:n], wuq[:, kc, h, :], cqn[:, kc, 0:n], kc == 0, kc == 2, [wuq, cqn], [PQ1])
                        for kc in range(3):
                            self.mm(PQ2[0:96, 0:n], wuqs[:, kc, h, :], cqn[:, kc, 0:n], kc == 0, kc == 2, [wuqs, cqn], [PQ2])
                        self.tt(q1[:, 0:n], PQ1[0:96, 0:n], tq[:, 0, 0:n], ALU.mult, [PQ1, tq], [q1])
                        self.tt(q2[:, 0:n], PQ2[0:96, 0:n], tq[:, 1, 0:n], ALU.mult, [PQ2, tq], [q2])
                        self.tt(qf_[:, 0:n], q1[:, 0:n], q2[:, 0:n], ALU.add, [q1, q2], [qf_], eng=POOL)
                        self.dma(self.qT_d[h, :, t0:t0 + n], qf_[:, 0:n], [qf_], [self.qT_d])
                        for kc in range(2):
                            self.mm(PK[0:96, 0:n], wkp[:, kc, h, :], ckvn[:, kc, 0:n], kc == 0, False, [wkp, ckvn], [PK])
                        self.mm(PK[0:96, 0:n], esel[:], krr[:, 0:n], False, True, [esel, krr], [PK])
                        self.act(kf_[:, 0:n], PK[0:96, 0:n], AF.Copy, [PK], [kf_])
                        self.dma(self.kfT_d[h, :, t0:t0 + n], kf_[:, 0:n], [kf_], [self.kfT_d])
                    for s in range(n // 128):
                        ch = (t0 + s * 128) // 128
                        for kc in range(2):
                            self.mm(PVv[:], ckvn[:, kc, s * 128:(s + 1) * 128], wvv[:, kc, :, :].rearrange("p h e -> p (h e)"),
                                    kc == 0, kc == 1, [ckvn, wvv], [PVv])
                        self.act(v_all[:, ch, :], PVv[:], AF.Copy, [PVv], [v_all])
                    for rc in range(4):
                        sg_ = sgo[rc % 2]
                        for kc in range(8):
                            self.mm(PG[:, 0:n], wgate[:, kc, rc * 128:(rc + 1) * 128], h_[:, kc, 0:n], kc == 0, kc == 7, [wgate, h_], [PG])
                        self.act(ge[:, 0:n], PG[:, 0:n], AF.Exp, [PG], [ge], scale=-1.0)
                        self.sigm(ge[:, 0:n], ge)
                        self.tt(sg_[:, 0:n], PG[:, 0:n], ge[:, 0:n], ALU.mult, [PG, ge], [sg_])
                        self.dma(sgv[:, rc, t0:t0 + n], sg_[:, 0:n], [sg_], [self.sgT_d])
            self.P.barrier()
            import os as _os
            if _os.environ.get("NO_M2"):
                return
            with ExitStack() as st:
                kfh = [self.sb(st, [96, NTOK], BF16, "kfh") for _ in range(2)]
                qh = [self.sb(st, [96, NTOK], BF16, "qh") for _ in range(2)]
                vaug = [self.sb(st, [128, NCH, 128], BF16, "vaug") for _ in range(2)]
                self.memset(vaug[0][:, :, 64:128], 1.0, [vaug[0]])
                self.memset(vaug[1][:, :, 0:64], 1.0, [vaug[1]])
                pT = [self.sb(st, [128, 512], BF16, "pT") for _ in range(3)]
                sgh = [self.sb(st, [128, 512], F32, "sgh") for _ in range(2)]
                rden = self.sb(st, [128, 512], F32, "rden")
                ot = self.sb(st, [128, 512], F32, "ot")
                ob = [self.sb(st, [128, 512], BF16, "ob") for _ in range(2)]
                PSc = [self.ps(st, [128, 512], F32, "PSc") for _ in range(3)]
                PO = [self.ps(st, [128, 512], F32, "PO") for _ in range(2)]
                ci = 0
                bi_ = 0
                for h in range(int(_os.environ.get("M2_HEADS", "8"))):
                    par = h % 2
                    r0 = 64 * par
                    d0 = 64 - r0
                    kf_ = kfh[h % 2]; q_ = qh[h % 2]; va = vaug[par]
                    self.dma(kf_[:], self.kfT_d[h], [self.kfT_d], [kf_])
                    self.dma(q_[:], self.qT_d[h], [self.qT_d], [q_])
                    self.cp(va[:, :, r0:r0 + 64], v_all[:, :, h * 64:(h + 1) * 64], [v_all], [va], eng=POOL)
                    its = []
                    for (t0, n) in blocks:
                        if t0 == 0:
                            if l == DEPTH - 1:
                                continue
                            kcs = [0, 1]
                        else:
                            kcs = list(range(NCH))
                        po = PO[bi_ % 2]; sg_ = sgh[bi_ % 2]; ob_ = ob[bi_ % 2]
                        bi_ += 1
                        for i, kc in enumerate(kcs):
                            its.append((t0, n, kc, i == 0, i == len(kcs) - 1, po, sg_, ob_))
                    LOOK = 2
                    for j in range(len(its) + LOOK):
                        if j < len(its):
                            (t0, n, kc, first, last, po, sg_, ob_) = its[j]
                            if first:
                                self.dma(sg_[r0:r0 + 64, 0:n], self.sgT_d[64 * h:64 * (h + 1), t0:t0 + n], [self.sgT_d], [sg_])
                            psc = PSc[(ci + j) % 3]
                            self.mm(psc[:, 0:n], kf_[:, kc * 128:(kc + 1) * 128], q_[:, t0:t0 + n], True, True, [kf_, q_], [psc])
                        jj = j - LOOK
                        if jj >= 0:
                            (t0, n, kc, first, last, po, sg_, ob_) = its[jj]
                            psc = PSc[(ci + jj) % 3]; pt = pT[(ci + jj) % 3]
                            self.act(pt[:, 0:n], psc[:, 0:n], AF.Exp, [psc], [pt], scale=MLA_SCALE)
                            self.mm(po[:, 0:n], va[:, kc, :], pt[:, 0:n], first, last, [va, pt], [po])
                            if last:
                                self.P.op(DVE, (lambda a_, b_: (lambda e: e.reciprocal(out=a_, in_=b_)))(rden[d0:d0 + 64, 0:n], po[d0:d0 + 64, 0:n]),
                                          [po.b], [rden.b])
                                self.tt(ot[r0:r0 + 64, 0:n], po[r0:r0 + 64, 0:n], rden[d0:d0 + 64, 0:n], ALU.mult, [po, rden], [ot])
                                self.tt(ob_[r0:r0 + 64, 0:n], ot[r0:r0 + 64, 0:n], sg_[r0:r0 + 64, 0:n], ALU.mult, [ot, sg_], [ob_], eng=POOL)
                                self.dma(self.mixT_d[1024 + h * 64:1024 + (h + 1) * 64, t0:t0 + n], ob_[r0:r0 + 64, 0:n], [ob_], [self.mixT_d])
                    ci += len(its)

    def mm_group_kr(self, h_, n, wm, PQ1, PQ2):
        for kc in range(8):
            self.mm(PQ1[0:32, 0:n], wm[:, kc, 640:672], h_[:, kc, 0:n], kc == 0, kc == 7, [wm, h_], [PQ1])
        for kc in range(8):
            self.mm(PQ2[0:32, 0:n], wm[:, kc, 672:704], h_[:, kc, 0:n], kc == 0, kc == 7, [wm, h_], [PQ2])

    def stage_E(self, l):
        I = self.I
        with ExitStack() as st:
            wo = self.sb(st, [128, 16, 1024], BF16, "wo")
            wov = I["w_out"][l].rearrange("(k p) c -> p k c", p=128)
            for kc in range(16):
                self.load_w(wo[:, kc, :], wo, wov[:, kc, :], I["w_out"])
            lng = self.bvec(st, "ln_g", l, 1024)
            lnb = self.bvec(st, "ln_b", l, 1024)
            sets = []
            for s_ in range(3):
                B = {}
                B["mt"] = self.sb(st, [128, 16, 128], BF16, "mt")
                B["xt"] = self.sb(st, [128, D], F32, "xt")
                B["v1"] = self.sb(st, [128, D], F32, "v1")
                B["v2"] = self.sb(st, [128, D], F32, "v2")
                B["xo"] = self.sb(st, [128, D], F32, "xo")
                B["stats"] = self.sb(st, [128, 2, 6], F32, "stats")
                B["mv"] = self.sb(st, [128, 2], F32, "mv")
                B["PZ"] = [self.ps(st, [128, 512], F32, "PZ") for _ in range(2)]
                sets.append(B)
            tiles = list(range(NCH)) if l < DEPTH - 1 else list(range(2, NCH))
            gens = [(lambda tt_: (lambda slot: self.e_tile(l, tt_, sets[slot], wo, lng, lnb)))(t) for t in tiles]
            self.interleave(gens, 3)

    def e_tile(self, l, t, B, wo, lng, lnb):
        I = self.I
        m_ = B["mt"]; x_ = B["xt"]; v1 = B["v1"]; v2 = B["v2"]; xo_ = B["xo"]; stats = B["stats"]; mv = B["mv"]; PZ = B["PZ"]
        mixv = self.mixT_d[:].rearrange("(r p) t -> p r t", p=128)
        typ = 1 if t < 2 else 0
        tok = slice(t * 128, (t + 1) * 128)
        self.dma(m_[:], mixv[:, :, tok], [self.mixT_d], [m_], slow=True)
        if l == 0:
            src_t = I["ctx"] if t < 2 else I["x"]
            src = src_t[t * 128:(t + 1) * 128, :] if t < 2 else src_t[(t - 2) * 128:(t - 1) * 128, :]
        else:
            src_t = self.xres_d
            src = src_t[tok, :]
        self.dma(x_[:], src, [src_t], [x_])
        yield
        for nb in range(2):
            for kc in range(16):
                self.mm(PZ[nb][:], m_[:, kc, :], wo[:, kc, nb * 512:(nb + 1) * 512], kc == 0, kc == 15, [m_, wo], [PZ[nb]])
            yield
        for nb in range(2):
            self.tt(v1[:, nb * 512:(nb + 1) * 512], PZ[nb][:], self.gB[:, typ, nb * 512:(nb + 1) * 512], ALU.mult, [PZ[nb], self.gB], [v1])
        yield
        self.stt(v2[:], x_[:], ALPHA, v1[:], ALU.mult, ALU.add, [x_, v1], [v2])
        yield
        for s in range(2):
            self.P.op(DVE, (lambda ss: (lambda e: e.bn_stats(out=stats[:, ss, :], in_=v2[:, ss * 512:(ss + 1) * 512])))(s),
                      [v2.b], [stats.b])
        self.P.op(DVE, lambda e: e.bn_aggr(out=mv[:], in_=stats[:]), [stats.b], [mv.b])
        self.ts(mv[:, 1:2], mv[:, 1:2], LN_EPS, None, ALU.add, None, [mv], [mv])
        yield
        self.rsqrt(mv[:, 1:2], mv)
        yield
        self.ts(v1[:], v2[:], mv[:, 0:1], mv[:, 1:2], ALU.subtract, ALU.mult, [v2, mv], [v1])
        yield
        self.tt(v2[:], v1[:], lng[:], ALU.mult, [v1, lng], [v2], eng=POOL)
        yield
        self.tt(xo_[:], v2[:], lnb[:], ALU.add, [v2, lnb], [xo_], eng=POOL)
        yield
        if l < DEPTH - 1:
            self.dma(self.xres_d[tok, :], xo_[:], [xo_], [self.xres_d])
        else:
            self.dma(self.out[(t - 2) * 128:(t - 1) * 128, :], xo_[:], [xo_], [self.out])
        yield


C_ID = 0
C_UF = 128
C_LF = 256
C_UB = 384
C_LB = 512
C_MF = 640
C_MB = 768
C_RIF = 896
C_RIB = 1024
C_GF = 1152
C_GB = 1280
C_TEF = 1408
C_TEB = 1409
CST_W = 1410


def make_consts():
    k = np.arange(128)[:, None].astype(np.float32)
    i = np.arange(128)[None, :].astype(np.float32)
    cst = np.zeros((128, CST_W), np.float32)
    cst[:, C_ID:C_ID + 128] = (k == i)
    cst[:, C_UF:C_UF + 128] = (k <= i)
    cst[:, C_LF:C_LF + 128] = (k > i)
    cst[:, C_UB:C_UB + 128] = (k >= i)
    cst[:, C_LB:C_LB + 128] = (k < i)
    cst[:, C_MF:C_MF + 128] = (k <= i)
    cst[:, C_MB:C_MB + 128] = (k >= i)
    cst[:, C_RIF:C_RIF + 128] = np.maximum(i - k, 0)
    cst[:, C_RIB:C_RIB + 128] = np.maximum(k - i, 0)
    cst[:, C_GF:C_GF + 128] = np.broadcast_to(i + 1, (128, 128))
    cst[:, C_GB:C_GB + 128] = np.broadcast_to(128 - i, (128, 128))
    cst[:, C_TEF] = 127 - k[:, 0]
    cst[:, C_TEB] = k[:, 0]
    return cst


def rope_tables():
    rows = SEQ // 64
    t = np.arange(rows * 64)
    row = (t // 64).astype(np.float32)
    col = (t % 64).astype(np.float32)

    def cs(rot):
        nf = rot // 4
        inv = (np.float32(10000.0) ** (-np.arange(nf, dtype=np.float32) / np.float32(nf))).astype(np.float32)
        ang = np.concatenate([row[:, None] * inv, col[:, None] * inv], -1).astype(np.float32)
        return np.cos(ang).astype(np.float32), np.sin(ang).astype(np.float32)
    cm, sm = cs(32)
    mla = np.zeros((32, 2, NTOK), np.float32)
    mla[:, 0, :CTX] = 1.0
    mla[0:16, 0, CTX:] = cm.T; mla[16:32, 0, CTX:] = cm.T
    mla[0:16, 1, CTX:] = -sm.T; mla[16:32, 1, CTX:] = sm.T
    cr, sr = cs(64)
    ret = np.zeros((128, 4, NTOK), np.float32)
    ret[:, 0, :CTX] = 1.0
    for hh in range(2):
        b = hh * 64
        ret[b:b + 32, 0, CTX:] = cr.T; ret[b + 32:b + 64, 0, CTX:] = cr.T
        ret[b:b + 32, 1, CTX:] = -sr.T; ret[b + 32:b + 64, 1, CTX:] = sr.T
    ret[:, 2] = ret[:, 0] * 0.125
    ret[:, 3] = ret[:, 1] * 0.125
    return mla, ret


_CACHE = {}


def prep_inputs(inputs):
    f = lambda a: np.ascontiguousarray(np.asarray(a, dtype=np.float32))
    w_in = f(inputs["w_in"])
    kr = w_in[:, :, 3216:3248]
    kr_sw = np.concatenate([kr[:, :, 16:32], kr[:, :, 0:16]], -1)

    def sw64(a):
        a = a.reshape(2, D, 4, 2, 32)
        return np.ascontiguousarray(a[:, :, :, ::-1, :]).reshape(2, D, 256)
    q_sw = sw64(w_in[:, :, 3760:4016])
    k_sw = sw64(w_in[:, :, 4016:4272])
    w_ext = np.ascontiguousarray(np.concatenate([w_in, kr_sw, q_sw, k_sw], -1))
    uq = f(inputs["mla_w_uq"]).reshape(2, 384, 8, 96)
    uq_r = uq[:, :, :, 64:96]
    uq_sw = np.ascontiguousarray(np.concatenate([uq_r[..., 16:32], uq_r[..., 0:16]], -1)).reshape(2, 384, 256)
    mla_tab, ret_tab = rope_tables()
    shared = {
        "w_ada": f(inputs["w_ada"]), "b_ada": f(inputs["b_ada"]), "w_in": w_ext,
        "ssd_conv_w": f(inputs["ssd_conv_w"]), "ssd_conv_b": f(inputs["ssd_conv_b"]),
        "ssd_norm_w": f(inputs["ssd_norm_w"]), "mla_q_norm": f(inputs["mla_q_norm"]),
        "mla_w_uq": f(inputs["mla_w_uq"]), "w_uq_sw": uq_sw, "mla_kv_norm": f(inputs["mla_kv_norm"]),
        "mla_w_ukv": f(inputs["mla_w_ukv"]), "ret_log_rate_f": f(inputs["ret_log_rate_f"]),
        "ret_log_rate_b": f(inputs["ret_log_rate_b"]), "w_out": f(inputs["w_out"]),
        "ln_g": f(inputs["ln_g"]), "ln_b": f(inputs["ln_b"]),
        "cst": make_consts(), "mla_tab": mla_tab, "ret_tab": ret_tab,
    }
    for n in ("ssd_a_log_f", "ssd_a_log_b", "ssd_dt_bias_f", "ssd_dt_bias_b", "ssd_d"):
        shared[n] = f(inputs[n])
    x = f(inputs["x"]); c = f(inputs["c"]); ctx = f(inputs["ctx"]); c_ctx = f(inputs["c_ctx"])
    maps = []
    for b in range(8):
        m = dict(shared)
        m["x"] = x[b]
        m["ctx"] = ctx[b]
        m["cvec"] = np.ascontiguousarray(np.stack([c[b], c_ctx], 0))
        maps.append(m)
    return maps


def kernel(**inputs):
    if "nc" not in _CACHE:
        _CACHE["nc"] = K().build()
    nc = _CACHE["nc"]
    maps = prep_inputs(inputs)
    res = run_bass_kernel_spmd(nc, maps, core_ids=list(range(8)))
    return np.stack([np.asarray(r["out"], dtype=np.float32) for r in res.results], 0)
```

```python
import math
from contextlib import ExitStack
import numpy as np
import concourse.bass as bass
import concourse.mybir as mybir
from concourse.bass_utils import run_bass_kernel_spmd

F32 = mybir.dt.float32
BF16 = mybir.dt.bfloat16
AF = mybir.ActivationFunctionType
ALU = mybir.AluOpType

PE, ACT, DVE, POOL, SP = "pe", "act", "dve", "pool", "sp"
EPOCH = 30000
DMA_K = 8
DMA_EPOCH = 1800

D = 1024
SEQ = 4096
CTX = 256
NTOK = SEQ + CTX
NCH = NTOK // 128
HC = NTOK + 8
DEPTH = 2
ALPHA = (2 * DEPTH) ** 0.25
LN_EPS = 1e-5
RMS_EPS = 1e-6
MLA_SCALE = 96 ** -0.5
WEXT = 5296 + 32 + 256 + 256


def colof(t):
    return t + 2 if t < CTX else t + 6


class Buf:
    __slots__ = ("name", "lw", "rd", "psum")

    def __init__(self, name=""):
        self.name = name
        self.lw = None
        self.rd = []
        self.psum = False


class Op:
    __slots__ = ("eng", "fn", "deps", "sig", "idx", "dma_slot", "dma_prev")

    def __init__(self, eng, fn):
        self.eng = eng
        self.fn = fn
        self.deps = set()
        self.sig = None
        self.dma_slot = None
        self.dma_prev = None


class Prog:
    def __init__(self, nc):
        self.nc = nc
        self.ops = []
        self.eng = {PE: nc.tensor, ACT: nc.scalar, DVE: nc.vector, POOL: nc.gpsimd, SP: nc.sync}
        self.dma_lists = {}
        self.last = {}

    def op(self, eng, fn, reads=(), writes=(), dma=False):
        o = Op(eng, fn)
        o.idx = len(self.ops)
        for b in reads:
            if b.lw is not None:
                o.deps.add(b.lw)
            if b.psum:
                for r in b.rd:
                    if self.ops[r].eng != eng:
                        o.deps.add(r)
        for b in writes:
            if b.lw is not None:
                o.deps.add(b.lw)
            for r in b.rd:
                o.deps.add(r)
        for b in reads:
            b.rd.append(o.idx)
        for b in writes:
            b.lw = o.idx
            b.rd = []
        if dma:
            lst = self.dma_lists.setdefault(eng, [])
            o.dma_slot = len(lst)
            if len(lst) >= DMA_K:
                o.dma_prev = lst[len(lst) - DMA_K]
            lst.append(o.idx)
        o.deps.discard(o.idx)
        self.ops.append(o)
        self.last[eng] = o.idx
        return o

    def barrier(self):
        bufs = {}
        for e in (PE, ACT, DVE, POOL, SP):
            bufs[e] = Buf("bar" + e)
            o = self.op(e, lambda en: en.nop(), writes=[bufs[e]])
            for lst in self.dma_lists.values():
                for d in lst[-DMA_K:]:
                    if d != o.idx:
                        o.deps.add(d)
        for e in (PE, ACT, DVE, POOL, SP):
            self.op(e, lambda en: en.nop(), reads=list(bufs.values()))

    def emit(self, stack):
        nc = self.nc
        ops = self.ops
        needed = set()
        for o in ops:
            for d in o.deps:
                do = ops[d]
                if do.eng == o.eng and o.eng == PE and do.dma_slot is None:
                    continue
                needed.add(d)
            if o.dma_prev is not None:
                needed.add(o.dma_prev)
        cnt = {}
        sems = {}
        dma_sems = {}
        for o in ops:
            if o.dma_slot is not None:
                k = o.dma_slot % DMA_K
                n = o.dma_slot // DMA_K
                key = (o.eng, k, n // DMA_EPOCH)
                if key not in dma_sems:
                    dma_sems[key] = stack.enter_context(nc.semaphore("dq%s%d_%d" % key))
                o.sig = (dma_sems[key], 16 * (n % DMA_EPOCH + 1))
            elif o.idx in needed:
                c = cnt.get(o.eng, 0)
                key = (o.eng, c // EPOCH)
                if key not in sems:
                    sems[key] = stack.enter_context(nc.semaphore("s%s_%d" % key))
                o.sig = (sems[key], c % EPOCH + 1)
                cnt[o.eng] = c + 1
        waited = {}
        nw = 0
        for o in ops:
            e = self.eng[o.eng]
            deps = set(o.deps)
            if o.dma_prev is not None:
                deps.add(o.dma_prev)
            for d in sorted(deps):
                do = ops[d]
                if do.sig is None:
                    continue
                sem, val = do.sig
                key = (o.eng, id(sem))
                if waited.get(key, 0) >= val:
                    continue
                waited[key] = val
                e.wait_ge(sem, val)
                nw += 1
            ins = o.fn(e)
            if o.sig is not None:
                sem, val = o.sig
                ins.then_inc(sem, 16 if o.dma_slot is not None else 1)
        self.nwaits = nw


class T:
    __slots__ = ("t", "b")

    def __init__(self, t, name=""):
        self.t = t
        self.b = Buf(name)

    def __getitem__(self, k):
        return self.t[k]


class K:
    def __init__(self, debug=False, stop_after=None, skip=()):
        self.skip = set(skip)
        self.debug = debug
        self.stop_after = stop_after
        self.nc = bass.Bass("TRN2", target_bir_lowering=False)
        self.P = Prog(self.nc)
        self.uid = 0

    def dram(self, name, shape, dt, kind="Internal"):
        return T(self.nc.dram_tensor(name, list(shape), dt, kind=kind).ap(), name)

    def sb(self, st, shape, dt, name=None):
        self.uid += 1
        name = "%s_%d" % (name or "t", self.uid)
        return T(st.enter_context(self.nc.sbuf_tensor(name, list(shape), dt)), name)

    def ps(self, st, shape, dt=F32, name=None):
        self.uid += 1
        name = "%s_%d" % (name or "p", self.uid)
        t = T(st.enter_context(self.nc.psum_tensor(name, list(shape), dt)), name)
        t.b.psum = True
        return t

    def dma(self, out, in_, reads, writes, eng=SP, slow=False):
        if slow:
            return self.P.op(eng, lambda e: e.dma_start(out=out, in_=in_, allow_slow_non_contiguous=True),
                             [x.b for x in reads], [x.b for x in writes], dma=True)
        return self.P.op(eng, lambda e: e.dma_start(out=out, in_=in_), [x.b for x in reads], [x.b for x in writes], dma=True)

    def mm(self, out, lhsT, rhs, start, stop, reads, writes):
        return self.P.op(PE, lambda e: e.matmul(out, lhsT=lhsT, rhs=rhs, start=start, stop=stop),
                         [x.b for x in reads], [x.b for x in writes])

    def tr(self, out, in_, ident, reads, writes):
        return self.P.op(PE, lambda e: e.transpose(out=out, in_=in_, identity=ident),
                         [x.b for x in reads], [x.b for x in writes])

    def act(self, out, in_, func, reads, writes, bias=None, scale=None, accum_out=None):
        kw = {}
        if bias is not None:
            kw["bias"] = bias
        if scale is not None:
            kw["scale"] = scale
        if accum_out is not None:
            kw["accum_out"] = accum_out
        return self.P.op(ACT, lambda e: e.activation(out=out, in_=in_, func=func, **kw),
                         [x.b for x in reads], [x.b for x in writes])

    def tt(self, out, in0, in1, op, reads, writes, eng=DVE):
        return self.P.op(eng, lambda e: e.tensor_tensor(out=out, in0=in0, in1=in1, op=op),
                         [x.b for x in reads], [x.b for x in writes])

    def ts(self, out, in0, s1, s2, op0, op1, reads, writes, eng=DVE):
        if op1 is None:
            return self.P.op(eng, lambda e: e.tensor_scalar(out=out, in0=in0, scalar1=s1, scalar2=None, op0=op0),
                             [x.b for x in reads], [x.b for x in writes])
        return self.P.op(eng, lambda e: e.tensor_scalar(out=out, in0=in0, scalar1=s1, scalar2=s2, op0=op0, op1=op1),
                         [x.b for x in reads], [x.b for x in writes])

    def stt(self, out, in0, scalar, in1, op0, op1, reads, writes, eng=DVE):
        return self.P.op(eng, lambda e: e.scalar_tensor_tensor(out=out, in0=in0, scalar=scalar, in1=in1, op0=op0, op1=op1),
                         [x.b for x in reads], [x.b for x in writes])

    def cp(self, out, in_, reads, writes, eng=DVE):
        return self.P.op(eng, lambda e: e.tensor_copy(out=out, in_=in_), [x.b for x in reads], [x.b for x in writes])

    def memset(self, out, val, writes, eng=POOL):
        return self.P.op(eng, lambda e: e.memset(out, val), [], [x.b for x in writes])

    def sigm(self, ap, t):
        self.act(ap, ap, AF.Ln, [t], [t], bias=1.0)
        self.act(ap, ap, AF.Exp, [t], [t], scale=-1.0)

    def rsqrt(self, ap, t):
        self.act(ap, ap, AF.Ln, [t], [t])
        self.act(ap, ap, AF.Exp, [t], [t], scale=-0.5)

    def build(self):
        nc = self.nc
        dbg = self.debug
        I = {}

        def inp(name, shape):
            I[name] = self.dram(name, shape, F32, kind="ExternalInput")
        inp("x", [SEQ, D]); inp("ctx", [CTX, D]); inp("cvec", [2, D])
        inp("w_ada", [2, D, 3 * D]); inp("b_ada", [2, 3 * D]); inp("w_in", [2, D, WEXT])
        inp("ssd_conv_w", [2, 5, 1536]); inp("ssd_conv_b", [2, 1536])
        for n in ("ssd_a_log_f", "ssd_a_log_b", "ssd_dt_bias_f", "ssd_dt_bias_b", "ssd_d"):
            inp(n, [2, 16])
        inp("ssd_norm_w", [2, 1024]); inp("mla_q_norm", [2, 384]); inp("mla_w_uq", [2, 384, 768])
        inp("w_uq_sw", [2, 384, 256]); inp("mla_kv_norm", [2, 256]); inp("mla_w_ukv", [2, 256, 1024])
        inp("ret_log_rate_f", [2, 4]); inp("ret_log_rate_b", [2, 4]); inp("w_out", [2, 2048, D])
        inp("ln_g", [2, D]); inp("ln_b", [2, D])
        inp("cst", [128, CST_W]); inp("mla_tab", [32, 2, NTOK]); inp("ret_tab", [128, 4, NTOK])
        self.I = I
        okind = "ExternalOutput" if dbg else "Internal"
        self.out = self.dram("out", [SEQ, D], F32, kind="ExternalOutput")
        self.hT_d = self.dram("hT_d", [128, 8 * HC], BF16, kind=okind)
        self.ypart_d = self.dram("ypart_d", [NTOK, 1024], F32)
        self.rpart_d = self.dram("rpart_d", [NTOK, 512], F32)
        self.ypartb_d = self.dram("ypartb_d", [NTOK, 1024], F32)
        self.Fb_d = self.dram("Fb_d", [NCH, 128, 1536], BF16)
        self.Ff_d = self.dram("Ff_d", [NCH, 128, 272], F32)
        self.rpartb_d = self.dram("rpartb_d", [NTOK, 512], F32)
        self.mixT_d = self.dram("mixT_d", [2048, NTOK], BF16, kind=okind)
        self.qT_d = self.dram("qT_d", [8, 96, NTOK], BF16)
        self.kfT_d = self.dram("kfT_d", [8, 96, NTOK], BF16)
        self.sgT_d = self.dram("sgT_d", [512, NTOK], F32)
        self.xres_d = self.dram("xres_d", [NTOK, D], F32, kind=okind)

        with ExitStack() as gst:
            self.gst = gst
            self.cst = self.sb(gst, [128, CST_W], F32, "cst")
            self.dma(self.cst[:], I["cst"][:], [I["cst"]], [self.cst])
            self.identb = self.sb(gst, [128, 128], BF16, "identb")
            self.cp(self.identb[:], self.cst[:, C_ID:C_ID + 128], [self.cst], [self.identb])
            self.ones = self.sb(gst, [128, 128], F32, "ones")
            self.memset(self.ones[:], 1.0, [self.ones])
            self.gB = self.sb(gst, [128, 2, 1024], F32, "gB")
            zt = self.sb(gst, [128, 8, 4], BF16, "zt")
            self.memset(zt[:], 0.0, [zt])
            hv = self.hT_d[:].rearrange("p (k c) -> p k c", k=8)
            self.hv = hv
            for (a, b) in ((0, 2), (258, 262), (4358, 4360)):
                self.dma(hv[:, :, a:b], zt[:, :, 0:b - a], [zt], [self.hT_d], slow=True)
            self.P.barrier()
            stages = []
            for l in range(DEPTH):
                stages += [("A", l), ("S", l), ("M", l), ("R", l), ("E", l)]
            for (s, l) in stages:
                if s in self.skip:
                    continue
                if s == "A":
                    self.stage_A(l)
                elif s == "S":
                    self.stage_S(l)
                elif s == "M":
                    self.stage_M(l)
                elif s == "R":
                    self.stage_R(l)
                else:
                    self.stage_E(l)
                self.P.barrier()
                if self.stop_after == (s, l):
                    break
            self.P.barrier()
            self.P.emit(gst)
        return nc

    def silu_psum(self, st, src_ap, src_t, out_ap, out_t, e_t, e_ap, r_ap):
        self.act(e_ap, src_ap, AF.Exp, [src_t], [e_t], scale=-1.0)
        self.sigm(e_ap, e_t)
        self.tt(out_ap, src_ap, r_ap, ALU.mult, [src_t, e_t], [out_t])

    def stage_A(self, l):
        I = self.I
        with ExitStack() as st:
            wada = [self.sb(st, [128, 8, 512], F32, "wada") for _ in range(2)]
            craw = self.sb(st, [128, 8, 2], F32, "craw")
            ce = self.sb(st, [128, 8, 2], F32, "ce")
            scT = self.sb(st, [128, 8, 2], F32, "scT")
            modT = self.sb(st, [128, 24, 2], F32, "modT")
            scale1 = self.sb(st, [128, 8, 2], F32, "scale1")
            brow = self.sb(st, [1, 3 * D], F32, "brow")
            pm = self.ps(st, [128, 512], F32, "pm")
            pg = [self.ps(st, [128, 512], F32, "pg") for _ in range(2)]
            for j in range(2):
                self.dma(craw[:, :, j], I["cvec"][j].rearrange("(k p) -> p k", p=128), [I["cvec"]], [craw], slow=True)
            self.dma(brow[:], I["b_ada"][l:l + 1, :], [I["b_ada"]], [brow])
            self.act(ce[:], craw[:], AF.Exp, [craw], [ce], scale=-1.0)
            self.sigm(ce[:], ce)
            self.tt(scT[:], craw[:], ce[:], ALU.mult, [craw, ce], [scT])
            wv = I["w_ada"][l].rearrange("(k p) c -> p k c", p=128)
            for cb in range(6):
                w = wada[cb % 2]
                self.dma(w[:], wv[:, :, cb * 512:(cb + 1) * 512], [I["w_ada"]], [w])
                if cb < 4:
                    for dj in range(4):
                        j = cb * 4 + dj
                        for kc in range(8):
                            self.mm(pm[:, 2 * dj:2 * dj + 2], w[:, kc, dj * 128:(dj + 1) * 128], scT[:, kc, :],
                                    kc == 0, False, [w, scT], [pm])
                        self.mm(pm[:, 2 * dj:2 * dj + 2], brow[0:1, j * 128:(j + 1) * 128], self.ones[0:1, 0:2],
                                False, True, [brow, self.ones], [pm])
                        self.cp(modT[:, j, :], pm[:, 2 * dj:2 * dj + 2], [pm], [modT])
                else:
                    for typ in range(2):
                        p = pg[typ]
                        for kc in range(8):
                            self.mm(p[:], scT[:, kc, typ:typ + 1].to_broadcast([128, 128]), w[:, kc, :],
                                    kc == 0, False, [w, scT], [p])
                        self.mm(p[:], self.ones[0:1, 0:128], brow[0:1, cb * 512:(cb + 1) * 512], False, True,
                                [brow, self.ones], [p])
                        self.cp(self.gB[:, typ, (cb - 4) * 512:(cb - 3) * 512], p[:], [p], [self.gB])
            self.ts(scale1[:], modT[:, 8:16, :], 1.0, None, ALU.add, None, [modT], [scale1])
            xt = [self.sb(st, [128, D], F32, "xt") for _ in range(2)]
            ht = [self.sb(st, [128, 8, 128], BF16, "ht") for _ in range(2)]
            pT = [self.ps(st, [128, 1024], F32, "pT") for _ in range(2)]
            for t in range(NCH):
                typ = 1 if t < 2 else 0
                if l == 0:
                    src_t = I["ctx"] if t < 2 else I["x"]
                    src = src_t[t * 128:(t + 1) * 128, :] if t < 2 else src_t[(t - 2) * 128:(t - 1) * 128, :]
                else:
                    src_t = self.xres_d
                    src = src_t[t * 128:(t + 1) * 128, :]
                x_ = xt[t % 2]; h_ = ht[t % 2]; p_ = pT[t % 2]
                self.dma(x_[:], src, [src_t], [x_])
                for kc in range(8):
                    self.tr(p_[:, kc * 128:(kc + 1) * 128], x_[:, kc * 128:(kc + 1) * 128], self.cst[:, C_ID:C_ID + 128],
                            [x_, self.cst], [p_])
                for kc in range(8):
                    self.act(h_[:, kc, :], p_[:, kc * 128:(kc + 1) * 128], AF.Identity, [p_, scale1, modT], [h_],
                             bias=modT[:, kc, typ:typ + 1], scale=scale1[:, kc, typ:typ + 1])
                c0 = colof(t * 128)
                self.dma(self.hv[:, :, c0:c0 + 128], h_[:], [h_], [self.hT_d])

    def load_w(self, dst_ap, dst_t, src_ap, src_t):
        self.dma(dst_ap, src_ap, [src_t], [dst_t], eng=POOL, slow=True)

    def bvec(self, st, name, l, n):
        t = self.sb(st, [128, n], F32, name)
        self.dma(t[:], self.I[name][l:l + 1, :].to_broadcast([128, n]), [self.I[name]], [t], slow=True)
        return t

    def interleave(self, factories, width):
        pending = list(factories)
        active = []
        for s in range(width):
            if pending:
                active.append((s, pending.pop(0)(s)))
        while active:
            nxt = []
            for (s, g) in active:
                try:
                    next(g)
                    nxt.append((s, g))
                except StopIteration:
                    if pending:
                        nxt.append((s, pending.pop(0)(s)))
            active = nxt

    def stage_S(self, l):
        I = self.I
        cst = self.cst
        import os as _os
        with ExitStack() as st:
            wv = I["w_in"][l].rearrange("(k p) c -> p k c", p=128)
            with ExitStack() as stf:
                wx = self.sb(stf, [128, 8, 1536], BF16, "wx")
                wdt = self.sb(stf, [128, 8, 16], BF16, "wdt")
                for kc in range(8):
                    self.load_w(wx[:, kc, :], wx, wv[:, kc, 1024:2560], I["w_in"])
                self.load_w(wdt[:], wdt, wv[:, :, 2560:2576], I["w_in"])
                convw = self.sb(stf, [128, 12, 5], F32, "convw")
                for k in range(5):
                    self.dma(convw[:, :, k], I["ssd_conv_w"][l, k].rearrange("(r p) -> p r", p=128), [I["ssd_conv_w"]], [convw], slow=True)
                convb = self.sb(stf, [128, 12], F32, "convb")
                self.dma(convb[:], I["ssd_conv_b"][l].rearrange("(r p) -> p r", p=128), [I["ssd_conv_b"]], [convb], slow=True)
                dg = convw
                cbrow = convb
                sets = []
                for s_ in range(2):
                    B = {}
                    B["hc"] = self.sb(stf, [128, 8, 132], BF16, "hcf")
                    B["xbc"] = self.sb(stf, [128, 12, 132], F32, "xbc")
                    B["acc"] = self.sb(stf, [128, 12, 128], F32, "acc")
                    B["tmp"] = [self.sb(stf, [128, 12, 128], F32, "ctmp") for _ in range(2)]
                    B["esb"] = self.sb(stf, [128, 1536], F32, "esbf")
                    B["ubf"] = self.sb(stf, [128, 12, 128], BF16, "ubf")
                    B["Fb"] = self.sb(stf, [128, 1536], BF16, "Fbw")
                    B["Ff"] = self.sb(stf, [128, 272], F32, "Ffw")
                    B["Q"] = [self.ps(stf, [128, 512], F32, "QF") for _ in range(4)]
                    sets.append(B)
                gens = [(lambda cc: (lambda slot: self.ssd_front(cc, sets[slot], wx, wdt, dg, cbrow)))(c) for c in range(NCH)]
                self.interleave(gens, 2)
            self.P.barrier()
            Dsk = self.bvec(st, "ssd_d", l, 16)
            prm = {}
            for d_, sfx in ((0, "f"), (1, "b")):
                al = self.bvec(st, "ssd_a_log_" + sfx, l, 16)
                self.act(al[:], al[:], AF.Exp, [al], [al])
                self.ts(al[:], al[:], -1.0, None, ALU.mult, None, [al], [al])
                dtb = self.bvec(st, "ssd_dt_bias_" + sfx, l, 16)
                Ub = self.sb(st, [128, 128], BF16, "Ub16")
                Lb = self.sb(st, [128, 128], BF16, "Lb16")
                Uo = C_UF if d_ == 0 else C_UB
                Lo = C_LF if d_ == 0 else C_LB
                self.cp(Ub[:], cst[:, Uo:Uo + 128], [cst], [Ub])
                self.cp(Lb[:], cst[:, Lo:Lo + 128], [cst], [Lb])
                prm[d_] = (al, dtb, Ub, Lb)
            with ExitStack() as st2:
                gens = []
                for d_ in (0, 1):
                    gens.append((lambda dd: (lambda slot: self.ssd_sweep(l, dd, st2, Dsk, prm[dd])))(d_))
                self.interleave(gens, 2)
            self.P.barrier()
            with ExitStack() as st3:
                wz = self.sb(st3, [128, 8, 1024], BF16, "wz")
                for kc in range(8):
                    self.load_w(wz[:, kc, :], wz, wv[:, kc, 0:1024], I["w_in"])
                nwB = self.bvec(st3, "ssd_norm_w", l, 1024)
                sets = []
                for s in range(2):
                    B = {}
                    B["hc"] = self.sb(st3, [128, 8, 128], BF16, "hc3")
                    B["ypf"] = self.sb(st3, [128, 1024], F32, "ypf")
                    B["ypb"] = self.sb(st3, [128, 1024], F32, "ypb")
                    B["e"] = self.sb(st3, [128, 1024], F32, "e3")
                    B["t1"] = self.sb(st3, [128, 1024], F32, "t13")
                    B["bst"] = self.sb(st3, [128, 2, 6], F32, "bst3")
                    B["ssq"] = self.sb(st3, [128, 2], F32, "ssq3")
                    B["ob"] = self.sb(st3, [128, 1024], BF16, "ob3")
                    B["oT"] = self.sb(st3, [128, 8, 128], BF16, "oT3")
                    B["PZ"] = [self.ps(st3, [128, 512], F32, "PZ3") for _ in range(2)]
                    B["PT"] = self.ps(st3, [128, 1024], BF16, "PT3")
                    sets.append(B)
                gens = [(lambda cc: (lambda slot: self.ssd_final(cc, sets[slot], wz, nwB)))(c) for c in range(NCH)]
                self.interleave(gens, 2)

    def ssd_final(self, c, B, wz, nwB):
        h_ = B["hc"]; ypf = B["ypf"]; ypb = B["ypb"]; e = B["e"]; t1 = B["t1"]; bst = B["bst"]; ssq = B["ssq"]
        ob = B["ob"]; oT_ = B["oT"]; PZ = B["PZ"]; PT = B["PT"]
        mixv = self.mixT_d[:].rearrange("(r p) t -> p r t", p=128)
        c0 = colof(c * 128)
        tok = slice(c * 128, (c + 1) * 128)
        self.dma(h_[:], self.hv[:, :, c0:c0 + 128], [self.hT_d], [h_], slow=True)
        self.dma(ypf[:], self.ypart_d[tok, :], [self.ypart_d], [ypf])
        self.dma(ypb[:], self.ypartb_d[tok, :], [self.ypartb_d], [ypb])
        yield
        for n in range(2):
            for kc in range(8):
                self.mm(PZ[n][:], h_[:, kc, :], wz[:, kc, n * 512:(n + 1) * 512], kc == 0, kc == 7, [h_, wz], [PZ[n]])
        self.tt(ypf[:], ypf[:], ypb[:], ALU.add, [ypf, ypb], [ypf])
        yield
        for n in range(2):
            self.act(e[:, n * 512:(n + 1) * 512], PZ[n][:], AF.Exp, [PZ[n]], [e], scale=-1.0)
        yield
        self.sigm(e[:], e)
        yield
        for n in range(2):
            self.tt(t1[:, n * 512:(n + 1) * 512], PZ[n][:], e[:, n * 512:(n + 1) * 512], ALU.mult, [PZ[n], e], [t1])
        yield
        self.tt(t1[:], t1[:], ypf[:], ALU.mult, [t1, ypf], [t1])
        yield
        for s_ in range(2):
            self.P.op(DVE, (lambda ss: (lambda en: en.bn_stats(out=bst[:, ss, :], in_=t1[:, ss * 512:(ss + 1) * 512])))(s_),
                      [t1.b], [bst.b])
        self.P.op(DVE, lambda en: en.bn_aggr(out=ssq[:], in_=bst[:]), [bst.b], [ssq.b])
        self.stt(ssq[:, 1:2], ssq[:, 0:1], ssq[:, 0:1], ssq[:, 1:2], ALU.mult, ALU.add, [ssq], [ssq])
        self.ts(ssq[:, 1:2], ssq[:, 1:2], RMS_EPS, None, ALU.add, None, [ssq], [ssq])
        yield
        self.rsqrt(ssq[:, 1:2], ssq)
        yield
        self.stt(ob[:], t1[:], ssq[:, 1:2], nwB[:], ALU.mult, ALU.mult, [t1, ssq, nwB], [ob])
        yield
        for r in range(8):
            self.tr(PT[:, r * 128:(r + 1) * 128], ob[:, r * 128:(r + 1) * 128], self.identb[:], [ob, self.identb], [PT])
        yield
        self.act(oT_[:], PT[:].rearrange("p (a b) -> p a b", a=8), AF.Copy, [PT], [oT_])
        yield
        self.dma(mixv[:, 0:8, tok], oT_[:], [oT_], [self.mixT_d], slow=True)
        yield

    def ssd_front(self, c, B, wx, wdt, dg, cbrow):
        h_ = B["hc"]; xbc = B["xbc"]; esb = B["esb"]; ubf = B["ubf"]; Fb = B["Fb"]; Ff = B["Ff"]; Q = B["Q"]
        Qb0 = Q[0][:].bitcast(BF16)
        Qb1 = Q[1][:].bitcast(BF16)
        c0 = colof(c * 128)
        self.dma(h_[:], self.hv[:, :, c0 - 2:c0 + 130], [self.hT_d], [h_], slow=True)
        yield
        for r in range(12):
            q_ = Q[r // 3]
            o_ = q_[:, (r % 3) * 132:(r % 3) * 132 + 132]
            for kc in range(8):
                self.mm(o_, wx[:, kc, r * 128:(r + 1) * 128], h_[:, kc, :], kc == 0, kc == 7, [wx, h_], [q_])
            if r % 3 == 2:
                yield
        for q in range(4):
            self.act(xbc[:, 3 * q:3 * q + 3, :], Q[q][:, 0:396].rearrange("p (a b) -> p a b", a=3), AF.Copy, [Q[q]], [xbc])
        yield
        for kc in range(8):
            self.mm(Q[3][:, 0:16], h_[:, kc, 2:130], wdt[:, kc, :], kc == 0, kc == 7, [h_, wdt], [Q[3]])
        yield
        acc = B["acc"]; tmp = B["tmp"]
        convw = dg; convb = cbrow
        self.tt(acc[:], xbc[:, :, 0:128], convw[:, :, 0:1].to_broadcast([128, 12, 128]), ALU.mult, [xbc, convw], [acc])
        for k in range(1, 5):
            tk_ = tmp[k % 2]
            self.tt(tk_[:], xbc[:, :, k:k + 128], convw[:, :, k:k + 1].to_broadcast([128, 12, 128]), ALU.mult, [xbc, convw], [tk_], eng=POOL)
            yield
            self.tt(acc[:], acc[:], tk_[:], ALU.add, [acc, tk_], [acc])
        self.cp(Ff[:, 256:272], Q[3][:, 0:16], [Q[3]], [Ff])
        yield
        self.tt(acc[:], acc[:], convb[:].unsqueeze(2).to_broadcast([128, 12, 128]), ALU.add, [acc, convb], [acc])
        yield
        self.act(esb[:], acc[:].rearrange("p a b -> p (a b)"), AF.Exp, [acc], [esb], scale=-1.0)
        yield
        self.sigm(esb[:], esb)
        yield
        self.tt(ubf[:], acc[:], esb[:].rearrange("p (a b) -> p a b", a=12), ALU.mult, [acc, esb], [ubf])
        yield
        for g in range(2):
            self.mm(Q[3][:, 256 + g * 128:256 + (g + 1) * 128], ubf[:, 8 + g, :], ubf[:, 10 + g, :], True, True, [ubf], [Q[3]])
        for r in range(8):
            self.tr(Qb0[:, r * 128:(r + 1) * 128], ubf[:, r, :], self.identb[:], [ubf, self.identb], [Q[0]])
        for r in range(2):
            self.tr(Qb1[:, r * 128:(r + 1) * 128], ubf[:, 8 + r, :], self.identb[:], [ubf, self.identb], [Q[1]])
        self.cp(Fb[:, 1280:1536], ubf[:, 10:12, :].rearrange("p a b -> p (a b)"), [ubf], [Fb], eng=POOL)
        yield
        self.cp(Ff[:, 0:256], Q[3][:, 256:512], [Q[3]], [Ff])
        self.act(Fb[:, 0:1024], Qb0[:, 0:1024], AF.Copy, [Q[0]], [Fb])
        self.cp(Fb[:, 1024:1280], Qb1[:, 0:256], [Q[1]], [Fb])
        yield
        self.dma(self.Fb_d[c], Fb[:], [Fb], [self.Fb_d])
        self.dma(self.Ff_d[c], Ff[:], [Ff], [self.Ff_d])
        yield

    def ssd_sweep(self, l, d_, st, Dsk, prm):
        cst = self.cst
        al, dtb, Ub, Lb = prm
        H = self.sb(st, [128, 1024], F32, "H")
        Hbf = self.sb(st, [128, 1024], BF16, "Hbf")
        Fb = [self.sb(st, [128, 1536], BF16, "Fbr") for _ in range(2)]
        Ff = [self.sb(st, [128, 272], F32, "Ffr") for _ in range(2)]
        esb = self.sb(st, [128, 2048], F32, "esb")
        dtx = self.sb(st, [128, 16], F32, "dtx")
        dt = self.sb(st, [128, 16], F32, "dt")
        la = self.sb(st, [128, 16], F32, "la")
        lah = self.sb(st, [128, 16], BF16, "lah")
        lah32 = self.sb(st, [128, 16], F32, "lah32")
        lal32 = self.sb(st, [128, 16], F32, "lal32")
        lalo = self.sb(st, [128, 16], BF16, "lalo")
        E3 = self.sb(st, [128, 48], F32, "E3")
        scm = self.sb(st, [128, 2, 128], F32, "scm")
        LaUh = self.sb(st, [128, 16, 128], BF16, "LaUh")
        LaUl = self.sb(st, [128, 16, 128], BF16, "LaUl")
        M = self.sb(st, [128, 16, 128], BF16, "M")
        v = self.sb(st, [128, 1024], BF16, "v")
        vte = self.sb(st, [128, 1024], BF16, "vte")
        t1 = self.sb(st, [128, 1024], F32, "t1")
        t2 = self.sb(st, [128, 1024], F32, "t2")
        yp = [self.sb(st, [128, 1024], F32, "yp") for _ in range(2)]
        Q = [self.ps(st, [128, 512], F32, "Q") for _ in range(4)]
        ydst = self.ypart_d if d_ == 0 else self.ypartb_d
        order = list(range(NCH)) if d_ == 0 else [1, 0] + list(range(NCH - 1, 1, -1))
        Uo = C_UF if d_ == 0 else C_UB
        Lo = C_LF if d_ == 0 else C_LB
        Mo = C_MF if d_ == 0 else C_MB
        self.memset(H[:], 0.0, [H])
        self.memset(Hbf[:], 0.0, [Hbf])

        def loads(ci, slot):
            c = order[ci]
            self.dma(Fb[slot][:], self.Fb_d[c], [self.Fb_d], [Fb[slot]])
            self.dma(Ff[slot][:], self.Ff_d[c], [self.Ff_d], [Ff[slot]])
        loads(0, 0)
        it = 0
        for ci, c in enumerate(order):
            fb = Fb[it % 2]; ff = Ff[it % 2]; ypt = yp[it % 2]
            it += 1
            tok = slice(c * 128, (c + 1) * 128)
            if ci + 1 < len(order):
                loads(ci + 1, it % 2)
            xs_tok = fb[:, 0:1024]
            self.tt(dtx[:], ff[:, 256:272], dtb[:], ALU.add, [ff, dtb], [dtx])
            self.tt(scm[:], ff[:, 0:256].rearrange("p (a b) -> p a b", a=2),
                    cst[:, Mo:Mo + 128].unsqueeze(1).to_broadcast([128, 2, 128]), ALU.mult, [ff, cst], [scm])
            yield
            self.act(dtx[:], dtx[:], AF.Exp, [dtx], [dtx])
            self.act(dt[:], dtx[:], AF.Ln, [dtx], [dt], bias=1.0)
            yield
            self.tt(la[:], dt[:], al[:], ALU.mult, [dt, al], [la])
            self.cp(lah[:], la[:], [la], [lah])
            self.cp(lah32[:], lah[:], [lah], [lah32])
            self.tt(lal32[:], la[:], lah32[:], ALU.subtract, [la, lah32], [lal32])
            self.cp(lalo[:], lal32[:], [lal32], [lalo])
            yield
            self.mm(Q[3][:, 16:32], cst[:, Uo:Uo + 128], la[:], True, True, [cst, la], [Q[3]])
            self.mm(Q[3][:, 32:48], cst[:, Lo:Lo + 128], la[:], True, True, [cst, la], [Q[3]])
            self.mm(Q[3][:, 48:64], self.ones[:], la[:], True, True, [self.ones, la], [Q[3]])
            self.tt(LaUh[:], lah[:].unsqueeze(2).to_broadcast([128, 16, 128]),
                    Ub[:].unsqueeze(1).to_broadcast([128, 16, 128]), ALU.mult, [lah, Ub], [LaUh], eng=POOL)
            self.tt(LaUl[:], lalo[:].unsqueeze(2).to_broadcast([128, 16, 128]),
                    Ub[:].unsqueeze(1).to_broadcast([128, 16, 128]), ALU.mult, [lalo, Ub], [LaUl])
            self.tt(v[:].rearrange("p (h e) -> p h e", h=16), xs_tok.rearrange("p (h e) -> p h e", h=16),
                    dt[:].unsqueeze(2).to_broadcast([128, 16, 64]), ALU.mult, [fb, dt], [v], eng=POOL)
            yield
            self.act(E3[:], Q[3][:, 16:64], AF.Exp, [Q[3]], [E3])
            yield
            for q in range(4):
                self.mm(Q[q][:], Lb[:], LaUh[:, 4 * q:4 * q + 4, :].rearrange("p a b -> p (a b)"), True, False, [Lb, LaUh], [Q[q]])
                self.mm(Q[q][:], Lb[:], LaUl[:, 4 * q:4 * q + 4, :].rearrange("p a b -> p (a b)"), False, True, [Lb, LaUl], [Q[q]])
                if q % 2 == 1:
                    yield
            self.tt(vte[:].rearrange("p (h e) -> p h e", h=16), v[:].rearrange("p (h e) -> p h e", h=16),
                    E3[:, 16:32].unsqueeze(2).to_broadcast([128, 16, 64]), ALU.mult, [v, E3], [vte], eng=POOL)
            for q in range(4):
                self.act(esb[:, q * 512:(q + 1) * 512], Q[q][:], AF.Exp, [Q[q]], [esb])
            yield
            self.tt(M[:].rearrange("p (g a) b -> p g a b", g=2), esb[:].rearrange("p (g a b) -> p g a b", g=2, a=8),
                    scm[:].unsqueeze(2).to_broadcast([128, 2, 8, 128]), ALU.mult, [esb, scm], [M])
            yield
            for g in range(2):
                self.mm(Q[2 + g][:], fb[:, 1280 + g * 128:1280 + (g + 1) * 128], Hbf[:, g * 512:(g + 1) * 512], True, True, [fb, Hbf], [Q[2 + g]])
            for h in range(16):
                q_ = Q[h // 8]
                self.mm(q_[:, (h % 8) * 64:(h % 8 + 1) * 64], M[:, h, :], v[:, h * 64:(h + 1) * 64], True, True, [M, v], [q_])
                if h % 8 == 7:
                    yield
            for g in range(2):
                self.tt(t1[:, g * 512:(g + 1) * 512].rearrange("p (h e) -> p h e", h=8),
                        Q[2 + g][:].rearrange("p (h e) -> p h e", h=8),
                        E3[:, g * 8:(g + 1) * 8].unsqueeze(2).to_broadcast([128, 8, 64]), ALU.mult, [Q[2 + g], E3], [t1])
            yield
            for g in range(2):
                self.tt(t2[:, g * 512:(g + 1) * 512], Q[g][:], t1[:, g * 512:(g + 1) * 512], ALU.add, [Q[g], t1], [t2])
            yield
            for g in range(2):
                self.mm(Q[g][:], fb[:, 1024 + g * 128:1024 + (g + 1) * 128], vte[:, g * 512:(g + 1) * 512], True, True, [fb, vte], [Q[g]])
            if d_ == 0:
                self.tt(t1[:].rearrange("p (h e) -> p h e", h=16), xs_tok.rearrange("p (h e) -> p h e", h=16),
                        Dsk[:].unsqueeze(2).to_broadcast([128, 16, 64]), ALU.mult, [fb, Dsk], [t1], eng=POOL)
                self.tt(ypt[:], t1[:], t2[:], ALU.add, [t1, t2], [ypt], eng=POOL)
            else:
                self.cp(ypt[:], t2[:], [t2], [ypt], eng=POOL)
            self.dma(ydst[tok, :], ypt[:], [ypt], [ydst])
            self.tt(H[:].rearrange("p (h e) -> p h e", h=16), H[:].rearrange("p (h e) -> p h e", h=16),
                    E3[:, 32:48].unsqueeze(2).to_broadcast([128, 16, 64]), ALU.mult, [H, E3], [H])
            yield
            for g in range(2):
                self.tt(H[:, g * 512:(g + 1) * 512], H[:, g * 512:(g + 1) * 512], Q[g][:], ALU.add, [H, Q[g]], [H])
            yield
            self.act(Hbf[:], H[:], AF.Copy, [H], [Hbf])
            yield

    def stage_R(self, l):
        I = self.I
        cst = self.cst
        import os as _os
        with ExitStack() as st:
            wv = I["w_in"][l].rearrange("(k p) c -> p k c", p=128)
            wqk = self.sb(st, [128, 8, 1024], BF16, "wqk")
            wvv = self.sb(st, [128, 8, 512], BF16, "wvr")
            for kc in range(8):
                self.load_w(wqk[:, kc, 0:512], wqk, wv[:, kc, 3760:4272], I["w_in"])
                self.load_w(wqk[:, kc, 512:1024], wqk, wv[:, kc, 5328:5840], I["w_in"])
                self.load_w(wvv[:, kc, :], wvv, wv[:, kc, 4272:4784], I["w_in"])
            prm = {}
            for d_, sfx in ((0, "f"), (1, "b")):
                nm = "ret_log_rate_" + sfx
                lgB = self.bvec(st, nm, l, 4)
                self.act(lgB[:], lgB[:], AF.Exp, [lgB], [lgB])
                self.ts(lgB[:], lgB[:], -1.0, None, ALU.mult, None, [lgB], [lgB])
                lgs = self.sb(st, [128, 2], F32, "lgs")
                src = I[nm][l:l + 1, :].rearrange("o (p two) -> o p two", two=2)
                self.dma(lgs[0:64, :], src[:, :, 0].to_broadcast([64, 2]), [I[nm]], [lgs], slow=True)
                self.dma(lgs[64:128, :], src[:, :, 1].to_broadcast([64, 2]), [I[nm]], [lgs], slow=True)
                self.act(lgs[:], lgs[:], AF.Exp, [lgs], [lgs])
                self.ts(lgs[:], lgs[:], -1.0, None, ALU.mult, None, [lgs], [lgs])
                RIo = C_RIF if d_ == 0 else C_RIB
                Mo = C_MF if d_ == 0 else C_MB
                Go = C_GF if d_ == 0 else C_GB
                To = C_TEF if d_ == 0 else C_TEB
                DmT = self.sb(st, [128, 4, 128], F32, "DmT")
                for h in range(4):
                    self.act(DmT[:, h, :], cst[:, RIo:RIo + 128], AF.Exp, [cst, lgB], [DmT], scale=lgB[:, h:h + 1])
                self.tt(DmT[:], DmT[:], cst[:, Mo:Mo + 128].unsqueeze(1).to_broadcast([128, 4, 128]), ALU.mult, [DmT, cst], [DmT])
                Gam = self.sb(st, [128, 2, 128], F32, "Gam")
                for p in range(2):
                    self.act(Gam[:, p, :], cst[:, Go:Go + 128], AF.Exp, [cst, lgs], [Gam], scale=lgs[:, p:p + 1])
                te = self.sb(st, [128, 4], F32, "te")
                self.act(te[:], lgB[:], AF.Exp, [lgB, cst], [te], scale=cst[:, To:To + 1])
                g128 = self.sb(st, [128, 2], F32, "g128")
                self.act(g128[:], lgs[:], AF.Exp, [lgs], [g128], scale=128.0)
                prm[d_] = (DmT, Gam, te, g128)
            with ExitStack() as st2:
                gens = []
                for d_ in (0, 1):
                    gens.append((lambda dd: (lambda slot: self.ret_sweep(l, dd, st2, wqk, wvv, prm[dd])))(d_))
                self.interleave(gens, 2)
            self.P.barrier()
            with ExitStack() as st3:
                wg = self.sb(st3, [128, 8, 512], BF16, "wg")
                for kc in range(8):
                    self.load_w(wg[:, kc, :], wg, wv[:, kc, 4784:5296], I["w_in"])
                sets = []
                for s in range(2):
                    B = {}
                    B["hc"] = self.sb(st3, [128, 8, 128], BF16, "hcr3")
                    B["rpf"] = self.sb(st3, [128, 512], F32, "rpf")
                    B["rpb"] = self.sb(st3, [128, 512], F32, "rpb")
                    B["ge"] = self.sb(st3, [128, 512], F32, "ge3")
                    B["sg"] = self.sb(st3, [128, 512], F32, "sg3")
                    B["stats"] = self.sb(st3, [128, 4, 6], F32, "stats3")
                    B["mv"] = self.sb(st3, [128, 4, 2], F32, "mv3")
                    B["yn"] = self.sb(st3, [128, 512], F32, "yn3")
                    B["ob"] = self.sb(st3, [128, 512], BF16, "obr3")
                    B["oT"] = self.sb(st3, [128, 4, 128], BF16, "oTr3")
                    B["PG"] = self.ps(st3, [128, 512], F32, "PGr3")
                    B["PT"] = self.ps(st3, [128, 1024], BF16, "PTr3")
                    sets.append(B)
                gens = [(lambda cc: (lambda slot: self.ret_final(cc, sets[slot], wg)))(c) for c in range(NCH)]
                self.interleave(gens, 2)

    def ret_final(self, c, B, wg):
        h_ = B["hc"]; rpf = B["rpf"]; rpb = B["rpb"]; ge = B["ge"]; sg = B["sg"]; stats = B["stats"]; mv = B["mv"]
        yn = B["yn"]; ob = B["ob"]; oT_ = B["oT"]; PG = B["PG"]; PT = B["PT"]
        mixv = self.mixT_d[:].rearrange("(r p) t -> p r t", p=128)
        c0 = colof(c * 128)
        tok = slice(c * 128, (c + 1) * 128)
        self.dma(h_[:], self.hv[:, :, c0:c0 + 128], [self.hT_d], [h_], slow=True)
        self.dma(rpf[:], self.rpart_d[tok, :], [self.rpart_d], [rpf])
        self.dma(rpb[:], self.rpartb_d[tok, :], [self.rpartb_d], [rpb])
        yield
        for kc in range(8):
            self.mm(PG[:], h_[:, kc, :], wg[:, kc, :], kc == 0, kc == 7, [h_, wg], [PG])
        self.tt(rpf[:], rpf[:], rpb[:], ALU.add, [rpf, rpb], [rpf])
        yield
        self.act(ge[:], PG[:], AF.Exp, [PG], [ge], scale=-1.0)
        for h in range(4):
            self.P.op(DVE, (lambda hh: (lambda e: e.bn_stats(out=stats[:, hh, :], in_=rpf[:, hh * 128:(hh + 1) * 128])))(h),
                      [rpf.b], [stats.b])
            self.P.op(DVE, (lambda hh: (lambda e: e.bn_aggr(out=mv[:, hh, :], in_=stats[:, hh, :])))(h),
                      [stats.b], [mv.b])
        self.ts(mv[:, :, 1], mv[:, :, 1], LN_EPS, None, ALU.add, None, [mv], [mv])
        yield
        self.rsqrt(mv[:, :, 1], mv)
        self.sigm(ge[:], ge)
        yield
        self.tt(sg[:], PG[:], ge[:], ALU.mult, [PG, ge], [sg])
        for h in range(4):
            self.ts(yn[:, h * 128:(h + 1) * 128], rpf[:, h * 128:(h + 1) * 128], mv[:, h, 0:1], mv[:, h, 1:2],
                    ALU.subtract, ALU.mult, [rpf, mv], [yn])
        yield
        self.tt(ob[:], yn[:], sg[:], ALU.mult, [yn, sg], [ob])
        yield
        for h in range(4):
            self.tr(PT[:, h * 128:(h + 1) * 128], ob[:, h * 128:(h + 1) * 128], self.identb[:], [ob, self.identb], [PT])
        yield
        self.act(oT_[:], PT[:, 0:512].rearrange("p (a b) -> p a b", a=4), AF.Copy, [PT], [oT_])
        yield
        self.dma(mixv[:, 12:16, tok], oT_[:], [oT_], [self.mixT_d], slow=True)
        yield

    def ret_sweep(self, l, d_, st, wqk, wvv, prm):
        I = self.I
        DmT, Gam, te, g128 = prm
        S = self.sb(st, [128, 2, 128], F32, "S")
        Sbf = self.sb(st, [128, 2, 128], BF16, "Sbf")
        hc = [self.sb(st, [128, 8, 128], BF16, "hcr") for _ in range(2)]
        tab = [self.sb(st, [128, 4, 128], F32, "tab") for _ in range(2)]
        r1 = self.sb(st, [128, 4, 128], F32, "r1")
        r2 = self.sb(st, [128, 4, 128], F32, "r2")
        qk = self.sb(st, [128, 4, 128], BF16, "qk")
        qz = [self.sb(st, [128, 2, 128], BF16, "qz") for _ in range(2)]
        qdz = [self.sb(st, [128, 2, 128], BF16, "qdz") for _ in range(2)]
        for par in range(2):
            self.memset(qz[par][:], 0.0, [qz[par]])
            self.memset(qdz[par][:], 0.0, [qdz[par]])
        k_tok = self.sb(st, [128, 256], BF16, "k_tok")
        vb = self.sb(st, [128, 512], BF16, "vb")
        vte = self.sb(st, [128, 512], BF16, "vte")
        Mr = self.sb(st, [128, 4, 128], BF16, "Mr")
        yp = [self.sb(st, [128, 512], F32, "ypr") for _ in range(2)]
        Q = [self.ps(st, [128, 512], F32, "QR") for _ in range(4)]
        Qb3 = Q[3][:].bitcast(BF16)
        ydst = self.rpart_d if d_ == 0 else self.rpartb_d
        order = list(range(NCH)) if d_ == 0 else [1, 0] + list(range(NCH - 1, 1, -1))
        self.memset(S[:], 0.0, [S])
        self.memset(Sbf[:], 0.0, [Sbf])
        it = 0

        def loads(ci, slot):
            c = order[ci]
            c0 = colof(c * 128)
            self.dma(hc[slot][:], self.hv[:, :, c0:c0 + 128], [self.hT_d], [hc[slot]], slow=True)
            self.dma(tab[slot][:], I["ret_tab"][:, :, c * 128:(c + 1) * 128], [I["ret_tab"]], [tab[slot]], slow=True)
        loads(0, 0)
        for ci, c in enumerate(order):
            h_ = hc[it % 2]; tb = tab[it % 2]; ypt = yp[it % 2]
            it += 1
            tok = slice(c * 128, (c + 1) * 128)
            if ci + 1 < len(order):
                loads(ci + 1, it % 2)
            for rc in range(8):
                q_ = Q[rc // 4]
                for kc in range(8):
                    self.mm(q_[:, (rc % 4) * 128:(rc % 4 + 1) * 128], wqk[:, kc, rc * 128:(rc + 1) * 128], h_[:, kc, :],
                            kc == 0, kc == 7, [wqk, h_], [q_])
                if rc % 2 == 1:
                    yield
            for kc in range(8):
                self.mm(Q[2][:], h_[:, kc, :], wvv[:, kc, :], kc == 0, kc == 7, [h_, wvv], [Q[2]])
            yield
            Q0v = Q[0][:].rearrange("p (a b) -> p a b", a=4)
            Q1v = Q[1][:].rearrange("p (a b) -> p a b", a=4)
            for half, (ci_, si_) in enumerate(((0, 1), (2, 3))):
                self.tt(r1[:, 2 * half:2 * half + 2, :], Q0v[:, 2 * half:2 * half + 2, :],
                        tb[:, ci_, :].unsqueeze(1).to_broadcast([128, 2, 128]), ALU.mult, [Q[0], tb], [r1])
                self.tt(r2[:, 2 * half:2 * half + 2, :], Q1v[:, 2 * half:2 * half + 2, :],
                        tb[:, si_, :].unsqueeze(1).to_broadcast([128, 2, 128]), ALU.mult, [Q[1], tb], [r2])
            yield
            self.act(vb[:], Q[2][:], AF.Copy, [Q[2]], [vb])
            self.tt(qk[:], r1[:], r2[:], ALU.add, [r1, r2], [qk], eng=POOL)
            yield
            self.tt(vte[:].rearrange("p (h e) -> p h e", h=4), vb[:].rearrange("p (h e) -> p h e", h=4),
                    te[:].unsqueeze(2).to_broadcast([128, 4, 128]), ALU.mult, [vb, te], [vte], eng=POOL)
            for par in range(2):
                rr = 64 * par
                self.cp(qz[par][rr:rr + 64, :, :], qk[rr:rr + 64, 0:2, :], [qk], [qz[par]], eng=POOL)
                self.tt(qdz[par][rr:rr + 64, :, :], qk[rr:rr + 64, 0:2, :], Gam[rr:rr + 64, :, :], ALU.mult,
                        [qk, Gam], [qdz[par]], eng=POOL)
            for p in range(2):
                self.tr(Qb3[:, p * 128:(p + 1) * 128], qk[:, 2 + p, :], self.identb[:], [qk, self.identb], [Q[3]])
            yield
            self.cp(k_tok[:], Qb3[:, 0:256], [Q[3]], [k_tok])
            yield
            for h in range(4):
                p = h // 2
                self.mm(Q[3][:, h * 128:(h + 1) * 128], qk[:, 2 + p, :], qz[h % 2][:, p, :], True, True, [qk, qz[h % 2]], [Q[3]])
            yield
            self.tt(Mr[:], Q[3][:].rearrange("p (a b) -> p a b", a=4), DmT[:], ALU.mult, [Q[3], DmT], [Mr])
            yield
            for h in range(4):
                p = h // 2
                self.mm(Q[0][:, h * 128:(h + 1) * 128], Mr[:, h, :], vb[:, h * 128:(h + 1) * 128], True, False, [Mr, vb], [Q[0]])
                self.mm(Q[0][:, h * 128:(h + 1) * 128], qdz[h % 2][:, p, :], Sbf[:, p, :], False, True, [qdz[h % 2], Sbf], [Q[0]])
            for h in range(4):
                p = h // 2
                self.mm(Q[1][:, h * 128:(h + 1) * 128], k_tok[:, p * 128:(p + 1) * 128], vte[:, h * 128:(h + 1) * 128],
                        True, True, [k_tok, vte], [Q[1]])
            yield
            self.cp(ypt[:], Q[0][:], [Q[0]], [ypt])
            self.dma(ydst[tok, :], ypt[:], [ypt], [ydst])
            for h in range(4):
                p, r0 = h // 2, (h % 2) * 64
                self.stt(S[r0:r0 + 64, p, :], S[r0:r0 + 64, p, :], g128[r0:r0 + 64, p:p + 1], Q[1][r0:r0 + 64, h * 128:(h + 1) * 128],
                         ALU.mult, ALU.add, [S, g128, Q[1]], [S])
            yield
            self.act(Sbf[:], S[:], AF.Copy, [S], [Sbf])
            yield

    def stage_M(self, l):
        I = self.I
        cst = self.cst
        blocks = [(0, 256)] + [(256 + 512 * i, 512) for i in range(8)]
        with ExitStack() as st1:
            v_all = self.sb(st1, [128, NCH, 512], BF16, "v_all")
            with ExitStack() as st:
                wv = I["w_in"][l].rearrange("(k p) c -> p k c", p=128)
                wm = self.sb(st, [128, 8, 704], BF16, "wm")
                wgate = self.sb(st, [128, 8, 512], BF16, "wgate")
                self.load_w(wm[:, :, 0:672], wm, wv[:, :, 2576:3248], I["w_in"])
                self.load_w(wm[:, :, 672:704], wm, wv[:, :, 5296:5328], I["w_in"])
                self.load_w(wgate[:], wgate, wv[:, :, 3248:3760], I["w_in"])
                wuq = self.sb(st, [128, 3, 8, 96], BF16, "wuq")
                wuqs = self.sb(st, [128, 3, 8, 96], BF16, "wuqs")
                wkp = self.sb(st, [128, 2, 8, 96], BF16, "wkp")
                wvv = self.sb(st, [128, 2, 8, 64], BF16, "wvv")
                self.memset(wuqs[:], 0.0, [wuqs])
                self.memset(wkp[:], 0.0, [wkp])
                uqv = I["mla_w_uq"][l].rearrange("(k p) (h e) -> p k h e", p=128, h=8)
                uqs = I["w_uq_sw"][l].rearrange("(k p) (h e) -> p k h e", p=128, h=8)
                ukv = I["mla_w_ukv"][l].rearrange("(k p) (h e) -> p k h e", p=128, h=8)
                for kc in range(3):
                    self.load_w(wuq[:, kc, :, :], wuq, uqv[:, kc, :, :], I["mla_w_uq"])
                    self.load_w(wuqs[:, kc, :, 64:96], wuqs, uqs[:, kc, :, :], I["w_uq_sw"])
                for kc in range(2):
                    self.load_w(wkp[:, kc, :, 0:64], wkp, ukv[:, kc, :, 0:64], I["mla_w_ukv"])
                    self.load_w(wvv[:, kc, :, :], wvv, ukv[:, kc, :, 64:128], I["mla_w_ukv"])
                esel = self.sb(st, [32, 96], BF16, "esel")
                self.memset(esel[:], 0.0, [esel])
                self.cp(esel[:, 64:96], self.identb[0:32, 0:32], [self.identb, esel], [esel])
                qn = self.sb(st, [128, 3], F32, "qn")
                kvn = self.sb(st, [128, 2], F32, "kvn")
                self.dma(qn[:], I["mla_q_norm"][l].rearrange("(k p) -> p k", p=128), [I["mla_q_norm"]], [qn], slow=True)
                self.dma(kvn[:], I["mla_kv_norm"][l].rearrange("(k p) -> p k", p=128), [I["mla_kv_norm"]], [kvn], slow=True)
                hb = [self.sb(st, [128, 8, 512], BF16, "hb") for _ in range(2)]
                tq = self.sb(st, [96, 2, 512], F32, "tq")
                self.memset(tq[0:64, 0, :], 1.0, [tq])
                self.memset(tq[0:64, 1, :], 0.0, [tq])
                tk = self.sb(st, [32, 2, 512], F32, "tk")
                cqs = self.sb(st, [128, 3, 512], F32, "cqs")
                sqs = self.sb(st, [128, 512], F32, "sqs")
                rstd = self.sb(st, [128, 512], F32, "rstd")
                cqn = self.sb(st, [128, 3, 512], BF16, "cqn")
                ckvn = self.sb(st, [128, 2, 512], BF16, "ckvn")
                kr1 = self.sb(st, [32, 512], F32, "kr1")
                kr2 = self.sb(st, [32, 512], F32, "kr2")
                krr = self.sb(st, [32, 512], BF16, "krr")
                q1 = self.sb(st, [96, 512], F32, "q1")
                q2 = self.sb(st, [96, 512], F32, "q2")
                qf = [self.sb(st, [96, 512], BF16, "qf") for _ in range(2)]
                kf = [self.sb(st, [96, 512], BF16, "kf") for _ in range(2)]
                ge = self.sb(st, [128, 512], F32, "ge")
                sgo = [self.sb(st, [128, 512], F32, "sgo") for _ in range(2)]
                P0 = [self.ps(st, [128, 512], F32, "P0") for _ in range(2)]
                PSS = self.ps(st, [128, 512], F32, "PSS")
                PQ1 = self.ps(st, [128, 512], F32, "PQ1")
                PQ2 = self.ps(st, [128, 512], F32, "PQ2")
                PK = self.ps(st, [128, 512], F32, "PK")
                PVv = self.ps(st, [128, 512], F32, "PVv")
                PG = self.ps(st, [128, 512], F32, "PG")
                sgv = self.sgT_d[:].rearrange("(r p) t -> p r t", p=128)
                ctr = 0
                for bi, (t0, n) in enumerate(blocks):
                    h_ = hb[bi % 2]
                    c0 = colof(t0)
                    self.dma(h_[:, :, 0:n], self.hv[:, :, c0:c0 + n], [self.hT_d], [h_], slow=True)
                    self.dma(tq[64:96, :, 0:n], I["mla_tab"][:, :, t0:t0 + n], [I["mla_tab"]], [tq], slow=True)
                    self.dma(tk[:, :, 0:n], I["mla_tab"][:, :, t0:t0 + n], [I["mla_tab"]], [tk], slow=True)
                    for (nrc, off, dst, nrm, dim, keep) in ((3, 0, cqn, qn, 384.0, None), (2, 384, ckvn, kvn, 256.0, None)):
                        for rc in range(nrc):
                            p_ = P0[ctr % 2]; ctr += 1
                            for kc in range(8):
                                self.mm(p_[:, 0:n], wm[:, kc, off + rc * 128:off + (rc + 1) * 128], h_[:, kc, 0:n], kc == 0, kc == 7, [wm, h_], [p_])
                            self.act(cqs[:, rc, 0:n], p_[:, 0:n], AF.Copy, [p_], [cqs])
                            self.act(sqs[:, 0:n], p_[:, 0:n], AF.Square, [p_], [sqs])
                            self.mm(PSS[:, 0:n], self.ones[:], sqs[:, 0:n], rc == 0, rc == nrc - 1, [self.ones, sqs], [PSS])
                        self.ts(rstd[:, 0:n], PSS[:, 0:n], 1.0 / dim, RMS_EPS, ALU.mult, ALU.add, [PSS], [rstd])
                        self.rsqrt(rstd[:, 0:n], rstd)
                        for rc in range(nrc):
                            self.stt(dst[:, rc, 0:n], cqs[:, rc, 0:n], nrm[:, rc:rc + 1], rstd[:, 0:n], ALU.mult, ALU.mult, [cqs, nrm, rstd], [dst])
                    self.mm_group_kr(h_, n, wm, PQ1, PQ2)
                    self.tt(kr1[:, 0:n], PQ1[0:32, 0:n], tk[:, 0, 0:n], ALU.mult, [PQ1, tk], [kr1])
                    self.tt(kr2[:, 0:n], PQ2[0:32, 0:n], tk[:, 1, 0:n], ALU.mult, [PQ2, tk], [kr2])
                    self.tt(krr[:, 0:n], kr1[:, 0:n], kr2[:, 0:n], ALU.add, [kr1, kr2], [krr])
                    for h in range(8):
                        qf_ = qf[h % 2]; kf_ = kf[h % 2]
                        for kc in range(3):
                            self.mm(PQ1[0:96, 0:n], wuq[:, kc, h, :], cqn[:, kc, 0:n], kc == 0, kc == 2, [wuq, cqn], [PQ1])
                        for kc in range(3):
                            self.mm(PQ2[0:96, 0:n], wuqs[:, kc, h, :], cqn[:, kc, 0:n], kc == 0, kc == 2, [wuqs, cqn], [PQ2])
                        self.tt(q1[:, 0:n], PQ1[0:96, 0:n], tq[:, 0, 0:n], ALU.mult, [PQ1, tq], [q1])
                        self.tt(q2[:, 0:n], PQ2[0:96, 0:n], tq[:, 1, 0:n], ALU.mult, [PQ2, tq], [q2])
                        self.tt(qf_[:, 0:n], q1[:, 0:n], q2[:, 0:n], ALU.add, [q1, q2], [qf_], eng=POOL)
                        self.dma(self.qT_d[h, :, t0:t0 + n], qf_[:, 0:n], [qf_], [self.qT_d])
                        for kc in range(2):
                            self.mm(PK[0:96, 0:n], wkp[:, kc, h, :], ckvn[:, kc, 0:n], kc == 0, False, [wkp, ckvn], [PK])
                        self.mm(PK[0:96, 0:n], esel[:], krr[:, 0:n], False, True, [esel, krr], [PK])
                        self.act(kf_[:, 0:n], PK[0:96, 0:n], AF.Copy, [PK], [kf_])
                        self.dma(self.kfT_d[h, :, t0:t0 + n], kf_[:, 0:n], [kf_], [self.kfT_d])
                    for s in range(n // 128):
                        ch = (t0 + s * 128) // 128
                        for kc in range(2):
                            self.mm(PVv[:], ckvn[:, kc, s * 128:(s + 1) * 128], wvv[:, kc, :, :].rearrange("p h e -> p (h e)"),
                                    kc == 0, kc == 1, [ckvn, wvv], [PVv])
                        self.act(v_all[:, ch, :], PVv[:], AF.Copy, [PVv], [v_all])
                    for rc in range(4):
                        sg_ = sgo[rc % 2]
                        for kc in range(8):
                            self.mm(PG[:, 0:n], wgate[:, kc, rc * 128:(rc + 1) * 128], h_[:, kc, 0:n], kc == 0, kc == 7, [wgate, h_], [PG])
                        self.act(ge[:, 0:n], PG[:, 0:n], AF.Exp, [PG], [ge], scale=-1.0)
                        self.sigm(ge[:, 0:n], ge)
                        self.tt(sg_[:, 0:n], PG[:, 0:n], ge[:, 0:n], ALU.mult, [PG, ge], [sg_])
                        self.dma(sgv[:, rc, t0:t0 + n], sg_[:, 0:n], [sg_], [self.sgT_d])
            self.P.barrier()
            import os as _os
            if _os.environ.get("NO_M2"):
                return
            with ExitStack() as st:
                kfh = [self.sb(st, [96, NTOK], BF16, "kfh") for _ in range(2)]
                qh = [self.sb(st, [96, NTOK], BF16, "qh") for _ in range(2)]
                vaug = [self.sb(st, [128, NCH, 128], BF16, "vaug") for _ in range(2)]
                self.memset(vaug[0][:, :, 64:128], 1.0, [vaug[0]])
                self.memset(vaug[1][:, :, 0:64], 1.0, [vaug[1]])
                pT = [self.sb(st, [128, 512], BF16, "pT") for _ in range(3)]
                sgh = [self.sb(st, [128, 512], F32, "sgh") for _ in range(2)]
                rden = self.sb(st, [128, 512], F32, "rden")
                ot = self.sb(st, [128, 512], F32, "ot")
                ob = [self.sb(st, [128, 512], BF16, "ob") for _ in range(2)]
                PSc = [self.ps(st, [128, 512], F32, "PSc") for _ in range(3)]
                PO = [self.ps(st, [128, 512], F32, "PO") for _ in range(2)]
                ci = 0
                bi_ = 0
                for h in range(int(_os.environ.get("M2_HEADS", "8"))):
                    par = h % 2
                    r0 = 64 * par
                    d0 = 64 - r0
                    kf_ = kfh[h % 2]; q_ = qh[h % 2]; va = vaug[par]
                    self.dma(kf_[:], self.kfT_d[h], [self.kfT_d], [kf_])
                    self.dma(q_[:], self.qT_d[h], [self.qT_d], [q_])
                    self.cp(va[:, :, r0:r0 + 64], v_all[:, :, h * 64:(h + 1) * 64], [v_all], [va], eng=POOL)
                    its = []
                    for (t0, n) in blocks:
                        if t0 == 0:
                            if l == DEPTH - 1:
                                continue
                            kcs = [0, 1]
                        else:
                            kcs = list(range(NCH))
                        po = PO[bi_ % 2]; sg_ = sgh[bi_ % 2]; ob_ = ob[bi_ % 2]
                        bi_ += 1
                        for i, kc in enumerate(kcs):
                            its.append((t0, n, kc, i == 0, i == len(kcs) - 1, po, sg_, ob_))
                    LOOK = 2
                    for j in range(len(its) + LOOK):
                        if j < len(its):
                            (t0, n, kc, first, last, po, sg_, ob_) = its[j]
                            if first:
                                self.dma(sg_[r0:r0 + 64, 0:n], self.sgT_d[64 * h:64 * (h + 1), t0:t0 + n], [self.sgT_d], [sg_])
                            psc = PSc[(ci + j) % 3]
                            self.mm(psc[:, 0:n], kf_[:, kc * 128:(kc + 1) * 128], q_[:, t0:t0 + n], True, True, [kf_, q_], [psc])
                        jj = j - LOOK
                        if jj >= 0:
                            (t0, n, kc, first, last, po, sg_, ob_) = its[jj]
                            psc = PSc[(ci + jj) % 3]; pt = pT[(ci + jj) % 3]
                            self.act(pt[:, 0:n], psc[:, 0:n], AF.Exp, [psc], [pt], scale=MLA_SCALE)
                            self.mm(po[:, 0:n], va[:, kc, :], pt[:, 0:n], first, last, [va, pt], [po])
                            if last:
                                self.P.op(DVE, (lambda a_, b_: (lambda e: e.reciprocal(out=a_, in_=b_)))(rden[d0:d0 + 64, 0:n], po[d0:d0 + 64, 0:n]),
                                          [po.b], [rden.b])
                                self.tt(ot[r0:r0 + 64, 0:n], po[r0:r0 + 64, 0:n], rden[d0:d0 + 64, 0:n], ALU.mult, [po, rden], [ot])
                                self.tt(ob_[r0:r0 + 64, 0:n], ot[r0:r0 + 64, 0:n], sg_[r0:r0 + 64, 0:n], ALU.mult, [ot, sg_], [ob_], eng=POOL)
                                self.dma(self.mixT_d[1024 + h * 64:1024 + (h + 1) * 64, t0:t0 + n], ob_[r0:r0 + 64, 0:n], [ob_], [self.mixT_d])
                    ci += len(its)

    def mm_group_kr(self, h_, n, wm, PQ1, PQ2):
        for kc in range(8):
            self.mm(PQ1[0:32, 0:n], wm[:, kc, 640:672], h_[:, kc, 0:n], kc == 0, kc == 7, [wm, h_], [PQ1])
        for kc in range(8):
            self.mm(PQ2[0:32, 0:n], wm[:, kc, 672:704], h_[:, kc, 0:n], kc == 0, kc == 7, [wm, h_], [PQ2])

    def stage_E(self, l):
        I = self.I
        with ExitStack() as st:
            wo = self.sb(st, [128, 16, 1024], BF16, "wo")
            wov = I["w_out"][l].rearrange("(k p) c -> p k c", p=128)
            for kc in range(16):
                self.load_w(wo[:, kc, :], wo, wov[:, kc, :], I["w_out"])
            lng = self.bvec(st, "ln_g", l, 1024)
            lnb = self.bvec(st, "ln_b", l, 1024)
            sets = []
            for s_ in range(3):
                B = {}
                B["mt"] = self.sb(st, [128, 16, 128], BF16, "mt")
                B["xt"] = self.sb(st, [128, D], F32, "xt")
                B["v1"] = self.sb(st, [128, D], F32, "v1")
                B["v2"] = self.sb(st, [128, D], F32, "v2")
                B["xo"] = self.sb(st, [128, D], F32, "xo")
                B["stats"] = self.sb(st, [128, 2, 6], F32, "stats")
                B["mv"] = self.sb(st, [128, 2], F32, "mv")
                B["PZ"] = [self.ps(st, [128, 512], F32, "PZ") for _ in range(2)]
                sets.append(B)
            tiles = list(range(NCH)) if l < DEPTH - 1 else list(range(2, NCH))
            gens = [(lambda tt_: (lambda slot: self.e_tile(l, tt_, sets[slot], wo, lng, lnb)))(t) for t in tiles]
            self.interleave(gens, 3)

    def e_tile(self, l, t, B, wo, lng, lnb):
        I = self.I
        m_ = B["mt"]; x_ = B["xt"]; v1 = B["v1"]; v2 = B["v2"]; xo_ = B["xo"]; stats = B["stats"]; mv = B["mv"]; PZ = B["PZ"]
        mixv = self.mixT_d[:].rearrange("(r p) t -> p r t", p=128)
        typ = 1 if t < 2 else 0
        tok = slice(t * 128, (t + 1) * 128)
        self.dma(m_[:], mixv[:, :, tok], [self.mixT_d], [m_], slow=True)
        if l == 0:
            src_t = I["ctx"] if t < 2 else I["x"]
            src = src_t[t * 128:(t + 1) * 128, :] if t < 2 else src_t[(t - 2) * 128:(t - 1) * 128, :]
        else:
            src_t = self.xres_d
            src = src_t[tok, :]
        self.dma(x_[:], src, [src_t], [x_])
        yield
        for nb in range(2):
            for kc in range(16):
                self.mm(PZ[nb][:], m_[:, kc, :], wo[:, kc, nb * 512:(nb + 1) * 512], kc == 0, kc == 15, [m_, wo], [PZ[nb]])
            yield
        for nb in range(2):
            self.tt(v1[:, nb * 512:(nb + 1) * 512], PZ[nb][:], self.gB[:, typ, nb * 512:(nb + 1) * 512], ALU.mult, [PZ[nb], self.gB], [v1])
        yield
        self.stt(v2[:], x_[:], ALPHA, v1[:], ALU.mult, ALU.add, [x_, v1], [v2])
        yield
        for s in range(2):
            self.P.op(DVE, (lambda ss: (lambda e: e.bn_stats(out=stats[:, ss, :], in_=v2[:, ss * 512:(ss + 1) * 512])))(s),
                      [v2.b], [stats.b])
        self.P.op(DVE, lambda e: e.bn_aggr(out=mv[:], in_=stats[:]), [stats.b], [mv.b])
        self.ts(mv[:, 1:2], mv[:, 1:2], LN_EPS, None, ALU.add, None, [mv], [mv])
        yield
        self.rsqrt(mv[:, 1:2], mv)
        yield
        self.ts(v1[:], v2[:], mv[:, 0:1], mv[:, 1:2], ALU.subtract, ALU.mult, [v2, mv], [v1])
        yield
        self.tt(v2[:], v1[:], lng[:], ALU.mult, [v1, lng], [v2], eng=POOL)
        yield
        self.tt(xo_[:], v2[:], lnb[:], ALU.add, [v2, lnb], [xo_], eng=POOL)
        yield
        if l < DEPTH - 1:
            self.dma(self.xres_d[tok, :], xo_[:], [xo_], [self.xres_d])
        else:
            self.dma(self.out[(t - 2) * 128:(t - 1) * 128, :], xo_[:], [xo_], [self.out])
        yield


C_ID = 0
C_UF = 128
C_LF = 256
C_UB = 384
C_LB = 512
C_MF = 640
C_MB = 768
C_RIF = 896
C_RIB = 1024
C_GF = 1152
C_GB = 1280
C_TEF = 1408
C_TEB = 1409
CST_W = 1410


def make_consts():
    k = np.arange(128)[:, None].astype(np.float32)
    i = np.arange(128)[None, :].astype(np.float32)
    cst = np.zeros((128, CST_W), np.float32)
    cst[:, C_ID:C_ID + 128] = (k == i)
    cst[:, C_UF:C_UF + 128] = (k <= i)
    cst[:, C_LF:C_LF + 128] = (k > i)
    cst[:, C_UB:C_UB + 128] = (k >= i)
    cst[:, C_LB:C_LB + 128] = (k < i)
    cst[:, C_MF:C_MF + 128] = (k <= i)
    cst[:, C_MB:C_MB + 128] = (k >= i)
    cst[:, C_RIF:C_RIF + 128] = np.maximum(i - k, 0)
    cst[:, C_RIB:C_RIB + 128] = np.maximum(k - i, 0)
    cst[:, C_GF:C_GF + 128] = np.broadcast_to(i + 1, (128, 128))
    cst[:, C_GB:C_GB + 128] = np.broadcast_to(128 - i, (128, 128))
    cst[:, C_TEF] = 127 - k[:, 0]
    cst[:, C_TEB] = k[:, 0]
    return cst


def rope_tables():
    rows = SEQ // 64
    t = np.arange(rows * 64)
    row = (t // 64).astype(np.float32)
    col = (t % 64).astype(np.float32)

    def cs(rot):
        nf = rot // 4
        inv = (np.float32(10000.0) ** (-np.arange(nf, dtype=np.float32) / np.float32(nf))).astype(np.float32)
        ang = np.concatenate([row[:, None] * inv, col[:, None] * inv], -1).astype(np.float32)
        return np.cos(ang).astype(np.float32), np.sin(ang).astype(np.float32)
    cm, sm = cs(32)
    mla = np.zeros((32, 2, NTOK), np.float32)
    mla[:, 0, :CTX] = 1.0
    mla[0:16, 0, CTX:] = cm.T; mla[16:32, 0, CTX:] = cm.T
    mla[0:16, 1, CTX:] = -sm.T; mla[16:32, 1, CTX:] = sm.T
    cr, sr = cs(64)
    ret = np.zeros((128, 4, NTOK), np.float32)
    ret[:, 0, :CTX] = 1.0
    for hh in range(2):
        b = hh * 64
        ret[b:b + 32, 0, CTX:] = cr.T; ret[b + 32:b + 64, 0, CTX:] = cr.T
        ret[b:b + 32, 1, CTX:] = -sr.T; ret[b + 32:b + 64, 1, CTX:] = sr.T
    ret[:, 2] = ret[:, 0] * 0.125
    ret[:, 3] = ret[:, 1] * 0.125
    return mla, ret


_CACHE = {}


def prep_inputs(inputs):
    f = lambda a: np.ascontiguousarray(np.asarray(a, dtype=np.float32))
    w_in = f(inputs["w_in"])
    kr = w_in[:, :, 3216:3248]
    kr_sw = np.concatenate([kr[:, :, 16:32], kr[:, :, 0:16]], -1)

    def sw64(a):
        a = a.reshape(2, D, 4, 2, 32)
        return np.ascontiguousarray(a[:, :, :, ::-1, :]).reshape(2, D, 256)
    q_sw = sw64(w_in[:, :, 3760:4016])
    k_sw = sw64(w_in[:, :, 4016:4272])
    w_ext = np.ascontiguousarray(np.concatenate([w_in, kr_sw, q_sw, k_sw], -1))
    uq = f(inputs["mla_w_uq"]).reshape(2, 384, 8, 96)
    uq_r = uq[:, :, :, 64:96]
    uq_sw = np.ascontiguousarray(np.concatenate([uq_r[..., 16:32], uq_r[..., 0:16]], -1)).reshape(2, 384, 256)
    mla_tab, ret_tab = rope_tables()
    shared = {
        "w_ada": f(inputs["w_ada"]), "b_ada": f(inputs["b_ada"]), "w_in": w_ext,
        "ssd_conv_w": f(inputs["ssd_conv_w"]), "ssd_conv_b": f(inputs["ssd_conv_b"]),
        "ssd_norm_w": f(inputs["ssd_norm_w"]), "mla_q_norm": f(inputs["mla_q_norm"]),
        "mla_w_uq": f(inputs["mla_w_uq"]), "w_uq_sw": uq_sw, "mla_kv_norm": f(inputs["mla_kv_norm"]),
        "mla_w_ukv": f(inputs["mla_w_ukv"]), "ret_log_rate_f": f(inputs["ret_log_rate_f"]),
        "ret_log_rate_b": f(inputs["ret_log_rate_b"]), "w_out": f(inputs["w_out"]),
        "ln_g": f(inputs["ln_g"]), "ln_b": f(inputs["ln_b"]),
        "cst": make_consts(), "mla_tab": mla_tab, "ret_tab": ret_tab,
    }
    for n in ("ssd_a_log_f", "ssd_a_log_b", "ssd_dt_bias_f", "ssd_dt_bias_b", "ssd_d"):
        shared[n] = f(inputs[n])
    x = f(inputs["x"]); c = f(inputs["c"]); ctx = f(inputs["ctx"]); c_ctx = f(inputs["c_ctx"])
    maps = []
    for b in range(8):
        m = dict(shared)
        m["x"] = x[b]
        m["ctx"] = ctx[b]
        m["cvec"] = np.ascontiguousarray(np.stack([c[b], c_ctx], 0))
        maps.append(m)
    return maps


def kernel(**inputs):
    if "nc" not in _CACHE:
        _CACHE["nc"] = K().build()
    nc = _CACHE["nc"]
    maps = prep_inputs(inputs)
    res = run_bass_kernel_spmd(nc, maps, core_ids=list(range(8)))
    return np.stack([np.asarray(r["out"], dtype=np.float32) for r in res.results], 0)
```

```python
import math
from contextlib import ExitStack
import numpy as np
import concourse.bass as bass
import concourse.mybir as mybir
from concourse.bass_utils import run_bass_kernel_spmd

F32 = mybir.dt.float32
BF16 = mybir.dt.bfloat16
AF = mybir.ActivationFunctionType
ALU = mybir.AluOpType

PE, ACT, DVE, POOL, SP = "pe", "act", "dve", "pool", "sp"
EPOCH = 30000
DMA_K = 8
DMA_EPOCH = 1800

D = 1024
SEQ = 4096
CTX = 256
NTOK = SEQ + CTX
NCH = NTOK // 128
HC = NTOK + 8
DEPTH = 2
ALPHA = (2 * DEPTH) ** 0.25
LN_EPS = 1e-5
RMS_EPS = 1e-6
MLA_SCALE = 96 ** -0.5
WEXT = 5296 + 32 + 256 + 256


def colof(t):
    return t + 2 if t < CTX else t + 6


class Buf:
    __slots__ = ("name", "lw", "rd", "rdd", "psum")

    def __init__(self, name=""):
        self.name = name
        self.lw = None
        self.rd = {}
        self.rdd = []
        self.psum = False


class Op:
    __slots__ = ("eng", "fn", "deps", "sig", "idx", "dma_slot", "dma_prev")

    def __init__(self, eng, fn):
        self.eng = eng
        self.fn = fn
        self.deps = set()
        self.sig = None
        self.dma_slot = None
        self.dma_prev = None


class Prog:
    def __init__(self, nc):
        self.nc = nc
        self.ops = []
        self.eng = {PE: nc.tensor, ACT: nc.scalar, DVE: nc.vector, POOL: nc.gpsimd, SP: nc.sync}
        self.dma_lists = {}
        self.last = {}

    def op(self, eng, fn, reads=(), writes=(), dma=False):
        o = Op(eng, fn)
        o.idx = len(self.ops)
        for b in reads:
            if b.lw is not None:
                o.deps.add(b.lw)
            if b.psum:
                for e2, r in b.rd.items():
                    if e2 != eng:
                        o.deps.add(r)
        for b in writes:
            if b.lw is not None:
                o.deps.add(b.lw)
            for r in b.rd.values():
                o.deps.add(r)
            for r in b.rdd:
                o.deps.add(r)
        for b in reads:
            if dma:
                b.rdd.append(o.idx)
            else:
                b.rd[eng] = o.idx
        for b in writes:
            b.lw = o.idx
            b.rd = {}
            b.rdd = []
        if dma:
            lst = self.dma_lists.setdefault(eng, [])
            o.dma_slot = len(lst)
            if len(lst) >= DMA_K:
                o.dma_prev = lst[len(lst) - DMA_K]
            lst.append(o.idx)
        o.deps.discard(o.idx)
        self.ops.append(o)
        self.last[eng] = o.idx
        return o

    def barrier(self):
        bufs = {}
        for e in (PE, ACT, DVE, POOL, SP):
            bufs[e] = Buf("bar" + e)
            o = self.op(e, lambda en: en.nop(), writes=[bufs[e]])
            for lst in self.dma_lists.values():
                for d in lst[-DMA_K:]:
                    if d != o.idx:
                        o.deps.add(d)
        for e in (PE, ACT, DVE, POOL, SP):
            self.op(e, lambda en: en.nop(), reads=list(bufs.values()))

    def emit(self, stack):
        nc = self.nc
        ops = self.ops
        needed = set()
        for o in ops:
            for d in o.deps:
                do = ops[d]
                if do.eng == o.eng and o.eng == PE and do.dma_slot is None:
                    continue
                needed.add(d)
            if o.dma_prev is not None:
                needed.add(o.dma_prev)
        cnt = {}
        sems = {}
        dma_sems = {}
        for o in ops:
            if o.dma_slot is not None:
                k = o.dma_slot % DMA_K
                n = o.dma_slot // DMA_K
                key = (o.eng, k, n // DMA_EPOCH)
                if key not in dma_sems:
                    dma_sems[key] = stack.enter_context(nc.semaphore("dq%s%d_%d" % key))
                o.sig = (dma_sems[key], 16 * (n % DMA_EPOCH + 1))
            elif o.idx in needed:
                c = cnt.get(o.eng, 0)
                key = (o.eng, c // EPOCH)
                if key not in sems:
                    sems[key] = stack.enter_context(nc.semaphore("s%s_%d" % key))
                o.sig = (sems[key], c % EPOCH + 1)
                cnt[o.eng] = c + 1
        waited = {}
        nw = 0
        for o in ops:
            e = self.eng[o.eng]
            deps = set(o.deps)
            if o.dma_prev is not None:
                deps.add(o.dma_prev)
            for d in sorted(deps):
                do = ops[d]
                if do.sig is None:
                    continue
                sem, val = do.sig
                key = (o.eng, id(sem))
                if waited.get(key, 0) >= val:
                    continue
                waited[key] = val
                e.wait_ge(sem, val)
                nw += 1
            ins = o.fn(e)
            if o.sig is not None:
                sem, val = o.sig
                ins.then_inc(sem, 16 if o.dma_slot is not None else 1)
        self.nwaits = nw


class T:
    __slots__ = ("t", "b")

    def __init__(self, t, name=""):
        self.t = t
        self.b = Buf(name)

    def __getitem__(self, k):
        return self.t[k]


class K:
    def __init__(self, debug=False, stop_after=None, skip=()):
        self.skip = set(skip)
        self.debug = debug
        self.stop_after = stop_after
        self.nc = bass.Bass("TRN2", target_bir_lowering=False)
        self.P = Prog(self.nc)
        self.uid = 0

    def dram(self, name, shape, dt, kind="Internal"):
        return T(self.nc.dram_tensor(name, list(shape), dt, kind=kind).ap(), name)

    def sb(self, st, shape, dt, name=None):
        self.uid += 1
        name = "%s_%d" % (name or "t", self.uid)
        return T(st.enter_context(self.nc.sbuf_tensor(name, list(shape), dt)), name)

    def ps(self, st, shape, dt=F32, name=None):
        self.uid += 1
        name = "%s_%d" % (name or "p", self.uid)
        t = T(st.enter_context(self.nc.psum_tensor(name, list(shape), dt)), name)
        t.b.psum = True
        return t

    def dma(self, out, in_, reads, writes, eng=SP, slow=False):
        if slow:
            return self.P.op(eng, lambda e: e.dma_start(out=out, in_=in_, allow_slow_non_contiguous=True),
                             [x.b for x in reads], [x.b for x in writes], dma=True)
        return self.P.op(eng, lambda e: e.dma_start(out=out, in_=in_), [x.b for x in reads], [x.b for x in writes], dma=True)

    def mm(self, out, lhsT, rhs, start, stop, reads, writes):
        return self.P.op(PE, lambda e: e.matmul(out, lhsT=lhsT, rhs=rhs, start=start, stop=stop),
                         [x.b for x in reads], [x.b for x in writes])

    def tr(self, out, in_, ident, reads, writes):
        return self.P.op(PE, lambda e: e.transpose(out=out, in_=in_, identity=ident),
                         [x.b for x in reads], [x.b for x in writes])

    def act(self, out, in_, func, reads, writes, bias=None, scale=None, accum_out=None):
        kw = {}
        if bias is not None:
            kw["bias"] = bias
        if scale is not None:
            kw["scale"] = scale
        if accum_out is not None:
            kw["accum_out"] = accum_out
        return self.P.op(ACT, lambda e: e.activation(out=out, in_=in_, func=func, **kw),
                         [x.b for x in reads], [x.b for x in writes])

    def tt(self, out, in0, in1, op, reads, writes, eng=DVE):
        return self.P.op(eng, lambda e: e.tensor_tensor(out=out, in0=in0, in1=in1, op=op),
                         [x.b for x in reads], [x.b for x in writes])

    def ts(self, out, in0, s1, s2, op0, op1, reads, writes, eng=DVE):
        if op1 is None:
            return self.P.op(eng, lambda e: e.tensor_scalar(out=out, in0=in0, scalar1=s1, scalar2=None, op0=op0),
                             [x.b for x in reads], [x.b for x in writes])
        return self.P.op(eng, lambda e: e.tensor_scalar(out=out, in0=in0, scalar1=s1, scalar2=s2, op0=op0, op1=op1),
                         [x.b for x in reads], [x.b for x in writes])

    def stt(self, out, in0, scalar, in1, op0, op1, reads, writes, eng=DVE):
        return self.P.op(eng, lambda e: e.scalar_tensor_tensor(out=out, in0=in0, scalar=scalar, in1=in1, op0=op0, op1=op1),
                         [x.b for x in reads], [x.b for x in writes])

    def cp(self, out, in_, reads, writes, eng=DVE):
        return self.P.op(eng, lambda e: e.tensor_copy(out=out, in_=in_), [x.b for x in reads], [x.b for x in writes])

    def memset(self, out, val, writes, eng=POOL):
        return self.P.op(eng, lambda e: e.memset(out, val), [], [x.b for x in writes])

    def sigm(self, ap, t):
        self.act(ap, ap, AF.Ln, [t], [t], bias=1.0)
        self.act(ap, ap, AF.Exp, [t], [t], scale=-1.0)

    def rsqrt(self, ap, t):
        self.act(ap, ap, AF.Ln, [t], [t])
        self.act(ap, ap, AF.Exp, [t], [t], scale=-0.5)

    def build(self):
        nc = self.nc
        dbg = self.debug
        I = {}

        def inp(name, shape):
            I[name] = self.dram(name, shape, F32, kind="ExternalInput")
        inp("x", [SEQ, D]); inp("ctx", [CTX, D]); inp("cvec", [2, D])
        inp("w_ada", [2, D, 3 * D]); inp("b_ada", [2, 3 * D]); inp("w_in", [2, D, WEXT])
        inp("ssd_conv_w", [2, 5, 1536]); inp("ssd_conv_b", [2, 1536])
        for n in ("ssd_a_log_f", "ssd_a_log_b", "ssd_dt_bias_f", "ssd_dt_bias_b", "ssd_d"):
            inp(n, [2, 16])
        inp("ssd_norm_w", [2, 1024]); inp("mla_q_norm", [2, 384]); inp("mla_w_uq", [2, 384, 768])
        inp("w_uq_sw", [2, 384, 256]); inp("mla_kv_norm", [2, 256]); inp("mla_w_ukv", [2, 256, 1024])
        inp("ret_log_rate_f", [2, 4]); inp("ret_log_rate_b", [2, 4]); inp("w_out", [2, 2048, D])
        inp("ln_g", [2, D]); inp("ln_b", [2, D])
        inp("cst", [128, CST_W]); inp("mla_tab", [32, 2, NTOK]); inp("ret_tab", [128, 4, NTOK]); inp("ret_tab2", [NTOK, 1024])
        self.I = I
        okind = "ExternalOutput" if dbg else "Internal"
        self.out = self.dram("out", [SEQ, D], F32, kind="ExternalOutput")
        self.hT_d = self.dram("hT_d", [128, 8 * HC], BF16, kind=okind)
        self.ypart_d = self.dram("ypart_d", [NTOK, 1024], F32)
        self.rpart_d = self.dram("rpart_d", [NTOK, 512], F32)
        self.ypartb_d = self.dram("ypartb_d", [NTOK, 1024], F32)
        self.Fb_d = self.dram("Fb_d", [NCH, 128, 1536], BF16)
        self.Rb_d = self.dram("Rb_d", [NCH, 128, 1280], BF16)
        self.Ff_d = self.dram("Ff_d", [NCH, 128, 272], F32)
        self.rpartb_d = self.dram("rpartb_d", [NTOK, 512], F32)
        self.mixT_d = self.dram("mixT_d", [2048, NTOK], BF16, kind=okind)
        self.qT_d = self.dram("qT_d", [8, 96, NTOK], BF16)
        self.kfT_d = self.dram("kfT_d", [8, 96, NTOK], BF16)
        self.sgT_d = self.dram("sgT_d", [512, NTOK], F32)
        self.xres_d = self.dram("xres_d", [NTOK, D], F32, kind=okind)

        with ExitStack() as gst:
            self.gst = gst
            self.cst = self.sb(gst, [128, CST_W], F32, "cst")
            self.dma(self.cst[:], I["cst"][:], [I["cst"]], [self.cst])
            self.identb = self.sb(gst, [128, 128], BF16, "identb")
            self.cp(self.identb[:], self.cst[:, C_ID:C_ID + 128], [self.cst], [self.identb])
            self.ones = self.sb(gst, [128, 128], F32, "ones")
            self.memset(self.ones[:], 1.0, [self.ones])
            self.gB = self.sb(gst, [128, 2, 1024], F32, "gB")
            zt = self.sb(gst, [128, 8, 4], BF16, "zt")
            self.memset(zt[:], 0.0, [zt])
            hv = self.hT_d[:].rearrange("p (k c) -> p k c", k=8)
            self.hv = hv
            for (a, b) in ((0, 2), (258, 262), (4358, 4360)):
                self.dma(hv[:, :, a:b], zt[:, :, 0:b - a], [zt], [self.hT_d], slow=True)
            self.P.barrier()
            stages = []
            for l in range(DEPTH):
                stages += [("A", l), ("S", l), ("M", l), ("R", l), ("E", l)]
            for (s, l) in stages:
                if s in self.skip:
                    continue
                if s == "A":
                    self.stage_A(l)
                elif s == "S":
                    self.stage_S(l)
                elif s == "M":
                    self.stage_M(l)
                elif s == "R":
                    self.stage_R(l)
                else:
                    self.stage_E(l)
                self.P.barrier()
                if self.stop_after == (s, l):
                    break
            self.P.barrier()
            self.P.emit(gst)
        return nc

    def silu_psum(self, st, src_ap, src_t, out_ap, out_t, e_t, e_ap, r_ap):
        self.act(e_ap, src_ap, AF.Exp, [src_t], [e_t], scale=-1.0)
        self.sigm(e_ap, e_t)
        self.tt(out_ap, src_ap, r_ap, ALU.mult, [src_t, e_t], [out_t])

    def stage_A(self, l):
        I = self.I
        with ExitStack() as st:
            wada = [self.sb(st, [128, 8, 512], F32, "wada") for _ in range(2)]
            craw = self.sb(st, [128, 8, 2], F32, "craw")
            ce = self.sb(st, [128, 8, 2], F32, "ce")
            scT = self.sb(st, [128, 8, 2], F32, "scT")
            modT = self.sb(st, [128, 24, 2], F32, "modT")
            scale1 = self.sb(st, [128, 8, 2], F32, "scale1")
            brow = self.sb(st, [1, 3 * D], F32, "brow")
            pm = self.ps(st, [128, 512], F32, "pm")
            pg = [self.ps(st, [128, 512], F32, "pg") for _ in range(2)]
            for j in range(2):
                self.dma(craw[:, :, j], I["cvec"][j].rearrange("(k p) -> p k", p=128), [I["cvec"]], [craw], slow=True)
            self.dma(brow[:], I["b_ada"][l:l + 1, :], [I["b_ada"]], [brow])
            self.act(ce[:], craw[:], AF.Exp, [craw], [ce], scale=-1.0)
            self.sigm(ce[:], ce)
            self.tt(scT[:], craw[:], ce[:], ALU.mult, [craw, ce], [scT])
            wv = I["w_ada"][l].rearrange("(k p) c -> p k c", p=128)
            for cb in range(6):
                w = wada[cb % 2]
                self.dma(w[:], wv[:, :, cb * 512:(cb + 1) * 512], [I["w_ada"]], [w])
                if cb < 4:
                    for dj in range(4):
                        j = cb * 4 + dj
                        for kc in range(8):
                            self.mm(pm[:, 2 * dj:2 * dj + 2], w[:, kc, dj * 128:(dj + 1) * 128], scT[:, kc, :],
                                    kc == 0, False, [w, scT], [pm])
                        self.mm(pm[:, 2 * dj:2 * dj + 2], brow[0:1, j * 128:(j + 1) * 128], self.ones[0:1, 0:2],
                                False, True, [brow, self.ones], [pm])
                        self.cp(modT[:, j, :], pm[:, 2 * dj:2 * dj + 2], [pm], [modT])
                else:
                    for typ in range(2):
                        p = pg[typ]
                        for kc in range(8):
                            self.mm(p[:], scT[:, kc, typ:typ + 1].to_broadcast([128, 128]), w[:, kc, :],
                                    kc == 0, False, [w, scT], [p])
                        self.mm(p[:], self.ones[0:1, 0:128], brow[0:1, cb * 512:(cb + 1) * 512], False, True,
                                [brow, self.ones], [p])
                        self.cp(self.gB[:, typ, (cb - 4) * 512:(cb - 3) * 512], p[:], [p], [self.gB])
            self.ts(scale1[:], modT[:, 8:16, :], 1.0, None, ALU.add, None, [modT], [scale1])
            xt = [self.sb(st, [128, D], F32, "xt") for _ in range(2)]
            ht = [self.sb(st, [128, 8, 128], BF16, "ht") for _ in range(2)]
            pT = [self.ps(st, [128, 1024], F32, "pT") for _ in range(2)]
            for t in range(NCH):
                typ = 1 if t < 2 else 0
                if l == 0:
                    src_t = I["ctx"] if t < 2 else I["x"]
                    src = src_t[t * 128:(t + 1) * 128, :] if t < 2 else src_t[(t - 2) * 128:(t - 1) * 128, :]
                else:
                    src_t = self.xres_d
                    src = src_t[t * 128:(t + 1) * 128, :]
                x_ = xt[t % 2]; h_ = ht[t % 2]; p_ = pT[t % 2]
                self.dma(x_[:], src, [src_t], [x_])
                for kc in range(8):
                    self.tr(p_[:, kc * 128:(kc + 1) * 128], x_[:, kc * 128:(kc + 1) * 128], self.cst[:, C_ID:C_ID + 128],
                            [x_, self.cst], [p_])
                for kc in range(8):
                    self.act(h_[:, kc, :], p_[:, kc * 128:(kc + 1) * 128], AF.Identity, [p_, scale1, modT], [h_],
                             bias=modT[:, kc, typ:typ + 1], scale=scale1[:, kc, typ:typ + 1])
                c0 = colof(t * 128)
                self.dma(self.hv[:, :, c0:c0 + 128], h_[:], [h_], [self.hT_d])

    def load_w(self, dst_ap, dst_t, src_ap, src_t):
        self.dma(dst_ap, src_ap, [src_t], [dst_t], eng=POOL, slow=True)

    def bvec(self, st, name, l, n):
        t = self.sb(st, [128, n], F32, name)
        self.dma(t[:], self.I[name][l:l + 1, :].to_broadcast([128, n]), [self.I[name]], [t], slow=True)
        return t

    def interleave(self, factories, width):
        pending = list(factories)
        active = []
        for s in range(width):
            if pending:
                active.append((s, pending.pop(0)(s)))
        while active:
            nxt = []
            for (s, g) in active:
                try:
                    next(g)
                    nxt.append((s, g))
                except StopIteration:
                    if pending:
                        nxt.append((s, pending.pop(0)(s)))
            active = nxt

    def stage_S(self, l):
        I = self.I
        cst = self.cst
        import os as _os
        with ExitStack() as st:
            wv = I["w_in"][l].rearrange("(k p) c -> p k c", p=128)
            with ExitStack() as stf:
                wx = self.sb(stf, [128, 8, 1536], BF16, "wx")
                wdt = self.sb(stf, [128, 8, 16], BF16, "wdt")
                for kc in range(8):
                    self.load_w(wx[:, kc, :], wx, wv[:, kc, 1024:2560], I["w_in"])
                self.load_w(wdt[:], wdt, wv[:, :, 2560:2576], I["w_in"])
                convw = self.sb(stf, [128, 12, 5], F32, "convw")
                for k in range(5):
                    self.dma(convw[:, :, k], I["ssd_conv_w"][l, k].rearrange("(r p) -> p r", p=128), [I["ssd_conv_w"]], [convw], slow=True)
                convb = self.sb(stf, [128, 12], F32, "convb")
                self.dma(convb[:], I["ssd_conv_b"][l].rearrange("(r p) -> p r", p=128), [I["ssd_conv_b"]], [convb], slow=True)
                dg = convw
                cbrow = convb
                sets = []
                for s_ in range(2):
                    B = {}
                    B["hc"] = self.sb(stf, [128, 8, 132], BF16, "hcf")
                    B["xbc"] = self.sb(stf, [128, 12, 132], F32, "xbc")
                    B["acc"] = self.sb(stf, [128, 12, 128], F32, "acc")
                    B["tmp"] = [self.sb(stf, [128, 12, 128], F32, "ctmp") for _ in range(2)]
                    B["esb"] = self.sb(stf, [128, 1536], F32, "esbf")
                    B["ubf"] = self.sb(stf, [128, 12, 128], BF16, "ubf")
                    B["Fb"] = self.sb(stf, [128, 1536], BF16, "Fbw")
                    B["Ff"] = self.sb(stf, [128, 272], F32, "Ffw")
                    B["Q"] = [self.ps(stf, [128, 512], F32, "QF") for _ in range(4)]
                    sets.append(B)
                gens = [(lambda cc: (lambda slot: self.ssd_front(cc, sets[slot], wx, wdt, dg, cbrow)))(c) for c in range(NCH)]
                self.interleave(gens, 2)
            self.P.barrier()
            Dsk = self.bvec(st, "ssd_d", l, 16)
            prm = {}
            for d_, sfx in ((0, "f"), (1, "b")):
                al = self.bvec(st, "ssd_a_log_" + sfx, l, 16)
                self.act(al[:], al[:], AF.Exp, [al], [al])
                self.ts(al[:], al[:], -1.0, None, ALU.mult, None, [al], [al])
                dtb = self.bvec(st, "ssd_dt_bias_" + sfx, l, 16)
                Ub = self.sb(st, [128, 128], BF16, "Ub16")
                Lb = self.sb(st, [128, 128], BF16, "Lb16")
                Uo = C_UF if d_ == 0 else C_UB
                Lo = C_LF if d_ == 0 else C_LB
                self.cp(Ub[:], cst[:, Uo:Uo + 128], [cst], [Ub])
                self.cp(Lb[:], cst[:, Lo:Lo + 128], [cst], [Lb])
                prm[d_] = (al, dtb, Ub, Lb)
            with ExitStack() as st2:
                gens = []
                for d_ in (0, 1):
                    gens.append((lambda dd: (lambda slot: self.ssd_sweep(l, dd, st2, Dsk, prm[dd])))(d_))
                self.interleave(gens, 2)
            self.P.barrier()
            with ExitStack() as st3:
                wz = self.sb(st3, [128, 8, 1024], BF16, "wz")
                for kc in range(8):
                    self.load_w(wz[:, kc, :], wz, wv[:, kc, 0:1024], I["w_in"])
                nwB = self.bvec(st3, "ssd_norm_w", l, 1024)
                sets = []
                for s in range(2):
                    B = {}
                    B["hc"] = self.sb(st3, [128, 8, 128], BF16, "hc3")
                    B["ypf"] = self.sb(st3, [128, 1024], F32, "ypf")
                    B["ypb"] = self.sb(st3, [128, 1024], F32, "ypb")
                    B["e"] = self.sb(st3, [128, 1024], F32, "e3")
                    B["t1"] = self.sb(st3, [128, 1024], F32, "t13")
                    B["bst"] = self.sb(st3, [128, 2, 6], F32, "bst3")
                    B["ssq"] = self.sb(st3, [128, 2], F32, "ssq3")
                    B["ob"] = self.sb(st3, [128, 1024], BF16, "ob3")
                    B["oT"] = self.sb(st3, [128, 8, 128], BF16, "oT3")
                    B["PZ"] = [self.ps(st3, [128, 512], F32, "PZ3") for _ in range(2)]
                    B["PT"] = self.ps(st3, [128, 1024], BF16, "PT3")
                    sets.append(B)
                gens = [(lambda cc: (lambda slot: self.ssd_final(cc, sets[slot], wz, nwB)))(c) for c in range(NCH)]
                self.interleave(gens, 2)

    def ssd_final(self, c, B, wz, nwB):
        h_ = B["hc"]; ypf = B["ypf"]; ypb = B["ypb"]; e = B["e"]; t1 = B["t1"]; bst = B["bst"]; ssq = B["ssq"]
        ob = B["ob"]; oT_ = B["oT"]; PZ = B["PZ"]; PT = B["PT"]
        mixv = self.mixT_d[:].rearrange("(r p) t -> p r t", p=128)
        c0 = colof(c * 128)
        tok = slice(c * 128, (c + 1) * 128)
        self.dma(h_[:], self.hv[:, :, c0:c0 + 128], [self.hT_d], [h_], slow=True)
        self.dma(ypf[:], self.ypart_d[tok, :], [self.ypart_d], [ypf])
        self.dma(ypb[:], self.ypartb_d[tok, :], [self.ypartb_d], [ypb])
        yield
        for n in range(2):
            for kc in range(8):
                self.mm(PZ[n][:], h_[:, kc, :], wz[:, kc, n * 512:(n + 1) * 512], kc == 0, kc == 7, [h_, wz], [PZ[n]])
        self.tt(ypf[:], ypf[:], ypb[:], ALU.add, [ypf, ypb], [ypf])
        yield
        for n in range(2):
            self.act(e[:, n * 512:(n + 1) * 512], PZ[n][:], AF.Exp, [PZ[n]], [e], scale=-1.0)
        yield
        self.sigm(e[:], e)
        yield
        for n in range(2):
            self.tt(t1[:, n * 512:(n + 1) * 512], PZ[n][:], e[:, n * 512:(n + 1) * 512], ALU.mult, [PZ[n], e], [t1])
        yield
        self.tt(t1[:], t1[:], ypf[:], ALU.mult, [t1, ypf], [t1])
        yield
        for s_ in range(2):
            self.P.op(DVE, (lambda ss: (lambda en: en.bn_stats(out=bst[:, ss, :], in_=t1[:, ss * 512:(ss + 1) * 512])))(s_),
                      [t1.b], [bst.b])
        self.P.op(DVE, lambda en: en.bn_aggr(out=ssq[:], in_=bst[:]), [bst.b], [ssq.b])
        self.stt(ssq[:, 1:2], ssq[:, 0:1], ssq[:, 0:1], ssq[:, 1:2], ALU.mult, ALU.add, [ssq], [ssq])
        self.ts(ssq[:, 1:2], ssq[:, 1:2], RMS_EPS, None, ALU.add, None, [ssq], [ssq])
        yield
        self.rsqrt(ssq[:, 1:2], ssq)
        yield
        self.stt(ob[:], t1[:], ssq[:, 1:2], nwB[:], ALU.mult, ALU.mult, [t1, ssq, nwB], [ob])
        yield
        for r in range(8):
            self.tr(PT[:, r * 128:(r + 1) * 128], ob[:, r * 128:(r + 1) * 128], self.identb[:], [ob, self.identb], [PT])
        yield
        self.act(oT_[:], PT[:].rearrange("p (a b) -> p a b", a=8), AF.Copy, [PT], [oT_])
        yield
        self.dma(mixv[:, 0:8, tok], oT_[:], [oT_], [self.mixT_d], slow=True)
        yield

    def ssd_front(self, c, B, wx, wdt, dg, cbrow):
        h_ = B["hc"]; xbc = B["xbc"]; esb = B["esb"]; ubf = B["ubf"]; Fb = B["Fb"]; Ff = B["Ff"]; Q = B["Q"]
        Qb0 = Q[0][:].bitcast(BF16)
        Qb1 = Q[1][:].bitcast(BF16)
        c0 = colof(c * 128)
        self.dma(h_[:], self.hv[:, :, c0 - 2:c0 + 130], [self.hT_d], [h_], slow=True)
        yield
        for r in range(12):
            q_ = Q[r // 3]
            o_ = q_[:, (r % 3) * 132:(r % 3) * 132 + 132]
            for kc in range(8):
                self.mm(o_, wx[:, kc, r * 128:(r + 1) * 128], h_[:, kc, :], kc == 0, kc == 7, [wx, h_], [q_])
            if r % 3 == 2:
                yield
        for q in range(4):
            self.act(xbc[:, 3 * q:3 * q + 3, :], Q[q][:, 0:396].rearrange("p (a b) -> p a b", a=3), AF.Copy, [Q[q]], [xbc])
        yield
        for kc in range(8):
            self.mm(Q[3][:, 0:16], h_[:, kc, 2:130], wdt[:, kc, :], kc == 0, kc == 7, [h_, wdt], [Q[3]])
        yield
        acc = B["acc"]; tmp = B["tmp"]
        convw = dg; convb = cbrow
        self.tt(acc[:], xbc[:, :, 0:128], convw[:, :, 0:1].to_broadcast([128, 12, 128]), ALU.mult, [xbc, convw], [acc])
        for k in range(1, 5):
            tk_ = tmp[k % 2]
            self.tt(tk_[:], xbc[:, :, k:k + 128], convw[:, :, k:k + 1].to_broadcast([128, 12, 128]), ALU.mult, [xbc, convw], [tk_], eng=POOL)
            yield
            self.tt(acc[:], acc[:], tk_[:], ALU.add, [acc, tk_], [acc])
        self.cp(Ff[:, 256:272], Q[3][:, 0:16], [Q[3]], [Ff])
        yield
        self.tt(acc[:], acc[:], convb[:].unsqueeze(2).to_broadcast([128, 12, 128]), ALU.add, [acc, convb], [acc])
        yield
        self.act(esb[:], acc[:].rearrange("p a b -> p (a b)"), AF.Exp, [acc], [esb], scale=-1.0)
        yield
        self.sigm(esb[:], esb)
        yield
        self.tt(ubf[:], acc[:], esb[:].rearrange("p (a b) -> p a b", a=12), ALU.mult, [acc, esb], [ubf])
        yield
        for g in range(2):
            self.mm(Q[3][:, 256 + g * 128:256 + (g + 1) * 128], ubf[:, 8 + g, :], ubf[:, 10 + g, :], True, True, [ubf], [Q[3]])
        for r in range(8):
            self.tr(Qb0[:, r * 128:(r + 1) * 128], ubf[:, r, :], self.identb[:], [ubf, self.identb], [Q[0]])
        for r in range(2):
            self.tr(Qb1[:, r * 128:(r + 1) * 128], ubf[:, 8 + r, :], self.identb[:], [ubf, self.identb], [Q[1]])
        self.cp(Fb[:, 1280:1536], ubf[:, 10:12, :].rearrange("p a b -> p (a b)"), [ubf], [Fb], eng=POOL)
        yield
        self.cp(Ff[:, 0:256], Q[3][:, 256:512], [Q[3]], [Ff])
        self.act(Fb[:, 0:1024], Qb0[:, 0:1024], AF.Copy, [Q[0]], [Fb])
        self.cp(Fb[:, 1024:1280], Qb1[:, 0:256], [Q[1]], [Fb])
        yield
        self.dma(self.Fb_d[c], Fb[:], [Fb], [self.Fb_d])
        self.dma(self.Ff_d[c], Ff[:], [Ff], [self.Ff_d])
        yield

    def ssd_sweep(self, l, d_, st, Dsk, prm):
        cst = self.cst
        al, dtb, Ub, Lb = prm
        H = self.sb(st, [128, 1024], F32, "H")
        Hbf = self.sb(st, [128, 1024], BF16, "Hbf")
        Fb = [self.sb(st, [128, 1536], BF16, "Fbr") for _ in range(2)]
        Ff = [self.sb(st, [128, 272], F32, "Ffr") for _ in range(2)]
        esb = self.sb(st, [128, 2048], F32, "esb")
        dtx = self.sb(st, [128, 16], F32, "dtx")
        dt = self.sb(st, [128, 16], F32, "dt")
        la = self.sb(st, [128, 16], F32, "la")
        lah = self.sb(st, [128, 16], BF16, "lah")
        lah32 = self.sb(st, [128, 16], F32, "lah32")
        lal32 = self.sb(st, [128, 16], F32, "lal32")
        lalo = self.sb(st, [128, 16], BF16, "lalo")
        E3 = self.sb(st, [128, 48], F32, "E3")
        scm = self.sb(st, [128, 2, 128], F32, "scm")
        LaUh = self.sb(st, [128, 16, 128], BF16, "LaUh")
        LaUl = self.sb(st, [128, 16, 128], BF16, "LaUl")
        M = self.sb(st, [128, 16, 128], BF16, "M")
        v = self.sb(st, [128, 1024], BF16, "v")
        vte = self.sb(st, [128, 1024], BF16, "vte")
        t1 = self.sb(st, [128, 1024], F32, "t1")
        t2 = self.sb(st, [128, 1024], F32, "t2")
        yp = [self.sb(st, [128, 1024], F32, "yp") for _ in range(2)]
        Q = [self.ps(st, [128, 512], F32, "Q") for _ in range(4)]
        ydst = self.ypart_d if d_ == 0 else self.ypartb_d
        order = list(range(NCH)) if d_ == 0 else [1, 0] + list(range(NCH - 1, 1, -1))
        Uo = C_UF if d_ == 0 else C_UB
        Lo = C_LF if d_ == 0 else C_LB
        Mo = C_MF if d_ == 0 else C_MB
        self.memset(H[:], 0.0, [H])
        self.memset(Hbf[:], 0.0, [Hbf])

        def loads(ci, slot):
            c = order[ci]
            self.dma(Fb[slot][:], self.Fb_d[c], [self.Fb_d], [Fb[slot]])
            self.dma(Ff[slot][:], self.Ff_d[c], [self.Ff_d], [Ff[slot]])
        loads(0, 0)
        it = 0
        for ci, c in enumerate(order):
            fb = Fb[it % 2]; ff = Ff[it % 2]; ypt = yp[it % 2]
            it += 1
            tok = slice(c * 128, (c + 1) * 128)
            if ci + 1 < len(order):
                loads(ci + 1, it % 2)
            xs_tok = fb[:, 0:1024]
            self.tt(dtx[:], ff[:, 256:272], dtb[:], ALU.add, [ff, dtb], [dtx])
            self.tt(scm[:], ff[:, 0:256].rearrange("p (a b) -> p a b", a=2),
                    cst[:, Mo:Mo + 128].unsqueeze(1).to_broadcast([128, 2, 128]), ALU.mult, [ff, cst], [scm])
            yield
            self.act(dtx[:], dtx[:], AF.Exp, [dtx], [dtx])
            self.act(dt[:], dtx[:], AF.Ln, [dtx], [dt], bias=1.0)
            yield
            self.tt(la[:], dt[:], al[:], ALU.mult, [dt, al], [la])
            self.cp(lah[:], la[:], [la], [lah])
            self.cp(lah32[:], lah[:], [lah], [lah32])
            self.tt(lal32[:], la[:], lah32[:], ALU.subtract, [la, lah32], [lal32])
            self.cp(lalo[:], lal32[:], [lal32], [lalo])
            yield
            self.mm(Q[3][:, 16:32], cst[:, Uo:Uo + 128], la[:], True, True, [cst, la], [Q[3]])
            self.mm(Q[3][:, 32:48], cst[:, Lo:Lo + 128], la[:], True, True, [cst, la], [Q[3]])
            self.mm(Q[3][:, 48:64], self.ones[:], la[:], True, True, [self.ones, la], [Q[3]])
            self.tt(LaUh[:], lah[:].unsqueeze(2).to_broadcast([128, 16, 128]),
                    Ub[:].unsqueeze(1).to_broadcast([128, 16, 128]), ALU.mult, [lah, Ub], [LaUh], eng=POOL)
            self.tt(LaUl[:], lalo[:].unsqueeze(2).to_broadcast([128, 16, 128]),
                    Ub[:].unsqueeze(1).to_broadcast([128, 16, 128]), ALU.mult, [lalo, Ub], [LaUl])
            self.tt(v[:].rearrange("p (h e) -> p h e", h=16), xs_tok.rearrange("p (h e) -> p h e", h=16),
                    dt[:].unsqueeze(2).to_broadcast([128, 16, 64]), ALU.mult, [fb, dt], [v], eng=POOL)
            yield
            self.act(E3[:], Q[3][:, 16:64], AF.Exp, [Q[3]], [E3])
            yield
            for q in range(4):
                self.mm(Q[q][:], Lb[:], LaUh[:, 4 * q:4 * q + 4, :].rearrange("p a b -> p (a b)"), True, False, [Lb, LaUh], [Q[q]])
                self.mm(Q[q][:], Lb[:], LaUl[:, 4 * q:4 * q + 4, :].rearrange("p a b -> p (a b)"), False, True, [Lb, LaUl], [Q[q]])
                if q % 2 == 1:
                    yield
            self.tt(vte[:].rearrange("p (h e) -> p h e", h=16), v[:].rearrange("p (h e) -> p h e", h=16),
                    E3[:, 16:32].unsqueeze(2).to_broadcast([128, 16, 64]), ALU.mult, [v, E3], [vte], eng=POOL)
            for q in range(4):
                self.act(esb[:, q * 512:(q + 1) * 512], Q[q][:], AF.Exp, [Q[q]], [esb])
            yield
            self.tt(M[:].rearrange("p (g a) b -> p g a b", g=2), esb[:].rearrange("p (g a b) -> p g a b", g=2, a=8),
                    scm[:].unsqueeze(2).to_broadcast([128, 2, 8, 128]), ALU.mult, [esb, scm], [M])
            yield
            for g in range(2):
                self.mm(Q[2 + g][:], fb[:, 1280 + g * 128:1280 + (g + 1) * 128], Hbf[:, g * 512:(g + 1) * 512], True, True, [fb, Hbf], [Q[2 + g]])
            for h in range(16):
                q_ = Q[h // 8]
                self.mm(q_[:, (h % 8) * 64:(h % 8 + 1) * 64], M[:, h, :], v[:, h * 64:(h + 1) * 64], True, True, [M, v], [q_])
                if h % 8 == 7:
                    yield
            for g in range(2):
                self.tt(t1[:, g * 512:(g + 1) * 512].rearrange("p (h e) -> p h e", h=8),
                        Q[2 + g][:].rearrange("p (h e) -> p h e", h=8),
                        E3[:, g * 8:(g + 1) * 8].unsqueeze(2).to_broadcast([128, 8, 64]), ALU.mult, [Q[2 + g], E3], [t1])
            yield
            for g in range(2):
                self.tt(t2[:, g * 512:(g + 1) * 512], Q[g][:], t1[:, g * 512:(g + 1) * 512], ALU.add, [Q[g], t1], [t2])
            yield
            for g in range(2):
                self.mm(Q[g][:], fb[:, 1024 + g * 128:1024 + (g + 1) * 128], vte[:, g * 512:(g + 1) * 512], True, True, [fb, vte], [Q[g]])
            if d_ == 0:
                self.tt(t1[:].rearrange("p (h e) -> p h e", h=16), xs_tok.rearrange("p (h e) -> p h e", h=16),
                        Dsk[:].unsqueeze(2).to_broadcast([128, 16, 64]), ALU.mult, [fb, Dsk], [t1], eng=POOL)
                self.tt(ypt[:], t1[:], t2[:], ALU.add, [t1, t2], [ypt], eng=POOL)
            else:
                self.cp(ypt[:], t2[:], [t2], [ypt], eng=POOL)
            self.dma(ydst[tok, :], ypt[:], [ypt], [ydst])
            self.tt(H[:].rearrange("p (h e) -> p h e", h=16), H[:].rearrange("p (h e) -> p h e", h=16),
                    E3[:, 32:48].unsqueeze(2).to_broadcast([128, 16, 64]), ALU.mult, [H, E3], [H])
            yield
            for g in range(2):
                self.tt(H[:, g * 512:(g + 1) * 512], H[:, g * 512:(g + 1) * 512], Q[g][:], ALU.add, [H, Q[g]], [H])
            yield
            self.act(Hbf[:], H[:], AF.Copy, [H], [Hbf])
            yield

    def stage_R(self, l):
        I = self.I
        cst = self.cst
        import os as _os
        with ExitStack() as st:
            wv = I["w_in"][l].rearrange("(k p) c -> p k c", p=128)
            wqk = self.sb(st, [128, 8, 1024], BF16, "wqk")
            wvv = self.sb(st, [128, 8, 512], BF16, "wvr")
            for kc in range(8):
                self.load_w(wqk[:, kc, 0:512], wqk, wv[:, kc, 3760:4272], I["w_in"])
                self.load_w(wqk[:, kc, 512:1024], wqk, wv[:, kc, 5328:5840], I["w_in"])
                self.load_w(wvv[:, kc, :], wvv, wv[:, kc, 4272:4784], I["w_in"])
            prm = {}
            for d_, sfx in ((0, "f"), (1, "b")):
                nm = "ret_log_rate_" + sfx
                lgB = self.bvec(st, nm, l, 4)
                self.act(lgB[:], lgB[:], AF.Exp, [lgB], [lgB])
                self.ts(lgB[:], lgB[:], -1.0, None, ALU.mult, None, [lgB], [lgB])
                lgs = self.sb(st, [128, 2], F32, "lgs")
                src = I[nm][l:l + 1, :].rearrange("o (p two) -> o p two", two=2)
                self.dma(lgs[0:64, :], src[:, :, 0].to_broadcast([64, 2]), [I[nm]], [lgs], slow=True)
                self.dma(lgs[64:128, :], src[:, :, 1].to_broadcast([64, 2]), [I[nm]], [lgs], slow=True)
                self.act(lgs[:], lgs[:], AF.Exp, [lgs], [lgs])
                self.ts(lgs[:], lgs[:], -1.0, None, ALU.mult, None, [lgs], [lgs])
                RIo = C_RIF if d_ == 0 else C_RIB
                Mo = C_MF if d_ == 0 else C_MB
                Go = C_GF if d_ == 0 else C_GB
                To = C_TEF if d_ == 0 else C_TEB
                DmT = self.sb(st, [128, 4, 128], F32, "DmT")
                for h in range(4):
                    self.act(DmT[:, h, :], cst[:, RIo:RIo + 128], AF.Exp, [cst, lgB], [DmT], scale=lgB[:, h:h + 1])
                self.tt(DmT[:], DmT[:], cst[:, Mo:Mo + 128].unsqueeze(1).to_broadcast([128, 4, 128]), ALU.mult, [DmT, cst], [DmT])
                Gam = self.sb(st, [128, 2, 128], F32, "Gam")
                for p in range(2):
                    self.act(Gam[:, p, :], cst[:, Go:Go + 128], AF.Exp, [cst, lgs], [Gam], scale=lgs[:, p:p + 1])
                te = self.sb(st, [128, 4], F32, "te")
                self.act(te[:], lgB[:], AF.Exp, [lgB, cst], [te], scale=cst[:, To:To + 1])
                g128 = self.sb(st, [128, 2], F32, "g128")
                self.act(g128[:], lgs[:], AF.Exp, [lgs], [g128], scale=128.0)
                prm[d_] = (DmT, Gam, te, g128)
            with ExitStack() as stf:
                sets = []
                for s_ in range(2):
                    B = {}
                    B["hc"] = self.sb(stf, [128, 8, 128], BF16, "hcrf")
                    B["tab"] = self.sb(stf, [128, 1024], F32, "tabrf")
                    B["r1"] = self.sb(stf, [128, 512], F32, "r1")
                    B["r2"] = self.sb(stf, [128, 512], F32, "r2")
                    B["qkt"] = self.sb(stf, [128, 512], BF16, "qkt")
                    B["Rb"] = self.sb(stf, [128, 1280], BF16, "Rbw")
                    B["Q"] = [self.ps(stf, [128, 512], F32, "QRF") for _ in range(4)]
                    sets.append(B)
                gens = [(lambda cc: (lambda slot: self.ret_front(cc, sets[slot], wqk, wvv)))(c) for c in range(NCH)]
                self.interleave(gens, 2)
            self.P.barrier()
            with ExitStack() as st2:
                gens = []
                for d_ in (0, 1):
                    gens.append((lambda dd: (lambda slot: self.ret_sweep(l, dd, st2, prm[dd])))(d_))
                self.interleave(gens, 2)
            self.P.barrier()
            with ExitStack() as st3:
                wg = self.sb(st3, [128, 8, 512], BF16, "wg")
                for kc in range(8):
                    self.load_w(wg[:, kc, :], wg, wv[:, kc, 4784:5296], I["w_in"])
                sets = []
                for s in range(2):
                    B = {}
                    B["hc"] = self.sb(st3, [128, 8, 128], BF16, "hcr3")
                    B["rpf"] = self.sb(st3, [128, 512], F32, "rpf")
                    B["rpb"] = self.sb(st3, [128, 512], F32, "rpb")
                    B["ge"] = self.sb(st3, [128, 512], F32, "ge3")
                    B["sg"] = self.sb(st3, [128, 512], F32, "sg3")
                    B["stats"] = self.sb(st3, [128, 4, 6], F32, "stats3")
                    B["mv"] = self.sb(st3, [128, 4, 2], F32, "mv3")
                    B["yn"] = self.sb(st3, [128, 512], F32, "yn3")
                    B["ob"] = self.sb(st3, [128, 512], BF16, "obr3")
                    B["oT"] = self.sb(st3, [128, 4, 128], BF16, "oTr3")
                    B["PG"] = self.ps(st3, [128, 512], F32, "PGr3")
                    B["PT"] = self.ps(st3, [128, 1024], BF16, "PTr3")
                    sets.append(B)
                gens = [(lambda cc: (lambda slot: self.ret_final(cc, sets[slot], wg)))(c) for c in range(NCH)]
                self.interleave(gens, 2)

    def ret_final(self, c, B, wg):
        h_ = B["hc"]; rpf = B["rpf"]; rpb = B["rpb"]; ge = B["ge"]; sg = B["sg"]; stats = B["stats"]; mv = B["mv"]
        yn = B["yn"]; ob = B["ob"]; oT_ = B["oT"]; PG = B["PG"]; PT = B["PT"]
        mixv = self.mixT_d[:].rearrange("(r p) t -> p r t", p=128)
        c0 = colof(c * 128)
        tok = slice(c * 128, (c + 1) * 128)
        self.dma(h_[:], self.hv[:, :, c0:c0 + 128], [self.hT_d], [h_], slow=True)
        self.dma(rpf[:], self.rpart_d[tok, :], [self.rpart_d], [rpf])
        self.dma(rpb[:], self.rpartb_d[tok, :], [self.rpartb_d], [rpb])
        yield
        for kc in range(8):
            self.mm(PG[:], h_[:, kc, :], wg[:, kc, :], kc == 0, kc == 7, [h_, wg], [PG])
        self.tt(rpf[:], rpf[:], rpb[:], ALU.add, [rpf, rpb], [rpf])
        yield
        self.act(ge[:], PG[:], AF.Exp, [PG], [ge], scale=-1.0)
        for h in range(4):
            self.P.op(DVE, (lambda hh: (lambda e: e.bn_stats(out=stats[:, hh, :], in_=rpf[:, hh * 128:(hh + 1) * 128])))(h),
                      [rpf.b], [stats.b])
            self.P.op(DVE, (lambda hh: (lambda e: e.bn_aggr(out=mv[:, hh, :], in_=stats[:, hh, :])))(h),
                      [stats.b], [mv.b])
        self.ts(mv[:, :, 1], mv[:, :, 1], LN_EPS, None, ALU.add, None, [mv], [mv])
        yield
        self.rsqrt(mv[:, :, 1], mv)
        self.sigm(ge[:], ge)
        yield
        self.tt(sg[:], PG[:], ge[:], ALU.mult, [PG, ge], [sg])
        for h in range(4):
            self.ts(yn[:, h * 128:(h + 1) * 128], rpf[:, h * 128:(h + 1) * 128], mv[:, h, 0:1], mv[:, h, 1:2],
                    ALU.subtract, ALU.mult, [rpf, mv], [yn])
        yield
        self.tt(ob[:], yn[:], sg[:], ALU.mult, [yn, sg], [ob])
        yield
        for h in range(4):
            self.tr(PT[:, h * 128:(h + 1) * 128], ob[:, h * 128:(h + 1) * 128], self.identb[:], [ob, self.identb], [PT])
        yield
        self.act(oT_[:], PT[:, 0:512].rearrange("p (a b) -> p a b", a=4), AF.Copy, [PT], [oT_])
        yield
        self.dma(mixv[:, 12:16, tok], oT_[:], [oT_], [self.mixT_d], slow=True)
        yield

    def ret_front(self, c, B, wqk, wvv):
        I = self.I
        h_ = B["hc"]; tab = B["tab"]; r1 = B["r1"]; r2 = B["r2"]; qkt = B["qkt"]; Rb = B["Rb"]; Q = B["Q"]
        Qb3 = Q[3][:].bitcast(BF16)
        c0 = colof(c * 128)
        self.dma(h_[:], self.hv[:, :, c0:c0 + 128], [self.hT_d], [h_], slow=True)
        self.dma(tab[:], I["ret_tab2"][c * 128:(c + 1) * 128, :], [I["ret_tab2"]], [tab])
        yield
        for j, (q_, w_, lo) in enumerate(((Q[0], wqk, 0), (Q[1], wqk, 512), (Q[2], wvv, 0))):
            for kc in range(8):
                self.mm(q_[:], h_[:, kc, :], w_[:, kc, lo:lo + 512], kc == 0, kc == 7, [h_, w_], [q_])
            yield
        self.tt(r1[:], Q[0][:], tab[:, 0:512], ALU.mult, [Q[0], tab], [r1])
        yield
        self.tt(r2[:], Q[1][:], tab[:, 512:1024], ALU.mult, [Q[1], tab], [r2])
        self.act(Rb[:, 768:1280], Q[2][:], AF.Copy, [Q[2]], [Rb])
        yield
        self.tt(qkt[:], r1[:], r2[:], ALU.add, [r1, r2], [qkt], eng=POOL)
        yield
        for j in range(4):
            self.tr(Qb3[:, j * 128:(j + 1) * 128], qkt[:, j * 128:(j + 1) * 128], self.identb[:], [qkt, self.identb], [Q[3]])
        self.cp(Rb[:, 512:768], qkt[:, 256:512], [qkt], [Rb], eng=POOL)
        yield
        self.act(Rb[:, 0:512], Qb3[:, 0:512], AF.Copy, [Q[3]], [Rb])
        yield
        self.dma(self.Rb_d[c], Rb[:], [Rb], [self.Rb_d])
        yield

    def ret_sweep(self, l, d_, st, prm):
        DmT, Gam, te, g128 = prm
        S = self.sb(st, [128, 2, 128], F32, "S")
        Sbf = self.sb(st, [128, 2, 128], BF16, "Sbf")
        Rb = [self.sb(st, [128, 1280], BF16, "Rbr") for _ in range(2)]
        qz = [self.sb(st, [128, 2, 128], BF16, "qz") for _ in range(2)]
        qdz = [self.sb(st, [128, 2, 128], BF16, "qdz") for _ in range(2)]
        for par in range(2):
            self.memset(qz[par][:], 0.0, [qz[par]])
            self.memset(qdz[par][:], 0.0, [qdz[par]])
        vte = self.sb(st, [128, 512], BF16, "vte")
        Mr = self.sb(st, [128, 4, 128], BF16, "Mr")
        yp = [self.sb(st, [128, 512], F32, "ypr") for _ in range(2)]
        Q = [self.ps(st, [128, 512], F32, "QR") for _ in range(3)]
        ydst = self.rpart_d if d_ == 0 else self.rpartb_d
        order = list(range(NCH)) if d_ == 0 else [1, 0] + list(range(NCH - 1, 1, -1))
        self.memset(S[:], 0.0, [S])
        self.memset(Sbf[:], 0.0, [Sbf])
        self.dma(Rb[0][:], self.Rb_d[order[0]], [self.Rb_d], [Rb[0]])
        it = 0
        for ci, c in enumerate(order):
            rb = Rb[it % 2]; ypt = yp[it % 2]
            it += 1
            tok = slice(c * 128, (c + 1) * 128)
            if ci + 1 < len(order):
                self.dma(Rb[it % 2][:], self.Rb_d[order[ci + 1]], [self.Rb_d], [Rb[it % 2]])
            qk = rb[:, 0:512].rearrange("p (a b) -> p a b", a=4)
            for par in range(2):
                rr = 64 * par
                self.cp(qz[par][rr:rr + 64, :, :], qk[rr:rr + 64, 0:2, :], [rb], [qz[par]], eng=POOL)
                self.tt(qdz[par][rr:rr + 64, :, :], qk[rr:rr + 64, 0:2, :], Gam[rr:rr + 64, :, :], ALU.mult,
                        [rb, Gam], [qdz[par]], eng=POOL)
            self.tt(vte[:].rearrange("p (h e) -> p h e", h=4), rb[:, 768:1280].rearrange("p (h e) -> p h e", h=4),
                    te[:].unsqueeze(2).to_broadcast([128, 4, 128]), ALU.mult, [rb, te], [vte], eng=POOL)
            yield
            for h in range(4):
                p = h // 2
                self.mm(Q[0][:, h * 128:(h + 1) * 128], qk[:, 2 + p, :], qz[h % 2][:, p, :], True, True, [rb, qz[h % 2]], [Q[0]])
            yield
            self.tt(Mr[:], Q[0][:].rearrange("p (a b) -> p a b", a=4), DmT[:], ALU.mult, [Q[0], DmT], [Mr])
            yield
            for h in range(4):
                p = h // 2
                self.mm(Q[1][:, h * 128:(h + 1) * 128], Mr[:, h, :], rb[:, 768 + h * 128:768 + (h + 1) * 128], True, False, [Mr, rb], [Q[1]])
                self.mm(Q[1][:, h * 128:(h + 1) * 128], qdz[h % 2][:, p, :], Sbf[:, p, :], False, True, [qdz[h % 2], Sbf], [Q[1]])
            for h in range(4):
                p = h // 2
                self.mm(Q[2][:, h * 128:(h + 1) * 128], rb[:, 512 + p * 128:512 + (p + 1) * 128], vte[:, h * 128:(h + 1) * 128],
                        True, True, [rb, vte], [Q[2]])
            yield
            self.cp(ypt[:], Q[1][:], [Q[1]], [ypt])
            self.dma(ydst[tok, :], ypt[:], [ypt], [ydst])
            for h in range(4):
                p, r0 = h // 2, (h % 2) * 64
                self.stt(S[r0:r0 + 64, p, :], S[r0:r0 + 64, p, :], g128[r0:r0 + 64, p:p + 1], Q[2][r0:r0 + 64, h * 128:(h + 1) * 128],
                         ALU.mult, ALU.add, [S, g128, Q[2]], [S])
            yield
            self.act(Sbf[:], S[:], AF.Copy, [S], [Sbf])
            yield

    def stage_M(self, l):
        I = self.I
        cst = self.cst
        blocks = [(0, 256)] + [(256 + 512 * i, 512) for i in range(8)]
        with ExitStack() as st1:
            v_all = self.sb(st1, [128, NCH, 512], BF16, "v_all")
            with ExitStack() as st:
                wv = I["w_in"][l].rearrange("(k p) c -> p k c", p=128)
                wm = self.sb(st, [128, 8, 704], BF16, "wm")
                wgate = self.sb(st, [128, 8, 512], BF16, "wgate")
                self.load_w(wm[:, :, 0:672], wm, wv[:, :, 2576:3248], I["w_in"])
                self.load_w(wm[:, :, 672:704], wm, wv[:, :, 5296:5328], I["w_in"])
                self.load_w(wgate[:], wgate, wv[:, :, 3248:3760], I["w_in"])
                wuq = self.sb(st, [128, 3, 8, 96], BF16, "wuq")
                wuqs = self.sb(st, [128, 3, 8, 96], BF16, "wuqs")
                wkp = self.sb(st, [128, 2, 8, 96], BF16, "wkp")
                wvv = self.sb(st, [128, 2, 8, 64], BF16, "wvv")
                self.memset(wuqs[:], 0.0, [wuqs])
                self.memset(wkp[:], 0.0, [wkp])
                uqv = I["mla_w_uq"][l].rearrange("(k p) (h e) -> p k h e", p=128, h=8)
                uqs = I["w_uq_sw"][l].rearrange("(k p) (h e) -> p k h e", p=128, h=8)
                ukv = I["mla_w_ukv"][l].rearrange("(k p) (h e) -> p k h e", p=128, h=8)
                for kc in range(3):
                    self.load_w(wuq[:, kc, :, :], wuq, uqv[:, kc, :, :], I["mla_w_uq"])
                    self.load_w(wuqs[:, kc, :, 64:96], wuqs, uqs[:, kc, :, :], I["w_uq_sw"])
                for kc in range(2):
                    self.load_w(wkp[:, kc, :, 0:64], wkp, ukv[:, kc, :, 0:64], I["mla_w_ukv"])
                    self.load_w(wvv[:, kc, :, :], wvv, ukv[:, kc, :, 64:128], I["mla_w_ukv"])
                esel = self.sb(st, [32, 96], BF16, "esel")
                self.memset(esel[:], 0.0, [esel])
                self.cp(esel[:, 64:96], self.identb[0:32, 0:32], [self.identb, esel], [esel])
                qn = self.sb(st, [128, 3], F32, "qn")
                kvn = self.sb(st, [128, 2], F32, "kvn")
                self.dma(qn[:], I["mla_q_norm"][l].rearrange("(k p) -> p k", p=128), [I["mla_q_norm"]], [qn], slow=True)
                self.dma(kvn[:], I["mla_kv_norm"][l].rearrange("(k p) -> p k", p=128), [I["mla_kv_norm"]], [kvn], slow=True)
                hb = [self.sb(st, [128, 8, 512], BF16, "hb") for _ in range(2)]
                tq = self.sb(st, [96, 2, 512], F32, "tq")
                self.memset(tq[0:64, 0, :], 1.0, [tq])
                self.memset(tq[0:64, 1, :], 0.0, [tq])
                tk = self.sb(st, [32, 2, 512], F32, "tk")
                cqs = self.sb(st, [128, 3, 512], F32, "cqs")
                sqs = self.sb(st, [128, 512], F32, "sqs")
                rstd = self.sb(st, [128, 512], F32, "rstd")
                cqn = self.sb(st, [128, 3, 512], BF16, "cqn")
                ckvn = self.sb(st, [128, 2, 512], BF16, "ckvn")
                kr1 = self.sb(st, [32, 512], F32, "kr1")
                kr2 = self.sb(st, [32, 512], F32, "kr2")
                krr = self.sb(st, [32, 512], BF16, "krr")
                q1 = self.sb(st, [96, 512], F32, "q1")
                q2 = self.sb(st, [96, 512], F32, "q2")
                qf = [self.sb(st, [96, 512], BF16, "qf") for _ in range(2)]
                kf = [self.sb(st, [96, 512], BF16, "kf") for _ in range(2)]
                ge = self.sb(st, [128, 512], F32, "ge")
                sgo = [self.sb(st, [128, 512], F32, "sgo") for _ in range(2)]
                P0 = [self.ps(st, [128, 512], F32, "P0") for _ in range(2)]
                PSS = self.ps(st, [128, 512], F32, "PSS")
                PQ1 = self.ps(st, [128, 512], F32, "PQ1")
                PQ2 = self.ps(st, [128, 512], F32, "PQ2")
                PK = self.ps(st, [128, 512], F32, "PK")
                PVv = self.ps(st, [128, 512], F32, "PVv")
                PG = self.ps(st, [128, 512], F32, "PG")
                sgv = self.sgT_d[:].rearrange("(r p) t -> p r t", p=128)
                ctr = 0
                for bi, (t0, n) in enumerate(blocks):
                    h_ = hb[bi % 2]
                    c0 = colof(t0)
                    self.dma(h_[:, :, 0:n], self.hv[:, :, c0:c0 + n], [self.hT_d], [h_], slow=True)
                    self.dma(tq[64:96, :, 0:n], I["mla_tab"][:, :, t0:t0 + n], [I["mla_tab"]], [tq], slow=True)
                    self.dma(tk[:, :, 0:n], I["mla_tab"][:, :, t0:t0 + n], [I["mla_tab"]], [tk], slow=True)
                    for (nrc, off, dst, nrm, dim, keep) in ((3, 0, cqn, qn, 384.0, None), (2, 384, ckvn, kvn, 256.0, None)):
                        for rc in range(nrc):
                            p_ = P0[ctr % 2]; ctr += 1
                            for kc in range(8):
                                self.mm(p_[:, 0:n], wm[:, kc, off + rc * 128:off + (rc + 1) * 128], h_[:, kc, 0:n], kc == 0, kc == 7, [wm, h_], [p_])
                            self.act(cqs[:, rc, 0:n], p_[:, 0:n], AF.Copy, [p_], [cqs])
                            self.act(sqs[:, 0:n], p_[:, 0:n], AF.Square, [p_], [sqs])
                            self.mm(PSS[:, 0:n], self.ones[:], sqs[:, 0:n], rc == 0, rc == nrc - 1, [self.ones, sqs], [PSS])
                        self.ts(rstd[:, 0:n], PSS[:, 0:n], 1.0 / dim, RMS_EPS, ALU.mult, ALU.add, [PSS], [rstd])
                        self.rsqrt(rstd[:, 0:n], rstd)
                        for rc in range(nrc):
                            self.stt(dst[:, rc, 0:n], cqs[:, rc, 0:n], nrm[:, rc:rc + 1], rstd[:, 0:n], ALU.mult, ALU.mult, [cqs, nrm, rstd], [dst])
                    self.mm_group_kr(h_, n, wm, PQ1, PQ2)
                    self.tt(kr1[:, 0:n], PQ1[0:32, 0:n], tk[:, 0, 0:n], ALU.mult, [PQ1, tk], [kr1])
                    self.tt(kr2[:, 0:n], PQ2[0:32, 0:n], tk[:, 1, 0:n], ALU.mult, [PQ2, tk], [kr2])
                    self.tt(krr[:, 0:n], kr1[:, 0:n], kr2[:, 0:n], ALU.add, [kr1, kr2], [krr])
                    for h in range(8):
                        qf_ = qf[h % 2]; kf_ = kf[h % 2]
                        for kc in range(3):
                            self.mm(PQ1[0:96, 0:n], wuq[:, kc, h, :], cqn[:, kc, 0:n], kc == 0, kc == 2, [wuq, cqn], [PQ1])
                        for kc in range(3):
                            self.mm(PQ2[0:96, 0:n], wuqs[:, kc, h, :], cqn[:, kc, 0:n], kc == 0, kc == 2, [wuqs, cqn], [PQ2])
                        self.tt(q1[:, 0:n], PQ1[0:96, 0:n], tq[:, 0, 0:n], ALU.mult, [PQ1, tq], [q1])
                        self.tt(q2[:, 0:n], PQ2[0:96, 0:n], tq[:, 1, 0:n], ALU.mult, [PQ2, tq], [q2])
                        self.tt(qf_[:, 0:n], q1[:, 0:n], q2[:, 0:n], ALU.add, [q1, q2], [qf_], eng=POOL)
                        self.dma(self.qT_d[h, :, t0:t0 + n], qf_[:, 0:n], [qf_], [self.qT_d])
                        for kc in range(2):
                            self.mm(PK[0:96, 0:n], wkp[:, kc, h, :], ckvn[:, kc, 0:n], kc == 0, False, [wkp, ckvn], [PK])
                        self.mm(PK[0:96, 0:n], esel[:], krr[:, 0:n], False, True, [esel, krr], [PK])
                        self.act(kf_[:, 0:n], PK[0:96, 0:n], AF.Copy, [PK], [kf_])
                        self.dma(self.kfT_d[h, :, t0:t0 + n], kf_[:, 0:n], [kf_], [self.kfT_d])
                    for s in range(n // 128):
                        ch = (t0 + s * 128) // 128
                        for kc in range(2):
                            self.mm(PVv[:], ckvn[:, kc, s * 128:(s + 1) * 128], wvv[:, kc, :, :].rearrange("p h e -> p (h e)"),
                                    kc == 0, kc == 1, [ckvn, wvv], [PVv])
                        self.act(v_all[:, ch, :], PVv[:], AF.Copy, [PVv], [v_all])
                    for rc in range(4):
                        sg_ = sgo[rc % 2]
                        for kc in range(8):
                            self.mm(PG[:, 0:n], wgate[:, kc, rc * 128:(rc + 1) * 128], h_[:, kc, 0:n], kc == 0, kc == 7, [wgate, h_], [PG])
                        self.act(ge[:, 0:n], PG[:, 0:n], AF.Exp, [PG], [ge], scale=-1.0)
                        self.sigm(ge[:, 0:n], ge)
                        self.tt(sg_[:, 0:n], PG[:, 0:n], ge[:, 0:n], ALU.mult, [PG, ge], [sg_])
                        self.dma(sgv[:, rc, t0:t0 + n], sg_[:, 0:n], [sg_], [self.sgT_d])
            self.P.barrier()
            import os as _os
            if _os.environ.get("NO_M2"):
                return
            with ExitStack() as st:
                kfh = [self.sb(st, [96, NTOK], BF16, "kfh") for _ in range(2)]
                qh = [self.sb(st, [96, NTOK], BF16, "qh") for _ in range(2)]
                vaug = [self.sb(st, [128, NCH, 128], BF16, "vaug") for _ in range(2)]
                self.memset(vaug[0][:, :, 64:128], 1.0, [vaug[0]])
                self.memset(vaug[1][:, :, 0:64], 1.0, [vaug[1]])
                pT = [self.sb(st, [128, 512], BF16, "pT") for _ in range(3)]
                sgh = [self.sb(st, [128, 512], F32, "sgh") for _ in range(2)]
                rden = self.sb(st, [128, 512], F32, "rden")
                ot = self.sb(st, [128, 512], F32, "ot")
                ob = [self.sb(st, [128, 512], BF16, "ob") for _ in range(2)]
                PSc = [self.ps(st, [128, 512], F32, "PSc") for _ in range(3)]
                PO = [self.ps(st, [128, 512], F32, "PO") for _ in range(2)]
                ci = 0
                bi_ = 0
                for h in range(int(_os.environ.get("M2_HEADS", "8"))):
                    par = h % 2
                    r0 = 64 * par
                    d0 = 64 - r0
                    kf_ = kfh[h % 2]; q_ = qh[h % 2]; va = vaug[par]
                    self.dma(kf_[:], self.kfT_d[h], [self.kfT_d], [kf_])
                    self.dma(q_[:], self.qT_d[h], [self.qT_d], [q_])
                    self.cp(va[:, :, r0:r0 + 64], v_all[:, :, h * 64:(h + 1) * 64], [v_all], [va], eng=POOL)
                    its = []
                    for (t0, n) in blocks:
                        if t0 == 0:
                            if l == DEPTH - 1:
                                continue
                            kcs = [0, 1]
                        else:
                            kcs = list(range(NCH))
                        po = PO[bi_ % 2]; sg_ = sgh[bi_ % 2]; ob_ = ob[bi_ % 2]
                        bi_ += 1
                        for i, kc in enumerate(kcs):
                            its.append((t0, n, kc, i == 0, i == len(kcs) - 1, po, sg_, ob_))
                    LOOK = 2
                    for j in range(len(its) + LOOK):
                        if j < len(its):
                            (t0, n, kc, first, last, po, sg_, ob_) = its[j]
                            if first:
                                self.dma(sg_[r0:r0 + 64, 0:n], self.sgT_d[64 * h:64 * (h + 1), t0:t0 + n], [self.sgT_d], [sg_])
                            psc = PSc[(ci + j) % 3]
                            self.mm(psc[:, 0:n], kf_[:, kc * 128:(kc + 1) * 128], q_[:, t0:t0 + n], True, True, [kf_, q_], [psc])
                        jj = j - LOOK
                        if jj >= 0:
                            (t0, n, kc, first, last, po, sg_, ob_) = its[jj]
                            psc = PSc[(ci + jj) % 3]; pt = pT[(ci + jj) % 3]
                            self.act(pt[:, 0:n], psc[:, 0:n], AF.Exp, [psc], [pt], scale=MLA_SCALE)
                            self.mm(po[:, 0:n], va[:, kc, :], pt[:, 0:n], first, last, [va, pt], [po])
                            if last:
                                self.P.op(DVE, (lambda a_, b_: (lambda e: e.reciprocal(out=a_, in_=b_)))(rden[d0:d0 + 64, 0:n], po[d0:d0 + 64, 0:n]),
                                          [po.b], [rden.b])
                                self.tt(ot[r0:r0 + 64, 0:n], po[r0:r0 + 64, 0:n], rden[d0:d0 + 64, 0:n], ALU.mult, [po, rden], [ot])
                                self.tt(ob_[r0:r0 + 64, 0:n], ot[r0:r0 + 64, 0:n], sg_[r0:r0 + 64, 0:n], ALU.mult, [ot, sg_], [ob_], eng=POOL)
                                self.dma(self.mixT_d[1024 + h * 64:1024 + (h + 1) * 64, t0:t0 + n], ob_[r0:r0 + 64, 0:n], [ob_], [self.mixT_d])
                    ci += len(its)

    def mm_group_kr(self, h_, n, wm, PQ1, PQ2):
        for kc in range(8):
            self.mm(PQ1[0:32, 0:n], wm[:, kc, 640:672], h_[:, kc, 0:n], kc == 0, kc == 7, [wm, h_], [PQ1])
        for kc in range(8):
            self.mm(PQ2[0:32, 0:n], wm[:, kc, 672:704], h_[:, kc, 0:n], kc == 0, kc == 7, [wm, h_], [PQ2])

    def stage_E(self, l):
        I = self.I
        with ExitStack() as st:
            wo = self.sb(st, [128, 16, 1024], BF16, "wo")
            wov = I["w_out"][l].rearrange("(k p) c -> p k c", p=128)
            for kc in range(16):
                self.load_w(wo[:, kc, :], wo, wov[:, kc, :], I["w_out"])
            lng = self.bvec(st, "ln_g", l, 1024)
            lnb = self.bvec(st, "ln_b", l, 1024)
            sets = []
            for s_ in range(3):
                B = {}
                B["mt"] = self.sb(st, [128, 16, 128], BF16, "mt")
                B["xt"] = self.sb(st, [128, D], F32, "xt")
                B["v1"] = self.sb(st, [128, D], F32, "v1")
                B["v2"] = self.sb(st, [128, D], F32, "v2")
                B["xo"] = self.sb(st, [128, D], F32, "xo")
                B["stats"] = self.sb(st, [128, 2, 6], F32, "stats")
                B["mv"] = self.sb(st, [128, 2], F32, "mv")
                B["PZ"] = [self.ps(st, [128, 512], F32, "PZ") for _ in range(2)]
                sets.append(B)
            tiles = list(range(NCH)) if l < DEPTH - 1 else list(range(2, NCH))
            gens = [(lambda tt_: (lambda slot: self.e_tile(l, tt_, sets[slot], wo, lng, lnb)))(t) for t in tiles]
            self.interleave(gens, 3)

    def e_tile(self, l, t, B, wo, lng, lnb):
        I = self.I
        m_ = B["mt"]; x_ = B["xt"]; v1 = B["v1"]; v2 = B["v2"]; xo_ = B["xo"]; stats = B["stats"]; mv = B["mv"]; PZ = B["PZ"]
        mixv = self.mixT_d[:].rearrange("(r p) t -> p r t", p=128)
        typ = 1 if t < 2 else 0
        tok = slice(t * 128, (t + 1) * 128)
        self.dma(m_[:], mixv[:, :, tok], [self.mixT_d], [m_], slow=True)
        if l == 0:
            src_t = I["ctx"] if t < 2 else I["x"]
            src = src_t[t * 128:(t + 1) * 128, :] if t < 2 else src_t[(t - 2) * 128:(t - 1) * 128, :]
        else:
            src_t = self.xres_d
            src = src_t[tok, :]
        self.dma(x_[:], src, [src_t], [x_])
        yield
        for nb in range(2):
            for kc in range(16):
                self.mm(PZ[nb][:], m_[:, kc, :], wo[:, kc, nb * 512:(nb + 1) * 512], kc == 0, kc == 15, [m_, wo], [PZ[nb]])
            yield
        for nb in range(2):
            self.tt(v1[:, nb * 512:(nb + 1) * 512], PZ[nb][:], self.gB[:, typ, nb * 512:(nb + 1) * 512], ALU.mult, [PZ[nb], self.gB], [v1])
        yield
        self.stt(v2[:], x_[:], ALPHA, v1[:], ALU.mult, ALU.add, [x_, v1], [v2])
        yield
        for s in range(2):
            self.P.op(DVE, (lambda ss: (lambda e: e.bn_stats(out=stats[:, ss, :], in_=v2[:, ss * 512:(ss + 1) * 512])))(s),
                      [v2.b], [stats.b])
        self.P.op(DVE, lambda e: e.bn_aggr(out=mv[:], in_=stats[:]), [stats.b], [mv.b])
        self.ts(mv[:, 1:2], mv[:, 1:2], LN_EPS, None, ALU.add, None, [mv], [mv])
        yield
        self.rsqrt(mv[:, 1:2], mv)
        yield
        self.ts(v1[:], v2[:], mv[:, 0:1], mv[:, 1:2], ALU.subtract, ALU.mult, [v2, mv], [v1])
        yield
        self.tt(v2[:], v1[:], lng[:], ALU.mult, [v1, lng], [v2], eng=POOL)
        yield
        self.tt(xo_[:], v2[:], lnb[:], ALU.add, [v2, lnb], [xo_], eng=POOL)
        yield
        if l < DEPTH - 1:
            self.dma(self.xres_d[tok, :], xo_[:], [xo_], [self.xres_d])
        else:
            self.dma(self.out[(t - 2) * 128:(t - 1) * 128, :], xo_[:], [xo_], [self.out])
        yield


C_ID = 0
C_UF = 128
C_LF = 256
C_UB = 384
C_LB = 512
C_MF = 640
C_MB = 768
C_RIF = 896
C_RIB = 1024
C_GF = 1152
C_GB = 1280
C_TEF = 1408
C_TEB = 1409
CST_W = 1410


def make_consts():
    k = np.arange(128)[:, None].astype(np.float32)
    i = np.arange(128)[None, :].astype(np.float32)
    cst = np.zeros((128, CST_W), np.float32)
    cst[:, C_ID:C_ID + 128] = (k == i)
    cst[:, C_UF:C_UF + 128] = (k <= i)
    cst[:, C_LF:C_LF + 128] = (k > i)
    cst[:, C_UB:C_UB + 128] = (k >= i)
    cst[:, C_LB:C_LB + 128] = (k < i)
    cst[:, C_MF:C_MF + 128] = (k <= i)
    cst[:, C_MB:C_MB + 128] = (k >= i)
    cst[:, C_RIF:C_RIF + 128] = np.maximum(i - k, 0)
    cst[:, C_RIB:C_RIB + 128] = np.maximum(k - i, 0)
    cst[:, C_GF:C_GF + 128] = np.broadcast_to(i + 1, (128, 128))
    cst[:, C_GB:C_GB + 128] = np.broadcast_to(128 - i, (128, 128))
    cst[:, C_TEF] = 127 - k[:, 0]
    cst[:, C_TEB] = k[:, 0]
    return cst


def rope_tables():
    rows = SEQ // 64
    t = np.arange(rows * 64)
    row = (t // 64).astype(np.float32)
    col = (t % 64).astype(np.float32)

    def cs(rot):
        nf = rot // 4
        inv = (np.float32(10000.0) ** (-np.arange(nf, dtype=np.float32) / np.float32(nf))).astype(np.float32)
        ang = np.concatenate([row[:, None] * inv, col[:, None] * inv], -1).astype(np.float32)
        return np.cos(ang).astype(np.float32), np.sin(ang).astype(np.float32)
    cm, sm = cs(32)
    mla = np.zeros((32, 2, NTOK), np.float32)
    mla[:, 0, :CTX] = 1.0
    mla[0:16, 0, CTX:] = cm.T; mla[16:32, 0, CTX:] = cm.T
    mla[0:16, 1, CTX:] = -sm.T; mla[16:32, 1, CTX:] = sm.T
    cr, sr = cs(64)
    ret = np.zeros((128, 4, NTOK), np.float32)
    ret[:, 0, :CTX] = 1.0
    for hh in range(2):
        b = hh * 64
        ret[b:b + 32, 0, CTX:] = cr.T; ret[b + 32:b + 64, 0, CTX:] = cr.T
        ret[b:b + 32, 1, CTX:] = -sr.T; ret[b + 32:b + 64, 1, CTX:] = sr.T
    ret[:, 2] = ret[:, 0] * 0.125
    ret[:, 3] = ret[:, 1] * 0.125
    ret2 = np.zeros((NTOK, 1024), np.float32)
    cc = np.ones((NTOK, 4, 64), np.float32)
    ss = np.zeros((NTOK, 4, 64), np.float32)
    cc[CTX:, :, 0:32] = cr[:, None, :]; cc[CTX:, :, 32:64] = cr[:, None, :]
    ss[CTX:, :, 0:32] = -sr[:, None, :]; ss[CTX:, :, 32:64] = sr[:, None, :]
    ret2[:, 0:256] = cc.reshape(NTOK, 256); ret2[:, 256:512] = 0.125 * cc.reshape(NTOK, 256)
    ret2[:, 512:768] = ss.reshape(NTOK, 256); ret2[:, 768:1024] = 0.125 * ss.reshape(NTOK, 256)
    return mla, ret, ret2


_CACHE = {}


def prep_inputs(inputs):
    f = lambda a: np.ascontiguousarray(np.asarray(a, dtype=np.float32))
    w_in = f(inputs["w_in"])
    kr = w_in[:, :, 3216:3248]
    kr_sw = np.concatenate([kr[:, :, 16:32], kr[:, :, 0:16]], -1)

    def sw64(a):
        a = a.reshape(2, D, 4, 2, 32)
        return np.ascontiguousarray(a[:, :, :, ::-1, :]).reshape(2, D, 256)
    q_sw = sw64(w_in[:, :, 3760:4016])
    k_sw = sw64(w_in[:, :, 4016:4272])
    w_ext = np.ascontiguousarray(np.concatenate([w_in, kr_sw, q_sw, k_sw], -1))
    uq = f(inputs["mla_w_uq"]).reshape(2, 384, 8, 96)
    uq_r = uq[:, :, :, 64:96]
    uq_sw = np.ascontiguousarray(np.concatenate([uq_r[..., 16:32], uq_r[..., 0:16]], -1)).reshape(2, 384, 256)
    mla_tab, ret_tab, ret_tab2 = rope_tables()
    shared = {
        "w_ada": f(inputs["w_ada"]), "b_ada": f(inputs["b_ada"]), "w_in": w_ext,
        "ssd_conv_w": f(inputs["ssd_conv_w"]), "ssd_conv_b": f(inputs["ssd_conv_b"]),
        "ssd_norm_w": f(inputs["ssd_norm_w"]), "mla_q_norm": f(inputs["mla_q_norm"]),
        "mla_w_uq": f(inputs["mla_w_uq"]), "w_uq_sw": uq_sw, "mla_kv_norm": f(inputs["mla_kv_norm"]),
        "mla_w_ukv": f(inputs["mla_w_ukv"]), "ret_log_rate_f": f(inputs["ret_log_rate_f"]),
        "ret_log_rate_b": f(inputs["ret_log_rate_b"]), "w_out": f(inputs["w_out"]),
        "ln_g": f(inputs["ln_g"]), "ln_b": f(inputs["ln_b"]),
        "cst": make_consts(), "mla_tab": mla_tab, "ret_tab": ret_tab, "ret_tab2": ret_tab2,
    }
    for n in ("ssd_a_log_f", "ssd_a_log_b", "ssd_dt_bias_f", "ssd_dt_bias_b", "ssd_d"):
        shared[n] = f(inputs[n])
    x = f(inputs["x"]); c = f(inputs["c"]); ctx = f(inputs["ctx"]); c_ctx = f(inputs["c_ctx"])
    maps = []
    for b in range(8):
        m = dict(shared)
        m["x"] = x[b]
        m["ctx"] = ctx[b]
        m["cvec"] = np.ascontiguousarray(np.stack([c[b], c_ctx], 0))
        maps.append(m)
    return maps


def kernel(**inputs):
    if "nc" not in _CACHE:
        _CACHE["nc"] = K().build()
    nc = _CACHE["nc"]
    maps = prep_inputs(inputs)
    res = run_bass_kernel_spmd(nc, maps, core_ids=list(range(8)))
    return np.stack([np.asarray(r["out"], dtype=np.float32) for r in res.results], 0)
```

```python
import math
from contextlib import ExitStack
import numpy as np
import concourse.bass as bass
import concourse.mybir as mybir
from concourse.bass_utils import run_bass_kernel_spmd

F32 = mybir.dt.float32
BF16 = mybir.dt.bfloat16
AF = mybir.ActivationFunctionType
ALU = mybir.AluOpType

PE, ACT, DVE, POOL, SP = "pe", "act", "dve", "pool", "sp"
EPOCH = 30000
DMA_K = 8
DMA_EPOCH = 1800

D = 1024
SEQ = 4096
CTX = 256
NTOK = SEQ + CTX
NCH = NTOK // 128
HC = NTOK + 8
DEPTH = 2
ALPHA = (2 * DEPTH) ** 0.25
LN_EPS = 1e-5
RMS_EPS = 1e-6
MLA_SCALE = 96 ** -0.5
WEXT = 5296 + 32 + 256 + 256


def colof(t):
    return t + 2 if t < CTX else t + 6


class Buf:
    __slots__ = ("name", "lw", "rd", "rdd", "psum")

    def __init__(self, name=""):
        self.name = name
        self.lw = None
        self.rd = {}
        self.rdd = []
        self.psum = False


class Op:
    __slots__ = ("eng", "fn", "deps", "sig", "idx", "dma_slot", "dma_prev")

    def __init__(self, eng, fn):
        self.eng = eng
        self.fn = fn
        self.deps = set()
        self.sig = None
        self.dma_slot = None
        self.dma_prev = None


class Prog:
    def __init__(self, nc):
        self.nc = nc
        self.ops = []
        self.eng = {PE: nc.tensor, ACT: nc.scalar, DVE: nc.vector, POOL: nc.gpsimd, SP: nc.sync}
        self.dma_lists = {}
        self.last = {}

    def op(self, eng, fn, reads=(), writes=(), dma=False):
        o = Op(eng, fn)
        o.idx = len(self.ops)
        for b in reads:
            if b.lw is not None:
                o.deps.add(b.lw)
            if b.psum:
                for e2, r in b.rd.items():
                    if e2 != eng:
                        o.deps.add(r)
        for b in writes:
            if b.lw is not None:
                o.deps.add(b.lw)
            for r in b.rd.values():
                o.deps.add(r)
            for r in b.rdd:
                o.deps.add(r)
        for b in reads:
            if dma:
                b.rdd.append(o.idx)
            else:
                b.rd[eng] = o.idx
        for b in writes:
            b.lw = o.idx
            b.rd = {}
            b.rdd = []
        if dma:
            lst = self.dma_lists.setdefault(eng, [])
            o.dma_slot = len(lst)
            if len(lst) >= DMA_K:
                o.dma_prev = lst[len(lst) - DMA_K]
            lst.append(o.idx)
        o.deps.discard(o.idx)
        self.ops.append(o)
        self.last[eng] = o.idx
        return o

    def barrier(self):
        bufs = {}
        for e in (PE, ACT, DVE, POOL, SP):
            bufs[e] = Buf("bar" + e)
            o = self.op(e, lambda en: en.nop(), writes=[bufs[e]])
            for lst in self.dma_lists.values():
                for d in lst[-DMA_K:]:
                    if d != o.idx:
                        o.deps.add(d)
        for e in (PE, ACT, DVE, POOL, SP):
            self.op(e, lambda en: en.nop(), reads=list(bufs.values()))

    def emit(self, stack):
        nc = self.nc
        ops = self.ops
        needed = set()
        for o in ops:
            for d in o.deps:
                do = ops[d]
                if do.eng == o.eng and o.eng == PE and do.dma_slot is None:
                    continue
                needed.add(d)
            if o.dma_prev is not None:
                needed.add(o.dma_prev)
        cnt = {}
        sems = {}
        dma_sems = {}
        for o in ops:
            if o.dma_slot is not None:
                k = o.dma_slot % DMA_K
                n = o.dma_slot // DMA_K
                key = (o.eng, k, n // DMA_EPOCH)
                if key not in dma_sems:
                    dma_sems[key] = stack.enter_context(nc.semaphore("dq%s%d_%d" % key))
                o.sig = (dma_sems[key], 16 * (n % DMA_EPOCH + 1))
            elif o.idx in needed:
                c = cnt.get(o.eng, 0)
                key = (o.eng, c // EPOCH)
                if key not in sems:
                    sems[key] = stack.enter_context(nc.semaphore("s%s_%d" % key))
                o.sig = (sems[key], c % EPOCH + 1)
                cnt[o.eng] = c + 1
        waited = {}
        nw = 0
        for o in ops:
            e = self.eng[o.eng]
            deps = set(o.deps)
            if o.dma_prev is not None:
                deps.add(o.dma_prev)
            for d in sorted(deps):
                do = ops[d]
                if do.sig is None:
                    continue
                sem, val = do.sig
                key = (o.eng, id(sem))
                if waited.get(key, 0) >= val:
                    continue
                waited[key] = val
                e.wait_ge(sem, val)
                nw += 1
            ins = o.fn(e)
            if o.sig is not None:
                sem, val = o.sig
                ins.then_inc(sem, 16 if o.dma_slot is not None else 1)
        self.nwaits = nw


class T:
    __slots__ = ("t", "b")

    def __init__(self, t, name=""):
        self.t = t
        self.b = Buf(name)

    def __getitem__(self, k):
        return self.t[k]


class K:
    def __init__(self, debug=False, stop_after=None, skip=()):
        self.skip = set(skip)
        self.debug = debug
        self.stop_after = stop_after
        self.nc = bass.Bass("TRN2", target_bir_lowering=False)
        self.P = Prog(self.nc)
        self.uid = 0

    def dram(self, name, shape, dt, kind="Internal"):
        return T(self.nc.dram_tensor(name, list(shape), dt, kind=kind).ap(), name)

    def sb(self, st, shape, dt, name=None):
        self.uid += 1
        name = "%s_%d" % (name or "t", self.uid)
        return T(st.enter_context(self.nc.sbuf_tensor(name, list(shape), dt)), name)

    def ps(self, st, shape, dt=F32, name=None):
        self.uid += 1
        name = "%s_%d" % (name or "p", self.uid)
        t = T(st.enter_context(self.nc.psum_tensor(name, list(shape), dt)), name)
        t.b.psum = True
        return t

    def dma(self, out, in_, reads, writes, eng=SP, slow=False):
        if slow:
            return self.P.op(eng, lambda e: e.dma_start(out=out, in_=in_, allow_slow_non_contiguous=True),
                             [x.b for x in reads], [x.b for x in writes], dma=True)
        return self.P.op(eng, lambda e: e.dma_start(out=out, in_=in_), [x.b for x in reads], [x.b for x in writes], dma=True)

    def mm(self, out, lhsT, rhs, start, stop, reads, writes):
        return self.P.op(PE, lambda e: e.matmul(out, lhsT=lhsT, rhs=rhs, start=start, stop=stop),
                         [x.b for x in reads], [x.b for x in writes])

    def tr(self, out, in_, ident, reads, writes):
        return self.P.op(PE, lambda e: e.transpose(out=out, in_=in_, identity=ident),
                         [x.b for x in reads], [x.b for x in writes])

    def act(self, out, in_, func, reads, writes, bias=None, scale=None, accum_out=None):
        kw = {}
        if bias is not None:
            kw["bias"] = bias
        if scale is not None:
            kw["scale"] = scale
        if accum_out is not None:
            kw["accum_out"] = accum_out
        return self.P.op(ACT, lambda e: e.activation(out=out, in_=in_, func=func, **kw),
                         [x.b for x in reads], [x.b for x in writes])

    def tt(self, out, in0, in1, op, reads, writes, eng=DVE):
        return self.P.op(eng, lambda e: e.tensor_tensor(out=out, in0=in0, in1=in1, op=op),
                         [x.b for x in reads], [x.b for x in writes])

    def ts(self, out, in0, s1, s2, op0, op1, reads, writes, eng=DVE):
        if op1 is None:
            return self.P.op(eng, lambda e: e.tensor_scalar(out=out, in0=in0, scalar1=s1, scalar2=None, op0=op0),
                             [x.b for x in reads], [x.b for x in writes])
        return self.P.op(eng, lambda e: e.tensor_scalar(out=out, in0=in0, scalar1=s1, scalar2=s2, op0=op0, op1=op1),
                         [x.b for x in reads], [x.b for x in writes])

    def stt(self, out, in0, scalar, in1, op0, op1, reads, writes, eng=DVE):
        return self.P.op(eng, lambda e: e.scalar_tensor_tensor(out=out, in0=in0, scalar=scalar, in1=in1, op0=op0, op1=op1),
                         [x.b for x in reads], [x.b for x in writes])

    def cp(self, out, in_, reads, writes, eng=DVE):
        return self.P.op(eng, lambda e: e.tensor_copy(out=out, in_=in_), [x.b for x in reads], [x.b for x in writes])

    def memset(self, out, val, writes, eng=POOL):
        return self.P.op(eng, lambda e: e.memset(out, val), [], [x.b for x in writes])

    def sigm(self, ap, t):
        self.act(ap, ap, AF.Ln, [t], [t], bias=1.0)
        self.act(ap, ap, AF.Exp, [t], [t], scale=-1.0)

    def rsqrt(self, ap, t):
        self.act(ap, ap, AF.Ln, [t], [t])
        self.act(ap, ap, AF.Exp, [t], [t], scale=-0.5)

    def build(self):
        nc = self.nc
        dbg = self.debug
        I = {}

        def inp(name, shape):
            I[name] = self.dram(name, shape, F32, kind="ExternalInput")
        inp("x", [SEQ, D]); inp("ctx", [CTX, D]); inp("cvec", [2, D])
        inp("w_ada", [2, D, 3 * D]); inp("b_ada", [2, 3 * D]); inp("w_in", [2, D, WEXT])
        inp("ssd_conv_w", [2, 5, 1536]); inp("ssd_conv_b", [2, 1536])
        for n in ("ssd_a_log_f", "ssd_a_log_b", "ssd_dt_bias_f", "ssd_dt_bias_b", "ssd_d"):
            inp(n, [2, 16])
        inp("ssd_norm_w", [2, 1024]); inp("mla_q_norm", [2, 384]); inp("mla_w_uq", [2, 384, 768])
        inp("w_uq_sw", [2, 384, 256]); inp("mla_kv_norm", [2, 256]); inp("mla_w_ukv", [2, 256, 1024])
        inp("ret_log_rate_f", [2, 4]); inp("ret_log_rate_b", [2, 4]); inp("w_out", [2, 2048, D])
        inp("ln_g", [2, D]); inp("ln_b", [2, D])
        inp("cst", [128, CST_W]); inp("mla_tab", [32, 2, NTOK]); inp("ret_tab", [128, 4, NTOK]); inp("ret_tab2", [NTOK, 1024])
        self.I = I
        okind = "ExternalOutput" if dbg else "Internal"
        self.out = self.dram("out", [SEQ, D], F32, kind="ExternalOutput")
        self.hT_d = self.dram("hT_d", [128, 8 * HC], BF16, kind=okind)
        self.ypart_d = self.dram("ypart_d", [NTOK, 1024], F32)
        self.rpart_d = self.dram("rpart_d", [NTOK, 512], F32)
        self.ypartb_d = self.dram("ypartb_d", [NTOK, 1024], F32)
        self.Fb_d = self.dram("Fb_d", [NCH, 128, 1536], BF16)
        self.Rb_d = self.dram("Rb_d", [NCH, 128, 1280], BF16)
        self.Ff_d = self.dram("Ff_d", [NCH, 128, 272], F32)
        self.rpartb_d = self.dram("rpartb_d", [NTOK, 512], F32)
        self.mixT_d = self.dram("mixT_d", [2048, NTOK], BF16, kind=okind)
        self.qT_d = self.dram("qT_d", [8, 96, NTOK], BF16)
        self.kfT_d = self.dram("kfT_d", [8, 96, NTOK], BF16)
        self.sgT_d = self.dram("sgT_d", [512, NTOK], F32)
        self.xres_d = self.dram("xres_d", [NTOK, D], F32, kind=okind)

        with ExitStack() as gst:
            self.gst = gst
            self.cst = self.sb(gst, [128, CST_W], F32, "cst")
            self.dma(self.cst[:], I["cst"][:], [I["cst"]], [self.cst])
            self.identb = self.sb(gst, [128, 128], BF16, "identb")
            self.cp(self.identb[:], self.cst[:, C_ID:C_ID + 128], [self.cst], [self.identb])
            self.ones = self.sb(gst, [128, 128], F32, "ones")
            self.memset(self.ones[:], 1.0, [self.ones])
            self.gB = self.sb(gst, [128, 2, 1024], F32, "gB")
            zt = self.sb(gst, [128, 8, 4], BF16, "zt")
            self.memset(zt[:], 0.0, [zt])
            hv = self.hT_d[:].rearrange("p (k c) -> p k c", k=8)
            self.hv = hv
            for (a, b) in ((0, 2), (258, 262), (4358, 4360)):
                self.dma(hv[:, :, a:b], zt[:, :, 0:b - a], [zt], [self.hT_d], slow=True)
            self.P.barrier()
            stages = []
            for l in range(DEPTH):
                stages += [("A", l), ("S", l), ("M", l), ("R", l), ("E", l)]
            for (s, l) in stages:
                if s in self.skip:
                    continue
                if s == "A":
                    self.stage_A(l)
                elif s == "S":
                    self.stage_S(l)
                elif s == "M":
                    self.stage_M(l)
                elif s == "R":
                    self.stage_R(l)
                else:
                    self.stage_E(l)
                self.P.barrier()
                if self.stop_after == (s, l):
                    break
            self.P.barrier()
            self.P.emit(gst)
        return nc

    def silu_psum(self, st, src_ap, src_t, out_ap, out_t, e_t, e_ap, r_ap):
        self.act(e_ap, src_ap, AF.Exp, [src_t], [e_t], scale=-1.0)
        self.sigm(e_ap, e_t)
        self.tt(out_ap, src_ap, r_ap, ALU.mult, [src_t, e_t], [out_t])

    def stage_A(self, l):
        I = self.I
        with ExitStack() as st:
            wada = [self.sb(st, [128, 8, 512], F32, "wada") for _ in range(2)]
            craw = self.sb(st, [128, 8, 2], F32, "craw")
            ce = self.sb(st, [128, 8, 2], F32, "ce")
            scT = self.sb(st, [128, 8, 2], F32, "scT")
            modT = self.sb(st, [128, 24, 2], F32, "modT")
            scale1 = self.sb(st, [128, 8, 2], F32, "scale1")
            brow = self.sb(st, [1, 3 * D], F32, "brow")
            pm = self.ps(st, [128, 512], F32, "pm")
            pg = [self.ps(st, [128, 512], F32, "pg") for _ in range(2)]
            for j in range(2):
                self.dma(craw[:, :, j], I["cvec"][j].rearrange("(k p) -> p k", p=128), [I["cvec"]], [craw], slow=True)
            self.dma(brow[:], I["b_ada"][l:l + 1, :], [I["b_ada"]], [brow])
            self.act(ce[:], craw[:], AF.Exp, [craw], [ce], scale=-1.0)
            self.sigm(ce[:], ce)
            self.tt(scT[:], craw[:], ce[:], ALU.mult, [craw, ce], [scT])
            wv = I["w_ada"][l].rearrange("(k p) c -> p k c", p=128)
            for cb in range(6):
                w = wada[cb % 2]
                self.dma(w[:], wv[:, :, cb * 512:(cb + 1) * 512], [I["w_ada"]], [w])
                if cb < 4:
                    for dj in range(4):
                        j = cb * 4 + dj
                        for kc in range(8):
                            self.mm(pm[:, 2 * dj:2 * dj + 2], w[:, kc, dj * 128:(dj + 1) * 128], scT[:, kc, :],
                                    kc == 0, False, [w, scT], [pm])
                        self.mm(pm[:, 2 * dj:2 * dj + 2], brow[0:1, j * 128:(j + 1) * 128], self.ones[0:1, 0:2],
                                False, True, [brow, self.ones], [pm])
                        self.cp(modT[:, j, :], pm[:, 2 * dj:2 * dj + 2], [pm], [modT])
                else:
                    for typ in range(2):
                        p = pg[typ]
                        for kc in range(8):
                            self.mm(p[:], scT[:, kc, typ:typ + 1].to_broadcast([128, 128]), w[:, kc, :],
                                    kc == 0, False, [w, scT], [p])
                        self.mm(p[:], self.ones[0:1, 0:128], brow[0:1, cb * 512:(cb + 1) * 512], False, True,
                                [brow, self.ones], [p])
                        self.cp(self.gB[:, typ, (cb - 4) * 512:(cb - 3) * 512], p[:], [p], [self.gB])
            self.ts(scale1[:], modT[:, 8:16, :], 1.0, None, ALU.add, None, [modT], [scale1])
            xt = [self.sb(st, [128, D], F32, "xt") for _ in range(2)]
            ht = [self.sb(st, [128, 8, 128], BF16, "ht") for _ in range(2)]
            pT = [self.ps(st, [128, 1024], F32, "pT") for _ in range(2)]
            for t in range(NCH):
                typ = 1 if t < 2 else 0
                if l == 0:
                    src_t = I["ctx"] if t < 2 else I["x"]
                    src = src_t[t * 128:(t + 1) * 128, :] if t < 2 else src_t[(t - 2) * 128:(t - 1) * 128, :]
                else:
                    src_t = self.xres_d
                    src = src_t[t * 128:(t + 1) * 128, :]
                x_ = xt[t % 2]; h_ = ht[t % 2]; p_ = pT[t % 2]
                self.dma(x_[:], src, [src_t], [x_])
                for kc in range(8):
                    self.tr(p_[:, kc * 128:(kc + 1) * 128], x_[:, kc * 128:(kc + 1) * 128], self.cst[:, C_ID:C_ID + 128],
                            [x_, self.cst], [p_])
                for kc in range(8):
                    self.act(h_[:, kc, :], p_[:, kc * 128:(kc + 1) * 128], AF.Identity, [p_, scale1, modT], [h_],
                             bias=modT[:, kc, typ:typ + 1], scale=scale1[:, kc, typ:typ + 1])
                c0 = colof(t * 128)
                self.dma(self.hv[:, :, c0:c0 + 128], h_[:], [h_], [self.hT_d])

    def load_w(self, dst_ap, dst_t, src_ap, src_t):
        self.dma(dst_ap, src_ap, [src_t], [dst_t], eng=POOL, slow=True)

    def bvec(self, st, name, l, n):
        t = self.sb(st, [128, n], F32, name)
        self.dma(t[:], self.I[name][l:l + 1, :].to_broadcast([128, n]), [self.I[name]], [t], slow=True)
        return t

    def interleave(self, factories, width):
        pending = list(factories)
        active = []
        for s in range(width):
            if pending:
                active.append((s, pending.pop(0)(s)))
        while active:
            nxt = []
            for (s, g) in active:
                try:
                    next(g)
                    nxt.append((s, g))
                except StopIteration:
                    if pending:
                        nxt.append((s, pending.pop(0)(s)))
            active = nxt

    def stage_S(self, l):
        I = self.I
        cst = self.cst
        import os as _os
        with ExitStack() as st:
            wv = I["w_in"][l].rearrange("(k p) c -> p k c", p=128)
            with ExitStack() as stf:
                wx = self.sb(stf, [128, 8, 1536], BF16, "wx")
                wdt = self.sb(stf, [128, 8, 16], BF16, "wdt")
                for kc in range(8):
                    self.load_w(wx[:, kc, :], wx, wv[:, kc, 1024:2560], I["w_in"])
                self.load_w(wdt[:], wdt, wv[:, :, 2560:2576], I["w_in"])
                convw = self.sb(stf, [128, 12, 5], F32, "convw")
                for k in range(5):
                    self.dma(convw[:, :, k], I["ssd_conv_w"][l, k].rearrange("(r p) -> p r", p=128), [I["ssd_conv_w"]], [convw], slow=True)
                dg = self.sb(stf, [128, 60, 128], BF16, "dg")
                for r in range(12):
                    for k in range(5):
                        self.ts(dg[:, r * 5 + k, :], self.identb[:], convw[:, r, k:k + 1], None, ALU.mult, None,
                                [self.identb, convw], [dg])
                cbrow = self.sb(stf, [1, 1536], F32, "cbrow")
                self.dma(cbrow[:], I["ssd_conv_b"][l:l + 1, :], [I["ssd_conv_b"]], [cbrow])
                sets = []
                for s_ in range(2):
                    B = {}
                    B["hc"] = self.sb(stf, [128, 8, 132], BF16, "hcf")
                    B["xbc"] = self.sb(stf, [128, 12, 132], BF16, "xbc")
                    B["esb"] = self.sb(stf, [128, 1536], F32, "esbf")
                    B["ubf"] = self.sb(stf, [128, 12, 128], BF16, "ubf")
                    B["Fb"] = self.sb(stf, [128, 1536], BF16, "Fbw")
                    B["Ff"] = self.sb(stf, [128, 272], F32, "Ffw")
                    B["Q"] = [self.ps(stf, [128, 512], F32, "QF") for _ in range(4)]
                    sets.append(B)
                gens = [(lambda cc: (lambda slot: self.ssd_front(cc, sets[slot], wx, wdt, dg, cbrow)))(c) for c in range(NCH)]
                self.interleave(gens, 2)
            self.P.barrier()
            Dsk = self.bvec(st, "ssd_d", l, 16)
            prm = {}
            for d_, sfx in ((0, "f"), (1, "b")):
                al = self.bvec(st, "ssd_a_log_" + sfx, l, 16)
                self.act(al[:], al[:], AF.Exp, [al], [al])
                self.ts(al[:], al[:], -1.0, None, ALU.mult, None, [al], [al])
                dtb = self.bvec(st, "ssd_dt_bias_" + sfx, l, 16)
                Ub = self.sb(st, [128, 128], BF16, "Ub16")
                Lb = self.sb(st, [128, 128], BF16, "Lb16")
                Uo = C_UF if d_ == 0 else C_UB
                Lo = C_LF if d_ == 0 else C_LB
                self.cp(Ub[:], cst[:, Uo:Uo + 128], [cst], [Ub])
                self.cp(Lb[:], cst[:, Lo:Lo + 128], [cst], [Lb])
                prm[d_] = (al, dtb, Ub, Lb)
            with ExitStack() as st2:
                gens = []
                for d_ in (0, 1):
                    gens.append((lambda dd: (lambda slot: self.ssd_sweep(l, dd, st2, Dsk, prm[dd])))(d_))
                self.interleave(gens, 2)
            self.P.barrier()
            with ExitStack() as st3:
                wz = self.sb(st3, [128, 8, 1024], BF16, "wz")
                for kc in range(8):
                    self.load_w(wz[:, kc, :], wz, wv[:, kc, 0:1024], I["w_in"])
                nwB = self.bvec(st3, "ssd_norm_w", l, 1024)
                sets = []
                for s in range(2):
                    B = {}
                    B["hc"] = self.sb(st3, [128, 8, 128], BF16, "hc3")
                    B["ypf"] = self.sb(st3, [128, 1024], F32, "ypf")
                    B["ypb"] = self.sb(st3, [128, 1024], F32, "ypb")
                    B["e"] = self.sb(st3, [128, 1024], F32, "e3")
                    B["t1"] = self.sb(st3, [128, 1024], F32, "t13")
                    B["bst"] = self.sb(st3, [128, 2, 6], F32, "bst3")
                    B["ssq"] = self.sb(st3, [128, 2], F32, "ssq3")
                    B["ob"] = self.sb(st3, [128, 1024], BF16, "ob3")
                    B["oT"] = self.sb(st3, [128, 8, 128], BF16, "oT3")
                    B["PZ"] = [self.ps(st3, [128, 512], F32, "PZ3") for _ in range(2)]
                    B["PT"] = self.ps(st3, [128, 1024], BF16, "PT3")
                    sets.append(B)
                gens = [(lambda cc: (lambda slot: self.ssd_final(cc, sets[slot], wz, nwB)))(c) for c in range(NCH)]
                self.interleave(gens, 2)

    def ssd_final(self, c, B, wz, nwB):
        h_ = B["hc"]; ypf = B["ypf"]; ypb = B["ypb"]; e = B["e"]; t1 = B["t1"]; bst = B["bst"]; ssq = B["ssq"]
        ob = B["ob"]; oT_ = B["oT"]; PZ = B["PZ"]; PT = B["PT"]
        mixv = self.mixT_d[:].rearrange("(r p) t -> p r t", p=128)
        c0 = colof(c * 128)
        tok = slice(c * 128, (c + 1) * 128)
        self.dma(h_[:], self.hv[:, :, c0:c0 + 128], [self.hT_d], [h_], slow=True)
        self.dma(ypf[:], self.ypart_d[tok, :], [self.ypart_d], [ypf])
        self.dma(ypb[:], self.ypartb_d[tok, :], [self.ypartb_d], [ypb])
        yield
        for n in range(2):
            for kc in range(8):
                self.mm(PZ[n][:], h_[:, kc, :], wz[:, kc, n * 512:(n + 1) * 512], kc == 0, kc == 7, [h_, wz], [PZ[n]])
        self.tt(ypf[:], ypf[:], ypb[:], ALU.add, [ypf, ypb], [ypf])
        yield
        for n in range(2):
            self.act(e[:, n * 512:(n + 1) * 512], PZ[n][:], AF.Exp, [PZ[n]], [e], scale=-1.0)
        yield
        self.sigm(e[:], e)
        yield
        for n in range(2):
            self.tt(t1[:, n * 512:(n + 1) * 512], PZ[n][:], e[:, n * 512:(n + 1) * 512], ALU.mult, [PZ[n], e], [t1])
        yield
        self.tt(t1[:], t1[:], ypf[:], ALU.mult, [t1, ypf], [t1])
        yield
        for s_ in range(2):
            self.P.op(DVE, (lambda ss: (lambda en: en.bn_stats(out=bst[:, ss, :], in_=t1[:, ss * 512:(ss + 1) * 512])))(s_),
                      [t1.b], [bst.b])
        self.P.op(DVE, lambda en: en.bn_aggr(out=ssq[:], in_=bst[:]), [bst.b], [ssq.b])
        self.stt(ssq[:, 1:2], ssq[:, 0:1], ssq[:, 0:1], ssq[:, 1:2], ALU.mult, ALU.add, [ssq], [ssq])
        self.ts(ssq[:, 1:2], ssq[:, 1:2], RMS_EPS, None, ALU.add, None, [ssq], [ssq])
        yield
        self.rsqrt(ssq[:, 1:2], ssq)
        yield
        self.stt(ob[:], t1[:], ssq[:, 1:2], nwB[:], ALU.mult, ALU.mult, [t1, ssq, nwB], [ob])
        yield
        for r in range(8):
            self.tr(PT[:, r * 128:(r + 1) * 128], ob[:, r * 128:(r + 1) * 128], self.identb[:], [ob, self.identb], [PT])
        yield
        self.act(oT_[:], PT[:].rearrange("p (a b) -> p a b", a=8), AF.Copy, [PT], [oT_])
        yield
        self.dma(mixv[:, 0:8, tok], oT_[:], [oT_], [self.mixT_d], slow=True)
        yield

    def ssd_front(self, c, B, wx, wdt, dg, cbrow):
        h_ = B["hc"]; xbc = B["xbc"]; esb = B["esb"]; ubf = B["ubf"]; Fb = B["Fb"]; Ff = B["Ff"]; Q = B["Q"]
        Qb0 = Q[0][:].bitcast(BF16)
        Qb1 = Q[1][:].bitcast(BF16)
        c0 = colof(c * 128)
        self.dma(h_[:], self.hv[:, :, c0 - 2:c0 + 130], [self.hT_d], [h_], slow=True)
        yield
        for r in range(12):
            q_ = Q[r // 3]
            o_ = q_[:, (r % 3) * 132:(r % 3) * 132 + 132]
            for kc in range(8):
                self.mm(o_, wx[:, kc, r * 128:(r + 1) * 128], h_[:, kc, :], kc == 0, kc == 7, [wx, h_], [q_])
            if r % 3 == 2:
                yield
        for q in range(4):
            self.act(xbc[:, 3 * q:3 * q + 3, :], Q[q][:, 0:396].rearrange("p (a b) -> p a b", a=3), AF.Copy, [Q[q]], [xbc])
        yield
        for kc in range(8):
            self.mm(Q[3][:, 0:16], h_[:, kc, 2:130], wdt[:, kc, :], kc == 0, kc == 7, [h_, wdt], [Q[3]])
        yield
        for r in range(12):
            q_ = Q[r // 4]
            o_ = q_[:, (r % 4) * 128:(r % 4 + 1) * 128]
            for k in range(5):
                self.mm(o_, dg[:, r * 5 + k, :], xbc[:, r, k:k + 128], k == 0, False, [dg, xbc], [q_])
            self.mm(o_, cbrow[0:1, r * 128:(r + 1) * 128], self.ones[0:1, 0:128], False, True, [cbrow, self.ones], [q_])
            if r % 4 == 3:
                yield
        self.cp(Ff[:, 256:272], Q[3][:, 0:16], [Q[3]], [Ff])
        for q in range(3):
            self.act(esb[:, q * 512:(q + 1) * 512], Q[q][:], AF.Exp, [Q[q]], [esb], scale=-1.0)
        yield
        self.sigm(esb[:], esb)
        yield
        for q in range(3):
            self.tt(ubf[:, 4 * q:4 * q + 4, :], Q[q][:].rearrange("p (a b) -> p a b", a=4),
                    esb[:, q * 512:(q + 1) * 512].rearrange("p (a b) -> p a b", a=4), ALU.mult, [Q[q], esb], [ubf])
        yield
        for g in range(2):
            self.mm(Q[3][:, 256 + g * 128:256 + (g + 1) * 128], ubf[:, 8 + g, :], ubf[:, 10 + g, :], True, True, [ubf], [Q[3]])
        for r in range(8):
            self.tr(Qb0[:, r * 128:(r + 1) * 128], ubf[:, r, :], self.identb[:], [ubf, self.identb], [Q[0]])
        for r in range(2):
            self.tr(Qb1[:, r * 128:(r + 1) * 128], ubf[:, 8 + r, :], self.identb[:], [ubf, self.identb], [Q[1]])
        self.cp(Fb[:, 1280:1536], ubf[:, 10:12, :].rearrange("p a b -> p (a b)"), [ubf], [Fb], eng=POOL)
        yield
        self.cp(Ff[:, 0:256], Q[3][:, 256:512], [Q[3]], [Ff])
        self.act(Fb[:, 0:1024], Qb0[:, 0:1024], AF.Copy, [Q[0]], [Fb])
        self.cp(Fb[:, 1024:1280], Qb1[:, 0:256], [Q[1]], [Fb])
        yield
        self.dma(self.Fb_d[c], Fb[:], [Fb], [self.Fb_d])
        self.dma(self.Ff_d[c], Ff[:], [Ff], [self.Ff_d])
        yield

    def ssd_sweep(self, l, d_, st, Dsk, prm):
        cst = self.cst
        al, dtb, Ub, Lb = prm
        H = self.sb(st, [128, 1024], F32, "H")
        Hbf = self.sb(st, [128, 1024], BF16, "Hbf")
        Fb = [self.sb(st, [128, 1536], BF16, "Fbr") for _ in range(2)]
        Ff = [self.sb(st, [128, 272], F32, "Ffr") for _ in range(2)]
        esb = self.sb(st, [128, 2048], F32, "esb")
        dtx = self.sb(st, [128, 16], F32, "dtx")
        dt = self.sb(st, [128, 16], F32, "dt")
        la = self.sb(st, [128, 16], F32, "la")
        lah = self.sb(st, [128, 16], BF16, "lah")
        lah32 = self.sb(st, [128, 16], F32, "lah32")
        lal32 = self.sb(st, [128, 16], F32, "lal32")
        lalo = self.sb(st, [128, 16], BF16, "lalo")
        E3 = self.sb(st, [128, 48], F32, "E3")
        scm = self.sb(st, [128, 2, 128], F32, "scm")
        LaUh = self.sb(st, [128, 16, 128], BF16, "LaUh")
        LaUl = self.sb(st, [128, 16, 128], BF16, "LaUl")
        M = self.sb(st, [128, 16, 128], BF16, "M")
        v = self.sb(st, [128, 1024], BF16, "v")
        vte = self.sb(st, [128, 1024], BF16, "vte")
        t1 = self.sb(st, [128, 1024], F32, "t1")
        t2 = self.sb(st, [128, 1024], F32, "t2")
        yp = [self.sb(st, [128, 1024], F32, "yp") for _ in range(2)]
        Q = [self.ps(st, [128, 512], F32, "Q") for _ in range(4)]
        ydst = self.ypart_d if d_ == 0 else self.ypartb_d
        order = list(range(NCH)) if d_ == 0 else [1, 0] + list(range(NCH - 1, 1, -1))
        Uo = C_UF if d_ == 0 else C_UB
        Lo = C_LF if d_ == 0 else C_LB
        Mo = C_MF if d_ == 0 else C_MB
        self.memset(H[:], 0.0, [H])
        self.memset(Hbf[:], 0.0, [Hbf])

        def loads(ci, slot):
            c = order[ci]
            self.dma(Fb[slot][:], self.Fb_d[c], [self.Fb_d], [Fb[slot]])
            self.dma(Ff[slot][:], self.Ff_d[c], [self.Ff_d], [Ff[slot]])
        loads(0, 0)
        it = 0
        for ci, c in enumerate(order):
            fb = Fb[it % 2]; ff = Ff[it % 2]; ypt = yp[it % 2]
            it += 1
            tok = slice(c * 128, (c + 1) * 128)
            if ci + 1 < len(order):
                loads(ci + 1, it % 2)
            xs_tok = fb[:, 0:1024]
            self.tt(dtx[:], ff[:, 256:272], dtb[:], ALU.add, [ff, dtb], [dtx])
            self.tt(scm[:], ff[:, 0:256].rearrange("p (a b) -> p a b", a=2),
                    cst[:, Mo:Mo + 128].unsqueeze(1).to_broadcast([128, 2, 128]), ALU.mult, [ff, cst], [scm])
            yield
            self.act(dtx[:], dtx[:], AF.Exp, [dtx], [dtx])
            self.act(dt[:], dtx[:], AF.Ln, [dtx], [dt], bias=1.0)
            yield
            self.tt(la[:], dt[:], al[:], ALU.mult, [dt, al], [la])
            self.cp(lah[:], la[:], [la], [lah])
            self.cp(lah32[:], lah[:], [lah], [lah32])
            self.tt(lal32[:], la[:], lah32[:], ALU.subtract, [la, lah32], [lal32])
            self.cp(lalo[:], lal32[:], [lal32], [lalo])
            yield
            self.mm(Q[3][:, 16:32], cst[:, Uo:Uo + 128], la[:], True, True, [cst, la], [Q[3]])
            self.mm(Q[3][:, 32:48], cst[:, Lo:Lo + 128], la[:], True, True, [cst, la], [Q[3]])
            self.mm(Q[3][:, 48:64], self.ones[:], la[:], True, True, [self.ones, la], [Q[3]])
            self.tt(LaUh[:], lah[:].unsqueeze(2).to_broadcast([128, 16, 128]),
                    Ub[:].unsqueeze(1).to_broadcast([128, 16, 128]), ALU.mult, [lah, Ub], [LaUh], eng=POOL)
            self.tt(LaUl[:], lalo[:].unsqueeze(2).to_broadcast([128, 16, 128]),
                    Ub[:].unsqueeze(1).to_broadcast([128, 16, 128]), ALU.mult, [lalo, Ub], [LaUl])
            self.tt(v[:].rearrange("p (h e) -> p h e", h=16), xs_tok.rearrange("p (h e) -> p h e", h=16),
                    dt[:].unsqueeze(2).to_broadcast([128, 16, 64]), ALU.mult, [fb, dt], [v], eng=POOL)
            yield
            self.act(E3[:], Q[3][:, 16:64], AF.Exp, [Q[3]], [E3])
            yield
            for q in range(4):
                self.mm(Q[q][:], Lb[:], LaUh[:, 4 * q:4 * q + 4, :].rearrange("p a b -> p (a b)"), True, False, [Lb, LaUh], [Q[q]])
                self.mm(Q[q][:], Lb[:], LaUl[:, 4 * q:4 * q + 4, :].rearrange("p a b -> p (a b)"), False, True, [Lb, LaUl], [Q[q]])
                if q % 2 == 1:
                    yield
            self.tt(vte[:].rearrange("p (h e) -> p h e", h=16), v[:].rearrange("p (h e) -> p h e", h=16),
                    E3[:, 16:32].unsqueeze(2).to_broadcast([128, 16, 64]), ALU.mult, [v, E3], [vte], eng=POOL)
            for q in range(4):
                self.act(esb[:, q * 512:(q + 1) * 512], Q[q][:], AF.Exp, [Q[q]], [esb])
            yield
            self.tt(M[:].rearrange("p (g a) b -> p g a b", g=2), esb[:].rearrange("p (g a b) -> p g a b", g=2, a=8),
                    scm[:].unsqueeze(2).to_broadcast([128, 2, 8, 128]), ALU.mult, [esb, scm], [M])
            yield
            for g in range(2):
                self.mm(Q[2 + g][:], fb[:, 1280 + g * 128:1280 + (g + 1) * 128], Hbf[:, g * 512:(g + 1) * 512], True, True, [fb, Hbf], [Q[2 + g]])
            for h in range(16):
                q_ = Q[h // 8]
                self.mm(q_[:, (h % 8) * 64:(h % 8 + 1) * 64], M[:, h, :], v[:, h * 64:(h + 1) * 64], True, True, [M, v], [q_])
                if h % 8 == 7:
                    yield
            for g in range(2):
                self.tt(t1[:, g * 512:(g + 1) * 512].rearrange("p (h e) -> p h e", h=8),
                        Q[2 + g][:].rearrange("p (h e) -> p h e", h=8),
                        E3[:, g * 8:(g + 1) * 8].unsqueeze(2).to_broadcast([128, 8, 64]), ALU.mult, [Q[2 + g], E3], [t1])
            yield
            for g in range(2):
                self.tt(t2[:, g * 512:(g + 1) * 512], Q[g][:], t1[:, g * 512:(g + 1) * 512], ALU.add, [Q[g], t1], [t2])
            yield
            for g in range(2):
                self.mm(Q[g][:], fb[:, 1024 + g * 128:1024 + (g + 1) * 128], vte[:, g * 512:(g + 1) * 512], True, True, [fb, vte], [Q[g]])
            if d_ == 0:
                self.tt(t1[:].rearrange("p (h e) -> p h e", h=16), xs_tok.rearrange("p (h e) -> p h e", h=16),
                        Dsk[:].unsqueeze(2).to_broadcast([128, 16, 64]), ALU.mult, [fb, Dsk], [t1], eng=POOL)
                self.tt(ypt[:], t1[:], t2[:], ALU.add, [t1, t2], [ypt], eng=POOL)
            else:
                self.cp(ypt[:], t2[:], [t2], [ypt], eng=POOL)
            self.dma(ydst[tok, :], ypt[:], [ypt], [ydst])
            self.tt(H[:].rearrange("p (h e) -> p h e", h=16), H[:].rearrange("p (h e) -> p h e", h=16),
                    E3[:, 32:48].unsqueeze(2).to_broadcast([128, 16, 64]), ALU.mult, [H, E3], [H])
            yield
            for g in range(2):
                self.tt(H[:, g * 512:(g + 1) * 512], H[:, g * 512:(g + 1) * 512], Q[g][:], ALU.add, [H, Q[g]], [H])
            yield
            self.act(Hbf[:], H[:], AF.Copy, [H], [Hbf])
            yield

    def stage_R(self, l):
        I = self.I
        cst = self.cst
        import os as _os
        with ExitStack() as st:
            wv = I["w_in"][l].rearrange("(k p) c -> p k c", p=128)
            wqk = self.sb(st, [128, 8, 1024], BF16, "wqk")
            wvv = self.sb(st, [128, 8, 512], BF16, "wvr")
            for kc in range(8):
                self.load_w(wqk[:, kc, 0:512], wqk, wv[:, kc, 3760:4272], I["w_in"])
                self.load_w(wqk[:, kc, 512:1024], wqk, wv[:, kc, 5328:5840], I["w_in"])
                self.load_w(wvv[:, kc, :], wvv, wv[:, kc, 4272:4784], I["w_in"])
            prm = {}
            for d_, sfx in ((0, "f"), (1, "b")):
                nm = "ret_log_rate_" + sfx
                lgB = self.bvec(st, nm, l, 4)
                self.act(lgB[:], lgB[:], AF.Exp, [lgB], [lgB])
                self.ts(lgB[:], lgB[:], -1.0, None, ALU.mult, None, [lgB], [lgB])
                lgs = self.sb(st, [128, 2], F32, "lgs")
                src = I[nm][l:l + 1, :].rearrange("o (p two) -> o p two", two=2)
                self.dma(lgs[0:64, :], src[:, :, 0].to_broadcast([64, 2]), [I[nm]], [lgs], slow=True)
                self.dma(lgs[64:128, :], src[:, :, 1].to_broadcast([64, 2]), [I[nm]], [lgs], slow=True)
                self.act(lgs[:], lgs[:], AF.Exp, [lgs], [lgs])
                self.ts(lgs[:], lgs[:], -1.0, None, ALU.mult, None, [lgs], [lgs])
                RIo = C_RIF if d_ == 0 else C_RIB
                Mo = C_MF if d_ == 0 else C_MB
                Go = C_GF if d_ == 0 else C_GB
                To = C_TEF if d_ == 0 else C_TEB
                DmT = self.sb(st, [128, 4, 128], F32, "DmT")
                for h in range(4):
                    self.act(DmT[:, h, :], cst[:, RIo:RIo + 128], AF.Exp, [cst, lgB], [DmT], scale=lgB[:, h:h + 1])
                self.tt(DmT[:], DmT[:], cst[:, Mo:Mo + 128].unsqueeze(1).to_broadcast([128, 4, 128]), ALU.mult, [DmT, cst], [DmT])
                Gam = self.sb(st, [128, 2, 128], F32, "Gam")
                for p in range(2):
                    self.act(Gam[:, p, :], cst[:, Go:Go + 128], AF.Exp, [cst, lgs], [Gam], scale=lgs[:, p:p + 1])
                te = self.sb(st, [128, 4], F32, "te")
                self.act(te[:], lgB[:], AF.Exp, [lgB, cst], [te], scale=cst[:, To:To + 1])
                g128 = self.sb(st, [128, 2], F32, "g128")
                self.act(g128[:], lgs[:], AF.Exp, [lgs], [g128], scale=128.0)
                prm[d_] = (DmT, Gam, te, g128)
            with ExitStack() as stf:
                sets = []
                for s_ in range(2):
                    B = {}
                    B["hc"] = self.sb(stf, [128, 8, 128], BF16, "hcrf")
                    B["tab"] = self.sb(stf, [128, 1024], F32, "tabrf")
                    B["r1"] = self.sb(stf, [128, 512], F32, "r1")
                    B["r2"] = self.sb(stf, [128, 512], F32, "r2")
                    B["qkt"] = self.sb(stf, [128, 512], BF16, "qkt")
                    B["Rb"] = self.sb(stf, [128, 1280], BF16, "Rbw")
                    B["Q"] = [self.ps(stf, [128, 512], F32, "QRF") for _ in range(4)]
                    sets.append(B)
                gens = [(lambda cc: (lambda slot: self.ret_front(cc, sets[slot], wqk, wvv)))(c) for c in range(NCH)]
                self.interleave(gens, 2)
            self.P.barrier()
            with ExitStack() as st2:
                gens = []
                for d_ in (0, 1):
                    gens.append((lambda dd: (lambda slot: self.ret_sweep(l, dd, st2, prm[dd])))(d_))
                self.interleave(gens, 2)
            self.P.barrier()
            with ExitStack() as st3:
                wg = self.sb(st3, [128, 8, 512], BF16, "wg")
                for kc in range(8):
                    self.load_w(wg[:, kc, :], wg, wv[:, kc, 4784:5296], I["w_in"])
                sets = []
                for s in range(2):
                    B = {}
                    B["hc"] = self.sb(st3, [128, 8, 128], BF16, "hcr3")
                    B["rpf"] = self.sb(st3, [128, 512], F32, "rpf")
                    B["rpb"] = self.sb(st3, [128, 512], F32, "rpb")
                    B["ge"] = self.sb(st3, [128, 512], F32, "ge3")
                    B["sg"] = self.sb(st3, [128, 512], F32, "sg3")
                    B["stats"] = self.sb(st3, [128, 4, 6], F32, "stats3")
                    B["mv"] = self.sb(st3, [128, 4, 2], F32, "mv3")
                    B["yn"] = self.sb(st3, [128, 512], F32, "yn3")
                    B["ob"] = self.sb(st3, [128, 512], BF16, "obr3")
                    B["oT"] = self.sb(st3, [128, 4, 128], BF16, "oTr3")
                    B["PG"] = self.ps(st3, [128, 512], F32, "PGr3")
                    B["PT"] = self.ps(st3, [128, 1024], BF16, "PTr3")
                    sets.append(B)
                gens = [(lambda cc: (lambda slot: self.ret_final(cc, sets[slot], wg)))(c) for c in range(NCH)]
                self.interleave(gens, 2)

    def ret_final(self, c, B, wg):
        h_ = B["hc"]; rpf = B["rpf"]; rpb = B["rpb"]; ge = B["ge"]; sg = B["sg"]; stats = B["stats"]; mv = B["mv"]
        yn = B["yn"]; ob = B["ob"]; oT_ = B["oT"]; PG = B["PG"]; PT = B["PT"]
        mixv = self.mixT_d[:].rearrange("(r p) t -> p r t", p=128)
        c0 = colof(c * 128)
        tok = slice(c * 128, (c + 1) * 128)
        self.dma(h_[:], self.hv[:, :, c0:c0 + 128], [self.hT_d], [h_], slow=True)
        self.dma(rpf[:], self.rpart_d[tok, :], [self.rpart_d], [rpf])
        self.dma(rpb[:], self.rpartb_d[tok, :], [self.rpartb_d], [rpb])
        yield
        for kc in range(8):
            self.mm(PG[:], h_[:, kc, :], wg[:, kc, :], kc == 0, kc == 7, [h_, wg], [PG])
        self.tt(rpf[:], rpf[:], rpb[:], ALU.add, [rpf, rpb], [rpf])
        yield
        self.act(ge[:], PG[:], AF.Exp, [PG], [ge], scale=-1.0)
        for h in range(4):
            self.P.op(DVE, (lambda hh: (lambda e: e.bn_stats(out=stats[:, hh, :], in_=rpf[:, hh * 128:(hh + 1) * 128])))(h),
                      [rpf.b], [stats.b])
            self.P.op(DVE, (lambda hh: (lambda e: e.bn_aggr(out=mv[:, hh, :], in_=stats[:, hh, :])))(h),
                      [stats.b], [mv.b])
        self.ts(mv[:, :, 1], mv[:, :, 1], LN_EPS, None, ALU.add, None, [mv], [mv])
        yield
        self.rsqrt(mv[:, :, 1], mv)
        self.sigm(ge[:], ge)
        yield
        self.tt(sg[:], PG[:], ge[:], ALU.mult, [PG, ge], [sg])
        for h in range(4):
            self.ts(yn[:, h * 128:(h + 1) * 128], rpf[:, h * 128:(h + 1) * 128], mv[:, h, 0:1], mv[:, h, 1:2],
                    ALU.subtract, ALU.mult, [rpf, mv], [yn])
        yield
        self.tt(ob[:], yn[:], sg[:], ALU.mult, [yn, sg], [ob])
        yield
        for h in range(4):
            self.tr(PT[:, h * 128:(h + 1) * 128], ob[:, h * 128:(h + 1) * 128], self.identb[:], [ob, self.identb], [PT])
        yield
        self.act(oT_[:], PT[:, 0:512].rearrange("p (a b) -> p a b", a=4), AF.Copy, [PT], [oT_])
        yield
        self.dma(mixv[:, 12:16, tok], oT_[:], [oT_], [self.mixT_d], slow=True)
        yield

    def ret_front(self, c, B, wqk, wvv):
        I = self.I
        h_ = B["hc"]; tab = B["tab"]; r1 = B["r1"]; r2 = B["r2"]; qkt = B["qkt"]; Rb = B["Rb"]; Q = B["Q"]
        Qb3 = Q[3][:].bitcast(BF16)
        c0 = colof(c * 128)
        self.dma(h_[:], self.hv[:, :, c0:c0 + 128], [self.hT_d], [h_], slow=True)
        self.dma(tab[:], I["ret_tab2"][c * 128:(c + 1) * 128, :], [I["ret_tab2"]], [tab])
        yield
        for j, (q_, w_, lo) in enumerate(((Q[0], wqk, 0), (Q[1], wqk, 512), (Q[2], wvv, 0))):
            for kc in range(8):
                self.mm(q_[:], h_[:, kc, :], w_[:, kc, lo:lo + 512], kc == 0, kc == 7, [h_, w_], [q_])
            yield
        self.tt(r1[:], Q[0][:], tab[:, 0:512], ALU.mult, [Q[0], tab], [r1])
        yield
        self.tt(r2[:], Q[1][:], tab[:, 512:1024], ALU.mult, [Q[1], tab], [r2])
        self.act(Rb[:, 768:1280], Q[2][:], AF.Copy, [Q[2]], [Rb])
        yield
        self.tt(qkt[:], r1[:], r2[:], ALU.add, [r1, r2], [qkt], eng=POOL)
        yield
        for j in range(4):
            self.tr(Qb3[:, j * 128:(j + 1) * 128], qkt[:, j * 128:(j + 1) * 128], self.identb[:], [qkt, self.identb], [Q[3]])
        self.cp(Rb[:, 512:768], qkt[:, 256:512], [qkt], [Rb], eng=POOL)
        yield
        self.act(Rb[:, 0:512], Qb3[:, 0:512], AF.Copy, [Q[3]], [Rb])
        yield
        self.dma(self.Rb_d[c], Rb[:], [Rb], [self.Rb_d])
        yield

    def ret_sweep(self, l, d_, st, prm):
        DmT, Gam, te, g128 = prm
        S = self.sb(st, [128, 2, 128], F32, "S")
        Sbf = self.sb(st, [128, 2, 128], BF16, "Sbf")
        Rb = [self.sb(st, [128, 1280], BF16, "Rbr") for _ in range(2)]
        qz = [self.sb(st, [128, 2, 128], BF16, "qz") for _ in range(2)]
        qdz = [self.sb(st, [128, 2, 128], BF16, "qdz") for _ in range(2)]
        for par in range(2):
            self.memset(qz[par][:], 0.0, [qz[par]])
            self.memset(qdz[par][:], 0.0, [qdz[par]])
        vte = self.sb(st, [128, 512], BF16, "vte")
        Mr = self.sb(st, [128, 4, 128], BF16, "Mr")
        yp = [self.sb(st, [128, 512], F32, "ypr") for _ in range(2)]
        Q = [self.ps(st, [128, 512], F32, "QR") for _ in range(3)]
        ydst = self.rpart_d if d_ == 0 else self.rpartb_d
        order = list(range(NCH)) if d_ == 0 else [1, 0] + list(range(NCH - 1, 1, -1))
        self.memset(S[:], 0.0, [S])
        self.memset(Sbf[:], 0.0, [Sbf])
        self.dma(Rb[0][:], self.Rb_d[order[0]], [self.Rb_d], [Rb[0]])
        it = 0
        for ci, c in enumerate(order):
            rb = Rb[it % 2]; ypt = yp[it % 2]
            it += 1
            tok = slice(c * 128, (c + 1) * 128)
            if ci + 1 < len(order):
                self.dma(Rb[it % 2][:], self.Rb_d[order[ci + 1]], [self.Rb_d], [Rb[it % 2]])
            qk = rb[:, 0:512].rearrange("p (a b) -> p a b", a=4)
            for par in range(2):
                rr = 64 * par
                self.cp(qz[par][rr:rr + 64, :, :], qk[rr:rr + 64, 0:2, :], [rb], [qz[par]], eng=POOL)
                self.tt(qdz[par][rr:rr + 64, :, :], qk[rr:rr + 64, 0:2, :], Gam[rr:rr + 64, :, :], ALU.mult,
                        [rb, Gam], [qdz[par]], eng=POOL)
            self.tt(vte[:].rearrange("p (h e) -> p h e", h=4), rb[:, 768:1280].rearrange("p (h e) -> p h e", h=4),
                    te[:].unsqueeze(2).to_broadcast([128, 4, 128]), ALU.mult, [rb, te], [vte], eng=POOL)
            yield
            for h in range(4):
                p = h // 2
                self.mm(Q[0][:, h * 128:(h + 1) * 128], qk[:, 2 + p, :], qz[h % 2][:, p, :], True, True, [rb, qz[h % 2]], [Q[0]])
            yield
            self.tt(Mr[:], Q[0][:].rearrange("p (a b) -> p a b", a=4), DmT[:], ALU.mult, [Q[0], DmT], [Mr])
            yield
            for h in range(4):
                p = h // 2
                self.mm(Q[1][:, h * 128:(h + 1) * 128], Mr[:, h, :], rb[:, 768 + h * 128:768 + (h + 1) * 128], True, False, [Mr, rb], [Q[1]])
                self.mm(Q[1][:, h * 128:(h + 1) * 128], qdz[h % 2][:, p, :], Sbf[:, p, :], False, True, [qdz[h % 2], Sbf], [Q[1]])
            for h in range(4):
                p = h // 2
                self.mm(Q[2][:, h * 128:(h + 1) * 128], rb[:, 512 + p * 128:512 + (p + 1) * 128], vte[:, h * 128:(h + 1) * 128],
                        True, True, [rb, vte], [Q[2]])
            yield
            self.cp(ypt[:], Q[1][:], [Q[1]], [ypt])
            self.dma(ydst[tok, :], ypt[:], [ypt], [ydst])
            for h in range(4):
                p, r0 = h // 2, (h % 2) * 64
                self.stt(S[r0:r0 + 64, p, :], S[r0:r0 + 64, p, :], g128[r0:r0 + 64, p:p + 1], Q[2][r0:r0 + 64, h * 128:(h + 1) * 128],
                         ALU.mult, ALU.add, [S, g128, Q[2]], [S])
            yield
            self.act(Sbf[:], S[:], AF.Copy, [S], [Sbf])
            yield

    def stage_M(self, l):
        I = self.I
        cst = self.cst
        blocks = [(0, 256)] + [(256 + 512 * i, 512) for i in range(8)]
        with ExitStack() as st1:
            v_all = self.sb(st1, [128, NCH, 512], BF16, "v_all")
            with ExitStack() as st:
                wv = I["w_in"][l].rearrange("(k p) c -> p k c", p=128)
                wm = self.sb(st, [128, 8, 704], BF16, "wm")
                wgate = self.sb(st, [128, 8, 512], BF16, "wgate")
                self.load_w(wm[:, :, 0:672], wm, wv[:, :, 2576:3248], I["w_in"])
                self.load_w(wm[:, :, 672:704], wm, wv[:, :, 5296:5328], I["w_in"])
                self.load_w(wgate[:], wgate, wv[:, :, 3248:3760], I["w_in"])
                wuq = self.sb(st, [128, 3, 8, 96], BF16, "wuq")
                wuqs = self.sb(st, [128, 3, 8, 96], BF16, "wuqs")
                wkp = self.sb(st, [128, 2, 8, 96], BF16, "wkp")
                wvv = self.sb(st, [128, 2, 8, 64], BF16, "wvv")
                self.memset(wuqs[:], 0.0, [wuqs])
                self.memset(wkp[:], 0.0, [wkp])
                uqv = I["mla_w_uq"][l].rearrange("(k p) (h e) -> p k h e", p=128, h=8)
                uqs = I["w_uq_sw"][l].rearrange("(k p) (h e) -> p k h e", p=128, h=8)
                ukv = I["mla_w_ukv"][l].rearrange("(k p) (h e) -> p k h e", p=128, h=8)
                for kc in range(3):
                    self.load_w(wuq[:, kc, :, :], wuq, uqv[:, kc, :, :], I["mla_w_uq"])
                    self.load_w(wuqs[:, kc, :, 64:96], wuqs, uqs[:, kc, :, :], I["w_uq_sw"])
                for kc in range(2):
                    self.load_w(wkp[:, kc, :, 0:64], wkp, ukv[:, kc, :, 0:64], I["mla_w_ukv"])
                    self.load_w(wvv[:, kc, :, :], wvv, ukv[:, kc, :, 64:128], I["mla_w_ukv"])
                esel = self.sb(st, [32, 96], BF16, "esel")
                self.memset(esel[:], 0.0, [esel])
                self.cp(esel[:, 64:96], self.identb[0:32, 0:32], [self.identb, esel], [esel])
                qn = self.sb(st, [128, 3], F32, "qn")
                kvn = self.sb(st, [128, 2], F32, "kvn")
                self.dma(qn[:], I["mla_q_norm"][l].rearrange("(k p) -> p k", p=128), [I["mla_q_norm"]], [qn], slow=True)
                self.dma(kvn[:], I["mla_kv_norm"][l].rearrange("(k p) -> p k", p=128), [I["mla_kv_norm"]], [kvn], slow=True)
                hb = [self.sb(st, [128, 8, 512], BF16, "hb") for _ in range(2)]
                tq = self.sb(st, [96, 2, 512], F32, "tq")
                self.memset(tq[0:64, 0, :], 1.0, [tq])
                self.memset(tq[0:64, 1, :], 0.0, [tq])
                tk = self.sb(st, [32, 2, 512], F32, "tk")
                cqs = self.sb(st, [128, 3, 512], F32, "cqs")
                sqs2 = [self.sb(st, [128, 512], F32, "sqs") for _ in range(2)]
                rstd = self.sb(st, [128, 512], F32, "rstd")
                cqn = self.sb(st, [128, 3, 512], BF16, "cqn")
                ckvn = self.sb(st, [128, 2, 512], BF16, "ckvn")
                kr1 = self.sb(st, [32, 512], F32, "kr1")
                kr2 = self.sb(st, [32, 512], F32, "kr2")
                krr = self.sb(st, [32, 512], BF16, "krr")
                q1s = [self.sb(st, [96, 512], F32, "q1") for _ in range(2)]
                q2s = [self.sb(st, [96, 512], F32, "q2") for _ in range(2)]
                qf = [self.sb(st, [96, 512], BF16, "qf") for _ in range(2)]
                kf = [self.sb(st, [96, 512], BF16, "kf") for _ in range(2)]
                ges = [self.sb(st, [128, 512], F32, "ge") for _ in range(2)]
                sgo = [self.sb(st, [128, 512], F32, "sgo") for _ in range(2)]
                P0 = [self.ps(st, [128, 512], F32, "P0") for _ in range(2)]
                PSS = self.ps(st, [128, 512], F32, "PSS")
                PQ1 = self.ps(st, [128, 512], F32, "PQ1")
                PQ2 = self.ps(st, [128, 512], F32, "PQ2")
                PK = self.ps(st, [128, 512], F32, "PK")
                PVv = self.ps(st, [128, 512], F32, "PVv")
                PG = self.ps(st, [128, 512], F32, "PG")
                sgv = self.sgT_d[:].rearrange("(r p) t -> p r t", p=128)
                ctr = 0
                for bi, (t0, n) in enumerate(blocks):
                    h_ = hb[bi % 2]
                    c0 = colof(t0)
                    self.dma(h_[:, :, 0:n], self.hv[:, :, c0:c0 + n], [self.hT_d], [h_], slow=True)
                    self.dma(tq[64:96, :, 0:n], I["mla_tab"][:, :, t0:t0 + n], [I["mla_tab"]], [tq], slow=True)
                    self.dma(tk[:, :, 0:n], I["mla_tab"][:, :, t0:t0 + n], [I["mla_tab"]], [tk], slow=True)
                    for (nrc, off, dst, nrm, dim, keep) in ((3, 0, cqn, qn, 384.0, None), (2, 384, ckvn, kvn, 256.0, None)):
                        for rc in range(nrc):
                            p_ = P0[ctr % 2]; ctr += 1
                            for kc in range(8):
                                self.mm(p_[:, 0:n], wm[:, kc, off + rc * 128:off + (rc + 1) * 128], h_[:, kc, 0:n], kc == 0, kc == 7, [wm, h_], [p_])
                            sqs = sqs2[ctr % 2]
                            self.act(cqs[:, rc, 0:n], p_[:, 0:n], AF.Copy, [p_], [cqs])
                            self.act(sqs[:, 0:n], p_[:, 0:n], AF.Square, [p_], [sqs])
                            self.mm(PSS[:, 0:n], self.ones[:], sqs[:, 0:n], rc == 0, rc == nrc - 1, [self.ones, sqs], [PSS])
                        self.ts(rstd[:, 0:n], PSS[:, 0:n], 1.0 / dim, RMS_EPS, ALU.mult, ALU.add, [PSS], [rstd])
                        self.rsqrt(rstd[:, 0:n], rstd)
                        for rc in range(nrc):
                            self.stt(dst[:, rc, 0:n], cqs[:, rc, 0:n], nrm[:, rc:rc + 1], rstd[:, 0:n], ALU.mult, ALU.mult, [cqs, nrm, rstd], [dst])
                    self.mm_group_kr(h_, n, wm, PQ1, PQ2)
                    self.tt(kr1[:, 0:n], PQ1[0:32, 0:n], tk[:, 0, 0:n], ALU.mult, [PQ1, tk], [kr1])
                    self.tt(kr2[:, 0:n], PQ2[0:32, 0:n], tk[:, 1, 0:n], ALU.mult, [PQ2, tk], [kr2])
                    self.tt(krr[:, 0:n], kr1[:, 0:n], kr2[:, 0:n], ALU.add, [kr1, kr2], [krr])
                    for h in range(8):
                        qf_ = qf[h % 2]; kf_ = kf[h % 2]
                        q1 = q1s[h % 2]; q2 = q2s[h % 2]
                        PQ1_, PQ2_, PK_ = (PQ1, PQ2, PK) if h % 2 == 0 else (P0[0], P0[1], PG)
                        for kc in range(3):
                            self.mm(PQ1_[0:96, 0:n], wuq[:, kc, h, :], cqn[:, kc, 0:n], kc == 0, kc == 2, [wuq, cqn], [PQ1_])
                        for kc in range(3):
                            self.mm(PQ2_[0:96, 0:n], wuqs[:, kc, h, :], cqn[:, kc, 0:n], kc == 0, kc == 2, [wuqs, cqn], [PQ2_])
                        self.tt(q1[:, 0:n], PQ1_[0:96, 0:n], tq[:, 0, 0:n], ALU.mult, [PQ1_, tq], [q1])
                        self.tt(q2[:, 0:n], PQ2_[0:96, 0:n], tq[:, 1, 0:n], ALU.mult, [PQ2_, tq], [q2])
                        self.tt(qf_[:, 0:n], q1[:, 0:n], q2[:, 0:n], ALU.add, [q1, q2], [qf_], eng=POOL)
                        self.dma(self.qT_d[h, :, t0:t0 + n], qf_[:, 0:n], [qf_], [self.qT_d])
                        for kc in range(2):
                            self.mm(PK_[0:96, 0:n], wkp[:, kc, h, :], ckvn[:, kc, 0:n], kc == 0, False, [wkp, ckvn], [PK_])
                        self.mm(PK_[0:96, 0:n], esel[:], krr[:, 0:n], False, True, [esel, krr], [PK_])
                        self.act(kf_[:, 0:n], PK_[0:96, 0:n], AF.Copy, [PK_], [kf_])
                        self.dma(self.kfT_d[h, :, t0:t0 + n], kf_[:, 0:n], [kf_], [self.kfT_d])
                    for s in range(n // 128):
                        ch = (t0 + s * 128) // 128
                        PV_ = PVv if s % 2 == 0 else PSS
                        for kc in range(2):
                            self.mm(PV_[:], ckvn[:, kc, s * 128:(s + 1) * 128], wvv[:, kc, :, :].rearrange("p h e -> p (h e)"),
                                    kc == 0, kc == 1, [ckvn, wvv], [PV_])
                        self.act(v_all[:, ch, :], PV_[:], AF.Copy, [PV_], [v_all])
                    for rc in range(4):
                        sg_ = sgo[rc % 2]
                        ge = ges[rc % 2]
                        PG_ = PG if rc % 2 == 0 else PK
                        for kc in range(8):
                            self.mm(PG_[:, 0:n], wgate[:, kc, rc * 128:(rc + 1) * 128], h_[:, kc, 0:n], kc == 0, kc == 7, [wgate, h_], [PG_])
                        self.act(ge[:, 0:n], PG_[:, 0:n], AF.Exp, [PG_], [ge], scale=-1.0)
                        self.sigm(ge[:, 0:n], ge)
                        self.tt(sg_[:, 0:n], PG_[:, 0:n], ge[:, 0:n], ALU.mult, [PG_, ge], [sg_])
                        self.dma(sgv[:, rc, t0:t0 + n], sg_[:, 0:n], [sg_], [self.sgT_d])
            self.P.barrier()
            import os as _os
            if _os.environ.get("NO_M2"):
                return
            with ExitStack() as st:
                kfh = [self.sb(st, [96, NTOK], BF16, "kfh") for _ in range(2)]
                qh = [self.sb(st, [96, NTOK], BF16, "qh") for _ in range(2)]
                vaug = [self.sb(st, [128, NCH, 128], BF16, "vaug") for _ in range(2)]
                self.memset(vaug[0][:, :, 64:128], 1.0, [vaug[0]])
                self.memset(vaug[1][:, :, 0:64], 1.0, [vaug[1]])
                pT = [self.sb(st, [128, 512], BF16, "pT") for _ in range(3)]
                sgh = [self.sb(st, [128, 512], F32, "sgh") for _ in range(2)]
                rden = self.sb(st, [128, 512], F32, "rden")
                ot = self.sb(st, [128, 512], F32, "ot")
                ob = [self.sb(st, [128, 512], BF16, "ob") for _ in range(2)]
                PSc = [self.ps(st, [128, 512], F32, "PSc") for _ in range(3)]
                PO = [self.ps(st, [128, 512], F32, "PO") for _ in range(2)]
                ci = 0
                bi_ = 0
                for h in range(int(_os.environ.get("M2_HEADS", "8"))):
                    par = h % 2
                    r0 = 64 * par
                    d0 = 64 - r0
                    kf_ = kfh[h % 2]; q_ = qh[h % 2]; va = vaug[par]
                    self.dma(kf_[:], self.kfT_d[h], [self.kfT_d], [kf_])
                    self.dma(q_[:], self.qT_d[h], [self.qT_d], [q_])
                    self.cp(va[:, :, r0:r0 + 64], v_all[:, :, h * 64:(h + 1) * 64], [v_all], [va], eng=POOL)
                    its = []
                    for (t0, n) in blocks:
                        if t0 == 0:
                            if l == DEPTH - 1:
                                continue
                            kcs = [0, 1]
                        else:
                            kcs = list(range(NCH))
                        po = PO[bi_ % 2]; sg_ = sgh[bi_ % 2]; ob_ = ob[bi_ % 2]
                        bi_ += 1
                        for i, kc in enumerate(kcs):
                            its.append((t0, n, kc, i == 0, i == len(kcs) - 1, po, sg_, ob_))
                    LOOK = 2
                    for j in range(len(its) + LOOK):
                        if j < len(its):
                            (t0, n, kc, first, last, po, sg_, ob_) = its[j]
                            if first:
                                self.dma(sg_[r0:r0 + 64, 0:n], self.sgT_d[64 * h:64 * (h + 1), t0:t0 + n], [self.sgT_d], [sg_])
                            psc = PSc[(ci + j) % 3]
                            self.mm(psc[:, 0:n], kf_[:, kc * 128:(kc + 1) * 128], q_[:, t0:t0 + n], True, True, [kf_, q_], [psc])
                        jj = j - LOOK
                        if jj >= 0:
                            (t0, n, kc, first, last, po, sg_, ob_) = its[jj]
                            psc = PSc[(ci + jj) % 3]; pt = pT[(ci + jj) % 3]
                            self.act(pt[:, 0:n], psc[:, 0:n], AF.Exp, [psc], [pt], scale=MLA_SCALE)
                            self.mm(po[:, 0:n], va[:, kc, :], pt[:, 0:n], first, last, [va, pt], [po])
                            if last:
                                self.P.op(DVE, (lambda a_, b_: (lambda e: e.reciprocal(out=a_, in_=b_)))(rden[d0:d0 + 64, 0:n], po[d0:d0 + 64, 0:n]),
                                          [po.b], [rden.b])
                                self.tt(ot[r0:r0 + 64, 0:n], po[r0:r0 + 64, 0:n], rden[d0:d0 + 64, 0:n], ALU.mult, [po, rden], [ot])
                                self.tt(ob_[r0:r0 + 64, 0:n], ot[r0:r0 + 64, 0:n], sg_[r0:r0 + 64, 0:n], ALU.mult, [ot, sg_], [ob_], eng=POOL)
                                self.dma(self.mixT_d[1024 + h * 64:1024 + (h + 1) * 64, t0:t0 + n], ob_[r0:r0 + 64, 0:n], [ob_], [self.mixT_d])
                    ci += len(its)

    def mm_group_kr(self, h_, n, wm, PQ1, PQ2):
        for kc in range(8):
            self.mm(PQ1[0:32, 0:n], wm[:, kc, 640:672], h_[:, kc, 0:n], kc == 0, kc == 7, [wm, h_], [PQ1])
        for kc in range(8):
            self.mm(PQ2[0:32, 0:n], wm[:, kc, 672:704], h_[:, kc, 0:n], kc == 0, kc == 7, [wm, h_], [PQ2])

    def stage_E(self, l):
        I = self.I
        with ExitStack() as st:
            wo = self.sb(st, [128, 16, 1024], BF16, "wo")
            wov = I["w_out"][l].rearrange("(k p) c -> p k c", p=128)
            for kc in range(16):
                self.load_w(wo[:, kc, :], wo, wov[:, kc, :], I["w_out"])
            lng = self.bvec(st, "ln_g", l, 1024)
            lnb = self.bvec(st, "ln_b", l, 1024)
            sets = []
            for s_ in range(3):
                B = {}
                B["mt"] = self.sb(st, [128, 16, 128], BF16, "mt")
                B["xt"] = self.sb(st, [128, D], F32, "xt")
                B["v1"] = self.sb(st, [128, D], F32, "v1")
                B["v2"] = self.sb(st, [128, D], F32, "v2")
                B["xo"] = self.sb(st, [128, D], F32, "xo")
                B["stats"] = self.sb(st, [128, 2, 6], F32, "stats")
                B["mv"] = self.sb(st, [128, 2], F32, "mv")
                B["PZ"] = [self.ps(st, [128, 512], F32, "PZ") for _ in range(2)]
                sets.append(B)
            tiles = list(range(NCH)) if l < DEPTH - 1 else list(range(2, NCH))
            gens = [(lambda tt_: (lambda slot: self.e_tile(l, tt_, sets[slot], wo, lng, lnb)))(t) for t in tiles]
            self.interleave(gens, 3)

    def e_tile(self, l, t, B, wo, lng, lnb):
        I = self.I
        m_ = B["mt"]; x_ = B["xt"]; v1 = B["v1"]; v2 = B["v2"]; xo_ = B["xo"]; stats = B["stats"]; mv = B["mv"]; PZ = B["PZ"]
        mixv = self.mixT_d[:].rearrange("(r p) t -> p r t", p=128)
        typ = 1 if t < 2 else 0
        tok = slice(t * 128, (t + 1) * 128)
        self.dma(m_[:], mixv[:, :, tok], [self.mixT_d], [m_], slow=True)
        if l == 0:
            src_t = I["ctx"] if t < 2 else I["x"]
            src = src_t[t * 128:(t + 1) * 128, :] if t < 2 else src_t[(t - 2) * 128:(t - 1) * 128, :]
        else:
            src_t = self.xres_d
            src = src_t[tok, :]
        self.dma(x_[:], src, [src_t], [x_])
        yield
        for nb in range(2):
            for kc in range(16):
                self.mm(PZ[nb][:], m_[:, kc, :], wo[:, kc, nb * 512:(nb + 1) * 512], kc == 0, kc == 15, [m_, wo], [PZ[nb]])
            yield
        for nb in range(2):
            self.tt(v1[:, nb * 512:(nb + 1) * 512], PZ[nb][:], self.gB[:, typ, nb * 512:(nb + 1) * 512], ALU.mult, [PZ[nb], self.gB], [v1])
        yield
        self.stt(v2[:], x_[:], ALPHA, v1[:], ALU.mult, ALU.add, [x_, v1], [v2])
        yield
        for s in range(2):
            self.P.op(DVE, (lambda ss: (lambda e: e.bn_stats(out=stats[:, ss, :], in_=v2[:, ss * 512:(ss + 1) * 512])))(s),
                      [v2.b], [stats.b])
        self.P.op(DVE, lambda e: e.bn_aggr(out=mv[:], in_=stats[:]), [stats.b], [mv.b])
        self.ts(mv[:, 1:2], mv[:, 1:2], LN_EPS, None, ALU.add, None, [mv], [mv])
        yield
        self.rsqrt(mv[:, 1:2], mv)
        yield
        self.ts(v1[:], v2[:], mv[:, 0:1], mv[:, 1:2], ALU.subtract, ALU.mult, [v2, mv], [v1])
        yield
        self.tt(v2[:], v1[:], lng[:], ALU.mult, [v1, lng], [v2], eng=POOL)
        yield
        self.tt(xo_[:], v2[:], lnb[:], ALU.add, [v2, lnb], [xo_], eng=POOL)
        yield
        if l < DEPTH - 1:
            self.dma(self.xres_d[tok, :], xo_[:], [xo_], [self.xres_d])
        else:
            self.dma(self.out[(t - 2) * 128:(t - 1) * 128, :], xo_[:], [xo_], [self.out])
        yield


C_ID = 0
C_UF = 128
C_LF = 256
C_UB = 384
C_LB = 512
C_MF = 640
C_MB = 768
C_RIF = 896
C_RIB = 1024
C_GF = 1152
C_GB = 1280
C_TEF = 1408
C_TEB = 1409
CST_W = 1410


def make_consts():
    k = np.arange(128)[:, None].astype(np.float32)
    i = np.arange(128)[None, :].astype(np.float32)
    cst = np.zeros((128, CST_W), np.float32)
    cst[:, C_ID:C_ID + 128] = (k == i)
    cst[:, C_UF:C_UF + 128] = (k <= i)
    cst[:, C_LF:C_LF + 128] = (k > i)
    cst[:, C_UB:C_UB + 128] = (k >= i)
    cst[:, C_LB:C_LB + 128] = (k < i)
    cst[:, C_MF:C_MF + 128] = (k <= i)
    cst[:, C_MB:C_MB + 128] = (k >= i)
    cst[:, C_RIF:C_RIF + 128] = np.maximum(i - k, 0)
    cst[:, C_RIB:C_RIB + 128] = np.maximum(k - i, 0)
    cst[:, C_GF:C_GF + 128] = np.broadcast_to(i + 1, (128, 128))
    cst[:, C_GB:C_GB + 128] = np.broadcast_to(128 - i, (128, 128))
    cst[:, C_TEF] = 127 - k[:, 0]
    cst[:, C_TEB] = k[:, 0]
    return cst


def rope_tables():
    rows = SEQ // 64
    t = np.arange(rows * 64)
    row = (t // 64).astype(np.float32)
    col = (t % 64).astype(np.float32)

    def cs(rot):
        nf = rot // 4
        inv = (np.float32(10000.0) ** (-np.arange(nf, dtype=np.float32) / np.float32(nf))).astype(np.float32)
        ang = np.concatenate([row[:, None] * inv, col[:, None] * inv], -1).astype(np.float32)
        return np.cos(ang).astype(np.float32), np.sin(ang).astype(np.float32)
    cm, sm = cs(32)
    mla = np.zeros((32, 2, NTOK), np.float32)
    mla[:, 0, :CTX] = 1.0
    mla[0:16, 0, CTX:] = cm.T; mla[16:32, 0, CTX:] = cm.T
    mla[0:16, 1, CTX:] = -sm.T; mla[16:32, 1, CTX:] = sm.T
    cr, sr = cs(64)
    ret = np.zeros((128, 4, NTOK), np.float32)
    ret[:, 0, :CTX] = 1.0
    for hh in range(2):
        b = hh * 64
        ret[b:b + 32, 0, CTX:] = cr.T; ret[b + 32:b + 64, 0, CTX:] = cr.T
        ret[b:b + 32, 1, CTX:] = -sr.T; ret[b + 32:b + 64, 1, CTX:] = sr.T
    ret[:, 2] = ret[:, 0] * 0.125
    ret[:, 3] = ret[:, 1] * 0.125
    ret2 = np.zeros((NTOK, 1024), np.float32)
    cc = np.ones((NTOK, 4, 64), np.float32)
    ss = np.zeros((NTOK, 4, 64), np.float32)
    cc[CTX:, :, 0:32] = cr[:, None, :]; cc[CTX:, :, 32:64] = cr[:, None, :]
    ss[CTX:, :, 0:32] = -sr[:, None, :]; ss[CTX:, :, 32:64] = sr[:, None, :]
    ret2[:, 0:256] = cc.reshape(NTOK, 256); ret2[:, 256:512] = 0.125 * cc.reshape(NTOK, 256)
    ret2[:, 512:768] = ss.reshape(NTOK, 256); ret2[:, 768:1024] = 0.125 * ss.reshape(NTOK, 256)
    return mla, ret, ret2


_CACHE = {}


def prep_inputs(inputs):
    f = lambda a: np.ascontiguousarray(np.asarray(a, dtype=np.float32))
    w_in = f(inputs["w_in"])
    kr = w_in[:, :, 3216:3248]
    kr_sw = np.concatenate([kr[:, :, 16:32], kr[:, :, 0:16]], -1)

    def sw64(a):
        a = a.reshape(2, D, 4, 2, 32)
        return np.ascontiguousarray(a[:, :, :, ::-1, :]).reshape(2, D, 256)
    q_sw = sw64(w_in[:, :, 3760:4016])
    k_sw = sw64(w_in[:, :, 4016:4272])
    w_ext = np.ascontiguousarray(np.concatenate([w_in, kr_sw, q_sw, k_sw], -1))
    uq = f(inputs["mla_w_uq"]).reshape(2, 384, 8, 96)
    uq_r = uq[:, :, :, 64:96]
    uq_sw = np.ascontiguousarray(np.concatenate([uq_r[..., 16:32], uq_r[..., 0:16]], -1)).reshape(2, 384, 256)
    mla_tab, ret_tab, ret_tab2 = rope_tables()
    shared = {
        "w_ada": f(inputs["w_ada"]), "b_ada": f(inputs["b_ada"]), "w_in": w_ext,
        "ssd_conv_w": f(inputs["ssd_conv_w"]), "ssd_conv_b": f(inputs["ssd_conv_b"]),
        "ssd_norm_w": f(inputs["ssd_norm_w"]), "mla_q_norm": f(inputs["mla_q_norm"]),
        "mla_w_uq": f(inputs["mla_w_uq"]), "w_uq_sw": uq_sw, "mla_kv_norm": f(inputs["mla_kv_norm"]),
        "mla_w_ukv": f(inputs["mla_w_ukv"]), "ret_log_rate_f": f(inputs["ret_log_rate_f"]),
        "ret_log_rate_b": f(inputs["ret_log_rate_b"]), "w_out": f(inputs["w_out"]),
        "ln_g": f(inputs["ln_g"]), "ln_b": f(inputs["ln_b"]),
        "cst": make_consts(), "mla_tab": mla_tab, "ret_tab": ret_tab, "ret_tab2": ret_tab2,
    }
    for n in ("ssd_a_log_f", "ssd_a_log_b", "ssd_dt_bias_f", "ssd_dt_bias_b", "ssd_d"):
        shared[n] = f(inputs[n])
    x = f(inputs["x"]); c = f(inputs["c"]); ctx = f(inputs["ctx"]); c_ctx = f(inputs["c_ctx"])
    maps = []
    for b in range(8):
        m = dict(shared)
        m["x"] = x[b]
        m["ctx"] = ctx[b]
        m["cvec"] = np.ascontiguousarray(np.stack([c[b], c_ctx], 0))
        maps.append(m)
    return maps


def kernel(**inputs):
    if "nc" not in _CACHE:
        _CACHE["nc"] = K().build()
    nc = _CACHE["nc"]
    maps = prep_inputs(inputs)
    res = run_bass_kernel_spmd(nc, maps, core_ids=list(range(8)))
    return np.stack([np.asarray(r["out"], dtype=np.float32) for r in res.results], 0)
```

```python
import math
from contextlib import ExitStack
import numpy as np
import concourse.bass as bass
import concourse.mybir as mybir
from concourse.bass_utils import run_bass_kernel_spmd

F32 = mybir.dt.float32
BF16 = mybir.dt.bfloat16
AF = mybir.ActivationFunctionType
ALU = mybir.AluOpType

PE, ACT, DVE, POOL, SP = "pe", "act", "dve", "pool", "sp"
EPOCH = 30000
DMA_K = 8
DMA_EPOCH = 1800

D = 1024
SEQ = 4096
CTX = 256
NTOK = SEQ + CTX
NCH = NTOK // 128
HC = NTOK + 8
DEPTH = 2
ALPHA = (2 * DEPTH) ** 0.25
LN_EPS = 1e-5
RMS_EPS = 1e-6
MLA_SCALE = 96 ** -0.5
WEXT = 5296 + 32 + 256 + 256


def colof(t):
    return t + 2 if t < CTX else t + 6


class Buf:
    __slots__ = ("name", "lw", "rd", "rdd", "psum")

    def __init__(self, name=""):
        self.name = name
        self.lw = None
        self.rd = {}
        self.rdd = []
        self.psum = False


class Op:
    __slots__ = ("eng", "fn", "deps", "sig", "idx", "dma_slot", "dma_prev")

    def __init__(self, eng, fn):
        self.eng = eng
        self.fn = fn
        self.deps = set()
        self.sig = None
        self.dma_slot = None
        self.dma_prev = None


class Prog:
    def __init__(self, nc):
        self.nc = nc
        self.ops = []
        self.eng = {PE: nc.tensor, ACT: nc.scalar, DVE: nc.vector, POOL: nc.gpsimd, SP: nc.sync}
        self.dma_lists = {}
        self.last = {}

    def op(self, eng, fn, reads=(), writes=(), dma=False):
        o = Op(eng, fn)
        o.idx = len(self.ops)
        for b in reads:
            if b.lw is not None:
                o.deps.add(b.lw)
            if b.psum:
                for e2, r in b.rd.items():
                    if e2 != eng:
                        o.deps.add(r)
        for b in writes:
            if b.lw is not None:
                o.deps.add(b.lw)
            for r in b.rd.values():
                o.deps.add(r)
            for r in b.rdd:
                o.deps.add(r)
        for b in reads:
            if dma:
                b.rdd.append(o.idx)
            else:
                b.rd[eng] = o.idx
        for b in writes:
            b.lw = o.idx
            b.rd = {}
            b.rdd = []
        if dma:
            lst = self.dma_lists.setdefault(eng, [])
            o.dma_slot = len(lst)
            if len(lst) >= DMA_K:
                o.dma_prev = lst[len(lst) - DMA_K]
            lst.append(o.idx)
        o.deps.discard(o.idx)
        self.ops.append(o)
        self.last[eng] = o.idx
        return o

    def barrier(self):
        bufs = {}
        for e in (PE, ACT, DVE, POOL, SP):
            bufs[e] = Buf("bar" + e)
            o = self.op(e, lambda en: en.nop(), writes=[bufs[e]])
            for lst in self.dma_lists.values():
                for d in lst[-DMA_K:]:
                    if d != o.idx:
                        o.deps.add(d)
        for e in (PE, ACT, DVE, POOL, SP):
            self.op(e, lambda en: en.nop(), reads=list(bufs.values()))

    def emit(self, stack):
        nc = self.nc
        ops = self.ops
        needed = set()
        for o in ops:
            for d in o.deps:
                do = ops[d]
                if do.eng == o.eng and o.eng == PE and do.dma_slot is None:
                    continue
                needed.add(d)
            if o.dma_prev is not None:
                needed.add(o.dma_prev)
        cnt = {}
        sems = {}
        dma_sems = {}
        for o in ops:
            if o.dma_slot is not None:
                k = o.dma_slot % DMA_K
                n = o.dma_slot // DMA_K
                key = (o.eng, k, n // DMA_EPOCH)
                if key not in dma_sems:
                    dma_sems[key] = stack.enter_context(nc.semaphore("dq%s%d_%d" % key))
                o.sig = (dma_sems[key], 16 * (n % DMA_EPOCH + 1))
            elif o.idx in needed:
                c = cnt.get(o.eng, 0)
                key = (o.eng, c // EPOCH)
                if key not in sems:
                    sems[key] = stack.enter_context(nc.semaphore("s%s_%d" % key))
                o.sig = (sems[key], c % EPOCH + 1)
                cnt[o.eng] = c + 1
        waited = {}
        nw = 0
        for o in ops:
            e = self.eng[o.eng]
            deps = set(o.deps)
            if o.dma_prev is not None:
                deps.add(o.dma_prev)
            for d in sorted(deps):
                do = ops[d]
                if do.sig is None:
                    continue
                sem, val = do.sig
                key = (o.eng, id(sem))
                if waited.get(key, 0) >= val:
                    continue
                waited[key] = val
                e.wait_ge(sem, val)
                nw += 1
            ins = o.fn(e)
            if o.sig is not None:
                sem, val = o.sig
                ins.then_inc(sem, 16 if o.dma_slot is not None else 1)
        self.nwaits = nw


class T:
    __slots__ = ("t", "b")

    def __init__(self, t, name=""):
        self.t = t
        self.b = Buf(name)

    def __getitem__(self, k):
        return self.t[k]


class K:
    def __init__(self, debug=False, stop_after=None, skip=()):
        self.skip = set(skip)
        self.debug = debug
        self.stop_after = stop_after
        self.nc = bass.Bass("TRN2", target_bir_lowering=False)
        self.P = Prog(self.nc)
        self.uid = 0

    def dram(self, name, shape, dt, kind="Internal"):
        return T(self.nc.dram_tensor(name, list(shape), dt, kind=kind).ap(), name)

    def sb(self, st, shape, dt, name=None):
        self.uid += 1
        name = "%s_%d" % (name or "t", self.uid)
        return T(st.enter_context(self.nc.sbuf_tensor(name, list(shape), dt)), name)

    def ps(self, st, shape, dt=F32, name=None):
        self.uid += 1
        name = "%s_%d" % (name or "p", self.uid)
        t = T(st.enter_context(self.nc.psum_tensor(name, list(shape), dt)), name)
        t.b.psum = True
        return t

    def dma(self, out, in_, reads, writes, eng=SP, slow=False):
        if slow:
            return self.P.op(eng, lambda e: e.dma_start(out=out, in_=in_, allow_slow_non_contiguous=True),
                             [x.b for x in reads], [x.b for x in writes], dma=True)
        return self.P.op(eng, lambda e: e.dma_start(out=out, in_=in_), [x.b for x in reads], [x.b for x in writes], dma=True)

    def mm(self, out, lhsT, rhs, start, stop, reads, writes):
        return self.P.op(PE, lambda e: e.matmul(out, lhsT=lhsT, rhs=rhs, start=start, stop=stop),
                         [x.b for x in reads], [x.b for x in writes])

    def tr(self, out, in_, ident, reads, writes):
        return self.P.op(PE, lambda e: e.transpose(out=out, in_=in_, identity=ident),
                         [x.b for x in reads], [x.b for x in writes])

    def act(self, out, in_, func, reads, writes, bias=None, scale=None, accum_out=None):
        kw = {}
        if bias is not None:
            kw["bias"] = bias
        if scale is not None:
            kw["scale"] = scale
        if accum_out is not None:
            kw["accum_out"] = accum_out
        return self.P.op(ACT, lambda e: e.activation(out=out, in_=in_, func=func, **kw),
                         [x.b for x in reads], [x.b for x in writes])

    def tt(self, out, in0, in1, op, reads, writes, eng=DVE):
        return self.P.op(eng, lambda e: e.tensor_tensor(out=out, in0=in0, in1=in1, op=op),
                         [x.b for x in reads], [x.b for x in writes])

    def ts(self, out, in0, s1, s2, op0, op1, reads, writes, eng=DVE):
        if op1 is None:
            return self.P.op(eng, lambda e: e.tensor_scalar(out=out, in0=in0, scalar1=s1, scalar2=None, op0=op0),
                             [x.b for x in reads], [x.b for x in writes])
        return self.P.op(eng, lambda e: e.tensor_scalar(out=out, in0=in0, scalar1=s1, scalar2=s2, op0=op0, op1=op1),
                         [x.b for x in reads], [x.b for x in writes])

    def stt(self, out, in0, scalar, in1, op0, op1, reads, writes, eng=DVE):
        return self.P.op(eng, lambda e: e.scalar_tensor_tensor(out=out, in0=in0, scalar=scalar, in1=in1, op0=op0, op1=op1),
                         [x.b for x in reads], [x.b for x in writes])

    def cp(self, out, in_, reads, writes, eng=DVE):
        return self.P.op(eng, lambda e: e.tensor_copy(out=out, in_=in_), [x.b for x in reads], [x.b for x in writes])

    def memset(self, out, val, writes, eng=POOL):
        return self.P.op(eng, lambda e: e.memset(out, val), [], [x.b for x in writes])

    def sigm(self, ap, t):
        self.act(ap, ap, AF.Ln, [t], [t], bias=1.0)
        self.act(ap, ap, AF.Exp, [t], [t], scale=-1.0)

    def rsqrt(self, ap, t):
        self.act(ap, ap, AF.Ln, [t], [t])
        self.act(ap, ap, AF.Exp, [t], [t], scale=-0.5)

    def build(self):
        nc = self.nc
        dbg = self.debug
        I = {}

        def inp(name, shape):
            I[name] = self.dram(name, shape, F32, kind="ExternalInput")
        inp("x", [SEQ, D]); inp("ctx", [CTX, D]); inp("cvec", [2, D])
        inp("w_ada", [2, D, 3 * D]); inp("b_ada", [2, 3 * D]); inp("w_in", [2, D, WEXT])
        inp("ssd_conv_w", [2, 5, 1536]); inp("ssd_conv_b", [2, 1536])
        for n in ("ssd_a_log_f", "ssd_a_log_b", "ssd_dt_bias_f", "ssd_dt_bias_b", "ssd_d"):
            inp(n, [2, 16])
        inp("ssd_norm_w", [2, 1024]); inp("mla_q_norm", [2, 384]); inp("mla_w_uq", [2, 384, 768])
        inp("w_uq_sw", [2, 384, 256]); inp("mla_kv_norm", [2, 256]); inp("mla_w_ukv", [2, 256, 1024])
        inp("ret_log_rate_f", [2, 4]); inp("ret_log_rate_b", [2, 4]); inp("w_out", [2, 2048, D])
        inp("ln_g", [2, D]); inp("ln_b", [2, D])
        inp("cst", [128, CST_W]); inp("mla_tab", [32, 2, NTOK]); inp("ret_tab", [128, 4, NTOK]); inp("ret_tab2", [NTOK, 1024])
        self.I = I
        okind = "ExternalOutput" if dbg else "Internal"
        self.out = self.dram("out", [SEQ, D], F32, kind="ExternalOutput")
        self.hT_d = self.dram("hT_d", [128, 8 * HC], BF16, kind=okind)
        self.ypart_d = self.dram("ypart_d", [NTOK, 1024], F32)
        self.rpart_d = self.dram("rpart_d", [NTOK, 512], F32)
        self.ypartb_d = self.dram("ypartb_d", [NTOK, 1024], F32)
        self.Fb_d = self.dram("Fb_d", [NCH, 128, 1536], BF16)
        self.Rb_d = self.dram("Rb_d", [NCH, 128, 1280], BF16)
        self.Ff_d = self.dram("Ff_d", [NCH, 128, 272], F32)
        self.rpartb_d = self.dram("rpartb_d", [NTOK, 512], F32)
        self.mixT_d = self.dram("mixT_d", [2048, NTOK], BF16, kind=okind)
        self.qT_d = self.dram("qT_d", [8, 96, NTOK], BF16)
        self.kfT_d = self.dram("kfT_d", [8, 96, NTOK], BF16)
        self.sgT_d = self.dram("sgT_d", [512, NTOK], F32)
        self.xres_d = self.dram("xres_d", [NTOK, D], F32, kind=okind)

        with ExitStack() as gst:
            self.gst = gst
            self.cst = self.sb(gst, [128, CST_W], F32, "cst")
            self.dma(self.cst[:], I["cst"][:], [I["cst"]], [self.cst])
            self.identb = self.sb(gst, [128, 128], BF16, "identb")
            self.cp(self.identb[:], self.cst[:, C_ID:C_ID + 128], [self.cst], [self.identb])
            self.ones = self.sb(gst, [128, 128], F32, "ones")
            self.memset(self.ones[:], 1.0, [self.ones])
            self.gB = self.sb(gst, [128, 2, 1024], F32, "gB")
            zt = self.sb(gst, [128, 8, 4], BF16, "zt")
            self.memset(zt[:], 0.0, [zt])
            hv = self.hT_d[:].rearrange("p (k c) -> p k c", k=8)
            self.hv = hv
            for (a, b) in ((0, 2), (258, 262), (4358, 4360)):
                self.dma(hv[:, :, a:b], zt[:, :, 0:b - a], [zt], [self.hT_d], slow=True)
            self.P.barrier()
            stages = []
            for l in range(DEPTH):
                stages += [("A", l), ("S", l), ("M", l), ("R", l), ("E", l)]
            for (s, l) in stages:
                if s in self.skip:
                    continue
                if s == "A":
                    self.stage_A(l)
                elif s == "S":
                    self.stage_S(l)
                elif s == "M":
                    self.stage_M(l)
                elif s == "R":
                    self.stage_R(l)
                else:
                    self.stage_E(l)
                self.P.barrier()
                if self.stop_after == (s, l):
                    break
            self.P.barrier()
            self.P.emit(gst)
        return nc

    def silu_psum(self, st, src_ap, src_t, out_ap, out_t, e_t, e_ap, r_ap):
        self.act(e_ap, src_ap, AF.Exp, [src_t], [e_t], scale=-1.0)
        self.sigm(e_ap, e_t)
        self.tt(out_ap, src_ap, r_ap, ALU.mult, [src_t, e_t], [out_t])

    def stage_A(self, l):
        I = self.I
        with ExitStack() as st:
            wada = [self.sb(st, [128, 8, 512], F32, "wada") for _ in range(2)]
            craw = self.sb(st, [128, 8, 2], F32, "craw")
            ce = self.sb(st, [128, 8, 2], F32, "ce")
            scT = self.sb(st, [128, 8, 2], F32, "scT")
            modT = self.sb(st, [128, 24, 2], F32, "modT")
            scale1 = self.sb(st, [128, 8, 2], F32, "scale1")
            brow = self.sb(st, [1, 3 * D], F32, "brow")
            pm = self.ps(st, [128, 512], F32, "pm")
            pg = [self.ps(st, [128, 512], F32, "pg") for _ in range(2)]
            for j in range(2):
                self.dma(craw[:, :, j], I["cvec"][j].rearrange("(k p) -> p k", p=128), [I["cvec"]], [craw], slow=True)
            self.dma(brow[:], I["b_ada"][l:l + 1, :], [I["b_ada"]], [brow])
            self.act(ce[:], craw[:], AF.Exp, [craw], [ce], scale=-1.0)
            self.sigm(ce[:], ce)
            self.tt(scT[:], craw[:], ce[:], ALU.mult, [craw, ce], [scT])
            wv = I["w_ada"][l].rearrange("(k p) c -> p k c", p=128)
            for cb in range(6):
                w = wada[cb % 2]
                self.dma(w[:], wv[:, :, cb * 512:(cb + 1) * 512], [I["w_ada"]], [w])
                if cb < 4:
                    for dj in range(4):
                        j = cb * 4 + dj
                        for kc in range(8):
                            self.mm(pm[:, 2 * dj:2 * dj + 2], w[:, kc, dj * 128:(dj + 1) * 128], scT[:, kc, :],
                                    kc == 0, False, [w, scT], [pm])
                        self.mm(pm[:, 2 * dj:2 * dj + 2], brow[0:1, j * 128:(j + 1) * 128], self.ones[0:1, 0:2],
                                False, True, [brow, self.ones], [pm])
                        self.cp(modT[:, j, :], pm[:, 2 * dj:2 * dj + 2], [pm], [modT])
                else:
                    for typ in range(2):
                        p = pg[typ]
                        for kc in range(8):
                            self.mm(p[:], scT[:, kc, typ:typ + 1].to_broadcast([128, 128]), w[:, kc, :],
                                    kc == 0, False, [w, scT], [p])
                        self.mm(p[:], self.ones[0:1, 0:128], brow[0:1, cb * 512:(cb + 1) * 512], False, True,
                                [brow, self.ones], [p])
                        self.cp(self.gB[:, typ, (cb - 4) * 512:(cb - 3) * 512], p[:], [p], [self.gB])
            self.ts(scale1[:], modT[:, 8:16, :], 1.0, None, ALU.add, None, [modT], [scale1])
            xt = [self.sb(st, [128, D], F32, "xt") for _ in range(2)]
            ht = [self.sb(st, [128, 8, 128], BF16, "ht") for _ in range(2)]
            pT = [self.ps(st, [128, 1024], F32, "pT") for _ in range(2)]
            for t in range(NCH):
                typ = 1 if t < 2 else 0
                if l == 0:
                    src_t = I["ctx"] if t < 2 else I["x"]
                    src = src_t[t * 128:(t + 1) * 128, :] if t < 2 else src_t[(t - 2) * 128:(t - 1) * 128, :]
                else:
                    src_t = self.xres_d
                    src = src_t[t * 128:(t + 1) * 128, :]
                x_ = xt[t % 2]; h_ = ht[t % 2]; p_ = pT[t % 2]
                self.dma(x_[:], src, [src_t], [x_])
                for kc in range(8):
                    self.tr(p_[:, kc * 128:(kc + 1) * 128], x_[:, kc * 128:(kc + 1) * 128], self.cst[:, C_ID:C_ID + 128],
                            [x_, self.cst], [p_])
                for kc in range(8):
                    self.act(h_[:, kc, :], p_[:, kc * 128:(kc + 1) * 128], AF.Identity, [p_, scale1, modT], [h_],
                             bias=modT[:, kc, typ:typ + 1], scale=scale1[:, kc, typ:typ + 1])
                c0 = colof(t * 128)
                self.dma(self.hv[:, :, c0:c0 + 128], h_[:], [h_], [self.hT_d])

    def load_w(self, dst_ap, dst_t, src_ap, src_t):
        self.dma(dst_ap, src_ap, [src_t], [dst_t], eng=POOL, slow=True)

    def bvec(self, st, name, l, n):
        t = self.sb(st, [128, n], F32, name)
        self.dma(t[:], self.I[name][l:l + 1, :].to_broadcast([128, n]), [self.I[name]], [t], slow=True)
        return t

    def interleave(self, factories, width):
        pending = list(factories)
        active = []
        for s in range(width):
            if pending:
                active.append((s, pending.pop(0)(s)))
        while active:
            nxt = []
            for (s, g) in active:
                try:
                    next(g)
                    nxt.append((s, g))
                except StopIteration:
                    if pending:
                        nxt.append((s, pending.pop(0)(s)))
            active = nxt

    def stage_S(self, l):
        I = self.I
        cst = self.cst
        import os as _os
        with ExitStack() as st:
            wv = I["w_in"][l].rearrange("(k p) c -> p k c", p=128)
            with ExitStack() as stf:
                wx = self.sb(stf, [128, 8, 1536], BF16, "wx")
                wdt = self.sb(stf, [128, 8, 16], BF16, "wdt")
                for kc in range(8):
                    self.load_w(wx[:, kc, :], wx, wv[:, kc, 1024:2560], I["w_in"])
                self.load_w(wdt[:], wdt, wv[:, :, 2560:2576], I["w_in"])
                convw = self.sb(stf, [128, 12, 5], F32, "convw")
                for k in range(5):
                    self.dma(convw[:, :, k], I["ssd_conv_w"][l, k].rearrange("(r p) -> p r", p=128), [I["ssd_conv_w"]], [convw], slow=True)
                dg = self.sb(stf, [128, 60, 128], BF16, "dg")
                for r in range(12):
                    for k in range(5):
                        self.ts(dg[:, r * 5 + k, :], self.identb[:], convw[:, r, k:k + 1], None, ALU.mult, None,
                                [self.identb, convw], [dg])
                cbrow = self.sb(stf, [1, 1536], F32, "cbrow")
                self.dma(cbrow[:], I["ssd_conv_b"][l:l + 1, :], [I["ssd_conv_b"]], [cbrow])
                sets = []
                for s_ in range(2):
                    B = {}
                    B["hc"] = self.sb(stf, [128, 8, 132], BF16, "hcf")
                    B["xbc"] = self.sb(stf, [128, 12, 132], BF16, "xbc")
                    B["esb"] = self.sb(stf, [128, 1536], F32, "esbf")
                    B["ubf"] = self.sb(stf, [128, 12, 128], BF16, "ubf")
                    B["Fb"] = self.sb(stf, [128, 1536], BF16, "Fbw")
                    B["Ff"] = self.sb(stf, [128, 272], F32, "Ffw")
                    B["Q"] = [self.ps(stf, [128, 512], F32, "QF") for _ in range(4)]
                    sets.append(B)
                gens = [(lambda cc: (lambda slot: self.ssd_front(cc, sets[slot], wx, wdt, dg, cbrow)))(c) for c in range(NCH)]
                self.interleave(gens, 2)
            self.P.barrier()
            Dsk = self.bvec(st, "ssd_d", l, 16)
            prm = {}
            for d_, sfx in ((0, "f"), (1, "b")):
                al = self.bvec(st, "ssd_a_log_" + sfx, l, 16)
                self.act(al[:], al[:], AF.Exp, [al], [al])
                self.ts(al[:], al[:], -1.0, None, ALU.mult, None, [al], [al])
                dtb = self.bvec(st, "ssd_dt_bias_" + sfx, l, 16)
                Ub = self.sb(st, [128, 128], BF16, "Ub16")
                Lb = self.sb(st, [128, 128], BF16, "Lb16")
                Uo = C_UF if d_ == 0 else C_UB
                Lo = C_LF if d_ == 0 else C_LB
                self.cp(Ub[:], cst[:, Uo:Uo + 128], [cst], [Ub])
                self.cp(Lb[:], cst[:, Lo:Lo + 128], [cst], [Lb])
                prm[d_] = (al, dtb, Ub, Lb)
            with ExitStack() as st2:
                gens = []
                self.yb = {}
                for d_ in (0, 1):
                    for g_ in (0, 1):
                        self.yb[(d_, g_)] = T((self.ypart_d if d_ == 0 else self.ypartb_d).t, "yb")
                        gens.append((lambda dd, gg: (lambda slot: self.ssd_sweep(l, dd, gg, st2, Dsk, prm[dd])))(d_, g_))
                self.interleave(gens, 4)
            self.P.barrier()
            with ExitStack() as st3:
                wz = self.sb(st3, [128, 8, 1024], BF16, "wz")
                for kc in range(8):
                    self.load_w(wz[:, kc, :], wz, wv[:, kc, 0:1024], I["w_in"])
                nwB = self.bvec(st3, "ssd_norm_w", l, 1024)
                sets = []
                for s in range(2):
                    B = {}
                    B["hc"] = self.sb(st3, [128, 8, 128], BF16, "hc3")
                    B["ypf"] = self.sb(st3, [128, 1024], F32, "ypf")
                    B["ypb"] = self.sb(st3, [128, 1024], F32, "ypb")
                    B["e"] = self.sb(st3, [128, 1024], F32, "e3")
                    B["t1"] = self.sb(st3, [128, 1024], F32, "t13")
                    B["bst"] = self.sb(st3, [128, 2, 6], F32, "bst3")
                    B["ssq"] = self.sb(st3, [128, 2], F32, "ssq3")
                    B["ob"] = self.sb(st3, [128, 1024], BF16, "ob3")
                    B["oT"] = self.sb(st3, [128, 8, 128], BF16, "oT3")
                    B["PZ"] = [self.ps(st3, [128, 512], F32, "PZ3") for _ in range(2)]
                    B["PT"] = self.ps(st3, [128, 1024], BF16, "PT3")
                    sets.append(B)
                gens = [(lambda cc: (lambda slot: self.ssd_final(cc, sets[slot], wz, nwB)))(c) for c in range(NCH)]
                self.interleave(gens, 2)

    def ssd_final(self, c, B, wz, nwB):
        h_ = B["hc"]; ypf = B["ypf"]; ypb = B["ypb"]; e = B["e"]; t1 = B["t1"]; bst = B["bst"]; ssq = B["ssq"]
        ob = B["ob"]; oT_ = B["oT"]; PZ = B["PZ"]; PT = B["PT"]
        mixv = self.mixT_d[:].rearrange("(r p) t -> p r t", p=128)
        c0 = colof(c * 128)
        tok = slice(c * 128, (c + 1) * 128)
        self.dma(h_[:], self.hv[:, :, c0:c0 + 128], [self.hT_d], [h_], slow=True)
        self.dma(ypf[:], self.ypart_d[tok, :], [self.yb[(0, 0)], self.yb[(0, 1)]], [ypf])
        self.dma(ypb[:], self.ypartb_d[tok, :], [self.yb[(1, 0)], self.yb[(1, 1)]], [ypb])
        yield
        for n in range(2):
            for kc in range(8):
                self.mm(PZ[n][:], h_[:, kc, :], wz[:, kc, n * 512:(n + 1) * 512], kc == 0, kc == 7, [h_, wz], [PZ[n]])
        self.tt(ypf[:], ypf[:], ypb[:], ALU.add, [ypf, ypb], [ypf])
        yield
        for n in range(2):
            self.act(e[:, n * 512:(n + 1) * 512], PZ[n][:], AF.Exp, [PZ[n]], [e], scale=-1.0)
        yield
        self.sigm(e[:], e)
        yield
        for n in range(2):
            self.tt(t1[:, n * 512:(n + 1) * 512], PZ[n][:], e[:, n * 512:(n + 1) * 512], ALU.mult, [PZ[n], e], [t1])
        yield
        self.tt(t1[:], t1[:], ypf[:], ALU.mult, [t1, ypf], [t1])
        yield
        for s_ in range(2):
            self.P.op(DVE, (lambda ss: (lambda en: en.bn_stats(out=bst[:, ss, :], in_=t1[:, ss * 512:(ss + 1) * 512])))(s_),
                      [t1.b], [bst.b])
        self.P.op(DVE, lambda en: en.bn_aggr(out=ssq[:], in_=bst[:]), [bst.b], [ssq.b])
        self.stt(ssq[:, 1:2], ssq[:, 0:1], ssq[:, 0:1], ssq[:, 1:2], ALU.mult, ALU.add, [ssq], [ssq])
        self.ts(ssq[:, 1:2], ssq[:, 1:2], RMS_EPS, None, ALU.add, None, [ssq], [ssq])
        yield
        self.rsqrt(ssq[:, 1:2], ssq)
        yield
        self.stt(ob[:], t1[:], ssq[:, 1:2], nwB[:], ALU.mult, ALU.mult, [t1, ssq, nwB], [ob])
        yield
        for r in range(8):
            self.tr(PT[:, r * 128:(r + 1) * 128], ob[:, r * 128:(r + 1) * 128], self.identb[:], [ob, self.identb], [PT])
        yield
        self.act(oT_[:], PT[:].rearrange("p (a b) -> p a b", a=8), AF.Copy, [PT], [oT_])
        yield
        self.dma(mixv[:, 0:8, tok], oT_[:], [oT_], [self.mixT_d], slow=True)
        yield

    def ssd_front(self, c, B, wx, wdt, dg, cbrow):
        h_ = B["hc"]; xbc = B["xbc"]; esb = B["esb"]; ubf = B["ubf"]; Fb = B["Fb"]; Ff = B["Ff"]; Q = B["Q"]
        Qb0 = Q[0][:].bitcast(BF16)
        Qb1 = Q[1][:].bitcast(BF16)
        c0 = colof(c * 128)
        self.dma(h_[:], self.hv[:, :, c0 - 2:c0 + 130], [self.hT_d], [h_], slow=True)
        yield
        for r in range(12):
            q_ = Q[r // 3]
            o_ = q_[:, (r % 3) * 132:(r % 3) * 132 + 132]
            for kc in range(8):
                self.mm(o_, wx[:, kc, r * 128:(r + 1) * 128], h_[:, kc, :], kc == 0, kc == 7, [wx, h_], [q_])
            if r % 3 == 2:
                yield
        for q in range(4):
            self.act(xbc[:, 3 * q:3 * q + 3, :], Q[q][:, 0:396].rearrange("p (a b) -> p a b", a=3), AF.Copy, [Q[q]], [xbc])
        yield
        for kc in range(8):
            self.mm(Q[3][:, 0:16], h_[:, kc, 2:130], wdt[:, kc, :], kc == 0, kc == 7, [h_, wdt], [Q[3]])
        yield
        for r in range(12):
            q_ = Q[r // 4]
            o_ = q_[:, (r % 4) * 128:(r % 4 + 1) * 128]
            for k in range(5):
                self.mm(o_, dg[:, r * 5 + k, :], xbc[:, r, k:k + 128], k == 0, False, [dg, xbc], [q_])
            self.mm(o_, cbrow[0:1, r * 128:(r + 1) * 128], self.ones[0:1, 0:128], False, True, [cbrow, self.ones], [q_])
            if r % 4 == 3:
                yield
        self.cp(Ff[:, 256:272], Q[3][:, 0:16], [Q[3]], [Ff])
        for q in range(3):
            self.act(esb[:, q * 512:(q + 1) * 512], Q[q][:], AF.Exp, [Q[q]], [esb], scale=-1.0)
        yield
        self.sigm(esb[:], esb)
        yield
        for q in range(3):
            self.tt(ubf[:, 4 * q:4 * q + 4, :], Q[q][:].rearrange("p (a b) -> p a b", a=4),
                    esb[:, q * 512:(q + 1) * 512].rearrange("p (a b) -> p a b", a=4), ALU.mult, [Q[q], esb], [ubf])
        yield
        for g in range(2):
            self.mm(Q[3][:, 256 + g * 128:256 + (g + 1) * 128], ubf[:, 8 + g, :], ubf[:, 10 + g, :], True, True, [ubf], [Q[3]])
        for r in range(8):
            self.tr(Qb0[:, r * 128:(r + 1) * 128], ubf[:, r, :], self.identb[:], [ubf, self.identb], [Q[0]])
        for r in range(2):
            self.tr(Qb1[:, r * 128:(r + 1) * 128], ubf[:, 8 + r, :], self.identb[:], [ubf, self.identb], [Q[1]])
        self.cp(Fb[:, 1280:1536], ubf[:, 10:12, :].rearrange("p a b -> p (a b)"), [ubf], [Fb], eng=POOL)
        yield
        self.cp(Ff[:, 0:256], Q[3][:, 256:512], [Q[3]], [Ff])
        self.act(Fb[:, 0:1024], Qb0[:, 0:1024], AF.Copy, [Q[0]], [Fb])
        self.cp(Fb[:, 1024:1280], Qb1[:, 0:256], [Q[1]], [Fb])
        yield
        self.dma(self.Fb_d[c], Fb[:], [Fb], [self.Fb_d])
        self.dma(self.Ff_d[c], Ff[:], [Ff], [self.Ff_d])
        yield

    def ssd_sweep(self, l, d_, g, st, Dsk, prm):
        cst = self.cst
        al, dtb, Ub, Lb = prm
        HS = slice(g * 8, (g + 1) * 8)
        H = self.sb(st, [128, 512], F32, "H")
        Hbf = self.sb(st, [128, 512], BF16, "Hbf")
        Fb = [self.sb(st, [128, 1536], BF16, "Fbr") for _ in range(2)]
        Ff = [self.sb(st, [128, 272], F32, "Ffr") for _ in range(2)]
        esb = self.sb(st, [128, 1024], F32, "esb")
        dtx = self.sb(st, [128, 8], F32, "dtx")
        dt = self.sb(st, [128, 8], F32, "dt")
        la = self.sb(st, [128, 8], F32, "la")
        lah = self.sb(st, [128, 8], BF16, "lah")
        lah32 = self.sb(st, [128, 8], F32, "lah32")
        lal32 = self.sb(st, [128, 8], F32, "lal32")
        lalo = self.sb(st, [128, 8], BF16, "lalo")
        E3 = self.sb(st, [128, 24], F32, "E3")
        scm = self.sb(st, [128, 128], F32, "scm")
        LaUh = self.sb(st, [128, 8, 128], BF16, "LaUh")
        LaUl = self.sb(st, [128, 8, 128], BF16, "LaUl")
        M = self.sb(st, [128, 8, 128], BF16, "M")
        v = self.sb(st, [128, 512], BF16, "v")
        vte = self.sb(st, [128, 512], BF16, "vte")
        t1 = self.sb(st, [128, 512], F32, "t1")
        t2 = self.sb(st, [128, 512], F32, "t2")
        yp = [self.sb(st, [128, 512], F32, "yp") for _ in range(2)]
        Q = [self.ps(st, [128, 512], F32, "Q") for _ in range(2)]
        ydst = self.ypart_d if d_ == 0 else self.ypartb_d
        order = list(range(NCH)) if d_ == 0 else [1, 0] + list(range(NCH - 1, 1, -1))
        Uo = C_UF if d_ == 0 else C_UB
        Lo = C_LF if d_ == 0 else C_LB
        Mo = C_MF if d_ == 0 else C_MB
        self.memset(H[:], 0.0, [H])
        self.memset(Hbf[:], 0.0, [Hbf])

        def loads(ci, slot):
            c = order[ci]
            self.dma(Fb[slot][:], self.Fb_d[c], [self.Fb_d], [Fb[slot]])
            self.dma(Ff[slot][:], self.Ff_d[c], [self.Ff_d], [Ff[slot]])
        loads(0, 0)
        it = 0
        for ci, c in enumerate(order):
            fb = Fb[it % 2]; ff = Ff[it % 2]; ypt = yp[it % 2]
            it += 1
            tok = slice(c * 128, (c + 1) * 128)
            if ci + 1 < len(order):
                loads(ci + 1, it % 2)
            xs_g = fb[:, g * 512:(g + 1) * 512]
            self.tt(dtx[:], ff[:, 256 + g * 8:264 + g * 8], dtb[:, HS], ALU.add, [ff, dtb], [dtx])
            self.tt(scm[:], ff[:, g * 128:(g + 1) * 128], cst[:, Mo:Mo + 128], ALU.mult, [ff, cst], [scm])
            yield
            self.act(dtx[:], dtx[:], AF.Exp, [dtx], [dtx])
            self.act(dt[:], dtx[:], AF.Ln, [dtx], [dt], bias=1.0)
            yield
            self.tt(la[:], dt[:], al[:, HS], ALU.mult, [dt, al], [la])
            self.cp(lah[:], la[:], [la], [lah])
            self.cp(lah32[:], lah[:], [lah], [lah32])
            self.tt(lal32[:], la[:], lah32[:], ALU.subtract, [la, lah32], [lal32])
            self.cp(lalo[:], lal32[:], [lal32], [lalo])
            yield
            self.mm(Q[1][:, 0:8], cst[:, Uo:Uo + 128], la[:], True, True, [cst, la], [Q[1]])
            self.mm(Q[1][:, 8:16], cst[:, Lo:Lo + 128], la[:], True, True, [cst, la], [Q[1]])
            self.mm(Q[1][:, 16:24], self.ones[:], la[:], True, True, [self.ones, la], [Q[1]])
            self.tt(LaUh[:], lah[:].unsqueeze(2).to_broadcast([128, 8, 128]),
                    Ub[:].unsqueeze(1).to_broadcast([128, 8, 128]), ALU.mult, [lah, Ub], [LaUh], eng=POOL)
            self.tt(LaUl[:], lalo[:].unsqueeze(2).to_broadcast([128, 8, 128]),
                    Ub[:].unsqueeze(1).to_broadcast([128, 8, 128]), ALU.mult, [lalo, Ub], [LaUl])
            self.tt(v[:].rearrange("p (h e) -> p h e", h=8), xs_g.rearrange("p (h e) -> p h e", h=8),
                    dt[:].unsqueeze(2).to_broadcast([128, 8, 64]), ALU.mult, [fb, dt], [v], eng=POOL)
            yield
            self.act(E3[:], Q[1][:, 0:24], AF.Exp, [Q[1]], [E3])
            yield
            for q in range(2):
                self.mm(Q[q][:], Lb[:], LaUh[:, 4 * q:4 * q + 4, :].rearrange("p a b -> p (a b)"), True, False, [Lb, LaUh], [Q[q]])
                self.mm(Q[q][:], Lb[:], LaUl[:, 4 * q:4 * q + 4, :].rearrange("p a b -> p (a b)"), False, True, [Lb, LaUl], [Q[q]])
            yield
            self.tt(vte[:].rearrange("p (h e) -> p h e", h=8), v[:].rearrange("p (h e) -> p h e", h=8),
                    E3[:, 8:16].unsqueeze(2).to_broadcast([128, 8, 64]), ALU.mult, [v, E3], [vte], eng=POOL)
            for q in range(2):
                self.act(esb[:, q * 512:(q + 1) * 512], Q[q][:], AF.Exp, [Q[q]], [esb])
            yield
            self.tt(M[:], esb[:].rearrange("p (a b) -> p a b", a=8),
                    scm[:].unsqueeze(1).to_broadcast([128, 8, 128]), ALU.mult, [esb, scm], [M])
            yield
            self.mm(Q[0][:], fb[:, 1280 + g * 128:1280 + (g + 1) * 128], Hbf[:], True, True, [fb, Hbf], [Q[0]])
            for hh in range(8):
                self.mm(Q[1][:, hh * 64:(hh + 1) * 64], M[:, hh, :], v[:, hh * 64:(hh + 1) * 64], True, True, [M, v], [Q[1]])
            yield
            self.tt(t1[:].rearrange("p (h e) -> p h e", h=8), Q[0][:].rearrange("p (h e) -> p h e", h=8),
                    E3[:, 0:8].unsqueeze(2).to_broadcast([128, 8, 64]), ALU.mult, [Q[0], E3], [t1])
            yield
            self.tt(t2[:], Q[1][:], t1[:], ALU.add, [Q[1], t1], [t2])
            self.mm(Q[0][:], fb[:, 1024 + g * 128:1024 + (g + 1) * 128], vte[:], True, True, [fb, vte], [Q[0]])
            yield
            if d_ == 0:
                self.tt(t1[:].rearrange("p (h e) -> p h e", h=8), xs_g.rearrange("p (h e) -> p h e", h=8),
                        Dsk[:, HS].unsqueeze(2).to_broadcast([128, 8, 64]), ALU.mult, [fb, Dsk], [t1], eng=POOL)
                self.tt(ypt[:], t1[:], t2[:], ALU.add, [t1, t2], [ypt], eng=POOL)
            else:
                self.cp(ypt[:], t2[:], [t2], [ypt], eng=POOL)
            self.dma(ydst[tok, g * 512:(g + 1) * 512], ypt[:], [ypt], [self.yb[(d_, g)]])
            self.tt(H[:].rearrange("p (h e) -> p h e", h=8), H[:].rearrange("p (h e) -> p h e", h=8),
                    E3[:, 16:24].unsqueeze(2).to_broadcast([128, 8, 64]), ALU.mult, [H, E3], [H])
            yield
            self.tt(H[:], H[:], Q[0][:], ALU.add, [H, Q[0]], [H])
            yield
            self.act(Hbf[:], H[:], AF.Copy, [H], [Hbf])
            yield

    def stage_R(self, l):
        I = self.I
        cst = self.cst
        import os as _os
        with ExitStack() as st:
            wv = I["w_in"][l].rearrange("(k p) c -> p k c", p=128)
            wqk = self.sb(st, [128, 8, 1024], BF16, "wqk")
            wvv = self.sb(st, [128, 8, 512], BF16, "wvr")
            for kc in range(8):
                self.load_w(wqk[:, kc, 0:512], wqk, wv[:, kc, 3760:4272], I["w_in"])
                self.load_w(wqk[:, kc, 512:1024], wqk, wv[:, kc, 5328:5840], I["w_in"])
                self.load_w(wvv[:, kc, :], wvv, wv[:, kc, 4272:4784], I["w_in"])
            prm = {}
            for d_, sfx in ((0, "f"), (1, "b")):
                nm = "ret_log_rate_" + sfx
                lgB = self.bvec(st, nm, l, 4)
                self.act(lgB[:], lgB[:], AF.Exp, [lgB], [lgB])
                self.ts(lgB[:], lgB[:], -1.0, None, ALU.mult, None, [lgB], [lgB])
                lgs = self.sb(st, [128, 2], F32, "lgs")
                src = I[nm][l:l + 1, :].rearrange("o (p two) -> o p two", two=2)
                self.dma(lgs[0:64, :], src[:, :, 0].to_broadcast([64, 2]), [I[nm]], [lgs], slow=True)
                self.dma(lgs[64:128, :], src[:, :, 1].to_broadcast([64, 2]), [I[nm]], [lgs], slow=True)
                self.act(lgs[:], lgs[:], AF.Exp, [lgs], [lgs])
                self.ts(lgs[:], lgs[:], -1.0, None, ALU.mult, None, [lgs], [lgs])
                RIo = C_RIF if d_ == 0 else C_RIB
                Mo = C_MF if d_ == 0 else C_MB
                Go = C_GF if d_ == 0 else C_GB
                To = C_TEF if d_ == 0 else C_TEB
                DmT = self.sb(st, [128, 4, 128], F32, "DmT")
                for h in range(4):
                    self.act(DmT[:, h, :], cst[:, RIo:RIo + 128], AF.Exp, [cst, lgB], [DmT], scale=lgB[:, h:h + 1])
                self.tt(DmT[:], DmT[:], cst[:, Mo:Mo + 128].unsqueeze(1).to_broadcast([128, 4, 128]), ALU.mult, [DmT, cst], [DmT])
                Gam = self.sb(st, [128, 2, 128], F32, "Gam")
                for p in range(2):
                    self.act(Gam[:, p, :], cst[:, Go:Go + 128], AF.Exp, [cst, lgs], [Gam], scale=lgs[:, p:p + 1])
                te = self.sb(st, [128, 4], F32, "te")
                self.act(te[:], lgB[:], AF.Exp, [lgB, cst], [te], scale=cst[:, To:To + 1])
                g128 = self.sb(st, [128, 2], F32, "g128")
                self.act(g128[:], lgs[:], AF.Exp, [lgs], [g128], scale=128.0)
                prm[d_] = (DmT, Gam, te, g128)
            with ExitStack() as stf:
                sets = []
                for s_ in range(2):
                    B = {}
                    B["hc"] = self.sb(stf, [128, 8, 128], BF16, "hcrf")
                    B["tab"] = self.sb(stf, [128, 1024], F32, "tabrf")
                    B["r1"] = self.sb(stf, [128, 512], F32, "r1")
                    B["r2"] = self.sb(stf, [128, 512], F32, "r2")
                    B["qkt"] = self.sb(stf, [128, 512], BF16, "qkt")
                    B["Rb"] = self.sb(stf, [128, 1280], BF16, "Rbw")
                    B["Q"] = [self.ps(stf, [128, 512], F32, "QRF") for _ in range(4)]
                    sets.append(B)
                gens = [(lambda cc: (lambda slot: self.ret_front(cc, sets[slot], wqk, wvv)))(c) for c in range(NCH)]
                self.interleave(gens, 2)
            self.P.barrier()
            with ExitStack() as st2:
                gens = []
                for d_ in (0, 1):
                    gens.append((lambda dd: (lambda slot: self.ret_sweep(l, dd, st2, prm[dd])))(d_))
                self.interleave(gens, 2)
            self.P.barrier()
            with ExitStack() as st3:
                wg = self.sb(st3, [128, 8, 512], BF16, "wg")
                for kc in range(8):
                    self.load_w(wg[:, kc, :], wg, wv[:, kc, 4784:5296], I["w_in"])
                sets = []
                for s in range(2):
                    B = {}
                    B["hc"] = self.sb(st3, [128, 8, 128], BF16, "hcr3")
                    B["rpf"] = self.sb(st3, [128, 512], F32, "rpf")
                    B["rpb"] = self.sb(st3, [128, 512], F32, "rpb")
                    B["ge"] = self.sb(st3, [128, 512], F32, "ge3")
                    B["sg"] = self.sb(st3, [128, 512], F32, "sg3")
                    B["stats"] = self.sb(st3, [128, 4, 6], F32, "stats3")
                    B["mv"] = self.sb(st3, [128, 4, 2], F32, "mv3")
                    B["yn"] = self.sb(st3, [128, 512], F32, "yn3")
                    B["ob"] = self.sb(st3, [128, 512], BF16, "obr3")
                    B["oT"] = self.sb(st3, [128, 4, 128], BF16, "oTr3")
                    B["PG"] = self.ps(st3, [128, 512], F32, "PGr3")
                    B["PT"] = self.ps(st3, [128, 1024], BF16, "PTr3")
                    sets.append(B)
                gens = [(lambda cc: (lambda slot: self.ret_final(cc, sets[slot], wg)))(c) for c in range(NCH)]
                self.interleave(gens, 2)

    def ret_final(self, c, B, wg):
        h_ = B["hc"]; rpf = B["rpf"]; rpb = B["rpb"]; ge = B["ge"]; sg = B["sg"]; stats = B["stats"]; mv = B["mv"]
        yn = B["yn"]; ob = B["ob"]; oT_ = B["oT"]; PG = B["PG"]; PT = B["PT"]
        mixv = self.mixT_d[:].rearrange("(r p) t -> p r t", p=128)
        c0 = colof(c * 128)
        tok = slice(c * 128, (c + 1) * 128)
        self.dma(h_[:], self.hv[:, :, c0:c0 + 128], [self.hT_d], [h_], slow=True)
        self.dma(rpf[:], self.rpart_d[tok, :], [self.rpart_d], [rpf])
        self.dma(rpb[:], self.rpartb_d[tok, :], [self.rpartb_d], [rpb])
        yield
        for kc in range(8):
            self.mm(PG[:], h_[:, kc, :], wg[:, kc, :], kc == 0, kc == 7, [h_, wg], [PG])
        self.tt(rpf[:], rpf[:], rpb[:], ALU.add, [rpf, rpb], [rpf])
        yield
        self.act(ge[:], PG[:], AF.Exp, [PG], [ge], scale=-1.0)
        for h in range(4):
            self.P.op(DVE, (lambda hh: (lambda e: e.bn_stats(out=stats[:, hh, :], in_=rpf[:, hh * 128:(hh + 1) * 128])))(h),
                      [rpf.b], [stats.b])
            self.P.op(DVE, (lambda hh: (lambda e: e.bn_aggr(out=mv[:, hh, :], in_=stats[:, hh, :])))(h),
                      [stats.b], [mv.b])
        self.ts(mv[:, :, 1], mv[:, :, 1], LN_EPS, None, ALU.add, None, [mv], [mv])
        yield
        self.rsqrt(mv[:, :, 1], mv)
        self.sigm(ge[:], ge)
        yield
        self.tt(sg[:], PG[:], ge[:], ALU.mult, [PG, ge], [sg])
        for h in range(4):
            self.ts(yn[:, h * 128:(h + 1) * 128], rpf[:, h * 128:(h + 1) * 128], mv[:, h, 0:1], mv[:, h, 1:2],
                    ALU.subtract, ALU.mult, [rpf, mv], [yn])
        yield
        self.tt(ob[:], yn[:], sg[:], ALU.mult, [yn, sg], [ob])
        yield
        for h in range(4):
            self.tr(PT[:, h * 128:(h + 1) * 128], ob[:, h * 128:(h + 1) * 128], self.identb[:], [ob, self.identb], [PT])
        yield
        self.act(oT_[:], PT[:, 0:512].rearrange("p (a b) -> p a b", a=4), AF.Copy, [PT], [oT_])
        yield
        self.dma(mixv[:, 12:16, tok], oT_[:], [oT_], [self.mixT_d], slow=True)
        yield

    def ret_front(self, c, B, wqk, wvv):
        I = self.I
        h_ = B["hc"]; tab = B["tab"]; r1 = B["r1"]; r2 = B["r2"]; qkt = B["qkt"]; Rb = B["Rb"]; Q = B["Q"]
        Qb3 = Q[3][:].bitcast(BF16)
        c0 = colof(c * 128)
        self.dma(h_[:], self.hv[:, :, c0:c0 + 128], [self.hT_d], [h_], slow=True)
        self.dma(tab[:], I["ret_tab2"][c * 128:(c + 1) * 128, :], [I["ret_tab2"]], [tab])
        yield
        for j, (q_, w_, lo) in enumerate(((Q[0], wqk, 0), (Q[1], wqk, 512), (Q[2], wvv, 0))):
            for kc in range(8):
                self.mm(q_[:], h_[:, kc, :], w_[:, kc, lo:lo + 512], kc == 0, kc == 7, [h_, w_], [q_])
            yield
        self.tt(r1[:], Q[0][:], tab[:, 0:512], ALU.mult, [Q[0], tab], [r1])
        yield
        self.tt(r2[:], Q[1][:], tab[:, 512:1024], ALU.mult, [Q[1], tab], [r2])
        self.act(Rb[:, 768:1280], Q[2][:], AF.Copy, [Q[2]], [Rb])
        yield
        self.tt(qkt[:], r1[:], r2[:], ALU.add, [r1, r2], [qkt], eng=POOL)
        yield
        for j in range(4):
            self.tr(Qb3[:, j * 128:(j + 1) * 128], qkt[:, j * 128:(j + 1) * 128], self.identb[:], [qkt, self.identb], [Q[3]])
        self.cp(Rb[:, 512:768], qkt[:, 256:512], [qkt], [Rb], eng=POOL)
        yield
        self.act(Rb[:, 0:512], Qb3[:, 0:512], AF.Copy, [Q[3]], [Rb])
        yield
        self.dma(self.Rb_d[c], Rb[:], [Rb], [self.Rb_d])
        yield

    def ret_sweep(self, l, d_, st, prm):
        DmT, Gam, te, g128 = prm
        S = self.sb(st, [128, 2, 128], F32, "S")
        Sbf = self.sb(st, [128, 2, 128], BF16, "Sbf")
        Rb = [self.sb(st, [128, 1280], BF16, "Rbr") for _ in range(2)]
        qz = [self.sb(st, [128, 2, 128], BF16, "qz") for _ in range(2)]
        qdz = [self.sb(st, [128, 2, 128], BF16, "qdz") for _ in range(2)]
        for par in range(2):
            self.memset(qz[par][:], 0.0, [qz[par]])
            self.memset(qdz[par][:], 0.0, [qdz[par]])
        vte = self.sb(st, [128, 512], BF16, "vte")
        Mr = self.sb(st, [128, 4, 128], BF16, "Mr")
        yp = [self.sb(st, [128, 512], F32, "ypr") for _ in range(2)]
        Q = [self.ps(st, [128, 512], F32, "QR") for _ in range(3)]
        ydst = self.rpart_d if d_ == 0 else self.rpartb_d
        order = list(range(NCH)) if d_ == 0 else [1, 0] + list(range(NCH - 1, 1, -1))
        self.memset(S[:], 0.0, [S])
        self.memset(Sbf[:], 0.0, [Sbf])
        self.dma(Rb[0][:], self.Rb_d[order[0]], [self.Rb_d], [Rb[0]])
        it = 0
        for ci, c in enumerate(order):
            rb = Rb[it % 2]; ypt = yp[it % 2]
            it += 1
            tok = slice(c * 128, (c + 1) * 128)
            if ci + 1 < len(order):
                self.dma(Rb[it % 2][:], self.Rb_d[order[ci + 1]], [self.Rb_d], [Rb[it % 2]])
            qk = rb[:, 0:512].rearrange("p (a b) -> p a b", a=4)
            for par in range(2):
                rr = 64 * par
                self.cp(qz[par][rr:rr + 64, :, :], qk[rr:rr + 64, 0:2, :], [rb], [qz[par]], eng=POOL)
                self.tt(qdz[par][rr:rr + 64, :, :], qk[rr:rr + 64, 0:2, :], Gam[rr:rr + 64, :, :], ALU.mult,
                        [rb, Gam], [qdz[par]], eng=POOL)
            self.tt(vte[:].rearrange("p (h e) -> p h e", h=4), rb[:, 768:1280].rearrange("p (h e) -> p h e", h=4),
                    te[:].unsqueeze(2).to_broadcast([128, 4, 128]), ALU.mult, [rb, te], [vte], eng=POOL)
            yield
            for h in range(4):
                p = h // 2
                self.mm(Q[0][:, h * 128:(h + 1) * 128], qk[:, 2 + p, :], qz[h % 2][:, p, :], True, True, [rb, qz[h % 2]], [Q[0]])
            yield
            self.tt(Mr[:], Q[0][:].rearrange("p (a b) -> p a b", a=4), DmT[:], ALU.mult, [Q[0], DmT], [Mr])
            yield
            for h in range(4):
                p = h // 2
                self.mm(Q[1][:, h * 128:(h + 1) * 128], Mr[:, h, :], rb[:, 768 + h * 128:768 + (h + 1) * 128], True, False, [Mr, rb], [Q[1]])
                self.mm(Q[1][:, h * 128:(h + 1) * 128], qdz[h % 2][:, p, :], Sbf[:, p, :], False, True, [qdz[h % 2], Sbf], [Q[1]])
            for h in range(4):
                p = h // 2
                self.mm(Q[2][:, h * 128:(h + 1) * 128], rb[:, 512 + p * 128:512 + (p + 1) * 128], vte[:, h * 128:(h + 1) * 128],
                        True, True, [rb, vte], [Q[2]])
            yield
            self.cp(ypt[:], Q[1][:], [Q[1]], [ypt])
            self.dma(ydst[tok, :], ypt[:], [ypt], [ydst])
            for h in range(4):
                p, r0 = h // 2, (h % 2) * 64
                self.stt(S[r0:r0 + 64, p, :], S[r0:r0 + 64, p, :], g128[r0:r0 + 64, p:p + 1], Q[2][r0:r0 + 64, h * 128:(h + 1) * 128],
                         ALU.mult, ALU.add, [S, g128, Q[2]], [S])
            yield
            self.act(Sbf[:], S[:], AF.Copy, [S], [Sbf])
            yield

    def stage_M(self, l):
        I = self.I
        cst = self.cst
        blocks = [(0, 256)] + [(256 + 512 * i, 512) for i in range(8)]
        with ExitStack() as st1:
            v_all = self.sb(st1, [128, NCH, 512], BF16, "v_all")
            with ExitStack() as st:
                wv = I["w_in"][l].rearrange("(k p) c -> p k c", p=128)
                wm = self.sb(st, [128, 8, 704], BF16, "wm")
                wgate = self.sb(st, [128, 8, 512], BF16, "wgate")
                self.load_w(wm[:, :, 0:672], wm, wv[:, :, 2576:3248], I["w_in"])
                self.load_w(wm[:, :, 672:704], wm, wv[:, :, 5296:5328], I["w_in"])
                self.load_w(wgate[:], wgate, wv[:, :, 3248:3760], I["w_in"])
                wuq = self.sb(st, [128, 3, 8, 96], BF16, "wuq")
                wuqs = self.sb(st, [128, 3, 8, 96], BF16, "wuqs")
                wkp = self.sb(st, [128, 2, 8, 96], BF16, "wkp")
                wvv = self.sb(st, [128, 2, 8, 64], BF16, "wvv")
                self.memset(wuqs[:], 0.0, [wuqs])
                self.memset(wkp[:], 0.0, [wkp])
                uqv = I["mla_w_uq"][l].rearrange("(k p) (h e) -> p k h e", p=128, h=8)
                uqs = I["w_uq_sw"][l].rearrange("(k p) (h e) -> p k h e", p=128, h=8)
                ukv = I["mla_w_ukv"][l].rearrange("(k p) (h e) -> p k h e", p=128, h=8)
                for kc in range(3):
                    self.load_w(wuq[:, kc, :, :], wuq, uqv[:, kc, :, :], I["mla_w_uq"])
                    self.load_w(wuqs[:, kc, :, 64:96], wuqs, uqs[:, kc, :, :], I["w_uq_sw"])
                for kc in range(2):
                    self.load_w(wkp[:, kc, :, 0:64], wkp, ukv[:, kc, :, 0:64], I["mla_w_ukv"])
                    self.load_w(wvv[:, kc, :, :], wvv, ukv[:, kc, :, 64:128], I["mla_w_ukv"])
                esel = self.sb(st, [32, 96], BF16, "esel")
                self.memset(esel[:], 0.0, [esel])
                self.cp(esel[:, 64:96], self.identb[0:32, 0:32], [self.identb, esel], [esel])
                qn = self.sb(st, [128, 3], F32, "qn")
                kvn = self.sb(st, [128, 2], F32, "kvn")
                self.dma(qn[:], I["mla_q_norm"][l].rearrange("(k p) -> p k", p=128), [I["mla_q_norm"]], [qn], slow=True)
                self.dma(kvn[:], I["mla_kv_norm"][l].rearrange("(k p) -> p k", p=128), [I["mla_kv_norm"]], [kvn], slow=True)
                hb = [self.sb(st, [128, 8, 512], BF16, "hb") for _ in range(2)]
                tq = self.sb(st, [96, 2, 512], F32, "tq")
                self.memset(tq[0:64, 0, :], 1.0, [tq])
                self.memset(tq[0:64, 1, :], 0.0, [tq])
                tk = self.sb(st, [32, 2, 512], F32, "tk")
                cqs = self.sb(st, [128, 3, 512], F32, "cqs")
                sqs2 = [self.sb(st, [128, 512], F32, "sqs") for _ in range(2)]
                rstd = self.sb(st, [128, 512], F32, "rstd")
                cqn = self.sb(st, [128, 3, 512], BF16, "cqn")
                ckvn = self.sb(st, [128, 2, 512], BF16, "ckvn")
                kr1 = self.sb(st, [32, 512], F32, "kr1")
                kr2 = self.sb(st, [32, 512], F32, "kr2")
                krr = self.sb(st, [32, 512], BF16, "krr")
                q1s = [self.sb(st, [96, 512], F32, "q1") for _ in range(2)]
                q2s = [self.sb(st, [96, 512], F32, "q2") for _ in range(2)]
                qf = [self.sb(st, [96, 512], BF16, "qf") for _ in range(2)]
                kf = [self.sb(st, [96, 512], BF16, "kf") for _ in range(2)]
                ges = [self.sb(st, [128, 512], F32, "ge") for _ in range(2)]
                sgo = [self.sb(st, [128, 512], F32, "sgo") for _ in range(2)]
                P0 = [self.ps(st, [128, 512], F32, "P0") for _ in range(2)]
                PSS = self.ps(st, [128, 512], F32, "PSS")
                PQ1 = self.ps(st, [128, 512], F32, "PQ1")
                PQ2 = self.ps(st, [128, 512], F32, "PQ2")
                PK = self.ps(st, [128, 512], F32, "PK")
                PVv = self.ps(st, [128, 512], F32, "PVv")
                PG = self.ps(st, [128, 512], F32, "PG")
                sgv = self.sgT_d[:].rearrange("(r p) t -> p r t", p=128)
                ctr = 0
                for bi, (t0, n) in enumerate(blocks):
                    h_ = hb[bi % 2]
                    c0 = colof(t0)
                    self.dma(h_[:, :, 0:n], self.hv[:, :, c0:c0 + n], [self.hT_d], [h_], slow=True)
                    self.dma(tq[64:96, :, 0:n], I["mla_tab"][:, :, t0:t0 + n], [I["mla_tab"]], [tq], slow=True)
                    self.dma(tk[:, :, 0:n], I["mla_tab"][:, :, t0:t0 + n], [I["mla_tab"]], [tk], slow=True)
                    for (nrc, off, dst, nrm, dim, keep) in ((3, 0, cqn, qn, 384.0, None), (2, 384, ckvn, kvn, 256.0, None)):
                        for rc in range(nrc):
                            p_ = P0[ctr % 2]; ctr += 1
                            for kc in range(8):
                                self.mm(p_[:, 0:n], wm[:, kc, off + rc * 128:off + (rc + 1) * 128], h_[:, kc, 0:n], kc == 0, kc == 7, [wm, h_], [p_])
                            sqs = sqs2[ctr % 2]
                            self.act(cqs[:, rc, 0:n], p_[:, 0:n], AF.Copy, [p_], [cqs])
                            self.act(sqs[:, 0:n], p_[:, 0:n], AF.Square, [p_], [sqs])
                            self.mm(PSS[:, 0:n], self.ones[:], sqs[:, 0:n], rc == 0, rc == nrc - 1, [self.ones, sqs], [PSS])
                        self.ts(rstd[:, 0:n], PSS[:, 0:n], 1.0 / dim, RMS_EPS, ALU.mult, ALU.add, [PSS], [rstd])
                        self.rsqrt(rstd[:, 0:n], rstd)
                        for rc in range(nrc):
                            self.stt(dst[:, rc, 0:n], cqs[:, rc, 0:n], nrm[:, rc:rc + 1], rstd[:, 0:n], ALU.mult, ALU.mult, [cqs, nrm, rstd], [dst])
                    self.mm_group_kr(h_, n, wm, PQ1, PQ2)
                    self.tt(kr1[:, 0:n], PQ1[0:32, 0:n], tk[:, 0, 0:n], ALU.mult, [PQ1, tk], [kr1])
                    self.tt(kr2[:, 0:n], PQ2[0:32, 0:n], tk[:, 1, 0:n], ALU.mult, [PQ2, tk], [kr2])
                    self.tt(krr[:, 0:n], kr1[:, 0:n], kr2[:, 0:n], ALU.add, [kr1, kr2], [krr])
                    for h in range(8):
                        qf_ = qf[h % 2]; kf_ = kf[h % 2]
                        q1 = q1s[h % 2]; q2 = q2s[h % 2]
                        PQ1_, PQ2_, PK_ = (PQ1, PQ2, PK) if h % 2 == 0 else (P0[0], P0[1], PG)
                        for kc in range(3):
                            self.mm(PQ1_[0:96, 0:n], wuq[:, kc, h, :], cqn[:, kc, 0:n], kc == 0, kc == 2, [wuq, cqn], [PQ1_])
                        for kc in range(3):
                            self.mm(PQ2_[0:96, 0:n], wuqs[:, kc, h, :], cqn[:, kc, 0:n], kc == 0, kc == 2, [wuqs, cqn], [PQ2_])
                        self.tt(q1[:, 0:n], PQ1_[0:96, 0:n], tq[:, 0, 0:n], ALU.mult, [PQ1_, tq], [q1])
                        self.tt(q2[:, 0:n], PQ2_[0:96, 0:n], tq[:, 1, 0:n], ALU.mult, [PQ2_, tq], [q2])
                        self.tt(qf_[:, 0:n], q1[:, 0:n], q2[:, 0:n], ALU.add, [q1, q2], [qf_], eng=POOL)
                        self.dma(self.qT_d[h, :, t0:t0 + n], qf_[:, 0:n], [qf_], [self.qT_d])
                        for kc in range(2):
                            self.mm(PK_[0:96, 0:n], wkp[:, kc, h, :], ckvn[:, kc, 0:n], kc == 0, False, [wkp, ckvn], [PK_])
                        self.mm(PK_[0:96, 0:n], esel[:], krr[:, 0:n], False, True, [esel, krr], [PK_])
                        self.act(kf_[:, 0:n], PK_[0:96, 0:n], AF.Copy, [PK_], [kf_])
                        self.dma(self.kfT_d[h, :, t0:t0 + n], kf_[:, 0:n], [kf_], [self.kfT_d])
                    for s in range(n // 128):
                        ch = (t0 + s * 128) // 128
                        PV_ = PVv if s % 2 == 0 else PSS
                        for kc in range(2):
                            self.mm(PV_[:], ckvn[:, kc, s * 128:(s + 1) * 128], wvv[:, kc, :, :].rearrange("p h e -> p (h e)"),
                                    kc == 0, kc == 1, [ckvn, wvv], [PV_])
                        self.act(v_all[:, ch, :], PV_[:], AF.Copy, [PV_], [v_all])
                    for rc in range(4):
                        sg_ = sgo[rc % 2]
                        ge = ges[rc % 2]
                        PG_ = PG if rc % 2 == 0 else PK
                        for kc in range(8):
                            self.mm(PG_[:, 0:n], wgate[:, kc, rc * 128:(rc + 1) * 128], h_[:, kc, 0:n], kc == 0, kc == 7, [wgate, h_], [PG_])
                        self.act(ge[:, 0:n], PG_[:, 0:n], AF.Exp, [PG_], [ge], scale=-1.0)
                        self.sigm(ge[:, 0:n], ge)
                        self.tt(sg_[:, 0:n], PG_[:, 0:n], ge[:, 0:n], ALU.mult, [PG_, ge], [sg_])
                        self.dma(sgv[:, rc, t0:t0 + n], sg_[:, 0:n], [sg_], [self.sgT_d])
            self.P.barrier()
            import os as _os
            if _os.environ.get("NO_M2"):
                return
            with ExitStack() as st:
                kfh = [self.sb(st, [96, NTOK], BF16, "kfh") for _ in range(2)]
                qh = [self.sb(st, [96, NTOK], BF16, "qh") for _ in range(2)]
                vaug = [self.sb(st, [128, NCH, 128], BF16, "vaug") for _ in range(2)]
                self.memset(vaug[0][:, :, 64:128], 1.0, [vaug[0]])
                self.memset(vaug[1][:, :, 0:64], 1.0, [vaug[1]])
                pT = [self.sb(st, [128, 1024], BF16, "pT") for _ in range(3)]
                sgh = [self.sb(st, [128, 512], F32, "sgh") for _ in range(2)]
                rden = self.sb(st, [128, 512], F32, "rden")
                ot = self.sb(st, [128, 512], F32, "ot")
                ob = [self.sb(st, [128, 512], BF16, "ob") for _ in range(2)]
                PSc = [self.ps(st, [128, 1024], F32, "PSc") for _ in range(3)]
                PO = [self.ps(st, [128, 512], F32, "PO") for _ in range(2)]
                ci = 0
                bi_ = 0
                for h in range(int(_os.environ.get("M2_HEADS", "8"))):
                    par = h % 2
                    r0 = 64 * par
                    d0 = 64 - r0
                    kf_ = kfh[h % 2]; q_ = qh[h % 2]; va = vaug[par]
                    self.dma(kf_[:], self.kfT_d[h], [self.kfT_d], [kf_])
                    self.dma(q_[:], self.qT_d[h], [self.qT_d], [q_])
                    self.cp(va[:, :, r0:r0 + 64], v_all[:, :, h * 64:(h + 1) * 64], [v_all], [va], eng=POOL)
                    its = []
                    for (t0, n) in blocks:
                        if t0 == 0:
                            if l == DEPTH - 1:
                                continue
                            kcs = [0, 1]
                        else:
                            kcs = list(range(NCH))
                        po = PO[bi_ % 2]; sg_ = sgh[bi_ % 2]; ob_ = ob[bi_ % 2]
                        bi_ += 1
                        npair = len(kcs) // 2
                        for i in range(npair):
                            its.append((t0, n, kcs[2 * i], kcs[2 * i + 1], i == 0, i == npair - 1, po, sg_, ob_))
                    LOOK = 2
                    for j in range(len(its) + LOOK):
                        if j < len(its):
                            (t0, n, ka, kb, first, last, po, sg_, ob_) = its[j]
                            if first:
                                self.dma(sg_[r0:r0 + 64, 0:n], self.sgT_d[64 * h:64 * (h + 1), t0:t0 + n], [self.sgT_d], [sg_])
                            psc = PSc[(ci + j) % 3]
                            self.mm(psc[:, 0:n], kf_[:, ka * 128:(ka + 1) * 128], q_[:, t0:t0 + n], True, True, [kf_, q_], [psc])
                            self.mm(psc[:, 512:512 + n], kf_[:, kb * 128:(kb + 1) * 128], q_[:, t0:t0 + n], True, True, [kf_, q_], [psc])
                        jj = j - LOOK
                        if jj >= 0:
                            (t0, n, ka, kb, first, last, po, sg_, ob_) = its[jj]
                            psc = PSc[(ci + jj) % 3]; pt = pT[(ci + jj) % 3]
                            self.act(pt[:].rearrange("p (a b) -> p a b", a=2)[:, :, 0:n], psc[:].rearrange("p (a b) -> p a b", a=2)[:, :, 0:n],
                                     AF.Exp, [psc], [pt], scale=MLA_SCALE)
                            self.mm(po[:, 0:n], va[:, ka, :], pt[:, 0:n], first, False, [va, pt], [po])
                            self.mm(po[:, 0:n], va[:, kb, :], pt[:, 512:512 + n], False, last, [va, pt], [po])
                            if last:
                                self.P.op(DVE, (lambda a_, b_: (lambda e: e.reciprocal(out=a_, in_=b_)))(rden[d0:d0 + 64, 0:n], po[d0:d0 + 64, 0:n]),
                                          [po.b], [rden.b])
                                self.tt(ot[r0:r0 + 64, 0:n], po[r0:r0 + 64, 0:n], rden[d0:d0 + 64, 0:n], ALU.mult, [po, rden], [ot])
                                self.tt(ob_[r0:r0 + 64, 0:n], ot[r0:r0 + 64, 0:n], sg_[r0:r0 + 64, 0:n], ALU.mult, [ot, sg_], [ob_], eng=POOL)
                                self.dma(self.mixT_d[1024 + h * 64:1024 + (h + 1) * 64, t0:t0 + n], ob_[r0:r0 + 64, 0:n], [ob_], [self.mixT_d])
                    ci += len(its)

    def mm_group_kr(self, h_, n, wm, PQ1, PQ2):
        for kc in range(8):
            self.mm(PQ1[0:32, 0:n], wm[:, kc, 640:672], h_[:, kc, 0:n], kc == 0, kc == 7, [wm, h_], [PQ1])
        for kc in range(8):
            self.mm(PQ2[0:32, 0:n], wm[:, kc, 672:704], h_[:, kc, 0:n], kc == 0, kc == 7, [wm, h_], [PQ2])

    def stage_E(self, l):
        I = self.I
        with ExitStack() as st:
            wo = self.sb(st, [128, 16, 1024], BF16, "wo")
            wov = I["w_out"][l].rearrange("(k p) c -> p k c", p=128)
            for kc in range(16):
                self.load_w(wo[:, kc, :], wo, wov[:, kc, :], I["w_out"])
            lng = self.bvec(st, "ln_g", l, 1024)
            lnb = self.bvec(st, "ln_b", l, 1024)
            sets = []
            for s_ in range(3):
                B = {}
                B["mt"] = self.sb(st, [128, 16, 128], BF16, "mt")
                B["xt"] = self.sb(st, [128, D], F32, "xt")
                B["v1"] = self.sb(st, [128, D], F32, "v1")
                B["v2"] = self.sb(st, [128, D], F32, "v2")
                B["xo"] = self.sb(st, [128, D], F32, "xo")
                B["stats"] = self.sb(st, [128, 2, 6], F32, "stats")
                B["mv"] = self.sb(st, [128, 2], F32, "mv")
                B["PZ"] = [self.ps(st, [128, 512], F32, "PZ") for _ in range(2)]
                sets.append(B)
            tiles = list(range(NCH)) if l < DEPTH - 1 else list(range(2, NCH))
            gens = [(lambda tt_: (lambda slot: self.e_tile(l, tt_, sets[slot], wo, lng, lnb)))(t) for t in tiles]
            self.interleave(gens, 3)

    def e_tile(self, l, t, B, wo, lng, lnb):
        I = self.I
        m_ = B["mt"]; x_ = B["xt"]; v1 = B["v1"]; v2 = B["v2"]; xo_ = B["xo"]; stats = B["stats"]; mv = B["mv"]; PZ = B["PZ"]
        mixv = self.mixT_d[:].rearrange("(r p) t -> p r t", p=128)
        typ = 1 if t < 2 else 0
        tok = slice(t * 128, (t + 1) * 128)
        self.dma(m_[:], mixv[:, :, tok], [self.mixT_d], [m_], slow=True)
        if l == 0:
            src_t = I["ctx"] if t < 2 else I["x"]
            src = src_t[t * 128:(t + 1) * 128, :] if t < 2 else src_t[(t - 2) * 128:(t - 1) * 128, :]
        else:
            src_t = self.xres_d
            src = src_t[tok, :]
        self.dma(x_[:], src, [src_t], [x_])
        yield
        for nb in range(2):
            for kc in range(16):
                self.mm(PZ[nb][:], m_[:, kc, :], wo[:, kc, nb * 512:(nb + 1) * 512], kc == 0, kc == 15, [m_, wo], [PZ[nb]])
            yield
        for nb in range(2):
            self.tt(v1[:, nb * 512:(nb + 1) * 512], PZ[nb][:], self.gB[:, typ, nb * 512:(nb + 1) * 512], ALU.mult, [PZ[nb], self.gB], [v1])
        yield
        self.stt(v2[:], x_[:], ALPHA, v1[:], ALU.mult, ALU.add, [x_, v1], [v2])
        yield
        for s in range(2):
            self.P.op(DVE, (lambda ss: (lambda e: e.bn_stats(out=stats[:, ss, :], in_=v2[:, ss * 512:(ss + 1) * 512])))(s),
                      [v2.b], [stats.b])
        self.P.op(DVE, lambda e: e.bn_aggr(out=mv[:], in_=stats[:]), [stats.b], [mv.b])
        self.ts(mv[:, 1:2], mv[:, 1:2], LN_EPS, None, ALU.add, None, [mv], [mv])
        yield
        self.rsqrt(mv[:, 1:2], mv)
        yield
        self.ts(v1[:], v2[:], mv[:, 0:1], mv[:, 1:2], ALU.subtract, ALU.mult, [v2, mv], [v1])
        yield
        self.tt(v2[:], v1[:], lng[:], ALU.mult, [v1, lng], [v2], eng=POOL)
        yield
        self.tt(xo_[:], v2[:], lnb[:], ALU.add, [v2, lnb], [xo_], eng=POOL)
        yield
        if l < DEPTH - 1:
            self.dma(self.xres_d[tok, :], xo_[:], [xo_], [self.xres_d])
        else:
            self.dma(self.out[(t - 2) * 128:(t - 1) * 128, :], xo_[:], [xo_], [self.out])
        yield


C_ID = 0
C_UF = 128
C_LF = 256
C_UB = 384
C_LB = 512
C_MF = 640
C_MB = 768
C_RIF = 896
C_RIB = 1024
C_GF = 1152
C_GB = 1280
C_TEF = 1408
C_TEB = 1409
CST_W = 1410


def make_consts():
    k = np.arange(128)[:, None].astype(np.float32)
    i = np.arange(128)[None, :].astype(np.float32)
    cst = np.zeros((128, CST_W), np.float32)
    cst[:, C_ID:C_ID + 128] = (k == i)
    cst[:, C_UF:C_UF + 128] = (k <= i)
    cst[:, C_LF:C_LF + 128] = (k > i)
    cst[:, C_UB:C_UB + 128] = (k >= i)
    cst[:, C_LB:C_LB + 128] = (k < i)
    cst[:, C_MF:C_MF + 128] = (k <= i)
    cst[:, C_MB:C_MB + 128] = (k >= i)
    cst[:, C_RIF:C_RIF + 128] = np.maximum(i - k, 0)
    cst[:, C_RIB:C_RIB + 128] = np.maximum(k - i, 0)
    cst[:, C_GF:C_GF + 128] = np.broadcast_to(i + 1, (128, 128))
    cst[:, C_GB:C_GB + 128] = np.broadcast_to(128 - i, (128, 128))
    cst[:, C_TEF] = 127 - k[:, 0]
    cst[:, C_TEB] = k[:, 0]
    return cst


def rope_tables():
    rows = SEQ // 64
    t = np.arange(rows * 64)
    row = (t // 64).astype(np.float32)
    col = (t % 64).astype(np.float32)

    def cs(rot):
        nf = rot // 4
        inv = (np.float32(10000.0) ** (-np.arange(nf, dtype=np.float32) / np.float32(nf))).astype(np.float32)
        ang = np.concatenate([row[:, None] * inv, col[:, None] * inv], -1).astype(np.float32)
        return np.cos(ang).astype(np.float32), np.sin(ang).astype(np.float32)
    cm, sm = cs(32)
    mla = np.zeros((32, 2, NTOK), np.float32)
    mla[:, 0, :CTX] = 1.0
    mla[0:16, 0, CTX:] = cm.T; mla[16:32, 0, CTX:] = cm.T
    mla[0:16, 1, CTX:] = -sm.T; mla[16:32, 1, CTX:] = sm.T
    cr, sr = cs(64)
    ret = np.zeros((128, 4, NTOK), np.float32)
    ret[:, 0, :CTX] = 1.0
    for hh in range(2):
        b = hh * 64
        ret[b:b + 32, 0, CTX:] = cr.T; ret[b + 32:b + 64, 0, CTX:] = cr.T
        ret[b:b + 32, 1, CTX:] = -sr.T; ret[b + 32:b + 64, 1, CTX:] = sr.T
    ret[:, 2] = ret[:, 0] * 0.125
    ret[:, 3] = ret[:, 1] * 0.125
    ret2 = np.zeros((NTOK, 1024), np.float32)
    cc = np.ones((NTOK, 4, 64), np.float32)
    ss = np.zeros((NTOK, 4, 64), np.float32)
    cc[CTX:, :, 0:32] = cr[:, None, :]; cc[CTX:, :, 32:64] = cr[:, None, :]
    ss[CTX:, :, 0:32] = -sr[:, None, :]; ss[CTX:, :, 32:64] = sr[:, None, :]
    ret2[:, 0:256] = cc.reshape(NTOK, 256); ret2[:, 256:512] = 0.125 * cc.reshape(NTOK, 256)
    ret2[:, 512:768] = ss.reshape(NTOK, 256); ret2[:, 768:1024] = 0.125 * ss.reshape(NTOK, 256)
    return mla, ret, ret2


_CACHE = {}


def prep_inputs(inputs):
    f = lambda a: np.ascontiguousarray(np.asarray(a, dtype=np.float32))
    w_in = f(inputs["w_in"])
    kr = w_in[:, :, 3216:3248]
    kr_sw = np.concatenate([kr[:, :, 16:32], kr[:, :, 0:16]], -1)

    def sw64(a):
        a = a.reshape(2, D, 4, 2, 32)
        return np.ascontiguousarray(a[:, :, :, ::-1, :]).reshape(2, D, 256)
    q_sw = sw64(w_in[:, :, 3760:4016])
    k_sw = sw64(w_in[:, :, 4016:4272])
    w_ext = np.ascontiguousarray(np.concatenate([w_in, kr_sw, q_sw, k_sw], -1))
    uq = f(inputs["mla_w_uq"]).reshape(2, 384, 8, 96)
    uq_r = uq[:, :, :, 64:96]
    uq_sw = np.ascontiguousarray(np.concatenate([uq_r[..., 16:32], uq_r[..., 0:16]], -1)).reshape(2, 384, 256)
    mla_tab, ret_tab, ret_tab2 = rope_tables()
    shared = {
        "w_ada": f(inputs["w_ada"]), "b_ada": f(inputs["b_ada"]), "w_in": w_ext,
        "ssd_conv_w": f(inputs["ssd_conv_w"]), "ssd_conv_b": f(inputs["ssd_conv_b"]),
        "ssd_norm_w": f(inputs["ssd_norm_w"]), "mla_q_norm": f(inputs["mla_q_norm"]),
        "mla_w_uq": f(inputs["mla_w_uq"]), "w_uq_sw": uq_sw, "mla_kv_norm": f(inputs["mla_kv_norm"]),
        "mla_w_ukv": f(inputs["mla_w_ukv"]), "ret_log_rate_f": f(inputs["ret_log_rate_f"]),
        "ret_log_rate_b": f(inputs["ret_log_rate_b"]), "w_out": f(inputs["w_out"]),
        "ln_g": f(inputs["ln_g"]), "ln_b": f(inputs["ln_b"]),
        "cst": make_consts(), "mla_tab": mla_tab, "ret_tab": ret_tab, "ret_tab2": ret_tab2,
    }
    for n in ("ssd_a_log_f", "ssd_a_log_b", "ssd_dt_bias_f", "ssd_dt_bias_b", "ssd_d"):
        shared[n] = f(inputs[n])
    x = f(inputs["x"]); c = f(inputs["c"]); ctx = f(inputs["ctx"]); c_ctx = f(inputs["c_ctx"])
    maps = []
    for b in range(8):
        m = dict(shared)
        m["x"] = x[b]
        m["ctx"] = ctx[b]
        m["cvec"] = np.ascontiguousarray(np.stack([c[b], c_ctx], 0))
        maps.append(m)
    return maps


def kernel(**inputs):
    if "nc" not in _CACHE:
        _CACHE["nc"] = K().build()
    nc = _CACHE["nc"]
    maps = prep_inputs(inputs)
    res = run_bass_kernel_spmd(nc, maps, core_ids=list(range(8)))
    return np.stack([np.asarray(r["out"], dtype=np.float32) for r in res.results], 0)
```

```python
import math
from contextlib import ExitStack
import numpy as np
import concourse.bass as bass
import concourse.mybir as mybir
from concourse.bass_utils import run_bass_kernel_spmd

F32 = mybir.dt.float32
BF16 = mybir.dt.bfloat16
AF = mybir.ActivationFunctionType
ALU = mybir.AluOpType

PE, ACT, DVE, POOL, SP = "pe", "act", "dve", "pool", "sp"
EPOCH = 30000
DMA_K = 8
DMA_EPOCH = 1800

D = 1024
SEQ = 4096
CTX = 256
NTOK = SEQ + CTX
NCH = NTOK // 128
HC = NTOK + 8
DEPTH = 2
ALPHA = (2 * DEPTH) ** 0.25
LN_EPS = 1e-5
RMS_EPS = 1e-6
MLA_SCALE = 96 ** -0.5
WEXT = 5296 + 32 + 256 + 256


def colof(t):
    return t + 2 if t < CTX else t + 6


class Buf:
    __slots__ = ("name", "lw", "rd", "rdd", "psum")

    def __init__(self, name=""):
        self.name = name
        self.lw = None
        self.rd = {}
        self.rdd = []
        self.psum = False


class Op:
    __slots__ = ("eng", "fn", "deps", "sig", "idx", "dma_slot", "dma_prev")

    def __init__(self, eng, fn):
        self.eng = eng
        self.fn = fn
        self.deps = set()
        self.sig = None
        self.dma_slot = None
        self.dma_prev = None


class Prog:
    def __init__(self, nc):
        self.nc = nc
        self.ops = []
        self.eng = {PE: nc.tensor, ACT: nc.scalar, DVE: nc.vector, POOL: nc.gpsimd, SP: nc.sync}
        self.dma_lists = {}
        self.last = {}

    def op(self, eng, fn, reads=(), writes=(), dma=False):
        o = Op(eng, fn)
        o.idx = len(self.ops)
        for b in reads:
            if b.lw is not None:
                o.deps.add(b.lw)
            if b.psum:
                for e2, r in b.rd.items():
                    if e2 != eng:
                        o.deps.add(r)
        for b in writes:
            if b.lw is not None:
                o.deps.add(b.lw)
            for r in b.rd.values():
                o.deps.add(r)
            for r in b.rdd:
                o.deps.add(r)
        for b in reads:
            if dma:
                b.rdd.append(o.idx)
            else:
                b.rd[eng] = o.idx
        for b in writes:
            b.lw = o.idx
            b.rd = {}
            b.rdd = []
        if dma:
            lst = self.dma_lists.setdefault(eng, [])
            o.dma_slot = len(lst)
            if len(lst) >= DMA_K:
                o.dma_prev = lst[len(lst) - DMA_K]
            lst.append(o.idx)
        o.deps.discard(o.idx)
        self.ops.append(o)
        self.last[eng] = o.idx
        return o

    def barrier(self):
        bufs = {}
        for e in (PE, ACT, DVE, POOL, SP):
            bufs[e] = Buf("bar" + e)
            o = self.op(e, lambda en: en.nop(), writes=[bufs[e]])
            for lst in self.dma_lists.values():
                for d in lst[-DMA_K:]:
                    if d != o.idx:
                        o.deps.add(d)
        for e in (PE, ACT, DVE, POOL, SP):
            self.op(e, lambda en: en.nop(), reads=list(bufs.values()))

    def emit(self, stack):
        nc = self.nc
        ops = self.ops
        needed = set()
        for o in ops:
            for d in o.deps:
                do = ops[d]
                if do.eng == o.eng and o.eng == PE and do.dma_slot is None:
                    continue
                needed.add(d)
            if o.dma_prev is not None:
                needed.add(o.dma_prev)
        cnt = {}
        sems = {}
        dma_sems = {}
        for o in ops:
            if o.dma_slot is not None:
                k = o.dma_slot % DMA_K
                n = o.dma_slot // DMA_K
                key = (o.eng, k, n // DMA_EPOCH)
                if key not in dma_sems:
                    dma_sems[key] = stack.enter_context(nc.semaphore("dq%s%d_%d" % key))
                o.sig = (dma_sems[key], 16 * (n % DMA_EPOCH + 1))
            elif o.idx in needed:
                c = cnt.get(o.eng, 0)
                key = (o.eng, c // EPOCH)
                if key not in sems:
                    sems[key] = stack.enter_context(nc.semaphore("s%s_%d" % key))
                o.sig = (sems[key], c % EPOCH + 1)
                cnt[o.eng] = c + 1
        waited = {}
        nw = 0
        for o in ops:
            e = self.eng[o.eng]
            deps = set(o.deps)
            if o.dma_prev is not None:
                deps.add(o.dma_prev)
            for d in sorted(deps):
                do = ops[d]
                if do.sig is None:
                    continue
                sem, val = do.sig
                key = (o.eng, id(sem))
                if waited.get(key, 0) >= val:
                    continue
                waited[key] = val
                e.wait_ge(sem, val)
                nw += 1
            ins = o.fn(e)
            if o.sig is not None:
                sem, val = o.sig
                ins.then_inc(sem, 16 if o.dma_slot is not None else 1)
        self.nwaits = nw


class T:
    __slots__ = ("t", "b")

    def __init__(self, t, name=""):
        self.t = t
        self.b = Buf(name)

    def __getitem__(self, k):
        return self.t[k]


class K:
    def __init__(self, debug=False, stop_after=None, skip=()):
        self.skip = set(skip)
        self.debug = debug
        self.stop_after = stop_after
        self.nc = bass.Bass("TRN2", target_bir_lowering=False)
        self.P = Prog(self.nc)
        self.uid = 0

    def dram(self, name, shape, dt, kind="Internal"):
        return T(self.nc.dram_tensor(name, list(shape), dt, kind=kind).ap(), name)

    def sb(self, st, shape, dt, name=None):
        self.uid += 1
        name = "%s_%d" % (name or "t", self.uid)
        return T(st.enter_context(self.nc.sbuf_tensor(name, list(shape), dt)), name)

    def ps(self, st, shape, dt=F32, name=None):
        self.uid += 1
        name = "%s_%d" % (name or "p", self.uid)
        t = T(st.enter_context(self.nc.psum_tensor(name, list(shape), dt)), name)
        t.b.psum = True
        return t

    def dma(self, out, in_, reads, writes, eng=SP, slow=False):
        if slow:
            return self.P.op(eng, lambda e: e.dma_start(out=out, in_=in_, allow_slow_non_contiguous=True),
                             [x.b for x in reads], [x.b for x in writes], dma=True)
        return self.P.op(eng, lambda e: e.dma_start(out=out, in_=in_), [x.b for x in reads], [x.b for x in writes], dma=True)

    def mm(self, out, lhsT, rhs, start, stop, reads, writes):
        return self.P.op(PE, lambda e: e.matmul(out, lhsT=lhsT, rhs=rhs, start=start, stop=stop),
                         [x.b for x in reads], [x.b for x in writes])

    def tr(self, out, in_, ident, reads, writes):
        return self.P.op(PE, lambda e: e.transpose(out=out, in_=in_, identity=ident),
                         [x.b for x in reads], [x.b for x in writes])

    def act(self, out, in_, func, reads, writes, bias=None, scale=None, accum_out=None):
        kw = {}
        if bias is not None:
            kw["bias"] = bias
        if scale is not None:
            kw["scale"] = scale
        if accum_out is not None:
            kw["accum_out"] = accum_out
        return self.P.op(ACT, lambda e: e.activation(out=out, in_=in_, func=func, **kw),
                         [x.b for x in reads], [x.b for x in writes])

    def tt(self, out, in0, in1, op, reads, writes, eng=DVE):
        return self.P.op(eng, lambda e: e.tensor_tensor(out=out, in0=in0, in1=in1, op=op),
                         [x.b for x in reads], [x.b for x in writes])

    def ts(self, out, in0, s1, s2, op0, op1, reads, writes, eng=DVE):
        if op1 is None:
            return self.P.op(eng, lambda e: e.tensor_scalar(out=out, in0=in0, scalar1=s1, scalar2=None, op0=op0),
                             [x.b for x in reads], [x.b for x in writes])
        return self.P.op(eng, lambda e: e.tensor_scalar(out=out, in0=in0, scalar1=s1, scalar2=s2, op0=op0, op1=op1),
                         [x.b for x in reads], [x.b for x in writes])

    def stt(self, out, in0, scalar, in1, op0, op1, reads, writes, eng=DVE):
        return self.P.op(eng, lambda e: e.scalar_tensor_tensor(out=out, in0=in0, scalar=scalar, in1=in1, op0=op0, op1=op1),
                         [x.b for x in reads], [x.b for x in writes])

    def cp(self, out, in_, reads, writes, eng=DVE):
        return self.P.op(eng, lambda e: e.tensor_copy(out=out, in_=in_), [x.b for x in reads], [x.b for x in writes])

    def memset(self, out, val, writes, eng=POOL):
        return self.P.op(eng, lambda e: e.memset(out, val), [], [x.b for x in writes])

    def sigm(self, ap, t):
        self.act(ap, ap, AF.Ln, [t], [t], bias=1.0)
        self.act(ap, ap, AF.Exp, [t], [t], scale=-1.0)

    def rsqrt(self, ap, t):
        self.act(ap, ap, AF.Ln, [t], [t])
        self.act(ap, ap, AF.Exp, [t], [t], scale=-0.5)

    def build(self):
        nc = self.nc
        dbg = self.debug
        I = {}

        def inp(name, shape):
            I[name] = self.dram(name, shape, F32, kind="ExternalInput")
        inp("x", [SEQ, D]); inp("ctx", [CTX, D]); inp("cvec", [2, D])
        inp("w_ada", [2, D, 3 * D]); inp("b_ada", [2, 3 * D]); inp("w_in", [2, D, WEXT])
        inp("ssd_conv_w", [2, 5, 1536]); inp("ssd_conv_b", [2, 1536])
        for n in ("ssd_a_log_f", "ssd_a_log_b", "ssd_dt_bias_f", "ssd_dt_bias_b", "ssd_d"):
            inp(n, [2, 16])
        inp("ssd_norm_w", [2, 1024]); inp("mla_q_norm", [2, 384]); inp("mla_w_uq", [2, 384, 768])
        inp("w_uq_sw", [2, 384, 256]); inp("mla_kv_norm", [2, 256]); inp("mla_w_ukv", [2, 256, 1024])
        inp("ret_log_rate_f", [2, 4]); inp("ret_log_rate_b", [2, 4]); inp("w_out", [2, 2048, D])
        inp("ln_g", [2, D]); inp("ln_b", [2, D])
        inp("cst", [128, CST_W]); inp("mla_tab", [32, 2, NTOK]); inp("ret_tab", [128, 4, NTOK]); inp("ret_tab2", [NTOK, 1024])
        self.I = I
        okind = "ExternalOutput" if dbg else "Internal"
        self.out = self.dram("out", [SEQ, D], F32, kind="ExternalOutput")
        self.hT_d = self.dram("hT_d", [128, 8 * HC], BF16, kind=okind)
        self.ypart_d = self.dram("ypart_d", [NTOK, 1024], F32)
        self.rpart_d = self.dram("rpart_d", [NTOK, 512], F32)
        self.ypartb_d = self.dram("ypartb_d", [NTOK, 1024], F32)
        self.Fb_d = self.dram("Fb_d", [NCH, 128, 1536], BF16)
        self.Rb_d = self.dram("Rb_d", [NCH, 128, 1280], BF16)
        self.Ff_d = self.dram("Ff_d", [NCH, 128, 272], F32)
        self.rpartb_d = self.dram("rpartb_d", [NTOK, 512], F32)
        self.mixT_d = self.dram("mixT_d", [NCH, 128, 2048], BF16, kind=okind)
        self.qT_d = self.dram("qT_d", [8, 96, NTOK], BF16)
        self.kfT_d = self.dram("kfT_d", [8, 96, NTOK], BF16)
        self.sgT_d = self.dram("sgT_d", [512, NTOK], F32)
        self.xres_d = self.dram("xres_d", [NTOK, D], F32, kind=okind)

        with ExitStack() as gst:
            self.gst = gst
            self.cst = self.sb(gst, [128, CST_W], F32, "cst")
            self.dma(self.cst[:], I["cst"][:], [I["cst"]], [self.cst])
            self.identb = self.sb(gst, [128, 128], BF16, "identb")
            self.cp(self.identb[:], self.cst[:, C_ID:C_ID + 128], [self.cst], [self.identb])
            self.ones = self.sb(gst, [128, 128], F32, "ones")
            self.memset(self.ones[:], 1.0, [self.ones])
            self.gB = self.sb(gst, [128, 2, 1024], F32, "gB")
            zt = self.sb(gst, [128, 8, 4], BF16, "zt")
            self.memset(zt[:], 0.0, [zt])
            hv = self.hT_d[:].rearrange("p (k c) -> p k c", k=8)
            self.hv = hv
            for (a, b) in ((0, 2), (258, 262), (4358, 4360)):
                self.dma(hv[:, :, a:b], zt[:, :, 0:b - a], [zt], [self.hT_d], slow=True)
            self.P.barrier()
            stages = []
            for l in range(DEPTH):
                stages += [("A", l), ("S", l), ("M", l), ("R", l), ("E", l)]
            for (s, l) in stages:
                if s in self.skip:
                    continue
                if s == "A":
                    self.stage_A(l)
                elif s == "S":
                    self.stage_S(l)
                elif s == "M":
                    self.stage_M(l)
                elif s == "R":
                    self.stage_R(l)
                else:
                    self.stage_E(l)
                self.P.barrier()
                if self.stop_after == (s, l):
                    break
            self.P.barrier()
            self.P.emit(gst)
        return nc

    def silu_psum(self, st, src_ap, src_t, out_ap, out_t, e_t, e_ap, r_ap):
        self.act(e_ap, src_ap, AF.Exp, [src_t], [e_t], scale=-1.0)
        self.sigm(e_ap, e_t)
        self.tt(out_ap, src_ap, r_ap, ALU.mult, [src_t, e_t], [out_t])

    def stage_A(self, l):
        I = self.I
        with ExitStack() as st:
            wada = [self.sb(st, [128, 8, 512], F32, "wada") for _ in range(2)]
            craw = self.sb(st, [128, 8, 2], F32, "craw")
            ce = self.sb(st, [128, 8, 2], F32, "ce")
            scT = self.sb(st, [128, 8, 2], F32, "scT")
            modT = self.sb(st, [128, 24, 2], F32, "modT")
            scale1 = self.sb(st, [128, 8, 2], F32, "scale1")
            brow = self.sb(st, [1, 3 * D], F32, "brow")
            pm = self.ps(st, [128, 512], F32, "pm")
            pg = [self.ps(st, [128, 512], F32, "pg") for _ in range(2)]
            for j in range(2):
                self.dma(craw[:, :, j], I["cvec"][j].rearrange("(k p) -> p k", p=128), [I["cvec"]], [craw], slow=True)
            self.dma(brow[:], I["b_ada"][l:l + 1, :], [I["b_ada"]], [brow])
            self.act(ce[:], craw[:], AF.Exp, [craw], [ce], scale=-1.0)
            self.sigm(ce[:], ce)
            self.tt(scT[:], craw[:], ce[:], ALU.mult, [craw, ce], [scT])
            wv = I["w_ada"][l].rearrange("(k p) c -> p k c", p=128)
            for cb in range(6):
                w = wada[cb % 2]
                self.dma(w[:], wv[:, :, cb * 512:(cb + 1) * 512], [I["w_ada"]], [w])
                if cb < 4:
                    for dj in range(4):
                        j = cb * 4 + dj
                        for kc in range(8):
                            self.mm(pm[:, 2 * dj:2 * dj + 2], w[:, kc, dj * 128:(dj + 1) * 128], scT[:, kc, :],
                                    kc == 0, False, [w, scT], [pm])
                        self.mm(pm[:, 2 * dj:2 * dj + 2], brow[0:1, j * 128:(j + 1) * 128], self.ones[0:1, 0:2],
                                False, True, [brow, self.ones], [pm])
                        self.cp(modT[:, j, :], pm[:, 2 * dj:2 * dj + 2], [pm], [modT])
                else:
                    for typ in range(2):
                        p = pg[typ]
                        for kc in range(8):
                            self.mm(p[:], scT[:, kc, typ:typ + 1].to_broadcast([128, 128]), w[:, kc, :],
                                    kc == 0, False, [w, scT], [p])
                        self.mm(p[:], self.ones[0:1, 0:128], brow[0:1, cb * 512:(cb + 1) * 512], False, True,
                                [brow, self.ones], [p])
                        self.cp(self.gB[:, typ, (cb - 4) * 512:(cb - 3) * 512], p[:], [p], [self.gB])
            self.ts(scale1[:], modT[:, 8:16, :], 1.0, None, ALU.add, None, [modT], [scale1])
            sets = []
            for s_ in range(3):
                B = {}
                B["xt"] = self.sb(st, [128, D], F32, "xt")
                B["ht"] = self.sb(st, [128, 8, 128], BF16, "ht")
                B["pT"] = pg if s_ == 0 else [self.ps(st, [128, 512], F32, "pT") for _ in range(2)]
                sets.append(B)
            gens = [(lambda tt_: (lambda slot: self.a_tile(l, tt_, sets[slot], scale1, modT)))(t) for t in range(NCH)]
            self.interleave(gens, 3)

    def a_tile(self, l, t, B, scale1, modT):
        I = self.I
        x_ = B["xt"]; h_ = B["ht"]; pT = B["pT"]
        typ = 1 if t < 2 else 0
        if l == 0:
            src_t = I["ctx"] if t < 2 else I["x"]
            src = src_t[t * 128:(t + 1) * 128, :] if t < 2 else src_t[(t - 2) * 128:(t - 1) * 128, :]
        else:
            src_t = self.xres_d
            src = src_t[t * 128:(t + 1) * 128, :]
        self.dma(x_[:], src, [src_t], [x_])
        yield
        for kc in range(8):
            p_ = pT[kc // 4]
            self.tr(p_[:, (kc % 4) * 128:(kc % 4 + 1) * 128], x_[:, kc * 128:(kc + 1) * 128], self.cst[:, C_ID:C_ID + 128],
                    [x_, self.cst], [p_])
            if kc % 4 == 3:
                yield
        for kc in range(8):
            p_ = pT[kc // 4]
            self.act(h_[:, kc, :], p_[:, (kc % 4) * 128:(kc % 4 + 1) * 128], AF.Identity, [p_, scale1, modT], [h_],
                     bias=modT[:, kc, typ:typ + 1], scale=scale1[:, kc, typ:typ + 1])
            if kc % 4 == 3:
                yield
        c0 = colof(t * 128)
        self.dma(self.hv[:, :, c0:c0 + 128], h_[:], [h_], [self.hT_d])
        yield

    def load_w(self, dst_ap, dst_t, src_ap, src_t):
        self.dma(dst_ap, src_ap, [src_t], [dst_t], eng=POOL, slow=True)

    def bvec(self, st, name, l, n):
        t = self.sb(st, [128, n], F32, name)
        self.dma(t[:], self.I[name][l:l + 1, :].to_broadcast([128, n]), [self.I[name]], [t], slow=True)
        return t

    def interleave(self, factories, width):
        pending = list(factories)
        active = []
        for s in range(width):
            if pending:
                active.append((s, pending.pop(0)(s)))
        while active:
            nxt = []
            for (s, g) in active:
                try:
                    next(g)
                    nxt.append((s, g))
                except StopIteration:
                    if pending:
                        nxt.append((s, pending.pop(0)(s)))
            active = nxt

    def stage_S(self, l):
        I = self.I
        cst = self.cst
        import os as _os
        with ExitStack() as st:
            wv = I["w_in"][l].rearrange("(k p) c -> p k c", p=128)
            with ExitStack() as stf:
                wx = self.sb(stf, [128, 8, 1536], BF16, "wx")
                wdt = self.sb(stf, [128, 8, 16], BF16, "wdt")
                for kc in range(8):
                    self.load_w(wx[:, kc, :], wx, wv[:, kc, 1024:2560], I["w_in"])
                self.load_w(wdt[:], wdt, wv[:, :, 2560:2576], I["w_in"])
                convw = self.sb(stf, [128, 12, 5], F32, "convw")
                for k in range(5):
                    self.dma(convw[:, :, k], I["ssd_conv_w"][l, k].rearrange("(r p) -> p r", p=128), [I["ssd_conv_w"]], [convw], slow=True)
                dg = self.sb(stf, [128, 60, 128], BF16, "dg")
                for r in range(12):
                    for k in range(5):
                        self.ts(dg[:, r * 5 + k, :], self.identb[:], convw[:, r, k:k + 1], None, ALU.mult, None,
                                [self.identb, convw], [dg])
                cbrow = self.sb(stf, [1, 1536], F32, "cbrow")
                self.dma(cbrow[:], I["ssd_conv_b"][l:l + 1, :], [I["ssd_conv_b"]], [cbrow])
                sets = []
                for s_ in range(2):
                    B = {}
                    B["hc"] = self.sb(stf, [128, 8, 132], BF16, "hcf")
                    B["xbc"] = self.sb(stf, [128, 12, 132], BF16, "xbc")
                    B["esb"] = self.sb(stf, [128, 1536], F32, "esbf")
                    B["ubf"] = self.sb(stf, [128, 12, 128], BF16, "ubf")
                    B["Fb"] = self.sb(stf, [128, 1536], BF16, "Fbw")
                    B["Ff"] = self.sb(stf, [128, 272], F32, "Ffw")
                    B["Q"] = [self.ps(stf, [128, 512], F32, "QF") for _ in range(4)]
                    sets.append(B)
                gens = [(lambda cc: (lambda slot: self.ssd_front(cc, sets[slot], wx, wdt, dg, cbrow)))(c) for c in range(NCH)]
                self.interleave(gens, 2)
            self.P.barrier()
            Dsk = self.bvec(st, "ssd_d", l, 16)
            prm = {}
            for d_, sfx in ((0, "f"), (1, "b")):
                al = self.bvec(st, "ssd_a_log_" + sfx, l, 16)
                self.act(al[:], al[:], AF.Exp, [al], [al])
                self.ts(al[:], al[:], -1.0, None, ALU.mult, None, [al], [al])
                dtb = self.bvec(st, "ssd_dt_bias_" + sfx, l, 16)
                Ub = self.sb(st, [128, 128], BF16, "Ub16")
                Lb = self.sb(st, [128, 128], BF16, "Lb16")
                Uo = C_UF if d_ == 0 else C_UB
                Lo = C_LF if d_ == 0 else C_LB
                self.cp(Ub[:], cst[:, Uo:Uo + 128], [cst], [Ub])
                self.cp(Lb[:], cst[:, Lo:Lo + 128], [cst], [Lb])
                prm[d_] = (al, dtb, Ub, Lb)
            with ExitStack() as st2:
                gens = []
                self.yb = {}
                for d_ in (0, 1):
                    for g_ in (0, 1):
                        self.yb[(d_, g_)] = T((self.ypart_d if d_ == 0 else self.ypartb_d).t, "yb")
                        gens.append((lambda dd, gg: (lambda slot: self.ssd_sweep(l, dd, gg, st2, Dsk, prm[dd])))(d_, g_))
                self.interleave(gens, 4)
            self.P.barrier()
            with ExitStack() as st3:
                wz = self.sb(st3, [128, 8, 1024], BF16, "wz")
                for kc in range(8):
                    self.load_w(wz[:, kc, :], wz, wv[:, kc, 0:1024], I["w_in"])
                nwB = self.bvec(st3, "ssd_norm_w", l, 1024)
                sets = []
                for s in range(2):
                    B = {}
                    B["hc"] = self.sb(st3, [128, 8, 128], BF16, "hc3")
                    B["ypf"] = self.sb(st3, [128, 1024], F32, "ypf")
                    B["ypb"] = self.sb(st3, [128, 1024], F32, "ypb")
                    B["e"] = self.sb(st3, [128, 1024], F32, "e3")
                    B["t1"] = self.sb(st3, [128, 1024], F32, "t13")
                    B["bst"] = self.sb(st3, [128, 2, 6], F32, "bst3")
                    B["ssq"] = self.sb(st3, [128, 2], F32, "ssq3")
                    B["ob"] = self.sb(st3, [128, 1024], BF16, "ob3")
                    B["oT"] = self.sb(st3, [128, 8, 128], BF16, "oT3")
                    B["PZ"] = [self.ps(st3, [128, 512], F32, "PZ3") for _ in range(2)]
                    B["PT"] = self.ps(st3, [128, 1024], BF16, "PT3")
                    sets.append(B)
                gens = [(lambda cc: (lambda slot: self.ssd_final(cc, sets[slot], wz, nwB)))(c) for c in range(NCH)]
                self.interleave(gens, 2)

    def ssd_final(self, c, B, wz, nwB):
        h_ = B["hc"]; ypf = B["ypf"]; ypb = B["ypb"]; e = B["e"]; t1 = B["t1"]; bst = B["bst"]; ssq = B["ssq"]
        ob = B["ob"]; oT_ = B["oT"]; PZ = B["PZ"]; PT = B["PT"]
        c0 = colof(c * 128)
        tok = slice(c * 128, (c + 1) * 128)
        self.dma(h_[:], self.hv[:, :, c0:c0 + 128], [self.hT_d], [h_], slow=True)
        self.dma(ypf[:], self.ypart_d[tok, :], [self.yb[(0, 0)], self.yb[(0, 1)]], [ypf])
        self.dma(ypb[:], self.ypartb_d[tok, :], [self.yb[(1, 0)], self.yb[(1, 1)]], [ypb])
        yield
        for n in range(2):
            for kc in range(8):
                self.mm(PZ[n][:], h_[:, kc, :], wz[:, kc, n * 512:(n + 1) * 512], kc == 0, kc == 7, [h_, wz], [PZ[n]])
        self.tt(ypf[:], ypf[:], ypb[:], ALU.add, [ypf, ypb], [ypf])
        yield
        for n in range(2):
            self.act(e[:, n * 512:(n + 1) * 512], PZ[n][:], AF.Exp, [PZ[n]], [e], scale=-1.0)
        yield
        self.sigm(e[:], e)
        yield
        for n in range(2):
            self.tt(t1[:, n * 512:(n + 1) * 512], PZ[n][:], e[:, n * 512:(n + 1) * 512], ALU.mult, [PZ[n], e], [t1])
        yield
        self.tt(t1[:], t1[:], ypf[:], ALU.mult, [t1, ypf], [t1])
        yield
        for s_ in range(2):
            self.P.op(DVE, (lambda ss: (lambda en: en.bn_stats(out=bst[:, ss, :], in_=t1[:, ss * 512:(ss + 1) * 512])))(s_),
                      [t1.b], [bst.b])
        self.P.op(DVE, lambda en: en.bn_aggr(out=ssq[:], in_=bst[:]), [bst.b], [ssq.b])
        self.stt(ssq[:, 1:2], ssq[:, 0:1], ssq[:, 0:1], ssq[:, 1:2], ALU.mult, ALU.add, [ssq], [ssq])
        self.ts(ssq[:, 1:2], ssq[:, 1:2], RMS_EPS, None, ALU.add, None, [ssq], [ssq])
        yield
        self.rsqrt(ssq[:, 1:2], ssq)
        yield
        self.stt(ob[:], t1[:], ssq[:, 1:2], nwB[:], ALU.mult, ALU.mult, [t1, ssq, nwB], [ob])
        yield
        for r in range(8):
            self.tr(PT[:, r * 128:(r + 1) * 128], ob[:, r * 128:(r + 1) * 128], self.identb[:], [ob, self.identb], [PT])
        yield
        self.act(oT_[:], PT[:].rearrange("p (a b) -> p a b", a=8), AF.Copy, [PT], [oT_])
        yield
        self.dma(self.mixT_d[c, :, 0:1024], oT_[:].rearrange("p a b -> p (a b)"), [oT_], [self.mixT_d])
        yield

    def ssd_front(self, c, B, wx, wdt, dg, cbrow):
        h_ = B["hc"]; xbc = B["xbc"]; esb = B["esb"]; ubf = B["ubf"]; Fb = B["Fb"]; Ff = B["Ff"]; Q = B["Q"]
        Qb0 = Q[0][:].bitcast(BF16)
        Qb1 = Q[1][:].bitcast(BF16)
        c0 = colof(c * 128)
        self.dma(h_[:], self.hv[:, :, c0 - 2:c0 + 130], [self.hT_d], [h_], slow=True)
        yield
        for r in range(12):
            q_ = Q[r // 3]
            o_ = q_[:, (r % 3) * 132:(r % 3) * 132 + 132]
            for kc in range(8):
                self.mm(o_, wx[:, kc, r * 128:(r + 1) * 128], h_[:, kc, :], kc == 0, kc == 7, [wx, h_], [q_])
            if r % 3 == 2:
                yield
        for q in range(4):
            self.act(xbc[:, 3 * q:3 * q + 3, :], Q[q][:, 0:396].rearrange("p (a b) -> p a b", a=3), AF.Copy, [Q[q]], [xbc])
        yield
        for kc in range(8):
            self.mm(Q[3][:, 0:16], h_[:, kc, 2:130], wdt[:, kc, :], kc == 0, kc == 7, [h_, wdt], [Q[3]])
        yield
        for r in range(12):
            q_ = Q[r // 4]
            o_ = q_[:, (r % 4) * 128:(r % 4 + 1) * 128]
            for k in range(5):
                self.mm(o_, dg[:, r * 5 + k, :], xbc[:, r, k:k + 128], k == 0, False, [dg, xbc], [q_])
            self.mm(o_, cbrow[0:1, r * 128:(r + 1) * 128], self.ones[0:1, 0:128], False, True, [cbrow, self.ones], [q_])
            if r % 4 == 3:
                yield
        self.cp(Ff[:, 256:272], Q[3][:, 0:16], [Q[3]], [Ff])
        for q in range(3):
            self.act(esb[:, q * 512:(q + 1) * 512], Q[q][:], AF.Exp, [Q[q]], [esb], scale=-1.0)
        yield
        self.sigm(esb[:], esb)
        yield
        for q in range(3):
            self.tt(ubf[:, 4 * q:4 * q + 4, :], Q[q][:].rearrange("p (a b) -> p a b", a=4),
                    esb[:, q * 512:(q + 1) * 512].rearrange("p (a b) -> p a b", a=4), ALU.mult, [Q[q], esb], [ubf])
        yield
        for g in range(2):
            self.mm(Q[3][:, 256 + g * 128:256 + (g + 1) * 128], ubf[:, 8 + g, :], ubf[:, 10 + g, :], True, True, [ubf], [Q[3]])
        for r in range(8):
            self.tr(Qb0[:, r * 128:(r + 1) * 128], ubf[:, r, :], self.identb[:], [ubf, self.identb], [Q[0]])
        for r in range(2):
            self.tr(Qb1[:, r * 128:(r + 1) * 128], ubf[:, 8 + r, :], self.identb[:], [ubf, self.identb], [Q[1]])
        self.cp(Fb[:, 1280:1536], ubf[:, 10:12, :].rearrange("p a b -> p (a b)"), [ubf], [Fb], eng=POOL)
        yield
        self.cp(Ff[:, 0:256], Q[3][:, 256:512], [Q[3]], [Ff])
        self.act(Fb[:, 0:1024], Qb0[:, 0:1024], AF.Copy, [Q[0]], [Fb])
        self.cp(Fb[:, 1024:1280], Qb1[:, 0:256], [Q[1]], [Fb])
        yield
        self.dma(self.Fb_d[c], Fb[:], [Fb], [self.Fb_d])
        self.dma(self.Ff_d[c], Ff[:], [Ff], [self.Ff_d])
        yield

    def ssd_sweep(self, l, d_, g, st, Dsk, prm):
        cst = self.cst
        al, dtb, Ub, Lb = prm
        HS = slice(g * 8, (g + 1) * 8)
        H = self.sb(st, [128, 512], F32, "H")
        Hbf = self.sb(st, [128, 512], BF16, "Hbf")
        Fb = [self.sb(st, [128, 1536], BF16, "Fbr") for _ in range(2)]
        Ff = [self.sb(st, [128, 272], F32, "Ffr") for _ in range(2)]
        esb = self.sb(st, [128, 1024], F32, "esb")
        dtx = self.sb(st, [128, 8], F32, "dtx")
        dt = self.sb(st, [128, 8], F32, "dt")
        la = self.sb(st, [128, 8], F32, "la")
        lah = self.sb(st, [128, 8], BF16, "lah")
        lah32 = self.sb(st, [128, 8], F32, "lah32")
        lal32 = self.sb(st, [128, 8], F32, "lal32")
        lalo = self.sb(st, [128, 8], BF16, "lalo")
        E3 = self.sb(st, [128, 24], F32, "E3")
        scm = self.sb(st, [128, 128], F32, "scm")
        LaUh = self.sb(st, [128, 8, 128], BF16, "LaUh")
        LaUl = self.sb(st, [128, 8, 128], BF16, "LaUl")
        M = self.sb(st, [128, 8, 128], BF16, "M")
        v = self.sb(st, [128, 512], BF16, "v")
        vte = self.sb(st, [128, 512], BF16, "vte")
        t1 = self.sb(st, [128, 512], F32, "t1")
        t2 = self.sb(st, [128, 512], F32, "t2")
        yp = [self.sb(st, [128, 512], F32, "yp") for _ in range(2)]
        Q = [self.ps(st, [128, 512], F32, "Q") for _ in range(2)]
        ydst = self.ypart_d if d_ == 0 else self.ypartb_d
        order = list(range(NCH)) if d_ == 0 else [1, 0] + list(range(NCH - 1, 1, -1))
        Uo = C_UF if d_ == 0 else C_UB
        Lo = C_LF if d_ == 0 else C_LB
        Mo = C_MF if d_ == 0 else C_MB
        self.memset(H[:], 0.0, [H])
        self.memset(Hbf[:], 0.0, [Hbf])

        def loads(ci, slot):
            c = order[ci]
            self.dma(Fb[slot][:], self.Fb_d[c], [self.Fb_d], [Fb[slot]])
            self.dma(Ff[slot][:], self.Ff_d[c], [self.Ff_d], [Ff[slot]])
        loads(0, 0)
        it = 0
        for ci, c in enumerate(order):
            fb = Fb[it % 2]; ff = Ff[it % 2]; ypt = yp[it % 2]
            it += 1
            tok = slice(c * 128, (c + 1) * 128)
            if ci + 1 < len(order):
                loads(ci + 1, it % 2)
            xs_g = fb[:, g * 512:(g + 1) * 512]
            self.tt(dtx[:], ff[:, 256 + g * 8:264 + g * 8], dtb[:, HS], ALU.add, [ff, dtb], [dtx])
            self.tt(scm[:], ff[:, g * 128:(g + 1) * 128], cst[:, Mo:Mo + 128], ALU.mult, [ff, cst], [scm])
            yield
            self.act(dtx[:], dtx[:], AF.Exp, [dtx], [dtx])
            self.act(dt[:], dtx[:], AF.Ln, [dtx], [dt], bias=1.0)
            yield
            self.tt(la[:], dt[:], al[:, HS], ALU.mult, [dt, al], [la])
            self.cp(lah[:], la[:], [la], [lah])
            self.cp(lah32[:], lah[:], [lah], [lah32])
            self.tt(lal32[:], la[:], lah32[:], ALU.subtract, [la, lah32], [lal32])
            self.cp(lalo[:], lal32[:], [lal32], [lalo])
            yield
            self.mm(Q[1][:, 0:8], cst[:, Uo:Uo + 128], la[:], True, True, [cst, la], [Q[1]])
            self.mm(Q[1][:, 8:16], cst[:, Lo:Lo + 128], la[:], True, True, [cst, la], [Q[1]])
            self.mm(Q[1][:, 16:24], self.ones[:], la[:], True, True, [self.ones, la], [Q[1]])
            self.tt(LaUh[:], lah[:].unsqueeze(2).to_broadcast([128, 8, 128]),
                    Ub[:].unsqueeze(1).to_broadcast([128, 8, 128]), ALU.mult, [lah, Ub], [LaUh], eng=POOL)
            self.tt(LaUl[:], lalo[:].unsqueeze(2).to_broadcast([128, 8, 128]),
                    Ub[:].unsqueeze(1).to_broadcast([128, 8, 128]), ALU.mult, [lalo, Ub], [LaUl])
            self.tt(v[:].rearrange("p (h e) -> p h e", h=8), xs_g.rearrange("p (h e) -> p h e", h=8),
                    dt[:].unsqueeze(2).to_broadcast([128, 8, 64]), ALU.mult, [fb, dt], [v], eng=POOL)
            yield
            self.act(E3[:], Q[1][:, 0:24], AF.Exp, [Q[1]], [E3])
            yield
            for q in range(2):
                self.mm(Q[q][:], Lb[:], LaUh[:, 4 * q:4 * q + 4, :].rearrange("p a b -> p (a b)"), True, False, [Lb, LaUh], [Q[q]])
                self.mm(Q[q][:], Lb[:], LaUl[:, 4 * q:4 * q + 4, :].rearrange("p a b -> p (a b)"), False, True, [Lb, LaUl], [Q[q]])
            yield
            self.tt(vte[:].rearrange("p (h e) -> p h e", h=8), v[:].rearrange("p (h e) -> p h e", h=8),
                    E3[:, 8:16].unsqueeze(2).to_broadcast([128, 8, 64]), ALU.mult, [v, E3], [vte], eng=POOL)
            for q in range(2):
                self.act(esb[:, q * 512:(q + 1) * 512], Q[q][:], AF.Exp, [Q[q]], [esb])
            yield
            self.tt(M[:], esb[:].rearrange("p (a b) -> p a b", a=8),
                    scm[:].unsqueeze(1).to_broadcast([128, 8, 128]), ALU.mult, [esb, scm], [M])
            yield
            self.mm(Q[0][:], fb[:, 1280 + g * 128:1280 + (g + 1) * 128], Hbf[:], True, True, [fb, Hbf], [Q[0]])
            for hh in range(8):
                self.mm(Q[1][:, hh * 64:(hh + 1) * 64], M[:, hh, :], v[:, hh * 64:(hh + 1) * 64], True, True, [M, v], [Q[1]])
            yield
            self.tt(t1[:].rearrange("p (h e) -> p h e", h=8), Q[0][:].rearrange("p (h e) -> p h e", h=8),
                    E3[:, 0:8].unsqueeze(2).to_broadcast([128, 8, 64]), ALU.mult, [Q[0], E3], [t1])
            yield
            self.tt(t2[:], Q[1][:], t1[:], ALU.add, [Q[1], t1], [t2])
            self.mm(Q[0][:], fb[:, 1024 + g * 128:1024 + (g + 1) * 128], vte[:], True, True, [fb, vte], [Q[0]])
            yield
            if d_ == 0:
                self.tt(t1[:].rearrange("p (h e) -> p h e", h=8), xs_g.rearrange("p (h e) -> p h e", h=8),
                        Dsk[:, HS].unsqueeze(2).to_broadcast([128, 8, 64]), ALU.mult, [fb, Dsk], [t1], eng=POOL)
                self.tt(ypt[:], t1[:], t2[:], ALU.add, [t1, t2], [ypt], eng=POOL)
            else:
                self.cp(ypt[:], t2[:], [t2], [ypt], eng=POOL)
            self.dma(ydst[tok, g * 512:(g + 1) * 512], ypt[:], [ypt], [self.yb[(d_, g)]])
            self.tt(H[:].rearrange("p (h e) -> p h e", h=8), H[:].rearrange("p (h e) -> p h e", h=8),
                    E3[:, 16:24].unsqueeze(2).to_broadcast([128, 8, 64]), ALU.mult, [H, E3], [H])
            yield
            self.tt(H[:], H[:], Q[0][:], ALU.add, [H, Q[0]], [H])
            yield
            self.act(Hbf[:], H[:], AF.Copy, [H], [Hbf])
            yield

    def stage_R(self, l):
        I = self.I
        cst = self.cst
        import os as _os
        with ExitStack() as st:
            wv = I["w_in"][l].rearrange("(k p) c -> p k c", p=128)
            wqk = self.sb(st, [128, 8, 1024], BF16, "wqk")
            wvv = self.sb(st, [128, 8, 512], BF16, "wvr")
            for kc in range(8):
                self.load_w(wqk[:, kc, 0:512], wqk, wv[:, kc, 3760:4272], I["w_in"])
                self.load_w(wqk[:, kc, 512:1024], wqk, wv[:, kc, 5328:5840], I["w_in"])
                self.load_w(wvv[:, kc, :], wvv, wv[:, kc, 4272:4784], I["w_in"])
            prm = {}
            for d_, sfx in ((0, "f"), (1, "b")):
                nm = "ret_log_rate_" + sfx
                lgB = self.bvec(st, nm, l, 4)
                self.act(lgB[:], lgB[:], AF.Exp, [lgB], [lgB])
                self.ts(lgB[:], lgB[:], -1.0, None, ALU.mult, None, [lgB], [lgB])
                lgs = self.sb(st, [128, 2], F32, "lgs")
                src = I[nm][l:l + 1, :].rearrange("o (p two) -> o p two", two=2)
                self.dma(lgs[0:64, :], src[:, :, 0].to_broadcast([64, 2]), [I[nm]], [lgs], slow=True)
                self.dma(lgs[64:128, :], src[:, :, 1].to_broadcast([64, 2]), [I[nm]], [lgs], slow=True)
                self.act(lgs[:], lgs[:], AF.Exp, [lgs], [lgs])
                self.ts(lgs[:], lgs[:], -1.0, None, ALU.mult, None, [lgs], [lgs])
                RIo = C_RIF if d_ == 0 else C_RIB
                Mo = C_MF if d_ == 0 else C_MB
                Go = C_GF if d_ == 0 else C_GB
                To = C_TEF if d_ == 0 else C_TEB
                DmT = self.sb(st, [128, 4, 128], F32, "DmT")
                for h in range(4):
                    self.act(DmT[:, h, :], cst[:, RIo:RIo + 128], AF.Exp, [cst, lgB], [DmT], scale=lgB[:, h:h + 1])
                self.tt(DmT[:], DmT[:], cst[:, Mo:Mo + 128].unsqueeze(1).to_broadcast([128, 4, 128]), ALU.mult, [DmT, cst], [DmT])
                Gam = self.sb(st, [128, 2, 128], F32, "Gam")
                for p in range(2):
                    self.act(Gam[:, p, :], cst[:, Go:Go + 128], AF.Exp, [cst, lgs], [Gam], scale=lgs[:, p:p + 1])
                te = self.sb(st, [128, 4], F32, "te")
                self.act(te[:], lgB[:], AF.Exp, [lgB, cst], [te], scale=cst[:, To:To + 1])
                g128 = self.sb(st, [128, 2], F32, "g128")
                self.act(g128[:], lgs[:], AF.Exp, [lgs], [g128], scale=128.0)
                prm[d_] = (DmT, Gam, te, g128)
            with ExitStack() as stf:
                sets = []
                for s_ in range(2):
                    B = {}
                    B["hc"] = self.sb(stf, [128, 8, 128], BF16, "hcrf")
                    B["tab"] = self.sb(stf, [128, 1024], F32, "tabrf")
                    B["r1"] = self.sb(stf, [128, 512], F32, "r1")
                    B["r2"] = self.sb(stf, [128, 512], F32, "r2")
                    B["qkt"] = self.sb(stf, [128, 512], BF16, "qkt")
                    B["Rb"] = self.sb(stf, [128, 1280], BF16, "Rbw")
                    B["Q"] = [self.ps(stf, [128, 512], F32, "QRF") for _ in range(4)]
                    sets.append(B)
                gens = [(lambda cc: (lambda slot: self.ret_front(cc, sets[slot], wqk, wvv)))(c) for c in range(NCH)]
                self.interleave(gens, 2)
            self.P.barrier()
            with ExitStack() as st2:
                gens = []
                for d_ in (0, 1):
                    gens.append((lambda dd: (lambda slot: self.ret_sweep(l, dd, st2, prm[dd])))(d_))
                self.interleave(gens, 2)
            self.P.barrier()
            with ExitStack() as st3:
                wg = self.sb(st3, [128, 8, 512], BF16, "wg")
                for kc in range(8):
                    self.load_w(wg[:, kc, :], wg, wv[:, kc, 4784:5296], I["w_in"])
                sets = []
                for s in range(2):
                    B = {}
                    B["hc"] = self.sb(st3, [128, 8, 128], BF16, "hcr3")
                    B["rpf"] = self.sb(st3, [128, 512], F32, "rpf")
                    B["rpb"] = self.sb(st3, [128, 512], F32, "rpb")
                    B["ge"] = self.sb(st3, [128, 512], F32, "ge3")
                    B["sg"] = self.sb(st3, [128, 512], F32, "sg3")
                    B["stats"] = self.sb(st3, [128, 4, 6], F32, "stats3")
                    B["mv"] = self.sb(st3, [128, 4, 2], F32, "mv3")
                    B["yn"] = self.sb(st3, [128, 512], F32, "yn3")
                    B["ob"] = self.sb(st3, [128, 512], BF16, "obr3")
                    B["oT"] = self.sb(st3, [128, 4, 128], BF16, "oTr3")
                    B["PG"] = self.ps(st3, [128, 512], F32, "PGr3")
                    B["PT"] = self.ps(st3, [128, 1024], BF16, "PTr3")
                    sets.append(B)
                gens = [(lambda cc: (lambda slot: self.ret_final(cc, sets[slot], wg)))(c) for c in range(NCH)]
                self.interleave(gens, 2)

    def ret_final(self, c, B, wg):
        h_ = B["hc"]; rpf = B["rpf"]; rpb = B["rpb"]; ge = B["ge"]; sg = B["sg"]; stats = B["stats"]; mv = B["mv"]
        yn = B["yn"]; ob = B["ob"]; oT_ = B["oT"]; PG = B["PG"]; PT = B["PT"]
        c0 = colof(c * 128)
        tok = slice(c * 128, (c + 1) * 128)
        self.dma(h_[:], self.hv[:, :, c0:c0 + 128], [self.hT_d], [h_], slow=True)
        self.dma(rpf[:], self.rpart_d[tok, :], [self.rpart_d], [rpf])
        self.dma(rpb[:], self.rpartb_d[tok, :], [self.rpartb_d], [rpb])
        yield
        for kc in range(8):
            self.mm(PG[:], h_[:, kc, :], wg[:, kc, :], kc == 0, kc == 7, [h_, wg], [PG])
        self.tt(rpf[:], rpf[:], rpb[:], ALU.add, [rpf, rpb], [rpf])
        yield
        self.act(ge[:], PG[:], AF.Exp, [PG], [ge], scale=-1.0)
        for h in range(4):
            self.P.op(DVE, (lambda hh: (lambda e: e.bn_stats(out=stats[:, hh, :], in_=rpf[:, hh * 128:(hh + 1) * 128])))(h),
                      [rpf.b], [stats.b])
            self.P.op(DVE, (lambda hh: (lambda e: e.bn_aggr(out=mv[:, hh, :], in_=stats[:, hh, :])))(h),
                      [stats.b], [mv.b])
        self.ts(mv[:, :, 1], mv[:, :, 1], LN_EPS, None, ALU.add, None, [mv], [mv])
        yield
        self.rsqrt(mv[:, :, 1], mv)
        self.sigm(ge[:], ge)
        yield
        self.tt(sg[:], PG[:], ge[:], ALU.mult, [PG, ge], [sg])
        for h in range(4):
            self.ts(yn[:, h * 128:(h + 1) * 128], rpf[:, h * 128:(h + 1) * 128], mv[:, h, 0:1], mv[:, h, 1:2],
                    ALU.subtract, ALU.mult, [rpf, mv], [yn])
        yield
        self.tt(ob[:], yn[:], sg[:], ALU.mult, [yn, sg], [ob])
        yield
        for h in range(4):
            self.tr(PT[:, h * 128:(h + 1) * 128], ob[:, h * 128:(h + 1) * 128], self.identb[:], [ob, self.identb], [PT])
        yield
        self.act(oT_[:], PT[:, 0:512].rearrange("p (a b) -> p a b", a=4), AF.Copy, [PT], [oT_])
        yield
        self.dma(self.mixT_d[c, :, 1536:2048], oT_[:].rearrange("p a b -> p (a b)"), [oT_], [self.mixT_d])
        yield

    def ret_front(self, c, B, wqk, wvv):
        I = self.I
        h_ = B["hc"]; tab = B["tab"]; r1 = B["r1"]; r2 = B["r2"]; qkt = B["qkt"]; Rb = B["Rb"]; Q = B["Q"]
        Qb3 = Q[3][:].bitcast(BF16)
        c0 = colof(c * 128)
        self.dma(h_[:], self.hv[:, :, c0:c0 + 128], [self.hT_d], [h_], slow=True)
        self.dma(tab[:], I["ret_tab2"][c * 128:(c + 1) * 128, :], [I["ret_tab2"]], [tab])
        yield
        for j, (q_, w_, lo) in enumerate(((Q[0], wqk, 0), (Q[1], wqk, 512), (Q[2], wvv, 0))):
            for kc in range(8):
                self.mm(q_[:], h_[:, kc, :], w_[:, kc, lo:lo + 512], kc == 0, kc == 7, [h_, w_], [q_])
            yield
        self.tt(r1[:], Q[0][:], tab[:, 0:512], ALU.mult, [Q[0], tab], [r1])
        yield
        self.tt(r2[:], Q[1][:], tab[:, 512:1024], ALU.mult, [Q[1], tab], [r2])
        self.act(Rb[:, 768:1280], Q[2][:], AF.Copy, [Q[2]], [Rb])
        yield
        self.tt(qkt[:], r1[:], r2[:], ALU.add, [r1, r2], [qkt], eng=POOL)
        yield
        for j in range(4):
            self.tr(Qb3[:, j * 128:(j + 1) * 128], qkt[:, j * 128:(j + 1) * 128], self.identb[:], [qkt, self.identb], [Q[3]])
        self.cp(Rb[:, 512:768], qkt[:, 256:512], [qkt], [Rb], eng=POOL)
        yield
        self.act(Rb[:, 0:512], Qb3[:, 0:512], AF.Copy, [Q[3]], [Rb])
        yield
        self.dma(self.Rb_d[c], Rb[:], [Rb], [self.Rb_d])
        yield

    def ret_sweep(self, l, d_, st, prm):
        DmT, Gam, te, g128 = prm
        S = self.sb(st, [128, 2, 128], F32, "S")
        Sbf = self.sb(st, [128, 2, 128], BF16, "Sbf")
        Rb = [self.sb(st, [128, 1280], BF16, "Rbr") for _ in range(2)]
        qz = [self.sb(st, [128, 2, 128], BF16, "qz") for _ in range(2)]
        qdz = [self.sb(st, [128, 2, 128], BF16, "qdz") for _ in range(2)]
        for par in range(2):
            self.memset(qz[par][:], 0.0, [qz[par]])
            self.memset(qdz[par][:], 0.0, [qdz[par]])
        vte = self.sb(st, [128, 512], BF16, "vte")
        Mr = self.sb(st, [128, 4, 128], BF16, "Mr")
        yp = [self.sb(st, [128, 512], F32, "ypr") for _ in range(2)]
        Q = [self.ps(st, [128, 512], F32, "QR") for _ in range(3)]
        ydst = self.rpart_d if d_ == 0 else self.rpartb_d
        order = list(range(NCH)) if d_ == 0 else [1, 0] + list(range(NCH - 1, 1, -1))
        self.memset(S[:], 0.0, [S])
        self.memset(Sbf[:], 0.0, [Sbf])
        self.dma(Rb[0][:], self.Rb_d[order[0]], [self.Rb_d], [Rb[0]])
        it = 0
        for ci, c in enumerate(order):
            rb = Rb[it % 2]; ypt = yp[it % 2]
            it += 1
            tok = slice(c * 128, (c + 1) * 128)
            if ci + 1 < len(order):
                self.dma(Rb[it % 2][:], self.Rb_d[order[ci + 1]], [self.Rb_d], [Rb[it % 2]])
            qk = rb[:, 0:512].rearrange("p (a b) -> p a b", a=4)
            for par in range(2):
                rr = 64 * par
                self.cp(qz[par][rr:rr + 64, :, :], qk[rr:rr + 64, 0:2, :], [rb], [qz[par]], eng=POOL)
                self.tt(qdz[par][rr:rr + 64, :, :], qk[rr:rr + 64, 0:2, :], Gam[rr:rr + 64, :, :], ALU.mult,
                        [rb, Gam], [qdz[par]], eng=POOL)
            self.tt(vte[:].rearrange("p (h e) -> p h e", h=4), rb[:, 768:1280].rearrange("p (h e) -> p h e", h=4),
                    te[:].unsqueeze(2).to_broadcast([128, 4, 128]), ALU.mult, [rb, te], [vte], eng=POOL)
            yield
            for h in range(4):
                p = h // 2
                self.mm(Q[0][:, h * 128:(h + 1) * 128], qk[:, 2 + p, :], qz[h % 2][:, p, :], True, True, [rb, qz[h % 2]], [Q[0]])
            yield
            self.tt(Mr[:], Q[0][:].rearrange("p (a b) -> p a b", a=4), DmT[:], ALU.mult, [Q[0], DmT], [Mr])
            yield
            for h in range(4):
                p = h // 2
                self.mm(Q[1][:, h * 128:(h + 1) * 128], Mr[:, h, :], rb[:, 768 + h * 128:768 + (h + 1) * 128], True, False, [Mr, rb], [Q[1]])
                self.mm(Q[1][:, h * 128:(h + 1) * 128], qdz[h % 2][:, p, :], Sbf[:, p, :], False, True, [qdz[h % 2], Sbf], [Q[1]])
            for h in range(4):
                p = h // 2
                self.mm(Q[2][:, h * 128:(h + 1) * 128], rb[:, 512 + p * 128:512 + (p + 1) * 128], vte[:, h * 128:(h + 1) * 128],
                        True, True, [rb, vte], [Q[2]])
            yield
            self.cp(ypt[:], Q[1][:], [Q[1]], [ypt])
            self.dma(ydst[tok, :], ypt[:], [ypt], [ydst])
            for h in range(4):
                p, r0 = h // 2, (h % 2) * 64
                self.stt(S[r0:r0 + 64, p, :], S[r0:r0 + 64, p, :], g128[r0:r0 + 64, p:p + 1], Q[2][r0:r0 + 64, h * 128:(h + 1) * 128],
                         ALU.mult, ALU.add, [S, g128, Q[2]], [S])
            yield
            self.act(Sbf[:], S[:], AF.Copy, [S], [Sbf])
            yield

    def stage_M(self, l):
        I = self.I
        cst = self.cst
        blocks = [(0, 256)] + [(256 + 512 * i, 512) for i in range(8)]
        with ExitStack() as st1:
            v_all = self.sb(st1, [128, NCH, 512], BF16, "v_all")
            with ExitStack() as st:
                wv = I["w_in"][l].rearrange("(k p) c -> p k c", p=128)
                wm = self.sb(st, [128, 8, 704], BF16, "wm")
                wgate = self.sb(st, [128, 8, 512], BF16, "wgate")
                self.load_w(wm[:, :, 0:672], wm, wv[:, :, 2576:3248], I["w_in"])
                self.load_w(wm[:, :, 672:704], wm, wv[:, :, 5296:5328], I["w_in"])
                self.load_w(wgate[:], wgate, wv[:, :, 3248:3760], I["w_in"])
                wuq = self.sb(st, [128, 3, 8, 96], BF16, "wuq")
                wuqs = self.sb(st, [128, 3, 8, 96], BF16, "wuqs")
                wkp = self.sb(st, [128, 2, 8, 96], BF16, "wkp")
                wvv = self.sb(st, [128, 2, 8, 64], BF16, "wvv")
                self.memset(wuqs[:], 0.0, [wuqs])
                self.memset(wkp[:], 0.0, [wkp])
                uqv = I["mla_w_uq"][l].rearrange("(k p) (h e) -> p k h e", p=128, h=8)
                uqs = I["w_uq_sw"][l].rearrange("(k p) (h e) -> p k h e", p=128, h=8)
                ukv = I["mla_w_ukv"][l].rearrange("(k p) (h e) -> p k h e", p=128, h=8)
                for kc in range(3):
                    self.load_w(wuq[:, kc, :, :], wuq, uqv[:, kc, :, :], I["mla_w_uq"])
                    self.load_w(wuqs[:, kc, :, 64:96], wuqs, uqs[:, kc, :, :], I["w_uq_sw"])
                for kc in range(2):
                    self.load_w(wkp[:, kc, :, 0:64], wkp, ukv[:, kc, :, 0:64], I["mla_w_ukv"])
                    self.load_w(wvv[:, kc, :, :], wvv, ukv[:, kc, :, 64:128], I["mla_w_ukv"])
                esel = self.sb(st, [32, 96], BF16, "esel")
                self.memset(esel[:], 0.0, [esel])
                self.cp(esel[:, 64:96], self.identb[0:32, 0:32], [self.identb, esel], [esel])
                qn = self.sb(st, [128, 3], F32, "qn")
                kvn = self.sb(st, [128, 2], F32, "kvn")
                self.dma(qn[:], I["mla_q_norm"][l].rearrange("(k p) -> p k", p=128), [I["mla_q_norm"]], [qn], slow=True)
                self.dma(kvn[:], I["mla_kv_norm"][l].rearrange("(k p) -> p k", p=128), [I["mla_kv_norm"]], [kvn], slow=True)
                hb = [self.sb(st, [128, 8, 512], BF16, "hb") for _ in range(2)]
                tq = self.sb(st, [96, 2, 512], F32, "tq")
                self.memset(tq[0:64, 0, :], 1.0, [tq])
                self.memset(tq[0:64, 1, :], 0.0, [tq])
                tk = self.sb(st, [32, 2, 512], F32, "tk")
                cqs = self.sb(st, [128, 3, 512], F32, "cqs")
                sqs2 = [self.sb(st, [128, 512], F32, "sqs") for _ in range(2)]
                rstd = self.sb(st, [128, 512], F32, "rstd")
                cqn = self.sb(st, [128, 3, 512], BF16, "cqn")
                ckvn = self.sb(st, [128, 2, 512], BF16, "ckvn")
                kr1 = self.sb(st, [32, 512], F32, "kr1")
                kr2 = self.sb(st, [32, 512], F32, "kr2")
                krr = self.sb(st, [32, 512], BF16, "krr")
                q1s = [self.sb(st, [96, 512], F32, "q1") for _ in range(2)]
                q2s = [self.sb(st, [96, 512], F32, "q2") for _ in range(2)]
                qf = [self.sb(st, [96, 512], BF16, "qf") for _ in range(2)]
                kf = [self.sb(st, [96, 512], BF16, "kf") for _ in range(2)]
                ges = [self.sb(st, [128, 512], F32, "ge") for _ in range(2)]
                sgo = [self.sb(st, [128, 512], F32, "sgo") for _ in range(2)]
                P0 = [self.ps(st, [128, 512], F32, "P0") for _ in range(2)]
                PSS = self.ps(st, [128, 512], F32, "PSS")
                PQ1 = self.ps(st, [128, 512], F32, "PQ1")
                PQ2 = self.ps(st, [128, 512], F32, "PQ2")
                PK = self.ps(st, [128, 512], F32, "PK")
                PVv = self.ps(st, [128, 512], F32, "PVv")
                PG = self.ps(st, [128, 512], F32, "PG")
                sgv = self.sgT_d[:].rearrange("(r p) t -> p r t", p=128)
                ctr = 0
                for bi, (t0, n) in enumerate(blocks):
                    h_ = hb[bi % 2]
                    c0 = colof(t0)
                    self.dma(h_[:, :, 0:n], self.hv[:, :, c0:c0 + n], [self.hT_d], [h_], slow=True)
                    self.dma(tq[64:96, :, 0:n], I["mla_tab"][:, :, t0:t0 + n], [I["mla_tab"]], [tq], slow=True)
                    self.dma(tk[:, :, 0:n], I["mla_tab"][:, :, t0:t0 + n], [I["mla_tab"]], [tk], slow=True)
                    for (nrc, off, dst, nrm, dim, keep) in ((3, 0, cqn, qn, 384.0, None), (2, 384, ckvn, kvn, 256.0, None)):
                        for rc in range(nrc):
                            p_ = P0[ctr % 2]; ctr += 1
                            for kc in range(8):
                                self.mm(p_[:, 0:n], wm[:, kc, off + rc * 128:off + (rc + 1) * 128], h_[:, kc, 0:n], kc == 0, kc == 7, [wm, h_], [p_])
                            sqs = sqs2[ctr % 2]
                            self.act(cqs[:, rc, 0:n], p_[:, 0:n], AF.Copy, [p_], [cqs])
                            self.act(sqs[:, 0:n], p_[:, 0:n], AF.Square, [p_], [sqs])
                            self.mm(PSS[:, 0:n], self.ones[:], sqs[:, 0:n], rc == 0, rc == nrc - 1, [self.ones, sqs], [PSS])
                        self.ts(rstd[:, 0:n], PSS[:, 0:n], 1.0 / dim, RMS_EPS, ALU.mult, ALU.add, [PSS], [rstd])
                        self.rsqrt(rstd[:, 0:n], rstd)
                        for rc in range(nrc):
                            self.stt(dst[:, rc, 0:n], cqs[:, rc, 0:n], nrm[:, rc:rc + 1], rstd[:, 0:n], ALU.mult, ALU.mult, [cqs, nrm, rstd], [dst])
                    self.mm_group_kr(h_, n, wm, PQ1, PQ2)
                    self.tt(kr1[:, 0:n], PQ1[0:32, 0:n], tk[:, 0, 0:n], ALU.mult, [PQ1, tk], [kr1])
                    self.tt(kr2[:, 0:n], PQ2[0:32, 0:n], tk[:, 1, 0:n], ALU.mult, [PQ2, tk], [kr2])
                    self.tt(krr[:, 0:n], kr1[:, 0:n], kr2[:, 0:n], ALU.add, [kr1, kr2], [krr])
                    for h in range(8):
                        qf_ = qf[h % 2]; kf_ = kf[h % 2]
                        q1 = q1s[h % 2]; q2 = q2s[h % 2]
                        PQ1_, PQ2_, PK_ = (PQ1, PQ2, PK) if h % 2 == 0 else (P0[0], P0[1], PG)
                        for kc in range(3):
                            self.mm(PQ1_[0:96, 0:n], wuq[:, kc, h, :], cqn[:, kc, 0:n], kc == 0, kc == 2, [wuq, cqn], [PQ1_])
                        for kc in range(3):
                            self.mm(PQ2_[0:96, 0:n], wuqs[:, kc, h, :], cqn[:, kc, 0:n], kc == 0, kc == 2, [wuqs, cqn], [PQ2_])
                        self.tt(q1[:, 0:n], PQ1_[0:96, 0:n], tq[:, 0, 0:n], ALU.mult, [PQ1_, tq], [q1])
                        self.tt(q2[:, 0:n], PQ2_[0:96, 0:n], tq[:, 1, 0:n], ALU.mult, [PQ2_, tq], [q2])
                        self.tt(qf_[:, 0:n], q1[:, 0:n], q2[:, 0:n], ALU.add, [q1, q2], [qf_], eng=POOL)
                        self.dma(self.qT_d[h, :, t0:t0 + n], qf_[:, 0:n], [qf_], [self.qT_d])
                        for kc in range(2):
                            self.mm(PK_[0:96, 0:n], wkp[:, kc, h, :], ckvn[:, kc, 0:n], kc == 0, False, [wkp, ckvn], [PK_])
                        self.mm(PK_[0:96, 0:n], esel[:], krr[:, 0:n], False, True, [esel, krr], [PK_])
                        self.act(kf_[:, 0:n], PK_[0:96, 0:n], AF.Copy, [PK_], [kf_])
                        self.dma(self.kfT_d[h, :, t0:t0 + n], kf_[:, 0:n], [kf_], [self.kfT_d])
                    for s in range(n // 128):
                        ch = (t0 + s * 128) // 128
                        PV_ = PVv if s % 2 == 0 else PSS
                        for kc in range(2):
                            self.mm(PV_[:], ckvn[:, kc, s * 128:(s + 1) * 128], wvv[:, kc, :, :].rearrange("p h e -> p (h e)"),
                                    kc == 0, kc == 1, [ckvn, wvv], [PV_])
                        self.act(v_all[:, ch, :], PV_[:], AF.Copy, [PV_], [v_all])
                    for rc in range(4):
                        sg_ = sgo[rc % 2]
                        ge = ges[rc % 2]
                        PG_ = PG if rc % 2 == 0 else PK
                        for kc in range(8):
                            self.mm(PG_[:, 0:n], wgate[:, kc, rc * 128:(rc + 1) * 128], h_[:, kc, 0:n], kc == 0, kc == 7, [wgate, h_], [PG_])
                        self.act(ge[:, 0:n], PG_[:, 0:n], AF.Exp, [PG_], [ge], scale=-1.0)
                        self.sigm(ge[:, 0:n], ge)
                        self.tt(sg_[:, 0:n], PG_[:, 0:n], ge[:, 0:n], ALU.mult, [PG_, ge], [sg_])
                        self.dma(sgv[:, rc, t0:t0 + n], sg_[:, 0:n], [sg_], [self.sgT_d])
            self.P.barrier()
            import os as _os
            if _os.environ.get("NO_M2"):
                return
            with ExitStack() as st:
                kfh = [self.sb(st, [96, NTOK], BF16, "kfh") for _ in range(2)]
                qh = [self.sb(st, [96, NTOK], BF16, "qh") for _ in range(2)]
                vaug = [self.sb(st, [128, NCH, 128], BF16, "vaug") for _ in range(2)]
                self.memset(vaug[0][:, :, 64:128], 1.0, [vaug[0]])
                self.memset(vaug[1][:, :, 0:64], 1.0, [vaug[1]])
                pT = [self.sb(st, [128, 1024], BF16, "pT") for _ in range(3)]
                sgh = [self.sb(st, [128, 512], F32, "sgh") for _ in range(2)]
                rden = self.sb(st, [128, 512], F32, "rden")
                ot = self.sb(st, [128, 512], F32, "ot")
                ob = [self.sb(st, [128, 512], BF16, "ob") for _ in range(2)]
                PSc = [self.ps(st, [128, 1024], F32, "PSc") for _ in range(3)]
                PO = [self.ps(st, [128, 512], F32, "PO") for _ in range(2)]
                ci = 0
                bi_ = 0
                for h in range(int(_os.environ.get("M2_HEADS", "8"))):
                    par = h % 2
                    r0 = 64 * par
                    d0 = 64 - r0
                    kf_ = kfh[h % 2]; q_ = qh[h % 2]; va = vaug[par]
                    self.dma(kf_[:], self.kfT_d[h], [self.kfT_d], [kf_])
                    self.dma(q_[:], self.qT_d[h], [self.qT_d], [q_])
                    self.cp(va[:, :, r0:r0 + 64], v_all[:, :, h * 64:(h + 1) * 64], [v_all], [va], eng=POOL)
                    its = []
                    for (t0, n) in blocks:
                        if t0 == 0:
                            if l == DEPTH - 1:
                                continue
                            kcs = [0, 1]
                        else:
                            kcs = list(range(NCH))
                        po = PO[bi_ % 2]; sg_ = sgh[bi_ % 2]; ob_ = ob[bi_ % 2]
                        bi_ += 1
                        npair = len(kcs) // 2
                        for i in range(npair):
                            its.append((t0, n, kcs[2 * i], kcs[2 * i + 1], i == 0, i == npair - 1, po, sg_, ob_))
                    LOOK = 2
                    for j in range(len(its) + LOOK):
                        if j < len(its):
                            (t0, n, ka, kb, first, last, po, sg_, ob_) = its[j]
                            if first:
                                self.dma(sg_[r0:r0 + 64, 0:n], self.sgT_d[64 * h:64 * (h + 1), t0:t0 + n], [self.sgT_d], [sg_])
                            psc = PSc[(ci + j) % 3]
                            self.mm(psc[:, 0:n], kf_[:, ka * 128:(ka + 1) * 128], q_[:, t0:t0 + n], True, True, [kf_, q_], [psc])
                            self.mm(psc[:, 512:512 + n], kf_[:, kb * 128:(kb + 1) * 128], q_[:, t0:t0 + n], True, True, [kf_, q_], [psc])
                        jj = j - LOOK
                        if jj >= 0:
                            (t0, n, ka, kb, first, last, po, sg_, ob_) = its[jj]
                            psc = PSc[(ci + jj) % 3]; pt = pT[(ci + jj) % 3]
                            self.act(pt[:].rearrange("p (a b) -> p a b", a=2)[:, :, 0:n], psc[:].rearrange("p (a b) -> p a b", a=2)[:, :, 0:n],
                                     AF.Exp, [psc], [pt], scale=MLA_SCALE)
                            self.mm(po[:, 0:n], va[:, ka, :], pt[:, 0:n], first, False, [va, pt], [po])
                            self.mm(po[:, 0:n], va[:, kb, :], pt[:, 512:512 + n], False, last, [va, pt], [po])
                            if last:
                                self.P.op(DVE, (lambda a_, b_: (lambda e: e.reciprocal(out=a_, in_=b_)))(rden[d0:d0 + 64, 0:n], po[d0:d0 + 64, 0:n]),
                                          [po.b], [rden.b])
                                self.tt(ot[r0:r0 + 64, 0:n], po[r0:r0 + 64, 0:n], rden[d0:d0 + 64, 0:n], ALU.mult, [po, rden], [ot])
                                self.tt(ob_[r0:r0 + 64, 0:n], ot[r0:r0 + 64, 0:n], sg_[r0:r0 + 64, 0:n], ALU.mult, [ot, sg_], [ob_], eng=POOL)
                                kcm = 8 + h // 2
                                self.dma(self.mixT_d[t0 // 128:(t0 + n) // 128, r0:r0 + 64, kcm * 128:(kcm + 1) * 128].rearrange("c p t -> p c t"),
                                         ob_[r0:r0 + 64, 0:n].rearrange("p (c t) -> p c t", t=128), [ob_], [self.mixT_d], slow=True)
                    ci += len(its)

    def mm_group_kr(self, h_, n, wm, PQ1, PQ2):
        for kc in range(8):
            self.mm(PQ1[0:32, 0:n], wm[:, kc, 640:672], h_[:, kc, 0:n], kc == 0, kc == 7, [wm, h_], [PQ1])
        for kc in range(8):
            self.mm(PQ2[0:32, 0:n], wm[:, kc, 672:704], h_[:, kc, 0:n], kc == 0, kc == 7, [wm, h_], [PQ2])

    def stage_E(self, l):
        I = self.I
        with ExitStack() as st:
            wo = self.sb(st, [128, 16, 1024], BF16, "wo")
            wov = I["w_out"][l].rearrange("(k p) c -> p k c", p=128)
            for kc in range(16):
                self.load_w(wo[:, kc, :], wo, wov[:, kc, :], I["w_out"])
            lng = self.bvec(st, "ln_g", l, 1024)
            lnb = self.bvec(st, "ln_b", l, 1024)
            sets = []
            for s_ in range(3):
                B = {}
                B["mt"] = self.sb(st, [128, 16, 128], BF16, "mt")
                B["xt"] = self.sb(st, [128, D], F32, "xt")
                B["v1"] = self.sb(st, [128, D], F32, "v1")
                B["v2"] = self.sb(st, [128, D], F32, "v2")
                B["xo"] = self.sb(st, [128, D], F32, "xo")
                B["stats"] = self.sb(st, [128, 2, 6], F32, "stats")
                B["mv"] = self.sb(st, [128, 2], F32, "mv")
                B["PZ"] = [self.ps(st, [128, 512], F32, "PZ") for _ in range(2)]
                sets.append(B)
            tiles = list(range(NCH)) if l < DEPTH - 1 else list(range(2, NCH))
            gens = [(lambda tt_: (lambda slot: self.e_tile(l, tt_, sets[slot], wo, lng, lnb)))(t) for t in tiles]
            self.interleave(gens, 3)

    def e_tile(self, l, t, B, wo, lng, lnb):
        I = self.I
        m_ = B["mt"]; x_ = B["xt"]; v1 = B["v1"]; v2 = B["v2"]; xo_ = B["xo"]; stats = B["stats"]; mv = B["mv"]; PZ = B["PZ"]
        typ = 1 if t < 2 else 0
        tok = slice(t * 128, (t + 1) * 128)
        self.dma(m_[:].rearrange("p a b -> p (a b)"), self.mixT_d[t], [self.mixT_d], [m_])
        if l == 0:
            src_t = I["ctx"] if t < 2 else I["x"]
            src = src_t[t * 128:(t + 1) * 128, :] if t < 2 else src_t[(t - 2) * 128:(t - 1) * 128, :]
        else:
            src_t = self.xres_d
            src = src_t[tok, :]
        self.dma(x_[:], src, [src_t], [x_])
        yield
        for nb in range(2):
            for kc in range(16):
                self.mm(PZ[nb][:], m_[:, kc, :], wo[:, kc, nb * 512:(nb + 1) * 512], kc == 0, kc == 15, [m_, wo], [PZ[nb]])
            yield
        for nb in range(2):
            self.tt(v1[:, nb * 512:(nb + 1) * 512], PZ[nb][:], self.gB[:, typ, nb * 512:(nb + 1) * 512], ALU.mult, [PZ[nb], self.gB], [v1])
        yield
        self.stt(v2[:], x_[:], ALPHA, v1[:], ALU.mult, ALU.add, [x_, v1], [v2])
        yield
        for s in range(2):
            self.P.op(DVE, (lambda ss: (lambda e: e.bn_stats(out=stats[:, ss, :], in_=v2[:, ss * 512:(ss + 1) * 512])))(s),
                      [v2.b], [stats.b])
        self.P.op(DVE, lambda e: e.bn_aggr(out=mv[:], in_=stats[:]), [stats.b], [mv.b])
        self.ts(mv[:, 1:2], mv[:, 1:2], LN_EPS, None, ALU.add, None, [mv], [mv])
        yield
        self.rsqrt(mv[:, 1:2], mv)
        yield
        self.ts(v1[:], v2[:], mv[:, 0:1], mv[:, 1:2], ALU.subtract, ALU.mult, [v2, mv], [v1])
        yield
        self.tt(v2[:], v1[:], lng[:], ALU.mult, [v1, lng], [v2], eng=POOL)
        yield
        self.tt(xo_[:], v2[:], lnb[:], ALU.add, [v2, lnb], [xo_], eng=POOL)
        yield
        if l < DEPTH - 1:
            self.dma(self.xres_d[tok, :], xo_[:], [xo_], [self.xres_d])
        else:
            self.dma(self.out[(t - 2) * 128:(t - 1) * 128, :], xo_[:], [xo_], [self.out])
        yield


C_ID = 0
C_UF = 128
C_LF = 256
C_UB = 384
C_LB = 512
C_MF = 640
C_MB = 768
C_RIF = 896
C_RIB = 1024
C_GF = 1152
C_GB = 1280
C_TEF = 1408
C_TEB = 1409
CST_W = 1410


def make_consts():
    k = np.arange(128)[:, None].astype(np.float32)
    i = np.arange(128)[None, :].astype(np.float32)
    cst = np.zeros((128, CST_W), np.float32)
    cst[:, C_ID:C_ID + 128] = (k == i)
    cst[:, C_UF:C_UF + 128] = (k <= i)
    cst[:, C_LF:C_LF + 128] = (k > i)
    cst[:, C_UB:C_UB + 128] = (k >= i)
    cst[:, C_LB:C_LB + 128] = (k < i)
    cst[:, C_MF:C_MF + 128] = (k <= i)
    cst[:, C_MB:C_MB + 128] = (k >= i)
    cst[:, C_RIF:C_RIF + 128] = np.maximum(i - k, 0)
    cst[:, C_RIB:C_RIB + 128] = np.maximum(k - i, 0)
    cst[:, C_GF:C_GF + 128] = np.broadcast_to(i + 1, (128, 128))
    cst[:, C_GB:C_GB + 128] = np.broadcast_to(128 - i, (128, 128))
    cst[:, C_TEF] = 127 - k[:, 0]
    cst[:, C_TEB] = k[:, 0]
    return cst


def rope_tables():
    rows = SEQ // 64
    t = np.arange(rows * 64)
    row = (t // 64).astype(np.float32)
    col = (t % 64).astype(np.float32)

    def cs(rot):
        nf = rot // 4
        inv = (np.float32(10000.0) ** (-np.arange(nf, dtype=np.float32) / np.float32(nf))).astype(np.float32)
        ang = np.concatenate([row[:, None] * inv, col[:, None] * inv], -1).astype(np.float32)
        return np.cos(ang).astype(np.float32), np.sin(ang).astype(np.float32)
    cm, sm = cs(32)
    mla = np.zeros((32, 2, NTOK), np.float32)
    mla[:, 0, :CTX] = 1.0
    mla[0:16, 0, CTX:] = cm.T; mla[16:32, 0, CTX:] = cm.T
    mla[0:16, 1, CTX:] = -sm.T; mla[16:32, 1, CTX:] = sm.T
    cr, sr = cs(64)
    ret = np.zeros((128, 4, NTOK), np.float32)
    ret[:, 0, :CTX] = 1.0
    for hh in range(2):
        b = hh * 64
        ret[b:b + 32, 0, CTX:] = cr.T; ret[b + 32:b + 64, 0, CTX:] = cr.T
        ret[b:b + 32, 1, CTX:] = -sr.T; ret[b + 32:b + 64, 1, CTX:] = sr.T
    ret[:, 2] = ret[:, 0] * 0.125
    ret[:, 3] = ret[:, 1] * 0.125
    ret2 = np.zeros((NTOK, 1024), np.float32)
    cc = np.ones((NTOK, 4, 64), np.float32)
    ss = np.zeros((NTOK, 4, 64), np.float32)
    cc[CTX:, :, 0:32] = cr[:, None, :]; cc[CTX:, :, 32:64] = cr[:, None, :]
    ss[CTX:, :, 0:32] = -sr[:, None, :]; ss[CTX:, :, 32:64] = sr[:, None, :]
    ret2[:, 0:256] = cc.reshape(NTOK, 256); ret2[:, 256:512] = 0.125 * cc.reshape(NTOK, 256)
    ret2[:, 512:768] = ss.reshape(NTOK, 256); ret2[:, 768:1024] = 0.125 * ss.reshape(NTOK, 256)
    return mla, ret, ret2


_CACHE = {}


def prep_inputs(inputs):
    f = lambda a: np.ascontiguousarray(np.asarray(a, dtype=np.float32))
    w_in = f(inputs["w_in"])
    kr = w_in[:, :, 3216:3248]
    kr_sw = np.concatenate([kr[:, :, 16:32], kr[:, :, 0:16]], -1)

    def sw64(a):
        a = a.reshape(2, D, 4, 2, 32)
        return np.ascontiguousarray(a[:, :, :, ::-1, :]).reshape(2, D, 256)
    q_sw = sw64(w_in[:, :, 3760:4016])
    k_sw = sw64(w_in[:, :, 4016:4272])
    w_ext = np.ascontiguousarray(np.concatenate([w_in, kr_sw, q_sw, k_sw], -1))
    uq = f(inputs["mla_w_uq"]).reshape(2, 384, 8, 96)
    uq_r = uq[:, :, :, 64:96]
    uq_sw = np.ascontiguousarray(np.concatenate([uq_r[..., 16:32], uq_r[..., 0:16]], -1)).reshape(2, 384, 256)
    mla_tab, ret_tab, ret_tab2 = rope_tables()
    shared = {
        "w_ada": f(inputs["w_ada"]), "b_ada": f(inputs["b_ada"]), "w_in": w_ext,
        "ssd_conv_w": f(inputs["ssd_conv_w"]), "ssd_conv_b": f(inputs["ssd_conv_b"]),
        "ssd_norm_w": f(inputs["ssd_norm_w"]), "mla_q_norm": f(inputs["mla_q_norm"]),
        "mla_w_uq": f(inputs["mla_w_uq"]), "w_uq_sw": uq_sw, "mla_kv_norm": f(inputs["mla_kv_norm"]),
        "mla_w_ukv": f(inputs["mla_w_ukv"]), "ret_log_rate_f": f(inputs["ret_log_rate_f"]),
        "ret_log_rate_b": f(inputs["ret_log_rate_b"]), "w_out": f(inputs["w_out"]),
        "ln_g": f(inputs["ln_g"]), "ln_b": f(inputs["ln_b"]),
        "cst": make_consts(), "mla_tab": mla_tab, "ret_tab": ret_tab, "ret_tab2": ret_tab2,
    }
    for n in ("ssd_a_log_f", "ssd_a_log_b", "ssd_dt_bias_f", "ssd_dt_bias_b", "ssd_d"):
        shared[n] = f(inputs[n])
    x = f(inputs["x"]); c = f(inputs["c"]); ctx = f(inputs["ctx"]); c_ctx = f(inputs["c_ctx"])
    maps = []
    for b in range(8):
        m = dict(shared)
        m["x"] = x[b]
        m["ctx"] = ctx[b]
        m["cvec"] = np.ascontiguousarray(np.stack([c[b], c_ctx], 0))
        maps.append(m)
    return maps


def kernel(**inputs):
    if "nc" not in _CACHE:
        _CACHE["nc"] = K().build()
    nc = _CACHE["nc"]
    maps = prep_inputs(inputs)
    res = run_bass_kernel_spmd(nc, maps, core_ids=list(range(8)))
    return np.stack([np.asarray(r["out"], dtype=np.float32) for r in res.results], 0)
```

```python
import math
from contextlib import ExitStack
import numpy as np
import concourse.bass as bass
import concourse.mybir as mybir
from concourse.bass_utils import run_bass_kernel_spmd

F32 = mybir.dt.float32
BF16 = mybir.dt.bfloat16
AF = mybir.ActivationFunctionType
ALU = mybir.AluOpType

PE, ACT, DVE, POOL, SP = "pe", "act", "dve", "pool", "sp"
EPOCH = 30000
DMA_K = 8
DMA_EPOCH = 1800

D = 1024
SEQ = 4096
CTX = 256
NTOK = SEQ + CTX
NCH = NTOK // 128
HC = NTOK + 8
DEPTH = 2
ALPHA = (2 * DEPTH) ** 0.25
LN_EPS = 1e-5
RMS_EPS = 1e-6
MLA_SCALE = 96 ** -0.5
WEXT = 5296 + 32 + 256 + 256


def colof(t):
    return t + 2 if t < CTX else t + 6


class Buf:
    __slots__ = ("name", "lw", "rd", "rdd", "psum")

    def __init__(self, name=""):
        self.name = name
        self.lw = None
        self.rd = {}
        self.rdd = []
        self.psum = False


class Op:
    __slots__ = ("eng", "fn", "deps", "sig", "idx", "dma_slot", "dma_prev")

    def __init__(self, eng, fn):
        self.eng = eng
        self.fn = fn
        self.deps = set()
        self.sig = None
        self.dma_slot = None
        self.dma_prev = None


class Prog:
    def __init__(self, nc):
        self.nc = nc
        self.ops = []
        self.eng = {PE: nc.tensor, ACT: nc.scalar, DVE: nc.vector, POOL: nc.gpsimd, SP: nc.sync}
        self.dma_lists = {}
        self.last = {}

    def op(self, eng, fn, reads=(), writes=(), dma=False):
        o = Op(eng, fn)
        o.idx = len(self.ops)
        for b in reads:
            if b.lw is not None:
                o.deps.add(b.lw)
            if b.psum:
                for e2, r in b.rd.items():
                    if e2 != eng:
                        o.deps.add(r)
        for b in writes:
            if b.lw is not None:
                o.deps.add(b.lw)
            for r in b.rd.values():
                o.deps.add(r)
            for r in b.rdd:
                o.deps.add(r)
        for b in reads:
            if dma:
                b.rdd.append(o.idx)
            else:
                b.rd[eng] = o.idx
        for b in writes:
            b.lw = o.idx
            b.rd = {}
            b.rdd = []
        if dma:
            lst = self.dma_lists.setdefault(eng, [])
            o.dma_slot = len(lst)
            if len(lst) >= DMA_K:
                o.dma_prev = lst[len(lst) - DMA_K]
            lst.append(o.idx)
        o.deps.discard(o.idx)
        self.ops.append(o)
        self.last[eng] = o.idx
        return o

    def barrier(self):
        bufs = {}
        for e in (PE, ACT, DVE, POOL, SP):
            bufs[e] = Buf("bar" + e)
            o = self.op(e, lambda en: en.nop(), writes=[bufs[e]])
            for lst in self.dma_lists.values():
                for d in lst[-DMA_K:]:
                    if d != o.idx:
                        o.deps.add(d)
        for e in (PE, ACT, DVE, POOL, SP):
            self.op(e, lambda en: en.nop(), reads=list(bufs.values()))

    def emit(self, stack):
        nc = self.nc
        ops = self.ops
        needed = set()
        for o in ops:
            for d in o.deps:
                do = ops[d]
                if do.eng == o.eng and o.eng == PE and do.dma_slot is None:
                    continue
                needed.add(d)
            if o.dma_prev is not None:
                needed.add(o.dma_prev)
        cnt = {}
        sems = {}
        dma_sems = {}
        for o in ops:
            if o.dma_slot is not None:
                k = o.dma_slot % DMA_K
                n = o.dma_slot // DMA_K
                key = (o.eng, k, n // DMA_EPOCH)
                if key not in dma_sems:
                    dma_sems[key] = stack.enter_context(nc.semaphore("dq%s%d_%d" % key))
                o.sig = (dma_sems[key], 16 * (n % DMA_EPOCH + 1))
            elif o.idx in needed:
                c = cnt.get(o.eng, 0)
                key = (o.eng, c // EPOCH)
                if key not in sems:
                    sems[key] = stack.enter_context(nc.semaphore("s%s_%d" % key))
                o.sig = (sems[key], c % EPOCH + 1)
                cnt[o.eng] = c + 1
        waited = {}
        nw = 0
        for o in ops:
            e = self.eng[o.eng]
            deps = set(o.deps)
            if o.dma_prev is not None:
                deps.add(o.dma_prev)
            for d in sorted(deps):
                do = ops[d]
                if do.sig is None:
                    continue
                sem, val = do.sig
                key = (o.eng, id(sem))
                if waited.get(key, 0) >= val:
                    continue
                waited[key] = val
                e.wait_ge(sem, val)
                nw += 1
            ins = o.fn(e)
            if o.sig is not None:
                sem, val = o.sig
                ins.then_inc(sem, 16 if o.dma_slot is not None else 1)
        self.nwaits = nw


class T:
    __slots__ = ("t", "b")

    def __init__(self, t, name=""):
        self.t = t
        self.b = Buf(name)

    def __getitem__(self, k):
        return self.t[k]


class K:
    def __init__(self, debug=False, stop_after=None, skip=()):
        self.skip = set(skip)
        self.debug = debug
        self.stop_after = stop_after
        self.nc = bass.Bass("TRN2", target_bir_lowering=False)
        self.P = Prog(self.nc)
        self.uid = 0

    def dram(self, name, shape, dt, kind="Internal"):
        return T(self.nc.dram_tensor(name, list(shape), dt, kind=kind).ap(), name)

    def sb(self, st, shape, dt, name=None):
        self.uid += 1
        name = "%s_%d" % (name or "t", self.uid)
        return T(st.enter_context(self.nc.sbuf_tensor(name, list(shape), dt)), name)

    def ps(self, st, shape, dt=F32, name=None):
        self.uid += 1
        name = "%s_%d" % (name or "p", self.uid)
        t = T(st.enter_context(self.nc.psum_tensor(name, list(shape), dt)), name)
        t.b.psum = True
        return t

    def dma(self, out, in_, reads, writes, eng=SP, slow=False):
        if slow:
            return self.P.op(eng, lambda e: e.dma_start(out=out, in_=in_, allow_slow_non_contiguous=True),
                             [x.b for x in reads], [x.b for x in writes], dma=True)
        return self.P.op(eng, lambda e: e.dma_start(out=out, in_=in_), [x.b for x in reads], [x.b for x in writes], dma=True)

    def mm(self, out, lhsT, rhs, start, stop, reads, writes):
        return self.P.op(PE, lambda e: e.matmul(out, lhsT=lhsT, rhs=rhs, start=start, stop=stop),
                         [x.b for x in reads], [x.b for x in writes])

    def tr(self, out, in_, ident, reads, writes):
        return self.P.op(PE, lambda e: e.transpose(out=out, in_=in_, identity=ident),
                         [x.b for x in reads], [x.b for x in writes])

    def act(self, out, in_, func, reads, writes, bias=None, scale=None, accum_out=None):
        kw = {}
        if bias is not None:
            kw["bias"] = bias
        if scale is not None:
            kw["scale"] = scale
        if accum_out is not None:
            kw["accum_out"] = accum_out
        return self.P.op(ACT, lambda e: e.activation(out=out, in_=in_, func=func, **kw),
                         [x.b for x in reads], [x.b for x in writes])

    def tt(self, out, in0, in1, op, reads, writes, eng=DVE):
        return self.P.op(eng, lambda e: e.tensor_tensor(out=out, in0=in0, in1=in1, op=op),
                         [x.b for x in reads], [x.b for x in writes])

    def ts(self, out, in0, s1, s2, op0, op1, reads, writes, eng=DVE):
        if op1 is None:
            return self.P.op(eng, lambda e: e.tensor_scalar(out=out, in0=in0, scalar1=s1, scalar2=None, op0=op0),
                             [x.b for x in reads], [x.b for x in writes])
        return self.P.op(eng, lambda e: e.tensor_scalar(out=out, in0=in0, scalar1=s1, scalar2=s2, op0=op0, op1=op1),
                         [x.b for x in reads], [x.b for x in writes])

    def stt(self, out, in0, scalar, in1, op0, op1, reads, writes, eng=DVE):
        return self.P.op(eng, lambda e: e.scalar_tensor_tensor(out=out, in0=in0, scalar=scalar, in1=in1, op0=op0, op1=op1),
                         [x.b for x in reads], [x.b for x in writes])

    def cp(self, out, in_, reads, writes, eng=DVE):
        return self.P.op(eng, lambda e: e.tensor_copy(out=out, in_=in_), [x.b for x in reads], [x.b for x in writes])

    def memset(self, out, val, writes, eng=POOL):
        return self.P.op(eng, lambda e: e.memset(out, val), [], [x.b for x in writes])

    def sigm(self, ap, t):
        self.act(ap, ap, AF.Ln, [t], [t], bias=1.0)
        self.act(ap, ap, AF.Exp, [t], [t], scale=-1.0)

    def rsqrt(self, ap, t):
        self.act(ap, ap, AF.Ln, [t], [t])
        self.act(ap, ap, AF.Exp, [t], [t], scale=-0.5)

    def build(self):
        nc = self.nc
        dbg = self.debug
        I = {}

        def inp(name, shape):
            I[name] = self.dram(name, shape, F32, kind="ExternalInput")
        inp("x", [SEQ, D]); inp("ctx", [CTX, D]); inp("cvec", [2, D])
        inp("w_ada", [2, D, 3 * D]); inp("b_ada", [2, 3 * D]); inp("w_in", [2, D, WEXT])
        inp("ssd_conv_w", [2, 5, 1536]); inp("ssd_conv_b", [2, 1536])
        for n in ("ssd_a_log_f", "ssd_a_log_b", "ssd_dt_bias_f", "ssd_dt_bias_b", "ssd_d"):
            inp(n, [2, 16])
        inp("ssd_norm_w", [2, 1024]); inp("mla_q_norm", [2, 384]); inp("mla_w_uq", [2, 384, 768])
        inp("w_uq_sw", [2, 384, 256]); inp("mla_kv_norm", [2, 256]); inp("mla_w_ukv", [2, 256, 1024])
        inp("ret_log_rate_f", [2, 4]); inp("ret_log_rate_b", [2, 4]); inp("w_out", [2, 2048, D])
        inp("ln_g", [2, D]); inp("ln_b", [2, D])
        inp("cst", [128, CST_W]); inp("mla_tab", [32, 2, NTOK]); inp("ret_tab", [128, 4, NTOK]); inp("ret_tab2", [NTOK, 512])
        self.I = I
        okind = "ExternalOutput" if dbg else "Internal"
        self.out = self.dram("out", [SEQ, D], F32, kind="ExternalOutput")
        self.hT_d = self.dram("hT_d", [128, 8 * HC], BF16, kind=okind)
        self.ypart_d = self.dram("ypart_d", [NTOK, 1024], BF16)
        self.rpart_d = self.dram("rpart_d", [NTOK, 512], BF16)
        self.ypartb_d = self.dram("ypartb_d", [NTOK, 1024], BF16)
        self.Fb_d = self.dram("Fb_d", [NCH, 2, 128, 768], BF16)
        self.Rb_d = self.dram("Rb_d", [NCH, 128, 1280], BF16)
        self.Ff_d = self.dram("Ff_d", [NCH, 2, 128, 136], F32)
        self.rpartb_d = self.dram("rpartb_d", [NTOK, 512], BF16)
        self.mixT_d = self.dram("mixT_d", [NCH, 128, 2048], BF16, kind=okind)
        self.qT_d = self.dram("qT_d", [8, 96, NTOK], BF16)
        self.kfT_d = self.dram("kfT_d", [8, 96, NTOK], BF16)
        self.sgT_d = self.dram("sgT_d", [512, NTOK], F32)
        self.xres_d = self.dram("xres_d", [NTOK, D], F32, kind=okind)

        with ExitStack() as gst:
            self.gst = gst
            self.cst = self.sb(gst, [128, CST_W], F32, "cst")
            self.dma(self.cst[:], I["cst"][:], [I["cst"]], [self.cst])
            self.identb = self.sb(gst, [128, 128], BF16, "identb")
            self.cp(self.identb[:], self.cst[:, C_ID:C_ID + 128], [self.cst], [self.identb])
            self.ones = self.sb(gst, [128, 128], F32, "ones")
            self.memset(self.ones[:], 1.0, [self.ones])
            self.gB = self.sb(gst, [128, 2, 1024], F32, "gB")
            zt = self.sb(gst, [128, 8, 4], BF16, "zt")
            self.memset(zt[:], 0.0, [zt])
            hv = self.hT_d[:].rearrange("p (k c) -> p k c", k=8)
            self.hv = hv
            for (a, b) in ((0, 2), (258, 262), (4358, 4360)):
                self.dma(hv[:, :, a:b], zt[:, :, 0:b - a], [zt], [self.hT_d], slow=True)
            self.P.barrier()
            stages = []
            for l in range(DEPTH):
                stages += [("A", l), ("S", l), ("M", l), ("R", l), ("E", l)]
            for (s, l) in stages:
                if s in self.skip:
                    continue
                if s == "A":
                    self.stage_A(l)
                elif s == "S":
                    self.stage_S(l)
                elif s == "M":
                    self.stage_M(l)
                elif s == "R":
                    self.stage_R(l)
                else:
                    self.stage_E(l)
                self.P.barrier()
                if self.stop_after == (s, l):
                    break
            self.P.barrier()
            self.P.emit(gst)
        return nc

    def silu_psum(self, st, src_ap, src_t, out_ap, out_t, e_t, e_ap, r_ap):
        self.act(e_ap, src_ap, AF.Exp, [src_t], [e_t], scale=-1.0)
        self.sigm(e_ap, e_t)
        self.tt(out_ap, src_ap, r_ap, ALU.mult, [src_t, e_t], [out_t])

    def stage_A(self, l):
        I = self.I
        with ExitStack() as st:
            wada = [self.sb(st, [128, 8, 512], F32, "wada") for _ in range(2)]
            craw = self.sb(st, [128, 8, 2], F32, "craw")
            ce = self.sb(st, [128, 8, 2], F32, "ce")
            scT = self.sb(st, [128, 8, 2], F32, "scT")
            modT = self.sb(st, [128, 24, 2], F32, "modT")
            scale1 = self.sb(st, [128, 8, 2], F32, "scale1")
            brow = self.sb(st, [1, 3 * D], F32, "brow")
            pm = self.ps(st, [128, 512], F32, "pm")
            pg = [self.ps(st, [128, 512], F32, "pg") for _ in range(2)]
            for j in range(2):
                self.dma(craw[:, :, j], I["cvec"][j].rearrange("(k p) -> p k", p=128), [I["cvec"]], [craw], slow=True)
            self.dma(brow[:], I["b_ada"][l:l + 1, :], [I["b_ada"]], [brow])
            self.act(ce[:], craw[:], AF.Exp, [craw], [ce], scale=-1.0)
            self.sigm(ce[:], ce)
            self.tt(scT[:], craw[:], ce[:], ALU.mult, [craw, ce], [scT])
            wv = I["w_ada"][l].rearrange("(k p) c -> p k c", p=128)
            for cb in range(6):
                w = wada[cb % 2]
                self.dma(w[:], wv[:, :, cb * 512:(cb + 1) * 512], [I["w_ada"]], [w])
                if cb < 4:
                    for dj in range(4):
                        j = cb * 4 + dj
                        for kc in range(8):
                            self.mm(pm[:, 2 * dj:2 * dj + 2], w[:, kc, dj * 128:(dj + 1) * 128], scT[:, kc, :],
                                    kc == 0, False, [w, scT], [pm])
                        self.mm(pm[:, 2 * dj:2 * dj + 2], brow[0:1, j * 128:(j + 1) * 128], self.ones[0:1, 0:2],
                                False, True, [brow, self.ones], [pm])
                        self.cp(modT[:, j, :], pm[:, 2 * dj:2 * dj + 2], [pm], [modT])
                else:
                    for typ in range(2):
                        p = pg[typ]
                        for kc in range(8):
                            self.mm(p[:], scT[:, kc, typ:typ + 1].to_broadcast([128, 128]), w[:, kc, :],
                                    kc == 0, False, [w, scT], [p])
                        self.mm(p[:], self.ones[0:1, 0:128], brow[0:1, cb * 512:(cb + 1) * 512], False, True,
                                [brow, self.ones], [p])
                        self.cp(self.gB[:, typ, (cb - 4) * 512:(cb - 3) * 512], p[:], [p], [self.gB])
            self.ts(scale1[:], modT[:, 8:16, :], 1.0, None, ALU.add, None, [modT], [scale1])
            sets = []
            for s_ in range(3):
                B = {}
                B["xt"] = self.sb(st, [128, D], F32, "xt")
                B["ht"] = self.sb(st, [128, 8, 128], BF16, "ht")
                B["pT"] = pg if s_ == 0 else [self.ps(st, [128, 512], F32, "pT") for _ in range(2)]
                sets.append(B)
            gens = [(lambda tt_: (lambda slot: self.a_tile(l, tt_, sets[slot], scale1, modT)))(t) for t in range(NCH)]
            self.interleave(gens, 3)

    def a_tile(self, l, t, B, scale1, modT):
        I = self.I
        x_ = B["xt"]; h_ = B["ht"]; pT = B["pT"]
        typ = 1 if t < 2 else 0
        if l == 0:
            src_t = I["ctx"] if t < 2 else I["x"]
            src = src_t[t * 128:(t + 1) * 128, :] if t < 2 else src_t[(t - 2) * 128:(t - 1) * 128, :]
        else:
            src_t = self.xres_d
            src = src_t[t * 128:(t + 1) * 128, :]
        self.dma(x_[:], src, [src_t], [x_])
        yield
        for kc in range(8):
            p_ = pT[kc // 4]
            self.tr(p_[:, (kc % 4) * 128:(kc % 4 + 1) * 128], x_[:, kc * 128:(kc + 1) * 128], self.cst[:, C_ID:C_ID + 128],
                    [x_, self.cst], [p_])
            if kc % 4 == 3:
                yield
        for kc in range(8):
            p_ = pT[kc // 4]
            self.act(h_[:, kc, :], p_[:, (kc % 4) * 128:(kc % 4 + 1) * 128], AF.Identity, [p_, scale1, modT], [h_],
                     bias=modT[:, kc, typ:typ + 1], scale=scale1[:, kc, typ:typ + 1])
            if kc % 4 == 3:
                yield
        c0 = colof(t * 128)
        self.dma(self.hv[:, :, c0:c0 + 128], h_[:], [h_], [self.hT_d])
        yield

    def load_w(self, dst_ap, dst_t, src_ap, src_t):
        self.dma(dst_ap, src_ap, [src_t], [dst_t], eng=POOL, slow=True)

    def bvec(self, st, name, l, n):
        t = self.sb(st, [128, n], F32, name)
        self.dma(t[:], self.I[name][l:l + 1, :].to_broadcast([128, n]), [self.I[name]], [t], slow=True)
        return t

    def interleave(self, factories, width):
        pending = list(factories)
        active = []
        for s in range(width):
            if pending:
                active.append((s, pending.pop(0)(s)))
        while active:
            nxt = []
            for (s, g) in active:
                try:
                    next(g)
                    nxt.append((s, g))
                except StopIteration:
                    if pending:
                        nxt.append((s, pending.pop(0)(s)))
            active = nxt

    def stage_S(self, l):
        I = self.I
        cst = self.cst
        import os as _os
        with ExitStack() as st:
            wv = I["w_in"][l].rearrange("(k p) c -> p k c", p=128)
            with ExitStack() as stf:
                wx = self.sb(stf, [128, 8, 1536], BF16, "wx")
                wdt = self.sb(stf, [128, 8, 16], BF16, "wdt")
                for kc in range(8):
                    self.load_w(wx[:, kc, :], wx, wv[:, kc, 1024:2560], I["w_in"])
                self.load_w(wdt[:], wdt, wv[:, :, 2560:2576], I["w_in"])
                convw = self.sb(stf, [128, 12, 5], F32, "convw")
                for k in range(5):
                    self.dma(convw[:, :, k], I["ssd_conv_w"][l, k].rearrange("(r p) -> p r", p=128), [I["ssd_conv_w"]], [convw], slow=True)
                dg = self.sb(stf, [128, 60, 128], BF16, "dg")
                for r in range(12):
                    for k in range(5):
                        self.ts(dg[:, r * 5 + k, :], self.identb[:], convw[:, r, k:k + 1], None, ALU.mult, None,
                                [self.identb, convw], [dg])
                cbrow = self.sb(stf, [1, 1536], F32, "cbrow")
                self.dma(cbrow[:], I["ssd_conv_b"][l:l + 1, :], [I["ssd_conv_b"]], [cbrow])
                sets = []
                for s_ in range(2):
                    B = {}
                    B["hc"] = self.sb(stf, [128, 8, 132], BF16, "hcf")
                    B["xbc"] = self.sb(stf, [128, 12, 132], BF16, "xbc")
                    B["esb"] = self.sb(stf, [128, 1536], F32, "esbf")
                    B["ubf"] = self.sb(stf, [128, 12, 128], BF16, "ubf")
                    B["Fb"] = self.sb(stf, [128, 2, 768], BF16, "Fbw")
                    B["Ff"] = self.sb(stf, [128, 2, 136], F32, "Ffw")
                    B["Q"] = [self.ps(stf, [128, 512], F32, "QF") for _ in range(4)]
                    sets.append(B)
                gens = [(lambda cc: (lambda slot: self.ssd_front(cc, sets[slot], wx, wdt, dg, cbrow)))(c) for c in range(NCH)]
                self.interleave(gens, 2)
            self.P.barrier()
            Dsk = self.bvec(st, "ssd_d", l, 16)
            prm = {}
            for d_, sfx in ((0, "f"), (1, "b")):
                al = self.bvec(st, "ssd_a_log_" + sfx, l, 16)
                self.act(al[:], al[:], AF.Exp, [al], [al])
                self.ts(al[:], al[:], -1.0, None, ALU.mult, None, [al], [al])
                dtb = self.bvec(st, "ssd_dt_bias_" + sfx, l, 16)
                Ub = self.sb(st, [128, 128], BF16, "Ub16")
                Lb = self.sb(st, [128, 128], BF16, "Lb16")
                Uo = C_UF if d_ == 0 else C_UB
                Lo = C_LF if d_ == 0 else C_LB
                self.cp(Ub[:], cst[:, Uo:Uo + 128], [cst], [Ub])
                self.cp(Lb[:], cst[:, Lo:Lo + 128], [cst], [Lb])
                prm[d_] = (al, dtb, Ub, Lb)
            with ExitStack() as st2:
                gens = []
                self.yb = {}
                for d_ in (0, 1):
                    for g_ in (0, 1):
                        self.yb[(d_, g_)] = T((self.ypart_d if d_ == 0 else self.ypartb_d).t, "yb")
                        gens.append((lambda dd, gg: (lambda slot: self.ssd_sweep(l, dd, gg, st2, Dsk, prm[dd])))(d_, g_))
                self.interleave(gens, 4)
            self.P.barrier()
            with ExitStack() as st3:
                wz = self.sb(st3, [128, 8, 1024], BF16, "wz")
                for kc in range(8):
                    self.load_w(wz[:, kc, :], wz, wv[:, kc, 0:1024], I["w_in"])
                nwB = self.bvec(st3, "ssd_norm_w", l, 1024)
                sets = []
                for s in range(4):
                    B = {}
                    B["hc"] = self.sb(st3, [128, 8, 128], BF16, "hc3")
                    B["ypf"] = self.sb(st3, [128, 1024], BF16, "ypf")
                    B["ypb"] = self.sb(st3, [128, 1024], BF16, "ypb")
                    B["ys"] = self.sb(st3, [128, 1024], F32, "ys3")
                    B["e"] = self.sb(st3, [128, 1024], F32, "e3")
                    B["t1"] = self.sb(st3, [128, 1024], F32, "t13")
                    B["bst"] = self.sb(st3, [128, 2, 6], F32, "bst3")
                    B["ssq"] = self.sb(st3, [128, 2], F32, "ssq3")
                    B["ob"] = self.sb(st3, [128, 1024], BF16, "ob3")
                    B["oT"] = self.sb(st3, [128, 8, 128], BF16, "oT3")
                    B["PZ"] = [self.ps(st3, [128, 512], F32, "PZ3") for _ in range(2)]
                    B["PT"] = B["PZ"][0]
                    sets.append(B)
                gens = [(lambda cc: (lambda slot: self.ssd_final(cc, sets[slot], wz, nwB)))(c) for c in range(NCH)]
                self.interleave(gens, 4)

    def ssd_final(self, c, B, wz, nwB):
        h_ = B["hc"]; ypf = B["ypf"]; ypb = B["ypb"]; e = B["e"]; t1 = B["t1"]; bst = B["bst"]; ssq = B["ssq"]
        ob = B["ob"]; oT_ = B["oT"]; PZ = B["PZ"]; PT = B["PT"]
        c0 = colof(c * 128)
        tok = slice(c * 128, (c + 1) * 128)
        self.dma(h_[:], self.hv[:, :, c0:c0 + 128], [self.hT_d], [h_], slow=True)
        self.dma(ypf[:], self.ypart_d[tok, :], [self.yb[(0, 0)], self.yb[(0, 1)]], [ypf])
        self.dma(ypb[:], self.ypartb_d[tok, :], [self.yb[(1, 0)], self.yb[(1, 1)]], [ypb])
        yield
        for n in range(2):
            for kc in range(8):
                self.mm(PZ[n][:], h_[:, kc, :], wz[:, kc, n * 512:(n + 1) * 512], kc == 0, kc == 7, [h_, wz], [PZ[n]])
        ys = B["ys"]
        self.tt(ys[:], ypf[:], ypb[:], ALU.add, [ypf, ypb], [ys])
        yield
        for n in range(2):
            self.act(e[:, n * 512:(n + 1) * 512], PZ[n][:], AF.Exp, [PZ[n]], [e], scale=-1.0)
        yield
        self.sigm(e[:], e)
        yield
        for n in range(2):
            self.tt(t1[:, n * 512:(n + 1) * 512], PZ[n][:], e[:, n * 512:(n + 1) * 512], ALU.mult, [PZ[n], e], [t1])
        yield
        self.tt(t1[:], t1[:], ys[:], ALU.mult, [t1, ys], [t1])
        yield
        for s_ in range(2):
            self.P.op(DVE, (lambda ss: (lambda en: en.bn_stats(out=bst[:, ss, :], in_=t1[:, ss * 512:(ss + 1) * 512])))(s_),
                      [t1.b], [bst.b])
        self.P.op(DVE, lambda en: en.bn_aggr(out=ssq[:], in_=bst[:]), [bst.b], [ssq.b])
        self.stt(ssq[:, 1:2], ssq[:, 0:1], ssq[:, 0:1], ssq[:, 1:2], ALU.mult, ALU.add, [ssq], [ssq])
        self.ts(ssq[:, 1:2], ssq[:, 1:2], RMS_EPS, None, ALU.add, None, [ssq], [ssq])
        yield
        self.rsqrt(ssq[:, 1:2], ssq)
        yield
        self.stt(ob[:], t1[:], ssq[:, 1:2], nwB[:], ALU.mult, ALU.mult, [t1, ssq, nwB], [ob])
        yield
        PTb = PT[:].bitcast(BF16)
        for r in range(8):
            self.tr(PTb[:, r * 128:(r + 1) * 128], ob[:, r * 128:(r + 1) * 128], self.identb[:], [ob, self.identb], [PT])
        yield
        self.act(oT_[:], PTb[:, 0:1024].rearrange("p (a b) -> p a b", a=8), AF.Copy, [PT], [oT_])
        yield
        self.dma(self.mixT_d[c, :, 0:1024], oT_[:].rearrange("p a b -> p (a b)"), [oT_], [self.mixT_d])
        yield

    def ssd_front(self, c, B, wx, wdt, dg, cbrow):
        h_ = B["hc"]; xbc = B["xbc"]; esb = B["esb"]; ubf = B["ubf"]; Fb = B["Fb"]; Ff = B["Ff"]; Q = B["Q"]
        Qb0 = Q[0][:].bitcast(BF16)
        Qb1 = Q[1][:].bitcast(BF16)
        c0 = colof(c * 128)
        self.dma(h_[:], self.hv[:, :, c0 - 2:c0 + 130], [self.hT_d], [h_], slow=True)
        yield
        for r in range(12):
            q_ = Q[r // 3]
            o_ = q_[:, (r % 3) * 132:(r % 3) * 132 + 132]
            for kc in range(8):
                self.mm(o_, wx[:, kc, r * 128:(r + 1) * 128], h_[:, kc, :], kc == 0, kc == 7, [wx, h_], [q_])
            if r % 3 == 2:
                yield
        for q in range(4):
            self.act(xbc[:, 3 * q:3 * q + 3, :], Q[q][:, 0:396].rearrange("p (a b) -> p a b", a=3), AF.Copy, [Q[q]], [xbc])
        yield
        for kc in range(8):
            self.mm(Q[3][:, 0:16], h_[:, kc, 2:130], wdt[:, kc, :], kc == 0, kc == 7, [h_, wdt], [Q[3]])
        yield
        for r in range(12):
            q_ = Q[r // 4]
            o_ = q_[:, (r % 4) * 128:(r % 4 + 1) * 128]
            for k in range(5):
                self.mm(o_, dg[:, r * 5 + k, :], xbc[:, r, k:k + 128], k == 0, False, [dg, xbc], [q_])
            self.mm(o_, cbrow[0:1, r * 128:(r + 1) * 128], self.ones[0:1, 0:128], False, True, [cbrow, self.ones], [q_])
            if r % 4 == 3:
                yield
        self.cp(Ff[:, :, 128:136], Q[3][:, 0:16].rearrange("p (g n) -> p g n", g=2), [Q[3]], [Ff])
        for q in range(3):
            self.act(esb[:, q * 512:(q + 1) * 512], Q[q][:], AF.Exp, [Q[q]], [esb], scale=-1.0)
        yield
        self.sigm(esb[:], esb)
        yield
        for q in range(3):
            self.tt(ubf[:, 4 * q:4 * q + 4, :], Q[q][:].rearrange("p (a b) -> p a b", a=4),
                    esb[:, q * 512:(q + 1) * 512].rearrange("p (a b) -> p a b", a=4), ALU.mult, [Q[q], esb], [ubf])
        yield
        for g in range(2):
            self.mm(Q[3][:, 256 + g * 128:256 + (g + 1) * 128], ubf[:, 8 + g, :], ubf[:, 10 + g, :], True, True, [ubf], [Q[3]])
        for r in range(8):
            self.tr(Qb0[:, r * 128:(r + 1) * 128], ubf[:, r, :], self.identb[:], [ubf, self.identb], [Q[0]])
        for r in range(2):
            self.tr(Qb1[:, r * 128:(r + 1) * 128], ubf[:, 8 + r, :], self.identb[:], [ubf, self.identb], [Q[1]])
        self.cp(Fb[:, :, 640:768], ubf[:, 10:12, :], [ubf], [Fb], eng=POOL)
        yield
        self.cp(Ff[:, :, 0:128], Q[3][:, 256:512].rearrange("p (g n) -> p g n", g=2), [Q[3]], [Ff])
        self.act(Fb[:, :, 0:512], Qb0[:, 0:1024].rearrange("p (g n) -> p g n", g=2), AF.Copy, [Q[0]], [Fb])
        self.cp(Fb[:, :, 512:640], Qb1[:, 0:256].rearrange("p (g n) -> p g n", g=2), [Q[1]], [Fb])
        yield
        self.dma(self.Fb_d[c].rearrange("g p n -> p g n"), Fb[:], [Fb], [self.Fb_d])
        self.dma(self.Ff_d[c].rearrange("g p n -> p g n"), Ff[:], [Ff], [self.Ff_d])
        yield

    def ssd_sweep(self, l, d_, g, st, Dsk, prm):
        cst = self.cst
        al, dtb, Ub, Lb = prm
        HS = slice(g * 8, (g + 1) * 8)
        H = self.sb(st, [128, 512], F32, "H")
        Hbf = self.sb(st, [128, 512], BF16, "Hbf")
        Fb = [self.sb(st, [128, 768], BF16, "Fbr") for _ in range(2)]
        Ff = [self.sb(st, [128, 136], F32, "Ffr") for _ in range(2)]
        esb = self.sb(st, [128, 1024], F32, "esb")
        dtx = self.sb(st, [128, 8], F32, "dtx")
        dt = self.sb(st, [128, 8], F32, "dt")
        la = self.sb(st, [128, 8], F32, "la")
        lah = self.sb(st, [128, 8], BF16, "lah")
        lah32 = self.sb(st, [128, 8], F32, "lah32")
        lal32 = self.sb(st, [128, 8], F32, "lal32")
        lalo = self.sb(st, [128, 8], BF16, "lalo")
        E3 = self.sb(st, [128, 24], F32, "E3")
        scm = self.sb(st, [128, 128], F32, "scm")
        LaUh = self.sb(st, [128, 8, 128], BF16, "LaUh")
        LaUl = self.sb(st, [128, 8, 128], BF16, "LaUl")
        M = self.sb(st, [128, 8, 128], BF16, "M")
        v = self.sb(st, [128, 512], BF16, "v")
        vte = self.sb(st, [128, 512], BF16, "vte")
        t1 = self.sb(st, [128, 512], F32, "t1")
        t2 = self.sb(st, [128, 512], F32, "t2")
        yp = [self.sb(st, [128, 512], BF16, "yp") for _ in range(2)]
        Q = [self.ps(st, [128, 512], F32, "Q") for _ in range(2)]
        ydst = self.ypart_d if d_ == 0 else self.ypartb_d
        order = list(range(NCH)) if d_ == 0 else [1, 0] + list(range(NCH - 1, 1, -1))
        Uo = C_UF if d_ == 0 else C_UB
        Lo = C_LF if d_ == 0 else C_LB
        Mo = C_MF if d_ == 0 else C_MB
        self.memset(H[:], 0.0, [H])
        self.memset(Hbf[:], 0.0, [Hbf])

        def loads(ci, slot):
            c = order[ci]
            self.dma(Fb[slot][:], self.Fb_d[c, g], [self.Fb_d], [Fb[slot]])
            self.dma(Ff[slot][:], self.Ff_d[c, g], [self.Ff_d], [Ff[slot]])
        loads(0, 0)
        it = 0
        for ci, c in enumerate(order):
            fb = Fb[it % 2]; ff = Ff[it % 2]; ypt = yp[it % 2]
            it += 1
            tok = slice(c * 128, (c + 1) * 128)
            if ci + 1 < len(order):
                loads(ci + 1, it % 2)
            xs_g = fb[:, 0:512]
            self.tt(dtx[:], ff[:, 128:136], dtb[:, HS], ALU.add, [ff, dtb], [dtx])
            self.tt(scm[:], ff[:, 0:128], cst[:, Mo:Mo + 128], ALU.mult, [ff, cst], [scm])
            yield
            self.act(dtx[:], dtx[:], AF.Exp, [dtx], [dtx])
            self.act(dt[:], dtx[:], AF.Ln, [dtx], [dt], bias=1.0)
            yield
            self.tt(la[:], dt[:], al[:, HS], ALU.mult, [dt, al], [la])
            self.cp(lah[:], la[:], [la], [lah])
            self.cp(lah32[:], lah[:], [lah], [lah32])
            self.tt(lal32[:], la[:], lah32[:], ALU.subtract, [la, lah32], [lal32])
            self.cp(lalo[:], lal32[:], [lal32], [lalo])
            yield
            self.mm(Q[1][:, 0:8], cst[:, Uo:Uo + 128], la[:], True, True, [cst, la], [Q[1]])
            self.mm(Q[1][:, 8:16], cst[:, Lo:Lo + 128], la[:], True, True, [cst, la], [Q[1]])
            self.mm(Q[1][:, 16:24], self.ones[:], la[:], True, True, [self.ones, la], [Q[1]])
            self.tt(LaUh[:], lah[:].unsqueeze(2).to_broadcast([128, 8, 128]),
                    Ub[:].unsqueeze(1).to_broadcast([128, 8, 128]), ALU.mult, [lah, Ub], [LaUh], eng=POOL)
            self.tt(LaUl[:], lalo[:].unsqueeze(2).to_broadcast([128, 8, 128]),
                    Ub[:].unsqueeze(1).to_broadcast([128, 8, 128]), ALU.mult, [lalo, Ub], [LaUl])
            self.tt(v[:].rearrange("p (h e) -> p h e", h=8), xs_g.rearrange("p (h e) -> p h e", h=8),
                    dt[:].unsqueeze(2).to_broadcast([128, 8, 64]), ALU.mult, [fb, dt], [v], eng=POOL)
            yield
            self.act(E3[:], Q[1][:, 0:24], AF.Exp, [Q[1]], [E3])
            yield
            for q in range(2):
                self.mm(Q[q][:], Lb[:], LaUh[:, 4 * q:4 * q + 4, :].rearrange("p a b -> p (a b)"), True, False, [Lb, LaUh], [Q[q]])
                self.mm(Q[q][:], Lb[:], LaUl[:, 4 * q:4 * q + 4, :].rearrange("p a b -> p (a b)"), False, True, [Lb, LaUl], [Q[q]])
            yield
            self.tt(vte[:].rearrange("p (h e) -> p h e", h=8), v[:].rearrange("p (h e) -> p h e", h=8),
                    E3[:, 8:16].unsqueeze(2).to_broadcast([128, 8, 64]), ALU.mult, [v, E3], [vte], eng=POOL)
            for q in range(2):
                self.act(esb[:, q * 512:(q + 1) * 512], Q[q][:], AF.Exp, [Q[q]], [esb])
            yield
            self.tt(M[:], esb[:].rearrange("p (a b) -> p a b", a=8),
                    scm[:].unsqueeze(1).to_broadcast([128, 8, 128]), ALU.mult, [esb, scm], [M])
            yield
            self.mm(Q[0][:], fb[:, 640:768], Hbf[:], True, True, [fb, Hbf], [Q[0]])
            for hh in range(8):
                self.mm(Q[1][:, hh * 64:(hh + 1) * 64], M[:, hh, :], v[:, hh * 64:(hh + 1) * 64], True, True, [M, v], [Q[1]])
            yield
            self.tt(t1[:].rearrange("p (h e) -> p h e", h=8), Q[0][:].rearrange("p (h e) -> p h e", h=8),
                    E3[:, 0:8].unsqueeze(2).to_broadcast([128, 8, 64]), ALU.mult, [Q[0], E3], [t1])
            yield
            self.tt(t2[:], Q[1][:], t1[:], ALU.add, [Q[1], t1], [t2])
            self.mm(Q[0][:], fb[:, 512:640], vte[:], True, True, [fb, vte], [Q[0]])
            yield
            if d_ == 0:
                self.tt(t1[:].rearrange("p (h e) -> p h e", h=8), xs_g.rearrange("p (h e) -> p h e", h=8),
                        Dsk[:, HS].unsqueeze(2).to_broadcast([128, 8, 64]), ALU.mult, [fb, Dsk], [t1], eng=POOL)
                self.tt(ypt[:], t1[:], t2[:], ALU.add, [t1, t2], [ypt], eng=POOL)
            else:
                self.cp(ypt[:], t2[:], [t2], [ypt], eng=POOL)
            self.dma(ydst[tok, g * 512:(g + 1) * 512], ypt[:], [ypt], [self.yb[(d_, g)]])
            self.tt(H[:].rearrange("p (h e) -> p h e", h=8), H[:].rearrange("p (h e) -> p h e", h=8),
                    E3[:, 16:24].unsqueeze(2).to_broadcast([128, 8, 64]), ALU.mult, [H, E3], [H])
            yield
            self.tt(H[:], H[:], Q[0][:], ALU.add, [H, Q[0]], [H])
            yield
            self.act(Hbf[:], H[:], AF.Copy, [H], [Hbf])
            yield

    def stage_R(self, l):
        I = self.I
        cst = self.cst
        import os as _os
        with ExitStack() as st:
            wv = I["w_in"][l].rearrange("(k p) c -> p k c", p=128)
            wqk = self.sb(st, [128, 8, 1024], BF16, "wqk")
            wvv = self.sb(st, [128, 8, 512], BF16, "wvr")
            for kc in range(8):
                self.load_w(wqk[:, kc, 0:512], wqk, wv[:, kc, 3760:4272], I["w_in"])
                self.load_w(wqk[:, kc, 512:1024], wqk, wv[:, kc, 5328:5840], I["w_in"])
                self.load_w(wvv[:, kc, :], wvv, wv[:, kc, 4272:4784], I["w_in"])
            prm = {}
            for d_, sfx in ((0, "f"), (1, "b")):
                nm = "ret_log_rate_" + sfx
                lgB = self.bvec(st, nm, l, 4)
                self.act(lgB[:], lgB[:], AF.Exp, [lgB], [lgB])
                self.ts(lgB[:], lgB[:], -1.0, None, ALU.mult, None, [lgB], [lgB])
                lgs = self.sb(st, [128, 2], F32, "lgs")
                src = I[nm][l:l + 1, :].rearrange("o (p two) -> o p two", two=2)
                self.dma(lgs[0:64, :], src[:, :, 0].to_broadcast([64, 2]), [I[nm]], [lgs], slow=True)
                self.dma(lgs[64:128, :], src[:, :, 1].to_broadcast([64, 2]), [I[nm]], [lgs], slow=True)
                self.act(lgs[:], lgs[:], AF.Exp, [lgs], [lgs])
                self.ts(lgs[:], lgs[:], -1.0, None, ALU.mult, None, [lgs], [lgs])
                RIo = C_RIF if d_ == 0 else C_RIB
                Mo = C_MF if d_ == 0 else C_MB
                Go = C_GF if d_ == 0 else C_GB
                To = C_TEF if d_ == 0 else C_TEB
                DmT = self.sb(st, [128, 4, 128], F32, "DmT")
                for h in range(4):
                    self.act(DmT[:, h, :], cst[:, RIo:RIo + 128], AF.Exp, [cst, lgB], [DmT], scale=lgB[:, h:h + 1])
                self.tt(DmT[:], DmT[:], cst[:, Mo:Mo + 128].unsqueeze(1).to_broadcast([128, 4, 128]), ALU.mult, [DmT, cst], [DmT])
                self.ts(DmT[:], DmT[:], 0.125, None, ALU.mult, None, [DmT], [DmT])
                Gam = self.sb(st, [128, 2, 128], F32, "Gam")
                for p in range(2):
                    self.act(Gam[:, p, :], cst[:, Go:Go + 128], AF.Exp, [cst, lgs], [Gam], scale=lgs[:, p:p + 1])
                te = self.sb(st, [128, 4], F32, "te")
                self.act(te[:], lgB[:], AF.Exp, [lgB, cst], [te], scale=cst[:, To:To + 1])
                self.ts(te[:], te[:], 0.125, None, ALU.mult, None, [te], [te])
                g128 = self.sb(st, [128, 2], F32, "g128")
                self.act(g128[:], lgs[:], AF.Exp, [lgs], [g128], scale=128.0)
                prm[d_] = (DmT, Gam, te, g128)
            with ExitStack() as stf:
                sets = []
                for s_ in range(2):
                    B = {}
                    B["hc"] = self.sb(stf, [128, 8, 128], BF16, "hcrf")
                    B["tab"] = self.sb(stf, [128, 512], F32, "tabrf")
                    B["r1"] = self.sb(stf, [128, 512], F32, "r1")
                    B["r2"] = self.sb(stf, [128, 512], F32, "r2")
                    B["qkt"] = self.sb(stf, [128, 512], BF16, "qkt")
                    B["Rb"] = self.sb(stf, [128, 1280], BF16, "Rbw")
                    B["Q"] = [self.ps(stf, [128, 512], F32, "QRF") for _ in range(4)]
                    sets.append(B)
                gens = [(lambda cc: (lambda slot: self.ret_front(cc, sets[slot], wqk, wvv)))(c) for c in range(NCH)]
                self.interleave(gens, 2)
            self.P.barrier()
            with ExitStack() as st2:
                gens = []
                for d_ in (0, 1):
                    gens.append((lambda dd: (lambda slot: self.ret_sweep(l, dd, st2, prm[dd])))(d_))
                self.interleave(gens, 2)
            self.P.barrier()
            with ExitStack() as st3:
                wg = self.sb(st3, [128, 8, 512], BF16, "wg")
                for kc in range(8):
                    self.load_w(wg[:, kc, :], wg, wv[:, kc, 4784:5296], I["w_in"])
                sets = []
                for s in range(4):
                    B = {}
                    B["hc"] = self.sb(st3, [128, 8, 128], BF16, "hcr3")
                    B["rpf"] = self.sb(st3, [128, 512], BF16, "rpf")
                    B["rpb"] = self.sb(st3, [128, 512], BF16, "rpb")
                    B["rs"] = self.sb(st3, [128, 512], F32, "rs3")
                    B["ge"] = self.sb(st3, [128, 512], F32, "ge3")
                    B["sg"] = self.sb(st3, [128, 512], F32, "sg3")
                    B["stats"] = self.sb(st3, [128, 4, 6], F32, "stats3")
                    B["mv"] = self.sb(st3, [128, 4, 2], F32, "mv3")
                    B["yn"] = self.sb(st3, [128, 512], F32, "yn3")
                    B["ob"] = self.sb(st3, [128, 512], BF16, "obr3")
                    B["oT"] = self.sb(st3, [128, 4, 128], BF16, "oTr3")
                    B["PG"] = self.ps(st3, [128, 512], F32, "PGr3")
                    B["PT"] = self.ps(st3, [128, 1024], BF16, "PTr3")
                    sets.append(B)
                gens = [(lambda cc: (lambda slot: self.ret_final(cc, sets[slot], wg)))(c) for c in range(NCH)]
                self.interleave(gens, 4)

    def ret_final(self, c, B, wg):
        h_ = B["hc"]; rpf = B["rpf"]; rpb = B["rpb"]; ge = B["ge"]; sg = B["sg"]; stats = B["stats"]; mv = B["mv"]
        yn = B["yn"]; ob = B["ob"]; oT_ = B["oT"]; PG = B["PG"]; PT = B["PT"]
        c0 = colof(c * 128)
        tok = slice(c * 128, (c + 1) * 128)
        self.dma(h_[:], self.hv[:, :, c0:c0 + 128], [self.hT_d], [h_], slow=True)
        self.dma(rpf[:], self.rpart_d[tok, :], [self.rpart_d], [rpf])
        self.dma(rpb[:], self.rpartb_d[tok, :], [self.rpartb_d], [rpb])
        yield
        for kc in range(8):
            self.mm(PG[:], h_[:, kc, :], wg[:, kc, :], kc == 0, kc == 7, [h_, wg], [PG])
        rs = B["rs"]
        self.tt(rs[:], rpf[:], rpb[:], ALU.add, [rpf, rpb], [rs])
        yield
        self.act(ge[:], PG[:], AF.Exp, [PG], [ge], scale=-1.0)
        for h in range(4):
            self.P.op(DVE, (lambda hh: (lambda e: e.bn_stats(out=stats[:, hh, :], in_=rs[:, hh * 128:(hh + 1) * 128])))(h),
                      [rs.b], [stats.b])
            self.P.op(DVE, (lambda hh: (lambda e: e.bn_aggr(out=mv[:, hh, :], in_=stats[:, hh, :])))(h),
                      [stats.b], [mv.b])
        self.ts(mv[:, :, 1], mv[:, :, 1], LN_EPS, None, ALU.add, None, [mv], [mv])
        yield
        self.rsqrt(mv[:, :, 1], mv)
        self.sigm(ge[:], ge)
        yield
        self.tt(sg[:], PG[:], ge[:], ALU.mult, [PG, ge], [sg])
        for h in range(4):
            self.ts(yn[:, h * 128:(h + 1) * 128], rs[:, h * 128:(h + 1) * 128], mv[:, h, 0:1], mv[:, h, 1:2],
                    ALU.subtract, ALU.mult, [rs, mv], [yn])
        yield
        self.tt(ob[:], yn[:], sg[:], ALU.mult, [yn, sg], [ob])
        yield
        for h in range(4):
            self.tr(PT[:, h * 128:(h + 1) * 128], ob[:, h * 128:(h + 1) * 128], self.identb[:], [ob, self.identb], [PT])
        yield
        self.act(oT_[:], PT[:, 0:512].rearrange("p (a b) -> p a b", a=4), AF.Copy, [PT], [oT_])
        yield
        self.dma(self.mixT_d[c, :, 1536:2048], oT_[:].rearrange("p a b -> p (a b)"), [oT_], [self.mixT_d])
        yield

    def ret_front(self, c, B, wqk, wvv):
        I = self.I
        h_ = B["hc"]; tab = B["tab"]; r1 = B["r1"]; r2 = B["r2"]; qkt = B["qkt"]; Rb = B["Rb"]; Q = B["Q"]
        Qb3 = Q[3][:].bitcast(BF16)
        c0 = colof(c * 128)
        self.dma(h_[:], self.hv[:, :, c0:c0 + 128], [self.hT_d], [h_], slow=True)
        self.dma(tab[:], I["ret_tab2"][c * 128:(c + 1) * 128, :], [I["ret_tab2"]], [tab])
        yield
        for j, (q_, w_, lo) in enumerate(((Q[0], wqk, 0), (Q[1], wqk, 512), (Q[2], wvv, 0))):
            for kc in range(8):
                self.mm(q_[:], h_[:, kc, :], w_[:, kc, lo:lo + 512], kc == 0, kc == 7, [h_, w_], [q_])
            yield
        self.tt(r1[:].rearrange("p (a b) -> p a b", a=2), Q[0][:].rearrange("p (a b) -> p a b", a=2),
                tab[:, 0:256].unsqueeze(1).to_broadcast([128, 2, 256]), ALU.mult, [Q[0], tab], [r1])
        yield
        self.tt(r2[:].rearrange("p (a b) -> p a b", a=2), Q[1][:].rearrange("p (a b) -> p a b", a=2),
                tab[:, 256:512].unsqueeze(1).to_broadcast([128, 2, 256]), ALU.mult, [Q[1], tab], [r2])
        self.act(Rb[:, 768:1280], Q[2][:], AF.Copy, [Q[2]], [Rb])
        yield
        self.tt(qkt[:], r1[:], r2[:], ALU.add, [r1, r2], [qkt], eng=POOL)
        yield
        for j in range(4):
            self.tr(Qb3[:, j * 128:(j + 1) * 128], qkt[:, j * 128:(j + 1) * 128], self.identb[:], [qkt, self.identb], [Q[3]])
        self.cp(Rb[:, 512:768], qkt[:, 256:512], [qkt], [Rb], eng=POOL)
        yield
        self.act(Rb[:, 0:512], Qb3[:, 0:512], AF.Copy, [Q[3]], [Rb])
        yield
        self.dma(self.Rb_d[c], Rb[:], [Rb], [self.Rb_d])
        yield

    def ret_sweep(self, l, d_, st, prm):
        DmT, Gam, te, g128 = prm
        S = self.sb(st, [128, 2, 128], F32, "S")
        Sbf = self.sb(st, [128, 2, 128], BF16, "Sbf")
        Rb = [self.sb(st, [128, 1280], BF16, "Rbr") for _ in range(2)]
        qz = [self.sb(st, [128, 2, 128], BF16, "qz") for _ in range(2)]
        qdz = [self.sb(st, [128, 2, 128], BF16, "qdz") for _ in range(2)]
        for par in range(2):
            self.memset(qz[par][:], 0.0, [qz[par]])
            self.memset(qdz[par][:], 0.0, [qdz[par]])
        vte = self.sb(st, [128, 512], BF16, "vte")
        Mr = self.sb(st, [128, 4, 128], BF16, "Mr")
        yp = [self.sb(st, [128, 512], BF16, "ypr") for _ in range(2)]
        Q = [self.ps(st, [128, 512], F32, "QR") for _ in range(3)]
        ydst = self.rpart_d if d_ == 0 else self.rpartb_d
        order = list(range(NCH)) if d_ == 0 else [1, 0] + list(range(NCH - 1, 1, -1))
        self.memset(S[:], 0.0, [S])
        self.memset(Sbf[:], 0.0, [Sbf])
        self.dma(Rb[0][:], self.Rb_d[order[0]], [self.Rb_d], [Rb[0]])
        it = 0
        for ci, c in enumerate(order):
            rb = Rb[it % 2]; ypt = yp[it % 2]
            it += 1
            tok = slice(c * 128, (c + 1) * 128)
            if ci + 1 < len(order):
                self.dma(Rb[it % 2][:], self.Rb_d[order[ci + 1]], [self.Rb_d], [Rb[it % 2]])
            qk = rb[:, 0:512].rearrange("p (a b) -> p a b", a=4)
            for par in range(2):
                rr = 64 * par
                self.cp(qz[par][rr:rr + 64, :, :], qk[rr:rr + 64, 0:2, :], [rb], [qz[par]], eng=POOL)
                self.tt(qdz[par][rr:rr + 64, :, :], qk[rr:rr + 64, 0:2, :], Gam[rr:rr + 64, :, :], ALU.mult,
                        [rb, Gam], [qdz[par]], eng=POOL)
            self.tt(vte[:].rearrange("p (h e) -> p h e", h=4), rb[:, 768:1280].rearrange("p (h e) -> p h e", h=4),
                    te[:].unsqueeze(2).to_broadcast([128, 4, 128]), ALU.mult, [rb, te], [vte], eng=POOL)
            yield
            for h in range(4):
                p = h // 2
                self.mm(Q[0][:, h * 128:(h + 1) * 128], qk[:, 2 + p, :], qz[h % 2][:, p, :], True, True, [rb, qz[h % 2]], [Q[0]])
            yield
            self.tt(Mr[:], Q[0][:].rearrange("p (a b) -> p a b", a=4), DmT[:], ALU.mult, [Q[0], DmT], [Mr])
            yield
            for h in range(4):
                p = h // 2
                self.mm(Q[1][:, h * 128:(h + 1) * 128], Mr[:, h, :], rb[:, 768 + h * 128:768 + (h + 1) * 128], True, False, [Mr, rb], [Q[1]])
                self.mm(Q[1][:, h * 128:(h + 1) * 128], qdz[h % 2][:, p, :], Sbf[:, p, :], False, True, [qdz[h % 2], Sbf], [Q[1]])
            for h in range(4):
                p = h // 2
                self.mm(Q[2][:, h * 128:(h + 1) * 128], rb[:, 512 + p * 128:512 + (p + 1) * 128], vte[:, h * 128:(h + 1) * 128],
                        True, True, [rb, vte], [Q[2]])
            yield
            self.cp(ypt[:], Q[1][:], [Q[1]], [ypt])
            self.dma(ydst[tok, :], ypt[:], [ypt], [ydst])
            for h in range(4):
                p, r0 = h // 2, (h % 2) * 64
                self.stt(S[r0:r0 + 64, p, :], S[r0:r0 + 64, p, :], g128[r0:r0 + 64, p:p + 1], Q[2][r0:r0 + 64, h * 128:(h + 1) * 128],
                         ALU.mult, ALU.add, [S, g128, Q[2]], [S])
            yield
            self.act(Sbf[:], S[:], AF.Copy, [S], [Sbf])
            yield

    def stage_M(self, l):
        I = self.I
        cst = self.cst
        blocks = [(0, 256)] + [(256 + 512 * i, 512) for i in range(8)]
        with ExitStack() as st1:
            v_all = self.sb(st1, [128, NCH, 512], BF16, "v_all")
            with ExitStack() as st:
                wv = I["w_in"][l].rearrange("(k p) c -> p k c", p=128)
                wm = self.sb(st, [128, 8, 704], BF16, "wm")
                wgate = self.sb(st, [128, 8, 512], BF16, "wgate")
                self.load_w(wm[:, :, 0:672], wm, wv[:, :, 2576:3248], I["w_in"])
                self.load_w(wm[:, :, 672:704], wm, wv[:, :, 5296:5328], I["w_in"])
                self.load_w(wgate[:], wgate, wv[:, :, 3248:3760], I["w_in"])
                wuq = self.sb(st, [128, 3, 8, 96], BF16, "wuq")
                wuqs = self.sb(st, [128, 3, 8, 96], BF16, "wuqs")
                wkp = self.sb(st, [128, 2, 8, 96], BF16, "wkp")
                wvv = self.sb(st, [128, 2, 8, 64], BF16, "wvv")
                self.memset(wuqs[:], 0.0, [wuqs])
                self.memset(wkp[:], 0.0, [wkp])
                uqv = I["mla_w_uq"][l].rearrange("(k p) (h e) -> p k h e", p=128, h=8)
                uqs = I["w_uq_sw"][l].rearrange("(k p) (h e) -> p k h e", p=128, h=8)
                ukv = I["mla_w_ukv"][l].rearrange("(k p) (h e) -> p k h e", p=128, h=8)
                for kc in range(3):
                    self.load_w(wuq[:, kc, :, :], wuq, uqv[:, kc, :, :], I["mla_w_uq"])
                    self.load_w(wuqs[:, kc, :, 64:96], wuqs, uqs[:, kc, :, :], I["w_uq_sw"])
                for kc in range(2):
                    self.load_w(wkp[:, kc, :, 0:64], wkp, ukv[:, kc, :, 0:64], I["mla_w_ukv"])
                    self.load_w(wvv[:, kc, :, :], wvv, ukv[:, kc, :, 64:128], I["mla_w_ukv"])
                esel = self.sb(st, [32, 96], BF16, "esel")
                self.memset(esel[:], 0.0, [esel])
                self.cp(esel[:, 64:96], self.identb[0:32, 0:32], [self.identb, esel], [esel])
                qn = self.sb(st, [128, 3], F32, "qn")
                kvn = self.sb(st, [128, 2], F32, "kvn")
                self.dma(qn[:], I["mla_q_norm"][l].rearrange("(k p) -> p k", p=128), [I["mla_q_norm"]], [qn], slow=True)
                self.dma(kvn[:], I["mla_kv_norm"][l].rearrange("(k p) -> p k", p=128), [I["mla_kv_norm"]], [kvn], slow=True)
                hb = [self.sb(st, [128, 8, 512], BF16, "hb") for _ in range(2)]
                tq = self.sb(st, [96, 2, 512], F32, "tq")
                self.memset(tq[0:64, 0, :], 1.0, [tq])
                self.memset(tq[0:64, 1, :], 0.0, [tq])
                tk = self.sb(st, [32, 2, 512], F32, "tk")
                cqs = self.sb(st, [128, 3, 512], F32, "cqs")
                sqs2 = [self.sb(st, [128, 512], F32, "sqs") for _ in range(2)]
                rstd = self.sb(st, [128, 512], F32, "rstd")
                cqn = self.sb(st, [128, 3, 512], BF16, "cqn")
                ckvn = self.sb(st, [128, 2, 512], BF16, "ckvn")
                kr1 = self.sb(st, [32, 512], F32, "kr1")
                kr2 = self.sb(st, [32, 512], F32, "kr2")
                krr = self.sb(st, [32, 512], BF16, "krr")
                q1s = [self.sb(st, [96, 512], F32, "q1") for _ in range(2)]
                q2s = [self.sb(st, [96, 512], F32, "q2") for _ in range(2)]
                qf = [self.sb(st, [96, 512], BF16, "qf") for _ in range(2)]
                kf = [self.sb(st, [96, 512], BF16, "kf") for _ in range(2)]
                ges = [self.sb(st, [128, 512], F32, "ge") for _ in range(2)]
                sgo = [self.sb(st, [128, 512], F32, "sgo") for _ in range(2)]
                P0 = [self.ps(st, [128, 512], F32, "P0") for _ in range(2)]
                PSS = self.ps(st, [128, 512], F32, "PSS")
                PQ1 = self.ps(st, [128, 512], F32, "PQ1")
                PQ2 = self.ps(st, [128, 512], F32, "PQ2")
                PK = self.ps(st, [128, 512], F32, "PK")
                PVv = self.ps(st, [128, 512], F32, "PVv")
                PG = self.ps(st, [128, 512], F32, "PG")
                sgv = self.sgT_d[:].rearrange("(r p) t -> p r t", p=128)
                ctr = 0
                for bi, (t0, n) in enumerate(blocks):
                    h_ = hb[bi % 2]
                    c0 = colof(t0)
                    self.dma(h_[:, :, 0:n], self.hv[:, :, c0:c0 + n], [self.hT_d], [h_], slow=True)
                    self.dma(tq[64:96, :, 0:n], I["mla_tab"][:, :, t0:t0 + n], [I["mla_tab"]], [tq], slow=True)
                    self.dma(tk[:, :, 0:n], I["mla_tab"][:, :, t0:t0 + n], [I["mla_tab"]], [tk], slow=True)
                    for (nrc, off, dst, nrm, dim, keep) in ((3, 0, cqn, qn, 384.0, None), (2, 384, ckvn, kvn, 256.0, None)):
                        for rc in range(nrc):
                            p_ = P0[ctr % 2]; ctr += 1
                            for kc in range(8):
                                self.mm(p_[:, 0:n], wm[:, kc, off + rc * 128:off + (rc + 1) * 128], h_[:, kc, 0:n], kc == 0, kc == 7, [wm, h_], [p_])
                            sqs = sqs2[ctr % 2]
                            self.act(cqs[:, rc, 0:n], p_[:, 0:n], AF.Copy, [p_], [cqs])
                            self.act(sqs[:, 0:n], p_[:, 0:n], AF.Square, [p_], [sqs])
                            self.mm(PSS[:, 0:n], self.ones[:], sqs[:, 0:n], rc == 0, rc == nrc - 1, [self.ones, sqs], [PSS])
                        self.ts(rstd[:, 0:n], PSS[:, 0:n], 1.0 / dim, RMS_EPS, ALU.mult, ALU.add, [PSS], [rstd])
                        self.rsqrt(rstd[:, 0:n], rstd)
                        for rc in range(nrc):
                            self.stt(dst[:, rc, 0:n], cqs[:, rc, 0:n], nrm[:, rc:rc + 1], rstd[:, 0:n], ALU.mult, ALU.mult, [cqs, nrm, rstd], [dst])
                    self.mm_group_kr(h_, n, wm, PQ1, PQ2)
                    self.tt(kr1[:, 0:n], PQ1[0:32, 0:n], tk[:, 0, 0:n], ALU.mult, [PQ1, tk], [kr1])
                    self.tt(kr2[:, 0:n], PQ2[0:32, 0:n], tk[:, 1, 0:n], ALU.mult, [PQ2, tk], [kr2])
                    self.tt(krr[:, 0:n], kr1[:, 0:n], kr2[:, 0:n], ALU.add, [kr1, kr2], [krr])
                    for h in range(8):
                        qf_ = qf[h % 2]; kf_ = kf[h % 2]
                        q1 = q1s[h % 2]; q2 = q2s[h % 2]
                        PQ1_, PQ2_, PK_ = (PQ1, PQ2, PK) if h % 2 == 0 else (P0[0], P0[1], PG)
                        for kc in range(3):
                            self.mm(PQ1_[0:96, 0:n], wuq[:, kc, h, :], cqn[:, kc, 0:n], kc == 0, kc == 2, [wuq, cqn], [PQ1_])
                        for kc in range(3):
                            self.mm(PQ2_[0:96, 0:n], wuqs[:, kc, h, :], cqn[:, kc, 0:n], kc == 0, kc == 2, [wuqs, cqn], [PQ2_])
                        self.tt(q1[:, 0:n], PQ1_[0:96, 0:n], tq[:, 0, 0:n], ALU.mult, [PQ1_, tq], [q1])
                        self.tt(q2[:, 0:n], PQ2_[0:96, 0:n], tq[:, 1, 0:n], ALU.mult, [PQ2_, tq], [q2])
                        self.tt(qf_[:, 0:n], q1[:, 0:n], q2[:, 0:n], ALU.add, [q1, q2], [qf_], eng=POOL)
                        self.dma(self.qT_d[h, :, t0:t0 + n], qf_[:, 0:n], [qf_], [self.qT_d])
                        for kc in range(2):
                            self.mm(PK_[0:96, 0:n], wkp[:, kc, h, :], ckvn[:, kc, 0:n], kc == 0, False, [wkp, ckvn], [PK_])
                        self.mm(PK_[0:96, 0:n], esel[:], krr[:, 0:n], False, True, [esel, krr], [PK_])
                        self.act(kf_[:, 0:n], PK_[0:96, 0:n], AF.Copy, [PK_], [kf_])
                        self.dma(self.kfT_d[h, :, t0:t0 + n], kf_[:, 0:n], [kf_], [self.kfT_d])
                    for s in range(n // 128):
                        ch = (t0 + s * 128) // 128
                        PV_ = PVv if s % 2 == 0 else PSS
                        for kc in range(2):
                            self.mm(PV_[:], ckvn[:, kc, s * 128:(s + 1) * 128], wvv[:, kc, :, :].rearrange("p h e -> p (h e)"),
                                    kc == 0, kc == 1, [ckvn, wvv], [PV_])
                        self.act(v_all[:, ch, :], PV_[:], AF.Copy, [PV_], [v_all])
                    for rc in range(4):
                        sg_ = sgo[rc % 2]
                        ge = ges[rc % 2]
                        PG_ = PG if rc % 2 == 0 else PK
                        for kc in range(8):
                            self.mm(PG_[:, 0:n], wgate[:, kc, rc * 128:(rc + 1) * 128], h_[:, kc, 0:n], kc == 0, kc == 7, [wgate, h_], [PG_])
                        self.act(ge[:, 0:n], PG_[:, 0:n], AF.Exp, [PG_], [ge], scale=-1.0)
                        self.sigm(ge[:, 0:n], ge)
                        self.tt(sg_[:, 0:n], PG_[:, 0:n], ge[:, 0:n], ALU.mult, [PG_, ge], [sg_])
                        self.dma(sgv[:, rc, t0:t0 + n], sg_[:, 0:n], [sg_], [self.sgT_d])
            self.P.barrier()
            import os as _os
            if _os.environ.get("NO_M2"):
                return
            with ExitStack() as st:
                kfh = [self.sb(st, [96, NTOK], BF16, "kfh") for _ in range(2)]
                qh = [self.sb(st, [96, NTOK], BF16, "qh") for _ in range(2)]
                vaug = [self.sb(st, [128, NCH, 128], BF16, "vaug") for _ in range(2)]
                self.memset(vaug[0][:, :, 64:128], 1.0, [vaug[0]])
                self.memset(vaug[1][:, :, 0:64], 1.0, [vaug[1]])
                pT = [self.sb(st, [128, 1024], BF16, "pT") for _ in range(3)]
                sgh = [self.sb(st, [128, 512], F32, "sgh") for _ in range(2)]
                rden = self.sb(st, [128, 512], F32, "rden")
                ot = self.sb(st, [128, 512], F32, "ot")
                ob = [self.sb(st, [128, 512], BF16, "ob") for _ in range(2)]
                PSc = [self.ps(st, [128, 1024], F32, "PSc") for _ in range(3)]
                PO = [self.ps(st, [128, 512], F32, "PO") for _ in range(2)]
                ci = 0
                bi_ = 0
                for h in range(int(_os.environ.get("M2_HEADS", "8"))):
                    par = h % 2
                    r0 = 64 * par
                    d0 = 64 - r0
                    kf_ = kfh[h % 2]; q_ = qh[h % 2]; va = vaug[par]
                    self.dma(kf_[:], self.kfT_d[h], [self.kfT_d], [kf_])
                    self.dma(q_[:], self.qT_d[h], [self.qT_d], [q_])
                    self.cp(va[:, :, r0:r0 + 64], v_all[:, :, h * 64:(h + 1) * 64], [v_all], [va], eng=POOL)
                    its = []
                    for (t0, n) in blocks:
                        if t0 == 0:
                            if l == DEPTH - 1:
                                continue
                            kcs = [0, 1]
                        else:
                            kcs = list(range(NCH))
                        po = PO[bi_ % 2]; sg_ = sgh[bi_ % 2]; ob_ = ob[bi_ % 2]
                        bi_ += 1
                        npair = len(kcs) // 2
                        for i in range(npair):
                            its.append((t0, n, kcs[2 * i], kcs[2 * i + 1], i == 0, i == npair - 1, po, sg_, ob_))
                    LOOK = 2
                    for j in range(len(its) + LOOK):
                        if j < len(its):
                            (t0, n, ka, kb, first, last, po, sg_, ob_) = its[j]
                            if first:
                                self.dma(sg_[r0:r0 + 64, 0:n], self.sgT_d[64 * h:64 * (h + 1), t0:t0 + n], [self.sgT_d], [sg_])
                            psc = PSc[(ci + j) % 3]
                            self.mm(psc[:, 0:n], kf_[:, ka * 128:(ka + 1) * 128], q_[:, t0:t0 + n], True, True, [kf_, q_], [psc])
                            self.mm(psc[:, 512:512 + n], kf_[:, kb * 128:(kb + 1) * 128], q_[:, t0:t0 + n], True, True, [kf_, q_], [psc])
                        jj = j - LOOK
                        if jj >= 0:
                            (t0, n, ka, kb, first, last, po, sg_, ob_) = its[jj]
                            psc = PSc[(ci + jj) % 3]; pt = pT[(ci + jj) % 3]
                            self.act(pt[:].rearrange("p (a b) -> p a b", a=2)[:, :, 0:n], psc[:].rearrange("p (a b) -> p a b", a=2)[:, :, 0:n],
                                     AF.Exp, [psc], [pt], scale=MLA_SCALE)
                            self.mm(po[:, 0:n], va[:, ka, :], pt[:, 0:n], first, False, [va, pt], [po])
                            self.mm(po[:, 0:n], va[:, kb, :], pt[:, 512:512 + n], False, last, [va, pt], [po])
                            if last:
                                self.P.op(DVE, (lambda a_, b_: (lambda e: e.reciprocal(out=a_, in_=b_)))(rden[d0:d0 + 64, 0:n], po[d0:d0 + 64, 0:n]),
                                          [po.b], [rden.b])
                                self.tt(ot[r0:r0 + 64, 0:n], po[r0:r0 + 64, 0:n], rden[d0:d0 + 64, 0:n], ALU.mult, [po, rden], [ot])
                                self.tt(ob_[r0:r0 + 64, 0:n], ot[r0:r0 + 64, 0:n], sg_[r0:r0 + 64, 0:n], ALU.mult, [ot, sg_], [ob_], eng=POOL)
                                kcm = 8 + h // 2
                                self.dma(self.mixT_d[t0 // 128:(t0 + n) // 128, r0:r0 + 64, kcm * 128:(kcm + 1) * 128].rearrange("c p t -> p c t"),
                                         ob_[r0:r0 + 64, 0:n].rearrange("p (c t) -> p c t", t=128), [ob_], [self.mixT_d], slow=True)
                    ci += len(its)

    def mm_group_kr(self, h_, n, wm, PQ1, PQ2):
        for kc in range(8):
            self.mm(PQ1[0:32, 0:n], wm[:, kc, 640:672], h_[:, kc, 0:n], kc == 0, kc == 7, [wm, h_], [PQ1])
        for kc in range(8):
            self.mm(PQ2[0:32, 0:n], wm[:, kc, 672:704], h_[:, kc, 0:n], kc == 0, kc == 7, [wm, h_], [PQ2])

    def stage_E(self, l):
        I = self.I
        with ExitStack() as st:
            wo = self.sb(st, [128, 16, 1024], BF16, "wo")
            wov = I["w_out"][l].rearrange("(k p) c -> p k c", p=128)
            for kc in range(16):
                self.load_w(wo[:, kc, :], wo, wov[:, kc, :], I["w_out"])
            lng = self.bvec(st, "ln_g", l, 1024)
            lnb = self.bvec(st, "ln_b", l, 1024)
            sets = []
            for s_ in range(3):
                B = {}
                B["mt"] = self.sb(st, [128, 16, 128], BF16, "mt")
                B["xt"] = self.sb(st, [128, D], F32, "xt")
                B["v1"] = self.sb(st, [128, D], F32, "v1")
                B["v2"] = self.sb(st, [128, D], F32, "v2")
                B["xo"] = self.sb(st, [128, D], F32, "xo")
                B["stats"] = self.sb(st, [128, 2, 6], F32, "stats")
                B["mv"] = self.sb(st, [128, 2], F32, "mv")
                B["PZ"] = [self.ps(st, [128, 512], F32, "PZ") for _ in range(2)]
                sets.append(B)
            tiles = list(range(NCH)) if l < DEPTH - 1 else list(range(2, NCH))
            gens = [(lambda tt_: (lambda slot: self.e_tile(l, tt_, sets[slot], wo, lng, lnb)))(t) for t in tiles]
            self.interleave(gens, 3)

    def e_tile(self, l, t, B, wo, lng, lnb):
        I = self.I
        m_ = B["mt"]; x_ = B["xt"]; v1 = B["v1"]; v2 = B["v2"]; xo_ = B["xo"]; stats = B["stats"]; mv = B["mv"]; PZ = B["PZ"]
        typ = 1 if t < 2 else 0
        tok = slice(t * 128, (t + 1) * 128)
        self.dma(m_[:].rearrange("p a b -> p (a b)"), self.mixT_d[t], [self.mixT_d], [m_])
        if l == 0:
            src_t = I["ctx"] if t < 2 else I["x"]
            src = src_t[t * 128:(t + 1) * 128, :] if t < 2 else src_t[(t - 2) * 128:(t - 1) * 128, :]
        else:
            src_t = self.xres_d
            src = src_t[tok, :]
        self.dma(x_[:], src, [src_t], [x_])
        yield
        for nb in range(2):
            for kc in range(16):
                self.mm(PZ[nb][:], m_[:, kc, :], wo[:, kc, nb * 512:(nb + 1) * 512], kc == 0, kc == 15, [m_, wo], [PZ[nb]])
            yield
        for nb in range(2):
            self.tt(v1[:, nb * 512:(nb + 1) * 512], PZ[nb][:], self.gB[:, typ, nb * 512:(nb + 1) * 512], ALU.mult, [PZ[nb], self.gB], [v1])
        yield
        self.stt(v2[:], x_[:], ALPHA, v1[:], ALU.mult, ALU.add, [x_, v1], [v2])
        yield
        for s in range(2):
            self.P.op(DVE, (lambda ss: (lambda e: e.bn_stats(out=stats[:, ss, :], in_=v2[:, ss * 512:(ss + 1) * 512])))(s),
                      [v2.b], [stats.b])
        self.P.op(DVE, lambda e: e.bn_aggr(out=mv[:], in_=stats[:]), [stats.b], [mv.b])
        self.ts(mv[:, 1:2], mv[:, 1:2], LN_EPS, None, ALU.add, None, [mv], [mv])
        yield
        self.rsqrt(mv[:, 1:2], mv)
        yield
        self.ts(v1[:], v2[:], mv[:, 0:1], mv[:, 1:2], ALU.subtract, ALU.mult, [v2, mv], [v1])
        yield
        self.tt(v2[:], v1[:], lng[:], ALU.mult, [v1, lng], [v2], eng=POOL)
        yield
        self.tt(xo_[:], v2[:], lnb[:], ALU.add, [v2, lnb], [xo_], eng=POOL)
        yield
        if l < DEPTH - 1:
            self.dma(self.xres_d[tok, :], xo_[:], [xo_], [self.xres_d])
        else:
            self.dma(self.out[(t - 2) * 128:(t - 1) * 128, :], xo_[:], [xo_], [self.out])
        yield


C_ID = 0
C_UF = 128
C_LF = 256
C_UB = 384
C_LB = 512
C_MF = 640
C_MB = 768
C_RIF = 896
C_RIB = 1024
C_GF = 1152
C_GB = 1280
C_TEF = 1408
C_TEB = 1409
CST_W = 1410


def make_consts():
    k = np.arange(128)[:, None].astype(np.float32)
    i = np.arange(128)[None, :].astype(np.float32)
    cst = np.zeros((128, CST_W), np.float32)
    cst[:, C_ID:C_ID + 128] = (k == i)
    cst[:, C_UF:C_UF + 128] = (k <= i)
    cst[:, C_LF:C_LF + 128] = (k > i)
    cst[:, C_UB:C_UB + 128] = (k >= i)
    cst[:, C_LB:C_LB + 128] = (k < i)
    cst[:, C_MF:C_MF + 128] = (k <= i)
    cst[:, C_MB:C_MB + 128] = (k >= i)
    cst[:, C_RIF:C_RIF + 128] = np.maximum(i - k, 0)
    cst[:, C_RIB:C_RIB + 128] = np.maximum(k - i, 0)
    cst[:, C_GF:C_GF + 128] = np.broadcast_to(i + 1, (128, 128))
    cst[:, C_GB:C_GB + 128] = np.broadcast_to(128 - i, (128, 128))
    cst[:, C_TEF] = 127 - k[:, 0]
    cst[:, C_TEB] = k[:, 0]
    return cst


def rope_tables():
    rows = SEQ // 64
    t = np.arange(rows * 64)
    row = (t // 64).astype(np.float32)
    col = (t % 64).astype(np.float32)

    def cs(rot):
        nf = rot // 4
        inv = (np.float32(10000.0) ** (-np.arange(nf, dtype=np.float32) / np.float32(nf))).astype(np.float32)
        ang = np.concatenate([row[:, None] * inv, col[:, None] * inv], -1).astype(np.float32)
        return np.cos(ang).astype(np.float32), np.sin(ang).astype(np.float32)
    cm, sm = cs(32)
    mla = np.zeros((32, 2, NTOK), np.float32)
    mla[:, 0, :CTX] = 1.0
    mla[0:16, 0, CTX:] = cm.T; mla[16:32, 0, CTX:] = cm.T
    mla[0:16, 1, CTX:] = -sm.T; mla[16:32, 1, CTX:] = sm.T
    cr, sr = cs(64)
    ret = np.zeros((128, 4, NTOK), np.float32)
    ret[:, 0, :CTX] = 1.0
    for hh in range(2):
        b = hh * 64
        ret[b:b + 32, 0, CTX:] = cr.T; ret[b + 32:b + 64, 0, CTX:] = cr.T
        ret[b:b + 32, 1, CTX:] = -sr.T; ret[b + 32:b + 64, 1, CTX:] = sr.T
    ret[:, 2] = ret[:, 0] * 0.125
    ret[:, 3] = ret[:, 1] * 0.125
    ret2 = np.zeros((NTOK, 1024), np.float32)
    cc = np.ones((NTOK, 4, 64), np.float32)
    ss = np.zeros((NTOK, 4, 64), np.float32)
    cc[CTX:, :, 0:32] = cr[:, None, :]; cc[CTX:, :, 32:64] = cr[:, None, :]
    ss[CTX:, :, 0:32] = -sr[:, None, :]; ss[CTX:, :, 32:64] = sr[:, None, :]
    ret2 = np.zeros((NTOK, 512), np.float32)
    ret2[:, 0:256] = cc.reshape(NTOK, 256)
    ret2[:, 256:512] = ss.reshape(NTOK, 256)
    return mla, ret, ret2


_CACHE = {}


def prep_inputs(inputs):
    f = lambda a: np.ascontiguousarray(np.asarray(a, dtype=np.float32))
    w_in = f(inputs["w_in"])
    kr = w_in[:, :, 3216:3248]
    kr_sw = np.concatenate([kr[:, :, 16:32], kr[:, :, 0:16]], -1)

    def sw64(a):
        a = a.reshape(2, D, 4, 2, 32)
        return np.ascontiguousarray(a[:, :, :, ::-1, :]).reshape(2, D, 256)
    q_sw = sw64(w_in[:, :, 3760:4016])
    k_sw = sw64(w_in[:, :, 4016:4272])
    w_ext = np.ascontiguousarray(np.concatenate([w_in, kr_sw, q_sw, k_sw], -1))
    uq = f(inputs["mla_w_uq"]).reshape(2, 384, 8, 96)
    uq_r = uq[:, :, :, 64:96]
    uq_sw = np.ascontiguousarray(np.concatenate([uq_r[..., 16:32], uq_r[..., 0:16]], -1)).reshape(2, 384, 256)
    mla_tab, ret_tab, ret_tab2 = rope_tables()
    shared = {
        "w_ada": f(inputs["w_ada"]), "b_ada": f(inputs["b_ada"]), "w_in": w_ext,
        "ssd_conv_w": f(inputs["ssd_conv_w"]), "ssd_conv_b": f(inputs["ssd_conv_b"]),
        "ssd_norm_w": f(inputs["ssd_norm_w"]), "mla_q_norm": f(inputs["mla_q_norm"]),
        "mla_w_uq": f(inputs["mla_w_uq"]), "w_uq_sw": uq_sw, "mla_kv_norm": f(inputs["mla_kv_norm"]),
        "mla_w_ukv": f(inputs["mla_w_ukv"]), "ret_log_rate_f": f(inputs["ret_log_rate_f"]),
        "ret_log_rate_b": f(inputs["ret_log_rate_b"]), "w_out": f(inputs["w_out"]),
        "ln_g": f(inputs["ln_g"]), "ln_b": f(inputs["ln_b"]),
        "cst": make_consts(), "mla_tab": mla_tab, "ret_tab": ret_tab, "ret_tab2": ret_tab2,
    }
    for n in ("ssd_a_log_f", "ssd_a_log_b", "ssd_dt_bias_f", "ssd_dt_bias_b", "ssd_d"):
        shared[n] = f(inputs[n])
    x = f(inputs["x"]); c = f(inputs["c"]); ctx = f(inputs["ctx"]); c_ctx = f(inputs["c_ctx"])
    maps = []
    for b in range(8):
        m = dict(shared)
        m["x"] = x[b]
        m["ctx"] = ctx[b]
        m["cvec"] = np.ascontiguousarray(np.stack([c[b], c_ctx], 0))
        maps.append(m)
    return maps


def kernel(**inputs):
    if "nc" not in _CACHE:
        _CACHE["nc"] = K().build()
    nc = _CACHE["nc"]
    maps = prep_inputs(inputs)
    res = run_bass_kernel_spmd(nc, maps, core_ids=list(range(8)))
    return np.stack([np.asarray(r["out"], dtype=np.float32) for r in res.results], 0)
```

```python
import math
from contextlib import ExitStack
import numpy as np
import concourse.bass as bass
import concourse.mybir as mybir
from concourse.bass_utils import run_bass_kernel_spmd

F32 = mybir.dt.float32
BF16 = mybir.dt.bfloat16
AF = mybir.ActivationFunctionType
ALU = mybir.AluOpType

PE, ACT, DVE, POOL, SP = "pe", "act", "dve", "pool", "sp"
EPOCH = 30000
DMA_K = 8
DMA_EPOCH = 1800

D = 1024
SEQ = 4096
CTX = 256
NTOK = SEQ + CTX
NCH = NTOK // 128
HC = NTOK + 8
DEPTH = 2
ALPHA = (2 * DEPTH) ** 0.25
LN_EPS = 1e-5
RMS_EPS = 1e-6
MLA_SCALE = 96 ** -0.5
WEXT = 5296 + 32 + 256 + 256


def colof(t):
    return t + 2 if t < CTX else t + 6


class Buf:
    __slots__ = ("name", "lw", "rd", "rdd", "psum")

    def __init__(self, name=""):
        self.name = name
        self.lw = None
        self.rd = {}
        self.rdd = []
        self.psum = False


class Op:
    __slots__ = ("eng", "fn", "deps", "sig", "idx", "dma_slot", "dma_prev")

    def __init__(self, eng, fn):
        self.eng = eng
        self.fn = fn
        self.deps = set()
        self.sig = None
        self.dma_slot = None
        self.dma_prev = None


class Prog:
    def __init__(self, nc):
        self.nc = nc
        self.ops = []
        self.eng = {PE: nc.tensor, ACT: nc.scalar, DVE: nc.vector, POOL: nc.gpsimd, SP: nc.sync}
        self.dma_lists = {}
        self.last = {}

    def op(self, eng, fn, reads=(), writes=(), dma=False):
        o = Op(eng, fn)
        o.idx = len(self.ops)
        for b in reads:
            if b.lw is not None:
                o.deps.add(b.lw)
            if b.psum:
                for e2, r in b.rd.items():
                    if e2 != eng:
                        o.deps.add(r)
        for b in writes:
            if b.lw is not None:
                o.deps.add(b.lw)
            for r in b.rd.values():
                o.deps.add(r)
            for r in b.rdd:
                o.deps.add(r)
        for b in reads:
            if dma:
                b.rdd.append(o.idx)
            else:
                b.rd[eng] = o.idx
        for b in writes:
            b.lw = o.idx
            b.rd = {}
            b.rdd = []
        if dma:
            lst = self.dma_lists.setdefault(eng, [])
            o.dma_slot = len(lst)
            if len(lst) >= DMA_K:
                o.dma_prev = lst[len(lst) - DMA_K]
            lst.append(o.idx)
        o.deps.discard(o.idx)
        self.ops.append(o)
        self.last[eng] = o.idx
        return o

    def barrier(self):
        bufs = {}
        for e in (PE, ACT, DVE, POOL, SP):
            bufs[e] = Buf("bar" + e)
            o = self.op(e, lambda en: en.nop(), writes=[bufs[e]])
            for lst in self.dma_lists.values():
                for d in lst[-DMA_K:]:
                    if d != o.idx:
                        o.deps.add(d)
        for e in (PE, ACT, DVE, POOL, SP):
            self.op(e, lambda en: en.nop(), reads=list(bufs.values()))

    def emit(self, stack):
        nc = self.nc
        ops = self.ops
        needed = set()
        for o in ops:
            for d in o.deps:
                do = ops[d]
                if do.eng == o.eng and o.eng == PE and do.dma_slot is None:
                    continue
                needed.add(d)
            if o.dma_prev is not None:
                needed.add(o.dma_prev)
        cnt = {}
        sems = {}
        dma_sems = {}
        for o in ops:
            if o.dma_slot is not None:
                k = o.dma_slot % DMA_K
                n = o.dma_slot // DMA_K
                key = (o.eng, k, n // DMA_EPOCH)
                if key not in dma_sems:
                    dma_sems[key] = stack.enter_context(nc.semaphore("dq%s%d_%d" % key))
                o.sig = (dma_sems[key], 16 * (n % DMA_EPOCH + 1))
            elif o.idx in needed:
                c = cnt.get(o.eng, 0)
                key = (o.eng, c // EPOCH)
                if key not in sems:
                    sems[key] = stack.enter_context(nc.semaphore("s%s_%d" % key))
                o.sig = (sems[key], c % EPOCH + 1)
                cnt[o.eng] = c + 1
        waited = {}
        nw = 0
        for o in ops:
            e = self.eng[o.eng]
            deps = set(o.deps)
            if o.dma_prev is not None:
                deps.add(o.dma_prev)
            for d in sorted(deps):
                do = ops[d]
                if do.sig is None:
                    continue
                sem, val = do.sig
                key = (o.eng, id(sem))
                if waited.get(key, 0) >= val:
                    continue
                waited[key] = val
                e.wait_ge(sem, val)
                nw += 1
            ins = o.fn(e)
            if o.sig is not None:
                sem, val = o.sig
                ins.then_inc(sem, 16 if o.dma_slot is not None else 1)
        self.nwaits = nw


class T:
    __slots__ = ("t", "b")

    def __init__(self, t, name=""):
        self.t = t
        self.b = Buf(name)

    def __getitem__(self, k):
        return self.t[k]


class K:
    def __init__(self, debug=False, stop_after=None, skip=()):
        self.skip = set(skip)
        self.debug = debug
        self.stop_after = stop_after
        self.nc = bass.Bass("TRN2", target_bir_lowering=False)
        self.P = Prog(self.nc)
        self.uid = 0

    def dram(self, name, shape, dt, kind="Internal"):
        return T(self.nc.dram_tensor(name, list(shape), dt, kind=kind).ap(), name)

    def sb(self, st, shape, dt, name=None):
        self.uid += 1
        name = "%s_%d" % (name or "t", self.uid)
        return T(st.enter_context(self.nc.sbuf_tensor(name, list(shape), dt)), name)

    def ps(self, st, shape, dt=F32, name=None):
        self.uid += 1
        name = "%s_%d" % (name or "p", self.uid)
        t = T(st.enter_context(self.nc.psum_tensor(name, list(shape), dt)), name)
        t.b.psum = True
        return t

    def dma(self, out, in_, reads, writes, eng=SP, slow=False):
        if slow:
            return self.P.op(eng, lambda e: e.dma_start(out=out, in_=in_, allow_slow_non_contiguous=True),
                             [x.b for x in reads], [x.b for x in writes], dma=True)
        return self.P.op(eng, lambda e: e.dma_start(out=out, in_=in_), [x.b for x in reads], [x.b for x in writes], dma=True)

    def mm(self, out, lhsT, rhs, start, stop, reads, writes):
        return self.P.op(PE, lambda e: e.matmul(out, lhsT=lhsT, rhs=rhs, start=start, stop=stop),
                         [x.b for x in reads], [x.b for x in writes])

    def tr(self, out, in_, ident, reads, writes):
        return self.P.op(PE, lambda e: e.transpose(out=out, in_=in_, identity=ident),
                         [x.b for x in reads], [x.b for x in writes])

    def act(self, out, in_, func, reads, writes, bias=None, scale=None, accum_out=None):
        kw = {}
        if bias is not None:
            kw["bias"] = bias
        if scale is not None:
            kw["scale"] = scale
        if accum_out is not None:
            kw["accum_out"] = accum_out
        return self.P.op(ACT, lambda e: e.activation(out=out, in_=in_, func=func, **kw),
                         [x.b for x in reads], [x.b for x in writes])

    def tt(self, out, in0, in1, op, reads, writes, eng=DVE):
        return self.P.op(eng, lambda e: e.tensor_tensor(out=out, in0=in0, in1=in1, op=op),
                         [x.b for x in reads], [x.b for x in writes])

    def ts(self, out, in0, s1, s2, op0, op1, reads, writes, eng=DVE):
        if op1 is None:
            return self.P.op(eng, lambda e: e.tensor_scalar(out=out, in0=in0, scalar1=s1, scalar2=None, op0=op0),
                             [x.b for x in reads], [x.b for x in writes])
        return self.P.op(eng, lambda e: e.tensor_scalar(out=out, in0=in0, scalar1=s1, scalar2=s2, op0=op0, op1=op1),
                         [x.b for x in reads], [x.b for x in writes])

    def stt(self, out, in0, scalar, in1, op0, op1, reads, writes, eng=DVE):
        return self.P.op(eng, lambda e: e.scalar_tensor_tensor(out=out, in0=in0, scalar=scalar, in1=in1, op0=op0, op1=op1),
                         [x.b for x in reads], [x.b for x in writes])

    def cp(self, out, in_, reads, writes, eng=DVE):
        return self.P.op(eng, lambda e: e.tensor_copy(out=out, in_=in_), [x.b for x in reads], [x.b for x in writes])

    def memset(self, out, val, writes, eng=POOL):
        return self.P.op(eng, lambda e: e.memset(out, val), [], [x.b for x in writes])

    def sigm(self, ap, t):
        self.act(ap, ap, AF.Ln, [t], [t], bias=1.0)
        self.act(ap, ap, AF.Exp, [t], [t], scale=-1.0)

    def rsqrt(self, ap, t):
        self.act(ap, ap, AF.Ln, [t], [t])
        self.act(ap, ap, AF.Exp, [t], [t], scale=-0.5)

    def build(self):
        nc = self.nc
        dbg = self.debug
        I = {}

        def inp(name, shape):
            I[name] = self.dram(name, shape, F32, kind="ExternalInput")
        inp("x", [SEQ, D]); inp("ctx", [CTX, D]); inp("cvec", [2, D])
        inp("w_ada", [2, D, 3 * D]); inp("b_ada", [2, 3 * D]); inp("w_in", [2, D, WEXT])
        inp("ssd_conv_w", [2, 5, 1536]); inp("ssd_conv_b", [2, 1536])
        for n in ("ssd_a_log_f", "ssd_a_log_b", "ssd_dt_bias_f", "ssd_dt_bias_b", "ssd_d"):
            inp(n, [2, 16])
        inp("ssd_norm_w", [2, 1024]); inp("mla_q_norm", [2, 384]); inp("mla_w_uq", [2, 384, 768])
        inp("w_uq_sw", [2, 384, 256]); inp("mla_kv_norm", [2, 256]); inp("mla_w_ukv", [2, 256, 1024])
        inp("ret_log_rate_f", [2, 4]); inp("ret_log_rate_b", [2, 4]); inp("w_out", [2, 2048, D])
        inp("ln_g", [2, D]); inp("ln_b", [2, D])
        inp("cst", [128, CST_W]); inp("mla_tab", [32, 2, NTOK]); inp("ret_tab", [128, 4, NTOK]); inp("ret_tab2", [NTOK, 512])
        self.I = I
        okind = "ExternalOutput" if dbg else "Internal"
        self.out = self.dram("out", [SEQ, D], F32, kind="ExternalOutput")
        self.hT_d = self.dram("hT_d", [128, 8 * HC], BF16, kind=okind)
        self.ypart_d = self.dram("ypart_d", [NTOK, 1024], BF16)
        self.rpart_d = self.dram("rpart_d", [NTOK, 512], BF16)
        self.ypartb_d = self.dram("ypartb_d", [NTOK, 1024], BF16)
        self.Fb_d = self.dram("Fb_d", [NCH, 2, 128, 768], BF16)
        self.Rb_d = self.dram("Rb_d", [NCH, 128, 1280], BF16)
        self.Ff_d = self.dram("Ff_d", [NCH, 2, 128, 136], F32)
        self.rpartb_d = self.dram("rpartb_d", [NTOK, 512], BF16)
        self.mixT_d = self.dram("mixT_d", [NCH, 128, 2048], BF16, kind=okind)
        self.qT_d = self.dram("qT_d", [8, 96, NTOK], BF16)
        self.kfT_d = self.dram("kfT_d", [8, 96, NTOK], BF16)
        self.sgT_d = self.dram("sgT_d", [512, NTOK], F32)
        self.xres_d = self.dram("xres_d", [NTOK, D], F32, kind=okind)

        with ExitStack() as gst:
            self.gst = gst
            self.cst = self.sb(gst, [128, CST_W], F32, "cst")
            self.dma(self.cst[:], I["cst"][:], [I["cst"]], [self.cst])
            self.identb = self.sb(gst, [128, 128], BF16, "identb")
            self.cp(self.identb[:], self.cst[:, C_ID:C_ID + 128], [self.cst], [self.identb])
            self.ones = self.sb(gst, [128, 128], F32, "ones")
            self.memset(self.ones[:], 1.0, [self.ones])
            self.gB = self.sb(gst, [128, 2, 1024], F32, "gB")
            zt = self.sb(gst, [128, 8, 4], BF16, "zt")
            self.memset(zt[:], 0.0, [zt])
            hv = self.hT_d[:].rearrange("p (k c) -> p k c", k=8)
            self.hv = hv
            for (a, b) in ((0, 2), (258, 262), (4358, 4360)):
                self.dma(hv[:, :, a:b], zt[:, :, 0:b - a], [zt], [self.hT_d], slow=True)
            self.P.barrier()
            stages = []
            for l in range(DEPTH):
                stages += [("A", l), ("S", l), ("M", l), ("R", l), ("E", l)]
            for (s, l) in stages:
                if s in self.skip:
                    continue
                if s == "A":
                    self.stage_A(l)
                elif s == "S":
                    self.stage_S(l)
                elif s == "M":
                    self.stage_M(l)
                elif s == "R":
                    self.stage_R(l)
                else:
                    self.stage_E(l)
                self.P.barrier()
                if self.stop_after == (s, l):
                    break
            self.P.barrier()
            self.P.emit(gst)
        return nc

    def silu_psum(self, st, src_ap, src_t, out_ap, out_t, e_t, e_ap, r_ap):
        self.act(e_ap, src_ap, AF.Exp, [src_t], [e_t], scale=-1.0)
        self.sigm(e_ap, e_t)
        self.tt(out_ap, src_ap, r_ap, ALU.mult, [src_t, e_t], [out_t])

    def stage_A(self, l):
        I = self.I
        with ExitStack() as st:
            wada = [self.sb(st, [128, 8, 512], F32, "wada") for _ in range(2)]
            craw = self.sb(st, [128, 8, 2], F32, "craw")
            ce = self.sb(st, [128, 8, 2], F32, "ce")
            scT = self.sb(st, [128, 8, 2], F32, "scT")
            modT = self.sb(st, [128, 24, 2], F32, "modT")
            scale1 = self.sb(st, [128, 8, 2], F32, "scale1")
            brow = self.sb(st, [1, 3 * D], F32, "brow")
            pm = self.ps(st, [128, 512], F32, "pm")
            pg = [self.ps(st, [128, 512], F32, "pg") for _ in range(2)]
            for j in range(2):
                self.dma(craw[:, :, j], I["cvec"][j].rearrange("(k p) -> p k", p=128), [I["cvec"]], [craw], slow=True)
            self.dma(brow[:], I["b_ada"][l:l + 1, :], [I["b_ada"]], [brow])
            self.act(ce[:], craw[:], AF.Exp, [craw], [ce], scale=-1.0)
            self.sigm(ce[:], ce)
            self.tt(scT[:], craw[:], ce[:], ALU.mult, [craw, ce], [scT])
            wv = I["w_ada"][l].rearrange("(k p) c -> p k c", p=128)
            for cb in range(6):
                w = wada[cb % 2]
                self.dma(w[:], wv[:, :, cb * 512:(cb + 1) * 512], [I["w_ada"]], [w])
                if cb < 4:
                    for dj in range(4):
                        j = cb * 4 + dj
                        for kc in range(8):
                            self.mm(pm[:, 2 * dj:2 * dj + 2], w[:, kc, dj * 128:(dj + 1) * 128], scT[:, kc, :],
                                    kc == 0, False, [w, scT], [pm])
                        self.mm(pm[:, 2 * dj:2 * dj + 2], brow[0:1, j * 128:(j + 1) * 128], self.ones[0:1, 0:2],
                                False, True, [brow, self.ones], [pm])
                        self.cp(modT[:, j, :], pm[:, 2 * dj:2 * dj + 2], [pm], [modT])
                else:
                    for typ in range(2):
                        p = pg[typ]
                        for kc in range(8):
                            self.mm(p[:], scT[:, kc, typ:typ + 1].to_broadcast([128, 128]), w[:, kc, :],
                                    kc == 0, False, [w, scT], [p])
                        self.mm(p[:], self.ones[0:1, 0:128], brow[0:1, cb * 512:(cb + 1) * 512], False, True,
                                [brow, self.ones], [p])
                        self.cp(self.gB[:, typ, (cb - 4) * 512:(cb - 3) * 512], p[:], [p], [self.gB])
            self.ts(scale1[:], modT[:, 8:16, :], 1.0, None, ALU.add, None, [modT], [scale1])
            sets = []
            for s_ in range(3):
                B = {}
                B["xt"] = self.sb(st, [128, D], F32, "xt")
                B["ht"] = self.sb(st, [128, 8, 128], BF16, "ht")
                B["pT"] = pg if s_ == 0 else [self.ps(st, [128, 512], F32, "pT") for _ in range(2)]
                sets.append(B)
            gens = [(lambda tt_: (lambda slot: self.a_tile(l, tt_, sets[slot], scale1, modT)))(t) for t in range(NCH)]
            self.interleave(gens, 3)

    def a_tile(self, l, t, B, scale1, modT):
        I = self.I
        x_ = B["xt"]; h_ = B["ht"]; pT = B["pT"]
        typ = 1 if t < 2 else 0
        if l == 0:
            src_t = I["ctx"] if t < 2 else I["x"]
            src = src_t[t * 128:(t + 1) * 128, :] if t < 2 else src_t[(t - 2) * 128:(t - 1) * 128, :]
        else:
            src_t = self.xres_d
            src = src_t[t * 128:(t + 1) * 128, :]
        self.dma(x_[:], src, [src_t], [x_])
        yield
        for kc in range(8):
            p_ = pT[kc // 4]
            self.tr(p_[:, (kc % 4) * 128:(kc % 4 + 1) * 128], x_[:, kc * 128:(kc + 1) * 128], self.cst[:, C_ID:C_ID + 128],
                    [x_, self.cst], [p_])
            if kc % 4 == 3:
                yield
        for kc in range(8):
            p_ = pT[kc // 4]
            self.act(h_[:, kc, :], p_[:, (kc % 4) * 128:(kc % 4 + 1) * 128], AF.Identity, [p_, scale1, modT], [h_],
                     bias=modT[:, kc, typ:typ + 1], scale=scale1[:, kc, typ:typ + 1])
            if kc % 4 == 3:
                yield
        c0 = colof(t * 128)
        self.dma(self.hv[:, :, c0:c0 + 128], h_[:], [h_], [self.hT_d])
        yield

    def load_w(self, dst_ap, dst_t, src_ap, src_t):
        self.dma(dst_ap, src_ap, [src_t], [dst_t], eng=POOL, slow=True)

    def bvec(self, st, name, l, n):
        t = self.sb(st, [128, n], F32, name)
        self.dma(t[:], self.I[name][l:l + 1, :].to_broadcast([128, n]), [self.I[name]], [t], slow=True)
        return t

    def interleave(self, factories, width):
        pending = list(factories)
        active = []
        for s in range(width):
            if pending:
                active.append((s, pending.pop(0)(s)))
        while active:
            nxt = []
            for (s, g) in active:
                try:
                    next(g)
                    nxt.append((s, g))
                except StopIteration:
                    if pending:
                        nxt.append((s, pending.pop(0)(s)))
            active = nxt

    def stage_S(self, l):
        I = self.I
        cst = self.cst
        import os as _os
        with ExitStack() as st:
            wv = I["w_in"][l].rearrange("(k p) c -> p k c", p=128)
            with ExitStack() as stf:
                wx = self.sb(stf, [128, 8, 1536], BF16, "wx")
                wdt = self.sb(stf, [128, 8, 16], BF16, "wdt")
                for kc in range(8):
                    self.load_w(wx[:, kc, :], wx, wv[:, kc, 1024:2560], I["w_in"])
                self.load_w(wdt[:], wdt, wv[:, :, 2560:2576], I["w_in"])
                convw = self.sb(stf, [128, 12, 5], F32, "convw")
                for k in range(5):
                    self.dma(convw[:, :, k], I["ssd_conv_w"][l, k].rearrange("(r p) -> p r", p=128), [I["ssd_conv_w"]], [convw], slow=True)
                dg = self.sb(stf, [128, 60, 128], BF16, "dg")
                for r in range(12):
                    for k in range(5):
                        self.ts(dg[:, r * 5 + k, :], self.identb[:], convw[:, r, k:k + 1], None, ALU.mult, None,
                                [self.identb, convw], [dg])
                cbrow = self.sb(stf, [1, 1536], F32, "cbrow")
                self.dma(cbrow[:], I["ssd_conv_b"][l:l + 1, :], [I["ssd_conv_b"]], [cbrow])
                sets = []
                for s_ in range(2):
                    B = {}
                    B["hc"] = self.sb(stf, [128, 8, 132], BF16, "hcf")
                    B["xbc"] = self.sb(stf, [128, 12, 132], BF16, "xbc")
                    B["esb"] = self.sb(stf, [128, 1536], F32, "esbf")
                    B["ubf"] = self.sb(stf, [128, 12, 128], BF16, "ubf")
                    B["Fb"] = self.sb(stf, [128, 2, 768], BF16, "Fbw")
                    B["Ff"] = self.sb(stf, [128, 2, 136], F32, "Ffw")
                    B["Q"] = [self.ps(stf, [128, 512], F32, "QF") for _ in range(4)]
                    sets.append(B)
                gens = [(lambda cc: (lambda slot: self.ssd_front(cc, sets[slot], wx, wdt, dg, cbrow)))(c) for c in range(NCH)]
                self.interleave(gens, 2)
            self.P.barrier()
            Dsk = self.bvec(st, "ssd_d", l, 16)
            prm = {}
            for d_, sfx in ((0, "f"), (1, "b")):
                al = self.bvec(st, "ssd_a_log_" + sfx, l, 16)
                self.act(al[:], al[:], AF.Exp, [al], [al])
                self.ts(al[:], al[:], -1.0, None, ALU.mult, None, [al], [al])
                dtb = self.bvec(st, "ssd_dt_bias_" + sfx, l, 16)
                Ub = self.sb(st, [128, 128], BF16, "Ub16")
                Lb = self.sb(st, [128, 128], BF16, "Lb16")
                Uo = C_UF if d_ == 0 else C_UB
                Lo = C_LF if d_ == 0 else C_LB
                self.cp(Ub[:], cst[:, Uo:Uo + 128], [cst], [Ub])
                self.cp(Lb[:], cst[:, Lo:Lo + 128], [cst], [Lb])
                prm[d_] = (al, dtb, Ub, Lb)
            with ExitStack() as st2:
                gens = []
                self.yb = {}
                for d_ in (0, 1):
                    for g_ in (0, 1):
                        self.yb[(d_, g_)] = T((self.ypart_d if d_ == 0 else self.ypartb_d).t, "yb")
                        gens.append((lambda dd, gg: (lambda slot: self.ssd_sweep(l, dd, gg, st2, Dsk, prm[dd])))(d_, g_))
                self.interleave(gens, 4)
            self.P.barrier()
            with ExitStack() as st3:
                wz = self.sb(st3, [128, 8, 1024], BF16, "wz")
                for kc in range(8):
                    self.load_w(wz[:, kc, :], wz, wv[:, kc, 0:1024], I["w_in"])
                nwB = self.bvec(st3, "ssd_norm_w", l, 1024)
                sets = []
                for s in range(4):
                    B = {}
                    B["hc"] = self.sb(st3, [128, 8, 128], BF16, "hc3")
                    B["ypf"] = self.sb(st3, [128, 1024], BF16, "ypf")
                    B["ypb"] = self.sb(st3, [128, 1024], BF16, "ypb")
                    B["ys"] = self.sb(st3, [128, 1024], F32, "ys3")
                    B["e"] = self.sb(st3, [128, 1024], F32, "e3")
                    B["t1"] = self.sb(st3, [128, 1024], F32, "t13")
                    B["bst"] = self.sb(st3, [128, 2, 6], F32, "bst3")
                    B["ssq"] = self.sb(st3, [128, 2], F32, "ssq3")
                    B["ob"] = self.sb(st3, [128, 1024], BF16, "ob3")
                    B["oT"] = self.sb(st3, [128, 8, 128], BF16, "oT3")
                    B["PZ"] = [self.ps(st3, [128, 512], F32, "PZ3") for _ in range(2)]
                    B["PT"] = B["PZ"][0]
                    sets.append(B)
                gens = [(lambda cc: (lambda slot: self.ssd_final(cc, sets[slot], wz, nwB)))(c) for c in range(NCH)]
                self.interleave(gens, 4)

    def ssd_final(self, c, B, wz, nwB):
        h_ = B["hc"]; ypf = B["ypf"]; ypb = B["ypb"]; e = B["e"]; t1 = B["t1"]; bst = B["bst"]; ssq = B["ssq"]
        ob = B["ob"]; oT_ = B["oT"]; PZ = B["PZ"]; PT = B["PT"]
        c0 = colof(c * 128)
        tok = slice(c * 128, (c + 1) * 128)
        self.dma(h_[:], self.hv[:, :, c0:c0 + 128], [self.hT_d], [h_], slow=True)
        self.dma(ypf[:], self.ypart_d[tok, :], [self.yb[(0, 0)], self.yb[(0, 1)]], [ypf])
        self.dma(ypb[:], self.ypartb_d[tok, :], [self.yb[(1, 0)], self.yb[(1, 1)]], [ypb])
        yield
        for n in range(2):
            for kc in range(8):
                self.mm(PZ[n][:], h_[:, kc, :], wz[:, kc, n * 512:(n + 1) * 512], kc == 0, kc == 7, [h_, wz], [PZ[n]])
        ys = B["ys"]
        self.tt(ys[:], ypf[:], ypb[:], ALU.add, [ypf, ypb], [ys])
        yield
        for n in range(2):
            self.act(e[:, n * 512:(n + 1) * 512], PZ[n][:], AF.Exp, [PZ[n]], [e], scale=-1.0)
        yield
        self.sigm(e[:], e)
        yield
        for n in range(2):
            self.tt(t1[:, n * 512:(n + 1) * 512], PZ[n][:], e[:, n * 512:(n + 1) * 512], ALU.mult, [PZ[n], e], [t1])
        yield
        self.tt(t1[:], t1[:], ys[:], ALU.mult, [t1, ys], [t1])
        yield
        for s_ in range(2):
            self.P.op(DVE, (lambda ss: (lambda en: en.bn_stats(out=bst[:, ss, :], in_=t1[:, ss * 512:(ss + 1) * 512])))(s_),
                      [t1.b], [bst.b])
        self.P.op(DVE, lambda en: en.bn_aggr(out=ssq[:], in_=bst[:]), [bst.b], [ssq.b])
        self.stt(ssq[:, 1:2], ssq[:, 0:1], ssq[:, 0:1], ssq[:, 1:2], ALU.mult, ALU.add, [ssq], [ssq])
        self.ts(ssq[:, 1:2], ssq[:, 1:2], RMS_EPS, None, ALU.add, None, [ssq], [ssq])
        yield
        self.rsqrt(ssq[:, 1:2], ssq)
        yield
        self.stt(ob[:], t1[:], ssq[:, 1:2], nwB[:], ALU.mult, ALU.mult, [t1, ssq, nwB], [ob])
        yield
        PTb = PT[:].bitcast(BF16)
        for r in range(8):
            self.tr(PTb[:, r * 128:(r + 1) * 128], ob[:, r * 128:(r + 1) * 128], self.identb[:], [ob, self.identb], [PT])
        yield
        self.act(oT_[:], PTb[:, 0:1024].rearrange("p (a b) -> p a b", a=8), AF.Copy, [PT], [oT_])
        yield
        self.dma(self.mixT_d[c, :, 0:1024], oT_[:].rearrange("p a b -> p (a b)"), [oT_], [self.mixT_d])
        yield

    def ssd_front(self, c, B, wx, wdt, dg, cbrow):
        h_ = B["hc"]; xbc = B["xbc"]; esb = B["esb"]; ubf = B["ubf"]; Fb = B["Fb"]; Ff = B["Ff"]; Q = B["Q"]
        Qb0 = Q[0][:].bitcast(BF16)
        Qb1 = Q[1][:].bitcast(BF16)
        c0 = colof(c * 128)
        self.dma(h_[:], self.hv[:, :, c0 - 2:c0 + 130], [self.hT_d], [h_], slow=True)
        yield
        for r in range(12):
            q_ = Q[r // 3]
            o_ = q_[:, (r % 3) * 132:(r % 3) * 132 + 132]
            for kc in range(8):
                self.mm(o_, wx[:, kc, r * 128:(r + 1) * 128], h_[:, kc, :], kc == 0, kc == 7, [wx, h_], [q_])
            if r % 3 == 2:
                yield
        for q in range(4):
            self.act(xbc[:, 3 * q:3 * q + 3, :], Q[q][:, 0:396].rearrange("p (a b) -> p a b", a=3), AF.Copy, [Q[q]], [xbc])
        yield
        for kc in range(8):
            self.mm(Q[3][:, 0:16], h_[:, kc, 2:130], wdt[:, kc, :], kc == 0, kc == 7, [h_, wdt], [Q[3]])
        yield
        for r in range(12):
            q_ = Q[r // 4]
            o_ = q_[:, (r % 4) * 128:(r % 4 + 1) * 128]
            for k in range(5):
                self.mm(o_, dg[:, r * 5 + k, :], xbc[:, r, k:k + 128], k == 0, False, [dg, xbc], [q_])
            self.mm(o_, cbrow[0:1, r * 128:(r + 1) * 128], self.ones[0:1, 0:128], False, True, [cbrow, self.ones], [q_])
            if r % 4 == 3:
                yield
        self.cp(Ff[:, :, 128:136], Q[3][:, 0:16].rearrange("p (g n) -> p g n", g=2), [Q[3]], [Ff])
        for q in range(3):
            self.act(esb[:, q * 512:(q + 1) * 512], Q[q][:], AF.Exp, [Q[q]], [esb], scale=-1.0)
        yield
        self.sigm(esb[:], esb)
        yield
        for q in range(3):
            self.tt(ubf[:, 4 * q:4 * q + 4, :], Q[q][:].rearrange("p (a b) -> p a b", a=4),
                    esb[:, q * 512:(q + 1) * 512].rearrange("p (a b) -> p a b", a=4), ALU.mult, [Q[q], esb], [ubf])
        yield
        for g in range(2):
            self.mm(Q[3][:, 256 + g * 128:256 + (g + 1) * 128], ubf[:, 8 + g, :], ubf[:, 10 + g, :], True, True, [ubf], [Q[3]])
        for r in range(8):
            self.tr(Qb0[:, r * 128:(r + 1) * 128], ubf[:, r, :], self.identb[:], [ubf, self.identb], [Q[0]])
        for r in range(2):
            self.tr(Qb1[:, r * 128:(r + 1) * 128], ubf[:, 8 + r, :], self.identb[:], [ubf, self.identb], [Q[1]])
        self.cp(Fb[:, :, 640:768], ubf[:, 10:12, :], [ubf], [Fb], eng=POOL)
        yield
        self.cp(Ff[:, :, 0:128], Q[3][:, 256:512].rearrange("p (g n) -> p g n", g=2), [Q[3]], [Ff])
        self.act(Fb[:, :, 0:512], Qb0[:, 0:1024].rearrange("p (g n) -> p g n", g=2), AF.Copy, [Q[0]], [Fb])
        self.cp(Fb[:, :, 512:640], Qb1[:, 0:256].rearrange("p (g n) -> p g n", g=2), [Q[1]], [Fb])
        yield
        self.dma(self.Fb_d[c].rearrange("g p n -> p g n"), Fb[:], [Fb], [self.Fb_d])
        self.dma(self.Ff_d[c].rearrange("g p n -> p g n"), Ff[:], [Ff], [self.Ff_d])
        yield

    def ssd_sweep(self, l, d_, g, st, Dsk, prm):
        cst = self.cst
        al, dtb, Ub, Lb = prm
        HS = slice(g * 8, (g + 1) * 8)
        H = self.sb(st, [128, 512], F32, "H")
        H2 = self.sb(st, [128, 512], F32, "H2")
        Hbf = self.sb(st, [128, 512], BF16, "Hbf")
        Fb = [self.sb(st, [128, 768], BF16, "Fbr") for _ in range(2)]
        Ff = [self.sb(st, [128, 136], F32, "Ffr") for _ in range(2)]
        esb = self.sb(st, [128, 1024], F32, "esb")
        dtx = self.sb(st, [128, 8], F32, "dtx")
        dt = self.sb(st, [128, 8], F32, "dt")
        la = self.sb(st, [128, 8], F32, "la")
        lah = self.sb(st, [128, 8], BF16, "lah")
        lah32 = self.sb(st, [128, 8], F32, "lah32")
        lal32 = self.sb(st, [128, 8], F32, "lal32")
        lalo = self.sb(st, [128, 8], BF16, "lalo")
        E3 = self.sb(st, [128, 24], F32, "E3")
        scm = self.sb(st, [128, 128], F32, "scm")
        LaU = self.sb(st, [128, 8, 128], F32, "LaU")
        M = self.sb(st, [128, 8, 128], BF16, "M")
        v = self.sb(st, [128, 512], BF16, "v")
        vte = self.sb(st, [128, 512], BF16, "vte")
        t1 = self.sb(st, [128, 512], F32, "t1")
        t2 = self.sb(st, [128, 512], F32, "t2")
        yp = [self.sb(st, [128, 512], BF16, "yp") for _ in range(2)]
        Q = [self.ps(st, [128, 512], F32, "Q") for _ in range(2)]
        ydst = self.ypart_d if d_ == 0 else self.ypartb_d
        order = list(range(NCH)) if d_ == 0 else [1, 0] + list(range(NCH - 1, 1, -1))
        Uo = C_UF if d_ == 0 else C_UB
        Lo = C_LF if d_ == 0 else C_LB
        Mo = C_MF if d_ == 0 else C_MB
        self.memset(H[:], 0.0, [H])
        self.memset(Hbf[:], 0.0, [Hbf])

        def loads(ci, slot):
            c = order[ci]
            self.dma(Fb[slot][:], self.Fb_d[c, g], [self.Fb_d], [Fb[slot]])
            self.dma(Ff[slot][:], self.Ff_d[c, g], [self.Ff_d], [Ff[slot]])
        loads(0, 0)
        it = 0
        for ci, c in enumerate(order):
            fb = Fb[it % 2]; ff = Ff[it % 2]; ypt = yp[it % 2]
            it += 1
            tok = slice(c * 128, (c + 1) * 128)
            if ci + 1 < len(order):
                loads(ci + 1, it % 2)
            xs_g = fb[:, 0:512]
            self.tt(dtx[:], ff[:, 128:136], dtb[:, HS], ALU.add, [ff, dtb], [dtx])
            self.tt(scm[:], ff[:, 0:128], cst[:, Mo:Mo + 128], ALU.mult, [ff, cst], [scm])
            yield
            self.act(dtx[:], dtx[:], AF.Exp, [dtx], [dtx])
            self.act(dt[:], dtx[:], AF.Ln, [dtx], [dt], bias=1.0)
            yield
            self.tt(la[:], dt[:], al[:, HS], ALU.mult, [dt, al], [la])
            yield
            self.mm(Q[1][:, 0:8], cst[:, Uo:Uo + 128], la[:], True, True, [cst, la], [Q[1]])
            self.mm(Q[1][:, 8:16], cst[:, Lo:Lo + 128], la[:], True, True, [cst, la], [Q[1]])
            self.mm(Q[1][:, 16:24], self.ones[:], la[:], True, True, [self.ones, la], [Q[1]])
            self.tt(LaU[:], la[:].unsqueeze(2).to_broadcast([128, 8, 128]),
                    cst[:, Uo:Uo + 128].unsqueeze(1).to_broadcast([128, 8, 128]), ALU.mult, [la, cst], [LaU], eng=POOL)
            self.tt(v[:].rearrange("p (h e) -> p h e", h=8), xs_g.rearrange("p (h e) -> p h e", h=8),
                    dt[:].unsqueeze(2).to_broadcast([128, 8, 64]), ALU.mult, [fb, dt], [v], eng=POOL)
            yield
            self.act(E3[:], Q[1][:, 0:24], AF.Exp, [Q[1]], [E3])
            yield
            for q in range(2):
                self.mm(Q[q][:], cst[:, Lo:Lo + 128], LaU[:, 4 * q:4 * q + 4, :].rearrange("p a b -> p (a b)"), True, True, [cst, LaU], [Q[q]])
            yield
            self.tt(vte[:].rearrange("p (h e) -> p h e", h=8), v[:].rearrange("p (h e) -> p h e", h=8),
                    E3[:, 8:16].unsqueeze(2).to_broadcast([128, 8, 64]), ALU.mult, [v, E3], [vte])
            for q in range(2):
                self.act(esb[:, q * 512:(q + 1) * 512], Q[q][:], AF.Exp, [Q[q]], [esb])
            yield
            self.tt(M[:], esb[:].rearrange("p (a b) -> p a b", a=8),
                    scm[:].unsqueeze(1).to_broadcast([128, 8, 128]), ALU.mult, [esb, scm], [M])
            yield
            self.mm(Q[0][:], fb[:, 640:768], Hbf[:], True, True, [fb, Hbf], [Q[0]])
            for hh in range(8):
                self.mm(Q[1][:, hh * 64:(hh + 1) * 64], M[:, hh, :], v[:, hh * 64:(hh + 1) * 64], True, True, [M, v], [Q[1]])
            yield
            self.tt(t1[:].rearrange("p (h e) -> p h e", h=8), Q[0][:].rearrange("p (h e) -> p h e", h=8),
                    E3[:, 0:8].unsqueeze(2).to_broadcast([128, 8, 64]), ALU.mult, [Q[0], E3], [t1])
            yield
            self.tt(t2[:], Q[1][:], t1[:], ALU.add, [Q[1], t1], [t2])
            self.mm(Q[0][:], fb[:, 512:640], vte[:], True, True, [fb, vte], [Q[0]])
            yield
            if d_ == 0:
                self.tt(t1[:].rearrange("p (h e) -> p h e", h=8), xs_g.rearrange("p (h e) -> p h e", h=8),
                        Dsk[:, HS].unsqueeze(2).to_broadcast([128, 8, 64]), ALU.mult, [fb, Dsk], [t1], eng=POOL)
                self.tt(ypt[:], t1[:], t2[:], ALU.add, [t1, t2], [ypt])
            else:
                self.act(ypt[:], t2[:], AF.Copy, [t2], [ypt])
            self.dma(ydst[tok, g * 512:(g + 1) * 512], ypt[:], [ypt], [self.yb[(d_, g)]])
            self.tt(H2[:].rearrange("p (h e) -> p h e", h=8), H[:].rearrange("p (h e) -> p h e", h=8),
                    E3[:, 16:24].unsqueeze(2).to_broadcast([128, 8, 64]), ALU.mult, [H, E3], [H2], eng=POOL)
            yield
            self.tt(H[:], H2[:], Q[0][:], ALU.add, [H2, Q[0]], [H])
            yield
            self.act(Hbf[:], H[:], AF.Copy, [H], [Hbf])
            yield

    def stage_R(self, l):
        I = self.I
        cst = self.cst
        import os as _os
        with ExitStack() as st:
            wv = I["w_in"][l].rearrange("(k p) c -> p k c", p=128)
            wqk = self.sb(st, [128, 8, 1024], BF16, "wqk")
            wvv = self.sb(st, [128, 8, 512], BF16, "wvr")
            for kc in range(8):
                self.load_w(wqk[:, kc, 0:512], wqk, wv[:, kc, 3760:4272], I["w_in"])
                self.load_w(wqk[:, kc, 512:1024], wqk, wv[:, kc, 5328:5840], I["w_in"])
                self.load_w(wvv[:, kc, :], wvv, wv[:, kc, 4272:4784], I["w_in"])
            prm = {}
            for d_, sfx in ((0, "f"), (1, "b")):
                nm = "ret_log_rate_" + sfx
                lgB = self.bvec(st, nm, l, 4)
                self.act(lgB[:], lgB[:], AF.Exp, [lgB], [lgB])
                self.ts(lgB[:], lgB[:], -1.0, None, ALU.mult, None, [lgB], [lgB])
                lgs = self.sb(st, [128, 2], F32, "lgs")
                src = I[nm][l:l + 1, :].rearrange("o (p two) -> o p two", two=2)
                self.dma(lgs[0:64, :], src[:, :, 0].to_broadcast([64, 2]), [I[nm]], [lgs], slow=True)
                self.dma(lgs[64:128, :], src[:, :, 1].to_broadcast([64, 2]), [I[nm]], [lgs], slow=True)
                self.act(lgs[:], lgs[:], AF.Exp, [lgs], [lgs])
                self.ts(lgs[:], lgs[:], -1.0, None, ALU.mult, None, [lgs], [lgs])
                RIo = C_RIF if d_ == 0 else C_RIB
                Mo = C_MF if d_ == 0 else C_MB
                Go = C_GF if d_ == 0 else C_GB
                To = C_TEF if d_ == 0 else C_TEB
                DmT = self.sb(st, [128, 4, 128], F32, "DmT")
                for h in range(4):
                    self.act(DmT[:, h, :], cst[:, RIo:RIo + 128], AF.Exp, [cst, lgB], [DmT], scale=lgB[:, h:h + 1])
                self.tt(DmT[:], DmT[:], cst[:, Mo:Mo + 128].unsqueeze(1).to_broadcast([128, 4, 128]), ALU.mult, [DmT, cst], [DmT])
                self.ts(DmT[:], DmT[:], 0.125, None, ALU.mult, None, [DmT], [DmT])
                Gam = self.sb(st, [128, 2, 128], F32, "Gam")
                for p in range(2):
                    self.act(Gam[:, p, :], cst[:, Go:Go + 128], AF.Exp, [cst, lgs], [Gam], scale=lgs[:, p:p + 1])
                te = self.sb(st, [128, 4], F32, "te")
                self.act(te[:], lgB[:], AF.Exp, [lgB, cst], [te], scale=cst[:, To:To + 1])
                self.ts(te[:], te[:], 0.125, None, ALU.mult, None, [te], [te])
                g128 = self.sb(st, [128, 2], F32, "g128")
                self.act(g128[:], lgs[:], AF.Exp, [lgs], [g128], scale=128.0)
                prm[d_] = (DmT, Gam, te, g128)
            with ExitStack() as stf:
                sets = []
                for s_ in range(2):
                    B = {}
                    B["hc"] = self.sb(stf, [128, 8, 128], BF16, "hcrf")
                    B["tab"] = self.sb(stf, [128, 512], F32, "tabrf")
                    B["r1"] = self.sb(stf, [128, 512], F32, "r1")
                    B["r2"] = self.sb(stf, [128, 512], F32, "r2")
                    B["qkt"] = self.sb(stf, [128, 512], BF16, "qkt")
                    B["Rb"] = self.sb(stf, [128, 1280], BF16, "Rbw")
                    B["Q"] = [self.ps(stf, [128, 512], F32, "QRF") for _ in range(4)]
                    sets.append(B)
                gens = [(lambda cc: (lambda slot: self.ret_front(cc, sets[slot], wqk, wvv)))(c) for c in range(NCH)]
                self.interleave(gens, 2)
            self.P.barrier()
            with ExitStack() as st2:
                gens = []
                for d_ in (0, 1):
                    gens.append((lambda dd: (lambda slot: self.ret_sweep(l, dd, st2, prm[dd])))(d_))
                self.interleave(gens, 2)
            self.P.barrier()
            with ExitStack() as st3:
                wg = self.sb(st3, [128, 8, 512], BF16, "wg")
                for kc in range(8):
                    self.load_w(wg[:, kc, :], wg, wv[:, kc, 4784:5296], I["w_in"])
                sets = []
                for s in range(4):
                    B = {}
                    B["hc"] = self.sb(st3, [128, 8, 128], BF16, "hcr3")
                    B["rpf"] = self.sb(st3, [128, 512], BF16, "rpf")
                    B["rpb"] = self.sb(st3, [128, 512], BF16, "rpb")
                    B["rs"] = self.sb(st3, [128, 512], F32, "rs3")
                    B["ge"] = self.sb(st3, [128, 512], F32, "ge3")
                    B["sg"] = self.sb(st3, [128, 512], F32, "sg3")
                    B["stats"] = self.sb(st3, [128, 4, 6], F32, "stats3")
                    B["mv"] = self.sb(st3, [128, 4, 2], F32, "mv3")
                    B["yn"] = self.sb(st3, [128, 512], F32, "yn3")
                    B["ob"] = self.sb(st3, [128, 512], BF16, "obr3")
                    B["oT"] = self.sb(st3, [128, 4, 128], BF16, "oTr3")
                    B["PG"] = self.ps(st3, [128, 512], F32, "PGr3")
                    B["PT"] = self.ps(st3, [128, 1024], BF16, "PTr3")
                    sets.append(B)
                gens = [(lambda cc: (lambda slot: self.ret_final(cc, sets[slot], wg)))(c) for c in range(NCH)]
                self.interleave(gens, 4)

    def ret_final(self, c, B, wg):
        h_ = B["hc"]; rpf = B["rpf"]; rpb = B["rpb"]; ge = B["ge"]; sg = B["sg"]; stats = B["stats"]; mv = B["mv"]
        yn = B["yn"]; ob = B["ob"]; oT_ = B["oT"]; PG = B["PG"]; PT = B["PT"]
        c0 = colof(c * 128)
        tok = slice(c * 128, (c + 1) * 128)
        self.dma(h_[:], self.hv[:, :, c0:c0 + 128], [self.hT_d], [h_], slow=True)
        self.dma(rpf[:], self.rpart_d[tok, :], [self.rpart_d], [rpf])
        self.dma(rpb[:], self.rpartb_d[tok, :], [self.rpartb_d], [rpb])
        yield
        for kc in range(8):
            self.mm(PG[:], h_[:, kc, :], wg[:, kc, :], kc == 0, kc == 7, [h_, wg], [PG])
        rs = B["rs"]
        self.tt(rs[:], rpf[:], rpb[:], ALU.add, [rpf, rpb], [rs])
        yield
        self.act(ge[:], PG[:], AF.Exp, [PG], [ge], scale=-1.0)
        for h in range(4):
            self.P.op(DVE, (lambda hh: (lambda e: e.bn_stats(out=stats[:, hh, :], in_=rs[:, hh * 128:(hh + 1) * 128])))(h),
                      [rs.b], [stats.b])
            self.P.op(DVE, (lambda hh: (lambda e: e.bn_aggr(out=mv[:, hh, :], in_=stats[:, hh, :])))(h),
                      [stats.b], [mv.b])
        self.ts(mv[:, :, 1], mv[:, :, 1], LN_EPS, None, ALU.add, None, [mv], [mv])
        yield
        self.rsqrt(mv[:, :, 1], mv)
        self.sigm(ge[:], ge)
        yield
        self.tt(sg[:], PG[:], ge[:], ALU.mult, [PG, ge], [sg])
        for h in range(4):
            self.ts(yn[:, h * 128:(h + 1) * 128], rs[:, h * 128:(h + 1) * 128], mv[:, h, 0:1], mv[:, h, 1:2],
                    ALU.subtract, ALU.mult, [rs, mv], [yn])
        yield
        self.tt(ob[:], yn[:], sg[:], ALU.mult, [yn, sg], [ob])
        yield
        for h in range(4):
            self.tr(PT[:, h * 128:(h + 1) * 128], ob[:, h * 128:(h + 1) * 128], self.identb[:], [ob, self.identb], [PT])
        yield
        self.act(oT_[:], PT[:, 0:512].rearrange("p (a b) -> p a b", a=4), AF.Copy, [PT], [oT_])
        yield
        self.dma(self.mixT_d[c, :, 1536:2048], oT_[:].rearrange("p a b -> p (a b)"), [oT_], [self.mixT_d])
        yield

    def ret_front(self, c, B, wqk, wvv):
        I = self.I
        h_ = B["hc"]; tab = B["tab"]; r1 = B["r1"]; r2 = B["r2"]; qkt = B["qkt"]; Rb = B["Rb"]; Q = B["Q"]
        Qb3 = Q[3][:].bitcast(BF16)
        c0 = colof(c * 128)
        self.dma(h_[:], self.hv[:, :, c0:c0 + 128], [self.hT_d], [h_], slow=True)
        self.dma(tab[:], I["ret_tab2"][c * 128:(c + 1) * 128, :], [I["ret_tab2"]], [tab])
        yield
        for j, (q_, w_, lo) in enumerate(((Q[0], wqk, 0), (Q[1], wqk, 512), (Q[2], wvv, 0))):
            for kc in range(8):
                self.mm(q_[:], h_[:, kc, :], w_[:, kc, lo:lo + 512], kc == 0, kc == 7, [h_, w_], [q_])
            yield
        self.tt(r1[:].rearrange("p (a b) -> p a b", a=2), Q[0][:].rearrange("p (a b) -> p a b", a=2),
                tab[:, 0:256].unsqueeze(1).to_broadcast([128, 2, 256]), ALU.mult, [Q[0], tab], [r1])
        yield
        self.tt(r2[:].rearrange("p (a b) -> p a b", a=2), Q[1][:].rearrange("p (a b) -> p a b", a=2),
                tab[:, 256:512].unsqueeze(1).to_broadcast([128, 2, 256]), ALU.mult, [Q[1], tab], [r2])
        self.act(Rb[:, 768:1280], Q[2][:], AF.Copy, [Q[2]], [Rb])
        yield
        self.tt(qkt[:], r1[:], r2[:], ALU.add, [r1, r2], [qkt], eng=POOL)
        yield
        for j in range(4):
            self.tr(Qb3[:, j * 128:(j + 1) * 128], qkt[:, j * 128:(j + 1) * 128], self.identb[:], [qkt, self.identb], [Q[3]])
        self.cp(Rb[:, 512:768], qkt[:, 256:512], [qkt], [Rb], eng=POOL)
        yield
        self.act(Rb[:, 0:512], Qb3[:, 0:512], AF.Copy, [Q[3]], [Rb])
        yield
        self.dma(self.Rb_d[c], Rb[:], [Rb], [self.Rb_d])
        yield

    def ret_sweep(self, l, d_, st, prm):
        DmT, Gam, te, g128 = prm
        S = self.sb(st, [128, 2, 128], F32, "S")
        Sbf = self.sb(st, [128, 2, 128], BF16, "Sbf")
        Rb = [self.sb(st, [128, 1280], BF16, "Rbr") for _ in range(2)]
        qz = [self.sb(st, [128, 2, 128], BF16, "qz") for _ in range(2)]
        qdz = [self.sb(st, [128, 2, 128], BF16, "qdz") for _ in range(2)]
        for par in range(2):
            self.memset(qz[par][:], 0.0, [qz[par]])
            self.memset(qdz[par][:], 0.0, [qdz[par]])
        vte = self.sb(st, [128, 512], BF16, "vte")
        Mr = self.sb(st, [128, 4, 128], BF16, "Mr")
        yp = [self.sb(st, [128, 512], BF16, "ypr") for _ in range(2)]
        Q = [self.ps(st, [128, 512], F32, "QR") for _ in range(3)]
        ydst = self.rpart_d if d_ == 0 else self.rpartb_d
        order = list(range(NCH)) if d_ == 0 else [1, 0] + list(range(NCH - 1, 1, -1))
        self.memset(S[:], 0.0, [S])
        self.memset(Sbf[:], 0.0, [Sbf])
        self.dma(Rb[0][:], self.Rb_d[order[0]], [self.Rb_d], [Rb[0]])
        it = 0
        for ci, c in enumerate(order):
            rb = Rb[it % 2]; ypt = yp[it % 2]
            it += 1
            tok = slice(c * 128, (c + 1) * 128)
            if ci + 1 < len(order):
                self.dma(Rb[it % 2][:], self.Rb_d[order[ci + 1]], [self.Rb_d], [Rb[it % 2]])
            qk = rb[:, 0:512].rearrange("p (a b) -> p a b", a=4)
            for par in range(2):
                rr = 64 * par
                self.cp(qz[par][rr:rr + 64, :, :], qk[rr:rr + 64, 0:2, :], [rb], [qz[par]], eng=POOL)
                self.tt(qdz[par][rr:rr + 64, :, :], qk[rr:rr + 64, 0:2, :], Gam[rr:rr + 64, :, :], ALU.mult,
                        [rb, Gam], [qdz[par]], eng=POOL)
            self.tt(vte[:].rearrange("p (h e) -> p h e", h=4), rb[:, 768:1280].rearrange("p (h e) -> p h e", h=4),
                    te[:].unsqueeze(2).to_broadcast([128, 4, 128]), ALU.mult, [rb, te], [vte], eng=POOL)
            yield
            for h in range(4):
                p = h // 2
                self.mm(Q[0][:, h * 128:(h + 1) * 128], qk[:, 2 + p, :], qz[h % 2][:, p, :], True, True, [rb, qz[h % 2]], [Q[0]])
            yield
            self.tt(Mr[:], Q[0][:].rearrange("p (a b) -> p a b", a=4), DmT[:], ALU.mult, [Q[0], DmT], [Mr])
            yield
            for h in range(4):
                p = h // 2
                self.mm(Q[1][:, h * 128:(h + 1) * 128], Mr[:, h, :], rb[:, 768 + h * 128:768 + (h + 1) * 128], True, False, [Mr, rb], [Q[1]])
                self.mm(Q[1][:, h * 128:(h + 1) * 128], qdz[h % 2][:, p, :], Sbf[:, p, :], False, True, [qdz[h % 2], Sbf], [Q[1]])
            for h in range(4):
                p = h // 2
                self.mm(Q[2][:, h * 128:(h + 1) * 128], rb[:, 512 + p * 128:512 + (p + 1) * 128], vte[:, h * 128:(h + 1) * 128],
                        True, True, [rb, vte], [Q[2]])
            yield
            self.cp(ypt[:], Q[1][:], [Q[1]], [ypt])
            self.dma(ydst[tok, :], ypt[:], [ypt], [ydst])
            for h in range(4):
                p, r0 = h // 2, (h % 2) * 64
                self.stt(S[r0:r0 + 64, p, :], S[r0:r0 + 64, p, :], g128[r0:r0 + 64, p:p + 1], Q[2][r0:r0 + 64, h * 128:(h + 1) * 128],
                         ALU.mult, ALU.add, [S, g128, Q[2]], [S])
            yield
            self.act(Sbf[:], S[:], AF.Copy, [S], [Sbf])
            yield

    def stage_M(self, l):
        I = self.I
        cst = self.cst
        blocks = [(0, 256)] + [(256 + 512 * i, 512) for i in range(8)]
        with ExitStack() as st1:
            v_all = self.sb(st1, [128, NCH, 512], BF16, "v_all")
            with ExitStack() as st:
                wv = I["w_in"][l].rearrange("(k p) c -> p k c", p=128)
                wm = self.sb(st, [128, 8, 704], BF16, "wm")
                wgate = self.sb(st, [128, 8, 512], BF16, "wgate")
                self.load_w(wm[:, :, 0:672], wm, wv[:, :, 2576:3248], I["w_in"])
                self.load_w(wm[:, :, 672:704], wm, wv[:, :, 5296:5328], I["w_in"])
                self.load_w(wgate[:], wgate, wv[:, :, 3248:3760], I["w_in"])
                wuq = self.sb(st, [128, 3, 8, 96], BF16, "wuq")
                wuqs = self.sb(st, [128, 3, 8, 96], BF16, "wuqs")
                wkp = self.sb(st, [128, 2, 8, 96], BF16, "wkp")
                wvv = self.sb(st, [128, 2, 8, 64], BF16, "wvv")
                self.memset(wuqs[:], 0.0, [wuqs])
                self.memset(wkp[:], 0.0, [wkp])
                uqv = I["mla_w_uq"][l].rearrange("(k p) (h e) -> p k h e", p=128, h=8)
                uqs = I["w_uq_sw"][l].rearrange("(k p) (h e) -> p k h e", p=128, h=8)
                ukv = I["mla_w_ukv"][l].rearrange("(k p) (h e) -> p k h e", p=128, h=8)
                for kc in range(3):
                    self.load_w(wuq[:, kc, :, :], wuq, uqv[:, kc, :, :], I["mla_w_uq"])
                    self.load_w(wuqs[:, kc, :, 64:96], wuqs, uqs[:, kc, :, :], I["w_uq_sw"])
                for kc in range(2):
                    self.load_w(wkp[:, kc, :, 0:64], wkp, ukv[:, kc, :, 0:64], I["mla_w_ukv"])
                    self.load_w(wvv[:, kc, :, :], wvv, ukv[:, kc, :, 64:128], I["mla_w_ukv"])
                esel = self.sb(st, [32, 96], BF16, "esel")
                self.memset(esel[:], 0.0, [esel])
                self.cp(esel[:, 64:96], self.identb[0:32, 0:32], [self.identb, esel], [esel])
                qn = self.sb(st, [128, 3], F32, "qn")
                kvn = self.sb(st, [128, 2], F32, "kvn")
                self.dma(qn[:], I["mla_q_norm"][l].rearrange("(k p) -> p k", p=128), [I["mla_q_norm"]], [qn], slow=True)
                self.dma(kvn[:], I["mla_kv_norm"][l].rearrange("(k p) -> p k", p=128), [I["mla_kv_norm"]], [kvn], slow=True)
                hb = [self.sb(st, [128, 8, 512], BF16, "hb") for _ in range(2)]
                tq = self.sb(st, [96, 2, 512], F32, "tq")
                self.memset(tq[0:64, 0, :], 1.0, [tq])
                self.memset(tq[0:64, 1, :], 0.0, [tq])
                tk = self.sb(st, [32, 2, 512], F32, "tk")
                cqs = self.sb(st, [128, 3, 512], F32, "cqs")
                sqs2 = [self.sb(st, [128, 512], F32, "sqs") for _ in range(2)]
                rstd = self.sb(st, [128, 512], F32, "rstd")
                cqn = self.sb(st, [128, 3, 512], BF16, "cqn")
                ckvn = self.sb(st, [128, 2, 512], BF16, "ckvn")
                kr1 = self.sb(st, [32, 512], F32, "kr1")
                kr2 = self.sb(st, [32, 512], F32, "kr2")
                krr = self.sb(st, [32, 512], BF16, "krr")
                q1s = [self.sb(st, [96, 512], F32, "q1") for _ in range(2)]
                q2s = [self.sb(st, [96, 512], F32, "q2") for _ in range(2)]
                qf = [self.sb(st, [96, 512], BF16, "qf") for _ in range(2)]
                kf = [self.sb(st, [96, 512], BF16, "kf") for _ in range(2)]
                ges = [self.sb(st, [128, 512], F32, "ge") for _ in range(2)]
                sgo = [self.sb(st, [128, 512], F32, "sgo") for _ in range(2)]
                P0 = [self.ps(st, [128, 512], F32, "P0") for _ in range(2)]
                PSS = self.ps(st, [128, 512], F32, "PSS")
                PQ1 = self.ps(st, [128, 512], F32, "PQ1")
                PQ2 = self.ps(st, [128, 512], F32, "PQ2")
                PK = self.ps(st, [128, 512], F32, "PK")
                PVv = self.ps(st, [128, 512], F32, "PVv")
                PG = self.ps(st, [128, 512], F32, "PG")
                sgv = self.sgT_d[:].rearrange("(r p) t -> p r t", p=128)
                ctr = 0
                for bi, (t0, n) in enumerate(blocks):
                    h_ = hb[bi % 2]
                    c0 = colof(t0)
                    self.dma(h_[:, :, 0:n], self.hv[:, :, c0:c0 + n], [self.hT_d], [h_], slow=True)
                    self.dma(tq[64:96, :, 0:n], I["mla_tab"][:, :, t0:t0 + n], [I["mla_tab"]], [tq], slow=True)
                    self.dma(tk[:, :, 0:n], I["mla_tab"][:, :, t0:t0 + n], [I["mla_tab"]], [tk], slow=True)
                    for (nrc, off, dst, nrm, dim, keep) in ((3, 0, cqn, qn, 384.0, None), (2, 384, ckvn, kvn, 256.0, None)):
                        for rc in range(nrc):
                            p_ = P0[ctr % 2]; ctr += 1
                            for kc in range(8):
                                self.mm(p_[:, 0:n], wm[:, kc, off + rc * 128:off + (rc + 1) * 128], h_[:, kc, 0:n], kc == 0, kc == 7, [wm, h_], [p_])
                            sqs = sqs2[ctr % 2]
                            self.act(cqs[:, rc, 0:n], p_[:, 0:n], AF.Copy, [p_], [cqs])
                            self.act(sqs[:, 0:n], p_[:, 0:n], AF.Square, [p_], [sqs])
                            self.mm(PSS[:, 0:n], self.ones[:], sqs[:, 0:n], rc == 0, rc == nrc - 1, [self.ones, sqs], [PSS])
                        self.ts(rstd[:, 0:n], PSS[:, 0:n], 1.0 / dim, RMS_EPS, ALU.mult, ALU.add, [PSS], [rstd])
                        self.rsqrt(rstd[:, 0:n], rstd)
                        for rc in range(nrc):
                            self.stt(dst[:, rc, 0:n], cqs[:, rc, 0:n], nrm[:, rc:rc + 1], rstd[:, 0:n], ALU.mult, ALU.mult, [cqs, nrm, rstd], [dst])
                    self.mm_group_kr(h_, n, wm, PQ1, PQ2)
                    self.tt(kr1[:, 0:n], PQ1[0:32, 0:n], tk[:, 0, 0:n], ALU.mult, [PQ1, tk], [kr1])
                    self.tt(kr2[:, 0:n], PQ2[0:32, 0:n], tk[:, 1, 0:n], ALU.mult, [PQ2, tk], [kr2])
                    self.tt(krr[:, 0:n], kr1[:, 0:n], kr2[:, 0:n], ALU.add, [kr1, kr2], [krr])
                    for h in range(8):
                        qf_ = qf[h % 2]; kf_ = kf[h % 2]
                        q1 = q1s[h % 2]; q2 = q2s[h % 2]
                        PQ1_, PQ2_, PK_ = (PQ1, PQ2, PK) if h % 2 == 0 else (P0[0], P0[1], PG)
                        for kc in range(3):
                            self.mm(PQ1_[0:96, 0:n], wuq[:, kc, h, :], cqn[:, kc, 0:n], kc == 0, kc == 2, [wuq, cqn], [PQ1_])
                        for kc in range(3):
                            self.mm(PQ2_[0:96, 0:n], wuqs[:, kc, h, :], cqn[:, kc, 0:n], kc == 0, kc == 2, [wuqs, cqn], [PQ2_])
                        self.tt(q1[:, 0:n], PQ1_[0:96, 0:n], tq[:, 0, 0:n], ALU.mult, [PQ1_, tq], [q1])
                        self.tt(q2[:, 0:n], PQ2_[0:96, 0:n], tq[:, 1, 0:n], ALU.mult, [PQ2_, tq], [q2])
                        self.tt(qf_[:, 0:n], q1[:, 0:n], q2[:, 0:n], ALU.add, [q1, q2], [qf_], eng=POOL)
                        self.dma(self.qT_d[h, :, t0:t0 + n], qf_[:, 0:n], [qf_], [self.qT_d])
                        for kc in range(2):
                            self.mm(PK_[0:96, 0:n], wkp[:, kc, h, :], ckvn[:, kc, 0:n], kc == 0, False, [wkp, ckvn], [PK_])
                        self.mm(PK_[0:96, 0:n], esel[:], krr[:, 0:n], False, True, [esel, krr], [PK_])
                        self.act(kf_[:, 0:n], PK_[0:96, 0:n], AF.Copy, [PK_], [kf_])
                        self.dma(self.kfT_d[h, :, t0:t0 + n], kf_[:, 0:n], [kf_], [self.kfT_d])
                    for s in range(n // 128):
                        ch = (t0 + s * 128) // 128
                        PV_ = PVv if s % 2 == 0 else PSS
                        for kc in range(2):
                            self.mm(PV_[:], ckvn[:, kc, s * 128:(s + 1) * 128], wvv[:, kc, :, :].rearrange("p h e -> p (h e)"),
                                    kc == 0, kc == 1, [ckvn, wvv], [PV_])
                        self.act(v_all[:, ch, :], PV_[:], AF.Copy, [PV_], [v_all])
                    for rc in range(4):
                        sg_ = sgo[rc % 2]
                        ge = ges[rc % 2]
                        PG_ = PG if rc % 2 == 0 else PK
                        for kc in range(8):
                            self.mm(PG_[:, 0:n], wgate[:, kc, rc * 128:(rc + 1) * 128], h_[:, kc, 0:n], kc == 0, kc == 7, [wgate, h_], [PG_])
                        self.act(ge[:, 0:n], PG_[:, 0:n], AF.Exp, [PG_], [ge], scale=-1.0)
                        self.sigm(ge[:, 0:n], ge)
                        self.tt(sg_[:, 0:n], PG_[:, 0:n], ge[:, 0:n], ALU.mult, [PG_, ge], [sg_])
                        self.dma(sgv[:, rc, t0:t0 + n], sg_[:, 0:n], [sg_], [self.sgT_d])
            self.P.barrier()
            import os as _os
            if _os.environ.get("NO_M2"):
                return
            with ExitStack() as st:
                kfh = [self.sb(st, [96, NTOK], BF16, "kfh") for _ in range(2)]
                qh = [self.sb(st, [96, NTOK], BF16, "qh") for _ in range(2)]
                vaug = [self.sb(st, [128, NCH, 128], BF16, "vaug") for _ in range(2)]
                self.memset(vaug[0][:, :, 64:128], 1.0, [vaug[0]])
                self.memset(vaug[1][:, :, 0:64], 1.0, [vaug[1]])
                pT = [self.sb(st, [128, 1024], BF16, "pT") for _ in range(3)]
                sgh = [self.sb(st, [128, 512], F32, "sgh") for _ in range(2)]
                rden = self.sb(st, [128, 512], F32, "rden")
                ot = self.sb(st, [128, 512], F32, "ot")
                ob = [self.sb(st, [128, 512], BF16, "ob") for _ in range(2)]
                PSc = [self.ps(st, [128, 1024], F32, "PSc") for _ in range(3)]
                PO = [self.ps(st, [128, 512], F32, "PO") for _ in range(2)]
                ci = 0
                bi_ = 0
                for h in range(int(_os.environ.get("M2_HEADS", "8"))):
                    par = h % 2
                    r0 = 64 * par
                    d0 = 64 - r0
                    kf_ = kfh[h % 2]; q_ = qh[h % 2]; va = vaug[par]
                    self.dma(kf_[:], self.kfT_d[h], [self.kfT_d], [kf_])
                    self.dma(q_[:], self.qT_d[h], [self.qT_d], [q_])
                    self.cp(va[:, :, r0:r0 + 64], v_all[:, :, h * 64:(h + 1) * 64], [v_all], [va], eng=POOL)
                    its = []
                    for (t0, n) in blocks:
                        if t0 == 0:
                            if l == DEPTH - 1:
                                continue
                            kcs = [0, 1]
                        else:
                            kcs = list(range(NCH))
                        po = PO[bi_ % 2]; sg_ = sgh[bi_ % 2]; ob_ = ob[bi_ % 2]
                        bi_ += 1
                        npair = len(kcs) // 2
                        for i in range(npair):
                            its.append((t0, n, kcs[2 * i], kcs[2 * i + 1], i == 0, i == npair - 1, po, sg_, ob_))
                    LOOK = 2
                    for j in range(len(its) + LOOK):
                        if j < len(its):
                            (t0, n, ka, kb, first, last, po, sg_, ob_) = its[j]
                            if first:
                                self.dma(sg_[r0:r0 + 64, 0:n], self.sgT_d[64 * h:64 * (h + 1), t0:t0 + n], [self.sgT_d], [sg_])
                            psc = PSc[(ci + j) % 3]
                            self.mm(psc[:, 0:n], kf_[:, ka * 128:(ka + 1) * 128], q_[:, t0:t0 + n], True, True, [kf_, q_], [psc])
                            self.mm(psc[:, 512:512 + n], kf_[:, kb * 128:(kb + 1) * 128], q_[:, t0:t0 + n], True, True, [kf_, q_], [psc])
                        jj = j - LOOK
                        if jj >= 0:
                            (t0, n, ka, kb, first, last, po, sg_, ob_) = its[jj]
                            psc = PSc[(ci + jj) % 3]; pt = pT[(ci + jj) % 3]
                            self.act(pt[:].rearrange("p (a b) -> p a b", a=2)[:, :, 0:n], psc[:].rearrange("p (a b) -> p a b", a=2)[:, :, 0:n],
                                     AF.Exp, [psc], [pt], scale=MLA_SCALE)
                            self.mm(po[:, 0:n], va[:, ka, :], pt[:, 0:n], first, False, [va, pt], [po])
                            self.mm(po[:, 0:n], va[:, kb, :], pt[:, 512:512 + n], False, last, [va, pt], [po])
                            if last:
                                self.P.op(DVE, (lambda a_, b_: (lambda e: e.reciprocal(out=a_, in_=b_)))(rden[d0:d0 + 64, 0:n], po[d0:d0 + 64, 0:n]),
                                          [po.b], [rden.b])
                                self.tt(ot[r0:r0 + 64, 0:n], po[r0:r0 + 64, 0:n], rden[d0:d0 + 64, 0:n], ALU.mult, [po, rden], [ot])
                                self.tt(ob_[r0:r0 + 64, 0:n], ot[r0:r0 + 64, 0:n], sg_[r0:r0 + 64, 0:n], ALU.mult, [ot, sg_], [ob_], eng=POOL)
                                kcm = 8 + h // 2
                                self.dma(self.mixT_d[t0 // 128:(t0 + n) // 128, r0:r0 + 64, kcm * 128:(kcm + 1) * 128].rearrange("c p t -> p c t"),
                                         ob_[r0:r0 + 64, 0:n].rearrange("p (c t) -> p c t", t=128), [ob_], [self.mixT_d], slow=True)
                    ci += len(its)

    def mm_group_kr(self, h_, n, wm, PQ1, PQ2):
        for kc in range(8):
            self.mm(PQ1[0:32, 0:n], wm[:, kc, 640:672], h_[:, kc, 0:n], kc == 0, kc == 7, [wm, h_], [PQ1])
        for kc in range(8):
            self.mm(PQ2[0:32, 0:n], wm[:, kc, 672:704], h_[:, kc, 0:n], kc == 0, kc == 7, [wm, h_], [PQ2])

    def stage_E(self, l):
        I = self.I
        with ExitStack() as st:
            wo = self.sb(st, [128, 16, 1024], BF16, "wo")
            wov = I["w_out"][l].rearrange("(k p) c -> p k c", p=128)
            for kc in range(16):
                self.load_w(wo[:, kc, :], wo, wov[:, kc, :], I["w_out"])
            lng = self.bvec(st, "ln_g", l, 1024)
            lnb = self.bvec(st, "ln_b", l, 1024)
            sets = []
            for s_ in range(3):
                B = {}
                B["mt"] = self.sb(st, [128, 16, 128], BF16, "mt")
                B["xt"] = self.sb(st, [128, D], F32, "xt")
                B["v1"] = self.sb(st, [128, D], F32, "v1")
                B["v2"] = self.sb(st, [128, D], F32, "v2")
                B["xo"] = self.sb(st, [128, D], F32, "xo")
                B["stats"] = self.sb(st, [128, 2, 6], F32, "stats")
                B["mv"] = self.sb(st, [128, 2], F32, "mv")
                B["PZ"] = [self.ps(st, [128, 512], F32, "PZ") for _ in range(2)]
                sets.append(B)
            tiles = list(range(NCH)) if l < DEPTH - 1 else list(range(2, NCH))
            gens = [(lambda tt_: (lambda slot: self.e_tile(l, tt_, sets[slot], wo, lng, lnb)))(t) for t in tiles]
            self.interleave(gens, 3)

    def e_tile(self, l, t, B, wo, lng, lnb):
        I = self.I
        m_ = B["mt"]; x_ = B["xt"]; v1 = B["v1"]; v2 = B["v2"]; xo_ = B["xo"]; stats = B["stats"]; mv = B["mv"]; PZ = B["PZ"]
        typ = 1 if t < 2 else 0
        tok = slice(t * 128, (t + 1) * 128)
        self.dma(m_[:].rearrange("p a b -> p (a b)"), self.mixT_d[t], [self.mixT_d], [m_])
        if l == 0:
            src_t = I["ctx"] if t < 2 else I["x"]
            src = src_t[t * 128:(t + 1) * 128, :] if t < 2 else src_t[(t - 2) * 128:(t - 1) * 128, :]
        else:
            src_t = self.xres_d
            src = src_t[tok, :]
        self.dma(x_[:], src, [src_t], [x_])
        yield
        for nb in range(2):
            for kc in range(16):
                self.mm(PZ[nb][:], m_[:, kc, :], wo[:, kc, nb * 512:(nb + 1) * 512], kc == 0, kc == 15, [m_, wo], [PZ[nb]])
            yield
        for nb in range(2):
            self.tt(v1[:, nb * 512:(nb + 1) * 512], PZ[nb][:], self.gB[:, typ, nb * 512:(nb + 1) * 512], ALU.mult, [PZ[nb], self.gB], [v1])
        yield
        self.stt(v2[:], x_[:], ALPHA, v1[:], ALU.mult, ALU.add, [x_, v1], [v2])
        yield
        for s in range(2):
            self.P.op(DVE, (lambda ss: (lambda e: e.bn_stats(out=stats[:, ss, :], in_=v2[:, ss * 512:(ss + 1) * 512])))(s),
                      [v2.b], [stats.b])
        self.P.op(DVE, lambda e: e.bn_aggr(out=mv[:], in_=stats[:]), [stats.b], [mv.b])
        self.ts(mv[:, 1:2], mv[:, 1:2], LN_EPS, None, ALU.add, None, [mv], [mv])
        yield
        self.rsqrt(mv[:, 1:2], mv)
        yield
        self.ts(v1[:], v2[:], mv[:, 0:1], mv[:, 1:2], ALU.subtract, ALU.mult, [v2, mv], [v1])
        yield
        self.tt(v2[:], v1[:], lng[:], ALU.mult, [v1, lng], [v2], eng=POOL)
        yield
        self.tt(xo_[:], v2[:], lnb[:], ALU.add, [v2, lnb], [xo_], eng=POOL)
        yield
        if l < DEPTH - 1:
            self.dma(self.xres_d[tok, :], xo_[:], [xo_], [self.xres_d])
        else:
            self.dma(self.out[(t - 2) * 128:(t - 1) * 128, :], xo_[:], [xo_], [self.out])
        yield


C_ID = 0
C_UF = 128
C_LF = 256
C_UB = 384
C_LB = 512
C_MF = 640
C_MB = 768
C_RIF = 896
C_RIB = 1024
C_GF = 1152
C_GB = 1280
C_TEF = 1408
C_TEB = 1409
CST_W = 1410


def make_consts():
    k = np.arange(128)[:, None].astype(np.float32)
    i = np.arange(128)[None, :].astype(np.float32)
    cst = np.zeros((128, CST_W), np.float32)
    cst[:, C_ID:C_ID + 128] = (k == i)
    cst[:, C_UF:C_UF + 128] = (k <= i)
    cst[:, C_LF:C_LF + 128] = (k > i)
    cst[:, C_UB:C_UB + 128] = (k >= i)
    cst[:, C_LB:C_LB + 128] = (k < i)
    cst[:, C_MF:C_MF + 128] = (k <= i)
    cst[:, C_MB:C_MB + 128] = (k >= i)
    cst[:, C_RIF:C_RIF + 128] = np.maximum(i - k, 0)
    cst[:, C_RIB:C_RIB + 128] = np.maximum(k - i, 0)
    cst[:, C_GF:C_GF + 128] = np.broadcast_to(i + 1, (128, 128))
    cst[:, C_GB:C_GB + 128] = np.broadcast_to(128 - i, (128, 128))
    cst[:, C_TEF] = 127 - k[:, 0]
    cst[:, C_TEB] = k[:, 0]
    return cst


def rope_tables():
    rows = SEQ // 64
    t = np.arange(rows * 64)
    row = (t // 64).astype(np.float32)
    col = (t % 64).astype(np.float32)

    def cs(rot):
        nf = rot // 4
        inv = (np.float32(10000.0) ** (-np.arange(nf, dtype=np.float32) / np.float32(nf))).astype(np.float32)
        ang = np.concatenate([row[:, None] * inv, col[:, None] * inv], -1).astype(np.float32)
        return np.cos(ang).astype(np.float32), np.sin(ang).astype(np.float32)
    cm, sm = cs(32)
    mla = np.zeros((32, 2, NTOK), np.float32)
    mla[:, 0, :CTX] = 1.0
    mla[0:16, 0, CTX:] = cm.T; mla[16:32, 0, CTX:] = cm.T
    mla[0:16, 1, CTX:] = -sm.T; mla[16:32, 1, CTX:] = sm.T
    cr, sr = cs(64)
    ret = np.zeros((128, 4, NTOK), np.float32)
    ret[:, 0, :CTX] = 1.0
    for hh in range(2):
        b = hh * 64
        ret[b:b + 32, 0, CTX:] = cr.T; ret[b + 32:b + 64, 0, CTX:] = cr.T
        ret[b:b + 32, 1, CTX:] = -sr.T; ret[b + 32:b + 64, 1, CTX:] = sr.T
    ret[:, 2] = ret[:, 0] * 0.125
    ret[:, 3] = ret[:, 1] * 0.125
    ret2 = np.zeros((NTOK, 1024), np.float32)
    cc = np.ones((NTOK, 4, 64), np.float32)
    ss = np.zeros((NTOK, 4, 64), np.float32)
    cc[CTX:, :, 0:32] = cr[:, None, :]; cc[CTX:, :, 32:64] = cr[:, None, :]
    ss[CTX:, :, 0:32] = -sr[:, None, :]; ss[CTX:, :, 32:64] = sr[:, None, :]
    ret2 = np.zeros((NTOK, 512), np.float32)
    ret2[:, 0:256] = cc.reshape(NTOK, 256)
    ret2[:, 256:512] = ss.reshape(NTOK, 256)
    return mla, ret, ret2


_CACHE = {}


def prep_inputs(inputs):
    f = lambda a: np.ascontiguousarray(np.asarray(a, dtype=np.float32))
    w_in = f(inputs["w_in"])
    kr = w_in[:, :, 3216:3248]
    kr_sw = np.concatenate([kr[:, :, 16:32], kr[:, :, 0:16]], -1)

    def sw64(a):
        a = a.reshape(2, D, 4, 2, 32)
        return np.ascontiguousarray(a[:, :, :, ::-1, :]).reshape(2, D, 256)
    q_sw = sw64(w_in[:, :, 3760:4016])
    k_sw = sw64(w_in[:, :, 4016:4272])
    w_ext = np.ascontiguousarray(np.concatenate([w_in, kr_sw, q_sw, k_sw], -1))
    uq = f(inputs["mla_w_uq"]).reshape(2, 384, 8, 96)
    uq_r = uq[:, :, :, 64:96]
    uq_sw = np.ascontiguousarray(np.concatenate([uq_r[..., 16:32], uq_r[..., 0:16]], -1)).reshape(2, 384, 256)
    mla_tab, ret_tab, ret_tab2 = rope_tables()
    shared = {
        "w_ada": f(inputs["w_ada"]), "b_ada": f(inputs["b_ada"]), "w_in": w_ext,
        "ssd_conv_w": f(inputs["ssd_conv_w"]), "ssd_conv_b": f(inputs["ssd_conv_b"]),
        "ssd_norm_w": f(inputs["ssd_norm_w"]), "mla_q_norm": f(inputs["mla_q_norm"]),
        "mla_w_uq": f(inputs["mla_w_uq"]), "w_uq_sw": uq_sw, "mla_kv_norm": f(inputs["mla_kv_norm"]),
        "mla_w_ukv": f(inputs["mla_w_ukv"]), "ret_log_rate_f": f(inputs["ret_log_rate_f"]),
        "ret_log_rate_b": f(inputs["ret_log_rate_b"]), "w_out": f(inputs["w_out"]),
        "ln_g": f(inputs["ln_g"]), "ln_b": f(inputs["ln_b"]),
        "cst": make_consts(), "mla_tab": mla_tab, "ret_tab": ret_tab, "ret_tab2": ret_tab2,
    }
    for n in ("ssd_a_log_f", "ssd_a_log_b", "ssd_dt_bias_f", "ssd_dt_bias_b", "ssd_d"):
        shared[n] = f(inputs[n])
    x = f(inputs["x"]); c = f(inputs["c"]); ctx = f(inputs["ctx"]); c_ctx = f(inputs["c_ctx"])
    maps = []
    for b in range(8):
        m = dict(shared)
        m["x"] = x[b]
        m["ctx"] = ctx[b]
        m["cvec"] = np.ascontiguousarray(np.stack([c[b], c_ctx], 0))
        maps.append(m)
    return maps


def kernel(**inputs):
    if "nc" not in _CACHE:
        _CACHE["nc"] = K().build()
    nc = _CACHE["nc"]
    maps = prep_inputs(inputs)
    res = run_bass_kernel_spmd(nc, maps, core_ids=list(range(8)))
    return np.stack([np.asarray(r["out"], dtype=np.float32) for r in res.results], 0)
```

```python
import math
from contextlib import ExitStack
import numpy as np
import concourse.bass as bass
import concourse.mybir as mybir
from concourse.bass_utils import run_bass_kernel_spmd

F32 = mybir.dt.float32
BF16 = mybir.dt.bfloat16
AF = mybir.ActivationFunctionType
ALU = mybir.AluOpType

PE, ACT, DVE, POOL, SP = "pe", "act", "dve", "pool", "sp"
EPOCH = 30000
DMA_K = 16
DMA_EPOCH = 1800

D = 1024
SEQ = 4096
CTX = 256
NTOK = SEQ + CTX
NCH = NTOK // 128
HC = NTOK + 8
DEPTH = 2
ALPHA = (2 * DEPTH) ** 0.25
LN_EPS = 1e-5
RMS_EPS = 1e-6
MLA_SCALE = 96 ** -0.5
WEXT = 5296 + 32 + 256 + 256


def colof(t):
    return t + 2 if t < CTX else t + 6


class Buf:
    __slots__ = ("name", "lw", "rd", "rdd", "psum")

    def __init__(self, name=""):
        self.name = name
        self.lw = None
        self.rd = {}
        self.rdd = []
        self.psum = False


class Op:
    __slots__ = ("eng", "fn", "deps", "sig", "idx", "dma_slot", "dma_prev")

    def __init__(self, eng, fn):
        self.eng = eng
        self.fn = fn
        self.deps = set()
        self.sig = None
        self.dma_slot = None
        self.dma_prev = None


class Prog:
    def __init__(self, nc):
        self.nc = nc
        self.ops = []
        self.eng = {PE: nc.tensor, ACT: nc.scalar, DVE: nc.vector, POOL: nc.gpsimd, SP: nc.sync}
        self.dma_lists = {}
        self.last = {}

    def op(self, eng, fn, reads=(), writes=(), dma=False):
        o = Op(eng, fn)
        o.idx = len(self.ops)
        for b in reads:
            if b.lw is not None:
                o.deps.add(b.lw)
            if b.psum:
                for e2, r in b.rd.items():
                    if e2 != eng:
                        o.deps.add(r)
        for b in writes:
            if b.lw is not None:
                o.deps.add(b.lw)
            for r in b.rd.values():
                o.deps.add(r)
            for r in b.rdd:
                o.deps.add(r)
        for b in reads:
            if dma:
                b.rdd.append(o.idx)
            else:
                b.rd[eng] = o.idx
        for b in writes:
            b.lw = o.idx
            b.rd = {}
            b.rdd = []
        if dma:
            lst = self.dma_lists.setdefault(eng, [])
            o.dma_slot = len(lst)
            if len(lst) >= DMA_K:
                o.dma_prev = lst[len(lst) - DMA_K]
            lst.append(o.idx)
        o.deps.discard(o.idx)
        self.ops.append(o)
        self.last[eng] = o.idx
        return o

    def barrier(self):
        bufs = {}
        for e in (PE, ACT, DVE, POOL, SP):
            bufs[e] = Buf("bar" + e)
            o = self.op(e, lambda en: en.nop(), writes=[bufs[e]])
            for lst in self.dma_lists.values():
                for d in lst[-DMA_K:]:
                    if d != o.idx:
                        o.deps.add(d)
        for e in (PE, ACT, DVE, POOL, SP):
            self.op(e, lambda en: en.nop(), reads=list(bufs.values()))

    def emit(self, stack):
        nc = self.nc
        ops = self.ops
        needed = set()
        for o in ops:
            for d in o.deps:
                do = ops[d]
                if do.eng == o.eng and o.eng == PE and do.dma_slot is None:
                    continue
                needed.add(d)
            if o.dma_prev is not None:
                needed.add(o.dma_prev)
        cnt = {}
        sems = {}
        dma_sems = {}
        for o in ops:
            if o.dma_slot is not None:
                k = o.dma_slot % DMA_K
                n = o.dma_slot // DMA_K
                key = (o.eng, k, n // DMA_EPOCH)
                if key not in dma_sems:
                    dma_sems[key] = stack.enter_context(nc.semaphore("dq%s%d_%d" % key))
                o.sig = (dma_sems[key], 16 * (n % DMA_EPOCH + 1))
            elif o.idx in needed:
                c = cnt.get(o.eng, 0)
                key = (o.eng, c // EPOCH)
                if key not in sems:
                    sems[key] = stack.enter_context(nc.semaphore("s%s_%d" % key))
                o.sig = (sems[key], c % EPOCH + 1)
                cnt[o.eng] = c + 1
        waited = {}
        nw = 0
        for o in ops:
            e = self.eng[o.eng]
            deps = set(o.deps)
            if o.dma_prev is not None:
                deps.add(o.dma_prev)
            for d in sorted(deps):
                do = ops[d]
                if do.sig is None:
                    continue
                sem, val = do.sig
                key = (o.eng, id(sem))
                if waited.get(key, 0) >= val:
                    continue
                waited[key] = val
                e.wait_ge(sem, val)
                nw += 1
            ins = o.fn(e)
            if o.sig is not None:
                sem, val = o.sig
                ins.then_inc(sem, 16 if o.dma_slot is not None else 1)
        self.nwaits = nw


class T:
    __slots__ = ("t", "b")

    def __init__(self, t, name=""):
        self.t = t
        self.b = Buf(name)

    def __getitem__(self, k):
        return self.t[k]


class K:
    def __init__(self, debug=False, stop_after=None, skip=()):
        self.skip = set(skip)
        self.debug = debug
        self.stop_after = stop_after
        self.nc = bass.Bass("TRN2", target_bir_lowering=False)
        self.P = Prog(self.nc)
        self.uid = 0

    def dram(self, name, shape, dt, kind="Internal"):
        return T(self.nc.dram_tensor(name, list(shape), dt, kind=kind).ap(), name)

    def sb(self, st, shape, dt, name=None):
        self.uid += 1
        name = "%s_%d" % (name or "t", self.uid)
        return T(st.enter_context(self.nc.sbuf_tensor(name, list(shape), dt)), name)

    def ps(self, st, shape, dt=F32, name=None):
        self.uid += 1
        name = "%s_%d" % (name or "p", self.uid)
        t = T(st.enter_context(self.nc.psum_tensor(name, list(shape), dt)), name)
        t.b.psum = True
        return t

    def dma(self, out, in_, reads, writes, eng=SP, slow=False):
        if slow:
            return self.P.op(eng, lambda e: e.dma_start(out=out, in_=in_, allow_slow_non_contiguous=True),
                             [x.b for x in reads], [x.b for x in writes], dma=True)
        return self.P.op(eng, lambda e: e.dma_start(out=out, in_=in_), [x.b for x in reads], [x.b for x in writes], dma=True)

    def mm(self, out, lhsT, rhs, start, stop, reads, writes):
        return self.P.op(PE, lambda e: e.matmul(out, lhsT=lhsT, rhs=rhs, start=start, stop=stop),
                         [x.b for x in reads], [x.b for x in writes])

    def tr(self, out, in_, ident, reads, writes):
        return self.P.op(PE, lambda e: e.transpose(out=out, in_=in_, identity=ident),
                         [x.b for x in reads], [x.b for x in writes])

    def act(self, out, in_, func, reads, writes, bias=None, scale=None, accum_out=None):
        kw = {}
        if bias is not None:
            kw["bias"] = bias
        if scale is not None:
            kw["scale"] = scale
        if accum_out is not None:
            kw["accum_out"] = accum_out
        return self.P.op(ACT, lambda e: e.activation(out=out, in_=in_, func=func, **kw),
                         [x.b for x in reads], [x.b for x in writes])

    def tt(self, out, in0, in1, op, reads, writes, eng=DVE):
        return self.P.op(eng, lambda e: e.tensor_tensor(out=out, in0=in0, in1=in1, op=op),
                         [x.b for x in reads], [x.b for x in writes])

    def ts(self, out, in0, s1, s2, op0, op1, reads, writes, eng=DVE):
        if op1 is None:
            return self.P.op(eng, lambda e: e.tensor_scalar(out=out, in0=in0, scalar1=s1, scalar2=None, op0=op0),
                             [x.b for x in reads], [x.b for x in writes])
        return self.P.op(eng, lambda e: e.tensor_scalar(out=out, in0=in0, scalar1=s1, scalar2=s2, op0=op0, op1=op1),
                         [x.b for x in reads], [x.b for x in writes])

    def stt(self, out, in0, scalar, in1, op0, op1, reads, writes, eng=DVE):
        return self.P.op(eng, lambda e: e.scalar_tensor_tensor(out=out, in0=in0, scalar=scalar, in1=in1, op0=op0, op1=op1),
                         [x.b for x in reads], [x.b for x in writes])

    def cp(self, out, in_, reads, writes, eng=DVE):
        return self.P.op(eng, lambda e: e.tensor_copy(out=out, in_=in_), [x.b for x in reads], [x.b for x in writes])

    def memset(self, out, val, writes, eng=POOL):
        return self.P.op(eng, lambda e: e.memset(out, val), [], [x.b for x in writes])

    def sigm(self, ap, t):
        self.act(ap, ap, AF.Ln, [t], [t], bias=1.0)
        self.act(ap, ap, AF.Exp, [t], [t], scale=-1.0)

    def rsqrt(self, ap, t):
        self.act(ap, ap, AF.Ln, [t], [t])
        self.act(ap, ap, AF.Exp, [t], [t], scale=-0.5)

    def build(self):
        nc = self.nc
        dbg = self.debug
        I = {}

        def inp(name, shape):
            I[name] = self.dram(name, shape, F32, kind="ExternalInput")
        inp("x", [SEQ, D]); inp("ctx", [CTX, D]); inp("cvec", [2, D])
        inp("w_ada", [2, D, 3 * D]); inp("b_ada", [2, 3 * D]); inp("w_in", [2, D, WEXT])
        inp("ssd_conv_w", [2, 5, 1536]); inp("ssd_conv_b", [2, 1536])
        for n in ("ssd_a_log_f", "ssd_a_log_b", "ssd_dt_bias_f", "ssd_dt_bias_b", "ssd_d"):
            inp(n, [2, 16])
        inp("ssd_norm_w", [2, 1024]); inp("mla_q_norm", [2, 384]); inp("mla_w_uq", [2, 384, 768])
        inp("w_uq_sw", [2, 384, 256]); inp("mla_kv_norm", [2, 256]); inp("mla_w_ukv", [2, 256, 1024])
        inp("ret_log_rate_f", [2, 4]); inp("ret_log_rate_b", [2, 4]); inp("w_out", [2, 2048, D])
        inp("ln_g", [2, D]); inp("ln_b", [2, D])
        inp("cst", [128, CST_W]); inp("mla_tab", [32, 2, NTOK]); inp("ret_tab", [128, 4, NTOK]); inp("ret_tab2", [NTOK, 512])
        self.I = I
        okind = "ExternalOutput" if dbg else "Internal"
        self.out = self.dram("out", [SEQ, D], F32, kind="ExternalOutput")
        self.hT_d = self.dram("hT_d", [128, 8 * HC], BF16, kind=okind)
        self.ypart_d = self.dram("ypart_d", [NTOK, 1024], BF16)
        self.rpart_d = self.dram("rpart_d", [NTOK, 512], BF16)
        self.ypartb_d = self.dram("ypartb_d", [NTOK, 1024], BF16)
        self.Fb_d = self.dram("Fb_d", [NCH, 2, 128, 768], BF16)
        self.Rb_d = self.dram("Rb_d", [NCH, 128, 1280], BF16)
        self.Ff_d = self.dram("Ff_d", [NCH, 2, 128, 136], F32)
        self.rpartb_d = self.dram("rpartb_d", [NTOK, 512], BF16)
        self.mixT_d = self.dram("mixT_d", [NCH, 128, 2048], BF16, kind=okind)
        self.qT_d = self.dram("qT_d", [8, 96, NTOK], BF16)
        self.kfT_d = self.dram("kfT_d", [8, 96, NTOK], BF16)
        self.sgT_d = self.dram("sgT_d", [512, NTOK], F32)
        self.xres_d = self.dram("xres_d", [NTOK, D], F32, kind=okind)

        with ExitStack() as gst:
            self.gst = gst
            self.cst = self.sb(gst, [128, CST_W], F32, "cst")
            self.dma(self.cst[:], I["cst"][:], [I["cst"]], [self.cst])
            self.identb = self.sb(gst, [128, 128], BF16, "identb")
            self.cp(self.identb[:], self.cst[:, C_ID:C_ID + 128], [self.cst], [self.identb])
            self.ones = self.sb(gst, [128, 128], F32, "ones")
            self.memset(self.ones[:], 1.0, [self.ones])
            self.gB = self.sb(gst, [128, 2, 1024], F32, "gB")
            zt = self.sb(gst, [128, 8, 4], BF16, "zt")
            self.memset(zt[:], 0.0, [zt])
            hv = self.hT_d[:].rearrange("p (k c) -> p k c", k=8)
            self.hv = hv
            for (a, b) in ((0, 2), (258, 262), (4358, 4360)):
                self.dma(hv[:, :, a:b], zt[:, :, 0:b - a], [zt], [self.hT_d], slow=True)
            self.P.barrier()
            stages = []
            for l in range(DEPTH):
                stages += [("A", l), ("S", l), ("M", l), ("R", l), ("E", l)]
            for (s, l) in stages:
                if s in self.skip:
                    continue
                if s == "A":
                    self.stage_A(l)
                elif s == "S":
                    self.stage_S(l)
                elif s == "M":
                    self.stage_M(l)
                elif s == "R":
                    self.stage_R(l)
                else:
                    self.stage_E(l)
                self.P.barrier()
                if self.stop_after == (s, l):
                    break
            self.P.barrier()
            self.P.emit(gst)
        return nc

    def silu_psum(self, st, src_ap, src_t, out_ap, out_t, e_t, e_ap, r_ap):
        self.act(e_ap, src_ap, AF.Exp, [src_t], [e_t], scale=-1.0)
        self.sigm(e_ap, e_t)
        self.tt(out_ap, src_ap, r_ap, ALU.mult, [src_t, e_t], [out_t])

    def stage_A(self, l):
        I = self.I
        with ExitStack() as st:
            wada = [self.sb(st, [128, 8, 512], F32, "wada") for _ in range(2)]
            craw = self.sb(st, [128, 8, 2], F32, "craw")
            ce = self.sb(st, [128, 8, 2], F32, "ce")
            scT = self.sb(st, [128, 8, 2], F32, "scT")
            modT = self.sb(st, [128, 24, 2], F32, "modT")
            scale1 = self.sb(st, [128, 8, 2], F32, "scale1")
            brow = self.sb(st, [1, 3 * D], F32, "brow")
            pm = self.ps(st, [128, 512], F32, "pm")
            pg = [self.ps(st, [128, 512], F32, "pg") for _ in range(2)]
            for j in range(2):
                self.dma(craw[:, :, j], I["cvec"][j].rearrange("(k p) -> p k", p=128), [I["cvec"]], [craw], slow=True)
            self.dma(brow[:], I["b_ada"][l:l + 1, :], [I["b_ada"]], [brow])
            self.act(ce[:], craw[:], AF.Exp, [craw], [ce], scale=-1.0)
            self.sigm(ce[:], ce)
            self.tt(scT[:], craw[:], ce[:], ALU.mult, [craw, ce], [scT])
            wv = I["w_ada"][l].rearrange("(k p) c -> p k c", p=128)
            for cb in range(6):
                w = wada[cb % 2]
                self.dma(w[:], wv[:, :, cb * 512:(cb + 1) * 512], [I["w_ada"]], [w])
                if cb < 4:
                    for dj in range(4):
                        j = cb * 4 + dj
                        for kc in range(8):
                            self.mm(pm[:, 2 * dj:2 * dj + 2], w[:, kc, dj * 128:(dj + 1) * 128], scT[:, kc, :],
                                    kc == 0, False, [w, scT], [pm])
                        self.mm(pm[:, 2 * dj:2 * dj + 2], brow[0:1, j * 128:(j + 1) * 128], self.ones[0:1, 0:2],
                                False, True, [brow, self.ones], [pm])
                        self.cp(modT[:, j, :], pm[:, 2 * dj:2 * dj + 2], [pm], [modT])
                else:
                    for typ in range(2):
                        p = pg[typ]
                        for kc in range(8):
                            self.mm(p[:], scT[:, kc, typ:typ + 1].to_broadcast([128, 128]), w[:, kc, :],
                                    kc == 0, False, [w, scT], [p])
                        self.mm(p[:], self.ones[0:1, 0:128], brow[0:1, cb * 512:(cb + 1) * 512], False, True,
                                [brow, self.ones], [p])
                        self.cp(self.gB[:, typ, (cb - 4) * 512:(cb - 3) * 512], p[:], [p], [self.gB])
            self.ts(scale1[:], modT[:, 8:16, :], 1.0, None, ALU.add, None, [modT], [scale1])
            sets = []
            for s_ in range(3):
                B = {}
                B["xt"] = self.sb(st, [128, D], F32, "xt")
                B["ht"] = self.sb(st, [128, 8, 128], BF16, "ht")
                B["pT"] = pg if s_ == 0 else [self.ps(st, [128, 512], F32, "pT") for _ in range(2)]
                sets.append(B)
            gens = [(lambda tt_: (lambda slot: self.a_tile(l, tt_, sets[slot], scale1, modT)))(t) for t in range(NCH)]
            self.interleave(gens, 3)

    def a_tile(self, l, t, B, scale1, modT):
        I = self.I
        x_ = B["xt"]; h_ = B["ht"]; pT = B["pT"]
        typ = 1 if t < 2 else 0
        if l == 0:
            src_t = I["ctx"] if t < 2 else I["x"]
            src = src_t[t * 128:(t + 1) * 128, :] if t < 2 else src_t[(t - 2) * 128:(t - 1) * 128, :]
        else:
            src_t = self.xres_d
            src = src_t[t * 128:(t + 1) * 128, :]
        self.dma(x_[:], src, [src_t], [x_])
        yield
        for kc in range(8):
            p_ = pT[kc // 4]
            self.tr(p_[:, (kc % 4) * 128:(kc % 4 + 1) * 128], x_[:, kc * 128:(kc + 1) * 128], self.cst[:, C_ID:C_ID + 128],
                    [x_, self.cst], [p_])
            if kc % 4 == 3:
                yield
        for kc in range(8):
            p_ = pT[kc // 4]
            self.act(h_[:, kc, :], p_[:, (kc % 4) * 128:(kc % 4 + 1) * 128], AF.Identity, [p_, scale1, modT], [h_],
                     bias=modT[:, kc, typ:typ + 1], scale=scale1[:, kc, typ:typ + 1])
            if kc % 4 == 3:
                yield
        c0 = colof(t * 128)
        self.dma(self.hv[:, :, c0:c0 + 128], h_[:], [h_], [self.hT_d])
        yield

    def load_w(self, dst_ap, dst_t, src_ap, src_t):
        self.dma(dst_ap, src_ap, [src_t], [dst_t], eng=POOL, slow=True)

    def bvec(self, st, name, l, n):
        t = self.sb(st, [128, n], F32, name)
        self.dma(t[:], self.I[name][l:l + 1, :].to_broadcast([128, n]), [self.I[name]], [t], slow=True)
        return t

    def interleave(self, factories, width):
        pending = list(factories)
        active = []
        for s in range(width):
            if pending:
                active.append((s, pending.pop(0)(s)))
        while active:
            nxt = []
            for (s, g) in active:
                try:
                    next(g)
                    nxt.append((s, g))
                except StopIteration:
                    if pending:
                        nxt.append((s, pending.pop(0)(s)))
            active = nxt

    def stage_S(self, l):
        I = self.I
        cst = self.cst
        import os as _os
        with ExitStack() as st:
            wv = I["w_in"][l].rearrange("(k p) c -> p k c", p=128)
            with ExitStack() as stf:
                wx = self.sb(stf, [128, 8, 1536], BF16, "wx")
                wdt = self.sb(stf, [128, 8, 16], BF16, "wdt")
                for kc in range(8):
                    self.load_w(wx[:, kc, :], wx, wv[:, kc, 1024:2560], I["w_in"])
                self.load_w(wdt[:], wdt, wv[:, :, 2560:2576], I["w_in"])
                convw = self.sb(stf, [128, 12, 5], F32, "convw")
                for k in range(5):
                    self.dma(convw[:, :, k], I["ssd_conv_w"][l, k].rearrange("(r p) -> p r", p=128), [I["ssd_conv_w"]], [convw], slow=True)
                dg = self.sb(stf, [128, 60, 128], BF16, "dg")
                for r in range(12):
                    for k in range(5):
                        self.ts(dg[:, r * 5 + k, :], self.identb[:], convw[:, r, k:k + 1], None, ALU.mult, None,
                                [self.identb, convw], [dg])
                cb32 = self.sb(stf, [1, 1536], F32, "cb32")
                self.dma(cb32[:], I["ssd_conv_b"][l:l + 1, :], [I["ssd_conv_b"]], [cb32])
                cbh = self.sb(stf, [1, 1536], BF16, "cbh")
                cbh32 = self.sb(stf, [1, 1536], F32, "cbh32")
                cbl = self.sb(stf, [1, 1536], BF16, "cbl")
                self.cp(cbh[:], cb32[:], [cb32], [cbh])
                self.cp(cbh32[:], cbh[:], [cbh], [cbh32])
                self.tt(cbh32[:], cb32[:], cbh32[:], ALU.subtract, [cb32, cbh32], [cbh32])
                self.cp(cbl[:], cbh32[:], [cbh32], [cbl])
                cbrow = self.sb(stf, [2, 1536], BF16, "cbrow")
                self.dma(cbrow[0:1, :], cbh[:], [cbh], [cbrow])
                self.dma(cbrow[1:2, :], cbl[:], [cbl], [cbrow])
                ones2 = self.sb(stf, [2, 128], BF16, "ones2")
                self.memset(ones2[:], 1.0, [ones2])
                self.ones2 = ones2
                sets = []
                for s_ in range(2):
                    B = {}
                    B["hc"] = self.sb(stf, [128, 8, 132], BF16, "hcf")
                    B["xbc"] = self.sb(stf, [128, 12, 132], BF16, "xbc")
                    B["esb"] = self.sb(stf, [128, 1536], F32, "esbf")
                    B["ubf"] = self.sb(stf, [128, 12, 128], BF16, "ubf")
                    B["Fb"] = self.sb(stf, [128, 2, 768], BF16, "Fbw")
                    B["Ff"] = self.sb(stf, [128, 2, 136], F32, "Ffw")
                    B["Q"] = [self.ps(stf, [128, 512], F32, "QF") for _ in range(4)]
                    sets.append(B)
                gens = [(lambda cc: (lambda slot: self.ssd_front(cc, sets[slot], wx, wdt, dg, cbrow)))(c) for c in range(NCH)]
                self.interleave(gens, 2)
            self.P.barrier()
            Dsk = self.bvec(st, "ssd_d", l, 16)
            prm = {}
            for d_, sfx in ((0, "f"), (1, "b")):
                al = self.bvec(st, "ssd_a_log_" + sfx, l, 16)
                self.act(al[:], al[:], AF.Exp, [al], [al])
                self.ts(al[:], al[:], -1.0, None, ALU.mult, None, [al], [al])
                dtb = self.bvec(st, "ssd_dt_bias_" + sfx, l, 16)
                Ub = self.sb(st, [128, 128], BF16, "Ub16")
                Lb = self.sb(st, [128, 128], BF16, "Lb16")
                Uo = C_UF if d_ == 0 else C_UB
                Lo = C_LF if d_ == 0 else C_LB
                self.cp(Ub[:], cst[:, Uo:Uo + 128], [cst], [Ub])
                self.cp(Lb[:], cst[:, Lo:Lo + 128], [cst], [Lb])
                prm[d_] = (al, dtb, Ub, Lb)
            with ExitStack() as st2:
                gens = []
                self.yb = {}
                for d_ in (0, 1):
                    for g_ in (0, 1):
                        self.yb[(d_, g_)] = T((self.ypart_d if d_ == 0 else self.ypartb_d).t, "yb")
                        gens.append((lambda dd, gg: (lambda slot: self.ssd_sweep(l, dd, gg, st2, Dsk, prm[dd])))(d_, g_))
                self.interleave(gens, 4)
            self.P.barrier()
            with ExitStack() as st3:
                wz = self.sb(st3, [128, 8, 1024], BF16, "wz")
                for kc in range(8):
                    self.load_w(wz[:, kc, :], wz, wv[:, kc, 0:1024], I["w_in"])
                nwB = self.bvec(st3, "ssd_norm_w", l, 1024)
                sets = []
                for s in range(4):
                    B = {}
                    B["hc"] = self.sb(st3, [128, 8, 128], BF16, "hc3")
                    B["ypf"] = self.sb(st3, [128, 1024], BF16, "ypf")
                    B["ypb"] = self.sb(st3, [128, 1024], BF16, "ypb")
                    B["ys"] = self.sb(st3, [128, 1024], F32, "ys3")
                    B["e"] = self.sb(st3, [128, 1024], F32, "e3")
                    B["t1"] = self.sb(st3, [128, 1024], F32, "t13")
                    B["bst"] = self.sb(st3, [128, 2, 6], F32, "bst3")
                    B["ssq"] = self.sb(st3, [128, 2], F32, "ssq3")
                    B["ob"] = self.sb(st3, [128, 1024], BF16, "ob3")
                    B["oT"] = self.sb(st3, [128, 8, 128], BF16, "oT3")
                    B["PZ"] = [self.ps(st3, [128, 512], F32, "PZ3") for _ in range(2)]
                    B["PT"] = B["PZ"][0]
                    sets.append(B)
                gens = [(lambda cc: (lambda slot: self.ssd_final(cc, sets[slot], wz, nwB)))(c) for c in range(NCH)]
                self.interleave(gens, 4)

    def ssd_final(self, c, B, wz, nwB):
        h_ = B["hc"]; ypf = B["ypf"]; ypb = B["ypb"]; e = B["e"]; t1 = B["t1"]; bst = B["bst"]; ssq = B["ssq"]
        ob = B["ob"]; oT_ = B["oT"]; PZ = B["PZ"]; PT = B["PT"]
        c0 = colof(c * 128)
        tok = slice(c * 128, (c + 1) * 128)
        self.dma(h_[:], self.hv[:, :, c0:c0 + 128], [self.hT_d], [h_], slow=True)
        self.dma(ypf[:], self.ypart_d[tok, :], [self.yb[(0, 0)], self.yb[(0, 1)]], [ypf])
        self.dma(ypb[:], self.ypartb_d[tok, :], [self.yb[(1, 0)], self.yb[(1, 1)]], [ypb])
        yield
        for n in range(2):
            for kc in range(8):
                self.mm(PZ[n][:], h_[:, kc, :], wz[:, kc, n * 512:(n + 1) * 512], kc == 0, kc == 7, [h_, wz], [PZ[n]])
        ys = B["ys"]
        self.tt(ys[:], ypf[:], ypb[:], ALU.add, [ypf, ypb], [ys])
        yield
        for n in range(2):
            self.act(e[:, n * 512:(n + 1) * 512], PZ[n][:], AF.Exp, [PZ[n]], [e], scale=-1.0)
        yield
        self.sigm(e[:], e)
        yield
        for n in range(2):
            self.tt(t1[:, n * 512:(n + 1) * 512], PZ[n][:], e[:, n * 512:(n + 1) * 512], ALU.mult, [PZ[n], e], [t1])
        yield
        self.tt(t1[:], t1[:], ys[:], ALU.mult, [t1, ys], [t1])
        yield
        for s_ in range(2):
            self.P.op(DVE, (lambda ss: (lambda en: en.bn_stats(out=bst[:, ss, :], in_=t1[:, ss * 512:(ss + 1) * 512])))(s_),
                      [t1.b], [bst.b])
        self.P.op(DVE, lambda en: en.bn_aggr(out=ssq[:], in_=bst[:]), [bst.b], [ssq.b])
        self.stt(ssq[:, 1:2], ssq[:, 0:1], ssq[:, 0:1], ssq[:, 1:2], ALU.mult, ALU.add, [ssq], [ssq])
        self.ts(ssq[:, 1:2], ssq[:, 1:2], RMS_EPS, None, ALU.add, None, [ssq], [ssq])
        yield
        self.rsqrt(ssq[:, 1:2], ssq)
        yield
        self.stt(ob[:], t1[:], ssq[:, 1:2], nwB[:], ALU.mult, ALU.mult, [t1, ssq, nwB], [ob])
        yield
        PTb = PT[:].bitcast(BF16)
        for r in range(8):
            self.tr(PTb[:, r * 128:(r + 1) * 128], ob[:, r * 128:(r + 1) * 128], self.identb[:], [ob, self.identb], [PT])
        yield
        self.act(oT_[:], PTb[:, 0:1024].rearrange("p (a b) -> p a b", a=8), AF.Copy, [PT], [oT_])
        yield
        self.dma(self.mixT_d[c, :, 0:1024], oT_[:].rearrange("p a b -> p (a b)"), [oT_], [self.mixT_d])
        yield

    def ssd_front(self, c, B, wx, wdt, dg, cbrow):
        h_ = B["hc"]; xbc = B["xbc"]; esb = B["esb"]; ubf = B["ubf"]; Fb = B["Fb"]; Ff = B["Ff"]; Q = B["Q"]
        Qb0 = Q[0][:].bitcast(BF16)
        Qb1 = Q[1][:].bitcast(BF16)
        c0 = colof(c * 128)
        self.dma(h_[:], self.hv[:, :, c0 - 2:c0 + 130], [self.hT_d], [h_], slow=True)
        yield
        for r in range(12):
            q_ = Q[r // 3]
            o_ = q_[:, (r % 3) * 132:(r % 3) * 132 + 132]
            for kc in range(8):
                self.mm(o_, wx[:, kc, r * 128:(r + 1) * 128], h_[:, kc, :], kc == 0, kc == 7, [wx, h_], [q_])
            if r % 3 == 2:
                yield
        for q in range(4):
            self.act(xbc[:, 3 * q:3 * q + 3, :], Q[q][:, 0:396].rearrange("p (a b) -> p a b", a=3), AF.Copy, [Q[q]], [xbc])
        yield
        for kc in range(8):
            self.mm(Q[3][:, 0:16], h_[:, kc, 2:130], wdt[:, kc, :], kc == 0, kc == 7, [h_, wdt], [Q[3]])
        yield
        for r in range(12):
            q_ = Q[r // 4]
            o_ = q_[:, (r % 4) * 128:(r % 4 + 1) * 128]
            for k in range(5):
                self.mm(o_, dg[:, r * 5 + k, :], xbc[:, r, k:k + 128], k == 0, False, [dg, xbc], [q_])
            self.mm(o_, cbrow[:, r * 128:(r + 1) * 128], self.ones2[:], False, True, [cbrow, self.ones2], [q_])
            if r % 4 == 3:
                yield
        self.cp(Ff[:, :, 128:136], Q[3][:, 0:16].rearrange("p (g n) -> p g n", g=2), [Q[3]], [Ff])
        for q in range(3):
            self.act(esb[:, q * 512:(q + 1) * 512], Q[q][:], AF.Exp, [Q[q]], [esb], scale=-1.0)
        yield
        self.sigm(esb[:], esb)
        yield
        for q in range(3):
            self.tt(ubf[:, 4 * q:4 * q + 4, :], Q[q][:].rearrange("p (a b) -> p a b", a=4),
                    esb[:, q * 512:(q + 1) * 512].rearrange("p (a b) -> p a b", a=4), ALU.mult, [Q[q], esb], [ubf])
        yield
        for g in range(2):
            self.mm(Q[3][:, 256 + g * 128:256 + (g + 1) * 128], ubf[:, 8 + g, :], ubf[:, 10 + g, :], True, True, [ubf], [Q[3]])
        for r in range(8):
            self.tr(Qb0[:, r * 128:(r + 1) * 128], ubf[:, r, :], self.identb[:], [ubf, self.identb], [Q[0]])
        for r in range(2):
            self.tr(Qb1[:, r * 128:(r + 1) * 128], ubf[:, 8 + r, :], self.identb[:], [ubf, self.identb], [Q[1]])
        self.cp(Fb[:, :, 640:768], ubf[:, 10:12, :], [ubf], [Fb], eng=POOL)
        yield
        self.cp(Ff[:, :, 0:128], Q[3][:, 256:512].rearrange("p (g n) -> p g n", g=2), [Q[3]], [Ff])
        self.act(Fb[:, :, 0:512], Qb0[:, 0:1024].rearrange("p (g n) -> p g n", g=2), AF.Copy, [Q[0]], [Fb])
        self.cp(Fb[:, :, 512:640], Qb1[:, 0:256].rearrange("p (g n) -> p g n", g=2), [Q[1]], [Fb])
        yield
        self.dma(self.Fb_d[c].rearrange("g p n -> p g n"), Fb[:], [Fb], [self.Fb_d])
        self.dma(self.Ff_d[c].rearrange("g p n -> p g n"), Ff[:], [Ff], [self.Ff_d])
        yield

    def ssd_sweep(self, l, d_, g, st, Dsk, prm):
        cst = self.cst
        al, dtb, Ub, Lb = prm
        HS = slice(g * 8, (g + 1) * 8)
        H = self.sb(st, [128, 512], F32, "H")
        H2 = self.sb(st, [128, 512], F32, "H2")
        Hbf = self.sb(st, [128, 512], BF16, "Hbf")
        Fb = [self.sb(st, [128, 768], BF16, "Fbr") for _ in range(2)]
        Ff = [self.sb(st, [128, 136], F32, "Ffr") for _ in range(2)]
        esb = self.sb(st, [128, 1024], F32, "esb")
        dtx = self.sb(st, [128, 8], F32, "dtx")
        dt = self.sb(st, [128, 8], F32, "dt")
        la = self.sb(st, [128, 8], F32, "la")
        lah = self.sb(st, [128, 8], BF16, "lah")
        lah32 = self.sb(st, [128, 8], F32, "lah32")
        lal32 = self.sb(st, [128, 8], F32, "lal32")
        lalo = self.sb(st, [128, 8], BF16, "lalo")
        E3 = self.sb(st, [128, 24], F32, "E3")
        scm = self.sb(st, [128, 128], F32, "scm")
        LaU = self.sb(st, [128, 8, 128], F32, "LaU")
        M = self.sb(st, [128, 8, 128], BF16, "M")
        v = self.sb(st, [128, 512], BF16, "v")
        vte = self.sb(st, [128, 512], BF16, "vte")
        t1 = self.sb(st, [128, 512], F32, "t1")
        t2 = self.sb(st, [128, 512], F32, "t2")
        yp = [self.sb(st, [128, 512], BF16, "yp") for _ in range(2)]
        Q = [self.ps(st, [128, 512], F32, "Q") for _ in range(2)]
        ydst = self.ypart_d if d_ == 0 else self.ypartb_d
        order = list(range(NCH)) if d_ == 0 else [1, 0] + list(range(NCH - 1, 1, -1))
        Uo = C_UF if d_ == 0 else C_UB
        Lo = C_LF if d_ == 0 else C_LB
        Mo = C_MF if d_ == 0 else C_MB
        self.memset(H[:], 0.0, [H])
        self.memset(Hbf[:], 0.0, [Hbf])

        def loads(ci, slot):
            c = order[ci]
            self.dma(Fb[slot][:], self.Fb_d[c, g], [self.Fb_d], [Fb[slot]])
            self.dma(Ff[slot][:], self.Ff_d[c, g], [self.Ff_d], [Ff[slot]])
        loads(0, 0)
        it = 0
        for ci, c in enumerate(order):
            fb = Fb[it % 2]; ff = Ff[it % 2]; ypt = yp[it % 2]
            it += 1
            tok = slice(c * 128, (c + 1) * 128)
            if ci + 1 < len(order):
                loads(ci + 1, it % 2)
            xs_g = fb[:, 0:512]
            self.tt(dtx[:], ff[:, 128:136], dtb[:, HS], ALU.add, [ff, dtb], [dtx])
            self.tt(scm[:], ff[:, 0:128], cst[:, Mo:Mo + 128], ALU.mult, [ff, cst], [scm])
            yield
            self.act(dtx[:], dtx[:], AF.Exp, [dtx], [dtx])
            self.act(dt[:], dtx[:], AF.Ln, [dtx], [dt], bias=1.0)
            yield
            self.tt(la[:], dt[:], al[:, HS], ALU.mult, [dt, al], [la])
            yield
            self.mm(Q[1][:, 0:8], cst[:, Uo:Uo + 128], la[:], True, True, [cst, la], [Q[1]])
            self.mm(Q[1][:, 8:16], cst[:, Lo:Lo + 128], la[:], True, True, [cst, la], [Q[1]])
            self.mm(Q[1][:, 16:24], self.ones[:], la[:], True, True, [self.ones, la], [Q[1]])
            self.tt(LaU[:], la[:].unsqueeze(2).to_broadcast([128, 8, 128]),
                    cst[:, Uo:Uo + 128].unsqueeze(1).to_broadcast([128, 8, 128]), ALU.mult, [la, cst], [LaU], eng=POOL)
            self.tt(v[:].rearrange("p (h e) -> p h e", h=8), xs_g.rearrange("p (h e) -> p h e", h=8),
                    dt[:].unsqueeze(2).to_broadcast([128, 8, 64]), ALU.mult, [fb, dt], [v], eng=POOL)
            yield
            self.act(E3[:], Q[1][:, 0:24], AF.Exp, [Q[1]], [E3])
            yield
            for q in range(2):
                self.mm(Q[q][:], cst[:, Lo:Lo + 128], LaU[:, 4 * q:4 * q + 4, :].rearrange("p a b -> p (a b)"), True, True, [cst, LaU], [Q[q]])
            yield
            self.tt(vte[:].rearrange("p (h e) -> p h e", h=8), v[:].rearrange("p (h e) -> p h e", h=8),
                    E3[:, 8:16].unsqueeze(2).to_broadcast([128, 8, 64]), ALU.mult, [v, E3], [vte])
            for q in range(2):
                self.act(esb[:, q * 512:(q + 1) * 512], Q[q][:], AF.Exp, [Q[q]], [esb])
            yield
            self.tt(M[:], esb[:].rearrange("p (a b) -> p a b", a=8),
                    scm[:].unsqueeze(1).to_broadcast([128, 8, 128]), ALU.mult, [esb, scm], [M])
            yield
            self.mm(Q[0][:], fb[:, 640:768], Hbf[:], True, True, [fb, Hbf], [Q[0]])
            for hh in range(8):
                self.mm(Q[1][:, hh * 64:(hh + 1) * 64], M[:, hh, :], v[:, hh * 64:(hh + 1) * 64], True, True, [M, v], [Q[1]])
            yield
            self.tt(t1[:].rearrange("p (h e) -> p h e", h=8), Q[0][:].rearrange("p (h e) -> p h e", h=8),
                    E3[:, 0:8].unsqueeze(2).to_broadcast([128, 8, 64]), ALU.mult, [Q[0], E3], [t1])
            yield
            self.tt(t2[:], Q[1][:], t1[:], ALU.add, [Q[1], t1], [t2])
            self.mm(Q[0][:], fb[:, 512:640], vte[:], True, True, [fb, vte], [Q[0]])
            yield
            if d_ == 0:
                self.tt(t1[:].rearrange("p (h e) -> p h e", h=8), xs_g.rearrange("p (h e) -> p h e", h=8),
                        Dsk[:, HS].unsqueeze(2).to_broadcast([128, 8, 64]), ALU.mult, [fb, Dsk], [t1], eng=POOL)
                self.tt(ypt[:], t1[:], t2[:], ALU.add, [t1, t2], [ypt])
            else:
                self.act(ypt[:], t2[:], AF.Copy, [t2], [ypt])
            self.dma(ydst[tok, g * 512:(g + 1) * 512], ypt[:], [ypt], [self.yb[(d_, g)]])
            self.tt(H2[:].rearrange("p (h e) -> p h e", h=8), H[:].rearrange("p (h e) -> p h e", h=8),
                    E3[:, 16:24].unsqueeze(2).to_broadcast([128, 8, 64]), ALU.mult, [H, E3], [H2], eng=POOL)
            yield
            self.tt(H[:], H2[:], Q[0][:], ALU.add, [H2, Q[0]], [H])
            yield
            self.act(Hbf[:], H[:], AF.Copy, [H], [Hbf])
            yield

    def stage_R(self, l):
        I = self.I
        cst = self.cst
        import os as _os
        with ExitStack() as st:
            wv = I["w_in"][l].rearrange("(k p) c -> p k c", p=128)
            wqk = self.sb(st, [128, 8, 1024], BF16, "wqk")
            wvv = self.sb(st, [128, 8, 512], BF16, "wvr")
            for kc in range(8):
                self.load_w(wqk[:, kc, 0:512], wqk, wv[:, kc, 3760:4272], I["w_in"])
                self.load_w(wqk[:, kc, 512:1024], wqk, wv[:, kc, 5328:5840], I["w_in"])
                self.load_w(wvv[:, kc, :], wvv, wv[:, kc, 4272:4784], I["w_in"])
            prm = {}
            for d_, sfx in ((0, "f"), (1, "b")):
                nm = "ret_log_rate_" + sfx
                lgB = self.bvec(st, nm, l, 4)
                self.act(lgB[:], lgB[:], AF.Exp, [lgB], [lgB])
                self.ts(lgB[:], lgB[:], -1.0, None, ALU.mult, None, [lgB], [lgB])
                lgs = self.sb(st, [128, 2], F32, "lgs")
                src = I[nm][l:l + 1, :].rearrange("o (p two) -> o p two", two=2)
                self.dma(lgs[0:64, :], src[:, :, 0].to_broadcast([64, 2]), [I[nm]], [lgs], slow=True)
                self.dma(lgs[64:128, :], src[:, :, 1].to_broadcast([64, 2]), [I[nm]], [lgs], slow=True)
                self.act(lgs[:], lgs[:], AF.Exp, [lgs], [lgs])
                self.ts(lgs[:], lgs[:], -1.0, None, ALU.mult, None, [lgs], [lgs])
                RIo = C_RIF if d_ == 0 else C_RIB
                Mo = C_MF if d_ == 0 else C_MB
                Go = C_GF if d_ == 0 else C_GB
                To = C_TEF if d_ == 0 else C_TEB
                DmT = self.sb(st, [128, 4, 128], F32, "DmT")
                for h in range(4):
                    self.act(DmT[:, h, :], cst[:, RIo:RIo + 128], AF.Exp, [cst, lgB], [DmT], scale=lgB[:, h:h + 1])
                self.tt(DmT[:], DmT[:], cst[:, Mo:Mo + 128].unsqueeze(1).to_broadcast([128, 4, 128]), ALU.mult, [DmT, cst], [DmT])
                self.ts(DmT[:], DmT[:], 0.125, None, ALU.mult, None, [DmT], [DmT])
                Gam = self.sb(st, [128, 2, 128], F32, "Gam")
                for p in range(2):
                    self.act(Gam[:, p, :], cst[:, Go:Go + 128], AF.Exp, [cst, lgs], [Gam], scale=lgs[:, p:p + 1])
                te = self.sb(st, [128, 4], F32, "te")
                self.act(te[:], lgB[:], AF.Exp, [lgB, cst], [te], scale=cst[:, To:To + 1])
                self.ts(te[:], te[:], 0.125, None, ALU.mult, None, [te], [te])
                g128 = self.sb(st, [128, 2], F32, "g128")
                self.act(g128[:], lgs[:], AF.Exp, [lgs], [g128], scale=128.0)
                prm[d_] = (DmT, Gam, te, g128)
            with ExitStack() as stf:
                sets = []
                for s_ in range(2):
                    B = {}
                    B["hc"] = self.sb(stf, [128, 8, 128], BF16, "hcrf")
                    B["tab"] = self.sb(stf, [128, 512], F32, "tabrf")
                    B["r1"] = self.sb(stf, [128, 512], F32, "r1")
                    B["r2"] = self.sb(stf, [128, 512], F32, "r2")
                    B["qkt"] = self.sb(stf, [128, 512], BF16, "qkt")
                    B["Rb"] = self.sb(stf, [128, 1280], BF16, "Rbw")
                    B["Q"] = [self.ps(stf, [128, 512], F32, "QRF") for _ in range(4)]
                    sets.append(B)
                gens = [(lambda cc: (lambda slot: self.ret_front(cc, sets[slot], wqk, wvv)))(c) for c in range(NCH)]
                self.interleave(gens, 2)
            self.P.barrier()
            with ExitStack() as st2:
                gens = []
                for d_ in (0, 1):
                    gens.append((lambda dd: (lambda slot: self.ret_sweep(l, dd, st2, prm[dd])))(d_))
                self.interleave(gens, 2)
            self.P.barrier()
            with ExitStack() as st3:
                wg = self.sb(st3, [128, 8, 512], BF16, "wg")
                for kc in range(8):
                    self.load_w(wg[:, kc, :], wg, wv[:, kc, 4784:5296], I["w_in"])
                sets = []
                for s in range(4):
                    B = {}
                    B["hc"] = self.sb(st3, [128, 8, 128], BF16, "hcr3")
                    B["rpf"] = self.sb(st3, [128, 512], BF16, "rpf")
                    B["rpb"] = self.sb(st3, [128, 512], BF16, "rpb")
                    B["rs"] = self.sb(st3, [128, 512], F32, "rs3")
                    B["ge"] = self.sb(st3, [128, 512], F32, "ge3")
                    B["sg"] = self.sb(st3, [128, 512], F32, "sg3")
                    B["stats"] = self.sb(st3, [128, 4, 6], F32, "stats3")
                    B["mv"] = self.sb(st3, [128, 4, 2], F32, "mv3")
                    B["yn"] = self.sb(st3, [128, 512], F32, "yn3")
                    B["ob"] = self.sb(st3, [128, 512], BF16, "obr3")
                    B["oT"] = self.sb(st3, [128, 4, 128], BF16, "oTr3")
                    B["PG"] = self.ps(st3, [128, 512], F32, "PGr3")
                    B["PT"] = self.ps(st3, [128, 1024], BF16, "PTr3")
                    sets.append(B)
                gens = [(lambda cc: (lambda slot: self.ret_final(cc, sets[slot], wg)))(c) for c in range(NCH)]
                self.interleave(gens, 4)

    def ret_final(self, c, B, wg):
        h_ = B["hc"]; rpf = B["rpf"]; rpb = B["rpb"]; ge = B["ge"]; sg = B["sg"]; stats = B["stats"]; mv = B["mv"]
        yn = B["yn"]; ob = B["ob"]; oT_ = B["oT"]; PG = B["PG"]; PT = B["PT"]
        c0 = colof(c * 128)
        tok = slice(c * 128, (c + 1) * 128)
        self.dma(h_[:], self.hv[:, :, c0:c0 + 128], [self.hT_d], [h_], slow=True)
        self.dma(rpf[:], self.rpart_d[tok, :], [self.rpart_d], [rpf])
        self.dma(rpb[:], self.rpartb_d[tok, :], [self.rpartb_d], [rpb])
        yield
        for kc in range(8):
            self.mm(PG[:], h_[:, kc, :], wg[:, kc, :], kc == 0, kc == 7, [h_, wg], [PG])
        rs = B["rs"]
        self.tt(rs[:], rpf[:], rpb[:], ALU.add, [rpf, rpb], [rs])
        yield
        self.act(ge[:], PG[:], AF.Exp, [PG], [ge], scale=-1.0)
        for h in range(4):
            self.P.op(DVE, (lambda hh: (lambda e: e.bn_stats(out=stats[:, hh, :], in_=rs[:, hh * 128:(hh + 1) * 128])))(h),
                      [rs.b], [stats.b])
            self.P.op(DVE, (lambda hh: (lambda e: e.bn_aggr(out=mv[:, hh, :], in_=stats[:, hh, :])))(h),
                      [stats.b], [mv.b])
        self.ts(mv[:, :, 1], mv[:, :, 1], LN_EPS, None, ALU.add, None, [mv], [mv])
        yield
        self.rsqrt(mv[:, :, 1], mv)
        self.sigm(ge[:], ge)
        yield
        self.tt(sg[:], PG[:], ge[:], ALU.mult, [PG, ge], [sg])
        for h in range(4):
            self.ts(yn[:, h * 128:(h + 1) * 128], rs[:, h * 128:(h + 1) * 128], mv[:, h, 0:1], mv[:, h, 1:2],
                    ALU.subtract, ALU.mult, [rs, mv], [yn])
        yield
        self.tt(ob[:], yn[:], sg[:], ALU.mult, [yn, sg], [ob])
        yield
        for h in range(4):
            self.tr(PT[:, h * 128:(h + 1) * 128], ob[:, h * 128:(h + 1) * 128], self.identb[:], [ob, self.identb], [PT])
        yield
        self.act(oT_[:], PT[:, 0:512].rearrange("p (a b) -> p a b", a=4), AF.Copy, [PT], [oT_])
        yield
        self.dma(self.mixT_d[c, :, 1536:2048], oT_[:].rearrange("p a b -> p (a b)"), [oT_], [self.mixT_d])
        yield

    def ret_front(self, c, B, wqk, wvv):
        I = self.I
        h_ = B["hc"]; tab = B["tab"]; r1 = B["r1"]; r2 = B["r2"]; qkt = B["qkt"]; Rb = B["Rb"]; Q = B["Q"]
        Qb3 = Q[3][:].bitcast(BF16)
        c0 = colof(c * 128)
        self.dma(h_[:], self.hv[:, :, c0:c0 + 128], [self.hT_d], [h_], slow=True)
        self.dma(tab[:], I["ret_tab2"][c * 128:(c + 1) * 128, :], [I["ret_tab2"]], [tab])
        yield
        for j, (q_, w_, lo) in enumerate(((Q[0], wqk, 0), (Q[1], wqk, 512), (Q[2], wvv, 0))):
            for kc in range(8):
                self.mm(q_[:], h_[:, kc, :], w_[:, kc, lo:lo + 512], kc == 0, kc == 7, [h_, w_], [q_])
            yield
        self.tt(r1[:].rearrange("p (a b) -> p a b", a=2), Q[0][:].rearrange("p (a b) -> p a b", a=2),
                tab[:, 0:256].unsqueeze(1).to_broadcast([128, 2, 256]), ALU.mult, [Q[0], tab], [r1])
        yield
        self.tt(r2[:].rearrange("p (a b) -> p a b", a=2), Q[1][:].rearrange("p (a b) -> p a b", a=2),
                tab[:, 256:512].unsqueeze(1).to_broadcast([128, 2, 256]), ALU.mult, [Q[1], tab], [r2])
        self.act(Rb[:, 768:1280], Q[2][:], AF.Copy, [Q[2]], [Rb])
        yield
        self.tt(qkt[:], r1[:], r2[:], ALU.add, [r1, r2], [qkt], eng=POOL)
        yield
        for j in range(4):
            self.tr(Qb3[:, j * 128:(j + 1) * 128], qkt[:, j * 128:(j + 1) * 128], self.identb[:], [qkt, self.identb], [Q[3]])
        self.cp(Rb[:, 512:768], qkt[:, 256:512], [qkt], [Rb], eng=POOL)
        yield
        self.act(Rb[:, 0:512], Qb3[:, 0:512], AF.Copy, [Q[3]], [Rb])
        yield
        self.dma(self.Rb_d[c], Rb[:], [Rb], [self.Rb_d])
        yield

    def ret_sweep(self, l, d_, st, prm):
        DmT, Gam, te, g128 = prm
        S = self.sb(st, [128, 2, 128], F32, "S")
        Sbf = self.sb(st, [128, 2, 128], BF16, "Sbf")
        Rb = [self.sb(st, [128, 1280], BF16, "Rbr") for _ in range(2)]
        qz = [self.sb(st, [128, 2, 128], BF16, "qz") for _ in range(2)]
        qdz = [self.sb(st, [128, 2, 128], BF16, "qdz") for _ in range(2)]
        for par in range(2):
            self.memset(qz[par][:], 0.0, [qz[par]])
            self.memset(qdz[par][:], 0.0, [qdz[par]])
        vte = self.sb(st, [128, 512], BF16, "vte")
        Mr = self.sb(st, [128, 4, 128], BF16, "Mr")
        yp = [self.sb(st, [128, 512], BF16, "ypr") for _ in range(2)]
        Q = [self.ps(st, [128, 512], F32, "QR") for _ in range(3)]
        ydst = self.rpart_d if d_ == 0 else self.rpartb_d
        order = list(range(NCH)) if d_ == 0 else [1, 0] + list(range(NCH - 1, 1, -1))
        self.memset(S[:], 0.0, [S])
        self.memset(Sbf[:], 0.0, [Sbf])
        self.dma(Rb[0][:], self.Rb_d[order[0]], [self.Rb_d], [Rb[0]])
        it = 0
        for ci, c in enumerate(order):
            rb = Rb[it % 2]; ypt = yp[it % 2]
            it += 1
            tok = slice(c * 128, (c + 1) * 128)
            if ci + 1 < len(order):
                self.dma(Rb[it % 2][:], self.Rb_d[order[ci + 1]], [self.Rb_d], [Rb[it % 2]])
            qk = rb[:, 0:512].rearrange("p (a b) -> p a b", a=4)
            for par in range(2):
                rr = 64 * par
                self.cp(qz[par][rr:rr + 64, :, :], qk[rr:rr + 64, 0:2, :], [rb], [qz[par]], eng=POOL)
                self.tt(qdz[par][rr:rr + 64, :, :], qk[rr:rr + 64, 0:2, :], Gam[rr:rr + 64, :, :], ALU.mult,
                        [rb, Gam], [qdz[par]], eng=POOL)
            self.tt(vte[:].rearrange("p (h e) -> p h e", h=4), rb[:, 768:1280].rearrange("p (h e) -> p h e", h=4),
                    te[:].unsqueeze(2).to_broadcast([128, 4, 128]), ALU.mult, [rb, te], [vte], eng=POOL)
            yield
            for h in range(4):
                p = h // 2
                self.mm(Q[0][:, h * 128:(h + 1) * 128], qk[:, 2 + p, :], qz[h % 2][:, p, :], True, True, [rb, qz[h % 2]], [Q[0]])
            yield
            self.tt(Mr[:], Q[0][:].rearrange("p (a b) -> p a b", a=4), DmT[:], ALU.mult, [Q[0], DmT], [Mr])
            yield
            for h in range(4):
                p = h // 2
                self.mm(Q[1][:, h * 128:(h + 1) * 128], Mr[:, h, :], rb[:, 768 + h * 128:768 + (h + 1) * 128], True, False, [Mr, rb], [Q[1]])
                self.mm(Q[1][:, h * 128:(h + 1) * 128], qdz[h % 2][:, p, :], Sbf[:, p, :], False, True, [qdz[h % 2], Sbf], [Q[1]])
            for h in range(4):
                p = h // 2
                self.mm(Q[2][:, h * 128:(h + 1) * 128], rb[:, 512 + p * 128:512 + (p + 1) * 128], vte[:, h * 128:(h + 1) * 128],
                        True, True, [rb, vte], [Q[2]])
            yield
            self.cp(ypt[:], Q[1][:], [Q[1]], [ypt])
            self.dma(ydst[tok, :], ypt[:], [ypt], [ydst])
            for h in range(4):
                p, r0 = h // 2, (h % 2) * 64
                self.stt(S[r0:r0 + 64, p, :], S[r0:r0 + 64, p, :], g128[r0:r0 + 64, p:p + 1], Q[2][r0:r0 + 64, h * 128:(h + 1) * 128],
                         ALU.mult, ALU.add, [S, g128, Q[2]], [S])
            yield
            self.act(Sbf[:], S[:], AF.Copy, [S], [Sbf])
            yield

    def stage_M(self, l):
        I = self.I
        cst = self.cst
        blocks = [(0, 256)] + [(256 + 512 * i, 512) for i in range(8)]
        with ExitStack() as st1:
            v_all = self.sb(st1, [128, NCH, 512], BF16, "v_all")
            with ExitStack() as st:
                wv = I["w_in"][l].rearrange("(k p) c -> p k c", p=128)
                wm = self.sb(st, [128, 8, 704], BF16, "wm")
                wgate = self.sb(st, [128, 8, 512], BF16, "wgate")
                self.load_w(wm[:, :, 0:672], wm, wv[:, :, 2576:3248], I["w_in"])
                self.load_w(wm[:, :, 672:704], wm, wv[:, :, 5296:5328], I["w_in"])
                self.load_w(wgate[:], wgate, wv[:, :, 3248:3760], I["w_in"])
                wuq = self.sb(st, [128, 3, 8, 96], BF16, "wuq")
                wuqs = self.sb(st, [128, 3, 8, 96], BF16, "wuqs")
                wkp = self.sb(st, [128, 2, 8, 96], BF16, "wkp")
                wvv = self.sb(st, [128, 2, 8, 64], BF16, "wvv")
                self.memset(wuqs[:], 0.0, [wuqs])
                self.memset(wkp[:], 0.0, [wkp])
                uqv = I["mla_w_uq"][l].rearrange("(k p) (h e) -> p k h e", p=128, h=8)
                uqs = I["w_uq_sw"][l].rearrange("(k p) (h e) -> p k h e", p=128, h=8)
                ukv = I["mla_w_ukv"][l].rearrange("(k p) (h e) -> p k h e", p=128, h=8)
                for kc in range(3):
                    self.load_w(wuq[:, kc, :, :], wuq, uqv[:, kc, :, :], I["mla_w_uq"])
                    self.load_w(wuqs[:, kc, :, 64:96], wuqs, uqs[:, kc, :, :], I["w_uq_sw"])
                for kc in range(2):
                    self.load_w(wkp[:, kc, :, 0:64], wkp, ukv[:, kc, :, 0:64], I["mla_w_ukv"])
                    self.load_w(wvv[:, kc, :, :], wvv, ukv[:, kc, :, 64:128], I["mla_w_ukv"])
                esel = self.sb(st, [32, 96], BF16, "esel")
                self.memset(esel[:], 0.0, [esel])
                self.cp(esel[:, 64:96], self.identb[0:32, 0:32], [self.identb, esel], [esel])
                qn = self.sb(st, [128, 3], F32, "qn")
                kvn = self.sb(st, [128, 2], F32, "kvn")
                self.dma(qn[:], I["mla_q_norm"][l].rearrange("(k p) -> p k", p=128), [I["mla_q_norm"]], [qn], slow=True)
                self.dma(kvn[:], I["mla_kv_norm"][l].rearrange("(k p) -> p k", p=128), [I["mla_kv_norm"]], [kvn], slow=True)
                hb = [self.sb(st, [128, 8, 512], BF16, "hb") for _ in range(2)]
                tq = self.sb(st, [96, 2, 512], F32, "tq")
                self.memset(tq[0:64, 0, :], 1.0, [tq])
                self.memset(tq[0:64, 1, :], 0.0, [tq])
                tk = self.sb(st, [32, 2, 512], F32, "tk")
                cqs = self.sb(st, [128, 3, 512], F32, "cqs")
                sqs2 = [self.sb(st, [128, 512], F32, "sqs") for _ in range(2)]
                rstd = self.sb(st, [128, 512], F32, "rstd")
                cqn = self.sb(st, [128, 3, 512], BF16, "cqn")
                ckvn = self.sb(st, [128, 2, 512], BF16, "ckvn")
                kr1 = self.sb(st, [32, 512], F32, "kr1")
                kr2 = self.sb(st, [32, 512], F32, "kr2")
                krr = self.sb(st, [32, 512], BF16, "krr")
                q1s = [self.sb(st, [96, 512], F32, "q1") for _ in range(2)]
                q2s = [self.sb(st, [96, 512], F32, "q2") for _ in range(2)]
                qf = [self.sb(st, [96, 512], BF16, "qf") for _ in range(2)]
                kf = [self.sb(st, [96, 512], BF16, "kf") for _ in range(2)]
                ges = [self.sb(st, [128, 512], F32, "ge") for _ in range(2)]
                sgo = [self.sb(st, [128, 512], F32, "sgo") for _ in range(2)]
                P0 = [self.ps(st, [128, 512], F32, "P0") for _ in range(2)]
                PSS = self.ps(st, [128, 512], F32, "PSS")
                PQ1 = self.ps(st, [128, 512], F32, "PQ1")
                PQ2 = self.ps(st, [128, 512], F32, "PQ2")
                PK = self.ps(st, [128, 512], F32, "PK")
                PVv = self.ps(st, [128, 512], F32, "PVv")
                PG = self.ps(st, [128, 512], F32, "PG")
                sgv = self.sgT_d[:].rearrange("(r p) t -> p r t", p=128)
                ctr = 0
                for bi, (t0, n) in enumerate(blocks):
                    h_ = hb[bi % 2]
                    c0 = colof(t0)
                    self.dma(h_[:, :, 0:n], self.hv[:, :, c0:c0 + n], [self.hT_d], [h_], slow=True)
                    self.dma(tq[64:96, :, 0:n], I["mla_tab"][:, :, t0:t0 + n], [I["mla_tab"]], [tq], slow=True)
                    self.dma(tk[:, :, 0:n], I["mla_tab"][:, :, t0:t0 + n], [I["mla_tab"]], [tk], slow=True)
                    for (nrc, off, dst, nrm, dim, keep) in ((3, 0, cqn, qn, 384.0, None), (2, 384, ckvn, kvn, 256.0, None)):
                        for rc in range(nrc):
                            p_ = P0[ctr % 2]; ctr += 1
                            for kc in range(8):
                                self.mm(p_[:, 0:n], wm[:, kc, off + rc * 128:off + (rc + 1) * 128], h_[:, kc, 0:n], kc == 0, kc == 7, [wm, h_], [p_])
                            sqs = sqs2[ctr % 2]
                            self.act(cqs[:, rc, 0:n], p_[:, 0:n], AF.Copy, [p_], [cqs])
                            self.act(sqs[:, 0:n], p_[:, 0:n], AF.Square, [p_], [sqs])
                            self.mm(PSS[:, 0:n], self.ones[:], sqs[:, 0:n], rc == 0, rc == nrc - 1, [self.ones, sqs], [PSS])
                        self.ts(rstd[:, 0:n], PSS[:, 0:n], 1.0 / dim, RMS_EPS, ALU.mult, ALU.add, [PSS], [rstd])
                        self.rsqrt(rstd[:, 0:n], rstd)
                        for rc in range(nrc):
                            self.stt(dst[:, rc, 0:n], cqs[:, rc, 0:n], nrm[:, rc:rc + 1], rstd[:, 0:n], ALU.mult, ALU.mult, [cqs, nrm, rstd], [dst])
                    self.mm_group_kr(h_, n, wm, PQ1, PQ2)
                    self.tt(kr1[:, 0:n], PQ1[0:32, 0:n], tk[:, 0, 0:n], ALU.mult, [PQ1, tk], [kr1])
                    self.tt(kr2[:, 0:n], PQ2[0:32, 0:n], tk[:, 1, 0:n], ALU.mult, [PQ2, tk], [kr2])
                    self.tt(krr[:, 0:n], kr1[:, 0:n], kr2[:, 0:n], ALU.add, [kr1, kr2], [krr])
                    for h in range(8):
                        qf_ = qf[h % 2]; kf_ = kf[h % 2]
                        q1 = q1s[h % 2]; q2 = q2s[h % 2]
                        PQ1_, PQ2_, PK_ = (PQ1, PQ2, PK) if h % 2 == 0 else (P0[0], P0[1], PG)
                        for kc in range(3):
                            self.mm(PQ1_[0:96, 0:n], wuq[:, kc, h, :], cqn[:, kc, 0:n], kc == 0, kc == 2, [wuq, cqn], [PQ1_])
                        for kc in range(3):
                            self.mm(PQ2_[0:96, 0:n], wuqs[:, kc, h, :], cqn[:, kc, 0:n], kc == 0, kc == 2, [wuqs, cqn], [PQ2_])
                        self.tt(q1[:, 0:n], PQ1_[0:96, 0:n], tq[:, 0, 0:n], ALU.mult, [PQ1_, tq], [q1])
                        self.tt(q2[:, 0:n], PQ2_[0:96, 0:n], tq[:, 1, 0:n], ALU.mult, [PQ2_, tq], [q2])
                        self.tt(qf_[:, 0:n], q1[:, 0:n], q2[:, 0:n], ALU.add, [q1, q2], [qf_], eng=POOL)
                        self.dma(self.qT_d[h, :, t0:t0 + n], qf_[:, 0:n], [qf_], [self.qT_d])
                        for kc in range(2):
                            self.mm(PK_[0:96, 0:n], wkp[:, kc, h, :], ckvn[:, kc, 0:n], kc == 0, False, [wkp, ckvn], [PK_])
                        self.mm(PK_[0:96, 0:n], esel[:], krr[:, 0:n], False, True, [esel, krr], [PK_])
                        self.act(kf_[:, 0:n], PK_[0:96, 0:n], AF.Copy, [PK_], [kf_])
                        self.dma(self.kfT_d[h, :, t0:t0 + n], kf_[:, 0:n], [kf_], [self.kfT_d])
                    for s in range(n // 128):
                        ch = (t0 + s * 128) // 128
                        PV_ = PVv if s % 2 == 0 else PSS
                        for kc in range(2):
                            self.mm(PV_[:], ckvn[:, kc, s * 128:(s + 1) * 128], wvv[:, kc, :, :].rearrange("p h e -> p (h e)"),
                                    kc == 0, kc == 1, [ckvn, wvv], [PV_])
                        self.act(v_all[:, ch, :], PV_[:], AF.Copy, [PV_], [v_all])
                    for rc in range(4):
                        sg_ = sgo[rc % 2]
                        ge = ges[rc % 2]
                        PG_ = PG if rc % 2 == 0 else PK
                        for kc in range(8):
                            self.mm(PG_[:, 0:n], wgate[:, kc, rc * 128:(rc + 1) * 128], h_[:, kc, 0:n], kc == 0, kc == 7, [wgate, h_], [PG_])
                        self.act(ge[:, 0:n], PG_[:, 0:n], AF.Exp, [PG_], [ge], scale=-1.0)
                        self.sigm(ge[:, 0:n], ge)
                        self.tt(sg_[:, 0:n], PG_[:, 0:n], ge[:, 0:n], ALU.mult, [PG_, ge], [sg_])
                        self.dma(sgv[:, rc, t0:t0 + n], sg_[:, 0:n], [sg_], [self.sgT_d])
            self.P.barrier()
            import os as _os
            if _os.environ.get("NO_M2"):
                return
            with ExitStack() as st:
                kfh = [self.sb(st, [96, NTOK], BF16, "kfh") for _ in range(2)]
                qh = [self.sb(st, [96, NTOK], BF16, "qh") for _ in range(2)]
                vaug = [self.sb(st, [128, NCH, 128], BF16, "vaug") for _ in range(2)]
                self.memset(vaug[0][:, :, 64:128], 1.0, [vaug[0]])
                self.memset(vaug[1][:, :, 0:64], 1.0, [vaug[1]])
                pT = [self.sb(st, [128, 1024], BF16, "pT") for _ in range(3)]
                sgh = [self.sb(st, [128, 512], F32, "sgh") for _ in range(2)]
                rden = self.sb(st, [128, 512], F32, "rden")
                ot = self.sb(st, [128, 512], F32, "ot")
                ob = [self.sb(st, [128, 512], BF16, "ob") for _ in range(2)]
                PSc = [self.ps(st, [128, 1024], F32, "PSc") for _ in range(3)]
                PO = [self.ps(st, [128, 512], F32, "PO") for _ in range(2)]
                ci = 0
                bi_ = 0
                for h in range(int(_os.environ.get("M2_HEADS", "8"))):
                    par = h % 2
                    r0 = 64 * par
                    d0 = 64 - r0
                    kf_ = kfh[h % 2]; q_ = qh[h % 2]; va = vaug[par]
                    self.dma(kf_[:], self.kfT_d[h], [self.kfT_d], [kf_])
                    self.dma(q_[:], self.qT_d[h], [self.qT_d], [q_])
                    self.cp(va[:, :, r0:r0 + 64], v_all[:, :, h * 64:(h + 1) * 64], [v_all], [va], eng=POOL)
                    its = []
                    for (t0, n) in blocks:
                        if t0 == 0:
                            if l == DEPTH - 1:
                                continue
                            kcs = [0, 1]
                        else:
                            kcs = list(range(NCH))
                        po = PO[bi_ % 2]; sg_ = sgh[bi_ % 2]; ob_ = ob[bi_ % 2]
                        bi_ += 1
                        npair = len(kcs) // 2
                        for i in range(npair):
                            its.append((t0, n, kcs[2 * i], kcs[2 * i + 1], i == 0, i == npair - 1, po, sg_, ob_))
                    LOOK = 2
                    for j in range(len(its) + LOOK):
                        if j < len(its):
                            (t0, n, ka, kb, first, last, po, sg_, ob_) = its[j]
                            if first:
                                self.dma(sg_[r0:r0 + 64, 0:n], self.sgT_d[64 * h:64 * (h + 1), t0:t0 + n], [self.sgT_d], [sg_])
                            psc = PSc[(ci + j) % 3]
                            self.mm(psc[:, 0:n], kf_[:, ka * 128:(ka + 1) * 128], q_[:, t0:t0 + n], True, True, [kf_, q_], [psc])
                            self.mm(psc[:, 512:512 + n], kf_[:, kb * 128:(kb + 1) * 128], q_[:, t0:t0 + n], True, True, [kf_, q_], [psc])
                        jj = j - LOOK
                        if jj >= 0:
                            (t0, n, ka, kb, first, last, po, sg_, ob_) = its[jj]
                            psc = PSc[(ci + jj) % 3]; pt = pT[(ci + jj) % 3]
                            self.act(pt[:].rearrange("p (a b) -> p a b", a=2)[:, :, 0:n], psc[:].rearrange("p (a b) -> p a b", a=2)[:, :, 0:n],
                                     AF.Exp, [psc], [pt], scale=MLA_SCALE)
                            self.mm(po[:, 0:n], va[:, ka, :], pt[:, 0:n], first, False, [va, pt], [po])
                            self.mm(po[:, 0:n], va[:, kb, :], pt[:, 512:512 + n], False, last, [va, pt], [po])
                            if last:
                                self.P.op(DVE, (lambda a_, b_: (lambda e: e.reciprocal(out=a_, in_=b_)))(rden[d0:d0 + 64, 0:n], po[d0:d0 + 64, 0:n]),
                                          [po.b], [rden.b])
                                self.tt(ot[r0:r0 + 64, 0:n], po[r0:r0 + 64, 0:n], rden[d0:d0 + 64, 0:n], ALU.mult, [po, rden], [ot])
                                self.tt(ob_[r0:r0 + 64, 0:n], ot[r0:r0 + 64, 0:n], sg_[r0:r0 + 64, 0:n], ALU.mult, [ot, sg_], [ob_], eng=POOL)
                                kcm = 8 + h // 2
                                self.dma(self.mixT_d[t0 // 128:(t0 + n) // 128, r0:r0 + 64, kcm * 128:(kcm + 1) * 128].rearrange("c p t -> p c t"),
                                         ob_[r0:r0 + 64, 0:n].rearrange("p (c t) -> p c t", t=128), [ob_], [self.mixT_d], slow=True)
                    ci += len(its)

    def mm_group_kr(self, h_, n, wm, PQ1, PQ2):
        for kc in range(8):
            self.mm(PQ1[0:32, 0:n], wm[:, kc, 640:672], h_[:, kc, 0:n], kc == 0, kc == 7, [wm, h_], [PQ1])
        for kc in range(8):
            self.mm(PQ2[0:32, 0:n], wm[:, kc, 672:704], h_[:, kc, 0:n], kc == 0, kc == 7, [wm, h_], [PQ2])

    def stage_E(self, l):
        I = self.I
        with ExitStack() as st:
            wo = self.sb(st, [128, 16, 1024], BF16, "wo")
            wov = I["w_out"][l].rearrange("(k p) c -> p k c", p=128)
            for kc in range(16):
                self.load_w(wo[:, kc, :], wo, wov[:, kc, :], I["w_out"])
            lng = self.bvec(st, "ln_g", l, 1024)
            lnb = self.bvec(st, "ln_b", l, 1024)
            sets = []
            for s_ in range(4):
                B = {}
                B["mt"] = self.sb(st, [128, 16, 128], BF16, "mt")
                B["xt"] = self.sb(st, [128, D], F32, "xt")
                B["v1"] = self.sb(st, [128, D], F32, "v1")
                B["v2"] = self.sb(st, [128, D], F32, "v2")
                B["xo"] = self.sb(st, [128, D], F32, "xo")
                B["stats"] = self.sb(st, [128, 2, 6], F32, "stats")
                B["mv"] = self.sb(st, [128, 2], F32, "mv")
                B["PZ"] = [self.ps(st, [128, 512], F32, "PZ") for _ in range(2)]
                sets.append(B)
            tiles = list(range(NCH)) if l < DEPTH - 1 else list(range(2, NCH))
            gens = [(lambda tt_: (lambda slot: self.e_tile(l, tt_, sets[slot], wo, lng, lnb)))(t) for t in tiles]
            self.interleave(gens, 4)

    def e_tile(self, l, t, B, wo, lng, lnb):
        I = self.I
        m_ = B["mt"]; x_ = B["xt"]; v1 = B["v1"]; v2 = B["v2"]; xo_ = B["xo"]; stats = B["stats"]; mv = B["mv"]; PZ = B["PZ"]
        typ = 1 if t < 2 else 0
        tok = slice(t * 128, (t + 1) * 128)
        self.dma(m_[:].rearrange("p a b -> p (a b)"), self.mixT_d[t], [self.mixT_d], [m_])
        if l == 0:
            src_t = I["ctx"] if t < 2 else I["x"]
            src = src_t[t * 128:(t + 1) * 128, :] if t < 2 else src_t[(t - 2) * 128:(t - 1) * 128, :]
        else:
            src_t = self.xres_d
            src = src_t[tok, :]
        self.dma(x_[:], src, [src_t], [x_])
        yield
        for nb in range(2):
            for kc in range(16):
                self.mm(PZ[nb][:], m_[:, kc, :], wo[:, kc, nb * 512:(nb + 1) * 512], kc == 0, kc == 15, [m_, wo], [PZ[nb]])
            yield
        for nb in range(2):
            self.tt(v1[:, nb * 512:(nb + 1) * 512], PZ[nb][:], self.gB[:, typ, nb * 512:(nb + 1) * 512], ALU.mult, [PZ[nb], self.gB], [v1])
        yield
        self.stt(v2[:], x_[:], ALPHA, v1[:], ALU.mult, ALU.add, [x_, v1], [v2])
        yield
        for s in range(2):
            self.P.op(DVE, (lambda ss: (lambda e: e.bn_stats(out=stats[:, ss, :], in_=v2[:, ss * 512:(ss + 1) * 512])))(s),
                      [v2.b], [stats.b])
        self.P.op(DVE, lambda e: e.bn_aggr(out=mv[:], in_=stats[:]), [stats.b], [mv.b])
        self.ts(mv[:, 1:2], mv[:, 1:2], LN_EPS, None, ALU.add, None, [mv], [mv])
        yield
        self.rsqrt(mv[:, 1:2], mv)
        yield
        self.ts(v1[:], v2[:], mv[:, 0:1], mv[:, 1:2], ALU.subtract, ALU.mult, [v2, mv], [v1])
        yield
        self.tt(v2[:], v1[:], lng[:], ALU.mult, [v1, lng], [v2], eng=POOL)
        yield
        self.tt(xo_[:], v2[:], lnb[:], ALU.add, [v2, lnb], [xo_], eng=POOL)
        yield
        if l < DEPTH - 1:
            self.dma(self.xres_d[tok, :], xo_[:], [xo_], [self.xres_d])
        else:
            self.dma(self.out[(t - 2) * 128:(t - 1) * 128, :], xo_[:], [xo_], [self.out])
        yield


C_ID = 0
C_UF = 128
C_LF = 256
C_UB = 384
C_LB = 512
C_MF = 640
C_MB = 768
C_RIF = 896
C_RIB = 1024
C_GF = 1152
C_GB = 1280
C_TEF = 1408
C_TEB = 1409
CST_W = 1410


def make_consts():
    k = np.arange(128)[:, None].astype(np.float32)
    i = np.arange(128)[None, :].astype(np.float32)
    cst = np.zeros((128, CST_W), np.float32)
    cst[:, C_ID:C_ID + 128] = (k == i)
    cst[:, C_UF:C_UF + 128] = (k <= i)
    cst[:, C_LF:C_LF + 128] = (k > i)
    cst[:, C_UB:C_UB + 128] = (k >= i)
    cst[:, C_LB:C_LB + 128] = (k < i)
    cst[:, C_MF:C_MF + 128] = (k <= i)
    cst[:, C_MB:C_MB + 128] = (k >= i)
    cst[:, C_RIF:C_RIF + 128] = np.maximum(i - k, 0)
    cst[:, C_RIB:C_RIB + 128] = np.maximum(k - i, 0)
    cst[:, C_GF:C_GF + 128] = np.broadcast_to(i + 1, (128, 128))
    cst[:, C_GB:C_GB + 128] = np.broadcast_to(128 - i, (128, 128))
    cst[:, C_TEF] = 127 - k[:, 0]
    cst[:, C_TEB] = k[:, 0]
    return cst


def rope_tables():
    rows = SEQ // 64
    t = np.arange(rows * 64)
    row = (t // 64).astype(np.float32)
    col = (t % 64).astype(np.float32)

    def cs(rot):
        nf = rot // 4
        inv = (np.float32(10000.0) ** (-np.arange(nf, dtype=np.float32) / np.float32(nf))).astype(np.float32)
        ang = np.concatenate([row[:, None] * inv, col[:, None] * inv], -1).astype(np.float32)
        return np.cos(ang).astype(np.float32), np.sin(ang).astype(np.float32)
    cm, sm = cs(32)
    mla = np.zeros((32, 2, NTOK), np.float32)
    mla[:, 0, :CTX] = 1.0
    mla[0:16, 0, CTX:] = cm.T; mla[16:32, 0, CTX:] = cm.T
    mla[0:16, 1, CTX:] = -sm.T; mla[16:32, 1, CTX:] = sm.T
    cr, sr = cs(64)
    ret = np.zeros((128, 4, NTOK), np.float32)
    ret[:, 0, :CTX] = 1.0
    for hh in range(2):
        b = hh * 64
        ret[b:b + 32, 0, CTX:] = cr.T; ret[b + 32:b + 64, 0, CTX:] = cr.T
        ret[b:b + 32, 1, CTX:] = -sr.T; ret[b + 32:b + 64, 1, CTX:] = sr.T
    ret[:, 2] = ret[:, 0] * 0.125
    ret[:, 3] = ret[:, 1] * 0.125
    ret2 = np.zeros((NTOK, 1024), np.float32)
    cc = np.ones((NTOK, 4, 64), np.float32)
    ss = np.zeros((NTOK, 4, 64), np.float32)
    cc[CTX:, :, 0:32] = cr[:, None, :]; cc[CTX:, :, 32:64] = cr[:, None, :]
    ss[CTX:, :, 0:32] = -sr[:, None, :]; ss[CTX:, :, 32:64] = sr[:, None, :]
    ret2 = np.zeros((NTOK, 512), np.float32)
    ret2[:, 0:256] = cc.reshape(NTOK, 256)
    ret2[:, 256:512] = ss.reshape(NTOK, 256)
    return mla, ret, ret2


_CACHE = {}


def prep_inputs(inputs):
    f = lambda a: np.ascontiguousarray(np.asarray(a, dtype=np.float32))
    w_in = f(inputs["w_in"])
    kr = w_in[:, :, 3216:3248]
    kr_sw = np.concatenate([kr[:, :, 16:32], kr[:, :, 0:16]], -1)

    def sw64(a):
        a = a.reshape(2, D, 4, 2, 32)
        return np.ascontiguousarray(a[:, :, :, ::-1, :]).reshape(2, D, 256)
    q_sw = sw64(w_in[:, :, 3760:4016])
    k_sw = sw64(w_in[:, :, 4016:4272])
    w_ext = np.ascontiguousarray(np.concatenate([w_in, kr_sw, q_sw, k_sw], -1))
    uq = f(inputs["mla_w_uq"]).reshape(2, 384, 8, 96)
    uq_r = uq[:, :, :, 64:96]
    uq_sw = np.ascontiguousarray(np.concatenate([uq_r[..., 16:32], uq_r[..., 0:16]], -1)).reshape(2, 384, 256)
    mla_tab, ret_tab, ret_tab2 = rope_tables()
    shared = {
        "w_ada": f(inputs["w_ada"]), "b_ada": f(inputs["b_ada"]), "w_in": w_ext,
        "ssd_conv_w": f(inputs["ssd_conv_w"]), "ssd_conv_b": f(inputs["ssd_conv_b"]),
        "ssd_norm_w": f(inputs["ssd_norm_w"]), "mla_q_norm": f(inputs["mla_q_norm"]),
        "mla_w_uq": f(inputs["mla_w_uq"]), "w_uq_sw": uq_sw, "mla_kv_norm": f(inputs["mla_kv_norm"]),
        "mla_w_ukv": f(inputs["mla_w_ukv"]), "ret_log_rate_f": f(inputs["ret_log_rate_f"]),
        "ret_log_rate_b": f(inputs["ret_log_rate_b"]), "w_out": f(inputs["w_out"]),
        "ln_g": f(inputs["ln_g"]), "ln_b": f(inputs["ln_b"]),
        "cst": make_consts(), "mla_tab": mla_tab, "ret_tab": ret_tab, "ret_tab2": ret_tab2,
    }
    for n in ("ssd_a_log_f", "ssd_a_log_b", "ssd_dt_bias_f", "ssd_dt_bias_b", "ssd_d"):
        shared[n] = f(inputs[n])
    x = f(inputs["x"]); c = f(inputs["c"]); ctx = f(inputs["ctx"]); c_ctx = f(inputs["c_ctx"])
    maps = []
    for b in range(8):
        m = dict(shared)
        m["x"] = x[b]
        m["ctx"] = ctx[b]
        m["cvec"] = np.ascontiguousarray(np.stack([c[b], c_ctx], 0))
        maps.append(m)
    return maps


def kernel(**inputs):
    if "nc" not in _CACHE:
        _CACHE["nc"] = K().build()
    nc = _CACHE["nc"]
    maps = prep_inputs(inputs)
    res = run_bass_kernel_spmd(nc, maps, core_ids=list(range(8)))
    return np.stack([np.asarray(r["out"], dtype=np.float32) for r in res.results], 0)
```

```python
import math
from contextlib import ExitStack
import numpy as np
import concourse.bass as bass
import concourse.mybir as mybir
from concourse.bass_utils import run_bass_kernel_spmd

F32 = mybir.dt.float32
BF16 = mybir.dt.bfloat16
AF = mybir.ActivationFunctionType
ALU = mybir.AluOpType

PE, ACT, DVE, POOL, SP = "pe", "act", "dve", "pool", "sp"
EPOCH = 30000
DMA_K = 16
DMA_EPOCH = 1800

D = 1024
SEQ = 4096
CTX = 256
NTOK = SEQ + CTX
NCH = NTOK // 128
HC = NTOK + 8
DEPTH = 2
ALPHA = (2 * DEPTH) ** 0.25
LN_EPS = 1e-5
RMS_EPS = 1e-6
MLA_SCALE = 96 ** -0.5
WEXT = 5296 + 32 + 256 + 256


def colof(t):
    return t + 2 if t < CTX else t + 6


class Buf:
    __slots__ = ("name", "lw", "rd", "rdd", "psum")

    def __init__(self, name=""):
        self.name = name
        self.lw = None
        self.rd = {}
        self.rdd = []
        self.psum = False


class Op:
    __slots__ = ("eng", "fn", "deps", "sig", "idx", "dma_slot", "dma_prev")

    def __init__(self, eng, fn):
        self.eng = eng
        self.fn = fn
        self.deps = set()
        self.sig = None
        self.dma_slot = None
        self.dma_prev = None


class Prog:
    def __init__(self, nc):
        self.nc = nc
        self.ops = []
        self.eng = {PE: nc.tensor, ACT: nc.scalar, DVE: nc.vector, POOL: nc.gpsimd, SP: nc.sync}
        self.dma_lists = {}
        self.last = {}

    def op(self, eng, fn, reads=(), writes=(), dma=False):
        o = Op(eng, fn)
        o.idx = len(self.ops)
        for b in reads:
            if b.lw is not None:
                o.deps.add(b.lw)
            if b.psum:
                for e2, r in b.rd.items():
                    if e2 != eng:
                        o.deps.add(r)
        for b in writes:
            if b.lw is not None:
                o.deps.add(b.lw)
            for r in b.rd.values():
                o.deps.add(r)
            for r in b.rdd:
                o.deps.add(r)
        for b in reads:
            if dma:
                b.rdd.append(o.idx)
            else:
                b.rd[eng] = o.idx
        for b in writes:
            b.lw = o.idx
            b.rd = {}
            b.rdd = []
        if dma:
            lst = self.dma_lists.setdefault(eng, [])
            o.dma_slot = len(lst)
            if len(lst) >= DMA_K:
                o.dma_prev = lst[len(lst) - DMA_K]
            lst.append(o.idx)
        o.deps.discard(o.idx)
        self.ops.append(o)
        self.last[eng] = o.idx
        return o

    def barrier(self):
        bufs = {}
        for e in (PE, ACT, DVE, POOL, SP):
            bufs[e] = Buf("bar" + e)
            o = self.op(e, lambda en: en.nop(), writes=[bufs[e]])
            for lst in self.dma_lists.values():
                for d in lst[-DMA_K:]:
                    if d != o.idx:
                        o.deps.add(d)
        for e in (PE, ACT, DVE, POOL, SP):
            self.op(e, lambda en: en.nop(), reads=list(bufs.values()))

    def emit(self, stack):
        nc = self.nc
        ops = self.ops
        needed = set()
        for o in ops:
            for d in o.deps:
                do = ops[d]
                if do.eng == o.eng and o.eng == PE and do.dma_slot is None:
                    continue
                needed.add(d)
            if o.dma_prev is not None:
                needed.add(o.dma_prev)
        cnt = {}
        sems = {}
        dma_sems = {}
        for o in ops:
            if o.dma_slot is not None:
                k = o.dma_slot % DMA_K
                n = o.dma_slot // DMA_K
                key = (o.eng, k, n // DMA_EPOCH)
                if key not in dma_sems:
                    dma_sems[key] = stack.enter_context(nc.semaphore("dq%s%d_%d" % key))
                o.sig = (dma_sems[key], 16 * (n % DMA_EPOCH + 1))
            elif o.idx in needed:
                c = cnt.get(o.eng, 0)
                key = (o.eng, c // EPOCH)
                if key not in sems:
                    sems[key] = stack.enter_context(nc.semaphore("s%s_%d" % key))
                o.sig = (sems[key], c % EPOCH + 1)
                cnt[o.eng] = c + 1
        waited = {}
        nw = 0
        for o in ops:
            e = self.eng[o.eng]
            deps = set(o.deps)
            if o.dma_prev is not None:
                deps.add(o.dma_prev)
            for d in sorted(deps):
                do = ops[d]
                if do.sig is None:
                    continue
                sem, val = do.sig
                key = (o.eng, id(sem))
                if waited.get(key, 0) >= val:
                    continue
                waited[key] = val
                e.wait_ge(sem, val)
                nw += 1
            ins = o.fn(e)
            if o.sig is not None:
                sem, val = o.sig
                ins.then_inc(sem, 16 if o.dma_slot is not None else 1)
        self.nwaits = nw


class T:
    __slots__ = ("t", "b")

    def __init__(self, t, name=""):
        self.t = t
        self.b = Buf(name)

    def __getitem__(self, k):
        return self.t[k]


class K:
    def __init__(self, debug=False, stop_after=None, skip=()):
        self.skip = set(skip)
        self.debug = debug
        self.stop_after = stop_after
        self.nc = bass.Bass("TRN2", target_bir_lowering=False)
        self.P = Prog(self.nc)
        self.uid = 0

    def dram(self, name, shape, dt, kind="Internal"):
        return T(self.nc.dram_tensor(name, list(shape), dt, kind=kind).ap(), name)

    def sb(self, st, shape, dt, name=None):
        self.uid += 1
        name = "%s_%d" % (name or "t", self.uid)
        return T(st.enter_context(self.nc.sbuf_tensor(name, list(shape), dt)), name)

    def ps(self, st, shape, dt=F32, name=None):
        self.uid += 1
        name = "%s_%d" % (name or "p", self.uid)
        t = T(st.enter_context(self.nc.psum_tensor(name, list(shape), dt)), name)
        t.b.psum = True
        return t

    def dma(self, out, in_, reads, writes, eng=SP, slow=False):
        if slow:
            return self.P.op(eng, lambda e: e.dma_start(out=out, in_=in_, allow_slow_non_contiguous=True),
                             [x.b for x in reads], [x.b for x in writes], dma=True)
        return self.P.op(eng, lambda e: e.dma_start(out=out, in_=in_), [x.b for x in reads], [x.b for x in writes], dma=True)

    def mm(self, out, lhsT, rhs, start, stop, reads, writes):
        return self.P.op(PE, lambda e: e.matmul(out, lhsT=lhsT, rhs=rhs, start=start, stop=stop),
                         [x.b for x in reads], [x.b for x in writes])

    def tr(self, out, in_, ident, reads, writes):
        return self.P.op(PE, lambda e: e.transpose(out=out, in_=in_, identity=ident),
                         [x.b for x in reads], [x.b for x in writes])

    def act(self, out, in_, func, reads, writes, bias=None, scale=None, accum_out=None):
        kw = {}
        if bias is not None:
            kw["bias"] = bias
        if scale is not None:
            kw["scale"] = scale
        if accum_out is not None:
            kw["accum_out"] = accum_out
        return self.P.op(ACT, lambda e: e.activation(out=out, in_=in_, func=func, **kw),
                         [x.b for x in reads], [x.b for x in writes])

    def tt(self, out, in0, in1, op, reads, writes, eng=DVE):
        return self.P.op(eng, lambda e: e.tensor_tensor(out=out, in0=in0, in1=in1, op=op),
                         [x.b for x in reads], [x.b for x in writes])

    def ts(self, out, in0, s1, s2, op0, op1, reads, writes, eng=DVE):
        if op1 is None:
            return self.P.op(eng, lambda e: e.tensor_scalar(out=out, in0=in0, scalar1=s1, scalar2=None, op0=op0),
                             [x.b for x in reads], [x.b for x in writes])
        return self.P.op(eng, lambda e: e.tensor_scalar(out=out, in0=in0, scalar1=s1, scalar2=s2, op0=op0, op1=op1),
                         [x.b for x in reads], [x.b for x in writes])

    def stt(self, out, in0, scalar, in1, op0, op1, reads, writes, eng=DVE):
        return self.P.op(eng, lambda e: e.scalar_tensor_tensor(out=out, in0=in0, scalar=scalar, in1=in1, op0=op0, op1=op1),
                         [x.b for x in reads], [x.b for x in writes])

    def cp(self, out, in_, reads, writes, eng=DVE):
        return self.P.op(eng, lambda e: e.tensor_copy(out=out, in_=in_), [x.b for x in reads], [x.b for x in writes])

    def memset(self, out, val, writes, eng=POOL):
        return self.P.op(eng, lambda e: e.memset(out, val), [], [x.b for x in writes])

    def sigm(self, ap, t):
        self.act(ap, ap, AF.Ln, [t], [t], bias=1.0)
        self.act(ap, ap, AF.Exp, [t], [t], scale=-1.0)

    def rsqrt(self, ap, t):
        self.act(ap, ap, AF.Ln, [t], [t])
        self.act(ap, ap, AF.Exp, [t], [t], scale=-0.5)

    def build(self):
        nc = self.nc
        dbg = self.debug
        I = {}

        def inp(name, shape):
            I[name] = self.dram(name, shape, F32, kind="ExternalInput")
        inp("x", [SEQ, D]); inp("ctx", [CTX, D]); inp("cvec", [2, D])
        inp("w_ada", [2, D, 3 * D]); inp("b_ada", [2, 3 * D]); inp("w_in", [2, D, WEXT])
        inp("ssd_conv_w", [2, 5, 1536]); inp("ssd_conv_b", [2, 1536])
        for n in ("ssd_a_log_f", "ssd_a_log_b", "ssd_dt_bias_f", "ssd_dt_bias_b", "ssd_d"):
            inp(n, [2, 16])
        inp("ssd_norm_w", [2, 1024]); inp("mla_q_norm", [2, 384]); inp("mla_w_uq", [2, 384, 768])
        inp("w_uq_sw", [2, 384, 256]); inp("mla_kv_norm", [2, 256]); inp("mla_w_ukv", [2, 256, 1024])
        inp("ret_log_rate_f", [2, 4]); inp("ret_log_rate_b", [2, 4]); inp("w_out", [2, 2048, D])
        inp("ln_g", [2, D]); inp("ln_b", [2, D])
        inp("cst", [128, CST_W]); inp("mla_tab", [32, 2, NTOK]); inp("ret_tab", [128, 4, NTOK]); inp("ret_tab2", [NTOK, 512])
        self.I = I
        okind = "ExternalOutput" if dbg else "Internal"
        self.out = self.dram("out", [SEQ, D], F32, kind="ExternalOutput")
        self.hT_d = self.dram("hT_d", [128, 8 * HC], BF16, kind=okind)
        self.ypart_d = self.dram("ypart_d", [NTOK, 1024], BF16)
        self.rpart_d = self.dram("rpart_d", [NTOK, 512], BF16)
        self.ypartb_d = self.dram("ypartb_d", [NTOK, 1024], BF16)
        self.Fb_d = self.dram("Fb_d", [NCH, 2, 128, 768], BF16)
        self.Rb_d = self.dram("Rb_d", [NCH, 128, 1280], BF16)
        self.Ff_d = self.dram("Ff_d", [NCH, 2, 128, 136], F32)
        self.rpartb_d = self.dram("rpartb_d", [NTOK, 512], BF16)
        self.mixT_d = self.dram("mixT_d", [NCH, 128, 2048], BF16, kind=okind)
        self.qT_d = self.dram("qT_d", [8, 96, NTOK], BF16)
        self.kfT_d = self.dram("kfT_d", [8, 96, NTOK], BF16)
        self.sgT_d = self.dram("sgT_d", [512, NTOK], F32)
        self.xres_d = self.dram("xres_d", [NTOK, D], F32, kind=okind)

        with ExitStack() as gst:
            self.gst = gst
            self.cst = self.sb(gst, [128, CST_W], F32, "cst")
            self.dma(self.cst[:], I["cst"][:], [I["cst"]], [self.cst])
            self.identb = self.sb(gst, [128, 128], BF16, "identb")
            self.cp(self.identb[:], self.cst[:, C_ID:C_ID + 128], [self.cst], [self.identb])
            self.ones = self.sb(gst, [128, 128], F32, "ones")
            self.memset(self.ones[:], 1.0, [self.ones])
            self.gB = self.sb(gst, [128, 2, 1024], F32, "gB")
            zt = self.sb(gst, [128, 8, 4], BF16, "zt")
            self.memset(zt[:], 0.0, [zt])
            hv = self.hT_d[:].rearrange("p (k c) -> p k c", k=8)
            self.hv = hv
            for (a, b) in ((0, 2), (258, 262), (4358, 4360)):
                self.dma(hv[:, :, a:b], zt[:, :, 0:b - a], [zt], [self.hT_d], slow=True)
            self.P.barrier()
            stages = []
            for l in range(DEPTH):
                stages += [("A", l), ("S", l), ("M", l), ("R", l), ("E", l)]
            for (s, l) in stages:
                if s in self.skip:
                    continue
                if s == "A":
                    self.stage_A(l)
                elif s == "S":
                    self.stage_S(l)
                elif s == "M":
                    self.stage_M(l)
                elif s == "R":
                    self.stage_R(l)
                else:
                    self.stage_E(l)
                self.P.barrier()
                if self.stop_after == (s, l):
                    break
            self.P.barrier()
            self.P.emit(gst)
        return nc

    def silu_psum(self, st, src_ap, src_t, out_ap, out_t, e_t, e_ap, r_ap):
        self.act(e_ap, src_ap, AF.Exp, [src_t], [e_t], scale=-1.0)
        self.sigm(e_ap, e_t)
        self.tt(out_ap, src_ap, r_ap, ALU.mult, [src_t, e_t], [out_t])

    def stage_A(self, l):
        I = self.I
        with ExitStack() as st:
            wada = [self.sb(st, [128, 8, 512], F32, "wada") for _ in range(2)]
            craw = self.sb(st, [128, 8, 2], F32, "craw")
            ce = self.sb(st, [128, 8, 2], F32, "ce")
            scT = self.sb(st, [128, 8, 2], F32, "scT")
            modT = self.sb(st, [128, 24, 2], F32, "modT")
            scale1 = self.sb(st, [128, 8, 2], F32, "scale1")
            brow = self.sb(st, [1, 3 * D], F32, "brow")
            pm = self.ps(st, [128, 512], F32, "pm")
            pg = [self.ps(st, [128, 512], F32, "pg") for _ in range(2)]
            for j in range(2):
                self.dma(craw[:, :, j], I["cvec"][j].rearrange("(k p) -> p k", p=128), [I["cvec"]], [craw], slow=True)
            self.dma(brow[:], I["b_ada"][l:l + 1, :], [I["b_ada"]], [brow])
            self.act(ce[:], craw[:], AF.Exp, [craw], [ce], scale=-1.0)
            self.sigm(ce[:], ce)
            self.tt(scT[:], craw[:], ce[:], ALU.mult, [craw, ce], [scT])
            wv = I["w_ada"][l].rearrange("(k p) c -> p k c", p=128)
            for cb in range(6):
                w = wada[cb % 2]
                self.dma(w[:], wv[:, :, cb * 512:(cb + 1) * 512], [I["w_ada"]], [w])
                if cb < 4:
                    for dj in range(4):
                        j = cb * 4 + dj
                        for kc in range(8):
                            self.mm(pm[:, 2 * dj:2 * dj + 2], w[:, kc, dj * 128:(dj + 1) * 128], scT[:, kc, :],
                                    kc == 0, False, [w, scT], [pm])
                        self.mm(pm[:, 2 * dj:2 * dj + 2], brow[0:1, j * 128:(j + 1) * 128], self.ones[0:1, 0:2],
                                False, True, [brow, self.ones], [pm])
                        self.cp(modT[:, j, :], pm[:, 2 * dj:2 * dj + 2], [pm], [modT])
                else:
                    for typ in range(2):
                        p = pg[typ]
                        for kc in range(8):
                            self.mm(p[:], scT[:, kc, typ:typ + 1].to_broadcast([128, 128]), w[:, kc, :],
                                    kc == 0, False, [w, scT], [p])
                        self.mm(p[:], self.ones[0:1, 0:128], brow[0:1, cb * 512:(cb + 1) * 512], False, True,
                                [brow, self.ones], [p])
                        self.cp(self.gB[:, typ, (cb - 4) * 512:(cb - 3) * 512], p[:], [p], [self.gB])
            self.ts(scale1[:], modT[:, 8:16, :], 1.0, None, ALU.add, None, [modT], [scale1])
            sets = []
            for s_ in range(3):
                B = {}
                B["xt"] = self.sb(st, [128, D], F32, "xt")
                B["ht"] = self.sb(st, [128, 8, 128], BF16, "ht")
                B["pT"] = pg if s_ == 0 else [self.ps(st, [128, 512], F32, "pT") for _ in range(2)]
                sets.append(B)
            gens = [(lambda tt_: (lambda slot: self.a_tile(l, tt_, sets[slot], scale1, modT)))(t) for t in range(NCH)]
            self.interleave(gens, 3)

    def a_tile(self, l, t, B, scale1, modT):
        I = self.I
        x_ = B["xt"]; h_ = B["ht"]; pT = B["pT"]
        typ = 1 if t < 2 else 0
        if l == 0:
            src_t = I["ctx"] if t < 2 else I["x"]
            src = src_t[t * 128:(t + 1) * 128, :] if t < 2 else src_t[(t - 2) * 128:(t - 1) * 128, :]
        else:
            src_t = self.xres_d
            src = src_t[t * 128:(t + 1) * 128, :]
        self.dma(x_[:], src, [src_t], [x_])
        yield
        for kc in range(8):
            p_ = pT[kc // 4]
            self.tr(p_[:, (kc % 4) * 128:(kc % 4 + 1) * 128], x_[:, kc * 128:(kc + 1) * 128], self.cst[:, C_ID:C_ID + 128],
                    [x_, self.cst], [p_])
            if kc % 4 == 3:
                yield
        for kc in range(8):
            p_ = pT[kc // 4]
            self.act(h_[:, kc, :], p_[:, (kc % 4) * 128:(kc % 4 + 1) * 128], AF.Identity, [p_, scale1, modT], [h_],
                     bias=modT[:, kc, typ:typ + 1], scale=scale1[:, kc, typ:typ + 1])
            if kc % 4 == 3:
                yield
        c0 = colof(t * 128)
        self.dma(self.hv[:, :, c0:c0 + 128], h_[:], [h_], [self.hT_d], eng=ACT)
        yield

    def load_w(self, dst_ap, dst_t, src_ap, src_t):
        self.dma(dst_ap, src_ap, [src_t], [dst_t], eng=POOL, slow=True)

    def bvec(self, st, name, l, n):
        t = self.sb(st, [128, n], F32, name)
        self.dma(t[:], self.I[name][l:l + 1, :].to_broadcast([128, n]), [self.I[name]], [t], slow=True)
        return t

    def interleave(self, factories, width):
        pending = list(factories)
        active = []
        for s in range(width):
            if pending:
                active.append((s, pending.pop(0)(s)))
        while active:
            nxt = []
            for (s, g) in active:
                try:
                    next(g)
                    nxt.append((s, g))
                except StopIteration:
                    if pending:
                        nxt.append((s, pending.pop(0)(s)))
            active = nxt

    def stage_S(self, l):
        I = self.I
        cst = self.cst
        import os as _os
        with ExitStack() as st:
            wv = I["w_in"][l].rearrange("(k p) c -> p k c", p=128)
            with ExitStack() as stf:
                wx = self.sb(stf, [128, 8, 1536], BF16, "wx")
                wdt = self.sb(stf, [128, 8, 16], BF16, "wdt")
                for kc in range(8):
                    self.load_w(wx[:, kc, :], wx, wv[:, kc, 1024:2560], I["w_in"])
                self.load_w(wdt[:], wdt, wv[:, :, 2560:2576], I["w_in"])
                convw = self.sb(stf, [128, 12, 5], F32, "convw")
                for k in range(5):
                    self.dma(convw[:, :, k], I["ssd_conv_w"][l, k].rearrange("(r p) -> p r", p=128), [I["ssd_conv_w"]], [convw], slow=True)
                dg = self.sb(stf, [128, 60, 128], BF16, "dg")
                for r in range(12):
                    for k in range(5):
                        self.ts(dg[:, r * 5 + k, :], self.identb[:], convw[:, r, k:k + 1], None, ALU.mult, None,
                                [self.identb, convw], [dg])
                cb32 = self.sb(stf, [1, 1536], F32, "cb32")
                self.dma(cb32[:], I["ssd_conv_b"][l:l + 1, :], [I["ssd_conv_b"]], [cb32])
                cbh = self.sb(stf, [1, 1536], BF16, "cbh")
                cbh32 = self.sb(stf, [1, 1536], F32, "cbh32")
                cbl = self.sb(stf, [1, 1536], BF16, "cbl")
                self.cp(cbh[:], cb32[:], [cb32], [cbh])
                self.cp(cbh32[:], cbh[:], [cbh], [cbh32])
                self.tt(cbh32[:], cb32[:], cbh32[:], ALU.subtract, [cb32, cbh32], [cbh32])
                self.cp(cbl[:], cbh32[:], [cbh32], [cbl])
                cbrow = self.sb(stf, [2, 1536], BF16, "cbrow")
                self.dma(cbrow[0:1, :], cbh[:], [cbh], [cbrow])
                self.dma(cbrow[1:2, :], cbl[:], [cbl], [cbrow])
                ones2 = self.sb(stf, [2, 128], BF16, "ones2")
                self.memset(ones2[:], 1.0, [ones2])
                self.ones2 = ones2
                sets = []
                for s_ in range(2):
                    B = {}
                    B["hc"] = self.sb(stf, [128, 8, 132], BF16, "hcf")
                    B["xbc"] = self.sb(stf, [128, 12, 132], BF16, "xbc")
                    B["esb"] = self.sb(stf, [128, 1536], F32, "esbf")
                    B["ubf"] = self.sb(stf, [128, 12, 128], BF16, "ubf")
                    B["Fb"] = self.sb(stf, [128, 2, 768], BF16, "Fbw")
                    B["Ff"] = self.sb(stf, [128, 2, 136], F32, "Ffw")
                    B["Q"] = [self.ps(stf, [128, 512], F32, "QF") for _ in range(4)]
                    sets.append(B)
                gens = [(lambda cc: (lambda slot: self.ssd_front(cc, sets[slot], wx, wdt, dg, cbrow)))(c) for c in range(NCH)]
                self.interleave(gens, 2)
            self.P.barrier()
            Dsk = self.bvec(st, "ssd_d", l, 16)
            prm = {}
            for d_, sfx in ((0, "f"), (1, "b")):
                al = self.bvec(st, "ssd_a_log_" + sfx, l, 16)
                self.act(al[:], al[:], AF.Exp, [al], [al])
                self.ts(al[:], al[:], -1.0, None, ALU.mult, None, [al], [al])
                dtb = self.bvec(st, "ssd_dt_bias_" + sfx, l, 16)
                Ub = self.sb(st, [128, 128], BF16, "Ub16")
                Lb = self.sb(st, [128, 128], BF16, "Lb16")
                Uo = C_UF if d_ == 0 else C_UB
                Lo = C_LF if d_ == 0 else C_LB
                self.cp(Ub[:], cst[:, Uo:Uo + 128], [cst], [Ub])
                self.cp(Lb[:], cst[:, Lo:Lo + 128], [cst], [Lb])
                prm[d_] = (al, dtb, Ub, Lb)
            with ExitStack() as st2:
                gens = []
                self.yb = {}
                for d_ in (0, 1):
                    for g_ in (0, 1):
                        self.yb[(d_, g_)] = T((self.ypart_d if d_ == 0 else self.ypartb_d).t, "yb")
                        gens.append((lambda dd, gg: (lambda slot: self.ssd_sweep(l, dd, gg, st2, Dsk, prm[dd])))(d_, g_))
                self.interleave(gens, 4)
            self.P.barrier()
            with ExitStack() as st3:
                wz = self.sb(st3, [128, 8, 1024], BF16, "wz")
                for kc in range(8):
                    self.load_w(wz[:, kc, :], wz, wv[:, kc, 0:1024], I["w_in"])
                nwB = self.bvec(st3, "ssd_norm_w", l, 1024)
                sets = []
                for s in range(4):
                    B = {}
                    B["hc"] = self.sb(st3, [128, 8, 128], BF16, "hc3")
                    B["ypf"] = self.sb(st3, [128, 1024], BF16, "ypf")
                    B["ypb"] = self.sb(st3, [128, 1024], BF16, "ypb")
                    B["ys"] = self.sb(st3, [128, 1024], F32, "ys3")
                    B["e"] = self.sb(st3, [128, 1024], F32, "e3")
                    B["t1"] = self.sb(st3, [128, 1024], F32, "t13")
                    B["bst"] = self.sb(st3, [128, 2, 6], F32, "bst3")
                    B["ssq"] = self.sb(st3, [128, 2], F32, "ssq3")
                    B["ob"] = self.sb(st3, [128, 1024], BF16, "ob3")
                    B["oT"] = self.sb(st3, [128, 8, 128], BF16, "oT3")
                    B["PZ"] = [self.ps(st3, [128, 512], F32, "PZ3") for _ in range(2)]
                    B["PT"] = B["PZ"][0]
                    sets.append(B)
                gens = [(lambda cc: (lambda slot: self.ssd_final(cc, sets[slot], wz, nwB)))(c) for c in range(NCH)]
                self.interleave(gens, 4)

    def ssd_final(self, c, B, wz, nwB):
        h_ = B["hc"]; ypf = B["ypf"]; ypb = B["ypb"]; e = B["e"]; t1 = B["t1"]; bst = B["bst"]; ssq = B["ssq"]
        ob = B["ob"]; oT_ = B["oT"]; PZ = B["PZ"]; PT = B["PT"]
        c0 = colof(c * 128)
        tok = slice(c * 128, (c + 1) * 128)
        self.dma(h_[:], self.hv[:, :, c0:c0 + 128], [self.hT_d], [h_], slow=True)
        self.dma(ypf[:], self.ypart_d[tok, :], [self.yb[(0, 0)], self.yb[(0, 1)]], [ypf])
        self.dma(ypb[:], self.ypartb_d[tok, :], [self.yb[(1, 0)], self.yb[(1, 1)]], [ypb])
        yield
        for n in range(2):
            for kc in range(8):
                self.mm(PZ[n][:], h_[:, kc, :], wz[:, kc, n * 512:(n + 1) * 512], kc == 0, kc == 7, [h_, wz], [PZ[n]])
        ys = B["ys"]
        self.tt(ys[:], ypf[:], ypb[:], ALU.add, [ypf, ypb], [ys])
        yield
        for n in range(2):
            self.act(e[:, n * 512:(n + 1) * 512], PZ[n][:], AF.Exp, [PZ[n]], [e], scale=-1.0)
        yield
        self.sigm(e[:], e)
        yield
        for n in range(2):
            self.tt(t1[:, n * 512:(n + 1) * 512], PZ[n][:], e[:, n * 512:(n + 1) * 512], ALU.mult, [PZ[n], e], [t1])
        yield
        self.tt(t1[:], t1[:], ys[:], ALU.mult, [t1, ys], [t1])
        yield
        for s_ in range(2):
            self.P.op(DVE, (lambda ss: (lambda en: en.bn_stats(out=bst[:, ss, :], in_=t1[:, ss * 512:(ss + 1) * 512])))(s_),
                      [t1.b], [bst.b])
        self.P.op(DVE, lambda en: en.bn_aggr(out=ssq[:], in_=bst[:]), [bst.b], [ssq.b])
        self.stt(ssq[:, 1:2], ssq[:, 0:1], ssq[:, 0:1], ssq[:, 1:2], ALU.mult, ALU.add, [ssq], [ssq])
        self.ts(ssq[:, 1:2], ssq[:, 1:2], RMS_EPS, None, ALU.add, None, [ssq], [ssq])
        yield
        self.rsqrt(ssq[:, 1:2], ssq)
        yield
        self.stt(ob[:], t1[:], ssq[:, 1:2], nwB[:], ALU.mult, ALU.mult, [t1, ssq, nwB], [ob])
        yield
        PTb = PT[:].bitcast(BF16)
        for r in range(8):
            self.tr(PTb[:, r * 128:(r + 1) * 128], ob[:, r * 128:(r + 1) * 128], self.identb[:], [ob, self.identb], [PT])
        yield
        self.act(oT_[:], PTb[:, 0:1024].rearrange("p (a b) -> p a b", a=8), AF.Copy, [PT], [oT_])
        yield
        self.dma(self.mixT_d[c, :, 0:1024], oT_[:].rearrange("p a b -> p (a b)"), [oT_], [self.mixT_d], eng=ACT)
        yield

    def ssd_front(self, c, B, wx, wdt, dg, cbrow):
        h_ = B["hc"]; xbc = B["xbc"]; esb = B["esb"]; ubf = B["ubf"]; Fb = B["Fb"]; Ff = B["Ff"]; Q = B["Q"]
        Qb0 = Q[0][:].bitcast(BF16)
        Qb1 = Q[1][:].bitcast(BF16)
        c0 = colof(c * 128)
        self.dma(h_[:], self.hv[:, :, c0 - 2:c0 + 130], [self.hT_d], [h_], slow=True)
        yield
        for r in range(12):
            q_ = Q[r // 3]
            o_ = q_[:, (r % 3) * 132:(r % 3) * 132 + 132]
            for kc in range(8):
                self.mm(o_, wx[:, kc, r * 128:(r + 1) * 128], h_[:, kc, :], kc == 0, kc == 7, [wx, h_], [q_])
            if r % 3 == 2:
                yield
        for q in range(4):
            self.act(xbc[:, 3 * q:3 * q + 3, :], Q[q][:, 0:396].rearrange("p (a b) -> p a b", a=3), AF.Copy, [Q[q]], [xbc])
        yield
        for kc in range(8):
            self.mm(Q[3][:, 0:16], h_[:, kc, 2:130], wdt[:, kc, :], kc == 0, kc == 7, [h_, wdt], [Q[3]])
        yield
        for r in range(12):
            q_ = Q[r // 4]
            o_ = q_[:, (r % 4) * 128:(r % 4 + 1) * 128]
            for k in range(5):
                self.mm(o_, dg[:, r * 5 + k, :], xbc[:, r, k:k + 128], k == 0, False, [dg, xbc], [q_])
            self.mm(o_, cbrow[:, r * 128:(r + 1) * 128], self.ones2[:], False, True, [cbrow, self.ones2], [q_])
            if r % 4 == 3:
                yield
        self.cp(Ff[:, :, 128:136], Q[3][:, 0:16].rearrange("p (g n) -> p g n", g=2), [Q[3]], [Ff])
        for q in range(3):
            self.act(esb[:, q * 512:(q + 1) * 512], Q[q][:], AF.Exp, [Q[q]], [esb], scale=-1.0)
        yield
        self.sigm(esb[:], esb)
        yield
        for q in range(3):
            self.tt(ubf[:, 4 * q:4 * q + 4, :], Q[q][:].rearrange("p (a b) -> p a b", a=4),
                    esb[:, q * 512:(q + 1) * 512].rearrange("p (a b) -> p a b", a=4), ALU.mult, [Q[q], esb], [ubf])
        yield
        for g in range(2):
            self.mm(Q[3][:, 256 + g * 128:256 + (g + 1) * 128], ubf[:, 8 + g, :], ubf[:, 10 + g, :], True, True, [ubf], [Q[3]])
        for r in range(8):
            self.tr(Qb0[:, r * 128:(r + 1) * 128], ubf[:, r, :], self.identb[:], [ubf, self.identb], [Q[0]])
        for r in range(2):
            self.tr(Qb1[:, r * 128:(r + 1) * 128], ubf[:, 8 + r, :], self.identb[:], [ubf, self.identb], [Q[1]])
        self.cp(Fb[:, :, 640:768], ubf[:, 10:12, :], [ubf], [Fb], eng=POOL)
        yield
        self.cp(Ff[:, :, 0:128], Q[3][:, 256:512].rearrange("p (g n) -> p g n", g=2), [Q[3]], [Ff])
        self.act(Fb[:, :, 0:512], Qb0[:, 0:1024].rearrange("p (g n) -> p g n", g=2), AF.Copy, [Q[0]], [Fb])
        self.cp(Fb[:, :, 512:640], Qb1[:, 0:256].rearrange("p (g n) -> p g n", g=2), [Q[1]], [Fb])
        yield
        self.dma(self.Fb_d[c].rearrange("g p n -> p g n"), Fb[:], [Fb], [self.Fb_d], eng=ACT)
        self.dma(self.Ff_d[c].rearrange("g p n -> p g n"), Ff[:], [Ff], [self.Ff_d], eng=ACT)
        yield

    def ssd_sweep(self, l, d_, g, st, Dsk, prm):
        cst = self.cst
        al, dtb, Ub, Lb = prm
        HS = slice(g * 8, (g + 1) * 8)
        H = self.sb(st, [128, 512], F32, "H")
        H2 = self.sb(st, [128, 512], F32, "H2")
        Hbf = self.sb(st, [128, 512], BF16, "Hbf")
        Fb = [self.sb(st, [128, 768], BF16, "Fbr") for _ in range(2)]
        Ff = [self.sb(st, [128, 136], F32, "Ffr") for _ in range(2)]
        esb = self.sb(st, [128, 1024], F32, "esb")
        dtx = self.sb(st, [128, 8], F32, "dtx")
        dt = self.sb(st, [128, 8], F32, "dt")
        la = self.sb(st, [128, 8], F32, "la")
        lah = self.sb(st, [128, 8], BF16, "lah")
        lah32 = self.sb(st, [128, 8], F32, "lah32")
        lal32 = self.sb(st, [128, 8], F32, "lal32")
        lalo = self.sb(st, [128, 8], BF16, "lalo")
        E3 = self.sb(st, [128, 24], F32, "E3")
        scm = self.sb(st, [128, 128], F32, "scm")
        LaU = self.sb(st, [128, 8, 128], F32, "LaU")
        M = self.sb(st, [128, 8, 128], BF16, "M")
        v = self.sb(st, [128, 512], BF16, "v")
        vte = self.sb(st, [128, 512], BF16, "vte")
        t1 = self.sb(st, [128, 512], F32, "t1")
        t2 = self.sb(st, [128, 512], F32, "t2")
        yp = [self.sb(st, [128, 512], BF16, "yp") for _ in range(2)]
        Q = [self.ps(st, [128, 512], F32, "Q") for _ in range(2)]
        ydst = self.ypart_d if d_ == 0 else self.ypartb_d
        order = list(range(NCH)) if d_ == 0 else [1, 0] + list(range(NCH - 1, 1, -1))
        Uo = C_UF if d_ == 0 else C_UB
        Lo = C_LF if d_ == 0 else C_LB
        Mo = C_MF if d_ == 0 else C_MB
        self.memset(H[:], 0.0, [H])
        self.memset(Hbf[:], 0.0, [Hbf])

        def loads(ci, slot):
            c = order[ci]
            self.dma(Fb[slot][:], self.Fb_d[c, g], [self.Fb_d], [Fb[slot]])
            self.dma(Ff[slot][:], self.Ff_d[c, g], [self.Ff_d], [Ff[slot]])
        loads(0, 0)
        it = 0
        for ci, c in enumerate(order):
            fb = Fb[it % 2]; ff = Ff[it % 2]; ypt = yp[it % 2]
            it += 1
            tok = slice(c * 128, (c + 1) * 128)
            if ci + 1 < len(order):
                loads(ci + 1, it % 2)
            xs_g = fb[:, 0:512]
            self.tt(dtx[:], ff[:, 128:136], dtb[:, HS], ALU.add, [ff, dtb], [dtx])
            self.tt(scm[:], ff[:, 0:128], cst[:, Mo:Mo + 128], ALU.mult, [ff, cst], [scm])
            yield
            self.act(dtx[:], dtx[:], AF.Exp, [dtx], [dtx])
            self.act(dt[:], dtx[:], AF.Ln, [dtx], [dt], bias=1.0)
            yield
            self.tt(la[:], dt[:], al[:, HS], ALU.mult, [dt, al], [la])
            yield
            self.mm(Q[1][:, 0:8], cst[:, Uo:Uo + 128], la[:], True, True, [cst, la], [Q[1]])
            self.mm(Q[1][:, 8:16], cst[:, Lo:Lo + 128], la[:], True, True, [cst, la], [Q[1]])
            self.mm(Q[1][:, 16:24], self.ones[:], la[:], True, True, [self.ones, la], [Q[1]])
            self.tt(LaU[:], la[:].unsqueeze(2).to_broadcast([128, 8, 128]),
                    cst[:, Uo:Uo + 128].unsqueeze(1).to_broadcast([128, 8, 128]), ALU.mult, [la, cst], [LaU], eng=POOL)
            self.tt(v[:].rearrange("p (h e) -> p h e", h=8), xs_g.rearrange("p (h e) -> p h e", h=8),
                    dt[:].unsqueeze(2).to_broadcast([128, 8, 64]), ALU.mult, [fb, dt], [v], eng=POOL)
            yield
            self.act(E3[:], Q[1][:, 0:24], AF.Exp, [Q[1]], [E3])
            yield
            for q in range(2):
                self.mm(Q[q][:], cst[:, Lo:Lo + 128], LaU[:, 4 * q:4 * q + 4, :].rearrange("p a b -> p (a b)"), True, True, [cst, LaU], [Q[q]])
            yield
            self.tt(vte[:].rearrange("p (h e) -> p h e", h=8), v[:].rearrange("p (h e) -> p h e", h=8),
                    E3[:, 8:16].unsqueeze(2).to_broadcast([128, 8, 64]), ALU.mult, [v, E3], [vte])
            for q in range(2):
                self.act(esb[:, q * 512:(q + 1) * 512], Q[q][:], AF.Exp, [Q[q]], [esb])
            yield
            self.tt(M[:], esb[:].rearrange("p (a b) -> p a b", a=8),
                    scm[:].unsqueeze(1).to_broadcast([128, 8, 128]), ALU.mult, [esb, scm], [M])
            yield
            self.mm(Q[0][:], fb[:, 640:768], Hbf[:], True, True, [fb, Hbf], [Q[0]])
            for hh in range(8):
                self.mm(Q[1][:, hh * 64:(hh + 1) * 64], M[:, hh, :], v[:, hh * 64:(hh + 1) * 64], True, True, [M, v], [Q[1]])
            yield
            self.tt(t1[:].rearrange("p (h e) -> p h e", h=8), Q[0][:].rearrange("p (h e) -> p h e", h=8),
                    E3[:, 0:8].unsqueeze(2).to_broadcast([128, 8, 64]), ALU.mult, [Q[0], E3], [t1])
            yield
            self.tt(t2[:], Q[1][:], t1[:], ALU.add, [Q[1], t1], [t2])
            self.mm(Q[0][:], fb[:, 512:640], vte[:], True, True, [fb, vte], [Q[0]])
            yield
            if d_ == 0:
                self.tt(t1[:].rearrange("p (h e) -> p h e", h=8), xs_g.rearrange("p (h e) -> p h e", h=8),
                        Dsk[:, HS].unsqueeze(2).to_broadcast([128, 8, 64]), ALU.mult, [fb, Dsk], [t1], eng=POOL)
                self.tt(ypt[:], t1[:], t2[:], ALU.add, [t1, t2], [ypt])
            else:
                self.act(ypt[:], t2[:], AF.Copy, [t2], [ypt])
            self.dma(ydst[tok, g * 512:(g + 1) * 512], ypt[:], [ypt], [self.yb[(d_, g)]], eng=ACT)
            self.tt(H2[:].rearrange("p (h e) -> p h e", h=8), H[:].rearrange("p (h e) -> p h e", h=8),
                    E3[:, 16:24].unsqueeze(2).to_broadcast([128, 8, 64]), ALU.mult, [H, E3], [H2], eng=POOL)
            yield
            self.tt(H[:], H2[:], Q[0][:], ALU.add, [H2, Q[0]], [H])
            yield
            self.act(Hbf[:], H[:], AF.Copy, [H], [Hbf])
            yield

    def stage_R(self, l):
        I = self.I
        cst = self.cst
        import os as _os
        with ExitStack() as st:
            wv = I["w_in"][l].rearrange("(k p) c -> p k c", p=128)
            wqk = self.sb(st, [128, 8, 1024], BF16, "wqk")
            wvv = self.sb(st, [128, 8, 512], BF16, "wvr")
            for kc in range(8):
                self.load_w(wqk[:, kc, 0:512], wqk, wv[:, kc, 3760:4272], I["w_in"])
                self.load_w(wqk[:, kc, 512:1024], wqk, wv[:, kc, 5328:5840], I["w_in"])
                self.load_w(wvv[:, kc, :], wvv, wv[:, kc, 4272:4784], I["w_in"])
            prm = {}
            for d_, sfx in ((0, "f"), (1, "b")):
                nm = "ret_log_rate_" + sfx
                lgB = self.bvec(st, nm, l, 4)
                self.act(lgB[:], lgB[:], AF.Exp, [lgB], [lgB])
                self.ts(lgB[:], lgB[:], -1.0, None, ALU.mult, None, [lgB], [lgB])
                lgs = self.sb(st, [128, 2], F32, "lgs")
                src = I[nm][l:l + 1, :].rearrange("o (p two) -> o p two", two=2)
                self.dma(lgs[0:64, :], src[:, :, 0].to_broadcast([64, 2]), [I[nm]], [lgs], slow=True)
                self.dma(lgs[64:128, :], src[:, :, 1].to_broadcast([64, 2]), [I[nm]], [lgs], slow=True)
                self.act(lgs[:], lgs[:], AF.Exp, [lgs], [lgs])
                self.ts(lgs[:], lgs[:], -1.0, None, ALU.mult, None, [lgs], [lgs])
                RIo = C_RIF if d_ == 0 else C_RIB
                Mo = C_MF if d_ == 0 else C_MB
                Go = C_GF if d_ == 0 else C_GB
                To = C_TEF if d_ == 0 else C_TEB
                DmT = self.sb(st, [128, 4, 128], F32, "DmT")
                for h in range(4):
                    self.act(DmT[:, h, :], cst[:, RIo:RIo + 128], AF.Exp, [cst, lgB], [DmT], scale=lgB[:, h:h + 1])
                self.tt(DmT[:], DmT[:], cst[:, Mo:Mo + 128].unsqueeze(1).to_broadcast([128, 4, 128]), ALU.mult, [DmT, cst], [DmT])
                self.ts(DmT[:], DmT[:], 0.125, None, ALU.mult, None, [DmT], [DmT])
                Gam = self.sb(st, [128, 2, 128], F32, "Gam")
                for p in range(2):
                    self.act(Gam[:, p, :], cst[:, Go:Go + 128], AF.Exp, [cst, lgs], [Gam], scale=lgs[:, p:p + 1])
                te = self.sb(st, [128, 4], F32, "te")
                self.act(te[:], lgB[:], AF.Exp, [lgB, cst], [te], scale=cst[:, To:To + 1])
                self.ts(te[:], te[:], 0.125, None, ALU.mult, None, [te], [te])
                g128 = self.sb(st, [128, 2], F32, "g128")
                self.act(g128[:], lgs[:], AF.Exp, [lgs], [g128], scale=128.0)
                prm[d_] = (DmT, Gam, te, g128)
            with ExitStack() as stf:
                sets = []
                for s_ in range(2):
                    B = {}
                    B["hc"] = self.sb(stf, [128, 8, 128], BF16, "hcrf")
                    B["tab"] = self.sb(stf, [128, 512], F32, "tabrf")
                    B["r1"] = self.sb(stf, [128, 512], F32, "r1")
                    B["r2"] = self.sb(stf, [128, 512], F32, "r2")
                    B["qkt"] = self.sb(stf, [128, 512], BF16, "qkt")
                    B["Rb"] = self.sb(stf, [128, 1280], BF16, "Rbw")
                    B["Q"] = [self.ps(stf, [128, 512], F32, "QRF") for _ in range(4)]
                    sets.append(B)
                gens = [(lambda cc: (lambda slot: self.ret_front(cc, sets[slot], wqk, wvv)))(c) for c in range(NCH)]
                self.interleave(gens, 2)
            self.P.barrier()
            with ExitStack() as st2:
                gens = []
                for d_ in (0, 1):
                    gens.append((lambda dd: (lambda slot: self.ret_sweep(l, dd, st2, prm[dd])))(d_))
                self.interleave(gens, 2)
            self.P.barrier()
            with ExitStack() as st3:
                wg = self.sb(st3, [128, 8, 512], BF16, "wg")
                for kc in range(8):
                    self.load_w(wg[:, kc, :], wg, wv[:, kc, 4784:5296], I["w_in"])
                sets = []
                for s in range(4):
                    B = {}
                    B["hc"] = self.sb(st3, [128, 8, 128], BF16, "hcr3")
                    B["rpf"] = self.sb(st3, [128, 512], BF16, "rpf")
                    B["rpb"] = self.sb(st3, [128, 512], BF16, "rpb")
                    B["rs"] = self.sb(st3, [128, 512], F32, "rs3")
                    B["ge"] = self.sb(st3, [128, 512], F32, "ge3")
                    B["sg"] = self.sb(st3, [128, 512], F32, "sg3")
                    B["stats"] = self.sb(st3, [128, 4, 6], F32, "stats3")
                    B["mv"] = self.sb(st3, [128, 4, 2], F32, "mv3")
                    B["yn"] = self.sb(st3, [128, 512], F32, "yn3")
                    B["ob"] = self.sb(st3, [128, 512], BF16, "obr3")
                    B["oT"] = self.sb(st3, [128, 4, 128], BF16, "oTr3")
                    B["PG"] = self.ps(st3, [128, 512], F32, "PGr3")
                    B["PT"] = self.ps(st3, [128, 1024], BF16, "PTr3")
                    sets.append(B)
                gens = [(lambda cc: (lambda slot: self.ret_final(cc, sets[slot], wg)))(c) for c in range(NCH)]
                self.interleave(gens, 4)

    def ret_final(self, c, B, wg):
        h_ = B["hc"]; rpf = B["rpf"]; rpb = B["rpb"]; ge = B["ge"]; sg = B["sg"]; stats = B["stats"]; mv = B["mv"]
        yn = B["yn"]; ob = B["ob"]; oT_ = B["oT"]; PG = B["PG"]; PT = B["PT"]
        c0 = colof(c * 128)
        tok = slice(c * 128, (c + 1) * 128)
        self.dma(h_[:], self.hv[:, :, c0:c0 + 128], [self.hT_d], [h_], slow=True)
        self.dma(rpf[:], self.rpart_d[tok, :], [self.rpart_d], [rpf])
        self.dma(rpb[:], self.rpartb_d[tok, :], [self.rpartb_d], [rpb])
        yield
        for kc in range(8):
            self.mm(PG[:], h_[:, kc, :], wg[:, kc, :], kc == 0, kc == 7, [h_, wg], [PG])
        rs = B["rs"]
        self.tt(rs[:], rpf[:], rpb[:], ALU.add, [rpf, rpb], [rs])
        yield
        self.act(ge[:], PG[:], AF.Exp, [PG], [ge], scale=-1.0)
        for h in range(4):
            self.P.op(DVE, (lambda hh: (lambda e: e.bn_stats(out=stats[:, hh, :], in_=rs[:, hh * 128:(hh + 1) * 128])))(h),
                      [rs.b], [stats.b])
            self.P.op(DVE, (lambda hh: (lambda e: e.bn_aggr(out=mv[:, hh, :], in_=stats[:, hh, :])))(h),
                      [stats.b], [mv.b])
        self.ts(mv[:, :, 1], mv[:, :, 1], LN_EPS, None, ALU.add, None, [mv], [mv])
        yield
        self.rsqrt(mv[:, :, 1], mv)
        self.sigm(ge[:], ge)
        yield
        self.tt(sg[:], PG[:], ge[:], ALU.mult, [PG, ge], [sg])
        for h in range(4):
            self.ts(yn[:, h * 128:(h + 1) * 128], rs[:, h * 128:(h + 1) * 128], mv[:, h, 0:1], mv[:, h, 1:2],
                    ALU.subtract, ALU.mult, [rs, mv], [yn])
        yield
        self.tt(ob[:], yn[:], sg[:], ALU.mult, [yn, sg], [ob])
        yield
        for h in range(4):
            self.tr(PT[:, h * 128:(h + 1) * 128], ob[:, h * 128:(h + 1) * 128], self.identb[:], [ob, self.identb], [PT])
        yield
        self.act(oT_[:], PT[:, 0:512].rearrange("p (a b) -> p a b", a=4), AF.Copy, [PT], [oT_])
        yield
        self.dma(self.mixT_d[c, :, 1536:2048], oT_[:].rearrange("p a b -> p (a b)"), [oT_], [self.mixT_d], eng=ACT)
        yield

    def ret_front(self, c, B, wqk, wvv):
        I = self.I
        h_ = B["hc"]; tab = B["tab"]; r1 = B["r1"]; r2 = B["r2"]; qkt = B["qkt"]; Rb = B["Rb"]; Q = B["Q"]
        Qb3 = Q[3][:].bitcast(BF16)
        c0 = colof(c * 128)
        self.dma(h_[:], self.hv[:, :, c0:c0 + 128], [self.hT_d], [h_], slow=True)
        self.dma(tab[:], I["ret_tab2"][c * 128:(c + 1) * 128, :], [I["ret_tab2"]], [tab])
        yield
        for j, (q_, w_, lo) in enumerate(((Q[0], wqk, 0), (Q[1], wqk, 512), (Q[2], wvv, 0))):
            for kc in range(8):
                self.mm(q_[:], h_[:, kc, :], w_[:, kc, lo:lo + 512], kc == 0, kc == 7, [h_, w_], [q_])
            yield
        self.tt(r1[:].rearrange("p (a b) -> p a b", a=2), Q[0][:].rearrange("p (a b) -> p a b", a=2),
                tab[:, 0:256].unsqueeze(1).to_broadcast([128, 2, 256]), ALU.mult, [Q[0], tab], [r1])
        yield
        self.tt(r2[:].rearrange("p (a b) -> p a b", a=2), Q[1][:].rearrange("p (a b) -> p a b", a=2),
                tab[:, 256:512].unsqueeze(1).to_broadcast([128, 2, 256]), ALU.mult, [Q[1], tab], [r2])
        self.act(Rb[:, 768:1280], Q[2][:], AF.Copy, [Q[2]], [Rb])
        yield
        self.tt(qkt[:], r1[:], r2[:], ALU.add, [r1, r2], [qkt], eng=POOL)
        yield
        for j in range(4):
            self.tr(Qb3[:, j * 128:(j + 1) * 128], qkt[:, j * 128:(j + 1) * 128], self.identb[:], [qkt, self.identb], [Q[3]])
        self.cp(Rb[:, 512:768], qkt[:, 256:512], [qkt], [Rb], eng=POOL)
        yield
        self.act(Rb[:, 0:512], Qb3[:, 0:512], AF.Copy, [Q[3]], [Rb])
        yield
        self.dma(self.Rb_d[c], Rb[:], [Rb], [self.Rb_d], eng=ACT)
        yield

    def ret_sweep(self, l, d_, st, prm):
        DmT, Gam, te, g128 = prm
        S = self.sb(st, [128, 2, 128], F32, "S")
        Sbf = self.sb(st, [128, 2, 128], BF16, "Sbf")
        Rb = [self.sb(st, [128, 1280], BF16, "Rbr") for _ in range(2)]
        qz = [self.sb(st, [128, 2, 128], BF16, "qz") for _ in range(2)]
        qdz = [self.sb(st, [128, 2, 128], BF16, "qdz") for _ in range(2)]
        for par in range(2):
            self.memset(qz[par][:], 0.0, [qz[par]])
            self.memset(qdz[par][:], 0.0, [qdz[par]])
        vte = self.sb(st, [128, 512], BF16, "vte")
        Mr = self.sb(st, [128, 4, 128], BF16, "Mr")
        yp = [self.sb(st, [128, 512], BF16, "ypr") for _ in range(2)]
        Q = [self.ps(st, [128, 512], F32, "QR") for _ in range(3)]
        ydst = self.rpart_d if d_ == 0 else self.rpartb_d
        order = list(range(NCH)) if d_ == 0 else [1, 0] + list(range(NCH - 1, 1, -1))
        self.memset(S[:], 0.0, [S])
        self.memset(Sbf[:], 0.0, [Sbf])
        self.dma(Rb[0][:], self.Rb_d[order[0]], [self.Rb_d], [Rb[0]])
        it = 0
        for ci, c in enumerate(order):
            rb = Rb[it % 2]; ypt = yp[it % 2]
            it += 1
            tok = slice(c * 128, (c + 1) * 128)
            if ci + 1 < len(order):
                self.dma(Rb[it % 2][:], self.Rb_d[order[ci + 1]], [self.Rb_d], [Rb[it % 2]])
            qk = rb[:, 0:512].rearrange("p (a b) -> p a b", a=4)
            for par in range(2):
                rr = 64 * par
                self.cp(qz[par][rr:rr + 64, :, :], qk[rr:rr + 64, 0:2, :], [rb], [qz[par]], eng=POOL)
                self.tt(qdz[par][rr:rr + 64, :, :], qk[rr:rr + 64, 0:2, :], Gam[rr:rr + 64, :, :], ALU.mult,
                        [rb, Gam], [qdz[par]], eng=POOL)
            self.tt(vte[:].rearrange("p (h e) -> p h e", h=4), rb[:, 768:1280].rearrange("p (h e) -> p h e", h=4),
                    te[:].unsqueeze(2).to_broadcast([128, 4, 128]), ALU.mult, [rb, te], [vte], eng=POOL)
            yield
            for h in range(4):
                p = h // 2
                self.mm(Q[0][:, h * 128:(h + 1) * 128], qk[:, 2 + p, :], qz[h % 2][:, p, :], True, True, [rb, qz[h % 2]], [Q[0]])
            yield
            self.tt(Mr[:], Q[0][:].rearrange("p (a b) -> p a b", a=4), DmT[:], ALU.mult, [Q[0], DmT], [Mr])
            yield
            for h in range(4):
                p = h // 2
                self.mm(Q[1][:, h * 128:(h + 1) * 128], Mr[:, h, :], rb[:, 768 + h * 128:768 + (h + 1) * 128], True, False, [Mr, rb], [Q[1]])
                self.mm(Q[1][:, h * 128:(h + 1) * 128], qdz[h % 2][:, p, :], Sbf[:, p, :], False, True, [qdz[h % 2], Sbf], [Q[1]])
            for h in range(4):
                p = h // 2
                self.mm(Q[2][:, h * 128:(h + 1) * 128], rb[:, 512 + p * 128:512 + (p + 1) * 128], vte[:, h * 128:(h + 1) * 128],
                        True, True, [rb, vte], [Q[2]])
            yield
            self.cp(ypt[:], Q[1][:], [Q[1]], [ypt])
            self.dma(ydst[tok, :], ypt[:], [ypt], [ydst], eng=ACT)
            for h in range(4):
                p, r0 = h // 2, (h % 2) * 64
                self.stt(S[r0:r0 + 64, p, :], S[r0:r0 + 64, p, :], g128[r0:r0 + 64, p:p + 1], Q[2][r0:r0 + 64, h * 128:(h + 1) * 128],
                         ALU.mult, ALU.add, [S, g128, Q[2]], [S])
            yield
            self.act(Sbf[:], S[:], AF.Copy, [S], [Sbf])
            yield

    def stage_M(self, l):
        I = self.I
        cst = self.cst
        blocks = [(0, 256)] + [(256 + 512 * i, 512) for i in range(8)]
        with ExitStack() as st1:
            v_all = self.sb(st1, [128, NCH, 512], BF16, "v_all")
            with ExitStack() as st:
                wv = I["w_in"][l].rearrange("(k p) c -> p k c", p=128)
                wm = self.sb(st, [128, 8, 704], BF16, "wm")
                wgate = self.sb(st, [128, 8, 512], BF16, "wgate")
                self.load_w(wm[:, :, 0:672], wm, wv[:, :, 2576:3248], I["w_in"])
                self.load_w(wm[:, :, 672:704], wm, wv[:, :, 5296:5328], I["w_in"])
                self.load_w(wgate[:], wgate, wv[:, :, 3248:3760], I["w_in"])
                wuq = self.sb(st, [128, 3, 8, 96], BF16, "wuq")
                wuqs = self.sb(st, [128, 3, 8, 96], BF16, "wuqs")
                wkp = self.sb(st, [128, 2, 8, 96], BF16, "wkp")
                wvv = self.sb(st, [128, 2, 8, 64], BF16, "wvv")
                self.memset(wuqs[:], 0.0, [wuqs])
                self.memset(wkp[:], 0.0, [wkp])
                uqv = I["mla_w_uq"][l].rearrange("(k p) (h e) -> p k h e", p=128, h=8)
                uqs = I["w_uq_sw"][l].rearrange("(k p) (h e) -> p k h e", p=128, h=8)
                ukv = I["mla_w_ukv"][l].rearrange("(k p) (h e) -> p k h e", p=128, h=8)
                for kc in range(3):
                    self.load_w(wuq[:, kc, :, :], wuq, uqv[:, kc, :, :], I["mla_w_uq"])
                    self.load_w(wuqs[:, kc, :, 64:96], wuqs, uqs[:, kc, :, :], I["w_uq_sw"])
                for kc in range(2):
                    self.load_w(wkp[:, kc, :, 0:64], wkp, ukv[:, kc, :, 0:64], I["mla_w_ukv"])
                    self.load_w(wvv[:, kc, :, :], wvv, ukv[:, kc, :, 64:128], I["mla_w_ukv"])
                esel = self.sb(st, [32, 96], BF16, "esel")
                self.memset(esel[:], 0.0, [esel])
                self.cp(esel[:, 64:96], self.identb[0:32, 0:32], [self.identb, esel], [esel])
                qn = self.sb(st, [128, 3], F32, "qn")
                kvn = self.sb(st, [128, 2], F32, "kvn")
                self.dma(qn[:], I["mla_q_norm"][l].rearrange("(k p) -> p k", p=128), [I["mla_q_norm"]], [qn], slow=True)
                self.dma(kvn[:], I["mla_kv_norm"][l].rearrange("(k p) -> p k", p=128), [I["mla_kv_norm"]], [kvn], slow=True)
                hb = [self.sb(st, [128, 8, 512], BF16, "hb") for _ in range(2)]
                tq = self.sb(st, [96, 2, 512], F32, "tq")
                self.memset(tq[0:64, 0, :], 1.0, [tq])
                self.memset(tq[0:64, 1, :], 0.0, [tq])
                tk = self.sb(st, [32, 2, 512], F32, "tk")
                cqs = self.sb(st, [128, 3, 512], F32, "cqs")
                sqs2 = [self.sb(st, [128, 512], F32, "sqs") for _ in range(2)]
                rstd = self.sb(st, [128, 512], F32, "rstd")
                cqn = self.sb(st, [128, 3, 512], BF16, "cqn")
                ckvn = self.sb(st, [128, 2, 512], BF16, "ckvn")
                kr1 = self.sb(st, [32, 512], F32, "kr1")
                kr2 = self.sb(st, [32, 512], F32, "kr2")
                krr = self.sb(st, [32, 512], BF16, "krr")
                q1s = [self.sb(st, [96, 512], F32, "q1") for _ in range(2)]
                q2s = [self.sb(st, [96, 512], F32, "q2") for _ in range(2)]
                qf = [self.sb(st, [96, 512], BF16, "qf") for _ in range(2)]
                kf = [self.sb(st, [96, 512], BF16, "kf") for _ in range(2)]
                ges = [self.sb(st, [128, 512], F32, "ge") for _ in range(2)]
                sgo = [self.sb(st, [128, 512], F32, "sgo") for _ in range(2)]
                P0 = [self.ps(st, [128, 512], F32, "P0") for _ in range(2)]
                PSS = self.ps(st, [128, 512], F32, "PSS")
                PQ1 = self.ps(st, [128, 512], F32, "PQ1")
                PQ2 = self.ps(st, [128, 512], F32, "PQ2")
                PK = self.ps(st, [128, 512], F32, "PK")
                PVv = self.ps(st, [128, 512], F32, "PVv")
                PG = self.ps(st, [128, 512], F32, "PG")
                sgv = self.sgT_d[:].rearrange("(r p) t -> p r t", p=128)
                ctr = 0
                t00, n00 = blocks[0]
                self.dma(hb[0][:, :, 0:n00], self.hv[:, :, colof(t00):colof(t00) + n00], [self.hT_d], [hb[0]], slow=True)
                for bi, (t0, n) in enumerate(blocks):
                    h_ = hb[bi % 2]
                    c0 = colof(t0)
                    if bi + 1 < len(blocks):
                        tn_, nn_ = blocks[bi + 1]
                        self.dma(hb[(bi + 1) % 2][:, :, 0:nn_], self.hv[:, :, colof(tn_):colof(tn_) + nn_], [self.hT_d], [hb[(bi + 1) % 2]], slow=True)
                    self.dma(tq[64:96, :, 0:n], I["mla_tab"][:, :, t0:t0 + n], [I["mla_tab"]], [tq], slow=True)
                    self.dma(tk[:, :, 0:n], I["mla_tab"][:, :, t0:t0 + n], [I["mla_tab"]], [tk], slow=True)
                    for (nrc, off, dst, nrm, dim, keep) in ((3, 0, cqn, qn, 384.0, None), (2, 384, ckvn, kvn, 256.0, None)):
                        for rc in range(nrc):
                            p_ = P0[ctr % 2]; ctr += 1
                            for kc in range(8):
                                self.mm(p_[:, 0:n], wm[:, kc, off + rc * 128:off + (rc + 1) * 128], h_[:, kc, 0:n], kc == 0, kc == 7, [wm, h_], [p_])
                            sqs = sqs2[ctr % 2]
                            self.act(cqs[:, rc, 0:n], p_[:, 0:n], AF.Copy, [p_], [cqs])
                            self.act(sqs[:, 0:n], p_[:, 0:n], AF.Square, [p_], [sqs])
                            self.mm(PSS[:, 0:n], self.ones[:], sqs[:, 0:n], rc == 0, rc == nrc - 1, [self.ones, sqs], [PSS])
                        self.ts(rstd[:, 0:n], PSS[:, 0:n], 1.0 / dim, RMS_EPS, ALU.mult, ALU.add, [PSS], [rstd])
                        self.rsqrt(rstd[:, 0:n], rstd)
                        for rc in range(nrc):
                            self.stt(dst[:, rc, 0:n], cqs[:, rc, 0:n], nrm[:, rc:rc + 1], rstd[:, 0:n], ALU.mult, ALU.mult, [cqs, nrm, rstd], [dst])
                    self.mm_group_kr(h_, n, wm, PQ1, PQ2)
                    self.tt(kr1[:, 0:n], PQ1[0:32, 0:n], tk[:, 0, 0:n], ALU.mult, [PQ1, tk], [kr1])
                    self.tt(kr2[:, 0:n], PQ2[0:32, 0:n], tk[:, 1, 0:n], ALU.mult, [PQ2, tk], [kr2])
                    self.tt(krr[:, 0:n], kr1[:, 0:n], kr2[:, 0:n], ALU.add, [kr1, kr2], [krr])
                    for h in range(8):
                        qf_ = qf[h % 2]; kf_ = kf[h % 2]
                        q1 = q1s[h % 2]; q2 = q2s[h % 2]
                        PQ1_, PQ2_, PK_ = (PQ1, PQ2, PK) if h % 2 == 0 else (P0[0], P0[1], PG)
                        for kc in range(3):
                            self.mm(PQ1_[0:96, 0:n], wuq[:, kc, h, :], cqn[:, kc, 0:n], kc == 0, kc == 2, [wuq, cqn], [PQ1_])
                        for kc in range(3):
                            self.mm(PQ2_[0:96, 0:n], wuqs[:, kc, h, :], cqn[:, kc, 0:n], kc == 0, kc == 2, [wuqs, cqn], [PQ2_])
                        self.tt(q1[:, 0:n], PQ1_[0:96, 0:n], tq[:, 0, 0:n], ALU.mult, [PQ1_, tq], [q1])
                        self.tt(q2[:, 0:n], PQ2_[0:96, 0:n], tq[:, 1, 0:n], ALU.mult, [PQ2_, tq], [q2])
                        self.tt(qf_[:, 0:n], q1[:, 0:n], q2[:, 0:n], ALU.add, [q1, q2], [qf_], eng=POOL)
                        self.dma(self.qT_d[h, :, t0:t0 + n], qf_[:, 0:n], [qf_], [self.qT_d])
                        for kc in range(2):
                            self.mm(PK_[0:96, 0:n], wkp[:, kc, h, :], ckvn[:, kc, 0:n], kc == 0, False, [wkp, ckvn], [PK_])
                        self.mm(PK_[0:96, 0:n], esel[:], krr[:, 0:n], False, True, [esel, krr], [PK_])
                        self.act(kf_[:, 0:n], PK_[0:96, 0:n], AF.Copy, [PK_], [kf_])
                        self.dma(self.kfT_d[h, :, t0:t0 + n], kf_[:, 0:n], [kf_], [self.kfT_d], eng=ACT)
                    for s in range(n // 128):
                        ch = (t0 + s * 128) // 128
                        PV_ = PVv if s % 2 == 0 else PSS
                        for kc in range(2):
                            self.mm(PV_[:], ckvn[:, kc, s * 128:(s + 1) * 128], wvv[:, kc, :, :].rearrange("p h e -> p (h e)"),
                                    kc == 0, kc == 1, [ckvn, wvv], [PV_])
                        self.act(v_all[:, ch, :], PV_[:], AF.Copy, [PV_], [v_all])
                    for rc in range(4):
                        sg_ = sgo[rc % 2]
                        ge = ges[rc % 2]
                        PG_ = PG if rc % 2 == 0 else PK
                        for kc in range(8):
                            self.mm(PG_[:, 0:n], wgate[:, kc, rc * 128:(rc + 1) * 128], h_[:, kc, 0:n], kc == 0, kc == 7, [wgate, h_], [PG_])
                        self.act(ge[:, 0:n], PG_[:, 0:n], AF.Exp, [PG_], [ge], scale=-1.0)
                        self.sigm(ge[:, 0:n], ge)
                        self.tt(sg_[:, 0:n], PG_[:, 0:n], ge[:, 0:n], ALU.mult, [PG_, ge], [sg_])
                        self.dma(sgv[:, rc, t0:t0 + n], sg_[:, 0:n], [sg_], [self.sgT_d], eng=ACT)
            self.P.barrier()
            import os as _os
            if _os.environ.get("NO_M2"):
                return
            with ExitStack() as st:
                kfh = [self.sb(st, [96, NTOK], BF16, "kfh") for _ in range(2)]
                qh = [self.sb(st, [96, NTOK], BF16, "qh") for _ in range(2)]
                vaug = [self.sb(st, [128, NCH, 128], BF16, "vaug") for _ in range(2)]
                self.memset(vaug[0][:, :, 64:128], 1.0, [vaug[0]])
                self.memset(vaug[1][:, :, 0:64], 1.0, [vaug[1]])
                pT = [self.sb(st, [128, 1024], BF16, "pT") for _ in range(3)]
                sgh = [self.sb(st, [128, 512], F32, "sgh") for _ in range(2)]
                rden = self.sb(st, [128, 512], F32, "rden")
                ot = self.sb(st, [128, 512], F32, "ot")
                ob = [self.sb(st, [128, 512], BF16, "ob") for _ in range(2)]
                PSc = [self.ps(st, [128, 1024], F32, "PSc") for _ in range(3)]
                PO = [self.ps(st, [128, 512], F32, "PO") for _ in range(2)]
                ci = 0
                bi_ = 0
                NH_ = int(_os.environ.get("M2_HEADS", "8"))
                self.dma(kfh[0][:], self.kfT_d[0], [self.kfT_d], [kfh[0]])
                self.dma(qh[0][:], self.qT_d[0], [self.qT_d], [qh[0]])
                for h in range(NH_):
                    par = h % 2
                    r0 = 64 * par
                    d0 = 64 - r0
                    kf_ = kfh[h % 2]; q_ = qh[h % 2]; va = vaug[par]
                    if h + 1 < NH_:
                        self.dma(kfh[(h + 1) % 2][:], self.kfT_d[h + 1], [self.kfT_d], [kfh[(h + 1) % 2]])
                        self.dma(qh[(h + 1) % 2][:], self.qT_d[h + 1], [self.qT_d], [qh[(h + 1) % 2]])
                    self.cp(va[:, :, r0:r0 + 64], v_all[:, :, h * 64:(h + 1) * 64], [v_all], [va], eng=POOL)
                    its = []
                    for (t0, n) in blocks:
                        if t0 == 0:
                            if l == DEPTH - 1:
                                continue
                            kcs = [0, 1]
                        else:
                            kcs = list(range(NCH))
                        po = PO[bi_ % 2]; sg_ = sgh[bi_ % 2]; ob_ = ob[bi_ % 2]
                        bi_ += 1
                        npair = len(kcs) // 2
                        for i in range(npair):
                            its.append((t0, n, kcs[2 * i], kcs[2 * i + 1], i == 0, i == npair - 1, po, sg_, ob_))
                    LOOK = 2
                    for j in range(len(its) + LOOK):
                        if j < len(its):
                            (t0, n, ka, kb, first, last, po, sg_, ob_) = its[j]
                            if first:
                                self.dma(sg_[r0:r0 + 64, 0:n], self.sgT_d[64 * h:64 * (h + 1), t0:t0 + n], [self.sgT_d], [sg_])
                            psc = PSc[(ci + j) % 3]
                            self.mm(psc[:, 0:n], kf_[:, ka * 128:(ka + 1) * 128], q_[:, t0:t0 + n], True, True, [kf_, q_], [psc])
                            self.mm(psc[:, 512:512 + n], kf_[:, kb * 128:(kb + 1) * 128], q_[:, t0:t0 + n], True, True, [kf_, q_], [psc])
                        jj = j - LOOK
                        if jj >= 0:
                            (t0, n, ka, kb, first, last, po, sg_, ob_) = its[jj]
                            psc = PSc[(ci + jj) % 3]; pt = pT[(ci + jj) % 3]
                            self.act(pt[:].rearrange("p (a b) -> p a b", a=2)[:, :, 0:n], psc[:].rearrange("p (a b) -> p a b", a=2)[:, :, 0:n],
                                     AF.Exp, [psc], [pt], scale=MLA_SCALE)
                            self.mm(po[:, 0:n], va[:, ka, :], pt[:, 0:n], first, False, [va, pt], [po])
                            self.mm(po[:, 0:n], va[:, kb, :], pt[:, 512:512 + n], False, last, [va, pt], [po])
                            if last:
                                self.P.op(DVE, (lambda a_, b_: (lambda e: e.reciprocal(out=a_, in_=b_)))(rden[d0:d0 + 64, 0:n], po[d0:d0 + 64, 0:n]),
                                          [po.b], [rden.b])
                                self.tt(ot[r0:r0 + 64, 0:n], po[r0:r0 + 64, 0:n], rden[d0:d0 + 64, 0:n], ALU.mult, [po, rden], [ot])
                                self.tt(ob_[r0:r0 + 64, 0:n], ot[r0:r0 + 64, 0:n], sg_[r0:r0 + 64, 0:n], ALU.mult, [ot, sg_], [ob_], eng=POOL)
                                kcm = 8 + h // 2
                                self.dma(self.mixT_d[t0 // 128:(t0 + n) // 128, r0:r0 + 64, kcm * 128:(kcm + 1) * 128].rearrange("c p t -> p c t"),
                                         ob_[r0:r0 + 64, 0:n].rearrange("p (c t) -> p c t", t=128), [ob_], [self.mixT_d], slow=True)
                    ci += len(its)

    def mm_group_kr(self, h_, n, wm, PQ1, PQ2):
        for kc in range(8):
            self.mm(PQ1[0:32, 0:n], wm[:, kc, 640:672], h_[:, kc, 0:n], kc == 0, kc == 7, [wm, h_], [PQ1])
        for kc in range(8):
            self.mm(PQ2[0:32, 0:n], wm[:, kc, 672:704], h_[:, kc, 0:n], kc == 0, kc == 7, [wm, h_], [PQ2])

    def stage_E(self, l):
        I = self.I
        with ExitStack() as st:
            wo = self.sb(st, [128, 16, 1024], BF16, "wo")
            wov = I["w_out"][l].rearrange("(k p) c -> p k c", p=128)
            for kc in range(16):
                self.load_w(wo[:, kc, :], wo, wov[:, kc, :], I["w_out"])
            lng = self.bvec(st, "ln_g", l, 1024)
            lnb = self.bvec(st, "ln_b", l, 1024)
            sets = []
            for s_ in range(4):
                B = {}
                B["mt"] = self.sb(st, [128, 16, 128], BF16, "mt")
                B["xt"] = self.sb(st, [128, D], F32, "xt")
                B["v1"] = self.sb(st, [128, D], F32, "v1")
                B["v2"] = self.sb(st, [128, D], F32, "v2")
                B["xo"] = self.sb(st, [128, D], F32, "xo")
                B["stats"] = self.sb(st, [128, 2, 6], F32, "stats")
                B["mv"] = self.sb(st, [128, 2], F32, "mv")
                B["PZ"] = [self.ps(st, [128, 512], F32, "PZ") for _ in range(2)]
                sets.append(B)
            tiles = list(range(NCH)) if l < DEPTH - 1 else list(range(2, NCH))
            gens = [(lambda tt_: (lambda slot: self.e_tile(l, tt_, sets[slot], wo, lng, lnb)))(t) for t in tiles]
            self.interleave(gens, 4)

    def e_tile(self, l, t, B, wo, lng, lnb):
        I = self.I
        m_ = B["mt"]; x_ = B["xt"]; v1 = B["v1"]; v2 = B["v2"]; xo_ = B["xo"]; stats = B["stats"]; mv = B["mv"]; PZ = B["PZ"]
        typ = 1 if t < 2 else 0
        tok = slice(t * 128, (t + 1) * 128)
        self.dma(m_[:].rearrange("p a b -> p (a b)"), self.mixT_d[t], [self.mixT_d], [m_])
        if l == 0:
            src_t = I["ctx"] if t < 2 else I["x"]
            src = src_t[t * 128:(t + 1) * 128, :] if t < 2 else src_t[(t - 2) * 128:(t - 1) * 128, :]
        else:
            src_t = self.xres_d
            src = src_t[tok, :]
        self.dma(x_[:], src, [src_t], [x_])
        yield
        for nb in range(2):
            for kc in range(16):
                self.mm(PZ[nb][:], m_[:, kc, :], wo[:, kc, nb * 512:(nb + 1) * 512], kc == 0, kc == 15, [m_, wo], [PZ[nb]])
            yield
        for nb in range(2):
            self.tt(v1[:, nb * 512:(nb + 1) * 512], PZ[nb][:], self.gB[:, typ, nb * 512:(nb + 1) * 512], ALU.mult, [PZ[nb], self.gB], [v1])
        yield
        self.stt(v2[:], x_[:], ALPHA, v1[:], ALU.mult, ALU.add, [x_, v1], [v2])
        yield
        for s in range(2):
            self.P.op(DVE, (lambda ss: (lambda e: e.bn_stats(out=stats[:, ss, :], in_=v2[:, ss * 512:(ss + 1) * 512])))(s),
                      [v2.b], [stats.b])
        self.P.op(DVE, lambda e: e.bn_aggr(out=mv[:], in_=stats[:]), [stats.b], [mv.b])
        self.ts(mv[:, 1:2], mv[:, 1:2], LN_EPS, None, ALU.add, None, [mv], [mv])
        yield
        self.rsqrt(mv[:, 1:2], mv)
        yield
        self.ts(v1[:], v2[:], mv[:, 0:1], mv[:, 1:2], ALU.subtract, ALU.mult, [v2, mv], [v1])
        yield
        self.tt(v2[:], v1[:], lng[:], ALU.mult, [v1, lng], [v2], eng=POOL)
        yield
        self.tt(xo_[:], v2[:], lnb[:], ALU.add, [v2, lnb], [xo_], eng=POOL)
        yield
        if l < DEPTH - 1:
            self.dma(self.xres_d[tok, :], xo_[:], [xo_], [self.xres_d], eng=ACT)
        else:
            self.dma(self.out[(t - 2) * 128:(t - 1) * 128, :], xo_[:], [xo_], [self.out], eng=ACT)
        yield


C_ID = 0
C_UF = 128
C_LF = 256
C_UB = 384
C_LB = 512
C_MF = 640
C_MB = 768
C_RIF = 896
C_RIB = 1024
C_GF = 1152
C_GB = 1280
C_TEF = 1408
C_TEB = 1409
CST_W = 1410


def make_consts():
    k = np.arange(128)[:, None].astype(np.float32)
    i = np.arange(128)[None, :].astype(np.float32)
    cst = np.zeros((128, CST_W), np.float32)
    cst[:, C_ID:C_ID + 128] = (k == i)
    cst[:, C_UF:C_UF + 128] = (k <= i)
    cst[:, C_LF:C_LF + 128] = (k > i)
    cst[:, C_UB:C_UB + 128] = (k >= i)
    cst[:, C_LB:C_LB + 128] = (k < i)
    cst[:, C_MF:C_MF + 128] = (k <= i)
    cst[:, C_MB:C_MB + 128] = (k >= i)
    cst[:, C_RIF:C_RIF + 128] = np.maximum(i - k, 0)
    cst[:, C_RIB:C_RIB + 128] = np.maximum(k - i, 0)
    cst[:, C_GF:C_GF + 128] = np.broadcast_to(i + 1, (128, 128))
    cst[:, C_GB:C_GB + 128] = np.broadcast_to(128 - i, (128, 128))
    cst[:, C_TEF] = 127 - k[:, 0]
    cst[:, C_TEB] = k[:, 0]
    return cst


def rope_tables():
    rows = SEQ // 64
    t = np.arange(rows * 64)
    row = (t // 64).astype(np.float32)
    col = (t % 64).astype(np.float32)

    def cs(rot):
        nf = rot // 4
        inv = (np.float32(10000.0) ** (-np.arange(nf, dtype=np.float32) / np.float32(nf))).astype(np.float32)
        ang = np.concatenate([row[:, None] * inv, col[:, None] * inv], -1).astype(np.float32)
        return np.cos(ang).astype(np.float32), np.sin(ang).astype(np.float32)
    cm, sm = cs(32)
    mla = np.zeros((32, 2, NTOK), np.float32)
    mla[:, 0, :CTX] = 1.0
    mla[0:16, 0, CTX:] = cm.T; mla[16:32, 0, CTX:] = cm.T
    mla[0:16, 1, CTX:] = -sm.T; mla[16:32, 1, CTX:] = sm.T
    cr, sr = cs(64)
    ret = np.zeros((128, 4, NTOK), np.float32)
    ret[:, 0, :CTX] = 1.0
    for hh in range(2):
        b = hh * 64
        ret[b:b + 32, 0, CTX:] = cr.T; ret[b + 32:b + 64, 0, CTX:] = cr.T
        ret[b:b + 32, 1, CTX:] = -sr.T; ret[b + 32:b + 64, 1, CTX:] = sr.T
    ret[:, 2] = ret[:, 0] * 0.125
    ret[:, 3] = ret[:, 1] * 0.125
    ret2 = np.zeros((NTOK, 1024), np.float32)
    cc = np.ones((NTOK, 4, 64), np.float32)
    ss = np.zeros((NTOK, 4, 64), np.float32)
    cc[CTX:, :, 0:32] = cr[:, None, :]; cc[CTX:, :, 32:64] = cr[:, None, :]
    ss[CTX:, :, 0:32] = -sr[:, None, :]; ss[CTX:, :, 32:64] = sr[:, None, :]
    ret2 = np.zeros((NTOK, 512), np.float32)
    ret2[:, 0:256] = cc.reshape(NTOK, 256)
    ret2[:, 256:512] = ss.reshape(NTOK, 256)
    return mla, ret, ret2


_CACHE = {}


def prep_inputs(inputs):
    f = lambda a: np.ascontiguousarray(np.asarray(a, dtype=np.float32))
    w_in = f(inputs["w_in"])
    kr = w_in[:, :, 3216:3248]
    kr_sw = np.concatenate([kr[:, :, 16:32], kr[:, :, 0:16]], -1)

    def sw64(a):
        a = a.reshape(2, D, 4, 2, 32)
        return np.ascontiguousarray(a[:, :, :, ::-1, :]).reshape(2, D, 256)
    q_sw = sw64(w_in[:, :, 3760:4016])
    k_sw = sw64(w_in[:, :, 4016:4272])
    w_ext = np.ascontiguousarray(np.concatenate([w_in, kr_sw, q_sw, k_sw], -1))
    uq = f(inputs["mla_w_uq"]).reshape(2, 384, 8, 96)
    uq_r = uq[:, :, :, 64:96]
    uq_sw = np.ascontiguousarray(np.concatenate([uq_r[..., 16:32], uq_r[..., 0:16]], -1)).reshape(2, 384, 256)
    mla_tab, ret_tab, ret_tab2 = rope_tables()
    shared = {
        "w_ada": f(inputs["w_ada"]), "b_ada": f(inputs["b_ada"]), "w_in": w_ext,
        "ssd_conv_w": f(inputs["ssd_conv_w"]), "ssd_conv_b": f(inputs["ssd_conv_b"]),
        "ssd_norm_w": f(inputs["ssd_norm_w"]), "mla_q_norm": f(inputs["mla_q_norm"]),
        "mla_w_uq": f(inputs["mla_w_uq"]), "w_uq_sw": uq_sw, "mla_kv_norm": f(inputs["mla_kv_norm"]),
        "mla_w_ukv": f(inputs["mla_w_ukv"]), "ret_log_rate_f": f(inputs["ret_log_rate_f"]),
        "ret_log_rate_b": f(inputs["ret_log_rate_b"]), "w_out": f(inputs["w_out"]),
        "ln_g": f(inputs["ln_g"]), "ln_b": f(inputs["ln_b"]),
        "cst": make_consts(), "mla_tab": mla_tab, "ret_tab": ret_tab, "ret_tab2": ret_tab2,
    }
    for n in ("ssd_a_log_f", "ssd_a_log_b", "ssd_dt_bias_f", "ssd_dt_bias_b", "ssd_d"):
        shared[n] = f(inputs[n])
    x = f(inputs["x"]); c = f(inputs["c"]); ctx = f(inputs["ctx"]); c_ctx = f(inputs["c_ctx"])
    maps = []
    for b in range(8):
        m = dict(shared)
        m["x"] = x[b]
        m["ctx"] = ctx[b]
        m["cvec"] = np.ascontiguousarray(np.stack([c[b], c_ctx], 0))
        maps.append(m)
    return maps


def kernel(**inputs):
    if "nc" not in _CACHE:
        _CACHE["nc"] = K().build()
    nc = _CACHE["nc"]
    maps = prep_inputs(inputs)
    res = run_bass_kernel_spmd(nc, maps, core_ids=list(range(8)))
    return np.stack([np.asarray(r["out"], dtype=np.float32) for r in res.results], 0)
```
